# Optimizing a Trainium2 kernel written in Bass

```python
import math
import jax, jax.numpy as jnp
from jax import lax
import numpy as np

D_MODEL = 2048
BATCH = 4
SEQ = 2048
DEPTH = 2

FFN_DIM = int(math.ceil(8 * D_MODEL / 3 / 256)) * 256
N_SUBLAYERS = 3
RMS_EPS = 1e-6
S5_WIDTH = D_MODEL // 2
S5_GROUP_CH = 16
S5_GROUPS = S5_WIDTH // S5_GROUP_CH
S5_STATE = 64
SSD_INNER = D_MODEL
SSD_HEAD_DIM = 64
SSD_HEADS = SSD_INNER // SSD_HEAD_DIM
SSD_GROUPS = 8
SSD_STATE = 128
SSD_CONV = 4
SSD_CHUNK = 128
SSD_CONV_DIM = SSD_INNER + 2 * SSD_GROUPS * SSD_STATE
MIX_WIDTH = S5_WIDTH + SSD_INNER
IN_PROJ_DIM = S5_WIDTH + SSD_INNER + SSD_CONV_DIM + SSD_HEADS
RWKV_HEAD_DIM = 64
RWKV_HEADS = D_MODEL // RWKV_HEAD_DIM
DECAY_LORA = max(32, int(round(1.8 * math.sqrt(D_MODEL) / 32)) * 32)
AAA_LORA = max(32, int(round(1.8 * math.sqrt(D_MODEL) / 32)) * 32)
GATE_LORA = max(32, int(round(0.6 * D_MODEL ** 0.8 / 32)) * 32)
RWKV_GN_EPS = 64e-5

kernel_name = 'hybrid_s5_ssd_rwkv7_macaron_adaln'


def rmsnorm(x, g):
    xf = x.astype(jnp.float32)
    xn = xf * lax.rsqrt(jnp.mean(xf * xf, axis=-1, keepdims=True) + RMS_EPS)
    return xn.astype(x.dtype) * g


def adaln(x, g, m):
    return rmsnorm(x, g) * (1 + m[:, 1][:, None, :]) + m[:, 0][:, None, :]


def swiglu(h, w1, w3, w2):
    return (jax.nn.silu(h @ w1) * (h @ w3)) @ w2


def segsum(v):
    t = v.shape[-1]
    cs = jnp.cumsum(v, axis=-1)
    seg = cs[..., :, None] - cs[..., None, :]
    mask = jnp.tril(jnp.ones((t, t), dtype=bool))
    return jnp.where(mask, seg, -jnp.inf)


def s5_branch(u, lam_re, lam_im, log_dt, b_re, b_im, c_re, c_im, d_skip, glu_w, glu_b):
    bsz, seqlen, _ = u.shape
    uf = u.astype(jnp.float32).reshape(bsz, seqlen, S5_GROUPS, S5_GROUP_CH)
    dt = jnp.exp(log_dt.astype(jnp.float32))[:, None]
    lr = jnp.minimum(lam_re.astype(jnp.float32), -1e-4)
    li = lam_im.astype(jnp.float32)
    mag = jnp.exp(lr * dt)
    ang = li * dt
    lb_re, lb_im = mag * jnp.cos(ang), mag * jnp.sin(ang)
    den = lr * lr + li * li
    nr, ni = lb_re - 1.0, lb_im
    f_re = (nr * lr + ni * li) / den
    f_im = (ni * lr - nr * li) / den
    br, bi = b_re.astype(jnp.float32), b_im.astype(jnp.float32)
    bb_re = f_re[..., None] * br - f_im[..., None] * bi
    bb_im = f_re[..., None] * bi + f_im[..., None] * br
    bu_re = jnp.einsum('blgc,gpc->blgp', uf, bb_re)
    bu_im = jnp.einsum('blgc,gpc->blgp', uf, bb_im)
    a_re = jnp.broadcast_to(lb_re, (1, seqlen, S5_GROUPS, S5_STATE))
    a_im = jnp.broadcast_to(lb_im, (1, seqlen, S5_GROUPS, S5_STATE))

    def combine(e1, e2):
        a1r, a1i, b1r, b1i = e1
        a2r, a2i, b2r, b2i = e2
        return (a2r * a1r - a2i * a1i,
                a2r * a1i + a2i * a1r,
                a2r * b1r - a2i * b1i + b2r,
                a2r * b1i + a2i * b1r + b2i)

    _, _, s_re, s_im = lax.associative_scan(combine, (a_re, a_im, bu_re, bu_im), axis=1)
    y = (jnp.einsum('blgp,gcp->blgc', s_re, c_re.astype(jnp.float32))
         - jnp.einsum('blgp,gcp->blgc', s_im, c_im.astype(jnp.float32)))
    y = y.reshape(bsz, seqlen, S5_WIDTH) + d_skip.astype(jnp.float32) * uf.reshape(bsz, seqlen, S5_WIDTH)
    y = jax.nn.gelu(y)
    y = y * jax.nn.sigmoid(y @ glu_w.astype(jnp.float32) + glu_b.astype(jnp.float32))
    return y.astype(u.dtype)


def ssd_chunked(xs, dt, a, bs, cs):
    bsz, seqlen = xs.shape[:2]
    nc = seqlen // SSD_CHUNK
    rpg = SSD_HEADS // SSD_GROUPS
    xq = (xs * dt[..., None]).reshape(bsz, nc, SSD_CHUNK, SSD_GROUPS, rpg, SSD_HEAD_DIM)
    adt = jnp.moveaxis((dt * a).reshape(bsz, nc, SSD_CHUNK, SSD_GROUPS, rpg), 2, -1)
    bq = bs.reshape(bsz, nc, SSD_CHUNK, SSD_GROUPS, SSD_STATE)
    cq = cs.reshape(bsz, nc, SSD_CHUNK, SSD_GROUPS, SSD_STATE)
    a_cum = jnp.cumsum(adt, axis=-1)
    decay_in = jnp.exp(segsum(adt))
    cb = jnp.einsum('bclgn,bcsgn->bcgls', cq, bq)
    y_diag = jnp.einsum('bcgls,bcgrls,bcsgrp->bclgrp', cb, decay_in, xq)
    decay_to_end = jnp.exp(a_cum[..., -1:] - a_cum)
    states = jnp.einsum('bcsgn,bcgrs,bcsgrp->bcgrpn', bq, decay_to_end, xq)
    states = jnp.concatenate([jnp.zeros_like(states[:, :1]), states], axis=1)
    chunk_sum = jnp.pad(a_cum[..., -1], ((0, 0), (1, 0), (0, 0), (0, 0)))
    decay_chunk = jnp.exp(segsum(jnp.moveaxis(chunk_sum, 1, -1)))
    carried = jnp.einsum('bgrzc,bcgrpn->bzgrpn', decay_chunk, states)[:, :-1]
    y_off = jnp.einsum('bclgn,bcgrpn,bcgrl->bclgrp', cq, carried, jnp.exp(a_cum))
    return (y_diag + y_off).reshape(bsz, seqlen, SSD_HEADS, SSD_HEAD_DIM)


def ssd_branch(z, xbc, dt_raw, conv_w, conv_b, dt_bias, a_log, d_skip, norm_g):
    bsz, seqlen, _ = z.shape
    xbc = lax.conv_general_dilated(xbc, conv_w[:, None, :], window_strides=(1,),
                                   padding=[(SSD_CONV - 1, 0)],
                                   dimension_numbers=('NWC', 'WIO', 'NWC'),
                                   feature_group_count=SSD_CONV_DIM) + conv_b
    xbc = jax.nn.silu(xbc).astype(jnp.float32)
    xs = xbc[..., :SSD_INNER].reshape(bsz, seqlen, SSD_HEADS, SSD_HEAD_DIM)
    bs = xbc[..., SSD_INNER:SSD_INNER + SSD_GROUPS * SSD_STATE].reshape(bsz, seqlen, SSD_GROUPS, SSD_STATE)
    cs = xbc[..., SSD_INNER + SSD_GROUPS * SSD_STATE:].reshape(bsz, seqlen, SSD_GROUPS, SSD_STATE)
    dt = jax.nn.softplus(dt_raw.astype(jnp.float32) + dt_bias.astype(jnp.float32))
    a = -jnp.exp(a_log.astype(jnp.float32))
    y = ssd_chunked(xs, dt, a, bs, cs) + d_skip.astype(jnp.float32)[:, None] * xs
    y = y.reshape(bsz, seqlen, SSD_INNER) * jax.nn.silu(z.astype(jnp.float32))
    y = rmsnorm(y, norm_g.astype(jnp.float32))
    return y.astype(z.dtype)


def s5_ssd_mixer(h, w_in, w_out, lam_re, lam_im, log_dt, b_re, b_im, c_re, c_im, s5_d, glu_w, glu_b,
                 conv_w, conv_b, dt_bias, a_log, ssd_d, ssd_norm_g):
    proj = h @ w_in
    o1 = S5_WIDTH
    o2 = o1 + SSD_INNER
    o3 = o2 + SSD_CONV_DIM
    u, z, xbc, dt_raw = proj[..., :o1], proj[..., o1:o2], proj[..., o2:o3], proj[..., o3:]
    y_s5 = s5_branch(u, lam_re, lam_im, log_dt, b_re, b_im, c_re, c_im, s5_d, glu_w, glu_b)
    y_ssd = ssd_branch(z, xbc, dt_raw, conv_w, conv_b, dt_bias, a_log, ssd_d, ssd_norm_g)
    return jnp.concatenate([y_s5, y_ssd], axis=-1) @ w_out


def rwkv7_scan(r, w, k, v, a, b):
    bsz, _, nh, n = r.shape

    def step(s, inp):
        r_t, w_t, k_t, v_t, a_t, b_t = inp
        sa = jnp.einsum('bhvk,bhk->bhv', s, a_t)
        s = s * w_t[:, :, None, :] + sa[..., None] * b_t[:, :, None, :] + v_t[..., None] * k_t[:, :, None, :]
        return s, jnp.einsum('bhvk,bhk->bhv', s, r_t)

    xs = tuple(jnp.moveaxis(t, 1, 0) for t in (r, w, k, v, a, b))
    s0 = jnp.zeros((bsz, nh, n, n), jnp.float32)
    _, y = lax.scan(step, s0, xs)
    return jnp.moveaxis(y, 0, 1)


def rwkv7_mixer(h, mu, w_r, w_k, w_v, w_o, w0, w1, w2, a0, a1, a2, g1, g2, k_k, k_a, r_k, ln_g, ln_b):
    bsz, seqlen, _ = h.shape
    hs = (bsz, seqlen, RWKV_HEADS, RWKV_HEAD_DIM)
    xx = jnp.pad(h, ((0, 0), (1, 0), (0, 0)))[:, :-1] - h
    xr, xw, xk, xv, xa, xg = (h + xx * mu[i] for i in range(6))
    r = xr @ w_r
    k = xk @ w_k
    v = xv @ w_v
    w = -jax.nn.softplus(-(w0 + jnp.tanh(xw @ w1) @ w2)) - 0.5
    a = jax.nn.sigmoid(a0 + (xa @ a1) @ a2)
    g = jax.nn.sigmoid(xg @ g1) @ g2
    kk = (k * k_k).astype(jnp.float32).reshape(hs)
    kk = kk / jnp.maximum(jnp.linalg.norm(kk, axis=-1, keepdims=True), 1e-12)
    k = k * (1 + (a - 1) * k_a)
    rf = r.astype(jnp.float32).reshape(hs)
    kf = k.astype(jnp.float32).reshape(hs)
    vf = v.astype(jnp.float32).reshape(hs)
    af = a.astype(jnp.float32).reshape(hs)
    decay = jnp.exp(-jnp.exp(w.astype(jnp.float32))).reshape(hs)
    y = rwkv7_scan(rf, decay, kf, vf, -kk, kk * af)
    mean = jnp.mean(y, axis=-1, keepdims=True)
    var = jnp.mean(jnp.square(y - mean), axis=-1, keepdims=True)
    yn = ((y - mean) * lax.rsqrt(var + RWKV_GN_EPS)).reshape(bsz, seqlen, D_MODEL)
    yn = yn * ln_g.astype(jnp.float32) + ln_b.astype(jnp.float32)
    bonus = jnp.sum(rf * kf * r_k.astype(jnp.float32), axis=-1, keepdims=True) * vf
    y = yn + bonus.reshape(bsz, seqlen, D_MODEL)
    return (y.astype(h.dtype) * g) @ w_o


def setup_inputs(seed: int = 0) -> dict:
    key = jax.random.key(seed)
    keys = iter(jax.random.split(key, 48))
    nrm = lambda shape, s: jax.random.normal(next(keys), shape, jnp.float32) * s
    uni = lambda shape, lo, hi: jax.random.uniform(next(keys), shape, jnp.float32, lo, hi)
    ne = (DEPTH + 1) // 2
    no = DEPTH // 2
    d = D_MODEL
    dt0 = jnp.exp(uni((ne, SSD_HEADS), math.log(1e-3), math.log(1e-1)))
    lam_im0 = jnp.pi * jnp.arange(S5_STATE, dtype=jnp.float32)
    return {
        'x': nrm((BATCH, SEQ, d), 1.0),
        'c': nrm((BATCH, d), 1.0),
        'w_mod': nrm((DEPTH, d, N_SUBLAYERS * 3 * d), d ** -0.5),
        'b_mod': nrm((DEPTH, N_SUBLAYERS * 3 * d), 0.01),
        'norm_g': 1.0 + nrm((DEPTH, N_SUBLAYERS, d), 0.02),
        'ffn_w1': nrm((DEPTH, 2, d, FFN_DIM), d ** -0.5),
        'ffn_w3': nrm((DEPTH, 2, d, FFN_DIM), d ** -0.5),
        'ffn_w2': nrm((DEPTH, 2, FFN_DIM, d), FFN_DIM ** -0.5),
        'hyb_w_in': nrm((ne, d, IN_PROJ_DIM), d ** -0.5),
        'hyb_w_out': nrm((ne, MIX_WIDTH, d), MIX_WIDTH ** -0.5),
        's5_lambda_re': -0.5 + nrm((ne, S5_GROUPS, S5_STATE), 0.01),
        's5_lambda_im': lam_im0 + nrm((ne, S5_GROUPS, S5_STATE), 0.01),
        's5_log_dt': uni((ne, S5_GROUPS), math.log(1e-3), math.log(1e-1)),
        's5_b_re': nrm((ne, S5_GROUPS, S5_STATE, S5_GROUP_CH), (2 * S5_GROUP_CH) ** -0.5),
        's5_b_im': nrm((ne, S5_GROUPS, S5_STATE, S5_GROUP_CH), (2 * S5_GROUP_CH) ** -0.5),
        's5_c_re': nrm((ne, S5_GROUPS, S5_GROUP_CH, S5_STATE), (2 * S5_STATE) ** -0.5),
        's5_c_im': nrm((ne, S5_GROUPS, S5_GROUP_CH, S5_STATE), (2 * S5_STATE) ** -0.5),
        's5_d': nrm((ne, S5_WIDTH), 1.0),
        's5_glu_w': nrm((ne, S5_WIDTH, S5_WIDTH), S5_WIDTH ** -0.5),
        's5_glu_b': nrm((ne, S5_WIDTH), 0.01),
        'ssd_conv_w': nrm((ne, SSD_CONV, SSD_CONV_DIM), SSD_CONV ** -0.5),
        'ssd_conv_b': nrm((ne, SSD_CONV_DIM), 0.01),
        'ssd_dt_bias': dt0 + jnp.log(-jnp.expm1(-dt0)),
        'ssd_a_log': jnp.log(uni((ne, SSD_HEADS), 1.0, 16.0)),
        'ssd_d': 1.0 + nrm((ne, SSD_HEADS), 0.1),
        'ssd_norm_g': 1.0 + nrm((ne, SSD_INNER), 0.02),
        'rwkv_mu': uni((no, 6, d), 0.0, 1.0),
        'rwkv_w_r': nrm((no, d, d), d ** -0.5),
        'rwkv_w_k': nrm((no, d, d), d ** -0.5),
        'rwkv_w_v': nrm((no, d, d), d ** -0.5),
        'rwkv_w_o': nrm((no, d, d), d ** -0.5),
        'rwkv_w0': uni((no, d), -6.0, -1.0),
        'rwkv_w1': nrm((no, d, DECAY_LORA), d ** -0.5),
        'rwkv_w2': nrm((no, DECAY_LORA, d), 0.1 * DECAY_LORA ** -0.5),
        'rwkv_a0': nrm((no, d), 0.1),
        'rwkv_a1': nrm((no, d, AAA_LORA), d ** -0.5),
        'rwkv_a2': nrm((no, AAA_LORA, d), 0.1 * AAA_LORA ** -0.5),
        'rwkv_g1': nrm((no, d, GATE_LORA), d ** -0.5),
        'rwkv_g2': nrm((no, GATE_LORA, d), GATE_LORA ** -0.5),
        'rwkv_k_k': 0.85 + nrm((no, d), 0.02),
        'rwkv_k_a': 1.0 + nrm((no, d), 0.02),
        'rwkv_r_k': nrm((no, RWKV_HEADS, RWKV_HEAD_DIM), 0.1),
        'rwkv_ln_g': 1.0 + nrm((no, d), 0.02),
        'rwkv_ln_b': nrm((no, d), 0.01),
        'final_g': 1.0 + nrm((d,), 0.02),
    }


def reference(x, c, w_mod, b_mod, norm_g, ffn_w1, ffn_w3, ffn_w2, hyb_w_in, hyb_w_out,
              s5_lambda_re, s5_lambda_im, s5_log_dt, s5_b_re, s5_b_im, s5_c_re, s5_c_im, s5_d,
              s5_glu_w, s5_glu_b, ssd_conv_w, ssd_conv_b, ssd_dt_bias, ssd_a_log, ssd_d, ssd_norm_g,
              rwkv_mu, rwkv_w_r, rwkv_w_k, rwkv_w_v, rwkv_w_o, rwkv_w0, rwkv_w1, rwkv_w2,
              rwkv_a0, rwkv_a1, rwkv_a2, rwkv_g1, rwkv_g2, rwkv_k_k, rwkv_k_a, rwkv_r_k,
              rwkv_ln_g, rwkv_ln_b, final_g):
    bsz = x.shape[0]
    c_act = jax.nn.silu(c)
    for layer in range(DEPTH):
        mod = (c_act @ w_mod[layer] + b_mod[layer]).reshape(bsz, N_SUBLAYERS, 3, D_MODEL)
        h = adaln(x, norm_g[layer, 0], mod[:, 0])
        x = x + 0.5 * mod[:, 0, 2][:, None, :] * swiglu(h, ffn_w1[layer, 0], ffn_w3[layer, 0], ffn_w2[layer, 0])
        h = adaln(x, norm_g[layer, 1], mod[:, 1])
        i = layer // 2
        if layer % 2 == 0:
            y = s5_ssd_mixer(h, hyb_w_in[i], hyb_w_out[i], s5_lambda_re[i], s5_lambda_im[i], s5_log_dt[i],
                             s5_b_re[i], s5_b_im[i], s5_c_re[i], s5_c_im[i], s5_d[i], s5_glu_w[i], s5_glu_b[i],
                             ssd_conv_w[i], ssd_conv_b[i], ssd_dt_bias[i], ssd_a_log[i], ssd_d[i], ssd_norm_g[i])
        else:
            y = rwkv7_mixer(h, rwkv_mu[i], rwkv_w_r[i], rwkv_w_k[i], rwkv_w_v[i], rwkv_w_o[i],
                            rwkv_w0[i], rwkv_w1[i], rwkv_w2[i], rwkv_a0[i], rwkv_a1[i], rwkv_a2[i],
                            rwkv_g1[i], rwkv_g2[i], rwkv_k_k[i], rwkv_k_a[i], rwkv_r_k[i],
                            rwkv_ln_g[i], rwkv_ln_b[i])
        x = x + mod[:, 1, 2][:, None, :] * y
        h = adaln(x, norm_g[layer, 2], mod[:, 2])
        x = x + 0.5 * mod[:, 2, 2][:, None, :] * swiglu(h, ffn_w1[layer, 1], ffn_w3[layer, 1], ffn_w2[layer, 1])
    return rmsnorm(x, final_g)
```

```python
import numpy as np
import concourse.bass as bass
import concourse.mybir as mybir
from concourse.bass_utils import run_bass_kernel_spmd

F32 = mybir.dt.float32
BF16 = mybir.dt.bfloat16
AF = mybir.ActivationFunctionType
ALU = mybir.AluOpType

D = 2048
KT = 16
FFN = 5632
NCORES = 8
TOK = 1024
SEQ = 2048
EPS = 1e-6


SKIP_SELF = {"pe"}


class _Eng:
    def __init__(self, name, obj, sem):
        self.name, self.obj, self.sem = name, obj, sem
        self.count = 0
        self.seen = {}


class _Rec:
    def __getattr__(self, name):
        def f(*a, **k):
            self.call = (name, a, k)
            return self
        return f


class Ctx:
    def record(self, fn, *args):
        self.rec = []
        fn(*args)
        lst, self.rec = self.rec, None
        return lst

    def play(self, item):
        engname, (name, a, k), reads, writes = item
        return self.op(engname, lambda e: getattr(e, name)(*a, **k), reads, writes)

    def play_interleaved(self, la, lb):
        i = j = 0
        na, nb = len(la), len(lb)
        while i < na or j < nb:
            if i < na and (j >= nb or i * nb <= j * na):
                self.play(la[i])
                i += 1
            else:
                self.play(lb[j])
                j += 1

    def __init__(self, nc):
        self.nc = nc
        self.engs = {}
        for name, attr in (("pe", "tensor"), ("act", "scalar"), ("dve", "vector"),
                           ("pool", "gpsimd"), ("sp", "sync")):
            self.engs[name] = _Eng(name, getattr(nc, attr), nc.alloc_semaphore("sem_" + name))
        self.res = {}
        self.nslots = 0
        self.uid = 0
        self.stacks = []
        self.free_slots = []
        self.phase = 0
        self.rec = None

    def sb(self, name, shape, dtype=F32):
        self.uid += 1
        if self.stacks:
            return self.stacks[-1][0].enter_context(self.nc.sbuf_tensor(f"{name}_{self.uid}", list(shape), dtype))
        return self.nc.alloc_sbuf_tensor(f"{name}_{self.uid}", list(shape), dtype)

    def open_scope(self):
        import contextlib
        self.stacks.append((contextlib.ExitStack(), []))

    def close_scope(self):
        self.barrier()
        st, slots = self.stacks.pop()
        st.close()
        self.free_slots.extend(slots)

    def ps(self, name):
        self.uid += 1
        return self.nc.alloc_psum_tensor(f"{name}_{self.uid}", [128, 512], F32)

    def slot(self, name):
        if self.free_slots:
            sl = self.free_slots.pop()
        else:
            self.nslots += 1
            sl = {"sem": self.nc.alloc_semaphore(f"dsem_{name}_{self.nslots}"), "count": 0,
                  "key": f"slot{self.nslots}"}
        if self.stacks:
            self.stacks[-1][1].append(sl)
        return sl

    def _deps(self, reads, writes):
        deps = []
        for r in reads:
            st = self.res.get(r)
            if st and st["w"]:
                deps.append(st["w"])
            if st and r.startswith("bank"):
                deps.extend(st["r"].values())
        for w in writes:
            st = self.res.get(w)
            if st:
                if st["w"]:
                    deps.append(st["w"])
                deps.extend(st["r"].values())
        return deps

    def _wait(self, eng, deps, skip_self):
        for sem, val, key in deps:
            if skip_self and key == eng.name:
                continue
            if eng.seen.get(key, 0) < val:
                eng.obj.wait_ge(sem, val)
                eng.seen[key] = val

    def _record(self, tok, reads, writes):
        for r in reads:
            st = self.res.setdefault(r, {"w": None, "r": {}})
            st["r"][tok[2]] = tok
        for w in writes:
            self.res[w] = {"w": tok, "r": {}}

    def op(self, engname, emit, reads=(), writes=()):
        if self.rec is not None:
            r = _Rec()
            emit(r)
            self.rec.append((engname, r.call, tuple(reads), tuple(writes)))
            return None
        eng = self.engs[engname]
        self._wait(eng, self._deps(reads, writes), skip_self=(engname in SKIP_SELF))
        inst = emit(eng.obj)
        eng.count += 1
        inst.then_inc(eng.sem, 1)
        tok = (eng.sem, eng.count, engname)
        eng.seen[engname] = max(eng.seen.get(engname, 0), 0)
        self._record(tok, reads, writes)
        return tok

    def dma(self, qname, slot, pairs, reads=(), writes=(), **kw):
        eng = self.engs[qname]
        self._wait(eng, self._deps(reads, writes), skip_self=False)
        for out, in_ in pairs:
            eng.obj.dma_start(out=out, in_=in_, **kw).then_inc(slot["sem"], 16)
            slot["count"] += 16
        tok = (slot["sem"], slot["count"], slot["key"])
        self._record(tok, reads, writes)
        return tok

    def wait_all(self, engname):
        eng = self.engs[engname]
        deps = []
        for e in self.engs.values():
            if e.count:
                deps.append((e.sem, e.count, e.name))
        for st in self.res.values():
            if st["w"]:
                deps.append(st["w"])
            deps.extend(st["r"].values())
        self._wait(eng, deps, skip_self=False)

    def barrier(self):
        for n in self.engs:
            self.wait_all(n)

    def new_phase(self):
        self.barrier()
        self.phase += 1
        for e in self.engs.values():
            e.sem = self.nc.alloc_semaphore(f"sem_{e.name}_p{self.phase}")
            e.count = 0
            e.seen = {}
        self.res = {}


class WStream:
    NSLOT = 4
    ELEMS = 8192

    def __init__(self, cx, nslot=4, elems=8192):
        self.cx = cx
        self.NSLOT, self.ELEMS = nslot, elems
        self.tiles = [cx.sb(f"wslot{i}", [128, self.ELEMS], BF16) for i in range(self.NSLOT)]
        self.slots = [cx.slot(f"w{i}") for i in range(self.NSLOT)]
        self.plan = []
        self.issued = 0

    def add(self, src_ap, a, b):
        assert a * b <= self.ELEMS
        self.plan.append((src_ap, a, b))
        return len(self.plan) - 1

    def view(self, i):
        _, a, b = self.plan[i]
        t = self.tiles[i % self.NSLOT]
        return t[:, 0:a * b].rearrange("p (a b) -> p a b", a=a)

    def key(self, i):
        return f"wslot{i % self.NSLOT}"

    def _issue(self, i):
        src, a, b = self.plan[i]
        v = self.view(i)
        pairs = [(v[:, :, c0:min(b, c0 + 1024)], src[:, :, c0:min(b, c0 + 1024)]) for c0 in range(0, b, 1024)]
        self.cx.dma("pool", self.slots[i % self.NSLOT], pairs, writes=[self.key(i)])

    def need(self, i):
        upto = min(len(self.plan), i + self.NSLOT)
        while self.issued < upto:
            self._issue(self.issued)
            self.issued += 1


def _mk_env(G):
    if G is not None:
        return G["nc"], G["cx"], G["dram"], G["banks"], G["bankT"]
    nc = bass.Bass("TRN2", target_bir_lowering=False)
    cx = Ctx(nc)
    banks = [cx.ps(f"bank{i}") for i in range(7)]
    cx.uid += 1
    bankT = nc.alloc_psum_tensor(f"bankT_{cx.uid}", [128, 1024], BF16)
    return nc, cx, {}, banks, bankT


def build_T(stages, G=None, io=None):
    nc, cx, dram, banks, bankT = _mk_env(G)
    io = io or {}
    cx.open_scope()

    def din(name, shape, dtype=F32):
        if name in io:
            return io[name]
        if name not in dram:
            dram[name] = nc.dram_tensor(name, list(shape), dtype, kind="ExternalInput").ap()
        return dram[name]

    def dout(name, shape, dtype=F32):
        if name in io:
            return io[name]
        dram[name] = nc.dram_tensor(name, list(shape), dtype, kind="ExternalOutput").ap()
        return dram[name]

    xT_d = din("xT", [128, KT, TOK])
    cT_d = din("cT", [128, KT])

    x = cx.sb("x", [128, KT, TOK], F32)
    sq = [cx.sb(f"sq{i}", [128, TOK], F32) for i in range(2)]
    rstd = cx.sb("rstd", [128, TOK], F32)
    cact = cx.sb("cact", [128, KT], BF16)
    cin = cx.sb("cin", [128, KT], F32)
    ones = cx.sb("ones", [128, 128], F32)
    ws = WStream(cx)
    ld = cx.slot("ld")
    ld2 = cx.slot("ld2")

    cx.op("dve", lambda e: e.memset(ones[:], 1.0), writes=["ones"])
    cx.dma("sp", ld, [(x[:, 0:KT // 2, :], xT_d[:, 0:KT // 2, :]), (x[:, KT // 2:KT, :], xT_d[:, KT // 2:KT, :])],
           writes=["x"])
    cx.dma("sp", ld2, [(cin[:], cT_d)], writes=["cin"])
    cx.op("act", lambda e: e.activation(out=cact[:], in_=cin[:], func=AF.Silu),
          reads=["cin"], writes=["cact"])

    small_id = [0]

    def load_small(name, shape):
        small_id[0] += 1
        t = cx.sb(name, shape, F32)
        sl = cx.slot(name)
        cx.dma("sp", sl, [(t[:], din(name, shape))], writes=[name + str(small_id[0])])
        return t, name + str(small_id[0])

    def compute_mod(l, s, which):
        wmod = din(f"w_mod{l}", [D, 9 * D])
        bmod, bkey = load_small(f"b_modT{l}", [128, 9 * KT])
        out = cx.sb(f"mod{l}{s}", [128, 3, KT], F32)
        okey = f"mod{l}{s}"
        blocks = []
        for j in which:
            for cb in range(D // 512):
                c0 = s * 3 * D + j * D + cb * 512
                src = wmod[:, c0:c0 + 512].rearrange("(kt p) c -> p kt c", p=128)
                blocks.append((j, cb, ws.add(src, KT, 512)))
        bank = banks[6]
        for (j, cb, bid) in blocks:
            ws.need(bid)
            wv = ws.view(bid)
            for ft in range(4):
                for kt in range(KT):
                    cx.op("pe", lambda e, ft=ft, kt=kt, wv=wv: e.matmul(
                        bank[:, ft:ft + 1], wv[:, kt, ft * 128:(ft + 1) * 128], cact[:, kt:kt + 1],
                        start=(kt == 0), stop=(kt == KT - 1)),
                        reads=[ws.key(bid), "cact"], writes=["bank6"])
            jj = s * 3 + j
            col = jj * KT + cb * 4
            cx.op("dve", lambda e, j=j, cb=cb, col=col: e.tensor_tensor(
                out=out[:, j, cb * 4:cb * 4 + 4], in0=bank[:, 0:4], in1=bmod[:, col:col + 4], op=ALU.add),
                reads=["bank6", bkey], writes=[okey])
        return out, okey

    def rms_stats(xkey="x"):
        for kt in range(KT):
            s_ = sq[kt % 2]
            cx.op("act", lambda e, kt=kt, s_=s_: e.activation(out=s_[:], in_=x[:, kt, :], func=AF.Square),
                  reads=[xkey], writes=[f"sq{kt % 2}"])
            for t in range(2):
                cx.op("pe", lambda e, kt=kt, t=t, s_=s_: e.matmul(
                    banks[4 + t][:], ones[:], s_[:, t * 512:(t + 1) * 512],
                    start=(kt == 0), stop=(kt == KT - 1)),
                    reads=[f"sq{kt % 2}", "ones"], writes=[f"bank{4 + t}"])
        for t in range(2):
            cx.op("act", lambda e, t=t: e.activation(out=rstd[:, t * 512:(t + 1) * 512], in_=banks[4 + t][:],
                                                      func=AF.Sqrt, scale=1.0 / D, bias=epsb[:]),
                  reads=[f"bank{4 + t}", "epsb"], writes=["rstd"])
        cx.op("dve", lambda e: e.reciprocal(out=rstd[:], in_=rstd[:]), reads=["rstd"], writes=["rstd"])

    epsb = cx.sb("epsb", [128, 1], F32)
    cx.op("dve", lambda e: e.memset(epsb[:], EPS), writes=["epsb"])

    def adaln(l, s, mod, mkey, dst, dkey):
        ng, ngkey = load_small(f"norm_gT{l}{s}", [128, KT])
        a = cx.sb(f"a{l}{s}", [128, KT], F32)
        akey = f"a{l}{s}"
        cx.op("dve", lambda e: e.scalar_tensor_tensor(out=a[:], in0=mod[:, 1, :], scalar=1.0, in1=ng[:],
                                                      op0=ALU.add, op1=ALU.mult),
              reads=[mkey, ngkey], writes=[akey])
        rms_stats()
        for kt in range(KT):
            s_ = sq[kt % 2]
            cx.op("dve", lambda e, kt=kt, s_=s_: e.scalar_tensor_tensor(
                out=s_[:], in0=x[:, kt, :], scalar=a[:, kt:kt + 1], in1=rstd[:],
                op0=ALU.mult, op1=ALU.mult),
                reads=["x", akey, "rstd"], writes=[f"sq{kt % 2}"])
            cx.op("act", lambda e, kt=kt, s_=s_: e.activation(
                out=dst[:, kt, :], in_=s_[:], func=AF.Identity, bias=mod[:, 0, kt:kt + 1], scale=1.0),
                reads=[f"sq{kt % 2}", mkey], writes=[dkey])

    def ffn(l, s):
        fi = 0 if s == 0 else 1
        w1 = din(f"ffn_w1_{l}{fi}", [D, FFN])
        w3 = din(f"ffn_w3_{l}{fi}", [D, FFN])
        w2 = din(f"ffn_w2_{l}{fi}", [FFN, D])
        cx.open_scope()
        h = cx.sb("h", [128, KT, TOK], BF16)
        g = [cx.sb(f"g{i}", [128, 4, TOK], BF16) for i in range(2)]
        silu_t = [cx.sb(f"silu{i}", [128, 512], F32) for i in range(2)]
        mod, mkey = compute_mod(l, s, (0, 1, 2))
        adaln(l, s, mod, mkey, h, "h")
        hg = cx.sb(f"hg{l}{s}", [128, KT], F32)
        cx.op("dve", lambda e: e.tensor_scalar(out=hg[:], in0=mod[:, 2, :], scalar1=0.5, scalar2=None,
                                               op0=ALU.mult),
              reads=[mkey], writes=[f"hg{l}{s}"])
        NCH = FFN // 512
        blk = []
        for c in range(NCH):
            b1 = ws.add(w1[:, c * 512:(c + 1) * 512].rearrange("(kt p) c -> p kt c", p=128), KT, 512)
            b3 = ws.add(w3[:, c * 512:(c + 1) * 512].rearrange("(kt p) c -> p kt c", p=128), KT, 512)
            b2 = ws.add(w2[c * 512:(c + 1) * 512, :].rearrange("(kt p) c -> p kt c", p=128), 4, D)
            blk.append((b1, b3, b2))
        ev = 0
        for c in range(NCH):
            b1, b3, b2 = blk[c]
            gb = g[c % 2]
            gkey = f"g{c % 2}"
            ws.need(b1)
            w1v, w3v = ws.view(b1), ws.view(b3)
            for m in range(4):
                for t in range(2):
                    pa, pb = banks[t * 2], banks[t * 2 + 1]
                    ka, kb = f"bank{t * 2}", f"bank{t * 2 + 1}"
                    for kt in range(KT):
                        cx.op("pe", lambda e, kt=kt, m=m, t=t, pa=pa: e.matmul(
                            pa[:], w1v[:, kt, m * 128:(m + 1) * 128], h[:, kt, t * 512:(t + 1) * 512],
                            start=(kt == 0), stop=(kt == KT - 1)),
                            reads=[ws.key(b1), "h"], writes=[ka])
                    for kt in range(KT):
                        cx.op("pe", lambda e, kt=kt, m=m, t=t, pb=pb: e.matmul(
                            pb[:], w3v[:, kt, m * 128:(m + 1) * 128], h[:, kt, t * 512:(t + 1) * 512],
                            start=(kt == 0), stop=(kt == KT - 1)),
                            reads=[ws.key(b3), "h"], writes=[kb])
                    st_ = silu_t[ev % 2]
                    skey = f"silu{ev % 2}"
                    ev += 1
                    cx.op("act", lambda e, pa=pa, st_=st_: e.activation(out=st_[:], in_=pa[:], func=AF.Silu),
                          reads=[ka], writes=[skey])
                    cx.op("dve", lambda e, pb=pb, st_=st_, m=m, t=t, gb=gb: e.tensor_tensor(
                        out=gb[:, m, t * 512:(t + 1) * 512], in0=st_[:], in1=pb[:], op=ALU.mult),
                        reads=[skey, kb], writes=[gkey])
            ws.need(b2)
            w2v = ws.view(b2)
            for j in range(KT):
                for t in range(2):
                    bi = 4 + ((j * 2 + t) % 3)
                    po, ko = banks[bi], f"bank{bi}"
                    for m in range(4):
                        cx.op("pe", lambda e, m=m, j=j, t=t, po=po, gb=gb: e.matmul(
                            po[:], w2v[:, m, j * 128:(j + 1) * 128], gb[:, m, t * 512:(t + 1) * 512],
                            start=(m == 0), stop=(m == 3)),
                            reads=[ws.key(b2), gkey], writes=[ko])
                    cx.op("dve", lambda e, j=j, t=t, po=po: e.scalar_tensor_tensor(
                        out=x[:, j, t * 512:(t + 1) * 512], in0=po[:], scalar=hg[:, j:j + 1],
                        in1=x[:, j, t * 512:(t + 1) * 512], op0=ALU.mult, op1=ALU.add),
                        reads=[ko, f"hg{l}{s}"], writes=["x"])
        cx.close_scope()

    def outproj(wname, krows, src, skey, nkt, gate, gkey):
        wd = din(wname, [krows, D])
        blks = [ws.add(wd[:, j * 128:(j + 1) * 128].rearrange("(kt p) c -> p kt c", p=128), nkt, 128) for j in range(KT)]
        for j in range(KT):
            ws.need(blks[j])
            wv = ws.view(blks[j])
            for t in range(2):
                bi = 4 + ((j * 2 + t) % 3)
                po, ko = banks[bi], f"bank{bi}"
                for kt in range(nkt):
                    cx.op("pe", lambda e, kt=kt: e.matmul(po[:], wv[:, kt, :], src[:, kt, t * 512:(t + 1) * 512],
                                                           start=(kt == 0), stop=(kt == nkt - 1)),
                          reads=[ws.key(blks[j]), skey], writes=[ko])
                cx.op("dve", lambda e: e.scalar_tensor_tensor(
                    out=x[:, j, t * 512:(t + 1) * 512], in0=po[:], scalar=gate[:, j:j + 1],
                    in1=x[:, j, t * 512:(t + 1) * 512], op0=ALU.mult, op1=ALU.add),
                    reads=[ko, gkey], writes=["x"])

    def gath_select(gath, ntile, dests, gdt):
        selT, selk = load_small("selT", [128, 2])
        if gdt == F32:
            stA, kA = sq, ["sq0", "sq1"]
        else:
            stA, kA = [cx.sb(f"gsA{i}", [128, TOK], gdt) for i in range(2)], ["gsA0", "gsA1"]
        stB = [cx.sb(f"gsB{i}", [128, TOK], gdt) for i in range(2)]
        sls = [cx.slot(f"gs{i}") for i in range(2)]
        n = 0
        for dst, dkey, tiles in dests:
            for di, (r, lt) in enumerate(tiles):
                b_ = n % 2
                n += 1
                cx.dma("sp", sls[b_], [(stA[b_][:], gath(r, lt, 0)), (stB[b_][:], gath(r, lt, 1))],
                       writes=[kA[b_], f"gsB{b_}"])
                cx.op("dve", lambda e: e.tensor_scalar(out=stB[b_][:], in0=stB[b_][:], scalar1=selT[:, 1:2], scalar2=None, op0=ALU.mult),
                      reads=[f"gsB{b_}", selk], writes=[f"gsB{b_}"])
                cx.op("dve", lambda e: e.scalar_tensor_tensor(out=dst[:, di, :], in0=stA[b_][:], scalar=selT[:, 0:1], in1=stB[b_][:],
                                                               op0=ALU.mult, op1=ALU.add),
                      reads=[kA[b_], f"gsB{b_}", selk], writes=[dkey])

    def mix0_post():
        cx.open_scope()
        mod, mkey = compute_mod(0, 1, (2,))
        y5b = cx.sb("y5b", [128, 8, TOK], BF16)
        ysb = cx.sb("ysb", [128, 16, TOK], BF16)
        sig = cx.sb("sig", [128, 8, 512], BF16)
        sl5, sls = cx.slot("y5in"), cx.slot("ysin")
        if "y_gath" not in io:
            y5_d = din("y5T_in", [128, 8, TOK])
            ys_d = din("ysT_in", [128, 16, TOK])
        if "y_gath" in io:
            gath_select(io["y_gath"], 12, [(y5b, "y5b", [(ft // 4, ft % 4) for ft in range(8)]),
                                           (ysb, "ysb", [(kt // 8, 4 + kt % 8) for kt in range(16)])], F32)
        else:
            cx.dma("pool", sl5, cast_pairs(y5b[:], y5_d), writes=["y5b"])
            cx.dma("pool", sls, cast_pairs(ysb[:], ys_d), writes=["ysb"])
        glub, gbk = load_small("glu_bT", [128, 8])
        sng, sgk = load_small("ssd_norm_gT", [128, 16])
        gw = din("s5_glu_w", [1024, 1024])
        blks = [ws.add(gw[:, j * 128:(j + 1) * 128].rearrange("(kt p) c -> p kt c", p=128), 8, 128) for j in range(8)]
        for t in range(2):
            for j in range(8):
                ws.need(blks[j])
                bi = j % 4
                po, ko = banks[bi], f"bank{bi}"
                wv = ws.view(blks[j])
                for kt in range(8):
                    cx.op("pe", lambda e, kt=kt: e.matmul(po[:], wv[:, kt, :], y5b[:, kt, t * 512:(t + 1) * 512],
                                                           start=(kt == 0), stop=(kt == 7)),
                          reads=[ws.key(blks[j]), "y5b"], writes=[ko])
                cx.op("act", lambda e: e.activation(out=sig[:, j, :], in_=po[:], func=AF.Sigmoid, bias=glub[:, j:j + 1], scale=1.0),
                      reads=[ko, gbk], writes=["sig"])
            if t == 0:
                blks = [ws.add(gw[:, j * 128:(j + 1) * 128].rearrange("(kt p) c -> p kt c", p=128), 8, 128) for j in range(8)]
            for j in range(8):
                cx.op("dve", lambda e: e.tensor_tensor(out=y5b[:, j, t * 512:(t + 1) * 512], in0=y5b[:, j, t * 512:(t + 1) * 512],
                                                        in1=sig[:, j, :], op=ALU.mult), reads=["y5b", "sig"], writes=["y5b"])
        for kt in range(KT):
            s_ = sq[kt % 2]
            cx.op("act", lambda e: e.activation(out=s_[:], in_=ysb[:, kt, :], func=AF.Square), reads=["ysb"], writes=[f"sq{kt % 2}"])
            for t in range(2):
                cx.op("pe", lambda e: e.matmul(banks[4 + t][:], ones[:], s_[:, t * 512:(t + 1) * 512], start=(kt == 0), stop=(kt == KT - 1)),
                      reads=[f"sq{kt % 2}", "ones"], writes=[f"bank{4 + t}"])
        for t in range(2):
            cx.op("act", lambda e: e.activation(out=rstd[:, t * 512:(t + 1) * 512], in_=banks[4 + t][:], func=AF.Sqrt, scale=1.0 / D, bias=epsb[:]),
                  reads=[f"bank{4 + t}", "epsb"], writes=["rstd"])
        cx.op("dve", lambda e: e.reciprocal(out=rstd[:], in_=rstd[:]), reads=["rstd"], writes=["rstd"])
        for kt in range(KT):
            cx.op("dve", lambda e: e.scalar_tensor_tensor(out=ysb[:, kt, :], in0=ysb[:, kt, :], scalar=sng[:, kt:kt + 1], in1=rstd[:],
                                                           op0=ALU.mult, op1=ALU.mult), reads=["ysb", sgk, "rstd"], writes=["ysb"])
        wd = din("hyb_w_out", [3072, D])
        blk5 = [ws.add(wd[0:1024, j * 128:(j + 1) * 128].rearrange("(kt p) c -> p kt c", p=128), 8, 128) for j in range(KT)]
        gate = mod[:, 2, :]
        for j in range(KT):
            blks_ = ws.add(wd[1024:3072, j * 128:(j + 1) * 128].rearrange("(kt p) c -> p kt c", p=128), 16, 128)
            blk5[j] = (blk5[j], blks_)
        for part in range(2):
            for j in range(KT):
                b_ = blk5[j][part]
                ws.need(b_)
                wv = ws.view(b_)
                nk = 8 if part == 0 else 16
                srcb, skey = (y5b, "y5b") if part == 0 else (ysb, "ysb")
                for t in range(2):
                    bi = 4 + ((j * 2 + t) % 3)
                    po, ko = banks[bi], f"bank{bi}"
                    for kt in range(nk):
                        cx.op("pe", lambda e, kt=kt: e.matmul(po[:], wv[:, kt, :], srcb[:, kt, t * 512:(t + 1) * 512],
                                                               start=(kt == 0), stop=(kt == nk - 1)),
                              reads=[ws.key(b_), skey], writes=[ko])
                    cx.op("dve", lambda e: e.scalar_tensor_tensor(
                        out=x[:, j, t * 512:(t + 1) * 512], in0=po[:], scalar=gate[:, j:j + 1],
                        in1=x[:, j, t * 512:(t + 1) * 512], op0=ALU.mult, op1=ALU.add),
                        reads=[ko, mkey], writes=["x"])
        cx.close_scope()

    def rwkv_post():
        cx.open_scope()
        mod, mkey = compute_mod(1, 1, (2,))
        ygb = cx.sb("ygb", [128, 16, TOK], BF16)
        slg = cx.slot("ygin")
        if "yg_gath" in io:
            gath_select(io["yg_gath"], 8, [(ygb, "ygb", [(kt // 8, kt % 8) for kt in range(16)])], BF16)
        else:
            yg_d = din("ygT_in", [128, 16, TOK], BF16)
            cx.dma("sp", slg, [(ygb[:, 4 * i:4 * i + 4, :], yg_d[:, 4 * i:4 * i + 4, :]) for i in range(4)], writes=["ygb"])
        outproj("rwkv_w_o", D, ygb, "ygb", KT, mod[:, 2, :], mkey)
        cx.close_scope()

    st_slot = cx.slot("st")
    for stg in stages:
        kind = stg["kind"]
        if kind == "ffn":
            ffn(stg["l"], stg["s"])
        elif kind == "mix0_post":
            mix0_post()
        elif kind == "rwkv_post":
            rwkv_post()
        elif kind == "h_out":
            l = stg["l"]
            cx.open_scope()
            h = cx.sb("h", [128, KT, TOK], BF16)
            mod, mkey = compute_mod(l, 1, (0, 1))
            adaln(l, 1, mod, mkey, h, "h")
            if "hT_out_pairs" in io:
                cx.dma("sp", st_slot, io["hT_out_pairs"](h), reads=["h"])
            else:
                hout = dout("hT_out", [128, KT, TOK], BF16)
                cx.dma("sp", st_slot, [(hout[:, 0:KT // 2, :], h[:, 0:KT // 2, :]), (hout[:, KT // 2:KT, :], h[:, KT // 2:KT, :])],
                       reads=["h"])
            cx.close_scope()
        elif kind == "x_out":
            xout = dout("xT_out", [128, KT, TOK], F32)
            cx.dma("sp", st_slot, [(xout[:, 0:KT // 2, :], x[:, 0:KT // 2, :]), (xout[:, KT // 2:KT, :], x[:, KT // 2:KT, :])],
                   reads=["x"])
        elif kind == "final":
            fg, fkey = load_small("final_gT", [128, KT])
            rms_stats()
            for kt in range(KT):
                cx.op("dve", lambda e, kt=kt: e.scalar_tensor_tensor(
                    out=x[:, kt, :], in0=x[:, kt, :], scalar=fg[:, kt:kt + 1], in1=rstd[:],
                    op0=ALU.mult, op1=ALU.mult),
                    reads=["x", fkey, "rstd"], writes=["x"])
            xout = dout("xT_out", [128, KT, TOK], F32)
            cx.dma("sp", st_slot, [(xout[:, 0:KT // 2, :], x[:, 0:KT // 2, :]), (xout[:, KT // 2:KT, :], x[:, KT // 2:KT, :])],
                   reads=["x"])
    cx.close_scope()
    cx.wait_all("sp")
    return nc


def cast_pairs(dst, src):
    if len(dst.shape) == 2:
        n = dst.shape[1]
        return [(dst[:, c0:min(n, c0 + 1024)], src[:, c0:min(n, c0 + 1024)]) for c0 in range(0, n, 1024)]
    out = []
    for a in range(dst.shape[1]):
        n = dst.shape[2]
        for c0 in range(0, n, 1024):
            out.append((dst[:, a, c0:min(n, c0 + 1024)], src[:, a, c0:min(n, c0 + 1024)]))
    return out


def fm(v):
    v = np.asarray(v)
    return np.ascontiguousarray(v.reshape(-1, 128).T)


def to_xT(x):
    out = []
    for b in range(4):
        for j in range(2):
            xs = x[b, j * TOK:(j + 1) * TOK, :]
            out.append(np.ascontiguousarray(xs.T.reshape(KT, 128, TOK).transpose(1, 0, 2)))
    return out


def from_xT(tiles):
    x = np.empty((4, SEQ, D), np.float32)
    for b in range(4):
        for j in range(2):
            t = tiles[b * 2 + j]
            x[b, j * TOK:(j + 1) * TOK, :] = t.transpose(1, 0, 2).reshape(D, TOK).T
    return x


S5TC = 256
GELU_C = 0.7978845608028654
TWO_PI = 6.283185307179586


CHK_COUNT = 1
DBG_BANKS = [0, 1]
DBG_NODVE = False


class _Stop(Exception):
    pass


def build_M0(do_s5=True, do_ssd=True, stop=None, G=None, io=None):
    nc, cx, dram, banks, bankT = _mk_env(G)
    io = io or {}
    cx.open_scope()

    cnt = [CHK_COUNT]

    def chk(n):
        if stop == n:
            cnt[0] -= 1
            if cnt[0] <= 0:
                raise _Stop()
    try:
        _build_M0_body(nc, cx, dram, banks, bankT, io, do_s5, do_ssd, chk)
    except _Stop:
        pass
    while cx.stacks and stop is not None:
        cx.close_scope()
    if stop is None:
        cx.close_scope()
    cx.wait_all("sp")
    return nc


def _build_M0_body(nc, cx, dram, banks, bankT, io, do_s5, do_ssd, chk):

    def din(name, shape, dtype=F32):
        if name in io:
            return io[name]
        if name not in dram:
            dram[name] = nc.dram_tensor(name, list(shape), dtype, kind="ExternalInput").ap()
        return dram[name]

    def dout(name, shape, dtype=F32):
        if name in io:
            return io[name]
        dram[name] = nc.dram_tensor(name, list(shape), dtype, kind="ExternalOutput").ap()
        return dram[name]

    BF16S = "bf16_from_f32"

    def load(name, shape, dtype=F32, q="sp"):
        sdt, ddt = (BF16, F32) if dtype == BF16S else (dtype, dtype)
        t = cx.sb(name, shape, sdt)
        sl = cx.slot(name)
        pairs = cast_pairs(t[:], din(name, shape, ddt)) if dtype == BF16S else [(t[:], din(name, shape, ddt))]
        cx.dma(q, sl, pairs, writes=[name])
        return t

    w_d = din("w_in_c", [D, 3600])
    hT = cx.sb("hT", [128, KT, SEQ], BF16)
    sl = cx.slot("hT")
    if "hT_pairs" in io:
        cx.dma("sp", sl, io["hT_pairs"](hT, 0), writes=["hT"])
    else:
        hT_d = din("hT", [128, KT, SEQ], BF16)
        cx.dma("sp", sl, [(hT[:, 4 * i:4 * i + 4, :], hT_d[:, 4 * i:4 * i + 4, :]) for i in range(4)], writes=["hT"])
    ws = WStream(cx, nslot=4, elems=4096)
    ident = load("ident", [128, 128])
    identb = cx.sb("identb", [128, 128], BF16)
    cx.op("dve", lambda e: e.tensor_copy(out=identb[:], in_=ident[:]), reads=["ident"], writes=["identb"])
    st_slot = cx.slot("st")
    st_slot2 = cx.slot("st2")

    def proj(blk, col0, ncol_tiles, evac):
        wv = ws.view(blk)
        n = 0
        for ti in range(ncol_tiles):
            for tb in range(4):
                bk = DBG_BANKS[n % len(DBG_BANKS)]
                n += 1
                for kt in range(KT):
                    cx.op("pe", lambda e, kt=kt, ti=ti, tb=tb, bk=bk: e.matmul(
                        banks[bk][:], wv[:, kt, (col0 + ti) * 128:(col0 + ti + 1) * 128],
                        hT[:, kt, tb * 512:(tb + 1) * 512], start=(kt == 0), stop=(kt == KT - 1)),
                        reads=[ws.key(blk), "hT"], writes=[f"bank{bk}"])
                chk(53)
                evac(ti, tb, banks[bk], f"bank{bk}")
                chk(54)

    if do_s5:
        cx.open_scope()
        lre = load("s5_lre", [128, 16])
        lim = load("s5_lim", [128, 16])
        ldt = load("s5_ldt", [128, 16])
        d5 = load("s5_dT", [128, 4])
        cre = load("s5_cre", [128, 16, 128], BF16S, q="pool")
        cimn = load("s5_cim", [128, 16, 128], BF16S, q="pool")
        cx.op("dve", lambda e: e.tensor_scalar(out=cimn[:], in0=cimn[:], scalar1=-1.0, scalar2=None, op0=ALU.mult),
              reads=["s5_cim"], writes=["s5_cim"])
        bre = cx.sb("breT", [128, 16, 128], BF16)
        bim = cx.sb("bimT", [128, 16, 128], BF16)

        sm = {}

        def S(name):
            sm[name] = cx.sb("s5_" + name, [128, 16], F32)
            return sm[name]

        def tt(o, a, b, op, eng="dve"):
            cx.op(eng, lambda e: e.tensor_tensor(out=sm[o][:], in0=sm[a][:], in1=sm[b][:], op=op),
                  reads=["s5sm"], writes=["s5sm"])

        def ts(o, a, s1, op0, s2=None, op1=None):
            if op1 is None:
                cx.op("dve", lambda e: e.tensor_scalar(out=sm[o][:], in0=sm[a][:], scalar1=s1, scalar2=None, op0=op0),
                      reads=["s5sm"], writes=["s5sm"])
            else:
                cx.op("dve", lambda e: e.tensor_scalar(out=sm[o][:], in0=sm[a][:], scalar1=s1, scalar2=s2, op0=op0, op1=op1),
                      reads=["s5sm"], writes=["s5sm"])

        def act(o, a, func, scale=1.0):
            cx.op("act", lambda e: e.activation(out=sm[o][:], in_=sm[a][:], func=func, scale=scale),
                  reads=["s5sm"], writes=["s5sm"])

        chk(1)
        sm["lre"], sm["lim"], sm["ldt"] = lre, lim, ldt
        for n_ in ("lr", "dt", "mag", "ang", "cs", "sn", "t1", "t2", "t3", "den", "nr", "fre", "fim", "lbr", "lbi", "rden"):
            S(n_)
        cx.wait_all("dve")
        cx.wait_all("act")
        ts("lr", "lre", -1e-4, ALU.min)
        act("dt", "ldt", AF.Exp)
        tt("t1", "lr", "dt", ALU.mult)
        act("mag", "t1", AF.Exp)
        tt("ang", "lim", "dt", ALU.mult)

        def sincos(o, a, shift):
            ki = cx.sb("s5_ki", [128, 16], mybir.dt.int32)
            ts("t1", a, 1.0 / TWO_PI, ALU.mult, shift / TWO_PI, ALU.add)
            cx.op("dve", lambda e: e.tensor_copy(out=ki[:], in_=sm["t1"][:]), reads=["s5sm"], writes=["s5ki"])
            cx.op("dve", lambda e: e.tensor_copy(out=sm["t2"][:], in_=ki[:]), reads=["s5ki"], writes=["s5sm"])
            tt("t1", "t1", "t2", ALU.subtract)
            ts("t2", "t1", 0.5, ALU.is_gt)
            tt("t1", "t1", "t2", ALU.subtract)
            ts("t2", "t1", -0.5, ALU.is_lt)
            tt("t1", "t1", "t2", ALU.add)
            act(o, "t1", AF.Sin, scale=TWO_PI)

        sincos("sn", "ang", 0.0)
        sincos("cs", "ang", TWO_PI / 4)
        tt("lbr", "mag", "cs", ALU.mult)
        tt("lbi", "mag", "sn", ALU.mult)
        tt("t1", "lr", "lr", ALU.mult)
        tt("t2", "lim", "lim", ALU.mult)
        tt("den", "t1", "t2", ALU.add)
        cx.op("dve", lambda e: e.reciprocal(out=sm["rden"][:], in_=sm["den"][:]), reads=["s5sm"], writes=["s5sm"])
        ts("nr", "lbr", -1.0, ALU.add)
        tt("t1", "nr", "lr", ALU.mult)
        tt("t2", "lbi", "lim", ALU.mult)
        tt("t1", "t1", "t2", ALU.add)
        tt("fre", "t1", "rden", ALU.mult)
        tt("t1", "lbi", "lr", ALU.mult)
        tt("t2", "nr", "lim", ALU.mult)
        tt("t1", "t1", "t2", ALU.subtract)
        tt("fim", "t1", "rden", ALU.mult)
        S("nfim")
        ts("nfim", "fim", -1.0, ALU.mult)
        S("nsn")
        ts("nsn", "sn", -1.0, ALU.mult)

        chk(2)
        cx.open_scope()
        xbre = load("s5_xbre", [128, 16, 128], q="act")
        xbim = load("s5_xbim", [128, 16, 128], q="act")
        xt = [cx.sb(f"s5xt{i}", [128, 128], F32) for i in range(2)]
        for pr in range(16):
            for part, (A, fa, Bm, fb) in enumerate(((xbre, "fre", xbim, "nfim"), (xbim, "fre", xbre, "fim"))):
                t_ = xt[part]
                cx.op("dve", lambda e, pr=pr, A=A, fa=fa, t_=t_: e.tensor_scalar(
                    out=t_[:], in0=A[:, pr, :], scalar1=sm[fa][:, pr:pr + 1], scalar2=None, op0=ALU.mult),
                    reads=["s5sm", "s5_xbre", "s5_xbim"], writes=[f"s5xt{part}"])
                cx.op("dve", lambda e, pr=pr, Bm=Bm, fb=fb, t_=t_: e.scalar_tensor_tensor(
                    out=t_[:], in0=Bm[:, pr, :], scalar=sm[fb][:, pr:pr + 1], in1=t_[:], op0=ALU.mult, op1=ALU.add),
                    reads=["s5sm", "s5_xbre", "s5_xbim", f"s5xt{part}"], writes=[f"s5xt{part}"])
                cx.op("pe", lambda e, t_=t_, part=part: e.transpose(banks[2 + part][:, 0:128], t_[:], ident[:]),
                      reads=[f"s5xt{part}", "ident"], writes=[f"bank{2 + part}"])
                dst = bre if part == 0 else bim
                cx.op("act", lambda e, dst=dst, pr=pr, part=part: e.activation(
                    out=dst[:, pr, :], in_=banks[2 + part][:, 0:128], func=AF.Copy),
                    reads=[f"bank{2 + part}"], writes=["breT" if part == 0 else "bimT"])

        chk(3)
        cx.close_scope()
        chk(4)
        ctab = cx.sb("ctab", [128, 16, S5TC], F32)
        stab = cx.sb("stab", [128, 16, S5TC], F32)
        rho = cx.sb("rho", [128, 16, S5TC], F32)
        ec = cx.sb("ec", [128, 16], F32)
        es = cx.sb("es", [128, 16], F32)
        et = [cx.sb(f"et{i}", [128, 16], F32) for i in range(3)]
        cx.op("dve", lambda e: e.memset(ctab[:, :, 0:1], 1.0), writes=["tab"])
        cx.op("dve", lambda e: e.memset(stab[:, :, 0:1], 0.0), reads=["tab"], writes=["tab"])
        cx.op("dve", lambda e: e.tensor_copy(out=ec[:], in_=sm["cs"][:]), reads=["s5sm"], writes=["e"])
        cx.op("dve", lambda e: e.tensor_copy(out=es[:], in_=sm["sn"][:]), reads=["s5sm", "e"], writes=["e"])
        L = 1
        while L < S5TC:
            for pr in range(16):
                cx.op("dve", lambda e, pr=pr, L=L: e.tensor_scalar(
                    out=ctab[:, pr, L:2 * L], in0=ctab[:, pr, 0:L], scalar1=ec[:, pr:pr + 1], scalar2=None, op0=ALU.mult),
                    reads=["tab", "e"], writes=["tab"])
                cx.op("dve", lambda e, pr=pr, L=L: e.tensor_scalar(
                    out=stab[:, pr, L:2 * L], in0=ctab[:, pr, 0:L], scalar1=es[:, pr:pr + 1], scalar2=None, op0=ALU.mult),
                    reads=["tab", "e"], writes=["tab"])
            cx.op("dve", lambda e: e.tensor_scalar(out=et[0][:], in0=es[:], scalar1=-1.0, scalar2=None, op0=ALU.mult),
                  reads=["e"], writes=["et"])
            for pr in range(16):
                cx.op("dve", lambda e, pr=pr, L=L: e.scalar_tensor_tensor(
                    out=ctab[:, pr, L:2 * L], in0=stab[:, pr, 0:L], scalar=et[0][:, pr:pr + 1], in1=ctab[:, pr, L:2 * L],
                    op0=ALU.mult, op1=ALU.add), reads=["tab", "et"], writes=["tab"])
                cx.op("dve", lambda e, pr=pr, L=L: e.scalar_tensor_tensor(
                    out=stab[:, pr, L:2 * L], in0=stab[:, pr, 0:L], scalar=ec[:, pr:pr + 1], in1=stab[:, pr, L:2 * L],
                    op0=ALU.mult, op1=ALU.add), reads=["tab", "e"], writes=["tab"])
            cx.op("dve", lambda e: e.tensor_tensor(out=et[1][:], in0=ec[:], in1=ec[:], op=ALU.mult), reads=["e"], writes=["et1"])
            cx.op("dve", lambda e: e.tensor_tensor(out=et[2][:], in0=es[:], in1=es[:], op=ALU.mult), reads=["e"], writes=["et2"])
            cx.op("dve", lambda e: e.scalar_tensor_tensor(out=es[:], in0=es[:], scalar=2.0, in1=ec[:], op0=ALU.mult, op1=ALU.mult),
                  reads=["e"], writes=["e"])
            cx.op("dve", lambda e: e.tensor_tensor(out=ec[:], in0=et[1][:], in1=et[2][:], op=ALU.subtract),
                  reads=["et1", "et2", "e"], writes=["e"])
            L *= 2
        for pr in range(16):
            cx.op("act", lambda e, pr=pr: e.activation(out=rho[:, pr, :], in_=ctab[:, pr, :], func=AF.Identity,
                                                        scale=0.0, bias=sm["mag"][:, pr:pr + 1]),
                  reads=["tab", "s5sm"], writes=["rho"])

        chk(5)
        u32 = cx.sb("u32", [128, SEQ], F32)
        ubf = cx.sb("ubf", [128, SEQ], BF16)
        y5 = cx.sb("y5", [128, SEQ], F32)
        carry = [cx.sb(f"carry{i}", [128, 16], F32) for i in range(2)]
        cx.op("dve", lambda e: e.memset(carry[0][:], 0.0), writes=["carry"])
        cx.op("dve", lambda e: e.memset(carry[1][:], 0.0), reads=["carry"], writes=["carry"])
        chk(51)
        tmp = [[cx.sb(f"s5tmp{j}{i}", [128, S5TC], F32) for i in range(8)] for j in range(2)]
        sbf = [[cx.sb(f"s5sbf{i}{j}", [128, S5TC], BF16) for j in range(2)] for i in range(4)]
        gl = [cx.sb(f"s5gl{i}", [128, S5TC], F32) for i in range(4)]
        y5_d = None if "y_tile" in io else dout("y5T", [128, 4, SEQ])
        NCH = SEQ // S5TC
        for o in range(4):
            blk = ws.add(w_d[:, o * 128:(o + 1) * 128].rearrange("(kt p) c -> p kt c", p=128), KT, 128)
            ws.need(blk)
            chk(52)

            def ev_u(ti, tb, ps, key):
                cx.op("act", lambda e: e.activation(out=u32[:, tb * 512:(tb + 1) * 512], in_=ps[:], func=AF.Copy),
                      reads=[key], writes=["u32"])
                if not DBG_NODVE:
                    cx.op("dve", lambda e: e.tensor_copy(out=ubf[:, tb * 512:(tb + 1) * 512], in_=ps[:]),
                          reads=[key], writes=["ubf"])
            proj(blk, 0, 1, ev_u)
            chk(6)
            for ch in range(NCH):
                c0 = ch * S5TC
                if ch == 1:
                    chk(7)
                for pp in range(4):
                    pr = o * 4 + pp
                    kre, kim = f"bank{2 + (pp % 2) * 2}", f"bank{3 + (pp % 2) * 2}"
                    pre, pim = banks[2 + (pp % 2) * 2], banks[3 + (pp % 2) * 2]
                    cx.op("pe", lambda e: e.matmul(pre[:, 0:S5TC], bre[:, pr, :], ubf[:, c0:c0 + S5TC], start=True, stop=True),
                          reads=["breT", "ubf"], writes=[kre])
                    cx.op("pe", lambda e: e.matmul(pim[:, 0:S5TC], bim[:, pr, :], ubf[:, c0:c0 + S5TC], start=True, stop=True),
                          reads=["bimT", "ubf"], writes=[kim])
                    t = tmp[pp % 2]
                    tq = pp % 2
                    ct, stb = ctab[:, pr, :], stab[:, pr, :]
                    cx.op("dve", lambda e: e.tensor_tensor(out=t[0][:], in0=pre[:, 0:S5TC], in1=ct, op=ALU.mult),
                          reads=[kre, "tab"], writes=[f"t{tq}_0"])
                    cx.op("dve", lambda e: e.tensor_tensor(out=t[1][:], in0=pim[:, 0:S5TC], in1=stb, op=ALU.mult),
                          reads=[kim, "tab"], writes=[f"t{tq}_1"])
                    cx.op("dve", lambda e: e.tensor_tensor(out=t[2][:], in0=pim[:, 0:S5TC], in1=ct, op=ALU.mult),
                          reads=[kim, "tab"], writes=[f"t{tq}_2"])
                    cx.op("dve", lambda e: e.tensor_tensor(out=t[3][:], in0=pre[:, 0:S5TC], in1=stb, op=ALU.mult),
                          reads=[kre, "tab"], writes=[f"t{tq}_3"])
                    cx.op("pool", lambda e: e.tensor_tensor(out=t[0][:], in0=t[0][:], in1=t[1][:], op=ALU.add),
                          reads=[f"t{tq}_0", f"t{tq}_1"], writes=[f"t{tq}_0"])
                    cx.op("pool", lambda e: e.tensor_tensor(out=t[2][:], in0=t[2][:], in1=t[3][:], op=ALU.subtract),
                          reads=[f"t{tq}_2", f"t{tq}_3"], writes=[f"t{tq}_2"])
                    cx.op("dve", lambda e: e.tensor_tensor_scan(out=t[4][:], data0=rho[:, pr, :], data1=t[0][:],
                                                                 initial=carry[0][:, pr:pr + 1], op0=ALU.mult, op1=ALU.add),
                          reads=["rho", f"t{tq}_0", "carry"], writes=[f"t{tq}_4"])
                    cx.op("dve", lambda e: e.tensor_tensor_scan(out=t[5][:], data0=rho[:, pr, :], data1=t[2][:],
                                                                 initial=carry[1][:, pr:pr + 1], op0=ALU.mult, op1=ALU.add),
                          reads=["rho", f"t{tq}_2", "carry"], writes=[f"t{tq}_5"])
                    if ch < NCH - 1:
                        cx.op("dve", lambda e: e.tensor_scalar(out=et[1][:, 0:1], in0=t[5][:, S5TC - 1:S5TC], scalar1=es[:, pr:pr + 1],
                                                                scalar2=None, op0=ALU.mult), reads=[f"t{tq}_5", "e"], writes=["et1"])
                        cx.op("dve", lambda e: e.tensor_scalar(out=et[2][:, 0:1], in0=t[4][:, S5TC - 1:S5TC], scalar1=es[:, pr:pr + 1],
                                                                scalar2=None, op0=ALU.mult), reads=[f"t{tq}_4", "e"], writes=["et2"])
                        cx.op("dve", lambda e: e.scalar_tensor_tensor(out=carry[0][:, pr:pr + 1], in0=t[4][:, S5TC - 1:S5TC],
                                                                       scalar=ec[:, pr:pr + 1], in1=et[1][:, 0:1],
                                                                       op0=ALU.mult, op1=ALU.subtract),
                              reads=[f"t{tq}_4", "e", "et1", "carry"], writes=["carry"])
                        cx.op("dve", lambda e: e.scalar_tensor_tensor(out=carry[1][:, pr:pr + 1], in0=t[5][:, S5TC - 1:S5TC],
                                                                       scalar=ec[:, pr:pr + 1], in1=et[2][:, 0:1],
                                                                       op0=ALU.mult, op1=ALU.add),
                              reads=[f"t{tq}_5", "e", "et2", "carry"], writes=["carry"])
                    cx.op("pool", lambda e: e.tensor_tensor(out=t[6][:], in0=t[4][:], in1=ct, op=ALU.mult),
                          reads=[f"t{tq}_4", "tab"], writes=[f"t{tq}_6"])
                    cx.op("pool", lambda e: e.tensor_tensor(out=t[7][:], in0=t[5][:], in1=stb, op=ALU.mult),
                          reads=[f"t{tq}_5", "tab"], writes=[f"t{tq}_7"])
                    cx.op("pool", lambda e: e.tensor_tensor(out=sbf[pp][0][:], in0=t[6][:], in1=t[7][:], op=ALU.subtract),
                          reads=[f"t{tq}_6", f"t{tq}_7"], writes=[f"sbf{pp}0"])
                    cx.op("pool", lambda e: e.tensor_tensor(out=t[6][:], in0=t[4][:], in1=stb, op=ALU.mult),
                          reads=[f"t{tq}_4", "tab", f"t{tq}_6"], writes=[f"t{tq}_6"])
                    cx.op("pool", lambda e: e.tensor_tensor(out=t[7][:], in0=t[5][:], in1=ct, op=ALU.mult),
                          reads=[f"t{tq}_5", "tab", f"t{tq}_7"], writes=[f"t{tq}_7"])
                    cx.op("pool", lambda e: e.tensor_tensor(out=sbf[pp][1][:], in0=t[6][:], in1=t[7][:], op=ALU.add),
                          reads=[f"t{tq}_6", f"t{tq}_7"], writes=[f"sbf{pp}1"])
                py = banks[6]
                for pp in range(4):
                    pr = o * 4 + pp
                    cx.op("pe", lambda e: e.matmul(py[:, 0:S5TC], cre[:, pr, :], sbf[pp][0][:], start=(pp == 0), stop=False),
                          reads=["s5_cre", f"sbf{pp}0"], writes=["bank6"])
                    cx.op("pe", lambda e: e.matmul(py[:, 0:S5TC], cimn[:, pr, :], sbf[pp][1][:], start=False, stop=(pp == 3)),
                          reads=["s5_cim", f"sbf{pp}1"], writes=["bank6"])
                cx.op("dve", lambda e: e.scalar_tensor_tensor(out=gl[0][:], in0=u32[:, c0:c0 + S5TC], scalar=d5[:, o:o + 1],
                                                               in1=py[:, 0:S5TC], op0=ALU.mult, op1=ALU.add),
                      reads=["u32", "s5_dT", "bank6"], writes=["gl0"])
                cx.op("act", lambda e: e.activation(out=gl[1][:], in_=gl[0][:], func=AF.Square), reads=["gl0"], writes=["gl1"])
                cx.op("dve", lambda e: e.tensor_scalar(out=gl[1][:], in0=gl[1][:], scalar1=GELU_C * 0.044715, scalar2=GELU_C,
                                                        op0=ALU.mult, op1=ALU.add), reads=["gl1"], writes=["gl1"])
                cx.op("dve", lambda e: e.tensor_tensor(out=gl[1][:], in0=gl[1][:], in1=gl[0][:], op=ALU.mult),
                      reads=["gl1", "gl0"], writes=["gl1"])
                cx.op("act", lambda e: e.activation(out=gl[2][:], in_=gl[1][:], func=AF.Tanh), reads=["gl1"], writes=["gl2"])
                cx.op("act", lambda e: e.activation(out=gl[3][:], in_=gl[0][:], func=AF.Copy, scale=0.5), reads=["gl0"], writes=["gl3"])
                cx.op("dve", lambda e: e.scalar_tensor_tensor(out=y5[:, c0:c0 + S5TC], in0=gl[2][:], scalar=1.0, in1=gl[3][:],
                                                               op0=ALU.add, op1=ALU.mult),
                      reads=["gl2", "gl3"], writes=["y5"])
            cx.dma("sp", st_slot if o % 2 == 0 else st_slot2,
                   [(io["y_tile"](o) if "y_tile" in io else y5_d[:, o, :], y5[:])], reads=["y5"])
        cx.close_scope()

    if do_ssd:
        cx.open_scope()
        tri = load("tri", [128, 128])
        ones = load("ones128", [128, 128])
        maskneg = load("maskneg", [128, 128])
        cw = load("conv_wT", [128, 16, 4])
        cb = load("conv_bT", [128, 16])
        dtb = load("dt_bias_bc", [128, 16])
        alog = load("a_log_bc", [128, 16])
        dsk = load("ssd_dT", [128, 8])
        onec = cx.sb("onec", [128, 1], F32)
        cx.op("dve", lambda e: e.memset(onec[:], 1.0), writes=["onec"])
        abc = cx.sb("abc", [128, 16], F32)
        cx.op("act", lambda e: e.activation(out=abc[:], in_=alog[:], func=AF.Exp), reads=["a_log_bc"], writes=["abc"])
        cx.op("dve", lambda e: e.tensor_scalar(out=abc[:], in0=abc[:], scalar1=-1.0, scalar2=None, op0=ALU.mult),
              reads=["abc"], writes=["abc"])
        NC_ = SEQ // 128
        dt_all = cx.sb("dt_all", [128, NC_, 16], F32)
        adt = cx.sb("adt", [128, NC_, 16], F32)
        cum = cx.sb("cum", [128, NC_, 16], F32)
        dte = cx.sb("dte", [128, NC_, 16], F32)
        dectot = cx.sb("dectot", [128, NC_, 16], F32)
        dg = [cx.sb(f"dg{i}", [128, 128], F32) for i in range(2)]
        tsm = [cx.sb(f"tsm{i}", [128, 16], F32) for i in range(2)]
        bdt = ws.add(w_d[:, 3584:3600].rearrange("(kt p) c -> p kt c", p=128), KT, 16)
        ws.need(bdt)
        wdt = ws.view(bdt)
        b2 = banks[2]
        for c in range(NC_):
            for kt in range(KT):
                cx.op("pe", lambda e, kt=kt: e.matmul(b2[:, 0:16], hT[:, kt, c * 128:(c + 1) * 128], wdt[:, kt, :],
                                                       start=(kt == 0), stop=(kt == KT - 1)),
                      reads=[ws.key(bdt), "hT"], writes=["bank2"])
            cx.op("dve", lambda e: e.tensor_tensor(out=tsm[0][:], in0=b2[:, 0:16], in1=dtb[:], op=ALU.add),
                  reads=["bank2", "dt_bias_bc"], writes=["tsm0"])
            cx.op("act", lambda e: e.activation(out=tsm[0][:], in_=tsm[0][:], func=AF.Exp), reads=["tsm0"], writes=["tsm0"])
            cx.op("act", lambda e: e.activation(out=dt_all[:, c, :], in_=tsm[0][:], func=AF.Ln, bias=onec[:], scale=1.0),
                  reads=["tsm0", "onec"], writes=["dt_all"])
            cx.op("dve", lambda e: e.tensor_tensor(out=adt[:, c, :], in0=dt_all[:, c, :], in1=abc[:], op=ALU.mult),
                  reads=["dt_all", "abc"], writes=["adt"])
            cx.op("pe", lambda e: e.matmul(banks[3][:, 0:16], tri[:], adt[:, c, :], start=True, stop=True),
                  reads=["tri", "adt"], writes=["bank3"])
            cx.op("pe", lambda e: e.matmul(banks[4][:, 0:16], ones[:], adt[:, c, :], start=True, stop=True),
                  reads=["ones128", "adt"], writes=["bank4"])
            cx.op("act", lambda e: e.activation(out=cum[:, c, :], in_=banks[3][:, 0:16], func=AF.Copy), reads=["bank3"], writes=["cum"])
            cx.op("act", lambda e: e.activation(out=dectot[:, c, :], in_=banks[4][:, 0:16], func=AF.Exp), reads=["bank4"], writes=["dectot"])
            cx.op("dve", lambda e: e.tensor_tensor(out=tsm[1][:], in0=banks[4][:, 0:16], in1=cum[:, c, :], op=ALU.subtract),
                  reads=["bank4", "cum"], writes=["tsm1"])
            cx.op("act", lambda e: e.activation(out=dte[:, c, :], in_=tsm[1][:], func=AF.Exp), reads=["tsm1"], writes=["dte"])

        raw = cx.sb("raw", [128, 4, 3 + SEQ], F32)
        cx.op("dve", lambda e: e.memset(raw[:, :, 0:3], 0.0), writes=["raw"])
        sz = cx.sb("sz", [128, 2, SEQ], BF16)
        cv = [cx.sb("cv0", [128, SEQ], F32)]
        xs32 = cx.sb("xs32", [128, 2, SEQ], F32)
        xsb = cx.sb("xsb", [128, 2, SEQ], BF16)
        BT = cx.sb("BT", [128, SEQ], BF16)
        CT = cx.sb("CT", [128, SEQ], BF16)
        yout = raw[:, 0:2, 3:3 + SEQ]
        car32 = cx.sb("car32", [128, 4, 64], F32)
        carb = cx.sb("carb", [128, 4, 64], BF16)
        Btok = [cx.sb(f"Btok{i}", [128, 128], BF16) for i in range(2)]
        xq = [cx.sb(f"xq{i}", [128, 256], BF16) for i in range(2)]
        xqd = [cx.sb(f"xqd{i}", [128, 256], BF16) for i in range(2)]
        dm32 = [cx.sb(f"dm32{i}", [128, 128], F32) for i in range(2)]
        dmx = [cx.sb(f"dmx{i}", [128, 128], F32) for i in range(2)]
        MT = [[cx.sb(f"MT{i}{j}", [128, 128], BF16) for j in range(4)] for i in range(2)]
        ebc = [cx.sb(f"ebc{i}", [128, 128], F32) for i in range(2)]
        Cs = [[cx.sb(f"Cs{i}{j}", [128, 128], BF16) for j in range(4)] for i in range(2)]
        ytmp = [cx.sb(f"ytmp{i}", [128, 128], F32) for i in range(2)]
        ys_d = None if "y_tile" in io else dout("ysT", [128, 8, SEQ])
        b3, b4, b5 = banks[3], banks[4], banks[5]
        for gg in range(4):
            base = 512 + gg * 768
            bx = ws.add(w_d[:, base:base + 256].rearrange("(kt p) c -> p kt c", p=128), KT, 256)
            bbc = ws.add(w_d[:, base + 256:base + 512].rearrange("(kt p) c -> p kt c", p=128), KT, 256)
            bz = ws.add(w_d[:, base + 512:base + 768].rearrange("(kt p) c -> p kt c", p=128), KT, 256)

            def ev_raw(off):
                def f(ti, tb, ps, key):
                    cx.op("act", lambda e: e.activation(out=raw[:, off + ti, 3 + tb * 512:3 + (tb + 1) * 512], in_=ps[:], func=AF.Copy),
                          reads=[key], writes=["raw"])
                return f

            def ev_z(ti, tb, ps, key):
                cx.op("act", lambda e: e.activation(out=sz[:, ti, tb * 512:(tb + 1) * 512], in_=ps[:], func=AF.Silu),
                      reads=[key], writes=["sz"])
            ws.need(bx)
            proj(bx, 0, 2, ev_raw(0))
            ws.need(bbc)
            proj(bbc, 0, 2, ev_raw(2))
            ws.need(bz)
            proj(bz, 0, 2, ev_z)
            for ti in range(4):
                tidx = gg * 4 + ti
                cvt = cv[0]
                ck = "cv0"
                cx.op("dve", lambda e: e.tensor_scalar(out=cvt[:], in0=raw[:, ti, 0:SEQ], scalar1=cw[:, tidx, 0:1], scalar2=None,
                                                        op0=ALU.mult), reads=["raw", "conv_wT"], writes=[ck])
                for jj in range(1, 4):
                    cx.op("dve", lambda e, jj=jj: e.scalar_tensor_tensor(out=cvt[:], in0=raw[:, ti, jj:jj + SEQ],
                                                                       scalar=cw[:, tidx, jj:jj + 1], in1=cvt[:],
                                                                       op0=ALU.mult, op1=ALU.add),
                          reads=["raw", "conv_wT", ck], writes=[ck])
                if ti < 2:
                    cx.op("act", lambda e: e.activation(out=xs32[:, ti, :], in_=cvt[:], func=AF.Silu, bias=cb[:, tidx:tidx + 1], scale=1.0),
                          reads=[ck, "conv_bT"], writes=["xs32"])
                    cx.op("pool", lambda e: e.tensor_copy(out=xsb[:, ti, :], in_=xs32[:, ti, :]), reads=["xs32"], writes=["xsb"])
                else:
                    dst, dk = (BT, "BT") if ti == 2 else (CT, "CT")
                    cx.op("act", lambda e: e.activation(out=dst[:], in_=cvt[:], func=AF.Silu, bias=cb[:, tidx:tidx + 1], scale=1.0),
                          reads=[ck, "conv_bT"], writes=[dk])
            cx.op("dve", lambda e: e.memset(car32[:], 0.0), reads=["car32"], writes=["car32"])
            cx.op("dve", lambda e: e.memset(carb[:], 0.0), reads=["carb"], writes=["carb"])
            def partA(c):
                cs_ = slice(c * 128, (c + 1) * 128)
                par = c % 2
                cx.op("pe", lambda e: e.transpose(bankT[:, 0:128], BT[:, cs_], identb[:]), reads=["BT", "identb"], writes=["bankT"])
                cx.op("pe", lambda e: e.transpose(bankT[:, 128:256], xsb[:, 0, cs_], identb[:]), reads=["xsb", "identb"], writes=["bankT"])
                cx.op("pe", lambda e: e.transpose(bankT[:, 256:384], xsb[:, 1, cs_], identb[:]), reads=["xsb", "identb"], writes=["bankT"])
                cx.op("act", lambda e: e.activation(out=Btok[par][:], in_=bankT[:, 0:128], func=AF.Copy),
                      reads=["bankT"], writes=[f"Btok{par}"])
                for hh in range(4):
                    h_ = gg * 4 + hh
                    cx.op("dve", lambda e: e.tensor_scalar(out=xq[par][:, hh * 64:(hh + 1) * 64], in0=bankT[:, 128 + hh * 64:192 + hh * 64],
                                                            scalar1=dt_all[:, c, h_:h_ + 1], scalar2=None, op0=ALU.mult),
                          reads=["bankT", "dt_all"], writes=[f"xq{par}"])
                    cx.op("pool", lambda e: e.tensor_scalar(out=xqd[par][:, hh * 64:(hh + 1) * 64], in0=xq[par][:, hh * 64:(hh + 1) * 64],
                                                             scalar1=dte[:, c, h_:h_ + 1], scalar2=None, op0=ALU.mult),
                          reads=[f"xq{par}", "dte"], writes=[f"xqd{par}"])
                cx.op("pe", lambda e: e.matmul(b3[:, 0:128], BT[:, cs_], CT[:, cs_], start=True, stop=True),
                      reads=["BT", "CT"], writes=["bank3"])
                for hh in range(4):
                    h_ = gg * 4 + hh
                    hp = hh % 2
                    crow = banks[hh % 2][:, 0:128]
                    cx.op("pool", lambda e: e.tensor_scalar(out=dg[hp][:], in0=ident[:], scalar1=cum[:, c, h_:h_ + 1], scalar2=None,
                                                             op0=ALU.mult), reads=["ident", "cum"], writes=[f"dg{hp}"])
                    cx.op("pe", lambda e: e.matmul(crow, ones[:], dg[hp][:], start=True, stop=True),
                          reads=["ones128", f"dg{hp}"], writes=[f"bank{hh % 2}"])
                    cx.op("dve", lambda e: e.scalar_tensor_tensor(out=dm32[hp][:], in0=crow, scalar=cum[:, c, h_:h_ + 1], in1=maskneg[:],
                                                                   op0=ALU.subtract, op1=ALU.add),
                          reads=[f"bank{hh % 2}", "cum", "maskneg"], writes=[f"dm32{hp}"])
                    cx.op("act", lambda e: e.activation(out=dmx[hp][:], in_=dm32[hp][:], func=AF.Exp), reads=[f"dm32{hp}"], writes=[f"dmx{hp}"])
                    cx.op("dve", lambda e: e.tensor_tensor(out=MT[par][hh][:], in0=dmx[hp][:], in1=b3[:, 0:128], op=ALU.mult),
                          reads=[f"dmx{hp}", "bank3"], writes=[f"MT{par}{hh}"])
                    cx.op("act", lambda e: e.activation(out=ebc[hp][:], in_=crow, func=AF.Exp), reads=[f"bank{hh % 2}"], writes=[f"ebc{hp}"])
                    cx.op("pool", lambda e: e.tensor_tensor(out=Cs[par][hh][:], in0=CT[:, cs_], in1=ebc[hp][:], op=ALU.mult),
                          reads=["CT", f"ebc{hp}"], writes=[f"Cs{par}{hh}"])
                cx.op("pe", lambda e: e.matmul(banks[2][:, 0:256], Btok[par][:], xqd[par][:], start=True, stop=True),
                      reads=[f"Btok{par}", f"xqd{par}"], writes=["bank2"])

            def partB(c):
                cs_ = slice(c * 128, (c + 1) * 128)
                par = c % 2
                for hh in range(4):
                    pt, half = hh // 2, hh % 2
                    yo = banks[5 + par][half * 64:(half + 1) * 64, pt * 128:(pt + 1) * 128]
                    cx.op("pe", lambda e: e.matmul(yo, xq[par][:, hh * 64:(hh + 1) * 64], MT[par][hh][:], start=True, stop=False),
                          reads=[f"xq{par}", f"MT{par}{hh}"], writes=[f"bank{5 + par}"])
                    cx.op("pe", lambda e: e.matmul(yo, carb[:, hh, :], Cs[par][hh][:], start=False, stop=True),
                          reads=["carb", f"Cs{par}{hh}"], writes=[f"bank{5 + par}"])
                for hh in range(4):
                    h_ = gg * 4 + hh
                    cx.op("dve", lambda e: e.scalar_tensor_tensor(out=car32[:, hh, :], in0=car32[:, hh, :], scalar=dectot[:, c, h_:h_ + 1],
                                                                   in1=banks[2][:, hh * 64:(hh + 1) * 64], op0=ALU.mult, op1=ALU.add),
                          reads=["car32", "dectot", "bank2"], writes=["car32"])
                cx.op("pool", lambda e: e.tensor_copy(out=carb[:], in_=car32[:]), reads=["car32"], writes=["carb"])
                for pt in range(2):
                    cx.op("dve", lambda e: e.scalar_tensor_tensor(out=ytmp[pt][:], in0=xs32[:, pt, cs_], scalar=dsk[:, gg * 2 + pt:gg * 2 + pt + 1],
                                                                   in1=banks[5 + par][:, pt * 128:(pt + 1) * 128],
                                                                   op0=ALU.mult, op1=ALU.add),
                          reads=["xs32", "ssd_dT", f"bank{5 + par}"], writes=[f"ytmp{pt}"])
                    cx.op("pool", lambda e: e.tensor_tensor(out=yout[:, pt, cs_], in0=ytmp[pt][:], in1=sz[:, pt, cs_], op=ALU.mult),
                          reads=[f"ytmp{pt}", "sz"], writes=["raw"])

            for it_ in cx.record(partA, 0):
                cx.play(it_)
            for c in range(NC_):
                la = cx.record(partB, c)
                lb = cx.record(partA, c + 1) if c + 1 < NC_ else []
                cx.play_interleaved(la, lb)
            cx.dma("sp", st_slot if gg % 2 == 0 else st_slot2,
                   [((io["y_tile"](4 + gg * 2 + pt) if "y_tile" in io else ys_d[:, gg * 2 + pt, :]), yout[:, pt, :])
                    for pt in range(2)], reads=["raw"])
        cx.close_scope()


def prep_M0(inp, b, j, hT_full):
    m = {"hT": hT_full, "ident": np.eye(128, dtype=np.float32)}
    w = inp["hyb_w_in"][0]
    cols = [np.arange(j * 512, (j + 1) * 512)]
    for gg in range(4):
        G = j * 4 + gg
        cols.append(3072 + G * 256 + np.arange(256))
        cols.append(3072 + 2048 + G * 128 + np.arange(128))
        cols.append(3072 + 3072 + G * 128 + np.arange(128))
        cols.append(1024 + G * 256 + np.arange(256))
    cols.append(7168 + 16 * j + np.arange(16))
    cols = np.concatenate(cols)
    m["w_in_c"] = np.ascontiguousarray(w[:, cols])
    g0 = 32 * j
    lre = np.zeros((128, 16), np.float32)
    lim = np.zeros((128, 16), np.float32)
    ldt = np.zeros((128, 16), np.float32)
    xbre = np.zeros((128, 16, 128), np.float32)
    xbim = np.zeros((128, 16, 128), np.float32)
    cre = np.zeros((128, 16, 128), np.float32)
    cim = np.zeros((128, 16, 128), np.float32)
    for pr in range(16):
        pp = pr % 4
        for gi in range(2):
            g = g0 + 2 * pr + gi
            rows = slice(gi * 64, gi * 64 + 64)
            cs = slice(32 * pp + 16 * gi, 32 * pp + 16 * gi + 16)
            lre[rows, pr] = inp["s5_lambda_re"][0, g]
            lim[rows, pr] = inp["s5_lambda_im"][0, g]
            ldt[rows, pr] = inp["s5_log_dt"][0, g]
            xbre[rows, pr, cs] = inp["s5_b_re"][0, g]
            xbim[rows, pr, cs] = inp["s5_b_im"][0, g]
            cre[rows, pr, cs] = inp["s5_c_re"][0, g].T
            cim[rows, pr, cs] = inp["s5_c_im"][0, g].T
    m.update(s5_lre=lre, s5_lim=lim, s5_ldt=ldt, s5_xbre=xbre, s5_xbim=xbim, s5_cre=cre, s5_cim=cim)
    m["s5_dT"] = np.ascontiguousarray(inp["s5_d"][0, j * 512:(j + 1) * 512].reshape(4, 128).T)
    cwT = np.zeros((128, 16, 4), np.float32)
    cbT = np.zeros((128, 16), np.float32)
    dT = np.zeros((128, 8), np.float32)
    cwf, cbf = inp["ssd_conv_w"][0], inp["ssd_conv_b"][0]
    for gg in range(4):
        G = j * 4 + gg
        chans = [G * 256 + np.arange(128), G * 256 + 128 + np.arange(128),
                 2048 + G * 128 + np.arange(128), 3072 + G * 128 + np.arange(128)]
        for ti in range(4):
            cwT[:, gg * 4 + ti, :] = cwf[:, chans[ti]].T
            cbT[:, gg * 4 + ti] = cbf[chans[ti]]
        for pt in range(2):
            heads = (G * 256 + pt * 128 + np.arange(128)) // 64
            dT[:, gg * 2 + pt] = inp["ssd_d"][0][heads]
    hs = slice(16 * j, 16 * j + 16)
    m.update(conv_wT=cwT, conv_bT=cbT, ssd_dT=dT,
             dt_bias_bc=np.ascontiguousarray(np.broadcast_to(inp["ssd_dt_bias"][0, hs], (128, 16))),
             a_log_bc=np.ascontiguousarray(np.broadcast_to(inp["ssd_a_log"][0, hs], (128, 16))))
    tri = np.triu(np.ones((128, 128), np.float32))
    m["tri"] = tri
    m["ones128"] = np.ones((128, 128), np.float32)
    m["maskneg"] = np.where(np.arange(128)[None, :] >= np.arange(128)[:, None], 0.0, -30000.0).astype(np.float32)
    sel = np.zeros((16, 16, 128), np.float32)
    for h_ in range(16):
        sel[h_, h_, :] = 1.0
    m["sel"] = sel.reshape(16, 16 * 128)
    return m


RC = 64
LD_C = 0.6065306597126334
GN_EPS = 64e-5


def build_M1(G=None, io=None):
    nc, cx, dram, banks, bankT = _mk_env(G)
    io = io or {}
    cx.open_scope()

    def din(name, shape, dtype=F32):
        if name in io:
            return io[name]
        if name not in dram:
            dram[name] = nc.dram_tensor(name, list(shape), dtype, kind="ExternalInput").ap()
        return dram[name]

    def dout(name, shape, dtype=F32):
        if name in io:
            return io[name]
        dram[name] = nc.dram_tensor(name, list(shape), dtype, kind="ExternalOutput").ap()
        return dram[name]

    BF16S = "bf16_from_f32"

    def load(name, shape, dtype=F32, q="sp"):
        sdt, ddt = (BF16, F32) if dtype == BF16S else (dtype, dtype)
        t = cx.sb(name, shape, sdt)
        sl = cx.slot(name)
        pairs = cast_pairs(t[:], din(name, shape, ddt)) if dtype == BF16S else [(t[:], din(name, shape, ddt))]
        cx.dma(q, sl, pairs, writes=[name])
        return t

    hb = cx.sb("hbuf", [128, KT, SEQ + 1], BF16)
    cx.op("dve", lambda e: e.memset(hb[:, :, 0:1], 0.0), writes=["hb"])
    sl = cx.slot("hT")
    if "hT_pairs" in io:
        cx.dma("sp", sl, io["hT_pairs"](hb, 1), reads=["hb"], writes=["hb"])
    else:
        hT_d = din("hT", [128, KT, SEQ], BF16)
        cx.dma("sp", sl, [(hb[:, 4 * i:4 * i + 4, 1:SEQ + 1], hT_d[:, 4 * i:4 * i + 4, :]) for i in range(4)],
               reads=["hb"], writes=["hb"])
    ws = WStream(cx, nslot=2, elems=2048)
    ident = load("ident", [128, 128])
    identb = cx.sb("identb", [128, 128], BF16)
    cx.op("dve", lambda e: e.tensor_copy(out=identb[:], in_=ident[:]), reads=["ident"], writes=["identb"])
    mask3 = load("mask3", [128, 384])
    blockones = load("blockones", [128, 128])
    resetm = load("resetmask", [128, SEQ], BF16S, q="pool")
    muT = load("muT", [128, 6, KT])
    w0T = load("w0T", [128, 8])
    a0T = load("a0T", [128, 8])
    kkT = load("k_kT", [128, 8])
    kaT = load("k_aT", [128, 8])
    rkT = load("r_kT", [128, 8])
    lng = load("lng_stack", [128, 8, 64])
    lnb = load("lnb_stack", [128, 8, 64])
    w2c = load("w2c", [96, 1024], BF16S, q="pool")
    a2c = load("a2c", [96, 1024], BF16S, q="pool")
    g2c = load("g2c", [128, 2, 1024], BF16S, q="pool")
    onesb = cx.sb("onesb", [128, 2], BF16)
    cx.op("dve", lambda e: e.memset(onesb[:], 1.0), writes=["onesb"])
    epsg = cx.sb("epsg", [128, 1], F32)
    cx.op("dve", lambda e: e.memset(epsg[:], GN_EPS), writes=["epsg"])
    st_slots = [cx.slot("st0"), cx.slot("st1")]

    wder = [[cx.sb(f"wd{i}{j}", [128, KT, 128], BF16) for j in range(2)] for i in range(2)]
    nder = [0]

    def derive(blk, mu_i, ncol):
        i = nder[0] % 2
        nder[0] += 1
        wv = ws.view(blk)
        w1_, w2_ = wder[i][0], wder[i][1]
        for kt in range(KT):
            eng = "dve" if kt % 2 == 0 else "pool"
            cx.op(eng, lambda e, kt=kt: e.tensor_scalar(out=w2_[:, kt, 0:ncol], in0=wv[:, kt, :], scalar1=muT[:, mu_i, kt:kt + 1],
                                                        scalar2=None, op0=ALU.mult),
                  reads=[ws.key(blk), "muT"], writes=[f"wd{i}1"])
        cx.op("pool", lambda e: e.tensor_tensor(out=w1_[:, :, 0:ncol], in0=wv[:], in1=w2_[:, :, 0:ncol], op=ALU.subtract),
              reads=[ws.key(blk), f"wd{i}1"], writes=[f"wd{i}0"])
        return w1_, w2_, f"wd{i}0", f"wd{i}1"

    pj = [0]

    def proj2(der, ncol, evac):
        w1_, w2_, k1, k2 = der
        for tb in range(4):
            bk = pj[0] % 2
            pj[0] += 1
            for kt in range(KT):
                cx.op("pe", lambda e, kt=kt: e.matmul(banks[bk][0:ncol, :], w1_[:, kt, 0:ncol], hb[:, kt, 1 + tb * 512:1 + (tb + 1) * 512],
                                                       start=(kt == 0), stop=False),
                      reads=[k1, "hb"], writes=[f"bank{bk}"])
            for kt in range(KT):
                cx.op("pe", lambda e, kt=kt: e.matmul(banks[bk][0:ncol, :], w2_[:, kt, 0:ncol], hb[:, kt, tb * 512:(tb + 1) * 512],
                                                       start=False, stop=(kt == KT - 1)),
                      reads=[k2, "hb"], writes=[f"bank{bk}"])
            evac(tb, banks[bk], f"bank{bk}")

    def wblock(name, shape_cols, c0, ncol):
        src = din(name, [D, shape_cols])
        return ws.add(src[:, c0:c0 + ncol].rearrange("(kt p) c -> p kt c", p=128), KT, ncol)

    tw = cx.sb("tw", [96, SEQ], BF16)
    ta = cx.sb("ta", [96, SEQ], BF16)
    tg = cx.sb("tg", [128, 2, SEQ], BF16)
    b_w1 = wblock("w1", 96, 0, 96)
    b_a1 = wblock("a1", 96, 0, 96)
    b_g1 = [wblock("g1", 256, i * 128, 128) for i in range(2)]
    ws.need(b_w1)
    proj2(derive(b_w1, 1, 96), 96, lambda tb, ps, key: cx.op(
        "act", lambda e: e.activation(out=tw[:, tb * 512:(tb + 1) * 512], in_=ps[0:96, :], func=AF.Tanh), reads=[key], writes=["tw"]))
    ws.need(b_a1)
    proj2(derive(b_a1, 4, 96), 96, lambda tb, ps, key: cx.op(
        "act", lambda e: e.activation(out=ta[:, tb * 512:(tb + 1) * 512], in_=ps[0:96, :], func=AF.Copy), reads=[key], writes=["ta"]))
    for i in range(2):
        ws.need(b_g1[i])
        proj2(derive(b_g1[i], 5, 128), 128, lambda tb, ps, key, i=i: cx.op(
            "act", lambda e: e.activation(out=tg[:, i, tb * 512:(tb + 1) * 512], in_=ps[:], func=AF.Sigmoid), reads=[key], writes=["tg"]))

    r_bf = cx.sb("r_bf", [128, SEQ], BF16)
    k32 = cx.sb("k32", [128, SEQ], F32)
    v_bf = cx.sb("v_bf", [128, SEQ], BF16)
    a32 = cx.sb("a32", [128, SEQ], F32)
    kk32 = cx.sb("kk32", [128, SEQ], F32)
    ld32 = cx.sb("ld32", [128, SEQ], F32)
    cl32 = cx.sb("cl32", [128, SEQ], F32)
    ecl = cx.sb("ecl", [128, SEQ], F32)
    z_bf = cx.sb("z_bf", [128, SEQ], BF16)
    g_bf = cx.sb("g_bf", [128, SEQ], BF16)
    yg = cx.sb("yg", [128, SEQ], BF16)
    sqt = [cx.sb(f"sqt{i}", [128, 512], F32) for i in range(2)]
    Ear = [cx.sb(f"Ear{i}", [128, 256], BF16) for i in range(2)]
    Eb = [cx.sb(f"Eb{i}", [128, 128], BF16) for i in range(2)]
    Ek = [cx.sb(f"Ek{i}", [128, 128], BF16) for i in range(2)]
    Ez = [cx.sb(f"Ez{i}", [128, 128], BF16) for i in range(2)]
    for i in range(2):
        for t_, k_ in ((Ear[i], f"Ear{i}"), (Eb[i], f"Eb{i}"), (Ek[i], f"Ek{i}"), (Ez[i], f"Ez{i}")):
            cx.op("pool", lambda e, t_=t_: e.memset(t_[:], 0.0), writes=[k_])
    EbkT = [cx.sb(f"EbkT{i}", [128, 256], BF16) for i in range(2)]
    Pm = [cx.sb(f"Pm{i}", [128, 128], F32) for i in range(2)]
    PTm = [cx.sb(f"PTm{i}", [128, 128], F32) for i in range(2)]
    Rm = cx.sb("Rm", [128, 128], F32)
    Rb = [cx.sb(f"Rb{i}", [128, 128], BF16) for i in range(2)]
    Arb = [cx.sb(f"Arb{i}", [128, 128], BF16) for i in range(2)]
    Aak_rk = [cx.sb(f"Aakrk{i}", [128, 256], BF16) for i in range(2)]
    Vs = [cx.sb(f"Vs{i}", [128, 64], BF16) for i in range(2)]
    Xb = cx.sb("Xb", [128, 64], BF16)
    Ub = cx.sb("Ub", [128, 64], BF16)
    S32 = cx.sb("S32", [128, 64], F32)
    S0b = cx.sb("S0b", [128, 64], BF16)
    ys = cx.sb("ys", [128, 64], F32)
    ysq = cx.sb("ysq", [128, 64], F32)
    yn = cx.sb("yn", [128, 64], F32)
    yob = cx.sb("yob", [128, 64], BF16)
    stat = cx.sb("stat", [128, 8], F32)
    bon = cx.sb("bon", [128, 2], F32)
    yg_d = None if "yg_tile" in io else dout("ygT", [128, 8, SEQ], BF16)

    for P in range(8):
        c0 = P * 128
        b_r = wblock("wr_c", 1024, c0, 128)
        b_k = wblock("wk_c", 1024, c0, 128)
        b_v = wblock("wv_c", 1024, c0, 128)
        ws.need(b_r)
        proj2(derive(b_r, 0, 128), 128, lambda tb, ps, key: cx.op(
            "act", lambda e: e.activation(out=r_bf[:, tb * 512:(tb + 1) * 512], in_=ps[:], func=AF.Copy), reads=[key], writes=["r_bf"]))
        ws.need(b_k)
        proj2(derive(b_k, 2, 128), 128, lambda tb, ps, key: cx.op(
            "act", lambda e: e.activation(out=k32[:, tb * 512:(tb + 1) * 512], in_=ps[:], func=AF.Copy), reads=[key], writes=["k32"]))
        ws.need(b_v)
        proj2(derive(b_v, 3, 128), 128, lambda tb, ps, key: cx.op(
            "act", lambda e: e.activation(out=v_bf[:, tb * 512:(tb + 1) * 512], in_=ps[:], func=AF.Copy), reads=[key], writes=["v_bf"]))
        for tb in range(4):
            ts_ = slice(tb * 512, (tb + 1) * 512)
            bk = pj[0] % 2
            pj[0] += 1
            cx.op("pe", lambda e: e.matmul(banks[bk][:], w2c[:, c0:c0 + 128], tw[:, ts_], start=True, stop=True),
                  reads=["w2c", "tw"], writes=[f"bank{bk}"])
            cx.op("act", lambda e: e.activation(out=ld32[:, ts_], in_=banks[bk][:], func=AF.Sigmoid, bias=w0T[:, P:P + 1], scale=1.0),
                  reads=[f"bank{bk}", "w0T"], writes=["ld32"])
            bk = pj[0] % 2
            pj[0] += 1
            cx.op("pe", lambda e: e.matmul(banks[bk][:], a2c[:, c0:c0 + 128], ta[:, ts_], start=True, stop=True),
                  reads=["a2c", "ta"], writes=[f"bank{bk}"])
            cx.op("act", lambda e: e.activation(out=a32[:, ts_], in_=banks[bk][:], func=AF.Sigmoid, bias=a0T[:, P:P + 1], scale=1.0),
                  reads=[f"bank{bk}", "a0T"], writes=["a32"])
            bk = pj[0] % 2
            pj[0] += 1
            for i in range(2):
                cx.op("pe", lambda e, i=i: e.matmul(banks[bk][:], g2c[:, i, c0:c0 + 128], tg[:, i, ts_], start=(i == 0), stop=(i == 1)),
                      reads=["g2c", "tg"], writes=[f"bank{bk}"])
            cx.op("act", lambda e: e.activation(out=g_bf[:, ts_], in_=banks[bk][:], func=AF.Copy), reads=[f"bank{bk}"], writes=["g_bf"])
        cx.op("dve", lambda e: e.tensor_scalar(out=ld32[:], in0=ld32[:], scalar1=-LD_C, scalar2=None, op0=ALU.mult),
              reads=["ld32"], writes=["ld32"])
        cx.op("dve", lambda e: e.tensor_scalar(out=kk32[:], in0=k32[:], scalar1=kkT[:, P:P + 1], scalar2=None, op0=ALU.mult),
              reads=["k32", "k_kT"], writes=["kk32"])
        for tb in range(4):
            ts_ = slice(tb * 512, (tb + 1) * 512)
            sq_, sk = sqt[tb % 2], f"sqt{tb % 2}"
            cx.op("act", lambda e: e.activation(out=sq_[:], in_=kk32[:, ts_], func=AF.Square), reads=["kk32"], writes=[sk])
            bk = pj[0] % 2
            pj[0] += 1
            cx.op("pe", lambda e: e.matmul(banks[bk][:], blockones[:], sq_[:], start=True, stop=True),
                  reads=["blockones", sk], writes=[f"bank{bk}"])
            cx.op("act", lambda e: e.activation(out=sq_[:], in_=banks[bk][:], func=AF.Sqrt), reads=[f"bank{bk}"], writes=[sk])
            cx.op("dve", lambda e: e.tensor_scalar(out=sq_[:], in0=sq_[:], scalar1=1e-12, scalar2=None, op0=ALU.max), reads=[sk], writes=[sk])
            cx.op("dve", lambda e: e.reciprocal(out=sq_[:], in_=sq_[:]), reads=[sk], writes=[sk])
            cx.op("dve", lambda e: e.tensor_tensor(out=kk32[:, ts_], in0=kk32[:, ts_], in1=sq_[:], op=ALU.mult),
                  reads=["kk32", sk], writes=["kk32"])
        cx.op("dve", lambda e: e.tensor_scalar(out=ecl[:], in0=a32[:], scalar1=-1.0, scalar2=kaT[:, P:P + 1], op0=ALU.add, op1=ALU.mult),
              reads=["a32", "k_aT"], writes=["ecl"])
        cx.op("dve", lambda e: e.scalar_tensor_tensor(out=k32[:], in0=ecl[:], scalar=1.0, in1=k32[:], op0=ALU.add, op1=ALU.mult),
              reads=["ecl", "k32"], writes=["k32"])
        cx.op("dve", lambda e: e.scalar_tensor_tensor(out=z_bf[:], in0=k32[:], scalar=rkT[:, P:P + 1], in1=r_bf[:], op0=ALU.mult, op1=ALU.mult),
              reads=["k32", "r_kT", "r_bf"], writes=["z_bf"])
        cx.op("pool", lambda e: e.tensor_tensor(out=a32[:], in0=a32[:], in1=kk32[:], op=ALU.mult), reads=["a32", "kk32"], writes=["a32"])
        cx.op("dve", lambda e: e.tensor_tensor_scan(out=cl32[:], data0=resetm[:], data1=ld32[:], initial=0.0, op0=ALU.mult, op1=ALU.add),
              reads=["resetmask", "ld32"], writes=["cl32"])
        cx.op("pool", lambda e: e.tensor_tensor(out=ld32[:], in0=cl32[:], in1=ld32[:], op=ALU.subtract), reads=["cl32", "ld32"], writes=["ld32"])
        cx.op("act", lambda e: e.activation(out=ld32[:], in_=ld32[:], func=AF.Exp), reads=["ld32"], writes=["ld32"])
        cx.op("act", lambda e: e.activation(out=ecl[:], in_=cl32[:], func=AF.Exp), reads=["cl32", "ecl"], writes=["ecl"])
        cx.op("act", lambda e: e.activation(out=cl32[:], in_=cl32[:], func=AF.Exp, scale=-1.0), reads=["cl32"], writes=["cl32"])
        eclm, encl, beta, kfin = ld32, cl32, a32, k32
        cx.op("dve", lambda e: e.memset(S32[:], 0.0), reads=["S32"], writes=["S32"])
        cx.op("dve", lambda e: e.memset(S0b[:], 0.0), reads=["S0b"], writes=["S0b"])
        def part1(c):
            cs = slice(c * RC, (c + 1) * RC)
            par = c % 2
            for hd in range(2):
                R_ = slice(hd * 64, hd * 64 + 64)
                e1, e2 = ("dve", "pool") if hd == 0 else ("pool", "dve")
                cx.op(e1, lambda e: e.scalar_tensor_tensor(out=Ear[par][R_, hd * 64:hd * 64 + 64], in0=kk32[R_, cs], scalar=-1.0, in1=eclm[R_, cs],
                                                           op0=ALU.mult, op1=ALU.mult) if e1 == "dve" else
                      e.tensor_tensor(out=Ear[par][R_, hd * 64:hd * 64 + 64], in0=kk32[R_, cs], in1=eclm[R_, cs], op=ALU.mult),
                      reads=["kk32", "ld32"], writes=[f"Ear{par}"])
                if e1 != "dve":
                    cx.op("pool", lambda e: e.tensor_scalar(out=Ear[par][R_, hd * 64:hd * 64 + 64], in0=Ear[par][R_, hd * 64:hd * 64 + 64],
                                                             scalar1=-1.0, scalar2=None, op0=ALU.mult),
                          reads=[f"Ear{par}"], writes=[f"Ear{par}"])
                cx.op(e2, lambda e: e.tensor_tensor(out=Ear[par][R_, 128 + hd * 64:128 + hd * 64 + 64], in0=r_bf[R_, cs], in1=ecl[R_, cs], op=ALU.mult),
                      reads=["r_bf", "ecl"], writes=[f"Ear{par}"])
                cx.op(e1, lambda e: e.tensor_tensor(out=Eb[par][R_, hd * 64:hd * 64 + 64], in0=beta[R_, cs], in1=encl[R_, cs], op=ALU.mult),
                      reads=["a32", "cl32"], writes=[f"Eb{par}"])
                cx.op(e2, lambda e: e.tensor_tensor(out=Ek[par][R_, hd * 64:hd * 64 + 64], in0=kfin[R_, cs], in1=encl[R_, cs], op=ALU.mult),
                      reads=["k32", "cl32"], writes=[f"Ek{par}"])
                cx.op("act", lambda e: e.activation(out=Ez[par][R_, hd * 64:hd * 64 + 64], in_=z_bf[R_, cs], func=AF.Copy),
                      reads=["z_bf"], writes=[f"Ez{par}"])
                cx.op("pe", lambda e: e.transpose(bankT[R_, 0:64], v_bf[R_, cs], identb[R_, hd * 64:hd * 64 + 64]),
                      reads=["v_bf", "identb"], writes=["bankT"])
            cx.op("act", lambda e: e.activation(out=Vs[par][:], in_=bankT[:, 0:64], func=AF.Copy), reads=["bankT"], writes=[f"Vs{par}"])
            cx.op("pe", lambda e: e.matmul(banks[2][:, 0:256], Eb[par][:], Ear[par][:], start=True, stop=True),
                  reads=[f"Eb{par}", f"Ear{par}"], writes=["bank2"])
            cx.op("pe", lambda e: e.matmul(banks[3][:, 0:256], Ek[par][:], Ear[par][:], start=True, stop=True),
                  reads=[f"Ek{par}", f"Ear{par}"], writes=["bank3"])
            cx.op("pe", lambda e: e.matmul(banks[4][:, 0:128], Ear[par][:, 0:128], Eb[par][:], start=True, stop=True),
                  reads=[f"Eb{par}", f"Ear{par}"], writes=["bank4"])
            cx.op("pe", lambda e: e.transpose(bankT[:, 128:256], Eb[par][:], identb[:]), reads=[f"Eb{par}", "identb"], writes=["bankT"])
            cx.op("pe", lambda e: e.transpose(bankT[:, 256:384], Ek[par][:], identb[:]), reads=[f"Ek{par}", "identb"], writes=["bankT"])
            cx.op("dve", lambda e: e.tensor_tensor(out=Pm[0][:], in0=banks[2][:, 0:128], in1=mask3[:, 0:128], op=ALU.mult),
                  reads=["bank2", "mask3"], writes=["Pm0"])
            cx.op("dve", lambda e: e.tensor_tensor(out=Arb[par][:], in0=banks[2][:, 128:256], in1=mask3[:, 128:256], op=ALU.mult),
                  reads=["bank2", "mask3"], writes=[f"Arb{par}"])
            cx.op("dve", lambda e: e.tensor_tensor(out=Aak_rk[par][:], in0=banks[3][:, 0:256], in1=mask3[:, 0:256], op=ALU.mult),
                  reads=["bank3", "mask3"], writes=[f"Aakrk{par}"])
            cx.op("dve", lambda e: e.tensor_tensor(out=PTm[0][:], in0=banks[4][:, 0:128], in1=mask3[:, 256:384], op=ALU.mult),
                  reads=["bank4", "mask3"], writes=["PTm0"])
            cx.op("act", lambda e: e.activation(out=EbkT[par][:], in_=bankT[:, 128:384], func=AF.Copy), reads=["bankT"], writes=[f"EbkT{par}"])
            cx.op("pool", lambda e: e.tensor_tensor(out=Rm[:], in0=Pm[0][:], in1=ident[:], op=ALU.add), reads=["Pm0", "ident"], writes=["Rm"])
            cur = 0
            for lvl in range(1, 6):
                nxt = 1 - cur
                if lvl < 5:
                    cx.op("pe", lambda e: e.matmul(banks[5][:, 0:128], PTm[cur][:], Pm[cur][:], start=True, stop=True),
                          reads=[f"PTm{cur}", f"Pm{cur}"], writes=["bank5"])
                cx.op("pe", lambda e: e.matmul(banks[6][:, 0:128], Pm[cur][:], PTm[cur][:], start=True, stop=True),
                      reads=[f"PTm{cur}", f"Pm{cur}"], writes=["bank6"])
                if lvl < 5:
                    cx.op("act", lambda e: e.activation(out=Pm[nxt][:], in_=banks[5][:, 0:128], func=AF.Copy),
                          reads=["bank5"], writes=[f"Pm{nxt}"])
                cx.op("dve", lambda e: e.tensor_copy(out=PTm[nxt][:], in_=banks[6][:, 0:128]), reads=["bank6"], writes=[f"PTm{nxt}"])
                cx.op("pe", lambda e: e.matmul(banks[4][:, 0:128], PTm[nxt][:], Rm[:], start=True, stop=True),
                      reads=[f"PTm{nxt}", "Rm"], writes=["bank4"])
                cx.op("dve", lambda e: e.tensor_tensor(out=Rm[:], in0=Rm[:], in1=banks[4][:, 0:128], op=ALU.add),
                      reads=["Rm", "bank4"], writes=["Rm"])
                cur = nxt
            cx.op("act", lambda e: e.activation(out=Rb[par][:], in_=Rm[:], func=AF.Copy), reads=["Rm"], writes=[f"Rb{par}"])

        def part2(c):
            cs = slice(c * RC, (c + 1) * RC)
            par = c % 2
            cx.op("pe", lambda e: e.matmul(banks[0][:, 0:64], Ear[par][:, 0:128], S0b[:], start=True, stop=False),
                  reads=[f"Ear{par}", "S0b"], writes=["bank0"])
            cx.op("pe", lambda e: e.matmul(banks[0][:, 0:64], Aak_rk[par][:, 0:128], Vs[par][:], start=False, stop=True),
                  reads=[f"Aakrk{par}", f"Vs{par}"], writes=["bank0"])
            cx.op("act", lambda e: e.activation(out=Xb[:], in_=banks[0][:, 0:64], func=AF.Copy), reads=["bank0"], writes=["Xb"])
            cx.op("pe", lambda e: e.matmul(banks[1][:, 0:64], Rb[par][:], Xb[:], start=True, stop=True), reads=[f"Rb{par}", "Xb"], writes=["bank1"])
            cx.op("act", lambda e: e.activation(out=Ub[:], in_=banks[1][:, 0:64], func=AF.Copy), reads=["bank1"], writes=["Ub"])
            cx.op("pe", lambda e: e.matmul(banks[0][:, 0:64], Ear[par][:, 128:256], S0b[:], start=True, stop=False),
                  reads=[f"Ear{par}", "S0b"], writes=["bank0"])
            cx.op("pe", lambda e: e.matmul(banks[0][:, 0:64], Arb[par][:], Ub[:], start=False, stop=False), reads=[f"Arb{par}", "Ub"], writes=["bank0"])
            cx.op("pe", lambda e: e.matmul(banks[0][:, 0:64], Aak_rk[par][:, 128:256], Vs[par][:], start=False, stop=True),
                  reads=[f"Aakrk{par}", f"Vs{par}"], writes=["bank0"])
            cx.op("pe", lambda e: e.matmul(banks[0][:, 64:66], Ez[par][:], onesb[:], start=True, stop=True),
                  reads=[f"Ez{par}", "onesb"], writes=["bank0"])
            cx.op("pe", lambda e: e.matmul(banks[1][:, 0:64], EbkT[par][:, 0:128], Ub[:], start=True, stop=False), reads=[f"EbkT{par}", "Ub"], writes=["bank1"])
            cx.op("pe", lambda e: e.matmul(banks[1][:, 0:64], EbkT[par][:, 128:256], Vs[par][:], start=False, stop=True), reads=[f"EbkT{par}", f"Vs{par}"], writes=["bank1"])
            cx.op("dve", lambda e: e.tensor_tensor(out=S32[:], in0=S32[:], in1=banks[1][:, 0:64], op=ALU.add), reads=["S32", "bank1"], writes=["S32"])
            wc = ecl[:, c * RC + RC - 1:c * RC + RC]
            cx.op("dve", lambda e: e.tensor_scalar(out=S32[:], in0=S32[:], scalar1=wc, scalar2=None, op0=ALU.mult),
                  reads=["S32", "ecl"], writes=["S32"])
            cx.op("pool", lambda e: e.tensor_copy(out=S0b[:], in_=S32[:]), reads=["S32"], writes=["S0b"])
            cx.op("act", lambda e: e.activation(out=ys[:], in_=banks[0][:, 0:64], func=AF.Copy, accum_out=stat[:, 0:1]),
                  reads=["bank0"], writes=["ys", "stat"])
            cx.op("act", lambda e: e.activation(out=ysq[:], in_=ys[:], func=AF.Square, accum_out=stat[:, 1:2]),
                  reads=["ys"], writes=["ysq", "stat"])
            cx.op("dve", lambda e: e.tensor_scalar(out=stat[:, 2:3], in0=stat[:, 0:1], scalar1=1.0 / 64, scalar2=None, op0=ALU.mult),
                  reads=["stat"], writes=["stat"])
            cx.op("dve", lambda e: e.tensor_tensor(out=stat[:, 3:4], in0=stat[:, 2:3], in1=stat[:, 2:3], op=ALU.mult),
                  reads=["stat"], writes=["stat"])
            cx.op("dve", lambda e: e.scalar_tensor_tensor(out=stat[:, 4:5], in0=stat[:, 1:2], scalar=1.0 / 64, in1=stat[:, 3:4],
                                                           op0=ALU.mult, op1=ALU.subtract), reads=["stat"], writes=["stat"])
            cx.op("act", lambda e: e.activation(out=stat[:, 5:6], in_=stat[:, 4:5], func=AF.Sqrt, bias=epsg[:], scale=1.0),
                  reads=["stat", "epsg"], writes=["stat"])
            cx.op("dve", lambda e: e.reciprocal(out=stat[:, 5:6], in_=stat[:, 5:6]), reads=["stat"], writes=["stat"])
            cx.op("dve", lambda e: e.tensor_scalar(out=yn[:], in0=ys[:], scalar1=stat[:, 2:3], scalar2=stat[:, 5:6],
                                                    op0=ALU.subtract, op1=ALU.mult), reads=["ys", "stat"], writes=["yn"])
            cx.op("pool", lambda e: e.tensor_tensor(out=yn[:], in0=yn[:], in1=lng[:, P, :], op=ALU.mult), reads=["yn", "lng_stack"], writes=["yn"])
            cx.op("pool", lambda e: e.tensor_tensor(out=yn[:], in0=yn[:], in1=lnb[:, P, :], op=ALU.add), reads=["yn", "lnb_stack"], writes=["yn"])
            cx.op("act", lambda e: e.activation(out=bon[:], in_=banks[0][:, 64:66], func=AF.Copy), reads=["bank0"], writes=["bon"])
            cx.op("dve", lambda e: e.scalar_tensor_tensor(out=yob[:], in0=Vs[par][:], scalar=bon[:, 0:1], in1=yn[:], op0=ALU.mult, op1=ALU.add),
                  reads=[f"Vs{par}", "bon", "yn"], writes=["yob"])
            for hd in range(2):
                R_ = slice(hd * 64, hd * 64 + 64)
                cx.op("pe", lambda e: e.transpose(bankT[R_, 512:576], yob[R_, :], identb[R_, hd * 64:hd * 64 + 64]),
                      reads=["yob", "identb"], writes=["bankT"])
            cx.op("dve", lambda e: e.tensor_tensor(out=yg[:, cs], in0=bankT[:, 512:576], in1=g_bf[:, cs], op=ALU.mult),
                  reads=["bankT", "g_bf"], writes=["yg"])
        NCH_ = SEQ // RC
        for it in cx.record(part1, 0):
            cx.play(it)
        for c in range(NCH_):
            la = cx.record(part2, c)
            lb = cx.record(part1, c + 1) if c + 1 < NCH_ else []
            cx.play_interleaved(la, lb)
        cx.dma("sp", st_slots[P % 2], [(io["yg_tile"](P) if "yg_tile" in io else yg_d[:, P, :], yg[:])], reads=["yg"])
    cx.close_scope()
    cx.wait_all("sp")
    return nc


def prep_M1(inp, b, j, hT_full):
    m = {"hT": hT_full, "ident": np.eye(128, dtype=np.float32)}
    cs = slice(j * 1024, (j + 1) * 1024)
    m["wr_c"] = np.ascontiguousarray(inp["rwkv_w_r"][0][:, cs])
    m["wk_c"] = np.ascontiguousarray(inp["rwkv_w_k"][0][:, cs])
    m["wv_c"] = np.ascontiguousarray(inp["rwkv_w_v"][0][:, cs])
    m["w1"] = inp["rwkv_w1"][0]
    m["a1"] = inp["rwkv_a1"][0]
    m["g1"] = inp["rwkv_g1"][0]
    m["w2c"] = np.ascontiguousarray(inp["rwkv_w2"][0][:, cs])
    m["a2c"] = np.ascontiguousarray(inp["rwkv_a2"][0][:, cs])
    m["g2c"] = np.ascontiguousarray(inp["rwkv_g2"][0][:, cs].reshape(2, 128, 1024).transpose(1, 0, 2))
    m["muT"] = np.ascontiguousarray(inp["rwkv_mu"][0].reshape(6, KT, 128).transpose(2, 0, 1))
    for nm, key in (("w0T", "rwkv_w0"), ("a0T", "rwkv_a0"), ("k_kT", "rwkv_k_k"), ("k_aT", "rwkv_k_a")):
        m[nm] = np.ascontiguousarray(inp[key][0][cs].reshape(8, 128).T)
    m["r_kT"] = np.ascontiguousarray(inp["rwkv_r_k"][0].reshape(-1)[cs].reshape(8, 128).T)
    lg = inp["rwkv_ln_g"][0][cs].reshape(8, 2, 64)
    lb = inp["rwkv_ln_b"][0][cs].reshape(8, 2, 64)
    lng = np.zeros((128, 8, 64), np.float32)
    lnb = np.zeros((128, 8, 64), np.float32)
    for hd in range(2):
        lng[hd * 64:(hd + 1) * 64] = lg[None, :, hd, :]
        lnb[hd * 64:(hd + 1) * 64] = lb[None, :, hd, :]
    m["lng_stack"], m["lnb_stack"] = lng, lnb
    s_ = np.arange(64)
    blk = np.kron(np.eye(2, dtype=np.float32), np.ones((64, 64), np.float32))
    mS = np.kron(np.eye(2, dtype=np.float32), (s_[:, None] < s_[None, :]).astype(np.float32))
    mI = np.kron(np.eye(2, dtype=np.float32), (s_[:, None] <= s_[None, :]).astype(np.float32))
    m["mask3"] = np.ascontiguousarray(np.concatenate([mS, mI, mS.T], axis=1))
    m["blockones"] = blk
    rm = np.ones((128, SEQ), np.float32)
    rm[:, ::RC] = 0.0
    m["resetmask"] = rm
    return m


def _T_maps(inp, stages, xT, extra):
    maps = []
    for core in range(NCORES):
        b = core // 2
        m = {"xT": xT[core], "cT": fm(inp["c"][b])}
        for stg in stages:
            kind = stg["kind"]
            if kind == "ffn":
                l, s_ = stg["l"], stg["s"]
                fi = 0 if s_ == 0 else 1
                m[f"w_mod{l}"] = inp["w_mod"][l]
                m[f"b_modT{l}"] = fm(inp["b_mod"][l])
                m[f"norm_gT{l}{s_}"] = fm(inp["norm_g"][l, s_])
                m[f"ffn_w1_{l}{fi}"] = inp["ffn_w1"][l, fi]
                m[f"ffn_w3_{l}{fi}"] = inp["ffn_w3"][l, fi]
                m[f"ffn_w2_{l}{fi}"] = inp["ffn_w2"][l, fi]
            elif kind == "h_out":
                l = stg["l"]
                m[f"w_mod{l}"] = inp["w_mod"][l]
                m[f"b_modT{l}"] = fm(inp["b_mod"][l])
                m[f"norm_gT{l}1"] = fm(inp["norm_g"][l, 1])
            elif kind == "mix0_post":
                m["w_mod0"] = inp["w_mod"][0]
                m["b_modT0"] = fm(inp["b_mod"][0])
                m["glu_bT"] = fm(inp["s5_glu_b"][0])
                m["ssd_norm_gT"] = fm(inp["ssd_norm_g"][0])
                m["s5_glu_w"] = inp["s5_glu_w"][0]
                m["hyb_w_out"] = inp["hyb_w_out"][0]
            elif kind == "rwkv_post":
                m["w_mod1"] = inp["w_mod"][1]
                m["b_modT1"] = fm(inp["b_mod"][1])
                m["rwkv_w_o"] = inp["rwkv_w_o"][0]
            elif kind == "final":
                m["final_gT"] = fm(inp["final_g"])
        m.update(extra[core])
        maps.append(m)
    return maps


def _run(nc, maps):
    return run_bass_kernel_spmd(nc, maps, core_ids=list(range(NCORES))).results


def _only_declared(maps):
    keep = set(LAST_DRAM.keys())
    return [{k: v for k, v in m.items() if k in keep} for m in maps]


def _pair_cat_tokens(tiles, b):
    return np.ascontiguousarray(np.concatenate([tiles[2 * b], tiles[2 * b + 1]], axis=2))


GROUPS = [[0, 1], [2, 3], [4, 5], [6, 7]]
ST0 = [{"kind": "ffn", "l": 0, "s": 0}, {"kind": "h_out", "l": 0}, {"kind": "x_out"}]
ST1 = [{"kind": "mix0_post"}, {"kind": "ffn", "l": 0, "s": 2}, {"kind": "ffn", "l": 1, "s": 0},
       {"kind": "h_out", "l": 1}, {"kind": "x_out"}]
ST2 = [{"kind": "rwkv_post"}, {"kind": "ffn", "l": 1, "s": 2}, {"kind": "final"}]


def build_fused(upto=None):
    nc = bass.Bass("TRN2", target_bir_lowering=False)
    cx = Ctx(nc)
    banks = [cx.ps(f"bank{i}") for i in range(7)]
    cx.uid += 1
    bankT = nc.alloc_psum_tensor(f"bankT_{cx.uid}", [128, 1024], BF16)
    G = {"nc": nc, "cx": cx, "dram": {}, "banks": banks, "bankT": bankT}

    def idram(name, shape, dt):
        return nc.dram_tensor(name, list(shape), dt).ap()

    ncc = [0]

    def allgather(src, dst):
        cx.barrier()
        sl = cx.slot("cc")
        nc.gpsimd.collective_compute("AllGather", ALU.bypass, replica_groups=GROUPS,
                                     ins=[src.opt()], outs=[dst.opt()]).then_inc(sl["sem"])
        sl["count"] += 1
        ncc[0] += 1
        cx._record((sl["sem"], sl["count"], sl["key"]), [], [f"cc{ncc[0]}"])
        cx.barrier()

    CH = 4096

    def chunks(name, ncol, dt):
        n = ncol // CH
        return ([idram(f"{name}_s{i}", [128, CH], dt) for i in range(n)],
                [idram(f"{name}_g{i}", [256, CH], dt) for i in range(n)])

    def gather_all(snd, rcv):
        for a, b in zip(snd, rcv):
            allgather(a, b)

    def h_out_pairs(snd):
        return lambda h: [(snd[c].rearrange("p (k t) -> p k t", k=4), h[:, 4 * c:4 * c + 4, :]) for c in range(4)]

    def h_loader(rcv):
        def f(tile, off):
            pairs = []
            for r in range(2):
                for c in range(4):
                    src = rcv[c][r * 128:(r + 1) * 128, :].rearrange("p (k t) -> p k t", k=4)
                    pairs.append((tile[:, 4 * c:4 * c + 4, off + r * TOK:off + (r + 1) * TOK], src))
            return pairs
        return f

    def tile_fn(snd):
        return lambda idx: snd[idx // 2][:, (idx % 2) * SEQ:(idx % 2 + 1) * SEQ]

    def gath_fn(rcv):
        return lambda r, lt, half: rcv[lt // 2][r * 128:(r + 1) * 128, (lt % 2) * SEQ + half * TOK:(lt % 2) * SEQ + (half + 1) * TOK]

    xs1 = idram("xs1", [128, KT, TOK], F32)
    xs2 = idram("xs2", [128, KT, TOK], F32)
    h0s, h0g = chunks("h0", KT * TOK, BF16)
    h1s, h1g = chunks("h1", KT * TOK, BF16)
    y0s, y0g = chunks("y0", 12 * SEQ, F32)
    y1s, y1g = chunks("y1", 8 * SEQ, BF16)

    def dbg(n, src, shape, dt):
        if upto != n:
            return False
        o = nc.dram_tensor("dbg", list(shape), dt, kind="ExternalOutput").ap()
        cx.dma("sp", cx.slot("dbg"), [(o, src)])
        cx.wait_all("sp")
        return True

    global LAST_DRAM
    LAST_DRAM = G["dram"]
    build_T(ST0, G, io={"hT_out_pairs": h_out_pairs(h0s), "xT_out": xs1})
    if dbg(1, xs1, [128, KT, TOK], F32):
        return nc
    cx.new_phase()
    gather_all(h0s, h0g)
    if dbg(2, h0g[3], [256, CH], BF16):
        return nc
    build_M0(G=G, io={"hT_pairs": h_loader(h0g), "y_tile": tile_fn(y0s)})
    if dbg(3, y0s[0], [128, CH], F32):
        return nc
    cx.new_phase()
    gather_all(y0s, y0g)
    if dbg(4, y0g[5], [256, CH], F32):
        return nc
    build_T(ST1, G, io={"xT": xs1, "y_gath": gath_fn(y0g), "hT_out_pairs": h_out_pairs(h1s), "xT_out": xs2})
    if dbg(5, xs2, [128, KT, TOK], F32):
        return nc
    cx.new_phase()
    gather_all(h1s, h1g)
    build_M1(G=G, io={"hT_pairs": h_loader(h1g), "yg_tile": tile_fn(y1s)})
    if dbg(6, y1s[0], [128, CH], BF16):
        return nc
    cx.new_phase()
    gather_all(y1s, y1g)
    build_T(ST2, G, io={"xT": xs2, "yg_gath": gath_fn(y1g)})
    cx.wait_all("sp")
    return nc


LAST_DRAM = {}


def kernel_unfused(**inputs):
    inp = {k: np.asarray(v) for k, v in inputs.items()}
    xT = to_xT(inp["x"].astype(np.float32, copy=False))
    st0 = [{"kind": "ffn", "l": 0, "s": 0}, {"kind": "h_out", "l": 0}, {"kind": "x_out"}]
    r = _run(build_T(st0), _T_maps(inp, st0, xT, [{}] * NCORES))
    xT = [np.asarray(q["xT_out"]) for q in r]
    hT = [np.asarray(q["hT_out"]) for q in r]
    maps = [prep_M0(inp, c // 2, c % 2, _pair_cat_tokens(hT, c // 2)) for c in range(NCORES)]
    r = _run(build_M0(), maps)
    y5 = [np.asarray(q["y5T"]) for q in r]
    ys = [np.asarray(q["ysT"]) for q in r]
    extra = []
    for c in range(NCORES):
        b, jt = c // 2, c % 2
        ts = slice(jt * TOK, (jt + 1) * TOK)
        extra.append({"y5T_in": np.ascontiguousarray(np.concatenate([y5[2 * b][:, :, ts], y5[2 * b + 1][:, :, ts]], axis=1)),
                      "ysT_in": np.ascontiguousarray(np.concatenate([ys[2 * b][:, :, ts], ys[2 * b + 1][:, :, ts]], axis=1))})
    st1 = [{"kind": "mix0_post"}, {"kind": "ffn", "l": 0, "s": 2}, {"kind": "ffn", "l": 1, "s": 0},
           {"kind": "h_out", "l": 1}, {"kind": "x_out"}]
    r = _run(build_T(st1), _T_maps(inp, st1, xT, extra))
    xT = [np.asarray(q["xT_out"]) for q in r]
    hT = [np.asarray(q["hT_out"]) for q in r]
    maps = [prep_M1(inp, c // 2, c % 2, _pair_cat_tokens(hT, c // 2)) for c in range(NCORES)]
    r = _run(build_M1(), maps)
    yg = [np.asarray(q["ygT"]) for q in r]
    extra = []
    for c in range(NCORES):
        b, jt = c // 2, c % 2
        ts = slice(jt * TOK, (jt + 1) * TOK)
        extra.append({"ygT_in": np.ascontiguousarray(np.concatenate([yg[2 * b][:, :, ts], yg[2 * b + 1][:, :, ts]], axis=1))})
    st2 = [{"kind": "rwkv_post"}, {"kind": "ffn", "l": 1, "s": 2}, {"kind": "final"}]
    r = _run(build_T(st2), _T_maps(inp, st2, xT, extra))
    out = from_xT([np.asarray(q["xT_out"]) for q in r])
    return out.astype(np.float32)


def fused_maps(inp):
    xT = to_xT(inp["x"].astype(np.float32, copy=False))
    maps = []
    for c in range(NCORES):
        b, j = c // 2, c % 2
        m = {}
        for st in (ST0, ST1, ST2):
            m.update(_T_maps(inp, st, xT, [{}] * NCORES)[c])
        m0 = prep_M0(inp, b, j, None)
        m1 = prep_M1(inp, b, j, None)
        m0.pop("hT")
        m1.pop("hT")
        m.update(m0)
        m.update(m1)
        sel = np.zeros((128, 2), np.float32)
        sel[:, j] = 1.0
        m["selT"] = sel
        maps.append(m)
    return maps


def kernel(**inputs):
    inp = {k: np.asarray(v) for k, v in inputs.items()}
    nc = build_fused()
    r = _run(nc, _only_declared(fused_maps(inp)))
    out = from_xT([np.asarray(q["xT_out"]) for q in r])
    return out.astype(np.float32)
```

```python
import numpy as np
import concourse.bass as bass
import concourse.mybir as mybir
from concourse.bass_utils import run_bass_kernel_spmd

F32 = mybir.dt.float32
BF16 = mybir.dt.bfloat16
AF = mybir.ActivationFunctionType
ALU = mybir.AluOpType

D = 2048
KT = 16
FFN = 5632
NCORES = 8
TOK = 1024
SEQ = 2048
EPS = 1e-6


SKIP_SELF = {"pe"}


class _Eng:
    def __init__(self, name, obj, sem):
        self.name, self.obj, self.sem = name, obj, sem
        self.count = 0
        self.seen = {}


class _Rec:
    def __getattr__(self, name):
        def f(*a, **k):
            self.call = (name, a, k)
            return self
        return f


class Ctx:
    def record(self, fn, *args):
        self.rec = []
        fn(*args)
        lst, self.rec = self.rec, None
        return lst

    def play(self, item):
        engname, (name, a, k), reads, writes = item
        return self.op(engname, lambda e: getattr(e, name)(*a, **k), reads, writes)

    def play_interleaved(self, la, lb):
        i = j = 0
        na, nb = len(la), len(lb)
        while i < na or j < nb:
            if i < na and (j >= nb or i * nb <= j * na):
                self.play(la[i])
                i += 1
            else:
                self.play(lb[j])
                j += 1

    def __init__(self, nc):
        self.nc = nc
        self.engs = {}
        for name, attr in (("pe", "tensor"), ("act", "scalar"), ("dve", "vector"),
                           ("pool", "gpsimd"), ("sp", "sync")):
            self.engs[name] = _Eng(name, getattr(nc, attr), nc.alloc_semaphore("sem_" + name))
        self.res = {}
        self.nslots = 0
        self.uid = 0
        self.stacks = []
        self.free_slots = []
        self.phase = 0
        self.rec = None

    def sb(self, name, shape, dtype=F32):
        self.uid += 1
        if self.stacks:
            return self.stacks[-1][0].enter_context(self.nc.sbuf_tensor(f"{name}_{self.uid}", list(shape), dtype))
        return self.nc.alloc_sbuf_tensor(f"{name}_{self.uid}", list(shape), dtype)

    def open_scope(self):
        import contextlib
        self.stacks.append((contextlib.ExitStack(), []))

    def close_scope(self):
        self.barrier()
        st, slots = self.stacks.pop()
        st.close()
        self.free_slots.extend(slots)

    def ps(self, name):
        self.uid += 1
        return self.nc.alloc_psum_tensor(f"{name}_{self.uid}", [128, 512], F32)

    def slot(self, name):
        if self.free_slots:
            sl = self.free_slots.pop()
        else:
            self.nslots += 1
            sl = {"sem": self.nc.alloc_semaphore(f"dsem_{name}_{self.nslots}"), "count": 0,
                  "key": f"slot{self.nslots}"}
        if self.stacks:
            self.stacks[-1][1].append(sl)
        return sl

    def _deps(self, reads, writes):
        deps = []
        for r in reads:
            st = self.res.get(r)
            if st and st["w"]:
                deps.append(st["w"])
            if st and r.startswith("bank"):
                deps.extend(st["r"].values())
        for w in writes:
            st = self.res.get(w)
            if st:
                if st["w"]:
                    deps.append(st["w"])
                deps.extend(st["r"].values())
        return deps

    def _wait(self, eng, deps, skip_self):
        for sem, val, key in deps:
            if skip_self and key == eng.name:
                continue
            if eng.seen.get(key, 0) < val:
                eng.obj.wait_ge(sem, val)
                eng.seen[key] = val

    def _record(self, tok, reads, writes):
        for r in reads:
            st = self.res.setdefault(r, {"w": None, "r": {}})
            st["r"][tok[2]] = tok
        for w in writes:
            self.res[w] = {"w": tok, "r": {}}

    def op(self, engname, emit, reads=(), writes=()):
        if self.rec is not None:
            r = _Rec()
            emit(r)
            self.rec.append((engname, r.call, tuple(reads), tuple(writes)))
            return None
        eng = self.engs[engname]
        self._wait(eng, self._deps(reads, writes), skip_self=(engname in SKIP_SELF))
        inst = emit(eng.obj)
        eng.count += 1
        inst.then_inc(eng.sem, 1)
        tok = (eng.sem, eng.count, engname)
        eng.seen[engname] = max(eng.seen.get(engname, 0), 0)
        self._record(tok, reads, writes)
        return tok

    def dma(self, qname, slot, pairs, reads=(), writes=(), **kw):
        eng = self.engs[qname]
        self._wait(eng, self._deps(reads, writes), skip_self=False)
        for out, in_ in pairs:
            eng.obj.dma_start(out=out, in_=in_, **kw).then_inc(slot["sem"], 16)
            slot["count"] += 16
        tok = (slot["sem"], slot["count"], slot["key"])
        self._record(tok, reads, writes)
        return tok

    def wait_all(self, engname):
        eng = self.engs[engname]
        deps = []
        for e in self.engs.values():
            if e.count:
                deps.append((e.sem, e.count, e.name))
        for st in self.res.values():
            if st["w"]:
                deps.append(st["w"])
            deps.extend(st["r"].values())
        self._wait(eng, deps, skip_self=False)

    def barrier(self):
        for n in self.engs:
            self.wait_all(n)

    def new_phase(self):
        self.barrier()
        self.phase += 1
        for e in self.engs.values():
            e.sem = self.nc.alloc_semaphore(f"sem_{e.name}_p{self.phase}")
            e.count = 0
            e.seen = {}
        self.res = {}


class WStream:
    NSLOT = 4
    ELEMS = 8192

    def __init__(self, cx, nslot=4, elems=8192):
        self.cx = cx
        self.NSLOT, self.ELEMS = nslot, elems
        self.tiles = [cx.sb(f"wslot{i}", [128, self.ELEMS], BF16) for i in range(self.NSLOT)]
        self.slots = [cx.slot(f"w{i}") for i in range(self.NSLOT)]
        self.plan = []
        self.issued = 0

    def add(self, src_ap, a, b):
        assert a * b <= self.ELEMS
        self.plan.append((src_ap, a, b))
        return len(self.plan) - 1

    def view(self, i):
        _, a, b = self.plan[i]
        t = self.tiles[i % self.NSLOT]
        return t[:, 0:a * b].rearrange("p (a b) -> p a b", a=a)

    def key(self, i):
        return f"wslot{i % self.NSLOT}"

    def _issue(self, i):
        src, a, b = self.plan[i]
        v = self.view(i)
        pairs = [(v[:, :, c0:min(b, c0 + 1024)], src[:, :, c0:min(b, c0 + 1024)]) for c0 in range(0, b, 1024)]
        self.cx.dma("pool", self.slots[i % self.NSLOT], pairs, writes=[self.key(i)])

    def need(self, i):
        upto = min(len(self.plan), i + self.NSLOT)
        while self.issued < upto:
            self._issue(self.issued)
            self.issued += 1


def _mk_env(G):
    if G is not None:
        return G["nc"], G["cx"], G["dram"], G["banks"], G["bankT"]
    nc = bass.Bass("TRN2", target_bir_lowering=False)
    cx = Ctx(nc)
    banks = [cx.ps(f"bank{i}") for i in range(7)]
    cx.uid += 1
    bankT = nc.alloc_psum_tensor(f"bankT_{cx.uid}", [128, 1024], BF16)
    return nc, cx, {}, banks, bankT


def build_T(stages, G=None, io=None):
    nc, cx, dram, banks, bankT = _mk_env(G)
    io = io or {}
    cx.open_scope()

    def din(name, shape, dtype=F32):
        if name in io:
            return io[name]
        if name not in dram:
            dram[name] = nc.dram_tensor(name, list(shape), dtype, kind="ExternalInput").ap()
        return dram[name]

    def dout(name, shape, dtype=F32):
        if name in io:
            return io[name]
        dram[name] = nc.dram_tensor(name, list(shape), dtype, kind="ExternalOutput").ap()
        return dram[name]

    xT_d = din("xT", [128, KT, TOK])
    cT_d = din("cT", [128, KT])

    x = cx.sb("x", [128, KT, TOK], F32)
    sq = [cx.sb(f"sq{i}", [128, TOK], F32) for i in range(2)]
    rstd = cx.sb("rstd", [128, TOK], F32)
    cact = cx.sb("cact", [128, KT], BF16)
    cin = cx.sb("cin", [128, KT], F32)
    ones = cx.sb("ones", [128, 128], F32)
    ws = WStream(cx)
    ld = cx.slot("ld")
    ld2 = cx.slot("ld2")

    cx.op("dve", lambda e: e.memset(ones[:], 1.0), writes=["ones"])
    cx.dma("sp", ld, [(x[:, 0:KT // 2, :], xT_d[:, 0:KT // 2, :]), (x[:, KT // 2:KT, :], xT_d[:, KT // 2:KT, :])],
           writes=["x"])
    cx.dma("sp", ld2, [(cin[:], cT_d)], writes=["cin"])
    cx.op("act", lambda e: e.activation(out=cact[:], in_=cin[:], func=AF.Silu),
          reads=["cin"], writes=["cact"])

    small_id = [0]

    def load_small(name, shape):
        small_id[0] += 1
        t = cx.sb(name, shape, F32)
        sl = cx.slot(name)
        cx.dma("sp", sl, [(t[:], din(name, shape))], writes=[name + str(small_id[0])])
        return t, name + str(small_id[0])

    def compute_mod(l, s, which):
        wmod = din(f"w_mod{l}", [D, 9 * D])
        bmod, bkey = load_small(f"b_modT{l}", [128, 9 * KT])
        out = cx.sb(f"mod{l}{s}", [128, 3, KT], F32)
        okey = f"mod{l}{s}"
        blocks = []
        for j in which:
            for cb in range(D // 512):
                c0 = s * 3 * D + j * D + cb * 512
                src = wmod[:, c0:c0 + 512].rearrange("(kt p) c -> p kt c", p=128)
                blocks.append((j, cb, ws.add(src, KT, 512)))
        bank = banks[6]
        for (j, cb, bid) in blocks:
            ws.need(bid)
            wv = ws.view(bid)
            for ft in range(4):
                for kt in range(KT):
                    cx.op("pe", lambda e, ft=ft, kt=kt, wv=wv: e.matmul(
                        bank[:, ft:ft + 1], wv[:, kt, ft * 128:(ft + 1) * 128], cact[:, kt:kt + 1],
                        start=(kt == 0), stop=(kt == KT - 1)),
                        reads=[ws.key(bid), "cact"], writes=["bank6"])
            jj = s * 3 + j
            col = jj * KT + cb * 4
            cx.op("dve", lambda e, j=j, cb=cb, col=col: e.tensor_tensor(
                out=out[:, j, cb * 4:cb * 4 + 4], in0=bank[:, 0:4], in1=bmod[:, col:col + 4], op=ALU.add),
                reads=["bank6", bkey], writes=[okey])
        return out, okey

    def rms_stats(xkey="x"):
        for kt in range(KT):
            s_ = sq[kt % 2]
            cx.op("act", lambda e, kt=kt, s_=s_: e.activation(out=s_[:], in_=x[:, kt, :], func=AF.Square),
                  reads=[xkey], writes=[f"sq{kt % 2}"])
            for t in range(2):
                cx.op("pe", lambda e, kt=kt, t=t, s_=s_: e.matmul(
                    banks[4 + t][:], ones[:], s_[:, t * 512:(t + 1) * 512],
                    start=(kt == 0), stop=(kt == KT - 1)),
                    reads=[f"sq{kt % 2}", "ones"], writes=[f"bank{4 + t}"])
        for t in range(2):
            cx.op("act", lambda e, t=t: e.activation(out=rstd[:, t * 512:(t + 1) * 512], in_=banks[4 + t][:],
                                                      func=AF.Sqrt, scale=1.0 / D, bias=epsb[:]),
                  reads=[f"bank{4 + t}", "epsb"], writes=["rstd"])
        cx.op("dve", lambda e: e.reciprocal(out=rstd[:], in_=rstd[:]), reads=["rstd"], writes=["rstd"])

    epsb = cx.sb("epsb", [128, 1], F32)
    cx.op("dve", lambda e: e.memset(epsb[:], EPS), writes=["epsb"])

    def adaln(l, s, mod, mkey, dst, dkey):
        ng, ngkey = load_small(f"norm_gT{l}{s}", [128, KT])
        a = cx.sb(f"a{l}{s}", [128, KT], F32)
        akey = f"a{l}{s}"
        cx.op("dve", lambda e: e.scalar_tensor_tensor(out=a[:], in0=mod[:, 1, :], scalar=1.0, in1=ng[:],
                                                      op0=ALU.add, op1=ALU.mult),
              reads=[mkey, ngkey], writes=[akey])
        rms_stats()
        for kt in range(KT):
            s_ = sq[kt % 2]
            cx.op("dve", lambda e, kt=kt, s_=s_: e.scalar_tensor_tensor(
                out=s_[:], in0=x[:, kt, :], scalar=a[:, kt:kt + 1], in1=rstd[:],
                op0=ALU.mult, op1=ALU.mult),
                reads=["x", akey, "rstd"], writes=[f"sq{kt % 2}"])
            cx.op("act", lambda e, kt=kt, s_=s_: e.activation(
                out=dst[:, kt, :], in_=s_[:], func=AF.Identity, bias=mod[:, 0, kt:kt + 1], scale=1.0),
                reads=[f"sq{kt % 2}", mkey], writes=[dkey])

    def ffn(l, s):
        fi = 0 if s == 0 else 1
        w1 = din(f"ffn_w1_{l}{fi}", [D, FFN])
        w3 = din(f"ffn_w3_{l}{fi}", [D, FFN])
        w2 = din(f"ffn_w2_{l}{fi}", [FFN, D])
        cx.open_scope()
        h = cx.sb("h", [128, KT, TOK], BF16)
        g = [cx.sb(f"g{i}", [128, 4, TOK], BF16) for i in range(2)]
        silu_t = [cx.sb(f"silu{i}", [128, 512], F32) for i in range(2)]
        mod, mkey = compute_mod(l, s, (0, 1, 2))
        adaln(l, s, mod, mkey, h, "h")
        hg = cx.sb(f"hg{l}{s}", [128, KT], F32)
        cx.op("dve", lambda e: e.tensor_scalar(out=hg[:], in0=mod[:, 2, :], scalar1=0.5, scalar2=None,
                                               op0=ALU.mult),
              reads=[mkey], writes=[f"hg{l}{s}"])
        NCH = FFN // 512
        blk = []
        for c in range(NCH):
            b1 = ws.add(w1[:, c * 512:(c + 1) * 512].rearrange("(kt p) c -> p kt c", p=128), KT, 512)
            b3 = ws.add(w3[:, c * 512:(c + 1) * 512].rearrange("(kt p) c -> p kt c", p=128), KT, 512)
            b2 = ws.add(w2[c * 512:(c + 1) * 512, :].rearrange("(kt p) c -> p kt c", p=128), 4, D)
            blk.append((b1, b3, b2))
        ev = 0
        for c in range(NCH):
            b1, b3, b2 = blk[c]
            gb = g[c % 2]
            gkey = f"g{c % 2}"
            ws.need(b1)
            w1v, w3v = ws.view(b1), ws.view(b3)
            for m in range(4):
                for t in range(2):
                    pa, pb = banks[t * 2], banks[t * 2 + 1]
                    ka, kb = f"bank{t * 2}", f"bank{t * 2 + 1}"
                    for kt in range(KT):
                        cx.op("pe", lambda e, kt=kt, m=m, t=t, pa=pa: e.matmul(
                            pa[:], w1v[:, kt, m * 128:(m + 1) * 128], h[:, kt, t * 512:(t + 1) * 512],
                            start=(kt == 0), stop=(kt == KT - 1)),
                            reads=[ws.key(b1), "h"], writes=[ka])
                    for kt in range(KT):
                        cx.op("pe", lambda e, kt=kt, m=m, t=t, pb=pb: e.matmul(
                            pb[:], w3v[:, kt, m * 128:(m + 1) * 128], h[:, kt, t * 512:(t + 1) * 512],
                            start=(kt == 0), stop=(kt == KT - 1)),
                            reads=[ws.key(b3), "h"], writes=[kb])
                    st_ = silu_t[ev % 2]
                    skey = f"silu{ev % 2}"
                    ev += 1
                    cx.op("act", lambda e, pa=pa, st_=st_: e.activation(out=st_[:], in_=pa[:], func=AF.Silu),
                          reads=[ka], writes=[skey])
                    cx.op("dve", lambda e, pb=pb, st_=st_, m=m, t=t, gb=gb: e.tensor_tensor(
                        out=gb[:, m, t * 512:(t + 1) * 512], in0=st_[:], in1=pb[:], op=ALU.mult),
                        reads=[skey, kb], writes=[gkey])
            ws.need(b2)
            w2v = ws.view(b2)
            for j in range(KT):
                for t in range(2):
                    bi = 4 + ((j * 2 + t) % 3)
                    po, ko = banks[bi], f"bank{bi}"
                    for m in range(4):
                        cx.op("pe", lambda e, m=m, j=j, t=t, po=po, gb=gb: e.matmul(
                            po[:], w2v[:, m, j * 128:(j + 1) * 128], gb[:, m, t * 512:(t + 1) * 512],
                            start=(m == 0), stop=(m == 3)),
                            reads=[ws.key(b2), gkey], writes=[ko])
                    cx.op("dve", lambda e, j=j, t=t, po=po: e.scalar_tensor_tensor(
                        out=x[:, j, t * 512:(t + 1) * 512], in0=po[:], scalar=hg[:, j:j + 1],
                        in1=x[:, j, t * 512:(t + 1) * 512], op0=ALU.mult, op1=ALU.add),
                        reads=[ko, f"hg{l}{s}"], writes=["x"])
        cx.close_scope()

    def outproj(wname, krows, src, skey, nkt, gate, gkey):
        wd = din(wname, [krows, D])
        blks = [ws.add(wd[:, j * 128:(j + 1) * 128].rearrange("(kt p) c -> p kt c", p=128), nkt, 128) for j in range(KT)]
        for j in range(KT):
            ws.need(blks[j])
            wv = ws.view(blks[j])
            for t in range(2):
                bi = 4 + ((j * 2 + t) % 3)
                po, ko = banks[bi], f"bank{bi}"
                for kt in range(nkt):
                    cx.op("pe", lambda e, kt=kt: e.matmul(po[:], wv[:, kt, :], src[:, kt, t * 512:(t + 1) * 512],
                                                           start=(kt == 0), stop=(kt == nkt - 1)),
                          reads=[ws.key(blks[j]), skey], writes=[ko])
                cx.op("dve", lambda e: e.scalar_tensor_tensor(
                    out=x[:, j, t * 512:(t + 1) * 512], in0=po[:], scalar=gate[:, j:j + 1],
                    in1=x[:, j, t * 512:(t + 1) * 512], op0=ALU.mult, op1=ALU.add),
                    reads=[ko, gkey], writes=["x"])

    def gath_select(gath, ntile, dests, gdt):
        selT, selk = load_small("selT", [128, 2])
        if gdt == F32:
            stA, kA = sq, ["sq0", "sq1"]
        else:
            stA, kA = [cx.sb(f"gsA{i}", [128, TOK], gdt) for i in range(2)], ["gsA0", "gsA1"]
        stB = [cx.sb(f"gsB{i}", [128, TOK], gdt) for i in range(2)]
        sls = [cx.slot(f"gs{i}") for i in range(2)]
        n = 0
        for dst, dkey, tiles in dests:
            for di, (r, lt) in enumerate(tiles):
                b_ = n % 2
                n += 1
                cx.dma("sp", sls[b_], [(stA[b_][:], gath(r, lt, 0)), (stB[b_][:], gath(r, lt, 1))],
                       reads=io.get("dep", []), writes=[kA[b_], f"gsB{b_}"])
                cx.op("dve", lambda e: e.tensor_scalar(out=stB[b_][:], in0=stB[b_][:], scalar1=selT[:, 1:2], scalar2=None, op0=ALU.mult),
                      reads=[f"gsB{b_}", selk], writes=[f"gsB{b_}"])
                cx.op("dve", lambda e: e.scalar_tensor_tensor(out=dst[:, di, :], in0=stA[b_][:], scalar=selT[:, 0:1], in1=stB[b_][:],
                                                               op0=ALU.mult, op1=ALU.add),
                      reads=[kA[b_], f"gsB{b_}", selk], writes=[dkey])

    def mix0_post():
        cx.open_scope()
        mod, mkey = compute_mod(0, 1, (2,))
        y5b = cx.sb("y5b", [128, 8, TOK], BF16)
        ysb = cx.sb("ysb", [128, 16, TOK], BF16)
        sig = cx.sb("sig", [128, 8, 512], BF16)
        sl5, sls = cx.slot("y5in"), cx.slot("ysin")
        if "y_gath" not in io:
            y5_d = din("y5T_in", [128, 8, TOK])
            ys_d = din("ysT_in", [128, 16, TOK])
        if "y_gath" in io:
            gath_select(io["y_gath"], 12, [(y5b, "y5b", [(ft // 4, ft % 4) for ft in range(8)]),
                                           (ysb, "ysb", [(kt // 8, 4 + kt % 8) for kt in range(16)])], F32)
        else:
            cx.dma("pool", sl5, cast_pairs(y5b[:], y5_d), writes=["y5b"])
            cx.dma("pool", sls, cast_pairs(ysb[:], ys_d), writes=["ysb"])
        glub, gbk = load_small("glu_bT", [128, 8])
        sng, sgk = load_small("ssd_norm_gT", [128, 16])
        gw = din("s5_glu_w", [1024, 1024])
        blks = [ws.add(gw[:, j * 128:(j + 1) * 128].rearrange("(kt p) c -> p kt c", p=128), 8, 128) for j in range(8)]
        for t in range(2):
            for j in range(8):
                ws.need(blks[j])
                bi = j % 4
                po, ko = banks[bi], f"bank{bi}"
                wv = ws.view(blks[j])
                for kt in range(8):
                    cx.op("pe", lambda e, kt=kt: e.matmul(po[:], wv[:, kt, :], y5b[:, kt, t * 512:(t + 1) * 512],
                                                           start=(kt == 0), stop=(kt == 7)),
                          reads=[ws.key(blks[j]), "y5b"], writes=[ko])
                cx.op("act", lambda e: e.activation(out=sig[:, j, :], in_=po[:], func=AF.Sigmoid, bias=glub[:, j:j + 1], scale=1.0),
                      reads=[ko, gbk], writes=["sig"])
            if t == 0:
                blks = [ws.add(gw[:, j * 128:(j + 1) * 128].rearrange("(kt p) c -> p kt c", p=128), 8, 128) for j in range(8)]
            for j in range(8):
                cx.op("dve", lambda e: e.tensor_tensor(out=y5b[:, j, t * 512:(t + 1) * 512], in0=y5b[:, j, t * 512:(t + 1) * 512],
                                                        in1=sig[:, j, :], op=ALU.mult), reads=["y5b", "sig"], writes=["y5b"])
        for kt in range(KT):
            s_ = sq[kt % 2]
            cx.op("act", lambda e: e.activation(out=s_[:], in_=ysb[:, kt, :], func=AF.Square), reads=["ysb"], writes=[f"sq{kt % 2}"])
            for t in range(2):
                cx.op("pe", lambda e: e.matmul(banks[4 + t][:], ones[:], s_[:, t * 512:(t + 1) * 512], start=(kt == 0), stop=(kt == KT - 1)),
                      reads=[f"sq{kt % 2}", "ones"], writes=[f"bank{4 + t}"])
        for t in range(2):
            cx.op("act", lambda e: e.activation(out=rstd[:, t * 512:(t + 1) * 512], in_=banks[4 + t][:], func=AF.Sqrt, scale=1.0 / D, bias=epsb[:]),
                  reads=[f"bank{4 + t}", "epsb"], writes=["rstd"])
        cx.op("dve", lambda e: e.reciprocal(out=rstd[:], in_=rstd[:]), reads=["rstd"], writes=["rstd"])
        for kt in range(KT):
            cx.op("dve", lambda e: e.scalar_tensor_tensor(out=ysb[:, kt, :], in0=ysb[:, kt, :], scalar=sng[:, kt:kt + 1], in1=rstd[:],
                                                           op0=ALU.mult, op1=ALU.mult), reads=["ysb", sgk, "rstd"], writes=["ysb"])
        wd = din("hyb_w_out", [3072, D])
        blk5 = [ws.add(wd[0:1024, j * 128:(j + 1) * 128].rearrange("(kt p) c -> p kt c", p=128), 8, 128) for j in range(KT)]
        gate = mod[:, 2, :]
        for j in range(KT):
            blks_ = ws.add(wd[1024:3072, j * 128:(j + 1) * 128].rearrange("(kt p) c -> p kt c", p=128), 16, 128)
            blk5[j] = (blk5[j], blks_)
        for part in range(2):
            for j in range(KT):
                b_ = blk5[j][part]
                ws.need(b_)
                wv = ws.view(b_)
                nk = 8 if part == 0 else 16
                srcb, skey = (y5b, "y5b") if part == 0 else (ysb, "ysb")
                for t in range(2):
                    bi = 4 + ((j * 2 + t) % 3)
                    po, ko = banks[bi], f"bank{bi}"
                    for kt in range(nk):
                        cx.op("pe", lambda e, kt=kt: e.matmul(po[:], wv[:, kt, :], srcb[:, kt, t * 512:(t + 1) * 512],
                                                               start=(kt == 0), stop=(kt == nk - 1)),
                              reads=[ws.key(b_), skey], writes=[ko])
                    cx.op("dve", lambda e: e.scalar_tensor_tensor(
                        out=x[:, j, t * 512:(t + 1) * 512], in0=po[:], scalar=gate[:, j:j + 1],
                        in1=x[:, j, t * 512:(t + 1) * 512], op0=ALU.mult, op1=ALU.add),
                        reads=[ko, mkey], writes=["x"])
        cx.close_scope()

    def rwkv_post():
        cx.open_scope()
        mod, mkey = compute_mod(1, 1, (2,))
        ygb = cx.sb("ygb", [128, 16, TOK], BF16)
        slg = cx.slot("ygin")
        if "yg_gath" in io:
            gath_select(io["yg_gath"], 8, [(ygb, "ygb", [(kt // 8, kt % 8) for kt in range(16)])], BF16)
        else:
            yg_d = din("ygT_in", [128, 16, TOK], BF16)
            cx.dma("sp", slg, [(ygb[:, 4 * i:4 * i + 4, :], yg_d[:, 4 * i:4 * i + 4, :]) for i in range(4)], writes=["ygb"])
        outproj("rwkv_w_o", D, ygb, "ygb", KT, mod[:, 2, :], mkey)
        cx.close_scope()

    st_slot = cx.slot("st")
    for stg in stages:
        kind = stg["kind"]
        if kind == "ffn":
            ffn(stg["l"], stg["s"])
        elif kind == "mix0_post":
            mix0_post()
        elif kind == "rwkv_post":
            rwkv_post()
        elif kind == "h_out":
            l = stg["l"]
            cx.open_scope()
            h = cx.sb("h", [128, KT, TOK], BF16)
            mod, mkey = compute_mod(l, 1, (0, 1))
            adaln(l, 1, mod, mkey, h, "h")
            if "hT_out_pairs" in io:
                cx.dma("sp", st_slot, io["hT_out_pairs"](h), reads=["h"])
            else:
                hout = dout("hT_out", [128, KT, TOK], BF16)
                cx.dma("sp", st_slot, [(hout[:, 0:KT // 2, :], h[:, 0:KT // 2, :]), (hout[:, KT // 2:KT, :], h[:, KT // 2:KT, :])],
                       reads=["h"])
            cx.close_scope()
        elif kind == "x_out":
            xout = dout("xT_out", [128, KT, TOK], F32)
            cx.dma("sp", st_slot, [(xout[:, 0:KT // 2, :], x[:, 0:KT // 2, :]), (xout[:, KT // 2:KT, :], x[:, KT // 2:KT, :])],
                   reads=["x"])
        elif kind == "final":
            fg, fkey = load_small("final_gT", [128, KT])
            rms_stats()
            for kt in range(KT):
                cx.op("dve", lambda e, kt=kt: e.scalar_tensor_tensor(
                    out=x[:, kt, :], in0=x[:, kt, :], scalar=fg[:, kt:kt + 1], in1=rstd[:],
                    op0=ALU.mult, op1=ALU.mult),
                    reads=["x", fkey, "rstd"], writes=["x"])
            xout = dout("xT_out", [128, KT, TOK], F32)
            cx.dma("sp", st_slot, [(xout[:, 0:KT // 2, :], x[:, 0:KT // 2, :]), (xout[:, KT // 2:KT, :], x[:, KT // 2:KT, :])],
                   reads=["x"])
    cx.close_scope()
    cx.wait_all("sp")
    return nc


def cast_pairs(dst, src):
    if len(dst.shape) == 2:
        n = dst.shape[1]
        return [(dst[:, c0:min(n, c0 + 1024)], src[:, c0:min(n, c0 + 1024)]) for c0 in range(0, n, 1024)]
    out = []
    for a in range(dst.shape[1]):
        n = dst.shape[2]
        for c0 in range(0, n, 1024):
            out.append((dst[:, a, c0:min(n, c0 + 1024)], src[:, a, c0:min(n, c0 + 1024)]))
    return out


def fm(v):
    v = np.asarray(v)
    return np.ascontiguousarray(v.reshape(-1, 128).T)


def to_xT(x):
    out = []
    for b in range(4):
        for j in range(2):
            xs = x[b, j * TOK:(j + 1) * TOK, :]
            out.append(np.ascontiguousarray(xs.T.reshape(KT, 128, TOK).transpose(1, 0, 2)))
    return out


def from_xT(tiles):
    x = np.empty((4, SEQ, D), np.float32)
    for b in range(4):
        for j in range(2):
            t = tiles[b * 2 + j]
            x[b, j * TOK:(j + 1) * TOK, :] = t.transpose(1, 0, 2).reshape(D, TOK).T
    return x


S5TC = 256
GELU_C = 0.7978845608028654
TWO_PI = 6.283185307179586


CHK_COUNT = 1
DBG_BANKS = [0, 1]
DBG_NODVE = False


class _Stop(Exception):
    pass


def build_M0(do_s5=True, do_ssd=True, stop=None, G=None, io=None):
    nc, cx, dram, banks, bankT = _mk_env(G)
    io = io or {}
    cx.open_scope()

    cnt = [CHK_COUNT]

    def chk(n):
        if stop == n:
            cnt[0] -= 1
            if cnt[0] <= 0:
                raise _Stop()
    try:
        _build_M0_body(nc, cx, dram, banks, bankT, io, do_s5, do_ssd, chk)
    except _Stop:
        pass
    while cx.stacks and stop is not None:
        cx.close_scope()
    if stop is None:
        cx.close_scope()
    cx.wait_all("sp")
    return nc


def _build_M0_body(nc, cx, dram, banks, bankT, io, do_s5, do_ssd, chk):

    def din(name, shape, dtype=F32):
        if name in io:
            return io[name]
        if name not in dram:
            dram[name] = nc.dram_tensor(name, list(shape), dtype, kind="ExternalInput").ap()
        return dram[name]

    def dout(name, shape, dtype=F32):
        if name in io:
            return io[name]
        dram[name] = nc.dram_tensor(name, list(shape), dtype, kind="ExternalOutput").ap()
        return dram[name]

    BF16S = "bf16_from_f32"

    def load(name, shape, dtype=F32, q="sp"):
        sdt, ddt = (BF16, F32) if dtype == BF16S else (dtype, dtype)
        t = cx.sb(name, shape, sdt)
        sl = cx.slot(name)
        pairs = cast_pairs(t[:], din(name, shape, ddt)) if dtype == BF16S else [(t[:], din(name, shape, ddt))]
        cx.dma(q, sl, pairs, writes=[name])
        return t

    w_d = din("w_in_c", [D, 3600])
    hT = cx.sb("hT", [128, KT, SEQ], BF16)
    sl = cx.slot("hT")
    if "hT_pairs" in io:
        cx.dma("sp", sl, io["hT_pairs"](hT, 0), reads=io.get("dep", []), writes=["hT"])
    else:
        hT_d = din("hT", [128, KT, SEQ], BF16)
        cx.dma("sp", sl, [(hT[:, 4 * i:4 * i + 4, :], hT_d[:, 4 * i:4 * i + 4, :]) for i in range(4)], writes=["hT"])
    ws = WStream(cx, nslot=4, elems=4096)
    ident = load("ident", [128, 128])
    identb = cx.sb("identb", [128, 128], BF16)
    cx.op("dve", lambda e: e.tensor_copy(out=identb[:], in_=ident[:]), reads=["ident"], writes=["identb"])
    st_slot = cx.slot("st")
    st_slot2 = cx.slot("st2")

    def proj(blk, col0, ncol_tiles, evac):
        wv = ws.view(blk)
        n = 0
        for ti in range(ncol_tiles):
            for tb in range(4):
                bk = DBG_BANKS[n % len(DBG_BANKS)]
                n += 1
                for kt in range(KT):
                    cx.op("pe", lambda e, kt=kt, ti=ti, tb=tb, bk=bk: e.matmul(
                        banks[bk][:], wv[:, kt, (col0 + ti) * 128:(col0 + ti + 1) * 128],
                        hT[:, kt, tb * 512:(tb + 1) * 512], start=(kt == 0), stop=(kt == KT - 1)),
                        reads=[ws.key(blk), "hT"], writes=[f"bank{bk}"])
                chk(53)
                evac(ti, tb, banks[bk], f"bank{bk}")
                chk(54)

    if do_s5:
        cx.open_scope()
        lre = load("s5_lre", [128, 16])
        lim = load("s5_lim", [128, 16])
        ldt = load("s5_ldt", [128, 16])
        d5 = load("s5_dT", [128, 4])
        cre = load("s5_cre", [128, 16, 128], BF16S, q="pool")
        cimn = load("s5_cim", [128, 16, 128], BF16S, q="pool")
        cx.op("dve", lambda e: e.tensor_scalar(out=cimn[:], in0=cimn[:], scalar1=-1.0, scalar2=None, op0=ALU.mult),
              reads=["s5_cim"], writes=["s5_cim"])
        bre = cx.sb("breT", [128, 16, 128], BF16)
        bim = cx.sb("bimT", [128, 16, 128], BF16)

        sm = {}

        def S(name):
            sm[name] = cx.sb("s5_" + name, [128, 16], F32)
            return sm[name]

        def tt(o, a, b, op, eng="dve"):
            cx.op(eng, lambda e: e.tensor_tensor(out=sm[o][:], in0=sm[a][:], in1=sm[b][:], op=op),
                  reads=["s5sm"], writes=["s5sm"])

        def ts(o, a, s1, op0, s2=None, op1=None):
            if op1 is None:
                cx.op("dve", lambda e: e.tensor_scalar(out=sm[o][:], in0=sm[a][:], scalar1=s1, scalar2=None, op0=op0),
                      reads=["s5sm"], writes=["s5sm"])
            else:
                cx.op("dve", lambda e: e.tensor_scalar(out=sm[o][:], in0=sm[a][:], scalar1=s1, scalar2=s2, op0=op0, op1=op1),
                      reads=["s5sm"], writes=["s5sm"])

        def act(o, a, func, scale=1.0):
            cx.op("act", lambda e: e.activation(out=sm[o][:], in_=sm[a][:], func=func, scale=scale),
                  reads=["s5sm"], writes=["s5sm"])

        chk(1)
        sm["lre"], sm["lim"], sm["ldt"] = lre, lim, ldt
        for n_ in ("lr", "dt", "mag", "ang", "cs", "sn", "t1", "t2", "t3", "den", "nr", "fre", "fim", "lbr", "lbi", "rden"):
            S(n_)
        cx.wait_all("dve")
        cx.wait_all("act")
        ts("lr", "lre", -1e-4, ALU.min)
        act("dt", "ldt", AF.Exp)
        tt("t1", "lr", "dt", ALU.mult)
        act("mag", "t1", AF.Exp)
        tt("ang", "lim", "dt", ALU.mult)

        def sincos(o, a, shift):
            ki = cx.sb("s5_ki", [128, 16], mybir.dt.int32)
            ts("t1", a, 1.0 / TWO_PI, ALU.mult, shift / TWO_PI, ALU.add)
            cx.op("dve", lambda e: e.tensor_copy(out=ki[:], in_=sm["t1"][:]), reads=["s5sm"], writes=["s5ki"])
            cx.op("dve", lambda e: e.tensor_copy(out=sm["t2"][:], in_=ki[:]), reads=["s5ki"], writes=["s5sm"])
            tt("t1", "t1", "t2", ALU.subtract)
            ts("t2", "t1", 0.5, ALU.is_gt)
            tt("t1", "t1", "t2", ALU.subtract)
            ts("t2", "t1", -0.5, ALU.is_lt)
            tt("t1", "t1", "t2", ALU.add)
            act(o, "t1", AF.Sin, scale=TWO_PI)

        sincos("sn", "ang", 0.0)
        sincos("cs", "ang", TWO_PI / 4)
        tt("lbr", "mag", "cs", ALU.mult)
        tt("lbi", "mag", "sn", ALU.mult)
        tt("t1", "lr", "lr", ALU.mult)
        tt("t2", "lim", "lim", ALU.mult)
        tt("den", "t1", "t2", ALU.add)
        cx.op("dve", lambda e: e.reciprocal(out=sm["rden"][:], in_=sm["den"][:]), reads=["s5sm"], writes=["s5sm"])
        ts("nr", "lbr", -1.0, ALU.add)
        tt("t1", "nr", "lr", ALU.mult)
        tt("t2", "lbi", "lim", ALU.mult)
        tt("t1", "t1", "t2", ALU.add)
        tt("fre", "t1", "rden", ALU.mult)
        tt("t1", "lbi", "lr", ALU.mult)
        tt("t2", "nr", "lim", ALU.mult)
        tt("t1", "t1", "t2", ALU.subtract)
        tt("fim", "t1", "rden", ALU.mult)
        S("nfim")
        ts("nfim", "fim", -1.0, ALU.mult)
        S("nsn")
        ts("nsn", "sn", -1.0, ALU.mult)

        chk(2)
        cx.open_scope()
        xbre = load("s5_xbre", [128, 16, 128], q="act")
        xbim = load("s5_xbim", [128, 16, 128], q="act")
        xt = [cx.sb(f"s5xt{i}", [128, 128], F32) for i in range(2)]
        for pr in range(16):
            for part, (A, fa, Bm, fb) in enumerate(((xbre, "fre", xbim, "nfim"), (xbim, "fre", xbre, "fim"))):
                t_ = xt[part]
                cx.op("dve", lambda e, pr=pr, A=A, fa=fa, t_=t_: e.tensor_scalar(
                    out=t_[:], in0=A[:, pr, :], scalar1=sm[fa][:, pr:pr + 1], scalar2=None, op0=ALU.mult),
                    reads=["s5sm", "s5_xbre", "s5_xbim"], writes=[f"s5xt{part}"])
                cx.op("dve", lambda e, pr=pr, Bm=Bm, fb=fb, t_=t_: e.scalar_tensor_tensor(
                    out=t_[:], in0=Bm[:, pr, :], scalar=sm[fb][:, pr:pr + 1], in1=t_[:], op0=ALU.mult, op1=ALU.add),
                    reads=["s5sm", "s5_xbre", "s5_xbim", f"s5xt{part}"], writes=[f"s5xt{part}"])
                cx.op("pe", lambda e, t_=t_, part=part: e.transpose(banks[2 + part][:, 0:128], t_[:], ident[:]),
                      reads=[f"s5xt{part}", "ident"], writes=[f"bank{2 + part}"])
                dst = bre if part == 0 else bim
                cx.op("act", lambda e, dst=dst, pr=pr, part=part: e.activation(
                    out=dst[:, pr, :], in_=banks[2 + part][:, 0:128], func=AF.Copy),
                    reads=[f"bank{2 + part}"], writes=["breT" if part == 0 else "bimT"])

        chk(3)
        cx.close_scope()
        chk(4)
        ctab = cx.sb("ctab", [128, 16, S5TC], F32)
        stab = cx.sb("stab", [128, 16, S5TC], F32)
        rho = cx.sb("rho", [128, 16, S5TC], F32)
        ec = cx.sb("ec", [128, 16], F32)
        es = cx.sb("es", [128, 16], F32)
        et = [cx.sb(f"et{i}", [128, 16], F32) for i in range(3)]
        cx.op("dve", lambda e: e.memset(ctab[:, :, 0:1], 1.0), writes=["tab"])
        cx.op("dve", lambda e: e.memset(stab[:, :, 0:1], 0.0), reads=["tab"], writes=["tab"])
        cx.op("dve", lambda e: e.tensor_copy(out=ec[:], in_=sm["cs"][:]), reads=["s5sm"], writes=["e"])
        cx.op("dve", lambda e: e.tensor_copy(out=es[:], in_=sm["sn"][:]), reads=["s5sm", "e"], writes=["e"])
        L = 1
        while L < S5TC:
            for pr in range(16):
                cx.op("dve", lambda e, pr=pr, L=L: e.tensor_scalar(
                    out=ctab[:, pr, L:2 * L], in0=ctab[:, pr, 0:L], scalar1=ec[:, pr:pr + 1], scalar2=None, op0=ALU.mult),
                    reads=["tab", "e"], writes=["tab"])
                cx.op("dve", lambda e, pr=pr, L=L: e.tensor_scalar(
                    out=stab[:, pr, L:2 * L], in0=ctab[:, pr, 0:L], scalar1=es[:, pr:pr + 1], scalar2=None, op0=ALU.mult),
                    reads=["tab", "e"], writes=["tab"])
            cx.op("dve", lambda e: e.tensor_scalar(out=et[0][:], in0=es[:], scalar1=-1.0, scalar2=None, op0=ALU.mult),
                  reads=["e"], writes=["et"])
            for pr in range(16):
                cx.op("dve", lambda e, pr=pr, L=L: e.scalar_tensor_tensor(
                    out=ctab[:, pr, L:2 * L], in0=stab[:, pr, 0:L], scalar=et[0][:, pr:pr + 1], in1=ctab[:, pr, L:2 * L],
                    op0=ALU.mult, op1=ALU.add), reads=["tab", "et"], writes=["tab"])
                cx.op("dve", lambda e, pr=pr, L=L: e.scalar_tensor_tensor(
                    out=stab[:, pr, L:2 * L], in0=stab[:, pr, 0:L], scalar=ec[:, pr:pr + 1], in1=stab[:, pr, L:2 * L],
                    op0=ALU.mult, op1=ALU.add), reads=["tab", "e"], writes=["tab"])
            cx.op("dve", lambda e: e.tensor_tensor(out=et[1][:], in0=ec[:], in1=ec[:], op=ALU.mult), reads=["e"], writes=["et1"])
            cx.op("dve", lambda e: e.tensor_tensor(out=et[2][:], in0=es[:], in1=es[:], op=ALU.mult), reads=["e"], writes=["et2"])
            cx.op("dve", lambda e: e.scalar_tensor_tensor(out=es[:], in0=es[:], scalar=2.0, in1=ec[:], op0=ALU.mult, op1=ALU.mult),
                  reads=["e"], writes=["e"])
            cx.op("dve", lambda e: e.tensor_tensor(out=ec[:], in0=et[1][:], in1=et[2][:], op=ALU.subtract),
                  reads=["et1", "et2", "e"], writes=["e"])
            L *= 2
        for pr in range(16):
            cx.op("act", lambda e, pr=pr: e.activation(out=rho[:, pr, :], in_=ctab[:, pr, :], func=AF.Identity,
                                                        scale=0.0, bias=sm["mag"][:, pr:pr + 1]),
                  reads=["tab", "s5sm"], writes=["rho"])

        chk(5)
        u32 = cx.sb("u32", [128, SEQ], F32)
        ubf = cx.sb("ubf", [128, SEQ], BF16)
        y5 = cx.sb("y5", [128, SEQ], F32)
        carry = [cx.sb(f"carry{i}", [128, 16], F32) for i in range(2)]
        cx.op("dve", lambda e: e.memset(carry[0][:], 0.0), writes=["carry"])
        cx.op("dve", lambda e: e.memset(carry[1][:], 0.0), reads=["carry"], writes=["carry"])
        chk(51)
        tmp = [[cx.sb(f"s5tmp{j}{i}", [128, S5TC], F32) for i in range(8)] for j in range(2)]
        sbf = [[cx.sb(f"s5sbf{i}{j}", [128, S5TC], BF16) for j in range(2)] for i in range(4)]
        gl = [cx.sb(f"s5gl{i}", [128, S5TC], F32) for i in range(4)]
        y5_d = None if "y_tile" in io else dout("y5T", [128, 4, SEQ])
        NCH = SEQ // S5TC
        for o in range(4):
            blk = ws.add(w_d[:, o * 128:(o + 1) * 128].rearrange("(kt p) c -> p kt c", p=128), KT, 128)
            ws.need(blk)
            chk(52)

            def ev_u(ti, tb, ps, key):
                cx.op("act", lambda e: e.activation(out=u32[:, tb * 512:(tb + 1) * 512], in_=ps[:], func=AF.Copy),
                      reads=[key], writes=["u32"])
                if not DBG_NODVE:
                    cx.op("dve", lambda e: e.tensor_copy(out=ubf[:, tb * 512:(tb + 1) * 512], in_=ps[:]),
                          reads=[key], writes=["ubf"])
            proj(blk, 0, 1, ev_u)
            chk(6)
            for ch in range(NCH):
                c0 = ch * S5TC
                if ch == 1:
                    chk(7)
                for pp in range(4):
                    pr = o * 4 + pp
                    kre, kim = f"bank{2 + (pp % 2) * 2}", f"bank{3 + (pp % 2) * 2}"
                    pre, pim = banks[2 + (pp % 2) * 2], banks[3 + (pp % 2) * 2]
                    cx.op("pe", lambda e: e.matmul(pre[:, 0:S5TC], bre[:, pr, :], ubf[:, c0:c0 + S5TC], start=True, stop=True),
                          reads=["breT", "ubf"], writes=[kre])
                    cx.op("pe", lambda e: e.matmul(pim[:, 0:S5TC], bim[:, pr, :], ubf[:, c0:c0 + S5TC], start=True, stop=True),
                          reads=["bimT", "ubf"], writes=[kim])
                    t = tmp[pp % 2]
                    tq = pp % 2
                    ct, stb = ctab[:, pr, :], stab[:, pr, :]
                    cx.op("dve", lambda e: e.tensor_tensor(out=t[0][:], in0=pre[:, 0:S5TC], in1=ct, op=ALU.mult),
                          reads=[kre, "tab"], writes=[f"t{tq}_0"])
                    cx.op("dve", lambda e: e.tensor_tensor(out=t[1][:], in0=pim[:, 0:S5TC], in1=stb, op=ALU.mult),
                          reads=[kim, "tab"], writes=[f"t{tq}_1"])
                    cx.op("dve", lambda e: e.tensor_tensor(out=t[2][:], in0=pim[:, 0:S5TC], in1=ct, op=ALU.mult),
                          reads=[kim, "tab"], writes=[f"t{tq}_2"])
                    cx.op("dve", lambda e: e.tensor_tensor(out=t[3][:], in0=pre[:, 0:S5TC], in1=stb, op=ALU.mult),
                          reads=[kre, "tab"], writes=[f"t{tq}_3"])
                    cx.op("pool", lambda e: e.tensor_tensor(out=t[0][:], in0=t[0][:], in1=t[1][:], op=ALU.add),
                          reads=[f"t{tq}_0", f"t{tq}_1"], writes=[f"t{tq}_0"])
                    cx.op("pool", lambda e: e.tensor_tensor(out=t[2][:], in0=t[2][:], in1=t[3][:], op=ALU.subtract),
                          reads=[f"t{tq}_2", f"t{tq}_3"], writes=[f"t{tq}_2"])
                    cx.op("dve", lambda e: e.tensor_tensor_scan(out=t[4][:], data0=rho[:, pr, :], data1=t[0][:],
                                                                 initial=carry[0][:, pr:pr + 1], op0=ALU.mult, op1=ALU.add),
                          reads=["rho", f"t{tq}_0", "carry"], writes=[f"t{tq}_4"])
                    cx.op("dve", lambda e: e.tensor_tensor_scan(out=t[5][:], data0=rho[:, pr, :], data1=t[2][:],
                                                                 initial=carry[1][:, pr:pr + 1], op0=ALU.mult, op1=ALU.add),
                          reads=["rho", f"t{tq}_2", "carry"], writes=[f"t{tq}_5"])
                    if ch < NCH - 1:
                        cx.op("dve", lambda e: e.tensor_scalar(out=et[1][:, 0:1], in0=t[5][:, S5TC - 1:S5TC], scalar1=es[:, pr:pr + 1],
                                                                scalar2=None, op0=ALU.mult), reads=[f"t{tq}_5", "e"], writes=["et1"])
                        cx.op("dve", lambda e: e.tensor_scalar(out=et[2][:, 0:1], in0=t[4][:, S5TC - 1:S5TC], scalar1=es[:, pr:pr + 1],
                                                                scalar2=None, op0=ALU.mult), reads=[f"t{tq}_4", "e"], writes=["et2"])
                        cx.op("dve", lambda e: e.scalar_tensor_tensor(out=carry[0][:, pr:pr + 1], in0=t[4][:, S5TC - 1:S5TC],
                                                                       scalar=ec[:, pr:pr + 1], in1=et[1][:, 0:1],
                                                                       op0=ALU.mult, op1=ALU.subtract),
                              reads=[f"t{tq}_4", "e", "et1", "carry"], writes=["carry"])
                        cx.op("dve", lambda e: e.scalar_tensor_tensor(out=carry[1][:, pr:pr + 1], in0=t[5][:, S5TC - 1:S5TC],
                                                                       scalar=ec[:, pr:pr + 1], in1=et[2][:, 0:1],
                                                                       op0=ALU.mult, op1=ALU.add),
                              reads=[f"t{tq}_5", "e", "et2", "carry"], writes=["carry"])
                    cx.op("pool", lambda e: e.tensor_tensor(out=t[6][:], in0=t[4][:], in1=ct, op=ALU.mult),
                          reads=[f"t{tq}_4", "tab"], writes=[f"t{tq}_6"])
                    cx.op("pool", lambda e: e.tensor_tensor(out=t[7][:], in0=t[5][:], in1=stb, op=ALU.mult),
                          reads=[f"t{tq}_5", "tab"], writes=[f"t{tq}_7"])
                    cx.op("pool", lambda e: e.tensor_tensor(out=sbf[pp][0][:], in0=t[6][:], in1=t[7][:], op=ALU.subtract),
                          reads=[f"t{tq}_6", f"t{tq}_7"], writes=[f"sbf{pp}0"])
                    cx.op("pool", lambda e: e.tensor_tensor(out=t[6][:], in0=t[4][:], in1=stb, op=ALU.mult),
                          reads=[f"t{tq}_4", "tab", f"t{tq}_6"], writes=[f"t{tq}_6"])
                    cx.op("pool", lambda e: e.tensor_tensor(out=t[7][:], in0=t[5][:], in1=ct, op=ALU.mult),
                          reads=[f"t{tq}_5", "tab", f"t{tq}_7"], writes=[f"t{tq}_7"])
                    cx.op("pool", lambda e: e.tensor_tensor(out=sbf[pp][1][:], in0=t[6][:], in1=t[7][:], op=ALU.add),
                          reads=[f"t{tq}_6", f"t{tq}_7"], writes=[f"sbf{pp}1"])
                py = banks[6]
                for pp in range(4):
                    pr = o * 4 + pp
                    cx.op("pe", lambda e: e.matmul(py[:, 0:S5TC], cre[:, pr, :], sbf[pp][0][:], start=(pp == 0), stop=False),
                          reads=["s5_cre", f"sbf{pp}0"], writes=["bank6"])
                    cx.op("pe", lambda e: e.matmul(py[:, 0:S5TC], cimn[:, pr, :], sbf[pp][1][:], start=False, stop=(pp == 3)),
                          reads=["s5_cim", f"sbf{pp}1"], writes=["bank6"])
                cx.op("dve", lambda e: e.scalar_tensor_tensor(out=gl[0][:], in0=u32[:, c0:c0 + S5TC], scalar=d5[:, o:o + 1],
                                                               in1=py[:, 0:S5TC], op0=ALU.mult, op1=ALU.add),
                      reads=["u32", "s5_dT", "bank6"], writes=["gl0"])
                cx.op("act", lambda e: e.activation(out=gl[1][:], in_=gl[0][:], func=AF.Square), reads=["gl0"], writes=["gl1"])
                cx.op("dve", lambda e: e.tensor_scalar(out=gl[1][:], in0=gl[1][:], scalar1=GELU_C * 0.044715, scalar2=GELU_C,
                                                        op0=ALU.mult, op1=ALU.add), reads=["gl1"], writes=["gl1"])
                cx.op("dve", lambda e: e.tensor_tensor(out=gl[1][:], in0=gl[1][:], in1=gl[0][:], op=ALU.mult),
                      reads=["gl1", "gl0"], writes=["gl1"])
                cx.op("act", lambda e: e.activation(out=gl[2][:], in_=gl[1][:], func=AF.Tanh), reads=["gl1"], writes=["gl2"])
                cx.op("act", lambda e: e.activation(out=gl[3][:], in_=gl[0][:], func=AF.Copy, scale=0.5), reads=["gl0"], writes=["gl3"])
                cx.op("dve", lambda e: e.scalar_tensor_tensor(out=y5[:, c0:c0 + S5TC], in0=gl[2][:], scalar=1.0, in1=gl[3][:],
                                                               op0=ALU.add, op1=ALU.mult),
                      reads=["gl2", "gl3"], writes=["y5"])
            cx.dma("sp", st_slot if o % 2 == 0 else st_slot2,
                   [(io["y_tile"](o) if "y_tile" in io else y5_d[:, o, :], y5[:])], reads=["y5"])
        cx.close_scope()

    if do_ssd:
        cx.open_scope()
        tri = load("tri", [128, 128])
        ones = load("ones128", [128, 128])
        maskneg = load("maskneg", [128, 128])
        cw = load("conv_wT", [128, 16, 4])
        cb = load("conv_bT", [128, 16])
        dtb = load("dt_bias_bc", [128, 16])
        alog = load("a_log_bc", [128, 16])
        dsk = load("ssd_dT", [128, 8])
        onec = cx.sb("onec", [128, 1], F32)
        cx.op("dve", lambda e: e.memset(onec[:], 1.0), writes=["onec"])
        abc = cx.sb("abc", [128, 16], F32)
        cx.op("act", lambda e: e.activation(out=abc[:], in_=alog[:], func=AF.Exp), reads=["a_log_bc"], writes=["abc"])
        cx.op("dve", lambda e: e.tensor_scalar(out=abc[:], in0=abc[:], scalar1=-1.0, scalar2=None, op0=ALU.mult),
              reads=["abc"], writes=["abc"])
        NC_ = SEQ // 128
        dt_all = cx.sb("dt_all", [128, NC_, 16], F32)
        adt = cx.sb("adt", [128, NC_, 16], F32)
        cum = cx.sb("cum", [128, NC_, 16], F32)
        dte = cx.sb("dte", [128, NC_, 16], F32)
        dectot = cx.sb("dectot", [128, NC_, 16], F32)
        dg = [cx.sb(f"dg{i}", [128, 128], F32) for i in range(2)]
        tsm = [cx.sb(f"tsm{i}", [128, 16], F32) for i in range(2)]
        bdt = ws.add(w_d[:, 3584:3600].rearrange("(kt p) c -> p kt c", p=128), KT, 16)
        ws.need(bdt)
        wdt = ws.view(bdt)
        b2 = banks[2]
        for c in range(NC_):
            for kt in range(KT):
                cx.op("pe", lambda e, kt=kt: e.matmul(b2[:, 0:16], hT[:, kt, c * 128:(c + 1) * 128], wdt[:, kt, :],
                                                       start=(kt == 0), stop=(kt == KT - 1)),
                      reads=[ws.key(bdt), "hT"], writes=["bank2"])
            cx.op("dve", lambda e: e.tensor_tensor(out=tsm[0][:], in0=b2[:, 0:16], in1=dtb[:], op=ALU.add),
                  reads=["bank2", "dt_bias_bc"], writes=["tsm0"])
            cx.op("act", lambda e: e.activation(out=tsm[0][:], in_=tsm[0][:], func=AF.Exp), reads=["tsm0"], writes=["tsm0"])
            cx.op("act", lambda e: e.activation(out=dt_all[:, c, :], in_=tsm[0][:], func=AF.Ln, bias=onec[:], scale=1.0),
                  reads=["tsm0", "onec"], writes=["dt_all"])
            cx.op("dve", lambda e: e.tensor_tensor(out=adt[:, c, :], in0=dt_all[:, c, :], in1=abc[:], op=ALU.mult),
                  reads=["dt_all", "abc"], writes=["adt"])
            cx.op("pe", lambda e: e.matmul(banks[3][:, 0:16], tri[:], adt[:, c, :], start=True, stop=True),
                  reads=["tri", "adt"], writes=["bank3"])
            cx.op("pe", lambda e: e.matmul(banks[4][:, 0:16], ones[:], adt[:, c, :], start=True, stop=True),
                  reads=["ones128", "adt"], writes=["bank4"])
            cx.op("act", lambda e: e.activation(out=cum[:, c, :], in_=banks[3][:, 0:16], func=AF.Copy), reads=["bank3"], writes=["cum"])
            cx.op("act", lambda e: e.activation(out=dectot[:, c, :], in_=banks[4][:, 0:16], func=AF.Exp), reads=["bank4"], writes=["dectot"])
            cx.op("dve", lambda e: e.tensor_tensor(out=tsm[1][:], in0=banks[4][:, 0:16], in1=cum[:, c, :], op=ALU.subtract),
                  reads=["bank4", "cum"], writes=["tsm1"])
            cx.op("act", lambda e: e.activation(out=dte[:, c, :], in_=tsm[1][:], func=AF.Exp), reads=["tsm1"], writes=["dte"])

        raw = cx.sb("raw", [128, 4, 3 + SEQ], F32)
        cx.op("dve", lambda e: e.memset(raw[:, :, 0:3], 0.0), writes=["raw"])
        sz = cx.sb("sz", [128, 2, SEQ], BF16)
        cv = [cx.sb("cv0", [128, SEQ], F32)]
        xs32 = cx.sb("xs32", [128, 2, SEQ], F32)
        xsb = cx.sb("xsb", [128, 2, SEQ], BF16)
        BT = cx.sb("BT", [128, SEQ], BF16)
        CT = cx.sb("CT", [128, SEQ], BF16)
        yout = raw[:, 0:2, 3:3 + SEQ]
        car32 = cx.sb("car32", [128, 4, 64], F32)
        carb = cx.sb("carb", [128, 4, 64], BF16)
        Btok = [cx.sb(f"Btok{i}", [128, 128], BF16) for i in range(2)]
        xq = [cx.sb(f"xq{i}", [128, 256], BF16) for i in range(2)]
        xqd = [cx.sb(f"xqd{i}", [128, 256], BF16) for i in range(2)]
        dm32 = [cx.sb(f"dm32{i}", [128, 128], F32) for i in range(2)]
        dmx = [cx.sb(f"dmx{i}", [128, 128], F32) for i in range(2)]
        MT = [[cx.sb(f"MT{i}{j}", [128, 128], BF16) for j in range(4)] for i in range(2)]
        ebc = [cx.sb(f"ebc{i}", [128, 128], F32) for i in range(2)]
        Cs = [[cx.sb(f"Cs{i}{j}", [128, 128], BF16) for j in range(4)] for i in range(2)]
        ytmp = [cx.sb(f"ytmp{i}", [128, 128], F32) for i in range(2)]
        ys_d = None if "y_tile" in io else dout("ysT", [128, 8, SEQ])
        b3, b4, b5 = banks[3], banks[4], banks[5]
        for gg in range(4):
            base = 512 + gg * 768
            bx = ws.add(w_d[:, base:base + 256].rearrange("(kt p) c -> p kt c", p=128), KT, 256)
            bbc = ws.add(w_d[:, base + 256:base + 512].rearrange("(kt p) c -> p kt c", p=128), KT, 256)
            bz = ws.add(w_d[:, base + 512:base + 768].rearrange("(kt p) c -> p kt c", p=128), KT, 256)

            def ev_raw(off):
                def f(ti, tb, ps, key):
                    cx.op("act", lambda e: e.activation(out=raw[:, off + ti, 3 + tb * 512:3 + (tb + 1) * 512], in_=ps[:], func=AF.Copy),
                          reads=[key], writes=["raw"])
                return f

            def ev_z(ti, tb, ps, key):
                cx.op("act", lambda e: e.activation(out=sz[:, ti, tb * 512:(tb + 1) * 512], in_=ps[:], func=AF.Silu),
                      reads=[key], writes=["sz"])
            ws.need(bx)
            proj(bx, 0, 2, ev_raw(0))
            ws.need(bbc)
            proj(bbc, 0, 2, ev_raw(2))
            ws.need(bz)
            proj(bz, 0, 2, ev_z)
            for ti in range(4):
                tidx = gg * 4 + ti
                cvt = cv[0]
                ck = "cv0"
                cx.op("dve", lambda e: e.tensor_scalar(out=cvt[:], in0=raw[:, ti, 0:SEQ], scalar1=cw[:, tidx, 0:1], scalar2=None,
                                                        op0=ALU.mult), reads=["raw", "conv_wT"], writes=[ck])
                for jj in range(1, 4):
                    cx.op("dve", lambda e, jj=jj: e.scalar_tensor_tensor(out=cvt[:], in0=raw[:, ti, jj:jj + SEQ],
                                                                       scalar=cw[:, tidx, jj:jj + 1], in1=cvt[:],
                                                                       op0=ALU.mult, op1=ALU.add),
                          reads=["raw", "conv_wT", ck], writes=[ck])
                if ti < 2:
                    cx.op("act", lambda e: e.activation(out=xs32[:, ti, :], in_=cvt[:], func=AF.Silu, bias=cb[:, tidx:tidx + 1], scale=1.0),
                          reads=[ck, "conv_bT"], writes=["xs32"])
                    cx.op("pool", lambda e: e.tensor_copy(out=xsb[:, ti, :], in_=xs32[:, ti, :]), reads=["xs32"], writes=["xsb"])
                else:
                    dst, dk = (BT, "BT") if ti == 2 else (CT, "CT")
                    cx.op("act", lambda e: e.activation(out=dst[:], in_=cvt[:], func=AF.Silu, bias=cb[:, tidx:tidx + 1], scale=1.0),
                          reads=[ck, "conv_bT"], writes=[dk])
            cx.op("dve", lambda e: e.memset(car32[:], 0.0), reads=["car32"], writes=["car32"])
            cx.op("dve", lambda e: e.memset(carb[:], 0.0), reads=["carb"], writes=["carb"])
            def partA(c):
                cs_ = slice(c * 128, (c + 1) * 128)
                par = c % 2
                cx.op("pe", lambda e: e.transpose(bankT[:, 0:128], BT[:, cs_], identb[:]), reads=["BT", "identb"], writes=["bankT"])
                cx.op("pe", lambda e: e.transpose(bankT[:, 128:256], xsb[:, 0, cs_], identb[:]), reads=["xsb", "identb"], writes=["bankT"])
                cx.op("pe", lambda e: e.transpose(bankT[:, 256:384], xsb[:, 1, cs_], identb[:]), reads=["xsb", "identb"], writes=["bankT"])
                cx.op("act", lambda e: e.activation(out=Btok[par][:], in_=bankT[:, 0:128], func=AF.Copy),
                      reads=["bankT"], writes=[f"Btok{par}"])
                for hh in range(4):
                    h_ = gg * 4 + hh
                    cx.op("dve", lambda e: e.tensor_scalar(out=xq[par][:, hh * 64:(hh + 1) * 64], in0=bankT[:, 128 + hh * 64:192 + hh * 64],
                                                            scalar1=dt_all[:, c, h_:h_ + 1], scalar2=None, op0=ALU.mult),
                          reads=["bankT", "dt_all"], writes=[f"xq{par}"])
                    cx.op("pool", lambda e: e.tensor_scalar(out=xqd[par][:, hh * 64:(hh + 1) * 64], in0=xq[par][:, hh * 64:(hh + 1) * 64],
                                                             scalar1=dte[:, c, h_:h_ + 1], scalar2=None, op0=ALU.mult),
                          reads=[f"xq{par}", "dte"], writes=[f"xqd{par}"])
                cx.op("pe", lambda e: e.matmul(b3[:, 0:128], BT[:, cs_], CT[:, cs_], start=True, stop=True),
                      reads=["BT", "CT"], writes=["bank3"])
                for hh in range(4):
                    h_ = gg * 4 + hh
                    hp = hh % 2
                    crow = banks[hh % 2][:, 0:128]
                    cx.op("pool", lambda e: e.tensor_scalar(out=dg[hp][:], in0=ident[:], scalar1=cum[:, c, h_:h_ + 1], scalar2=None,
                                                             op0=ALU.mult), reads=["ident", "cum"], writes=[f"dg{hp}"])
                    cx.op("pe", lambda e: e.matmul(crow, ones[:], dg[hp][:], start=True, stop=True),
                          reads=["ones128", f"dg{hp}"], writes=[f"bank{hh % 2}"])
                    cx.op("dve", lambda e: e.scalar_tensor_tensor(out=dm32[hp][:], in0=crow, scalar=cum[:, c, h_:h_ + 1], in1=maskneg[:],
                                                                   op0=ALU.subtract, op1=ALU.add),
                          reads=[f"bank{hh % 2}", "cum", "maskneg"], writes=[f"dm32{hp}"])
                    cx.op("act", lambda e: e.activation(out=dmx[hp][:], in_=dm32[hp][:], func=AF.Exp), reads=[f"dm32{hp}"], writes=[f"dmx{hp}"])
                    cx.op("dve", lambda e: e.tensor_tensor(out=MT[par][hh][:], in0=dmx[hp][:], in1=b3[:, 0:128], op=ALU.mult),
                          reads=[f"dmx{hp}", "bank3"], writes=[f"MT{par}{hh}"])
                    cx.op("act", lambda e: e.activation(out=ebc[hp][:], in_=crow, func=AF.Exp), reads=[f"bank{hh % 2}"], writes=[f"ebc{hp}"])
                    cx.op("pool", lambda e: e.tensor_tensor(out=Cs[par][hh][:], in0=CT[:, cs_], in1=ebc[hp][:], op=ALU.mult),
                          reads=["CT", f"ebc{hp}"], writes=[f"Cs{par}{hh}"])
                cx.op("pe", lambda e: e.matmul(banks[2][:, 0:256], Btok[par][:], xqd[par][:], start=True, stop=True),
                      reads=[f"Btok{par}", f"xqd{par}"], writes=["bank2"])

            def partB(c):
                cs_ = slice(c * 128, (c + 1) * 128)
                par = c % 2
                for hh in range(4):
                    pt, half = hh // 2, hh % 2
                    yo = banks[5 + par][half * 64:(half + 1) * 64, pt * 128:(pt + 1) * 128]
                    cx.op("pe", lambda e: e.matmul(yo, xq[par][:, hh * 64:(hh + 1) * 64], MT[par][hh][:], start=True, stop=False),
                          reads=[f"xq{par}", f"MT{par}{hh}"], writes=[f"bank{5 + par}"])
                    cx.op("pe", lambda e: e.matmul(yo, carb[:, hh, :], Cs[par][hh][:], start=False, stop=True),
                          reads=["carb", f"Cs{par}{hh}"], writes=[f"bank{5 + par}"])
                for hh in range(4):
                    h_ = gg * 4 + hh
                    cx.op("dve", lambda e: e.scalar_tensor_tensor(out=car32[:, hh, :], in0=car32[:, hh, :], scalar=dectot[:, c, h_:h_ + 1],
                                                                   in1=banks[2][:, hh * 64:(hh + 1) * 64], op0=ALU.mult, op1=ALU.add),
                          reads=["car32", "dectot", "bank2"], writes=["car32"])
                cx.op("pool", lambda e: e.tensor_copy(out=carb[:], in_=car32[:]), reads=["car32"], writes=["carb"])
                for pt in range(2):
                    cx.op("dve", lambda e: e.scalar_tensor_tensor(out=ytmp[pt][:], in0=xs32[:, pt, cs_], scalar=dsk[:, gg * 2 + pt:gg * 2 + pt + 1],
                                                                   in1=banks[5 + par][:, pt * 128:(pt + 1) * 128],
                                                                   op0=ALU.mult, op1=ALU.add),
                          reads=["xs32", "ssd_dT", f"bank{5 + par}"], writes=[f"ytmp{pt}"])
                    cx.op("pool", lambda e: e.tensor_tensor(out=yout[:, pt, cs_], in0=ytmp[pt][:], in1=sz[:, pt, cs_], op=ALU.mult),
                          reads=[f"ytmp{pt}", "sz"], writes=["raw"])

            for it_ in cx.record(partA, 0):
                cx.play(it_)
            for c in range(NC_):
                la = cx.record(partB, c)
                lb = cx.record(partA, c + 1) if c + 1 < NC_ else []
                cx.play_interleaved(la, lb)
            cx.dma("sp", st_slot if gg % 2 == 0 else st_slot2,
                   [((io["y_tile"](4 + gg * 2 + pt) if "y_tile" in io else ys_d[:, gg * 2 + pt, :]), yout[:, pt, :])
                    for pt in range(2)], reads=["raw"])
        cx.close_scope()


def prep_M0(inp, b, j, hT_full):
    m = {"hT": hT_full, "ident": np.eye(128, dtype=np.float32)}
    w = inp["hyb_w_in"][0]
    cols = [np.arange(j * 512, (j + 1) * 512)]
    for gg in range(4):
        G = j * 4 + gg
        cols.append(3072 + G * 256 + np.arange(256))
        cols.append(3072 + 2048 + G * 128 + np.arange(128))
        cols.append(3072 + 3072 + G * 128 + np.arange(128))
        cols.append(1024 + G * 256 + np.arange(256))
    cols.append(7168 + 16 * j + np.arange(16))
    cols = np.concatenate(cols)
    m["w_in_c"] = np.ascontiguousarray(w[:, cols])
    g0 = 32 * j
    lre = np.zeros((128, 16), np.float32)
    lim = np.zeros((128, 16), np.float32)
    ldt = np.zeros((128, 16), np.float32)
    xbre = np.zeros((128, 16, 128), np.float32)
    xbim = np.zeros((128, 16, 128), np.float32)
    cre = np.zeros((128, 16, 128), np.float32)
    cim = np.zeros((128, 16, 128), np.float32)
    for pr in range(16):
        pp = pr % 4
        for gi in range(2):
            g = g0 + 2 * pr + gi
            rows = slice(gi * 64, gi * 64 + 64)
            cs = slice(32 * pp + 16 * gi, 32 * pp + 16 * gi + 16)
            lre[rows, pr] = inp["s5_lambda_re"][0, g]
            lim[rows, pr] = inp["s5_lambda_im"][0, g]
            ldt[rows, pr] = inp["s5_log_dt"][0, g]
            xbre[rows, pr, cs] = inp["s5_b_re"][0, g]
            xbim[rows, pr, cs] = inp["s5_b_im"][0, g]
            cre[rows, pr, cs] = inp["s5_c_re"][0, g].T
            cim[rows, pr, cs] = inp["s5_c_im"][0, g].T
    m.update(s5_lre=lre, s5_lim=lim, s5_ldt=ldt, s5_xbre=xbre, s5_xbim=xbim, s5_cre=cre, s5_cim=cim)
    m["s5_dT"] = np.ascontiguousarray(inp["s5_d"][0, j * 512:(j + 1) * 512].reshape(4, 128).T)
    cwT = np.zeros((128, 16, 4), np.float32)
    cbT = np.zeros((128, 16), np.float32)
    dT = np.zeros((128, 8), np.float32)
    cwf, cbf = inp["ssd_conv_w"][0], inp["ssd_conv_b"][0]
    for gg in range(4):
        G = j * 4 + gg
        chans = [G * 256 + np.arange(128), G * 256 + 128 + np.arange(128),
                 2048 + G * 128 + np.arange(128), 3072 + G * 128 + np.arange(128)]
        for ti in range(4):
            cwT[:, gg * 4 + ti, :] = cwf[:, chans[ti]].T
            cbT[:, gg * 4 + ti] = cbf[chans[ti]]
        for pt in range(2):
            heads = (G * 256 + pt * 128 + np.arange(128)) // 64
            dT[:, gg * 2 + pt] = inp["ssd_d"][0][heads]
    hs = slice(16 * j, 16 * j + 16)
    m.update(conv_wT=cwT, conv_bT=cbT, ssd_dT=dT,
             dt_bias_bc=np.ascontiguousarray(np.broadcast_to(inp["ssd_dt_bias"][0, hs], (128, 16))),
             a_log_bc=np.ascontiguousarray(np.broadcast_to(inp["ssd_a_log"][0, hs], (128, 16))))
    tri = np.triu(np.ones((128, 128), np.float32))
    m["tri"] = tri
    m["ones128"] = np.ones((128, 128), np.float32)
    m["maskneg"] = np.where(np.arange(128)[None, :] >= np.arange(128)[:, None], 0.0, -30000.0).astype(np.float32)
    sel = np.zeros((16, 16, 128), np.float32)
    for h_ in range(16):
        sel[h_, h_, :] = 1.0
    m["sel"] = sel.reshape(16, 16 * 128)
    return m


RC = 64
LD_C = 0.6065306597126334
GN_EPS = 64e-5


def build_M1(G=None, io=None):
    nc, cx, dram, banks, bankT = _mk_env(G)
    io = io or {}
    cx.open_scope()

    def din(name, shape, dtype=F32):
        if name in io:
            return io[name]
        if name not in dram:
            dram[name] = nc.dram_tensor(name, list(shape), dtype, kind="ExternalInput").ap()
        return dram[name]

    def dout(name, shape, dtype=F32):
        if name in io:
            return io[name]
        dram[name] = nc.dram_tensor(name, list(shape), dtype, kind="ExternalOutput").ap()
        return dram[name]

    BF16S = "bf16_from_f32"

    def load(name, shape, dtype=F32, q="sp"):
        sdt, ddt = (BF16, F32) if dtype == BF16S else (dtype, dtype)
        t = cx.sb(name, shape, sdt)
        sl = cx.slot(name)
        pairs = cast_pairs(t[:], din(name, shape, ddt)) if dtype == BF16S else [(t[:], din(name, shape, ddt))]
        cx.dma(q, sl, pairs, writes=[name])
        return t

    hb = cx.sb("hbuf", [128, KT, SEQ + 1], BF16)
    cx.op("dve", lambda e: e.memset(hb[:, :, 0:1], 0.0), writes=["hb"])
    sl = cx.slot("hT")
    if "hT_pairs" in io:
        cx.dma("sp", sl, io["hT_pairs"](hb, 1), reads=["hb"] + io.get("dep", []), writes=["hb"])
    else:
        hT_d = din("hT", [128, KT, SEQ], BF16)
        cx.dma("sp", sl, [(hb[:, 4 * i:4 * i + 4, 1:SEQ + 1], hT_d[:, 4 * i:4 * i + 4, :]) for i in range(4)],
               reads=["hb"], writes=["hb"])
    ws = WStream(cx, nslot=2, elems=2048)
    ident = load("ident", [128, 128])
    identb = cx.sb("identb", [128, 128], BF16)
    cx.op("dve", lambda e: e.tensor_copy(out=identb[:], in_=ident[:]), reads=["ident"], writes=["identb"])
    mask3 = load("mask3", [128, 384])
    blockones = load("blockones", [128, 128])
    resetm = load("resetmask", [128, SEQ], BF16S, q="pool")
    muT = load("muT", [128, 6, KT])
    w0T = load("w0T", [128, 8])
    a0T = load("a0T", [128, 8])
    kkT = load("k_kT", [128, 8])
    kaT = load("k_aT", [128, 8])
    rkT = load("r_kT", [128, 8])
    lng = load("lng_stack", [128, 8, 64])
    lnb = load("lnb_stack", [128, 8, 64])
    w2c = load("w2c", [96, 1024], BF16S, q="pool")
    a2c = load("a2c", [96, 1024], BF16S, q="pool")
    g2c = load("g2c", [128, 2, 1024], BF16S, q="pool")
    onesb = cx.sb("onesb", [128, 2], BF16)
    cx.op("dve", lambda e: e.memset(onesb[:], 1.0), writes=["onesb"])
    epsg = cx.sb("epsg", [128, 1], F32)
    cx.op("dve", lambda e: e.memset(epsg[:], GN_EPS), writes=["epsg"])
    st_slots = [cx.slot("st0"), cx.slot("st1")]

    wder = [[cx.sb(f"wd{i}{j}", [128, KT, 128], BF16) for j in range(2)] for i in range(2)]
    nder = [0]

    def derive(blk, mu_i, ncol):
        i = nder[0] % 2
        nder[0] += 1
        wv = ws.view(blk)
        w1_, w2_ = wder[i][0], wder[i][1]
        for kt in range(KT):
            eng = "dve" if kt % 2 == 0 else "pool"
            cx.op(eng, lambda e, kt=kt: e.tensor_scalar(out=w2_[:, kt, 0:ncol], in0=wv[:, kt, :], scalar1=muT[:, mu_i, kt:kt + 1],
                                                        scalar2=None, op0=ALU.mult),
                  reads=[ws.key(blk), "muT"], writes=[f"wd{i}1"])
        cx.op("pool", lambda e: e.tensor_tensor(out=w1_[:, :, 0:ncol], in0=wv[:], in1=w2_[:, :, 0:ncol], op=ALU.subtract),
              reads=[ws.key(blk), f"wd{i}1"], writes=[f"wd{i}0"])
        return w1_, w2_, f"wd{i}0", f"wd{i}1"

    pj = [0]

    def proj2(der, ncol, evac):
        w1_, w2_, k1, k2 = der
        for tb in range(4):
            bk = pj[0] % 2
            pj[0] += 1
            for kt in range(KT):
                cx.op("pe", lambda e, kt=kt: e.matmul(banks[bk][0:ncol, :], w1_[:, kt, 0:ncol], hb[:, kt, 1 + tb * 512:1 + (tb + 1) * 512],
                                                       start=(kt == 0), stop=False),
                      reads=[k1, "hb"], writes=[f"bank{bk}"])
            for kt in range(KT):
                cx.op("pe", lambda e, kt=kt: e.matmul(banks[bk][0:ncol, :], w2_[:, kt, 0:ncol], hb[:, kt, tb * 512:(tb + 1) * 512],
                                                       start=False, stop=(kt == KT - 1)),
                      reads=[k2, "hb"], writes=[f"bank{bk}"])
            evac(tb, banks[bk], f"bank{bk}")

    def wblock(name, shape_cols, c0, ncol):
        src = din(name, [D, shape_cols])
        return ws.add(src[:, c0:c0 + ncol].rearrange("(kt p) c -> p kt c", p=128), KT, ncol)

    tw = cx.sb("tw", [96, SEQ], BF16)
    ta = cx.sb("ta", [96, SEQ], BF16)
    tg = cx.sb("tg", [128, 2, SEQ], BF16)
    b_w1 = wblock("w1", 96, 0, 96)
    b_a1 = wblock("a1", 96, 0, 96)
    b_g1 = [wblock("g1", 256, i * 128, 128) for i in range(2)]
    ws.need(b_w1)
    proj2(derive(b_w1, 1, 96), 96, lambda tb, ps, key: cx.op(
        "act", lambda e: e.activation(out=tw[:, tb * 512:(tb + 1) * 512], in_=ps[0:96, :], func=AF.Tanh), reads=[key], writes=["tw"]))
    ws.need(b_a1)
    proj2(derive(b_a1, 4, 96), 96, lambda tb, ps, key: cx.op(
        "act", lambda e: e.activation(out=ta[:, tb * 512:(tb + 1) * 512], in_=ps[0:96, :], func=AF.Copy), reads=[key], writes=["ta"]))
    for i in range(2):
        ws.need(b_g1[i])
        proj2(derive(b_g1[i], 5, 128), 128, lambda tb, ps, key, i=i: cx.op(
            "act", lambda e: e.activation(out=tg[:, i, tb * 512:(tb + 1) * 512], in_=ps[:], func=AF.Sigmoid), reads=[key], writes=["tg"]))

    r_bf = cx.sb("r_bf", [128, SEQ], BF16)
    k32 = cx.sb("k32", [128, SEQ], F32)
    v_bf = cx.sb("v_bf", [128, SEQ], BF16)
    a32 = cx.sb("a32", [128, SEQ], F32)
    kk32 = cx.sb("kk32", [128, SEQ], F32)
    ld32 = cx.sb("ld32", [128, SEQ], F32)
    cl32 = cx.sb("cl32", [128, SEQ], F32)
    ecl = cx.sb("ecl", [128, SEQ], F32)
    z_bf = cx.sb("z_bf", [128, SEQ], BF16)
    g_bf = cx.sb("g_bf", [128, SEQ], BF16)
    yg = cx.sb("yg", [128, SEQ], BF16)
    sqt = [cx.sb(f"sqt{i}", [128, 512], F32) for i in range(2)]
    Ear = [cx.sb(f"Ear{i}", [128, 256], BF16) for i in range(2)]
    Eb = [cx.sb(f"Eb{i}", [128, 128], BF16) for i in range(2)]
    Ek = [cx.sb(f"Ek{i}", [128, 128], BF16) for i in range(2)]
    Ez = [cx.sb(f"Ez{i}", [128, 128], BF16) for i in range(2)]
    for i in range(2):
        for t_, k_ in ((Ear[i], f"Ear{i}"), (Eb[i], f"Eb{i}"), (Ek[i], f"Ek{i}"), (Ez[i], f"Ez{i}")):
            cx.op("pool", lambda e, t_=t_: e.memset(t_[:], 0.0), writes=[k_])
    EbkT = [cx.sb(f"EbkT{i}", [128, 256], BF16) for i in range(2)]
    Pm = [cx.sb(f"Pm{i}", [128, 128], F32) for i in range(2)]
    PTm = [cx.sb(f"PTm{i}", [128, 128], F32) for i in range(2)]
    Rm = cx.sb("Rm", [128, 128], F32)
    Rb = [cx.sb(f"Rb{i}", [128, 128], BF16) for i in range(2)]
    Arb = [cx.sb(f"Arb{i}", [128, 128], BF16) for i in range(2)]
    Aak_rk = [cx.sb(f"Aakrk{i}", [128, 256], BF16) for i in range(2)]
    Vs = [cx.sb(f"Vs{i}", [128, 64], BF16) for i in range(2)]
    Xb = cx.sb("Xb", [128, 64], BF16)
    Ub = cx.sb("Ub", [128, 64], BF16)
    S32 = cx.sb("S32", [128, 64], F32)
    S0b = cx.sb("S0b", [128, 64], BF16)
    ys = cx.sb("ys", [128, 64], F32)
    ysq = cx.sb("ysq", [128, 64], F32)
    yn = cx.sb("yn", [128, 64], F32)
    yob = cx.sb("yob", [128, 64], BF16)
    stat = cx.sb("stat", [128, 8], F32)
    bon = cx.sb("bon", [128, 2], F32)
    yg_d = None if "yg_tile" in io else dout("ygT", [128, 8, SEQ], BF16)

    for P in range(8):
        c0 = P * 128
        b_r = wblock("wr_c", 1024, c0, 128)
        b_k = wblock("wk_c", 1024, c0, 128)
        b_v = wblock("wv_c", 1024, c0, 128)
        ws.need(b_r)
        proj2(derive(b_r, 0, 128), 128, lambda tb, ps, key: cx.op(
            "act", lambda e: e.activation(out=r_bf[:, tb * 512:(tb + 1) * 512], in_=ps[:], func=AF.Copy), reads=[key], writes=["r_bf"]))
        ws.need(b_k)
        proj2(derive(b_k, 2, 128), 128, lambda tb, ps, key: cx.op(
            "act", lambda e: e.activation(out=k32[:, tb * 512:(tb + 1) * 512], in_=ps[:], func=AF.Copy), reads=[key], writes=["k32"]))
        ws.need(b_v)
        proj2(derive(b_v, 3, 128), 128, lambda tb, ps, key: cx.op(
            "act", lambda e: e.activation(out=v_bf[:, tb * 512:(tb + 1) * 512], in_=ps[:], func=AF.Copy), reads=[key], writes=["v_bf"]))
        for tb in range(4):
            ts_ = slice(tb * 512, (tb + 1) * 512)
            bk = pj[0] % 2
            pj[0] += 1
            cx.op("pe", lambda e: e.matmul(banks[bk][:], w2c[:, c0:c0 + 128], tw[:, ts_], start=True, stop=True),
                  reads=["w2c", "tw"], writes=[f"bank{bk}"])
            cx.op("act", lambda e: e.activation(out=ld32[:, ts_], in_=banks[bk][:], func=AF.Sigmoid, bias=w0T[:, P:P + 1], scale=1.0),
                  reads=[f"bank{bk}", "w0T"], writes=["ld32"])
            bk = pj[0] % 2
            pj[0] += 1
            cx.op("pe", lambda e: e.matmul(banks[bk][:], a2c[:, c0:c0 + 128], ta[:, ts_], start=True, stop=True),
                  reads=["a2c", "ta"], writes=[f"bank{bk}"])
            cx.op("act", lambda e: e.activation(out=a32[:, ts_], in_=banks[bk][:], func=AF.Sigmoid, bias=a0T[:, P:P + 1], scale=1.0),
                  reads=[f"bank{bk}", "a0T"], writes=["a32"])
            bk = pj[0] % 2
            pj[0] += 1
            for i in range(2):
                cx.op("pe", lambda e, i=i: e.matmul(banks[bk][:], g2c[:, i, c0:c0 + 128], tg[:, i, ts_], start=(i == 0), stop=(i == 1)),
                      reads=["g2c", "tg"], writes=[f"bank{bk}"])
            cx.op("act", lambda e: e.activation(out=g_bf[:, ts_], in_=banks[bk][:], func=AF.Copy), reads=[f"bank{bk}"], writes=["g_bf"])
        cx.op("dve", lambda e: e.tensor_scalar(out=ld32[:], in0=ld32[:], scalar1=-LD_C, scalar2=None, op0=ALU.mult),
              reads=["ld32"], writes=["ld32"])
        cx.op("dve", lambda e: e.tensor_scalar(out=kk32[:], in0=k32[:], scalar1=kkT[:, P:P + 1], scalar2=None, op0=ALU.mult),
              reads=["k32", "k_kT"], writes=["kk32"])
        for tb in range(4):
            ts_ = slice(tb * 512, (tb + 1) * 512)
            sq_, sk = sqt[tb % 2], f"sqt{tb % 2}"
            cx.op("act", lambda e: e.activation(out=sq_[:], in_=kk32[:, ts_], func=AF.Square), reads=["kk32"], writes=[sk])
            bk = pj[0] % 2
            pj[0] += 1
            cx.op("pe", lambda e: e.matmul(banks[bk][:], blockones[:], sq_[:], start=True, stop=True),
                  reads=["blockones", sk], writes=[f"bank{bk}"])
            cx.op("act", lambda e: e.activation(out=sq_[:], in_=banks[bk][:], func=AF.Sqrt), reads=[f"bank{bk}"], writes=[sk])
            cx.op("dve", lambda e: e.tensor_scalar(out=sq_[:], in0=sq_[:], scalar1=1e-12, scalar2=None, op0=ALU.max), reads=[sk], writes=[sk])
            cx.op("dve", lambda e: e.reciprocal(out=sq_[:], in_=sq_[:]), reads=[sk], writes=[sk])
            cx.op("dve", lambda e: e.tensor_tensor(out=kk32[:, ts_], in0=kk32[:, ts_], in1=sq_[:], op=ALU.mult),
                  reads=["kk32", sk], writes=["kk32"])
        cx.op("dve", lambda e: e.tensor_scalar(out=ecl[:], in0=a32[:], scalar1=-1.0, scalar2=kaT[:, P:P + 1], op0=ALU.add, op1=ALU.mult),
              reads=["a32", "k_aT"], writes=["ecl"])
        cx.op("dve", lambda e: e.scalar_tensor_tensor(out=k32[:], in0=ecl[:], scalar=1.0, in1=k32[:], op0=ALU.add, op1=ALU.mult),
              reads=["ecl", "k32"], writes=["k32"])
        cx.op("dve", lambda e: e.scalar_tensor_tensor(out=z_bf[:], in0=k32[:], scalar=rkT[:, P:P + 1], in1=r_bf[:], op0=ALU.mult, op1=ALU.mult),
              reads=["k32", "r_kT", "r_bf"], writes=["z_bf"])
        cx.op("pool", lambda e: e.tensor_tensor(out=a32[:], in0=a32[:], in1=kk32[:], op=ALU.mult), reads=["a32", "kk32"], writes=["a32"])
        cx.op("dve", lambda e: e.tensor_tensor_scan(out=cl32[:], data0=resetm[:], data1=ld32[:], initial=0.0, op0=ALU.mult, op1=ALU.add),
              reads=["resetmask", "ld32"], writes=["cl32"])
        cx.op("pool", lambda e: e.tensor_tensor(out=ld32[:], in0=cl32[:], in1=ld32[:], op=ALU.subtract), reads=["cl32", "ld32"], writes=["ld32"])
        cx.op("act", lambda e: e.activation(out=ld32[:], in_=ld32[:], func=AF.Exp), reads=["ld32"], writes=["ld32"])
        cx.op("act", lambda e: e.activation(out=ecl[:], in_=cl32[:], func=AF.Exp), reads=["cl32", "ecl"], writes=["ecl"])
        cx.op("act", lambda e: e.activation(out=cl32[:], in_=cl32[:], func=AF.Exp, scale=-1.0), reads=["cl32"], writes=["cl32"])
        eclm, encl, beta, kfin = ld32, cl32, a32, k32
        cx.op("dve", lambda e: e.memset(S32[:], 0.0), reads=["S32"], writes=["S32"])
        cx.op("dve", lambda e: e.memset(S0b[:], 0.0), reads=["S0b"], writes=["S0b"])
        def part1(c):
            cs = slice(c * RC, (c + 1) * RC)
            par = c % 2
            for hd in range(2):
                R_ = slice(hd * 64, hd * 64 + 64)
                e1, e2 = ("dve", "pool") if hd == 0 else ("pool", "dve")
                cx.op(e1, lambda e: e.scalar_tensor_tensor(out=Ear[par][R_, hd * 64:hd * 64 + 64], in0=kk32[R_, cs], scalar=-1.0, in1=eclm[R_, cs],
                                                           op0=ALU.mult, op1=ALU.mult) if e1 == "dve" else
                      e.tensor_tensor(out=Ear[par][R_, hd * 64:hd * 64 + 64], in0=kk32[R_, cs], in1=eclm[R_, cs], op=ALU.mult),
                      reads=["kk32", "ld32"], writes=[f"Ear{par}"])
                if e1 != "dve":
                    cx.op("pool", lambda e: e.tensor_scalar(out=Ear[par][R_, hd * 64:hd * 64 + 64], in0=Ear[par][R_, hd * 64:hd * 64 + 64],
                                                             scalar1=-1.0, scalar2=None, op0=ALU.mult),
                          reads=[f"Ear{par}"], writes=[f"Ear{par}"])
                cx.op(e2, lambda e: e.tensor_tensor(out=Ear[par][R_, 128 + hd * 64:128 + hd * 64 + 64], in0=r_bf[R_, cs], in1=ecl[R_, cs], op=ALU.mult),
                      reads=["r_bf", "ecl"], writes=[f"Ear{par}"])
                cx.op(e1, lambda e: e.tensor_tensor(out=Eb[par][R_, hd * 64:hd * 64 + 64], in0=beta[R_, cs], in1=encl[R_, cs], op=ALU.mult),
                      reads=["a32", "cl32"], writes=[f"Eb{par}"])
                cx.op(e2, lambda e: e.tensor_tensor(out=Ek[par][R_, hd * 64:hd * 64 + 64], in0=kfin[R_, cs], in1=encl[R_, cs], op=ALU.mult),
                      reads=["k32", "cl32"], writes=[f"Ek{par}"])
                cx.op("act", lambda e: e.activation(out=Ez[par][R_, hd * 64:hd * 64 + 64], in_=z_bf[R_, cs], func=AF.Copy),
                      reads=["z_bf"], writes=[f"Ez{par}"])
                cx.op("pe", lambda e: e.transpose(bankT[R_, 0:64], v_bf[R_, cs], identb[R_, hd * 64:hd * 64 + 64]),
                      reads=["v_bf", "identb"], writes=["bankT"])
            cx.op("act", lambda e: e.activation(out=Vs[par][:], in_=bankT[:, 0:64], func=AF.Copy), reads=["bankT"], writes=[f"Vs{par}"])
            cx.op("pe", lambda e: e.matmul(banks[2][:, 0:256], Eb[par][:], Ear[par][:], start=True, stop=True),
                  reads=[f"Eb{par}", f"Ear{par}"], writes=["bank2"])
            cx.op("pe", lambda e: e.matmul(banks[3][:, 0:256], Ek[par][:], Ear[par][:], start=True, stop=True),
                  reads=[f"Ek{par}", f"Ear{par}"], writes=["bank3"])
            cx.op("pe", lambda e: e.matmul(banks[4][:, 0:128], Ear[par][:, 0:128], Eb[par][:], start=True, stop=True),
                  reads=[f"Eb{par}", f"Ear{par}"], writes=["bank4"])
            cx.op("pe", lambda e: e.transpose(bankT[:, 128:256], Eb[par][:], identb[:]), reads=[f"Eb{par}", "identb"], writes=["bankT"])
            cx.op("pe", lambda e: e.transpose(bankT[:, 256:384], Ek[par][:], identb[:]), reads=[f"Ek{par}", "identb"], writes=["bankT"])
            cx.op("dve", lambda e: e.tensor_tensor(out=Pm[0][:], in0=banks[2][:, 0:128], in1=mask3[:, 0:128], op=ALU.mult),
                  reads=["bank2", "mask3"], writes=["Pm0"])
            cx.op("dve", lambda e: e.tensor_tensor(out=Arb[par][:], in0=banks[2][:, 128:256], in1=mask3[:, 128:256], op=ALU.mult),
                  reads=["bank2", "mask3"], writes=[f"Arb{par}"])
            cx.op("dve", lambda e: e.tensor_tensor(out=Aak_rk[par][:], in0=banks[3][:, 0:256], in1=mask3[:, 0:256], op=ALU.mult),
                  reads=["bank3", "mask3"], writes=[f"Aakrk{par}"])
            cx.op("dve", lambda e: e.tensor_tensor(out=PTm[0][:], in0=banks[4][:, 0:128], in1=mask3[:, 256:384], op=ALU.mult),
                  reads=["bank4", "mask3"], writes=["PTm0"])
            cx.op("act", lambda e: e.activation(out=EbkT[par][:], in_=bankT[:, 128:384], func=AF.Copy), reads=["bankT"], writes=[f"EbkT{par}"])
            cx.op("pool", lambda e: e.tensor_tensor(out=Rm[:], in0=Pm[0][:], in1=ident[:], op=ALU.add), reads=["Pm0", "ident"], writes=["Rm"])
            cur = 0
            for lvl in range(1, 6):
                nxt = 1 - cur
                if lvl < 5:
                    cx.op("pe", lambda e: e.matmul(banks[5][:, 0:128], PTm[cur][:], Pm[cur][:], start=True, stop=True),
                          reads=[f"PTm{cur}", f"Pm{cur}"], writes=["bank5"])
                cx.op("pe", lambda e: e.matmul(banks[6][:, 0:128], Pm[cur][:], PTm[cur][:], start=True, stop=True),
                      reads=[f"PTm{cur}", f"Pm{cur}"], writes=["bank6"])
                if lvl < 5:
                    cx.op("act", lambda e: e.activation(out=Pm[nxt][:], in_=banks[5][:, 0:128], func=AF.Copy),
                          reads=["bank5"], writes=[f"Pm{nxt}"])
                cx.op("dve", lambda e: e.tensor_copy(out=PTm[nxt][:], in_=banks[6][:, 0:128]), reads=["bank6"], writes=[f"PTm{nxt}"])
                cx.op("pe", lambda e: e.matmul(banks[4][:, 0:128], PTm[nxt][:], Rm[:], start=True, stop=True),
                      reads=[f"PTm{nxt}", "Rm"], writes=["bank4"])
                cx.op("dve", lambda e: e.tensor_tensor(out=Rm[:], in0=Rm[:], in1=banks[4][:, 0:128], op=ALU.add),
                      reads=["Rm", "bank4"], writes=["Rm"])
                cur = nxt
            cx.op("act", lambda e: e.activation(out=Rb[par][:], in_=Rm[:], func=AF.Copy), reads=["Rm"], writes=[f"Rb{par}"])

        def part2(c):
            cs = slice(c * RC, (c + 1) * RC)
            par = c % 2
            cx.op("pe", lambda e: e.matmul(banks[0][:, 0:64], Ear[par][:, 0:128], S0b[:], start=True, stop=False),
                  reads=[f"Ear{par}", "S0b"], writes=["bank0"])
            cx.op("pe", lambda e: e.matmul(banks[0][:, 0:64], Aak_rk[par][:, 0:128], Vs[par][:], start=False, stop=True),
                  reads=[f"Aakrk{par}", f"Vs{par}"], writes=["bank0"])
            cx.op("act", lambda e: e.activation(out=Xb[:], in_=banks[0][:, 0:64], func=AF.Copy), reads=["bank0"], writes=["Xb"])
            cx.op("pe", lambda e: e.matmul(banks[1][:, 0:64], Rb[par][:], Xb[:], start=True, stop=True), reads=[f"Rb{par}", "Xb"], writes=["bank1"])
            cx.op("act", lambda e: e.activation(out=Ub[:], in_=banks[1][:, 0:64], func=AF.Copy), reads=["bank1"], writes=["Ub"])
            cx.op("pe", lambda e: e.matmul(banks[0][:, 0:64], Ear[par][:, 128:256], S0b[:], start=True, stop=False),
                  reads=[f"Ear{par}", "S0b"], writes=["bank0"])
            cx.op("pe", lambda e: e.matmul(banks[0][:, 0:64], Arb[par][:], Ub[:], start=False, stop=False), reads=[f"Arb{par}", "Ub"], writes=["bank0"])
            cx.op("pe", lambda e: e.matmul(banks[0][:, 0:64], Aak_rk[par][:, 128:256], Vs[par][:], start=False, stop=True),
                  reads=[f"Aakrk{par}", f"Vs{par}"], writes=["bank0"])
            cx.op("pe", lambda e: e.matmul(banks[0][:, 64:66], Ez[par][:], onesb[:], start=True, stop=True),
                  reads=[f"Ez{par}", "onesb"], writes=["bank0"])
            cx.op("pe", lambda e: e.matmul(banks[1][:, 0:64], EbkT[par][:, 0:128], Ub[:], start=True, stop=False), reads=[f"EbkT{par}", "Ub"], writes=["bank1"])
            cx.op("pe", lambda e: e.matmul(banks[1][:, 0:64], EbkT[par][:, 128:256], Vs[par][:], start=False, stop=True), reads=[f"EbkT{par}", f"Vs{par}"], writes=["bank1"])
            cx.op("dve", lambda e: e.tensor_tensor(out=S32[:], in0=S32[:], in1=banks[1][:, 0:64], op=ALU.add), reads=["S32", "bank1"], writes=["S32"])
            wc = ecl[:, c * RC + RC - 1:c * RC + RC]
            cx.op("dve", lambda e: e.tensor_scalar(out=S32[:], in0=S32[:], scalar1=wc, scalar2=None, op0=ALU.mult),
                  reads=["S32", "ecl"], writes=["S32"])
            cx.op("pool", lambda e: e.tensor_copy(out=S0b[:], in_=S32[:]), reads=["S32"], writes=["S0b"])
            cx.op("act", lambda e: e.activation(out=ys[:], in_=banks[0][:, 0:64], func=AF.Copy, accum_out=stat[:, 0:1]),
                  reads=["bank0"], writes=["ys", "stat"])
            cx.op("act", lambda e: e.activation(out=ysq[:], in_=ys[:], func=AF.Square, accum_out=stat[:, 1:2]),
                  reads=["ys"], writes=["ysq", "stat"])
            cx.op("dve", lambda e: e.tensor_scalar(out=stat[:, 2:3], in0=stat[:, 0:1], scalar1=1.0 / 64, scalar2=None, op0=ALU.mult),
                  reads=["stat"], writes=["stat"])
            cx.op("dve", lambda e: e.tensor_tensor(out=stat[:, 3:4], in0=stat[:, 2:3], in1=stat[:, 2:3], op=ALU.mult),
                  reads=["stat"], writes=["stat"])
            cx.op("dve", lambda e: e.scalar_tensor_tensor(out=stat[:, 4:5], in0=stat[:, 1:2], scalar=1.0 / 64, in1=stat[:, 3:4],
                                                           op0=ALU.mult, op1=ALU.subtract), reads=["stat"], writes=["stat"])
            cx.op("act", lambda e: e.activation(out=stat[:, 5:6], in_=stat[:, 4:5], func=AF.Sqrt, bias=epsg[:], scale=1.0),
                  reads=["stat", "epsg"], writes=["stat"])
            cx.op("dve", lambda e: e.reciprocal(out=stat[:, 5:6], in_=stat[:, 5:6]), reads=["stat"], writes=["stat"])
            cx.op("dve", lambda e: e.tensor_scalar(out=yn[:], in0=ys[:], scalar1=stat[:, 2:3], scalar2=stat[:, 5:6],
                                                    op0=ALU.subtract, op1=ALU.mult), reads=["ys", "stat"], writes=["yn"])
            cx.op("pool", lambda e: e.tensor_tensor(out=yn[:], in0=yn[:], in1=lng[:, P, :], op=ALU.mult), reads=["yn", "lng_stack"], writes=["yn"])
            cx.op("pool", lambda e: e.tensor_tensor(out=yn[:], in0=yn[:], in1=lnb[:, P, :], op=ALU.add), reads=["yn", "lnb_stack"], writes=["yn"])
            cx.op("act", lambda e: e.activation(out=bon[:], in_=banks[0][:, 64:66], func=AF.Copy), reads=["bank0"], writes=["bon"])
            cx.op("dve", lambda e: e.scalar_tensor_tensor(out=yob[:], in0=Vs[par][:], scalar=bon[:, 0:1], in1=yn[:], op0=ALU.mult, op1=ALU.add),
                  reads=[f"Vs{par}", "bon", "yn"], writes=["yob"])
            for hd in range(2):
                R_ = slice(hd * 64, hd * 64 + 64)
                cx.op("pe", lambda e: e.transpose(bankT[R_, 512:576], yob[R_, :], identb[R_, hd * 64:hd * 64 + 64]),
                      reads=["yob", "identb"], writes=["bankT"])
            cx.op("dve", lambda e: e.tensor_tensor(out=yg[:, cs], in0=bankT[:, 512:576], in1=g_bf[:, cs], op=ALU.mult),
                  reads=["bankT", "g_bf"], writes=["yg"])
        NCH_ = SEQ // RC
        for it in cx.record(part1, 0):
            cx.play(it)
        for c in range(NCH_):
            la = cx.record(part2, c)
            lb = cx.record(part1, c + 1) if c + 1 < NCH_ else []
            cx.play_interleaved(la, lb)
        cx.dma("sp", st_slots[P % 2], [(io["yg_tile"](P) if "yg_tile" in io else yg_d[:, P, :], yg[:])], reads=["yg"])
    cx.close_scope()
    cx.wait_all("sp")
    return nc


def prep_M1(inp, b, j, hT_full):
    m = {"hT": hT_full, "ident": np.eye(128, dtype=np.float32)}
    cs = slice(j * 1024, (j + 1) * 1024)
    m["wr_c"] = np.ascontiguousarray(inp["rwkv_w_r"][0][:, cs])
    m["wk_c"] = np.ascontiguousarray(inp["rwkv_w_k"][0][:, cs])
    m["wv_c"] = np.ascontiguousarray(inp["rwkv_w_v"][0][:, cs])
    m["w1"] = inp["rwkv_w1"][0]
    m["a1"] = inp["rwkv_a1"][0]
    m["g1"] = inp["rwkv_g1"][0]
    m["w2c"] = np.ascontiguousarray(inp["rwkv_w2"][0][:, cs])
    m["a2c"] = np.ascontiguousarray(inp["rwkv_a2"][0][:, cs])
    m["g2c"] = np.ascontiguousarray(inp["rwkv_g2"][0][:, cs].reshape(2, 128, 1024).transpose(1, 0, 2))
    m["muT"] = np.ascontiguousarray(inp["rwkv_mu"][0].reshape(6, KT, 128).transpose(2, 0, 1))
    for nm, key in (("w0T", "rwkv_w0"), ("a0T", "rwkv_a0"), ("k_kT", "rwkv_k_k"), ("k_aT", "rwkv_k_a")):
        m[nm] = np.ascontiguousarray(inp[key][0][cs].reshape(8, 128).T)
    m["r_kT"] = np.ascontiguousarray(inp["rwkv_r_k"][0].reshape(-1)[cs].reshape(8, 128).T)
    lg = inp["rwkv_ln_g"][0][cs].reshape(8, 2, 64)
    lb = inp["rwkv_ln_b"][0][cs].reshape(8, 2, 64)
    lng = np.zeros((128, 8, 64), np.float32)
    lnb = np.zeros((128, 8, 64), np.float32)
    for hd in range(2):
        lng[hd * 64:(hd + 1) * 64] = lg[None, :, hd, :]
        lnb[hd * 64:(hd + 1) * 64] = lb[None, :, hd, :]
    m["lng_stack"], m["lnb_stack"] = lng, lnb
    s_ = np.arange(64)
    blk = np.kron(np.eye(2, dtype=np.float32), np.ones((64, 64), np.float32))
    mS = np.kron(np.eye(2, dtype=np.float32), (s_[:, None] < s_[None, :]).astype(np.float32))
    mI = np.kron(np.eye(2, dtype=np.float32), (s_[:, None] <= s_[None, :]).astype(np.float32))
    m["mask3"] = np.ascontiguousarray(np.concatenate([mS, mI, mS.T], axis=1))
    m["blockones"] = blk
    rm = np.ones((128, SEQ), np.float32)
    rm[:, ::RC] = 0.0
    m["resetmask"] = rm
    return m


def _T_maps(inp, stages, xT, extra):
    maps = []
    for core in range(NCORES):
        b = core // 2
        m = {"xT": xT[core], "cT": fm(inp["c"][b])}
        for stg in stages:
            kind = stg["kind"]
            if kind == "ffn":
                l, s_ = stg["l"], stg["s"]
                fi = 0 if s_ == 0 else 1
                m[f"w_mod{l}"] = inp["w_mod"][l]
                m[f"b_modT{l}"] = fm(inp["b_mod"][l])
                m[f"norm_gT{l}{s_}"] = fm(inp["norm_g"][l, s_])
                m[f"ffn_w1_{l}{fi}"] = inp["ffn_w1"][l, fi]
                m[f"ffn_w3_{l}{fi}"] = inp["ffn_w3"][l, fi]
                m[f"ffn_w2_{l}{fi}"] = inp["ffn_w2"][l, fi]
            elif kind == "h_out":
                l = stg["l"]
                m[f"w_mod{l}"] = inp["w_mod"][l]
                m[f"b_modT{l}"] = fm(inp["b_mod"][l])
                m[f"norm_gT{l}1"] = fm(inp["norm_g"][l, 1])
            elif kind == "mix0_post":
                m["w_mod0"] = inp["w_mod"][0]
                m["b_modT0"] = fm(inp["b_mod"][0])
                m["glu_bT"] = fm(inp["s5_glu_b"][0])
                m["ssd_norm_gT"] = fm(inp["ssd_norm_g"][0])
                m["s5_glu_w"] = inp["s5_glu_w"][0]
                m["hyb_w_out"] = inp["hyb_w_out"][0]
            elif kind == "rwkv_post":
                m["w_mod1"] = inp["w_mod"][1]
                m["b_modT1"] = fm(inp["b_mod"][1])
                m["rwkv_w_o"] = inp["rwkv_w_o"][0]
            elif kind == "final":
                m["final_gT"] = fm(inp["final_g"])
        m.update(extra[core])
        maps.append(m)
    return maps


def _run(nc, maps):
    return run_bass_kernel_spmd(nc, maps, core_ids=list(range(NCORES))).results


def _only_declared(maps):
    keep = set(LAST_DRAM.keys())
    return [{k: v for k, v in m.items() if k in keep} for m in maps]


def _pair_cat_tokens(tiles, b):
    return np.ascontiguousarray(np.concatenate([tiles[2 * b], tiles[2 * b + 1]], axis=2))


GROUPS = [[0, 1], [2, 3], [4, 5], [6, 7]]
ST0 = [{"kind": "ffn", "l": 0, "s": 0}, {"kind": "h_out", "l": 0}, {"kind": "x_out"}]
ST1 = [{"kind": "mix0_post"}, {"kind": "ffn", "l": 0, "s": 2}, {"kind": "ffn", "l": 1, "s": 0},
       {"kind": "h_out", "l": 1}, {"kind": "x_out"}]
ST2 = [{"kind": "rwkv_post"}, {"kind": "ffn", "l": 1, "s": 2}, {"kind": "final"}]


def build_fused(upto=None):
    nc = bass.Bass("TRN2", target_bir_lowering=False)
    cx = Ctx(nc)
    banks = [cx.ps(f"bank{i}") for i in range(7)]
    cx.uid += 1
    bankT = nc.alloc_psum_tensor(f"bankT_{cx.uid}", [128, 1024], BF16)
    G = {"nc": nc, "cx": cx, "dram": {}, "banks": banks, "bankT": bankT}

    def idram(name, shape, dt):
        return nc.dram_tensor(name, list(shape), dt).ap()

    ncc = [0]

    def allgather(src, dst):
        sl = cx.slot("cc")
        nc.gpsimd.collective_compute("AllGather", ALU.bypass, replica_groups=GROUPS,
                                     ins=[src.opt()], outs=[dst.opt()]).then_inc(sl["sem"])
        sl["count"] += 1
        ncc[0] += 1
        cx._record((sl["sem"], sl["count"], sl["key"]), [], [f"cc{ncc[0]}"])
        return f"cc{ncc[0]}"

    CH = 4096

    def chunks(name, ncol, dt):
        n = ncol // CH
        return ([idram(f"{name}_s{i}", [128, CH], dt) for i in range(n)],
                [idram(f"{name}_g{i}", [256, CH], dt) for i in range(n)])

    def gather_all(snd, rcv):
        return [allgather(a, b) for a, b in zip(snd, rcv)]

    def h_out_pairs(snd):
        return lambda h: [(snd[c].rearrange("p (k t) -> p k t", k=4), h[:, 4 * c:4 * c + 4, :]) for c in range(4)]

    def h_loader(rcv):
        def f(tile, off):
            pairs = []
            for r in range(2):
                for c in range(4):
                    src = rcv[c][r * 128:(r + 1) * 128, :].rearrange("p (k t) -> p k t", k=4)
                    pairs.append((tile[:, 4 * c:4 * c + 4, off + r * TOK:off + (r + 1) * TOK], src))
            return pairs
        return f

    def tile_fn(snd):
        return lambda idx: snd[idx // 2][:, (idx % 2) * SEQ:(idx % 2 + 1) * SEQ]

    def gath_fn(rcv):
        return lambda r, lt, half: rcv[lt // 2][r * 128:(r + 1) * 128, (lt % 2) * SEQ + half * TOK:(lt % 2) * SEQ + (half + 1) * TOK]

    xs1 = idram("xs1", [128, KT, TOK], F32)
    xs2 = idram("xs2", [128, KT, TOK], F32)
    h0s, h0g = chunks("h0", KT * TOK, BF16)
    h1s, h1g = chunks("h1", KT * TOK, BF16)
    y0s, y0g = chunks("y0", 12 * SEQ, F32)
    y1s, y1g = chunks("y1", 8 * SEQ, BF16)

    def dbg(n, src, shape, dt):
        if upto != n:
            return False
        cx.barrier()
        o = nc.dram_tensor("dbg", list(shape), dt, kind="ExternalOutput").ap()
        cx.dma("sp", cx.slot("dbg"), [(o, src)])
        cx.wait_all("sp")
        return True

    global LAST_DRAM
    LAST_DRAM = G["dram"]
    build_T(ST0, G, io={"hT_out_pairs": h_out_pairs(h0s), "xT_out": xs1})
    if dbg(1, xs1, [128, KT, TOK], F32):
        return nc
    cx.new_phase()
    dep = gather_all(h0s, h0g)
    if dbg(2, h0g[3], [256, CH], BF16):
        return nc
    build_M0(G=G, io={"hT_pairs": h_loader(h0g), "y_tile": tile_fn(y0s), "dep": dep})
    if dbg(3, y0s[0], [128, CH], F32):
        return nc
    cx.new_phase()
    dep = gather_all(y0s, y0g)
    if dbg(4, y0g[5], [256, CH], F32):
        return nc
    build_T(ST1, G, io={"xT": xs1, "y_gath": gath_fn(y0g), "hT_out_pairs": h_out_pairs(h1s), "xT_out": xs2, "dep": dep})
    if dbg(5, xs2, [128, KT, TOK], F32):
        return nc
    cx.new_phase()
    dep = gather_all(h1s, h1g)
    build_M1(G=G, io={"hT_pairs": h_loader(h1g), "yg_tile": tile_fn(y1s), "dep": dep})
    if dbg(6, y1s[0], [128, CH], BF16):
        return nc
    cx.new_phase()
    dep = gather_all(y1s, y1g)
    build_T(ST2, G, io={"xT": xs2, "yg_gath": gath_fn(y1g), "dep": dep})
    cx.wait_all("sp")
    return nc


LAST_DRAM = {}


def kernel_unfused(**inputs):
    inp = {k: np.asarray(v) for k, v in inputs.items()}
    xT = to_xT(inp["x"].astype(np.float32, copy=False))
    st0 = [{"kind": "ffn", "l": 0, "s": 0}, {"kind": "h_out", "l": 0}, {"kind": "x_out"}]
    r = _run(build_T(st0), _T_maps(inp, st0, xT, [{}] * NCORES))
    xT = [np.asarray(q["xT_out"]) for q in r]
    hT = [np.asarray(q["hT_out"]) for q in r]
    maps = [prep_M0(inp, c // 2, c % 2, _pair_cat_tokens(hT, c // 2)) for c in range(NCORES)]
    r = _run(build_M0(), maps)
    y5 = [np.asarray(q["y5T"]) for q in r]
    ys = [np.asarray(q["ysT"]) for q in r]
    extra = []
    for c in range(NCORES):
        b, jt = c // 2, c % 2
        ts = slice(jt * TOK, (jt + 1) * TOK)
        extra.append({"y5T_in": np.ascontiguousarray(np.concatenate([y5[2 * b][:, :, ts], y5[2 * b + 1][:, :, ts]], axis=1)),
                      "ysT_in": np.ascontiguousarray(np.concatenate([ys[2 * b][:, :, ts], ys[2 * b + 1][:, :, ts]], axis=1))})
    st1 = [{"kind": "mix0_post"}, {"kind": "ffn", "l": 0, "s": 2}, {"kind": "ffn", "l": 1, "s": 0},
           {"kind": "h_out", "l": 1}, {"kind": "x_out"}]
    r = _run(build_T(st1), _T_maps(inp, st1, xT, extra))
    xT = [np.asarray(q["xT_out"]) for q in r]
    hT = [np.asarray(q["hT_out"]) for q in r]
    maps = [prep_M1(inp, c // 2, c % 2, _pair_cat_tokens(hT, c // 2)) for c in range(NCORES)]
    r = _run(build_M1(), maps)
    yg = [np.asarray(q["ygT"]) for q in r]
    extra = []
    for c in range(NCORES):
        b, jt = c // 2, c % 2
        ts = slice(jt * TOK, (jt + 1) * TOK)
        extra.append({"ygT_in": np.ascontiguousarray(np.concatenate([yg[2 * b][:, :, ts], yg[2 * b + 1][:, :, ts]], axis=1))})
    st2 = [{"kind": "rwkv_post"}, {"kind": "ffn", "l": 1, "s": 2}, {"kind": "final"}]
    r = _run(build_T(st2), _T_maps(inp, st2, xT, extra))
    out = from_xT([np.asarray(q["xT_out"]) for q in r])
    return out.astype(np.float32)


def fused_maps(inp):
    xT = to_xT(inp["x"].astype(np.float32, copy=False))
    maps = []
    for c in range(NCORES):
        b, j = c // 2, c % 2
        m = {}
        for st in (ST0, ST1, ST2):
            m.update(_T_maps(inp, st, xT, [{}] * NCORES)[c])
        m0 = prep_M0(inp, b, j, None)
        m1 = prep_M1(inp, b, j, None)
        m0.pop("hT")
        m1.pop("hT")
        m.update(m0)
        m.update(m1)
        sel = np.zeros((128, 2), np.float32)
        sel[:, j] = 1.0
        m["selT"] = sel
        maps.append(m)
    return maps


def kernel(**inputs):
    inp = {k: np.asarray(v) for k, v in inputs.items()}
    nc = build_fused()
    r = _run(nc, _only_declared(fused_maps(inp)))
    out = from_xT([np.asarray(q["xT_out"]) for q in r])
    return out.astype(np.float32)
```

```python
import numpy as np
import concourse.bass as bass
import concourse.mybir as mybir
from concourse.bass_utils import run_bass_kernel_spmd

F32 = mybir.dt.float32
BF16 = mybir.dt.bfloat16
AF = mybir.ActivationFunctionType
ALU = mybir.AluOpType

D = 2048
KT = 16
FFN = 5632
NCORES = 8
TOK = 1024
SEQ = 2048
EPS = 1e-6


SKIP_SELF = {"pe"}


class _Eng:
    def __init__(self, name, obj, sem):
        self.name, self.obj, self.sem = name, obj, sem
        self.count = 0
        self.seen = {}


class _Rec:
    def __getattr__(self, name):
        def f(*a, **k):
            self.call = (name, a, k)
            return self
        return f


class Ctx:
    def record(self, fn, *args):
        self.rec = []
        fn(*args)
        lst, self.rec = self.rec, None
        return lst

    def play(self, item):
        engname, (name, a, k), reads, writes = item
        return self.op(engname, lambda e: getattr(e, name)(*a, **k), reads, writes)

    def play_interleaved(self, la, lb):
        i = j = 0
        na, nb = len(la), len(lb)
        while i < na or j < nb:
            if i < na and (j >= nb or i * nb <= j * na):
                self.play(la[i])
                i += 1
            else:
                self.play(lb[j])
                j += 1

    def play_interleaved3(self, la, lb, lc):
        lists = [l for l in (la, lb, lc) if l]
        pos = [0] * len(lists)
        while any(p < len(l) for p, l in zip(pos, lists)):
            k = min((i for i in range(len(lists)) if pos[i] < len(lists[i])), key=lambda i: pos[i] / len(lists[i]))
            self.play(lists[k][pos[k]])
            pos[k] += 1

    def __init__(self, nc):
        self.nc = nc
        self.engs = {}
        for name, attr in (("pe", "tensor"), ("act", "scalar"), ("dve", "vector"),
                           ("pool", "gpsimd"), ("sp", "sync")):
            self.engs[name] = _Eng(name, getattr(nc, attr), nc.alloc_semaphore("sem_" + name))
        self.res = {}
        self.nslots = 0
        self.uid = 0
        self.stacks = []
        self.free_slots = []
        self.phase = 0
        self.rec = None

    def sb(self, name, shape, dtype=F32):
        self.uid += 1
        if self.stacks:
            return self.stacks[-1][0].enter_context(self.nc.sbuf_tensor(f"{name}_{self.uid}", list(shape), dtype))
        return self.nc.alloc_sbuf_tensor(f"{name}_{self.uid}", list(shape), dtype)

    def open_scope(self):
        import contextlib
        self.stacks.append((contextlib.ExitStack(), []))

    def close_scope(self):
        self.barrier()
        st, slots = self.stacks.pop()
        st.close()
        self.free_slots.extend(slots)

    def ps(self, name):
        self.uid += 1
        return self.nc.alloc_psum_tensor(f"{name}_{self.uid}", [128, 512], F32)

    def slot(self, name):
        if self.free_slots:
            sl = self.free_slots.pop()
        else:
            self.nslots += 1
            sl = {"sem": self.nc.alloc_semaphore(f"dsem_{name}_{self.nslots}"), "count": 0,
                  "key": f"slot{self.nslots}"}
        if self.stacks:
            self.stacks[-1][1].append(sl)
        return sl

    def _deps(self, reads, writes):
        deps = []
        for r in reads:
            st = self.res.get(r)
            if st and st["w"]:
                deps.append(st["w"])
            if st and r.startswith("bank"):
                deps.extend(st["r"].values())
        for w in writes:
            st = self.res.get(w)
            if st:
                if st["w"]:
                    deps.append(st["w"])
                deps.extend(st["r"].values())
        return deps

    def _wait(self, eng, deps, skip_self):
        for sem, val, key in deps:
            if skip_self and key == eng.name:
                continue
            if eng.seen.get(key, 0) < val:
                eng.obj.wait_ge(sem, val)
                eng.seen[key] = val

    def _record(self, tok, reads, writes):
        for r in reads:
            st = self.res.setdefault(r, {"w": None, "r": {}})
            st["r"][tok[2]] = tok
        for w in writes:
            self.res[w] = {"w": tok, "r": {}}

    def op(self, engname, emit, reads=(), writes=()):
        if self.rec is not None:
            r = _Rec()
            emit(r)
            self.rec.append((engname, r.call, tuple(reads), tuple(writes)))
            return None
        eng = self.engs[engname]
        self._wait(eng, self._deps(reads, writes), skip_self=(engname in SKIP_SELF))
        inst = emit(eng.obj)
        eng.count += 1
        inst.then_inc(eng.sem, 1)
        tok = (eng.sem, eng.count, engname)
        eng.seen[engname] = max(eng.seen.get(engname, 0), 0)
        self._record(tok, reads, writes)
        return tok

    def dma(self, qname, slot, pairs, reads=(), writes=(), **kw):
        eng = self.engs[qname]
        self._wait(eng, self._deps(reads, writes), skip_self=False)
        for out, in_ in pairs:
            eng.obj.dma_start(out=out, in_=in_, **kw).then_inc(slot["sem"], 16)
            slot["count"] += 16
        tok = (slot["sem"], slot["count"], slot["key"])
        self._record(tok, reads, writes)
        return tok

    def wait_all(self, engname):
        eng = self.engs[engname]
        deps = []
        for e in self.engs.values():
            if e.count:
                deps.append((e.sem, e.count, e.name))
        for st in self.res.values():
            if st["w"]:
                deps.append(st["w"])
            deps.extend(st["r"].values())
        self._wait(eng, deps, skip_self=False)

    def barrier(self):
        for n in self.engs:
            self.wait_all(n)

    def new_phase(self):
        self.barrier()
        self.phase += 1
        for e in self.engs.values():
            e.sem = self.nc.alloc_semaphore(f"sem_{e.name}_p{self.phase}")
            e.count = 0
            e.seen = {}
        self.res = {}


class WStream:
    NSLOT = 4
    ELEMS = 8192

    def __init__(self, cx, nslot=4, elems=8192):
        self.cx = cx
        self.NSLOT, self.ELEMS = nslot, elems
        self.tiles = [cx.sb(f"wslot{i}", [128, self.ELEMS], BF16) for i in range(self.NSLOT)]
        self.slots = [cx.slot(f"w{i}") for i in range(self.NSLOT)]
        self.plan = []
        self.issued = 0

    def add(self, src_ap, a, b):
        assert a * b <= self.ELEMS
        self.plan.append((src_ap, a, b))
        return len(self.plan) - 1

    def view(self, i):
        _, a, b = self.plan[i]
        t = self.tiles[i % self.NSLOT]
        return t[:, 0:a * b].rearrange("p (a b) -> p a b", a=a)

    def key(self, i):
        return f"wslot{i % self.NSLOT}"

    def _issue(self, i):
        src, a, b = self.plan[i]
        v = self.view(i)
        pairs = [(v[:, :, c0:min(b, c0 + 1024)], src[:, :, c0:min(b, c0 + 1024)]) for c0 in range(0, b, 1024)]
        self.cx.dma("pool", self.slots[i % self.NSLOT], pairs, writes=[self.key(i)])

    def need(self, i):
        upto = min(len(self.plan), i + self.NSLOT)
        while self.issued < upto:
            self._issue(self.issued)
            self.issued += 1


def _mk_env(G):
    if G is not None:
        return G["nc"], G["cx"], G["dram"], G["banks"], G["bankT"]
    nc = bass.Bass("TRN2", target_bir_lowering=False)
    cx = Ctx(nc)
    banks = [cx.ps(f"bank{i}") for i in range(7)]
    cx.uid += 1
    bankT = nc.alloc_psum_tensor(f"bankT_{cx.uid}", [128, 1024], BF16)
    return nc, cx, {}, banks, bankT


def build_T(stages, G=None, io=None):
    nc, cx, dram, banks, bankT = _mk_env(G)
    io = io or {}
    cx.open_scope()

    def din(name, shape, dtype=F32):
        if name in io:
            return io[name]
        if name not in dram:
            dram[name] = nc.dram_tensor(name, list(shape), dtype, kind="ExternalInput").ap()
        return dram[name]

    def dout(name, shape, dtype=F32):
        if name in io:
            return io[name]
        dram[name] = nc.dram_tensor(name, list(shape), dtype, kind="ExternalOutput").ap()
        return dram[name]

    xT_d = din("xT", [128, KT, TOK])
    cT_d = din("cT", [128, KT])

    x = cx.sb("x", [128, KT, TOK], F32)
    sq = [cx.sb(f"sq{i}", [128, TOK], F32) for i in range(2)]
    rstd = cx.sb("rstd", [128, TOK], F32)
    cact = cx.sb("cact", [128, KT], BF16)
    cin = cx.sb("cin", [128, KT], F32)
    ones = cx.sb("ones", [128, 128], F32)
    ws = WStream(cx)
    ld = cx.slot("ld")
    ld2 = cx.slot("ld2")

    cx.op("dve", lambda e: e.memset(ones[:], 1.0), writes=["ones"])
    cx.dma("sp", ld, [(x[:, 0:KT // 2, :], xT_d[:, 0:KT // 2, :]), (x[:, KT // 2:KT, :], xT_d[:, KT // 2:KT, :])],
           writes=["x"])
    cx.dma("sp", ld2, [(cin[:], cT_d)], writes=["cin"])
    cx.op("act", lambda e: e.activation(out=cact[:], in_=cin[:], func=AF.Silu),
          reads=["cin"], writes=["cact"])

    small_id = [0]

    def load_small(name, shape):
        small_id[0] += 1
        t = cx.sb(name, shape, F32)
        sl = cx.slot(name)
        cx.dma("sp", sl, [(t[:], din(name, shape))], writes=[name + str(small_id[0])])
        return t, name + str(small_id[0])

    def compute_mod(l, s, which):
        wmod = din(f"w_mod{l}", [D, 9 * D])
        bmod, bkey = load_small(f"b_modT{l}", [128, 9 * KT])
        out = cx.sb(f"mod{l}{s}", [128, 3, KT], F32)
        okey = f"mod{l}{s}"
        blocks = []
        for j in which:
            for cb in range(D // 512):
                c0 = s * 3 * D + j * D + cb * 512
                src = wmod[:, c0:c0 + 512].rearrange("(kt p) c -> p kt c", p=128)
                blocks.append((j, cb, ws.add(src, KT, 512)))
        bank = banks[6]
        for (j, cb, bid) in blocks:
            ws.need(bid)
            wv = ws.view(bid)
            for ft in range(4):
                for kt in range(KT):
                    cx.op("pe", lambda e, ft=ft, kt=kt, wv=wv: e.matmul(
                        bank[:, ft:ft + 1], wv[:, kt, ft * 128:(ft + 1) * 128], cact[:, kt:kt + 1],
                        start=(kt == 0), stop=(kt == KT - 1)),
                        reads=[ws.key(bid), "cact"], writes=["bank6"])
            jj = s * 3 + j
            col = jj * KT + cb * 4
            cx.op("dve", lambda e, j=j, cb=cb, col=col: e.tensor_tensor(
                out=out[:, j, cb * 4:cb * 4 + 4], in0=bank[:, 0:4], in1=bmod[:, col:col + 4], op=ALU.add),
                reads=["bank6", bkey], writes=[okey])
        return out, okey

    def rms_stats(xkey="x"):
        for kt in range(KT):
            s_ = sq[kt % 2]
            cx.op("act", lambda e, kt=kt, s_=s_: e.activation(out=s_[:], in_=x[:, kt, :], func=AF.Square),
                  reads=[xkey], writes=[f"sq{kt % 2}"])
            for t in range(2):
                cx.op("pe", lambda e, kt=kt, t=t, s_=s_: e.matmul(
                    banks[4 + t][:], ones[:], s_[:, t * 512:(t + 1) * 512],
                    start=(kt == 0), stop=(kt == KT - 1)),
                    reads=[f"sq{kt % 2}", "ones"], writes=[f"bank{4 + t}"])
        for t in range(2):
            cx.op("act", lambda e, t=t: e.activation(out=rstd[:, t * 512:(t + 1) * 512], in_=banks[4 + t][:],
                                                      func=AF.Sqrt, scale=1.0 / D, bias=epsb[:]),
                  reads=[f"bank{4 + t}", "epsb"], writes=["rstd"])
        cx.op("dve", lambda e: e.reciprocal(out=rstd[:], in_=rstd[:]), reads=["rstd"], writes=["rstd"])

    epsb = cx.sb("epsb", [128, 1], F32)
    cx.op("dve", lambda e: e.memset(epsb[:], EPS), writes=["epsb"])

    def adaln(l, s, mod, mkey, dst, dkey):
        ng, ngkey = load_small(f"norm_gT{l}{s}", [128, KT])
        a = cx.sb(f"a{l}{s}", [128, KT], F32)
        akey = f"a{l}{s}"
        cx.op("dve", lambda e: e.scalar_tensor_tensor(out=a[:], in0=mod[:, 1, :], scalar=1.0, in1=ng[:],
                                                      op0=ALU.add, op1=ALU.mult),
              reads=[mkey, ngkey], writes=[akey])
        rms_stats()
        for kt in range(KT):
            s_ = sq[kt % 2]
            cx.op("dve", lambda e, kt=kt, s_=s_: e.scalar_tensor_tensor(
                out=s_[:], in0=x[:, kt, :], scalar=a[:, kt:kt + 1], in1=rstd[:],
                op0=ALU.mult, op1=ALU.mult),
                reads=["x", akey, "rstd"], writes=[f"sq{kt % 2}"])
            cx.op("act", lambda e, kt=kt, s_=s_: e.activation(
                out=dst[:, kt, :], in_=s_[:], func=AF.Identity, bias=mod[:, 0, kt:kt + 1], scale=1.0),
                reads=[f"sq{kt % 2}", mkey], writes=[dkey])

    def ffn(l, s):
        fi = 0 if s == 0 else 1
        w1 = din(f"ffn_w1_{l}{fi}", [D, FFN])
        w3 = din(f"ffn_w3_{l}{fi}", [D, FFN])
        w2 = din(f"ffn_w2_{l}{fi}", [FFN, D])
        cx.open_scope()
        h = cx.sb("h", [128, KT, TOK], BF16)
        g = [cx.sb(f"g{i}", [128, 4, TOK], BF16) for i in range(2)]
        silu_t = [cx.sb(f"silu{i}", [128, 512], F32) for i in range(2)]
        mod, mkey = compute_mod(l, s, (0, 1, 2))
        adaln(l, s, mod, mkey, h, "h")
        hg = cx.sb(f"hg{l}{s}", [128, KT], F32)
        cx.op("dve", lambda e: e.tensor_scalar(out=hg[:], in0=mod[:, 2, :], scalar1=0.5, scalar2=None,
                                               op0=ALU.mult),
              reads=[mkey], writes=[f"hg{l}{s}"])
        NCH = FFN // 512
        blk = []
        for c in range(NCH):
            b1 = ws.add(w1[:, c * 512:(c + 1) * 512].rearrange("(kt p) c -> p kt c", p=128), KT, 512)
            b3 = ws.add(w3[:, c * 512:(c + 1) * 512].rearrange("(kt p) c -> p kt c", p=128), KT, 512)
            b2 = ws.add(w2[c * 512:(c + 1) * 512, :].rearrange("(kt p) c -> p kt c", p=128), 4, D)
            blk.append((b1, b3, b2))
        ev = 0
        for c in range(NCH):
            b1, b3, b2 = blk[c]
            gb = g[c % 2]
            gkey = f"g{c % 2}"
            ws.need(b1)
            w1v, w3v = ws.view(b1), ws.view(b3)
            for m in range(4):
                for t in range(2):
                    pa, pb = banks[t * 2], banks[t * 2 + 1]
                    ka, kb = f"bank{t * 2}", f"bank{t * 2 + 1}"
                    for kt in range(KT):
                        cx.op("pe", lambda e, kt=kt, m=m, t=t, pa=pa: e.matmul(
                            pa[:], w1v[:, kt, m * 128:(m + 1) * 128], h[:, kt, t * 512:(t + 1) * 512],
                            start=(kt == 0), stop=(kt == KT - 1)),
                            reads=[ws.key(b1), "h"], writes=[ka])
                    for kt in range(KT):
                        cx.op("pe", lambda e, kt=kt, m=m, t=t, pb=pb: e.matmul(
                            pb[:], w3v[:, kt, m * 128:(m + 1) * 128], h[:, kt, t * 512:(t + 1) * 512],
                            start=(kt == 0), stop=(kt == KT - 1)),
                            reads=[ws.key(b3), "h"], writes=[kb])
                    st_ = silu_t[ev % 2]
                    skey = f"silu{ev % 2}"
                    ev += 1
                    cx.op("act", lambda e, pa=pa, st_=st_: e.activation(out=st_[:], in_=pa[:], func=AF.Silu),
                          reads=[ka], writes=[skey])
                    cx.op("dve", lambda e, pb=pb, st_=st_, m=m, t=t, gb=gb: e.tensor_tensor(
                        out=gb[:, m, t * 512:(t + 1) * 512], in0=st_[:], in1=pb[:], op=ALU.mult),
                        reads=[skey, kb], writes=[gkey])
            ws.need(b2)
            w2v = ws.view(b2)
            for j in range(KT):
                for t in range(2):
                    bi = 4 + ((j * 2 + t) % 3)
                    po, ko = banks[bi], f"bank{bi}"
                    for m in range(4):
                        cx.op("pe", lambda e, m=m, j=j, t=t, po=po, gb=gb: e.matmul(
                            po[:], w2v[:, m, j * 128:(j + 1) * 128], gb[:, m, t * 512:(t + 1) * 512],
                            start=(m == 0), stop=(m == 3)),
                            reads=[ws.key(b2), gkey], writes=[ko])
                    cx.op("dve", lambda e, j=j, t=t, po=po: e.scalar_tensor_tensor(
                        out=x[:, j, t * 512:(t + 1) * 512], in0=po[:], scalar=hg[:, j:j + 1],
                        in1=x[:, j, t * 512:(t + 1) * 512], op0=ALU.mult, op1=ALU.add),
                        reads=[ko, f"hg{l}{s}"], writes=["x"])
        cx.close_scope()

    def outproj(wname, krows, src, skey, nkt, gate, gkey):
        wd = din(wname, [krows, D])
        blks = [ws.add(wd[:, j * 128:(j + 1) * 128].rearrange("(kt p) c -> p kt c", p=128), nkt, 128) for j in range(KT)]
        for j in range(KT):
            ws.need(blks[j])
            wv = ws.view(blks[j])
            for t in range(2):
                bi = 4 + ((j * 2 + t) % 3)
                po, ko = banks[bi], f"bank{bi}"
                for kt in range(nkt):
                    cx.op("pe", lambda e, kt=kt: e.matmul(po[:], wv[:, kt, :], src[:, kt, t * 512:(t + 1) * 512],
                                                           start=(kt == 0), stop=(kt == nkt - 1)),
                          reads=[ws.key(blks[j]), skey], writes=[ko])
                cx.op("dve", lambda e: e.scalar_tensor_tensor(
                    out=x[:, j, t * 512:(t + 1) * 512], in0=po[:], scalar=gate[:, j:j + 1],
                    in1=x[:, j, t * 512:(t + 1) * 512], op0=ALU.mult, op1=ALU.add),
                    reads=[ko, gkey], writes=["x"])

    def gath_select(gath, ntile, dests, gdt):
        selT, selk = load_small("selT", [128, 2])
        if gdt == F32:
            stA, kA = sq, ["sq0", "sq1"]
        else:
            stA, kA = [cx.sb(f"gsA{i}", [128, TOK], gdt) for i in range(2)], ["gsA0", "gsA1"]
        stB = [cx.sb(f"gsB{i}", [128, TOK], gdt) for i in range(2)]
        sls = [cx.slot(f"gs{i}") for i in range(2)]
        n = 0
        for dst, dkey, tiles in dests:
            for di, (r, lt) in enumerate(tiles):
                b_ = n % 2
                n += 1
                cx.dma("sp", sls[b_], [(stA[b_][:], gath(r, lt, 0)), (stB[b_][:], gath(r, lt, 1))],
                       reads=io.get("dep", []), writes=[kA[b_], f"gsB{b_}"])
                cx.op("dve", lambda e: e.tensor_scalar(out=stB[b_][:], in0=stB[b_][:], scalar1=selT[:, 1:2], scalar2=None, op0=ALU.mult),
                      reads=[f"gsB{b_}", selk], writes=[f"gsB{b_}"])
                cx.op("dve", lambda e: e.scalar_tensor_tensor(out=dst[:, di, :], in0=stA[b_][:], scalar=selT[:, 0:1], in1=stB[b_][:],
                                                               op0=ALU.mult, op1=ALU.add),
                      reads=[kA[b_], f"gsB{b_}", selk], writes=[dkey])

    def mix0_post():
        cx.open_scope()
        mod, mkey = compute_mod(0, 1, (2,))
        y5b = cx.sb("y5b", [128, 8, TOK], BF16)
        ysb = cx.sb("ysb", [128, 16, TOK], BF16)
        sig = cx.sb("sig", [128, 8, 512], BF16)
        sl5, sls = cx.slot("y5in"), cx.slot("ysin")
        if "y_gath" not in io:
            y5_d = din("y5T_in", [128, 8, TOK])
            ys_d = din("ysT_in", [128, 16, TOK])
        if "y_gath" in io:
            gath_select(io["y_gath"], 12, [(y5b, "y5b", [(ft // 4, ft % 4) for ft in range(8)]),
                                           (ysb, "ysb", [(kt // 8, 4 + kt % 8) for kt in range(16)])], F32)
        else:
            cx.dma("pool", sl5, cast_pairs(y5b[:], y5_d), writes=["y5b"])
            cx.dma("pool", sls, cast_pairs(ysb[:], ys_d), writes=["ysb"])
        glub, gbk = load_small("glu_bT", [128, 8])
        sng, sgk = load_small("ssd_norm_gT", [128, 16])
        gw = din("s5_glu_w", [1024, 1024])
        blks = [ws.add(gw[:, j * 128:(j + 1) * 128].rearrange("(kt p) c -> p kt c", p=128), 8, 128) for j in range(8)]
        for t in range(2):
            for j in range(8):
                ws.need(blks[j])
                bi = j % 4
                po, ko = banks[bi], f"bank{bi}"
                wv = ws.view(blks[j])
                for kt in range(8):
                    cx.op("pe", lambda e, kt=kt: e.matmul(po[:], wv[:, kt, :], y5b[:, kt, t * 512:(t + 1) * 512],
                                                           start=(kt == 0), stop=(kt == 7)),
                          reads=[ws.key(blks[j]), "y5b"], writes=[ko])
                cx.op("act", lambda e: e.activation(out=sig[:, j, :], in_=po[:], func=AF.Sigmoid, bias=glub[:, j:j + 1], scale=1.0),
                      reads=[ko, gbk], writes=["sig"])
            if t == 0:
                blks = [ws.add(gw[:, j * 128:(j + 1) * 128].rearrange("(kt p) c -> p kt c", p=128), 8, 128) for j in range(8)]
            for j in range(8):
                cx.op("dve", lambda e: e.tensor_tensor(out=y5b[:, j, t * 512:(t + 1) * 512], in0=y5b[:, j, t * 512:(t + 1) * 512],
                                                        in1=sig[:, j, :], op=ALU.mult), reads=["y5b", "sig"], writes=["y5b"])
        for kt in range(KT):
            s_ = sq[kt % 2]
            cx.op("act", lambda e: e.activation(out=s_[:], in_=ysb[:, kt, :], func=AF.Square), reads=["ysb"], writes=[f"sq{kt % 2}"])
            for t in range(2):
                cx.op("pe", lambda e: e.matmul(banks[4 + t][:], ones[:], s_[:, t * 512:(t + 1) * 512], start=(kt == 0), stop=(kt == KT - 1)),
                      reads=[f"sq{kt % 2}", "ones"], writes=[f"bank{4 + t}"])
        for t in range(2):
            cx.op("act", lambda e: e.activation(out=rstd[:, t * 512:(t + 1) * 512], in_=banks[4 + t][:], func=AF.Sqrt, scale=1.0 / D, bias=epsb[:]),
                  reads=[f"bank{4 + t}", "epsb"], writes=["rstd"])
        cx.op("dve", lambda e: e.reciprocal(out=rstd[:], in_=rstd[:]), reads=["rstd"], writes=["rstd"])
        for kt in range(KT):
            cx.op("dve", lambda e: e.scalar_tensor_tensor(out=ysb[:, kt, :], in0=ysb[:, kt, :], scalar=sng[:, kt:kt + 1], in1=rstd[:],
                                                           op0=ALU.mult, op1=ALU.mult), reads=["ysb", sgk, "rstd"], writes=["ysb"])
        wd = din("hyb_w_out", [3072, D])
        blk5 = [ws.add(wd[0:1024, j * 128:(j + 1) * 128].rearrange("(kt p) c -> p kt c", p=128), 8, 128) for j in range(KT)]
        gate = mod[:, 2, :]
        for j in range(KT):
            blks_ = ws.add(wd[1024:3072, j * 128:(j + 1) * 128].rearrange("(kt p) c -> p kt c", p=128), 16, 128)
            blk5[j] = (blk5[j], blks_)
        for part in range(2):
            for j in range(KT):
                b_ = blk5[j][part]
                ws.need(b_)
                wv = ws.view(b_)
                nk = 8 if part == 0 else 16
                srcb, skey = (y5b, "y5b") if part == 0 else (ysb, "ysb")
                for t in range(2):
                    bi = 4 + ((j * 2 + t) % 3)
                    po, ko = banks[bi], f"bank{bi}"
                    for kt in range(nk):
                        cx.op("pe", lambda e, kt=kt: e.matmul(po[:], wv[:, kt, :], srcb[:, kt, t * 512:(t + 1) * 512],
                                                               start=(kt == 0), stop=(kt == nk - 1)),
                              reads=[ws.key(b_), skey], writes=[ko])
                    cx.op("dve", lambda e: e.scalar_tensor_tensor(
                        out=x[:, j, t * 512:(t + 1) * 512], in0=po[:], scalar=gate[:, j:j + 1],
                        in1=x[:, j, t * 512:(t + 1) * 512], op0=ALU.mult, op1=ALU.add),
                        reads=[ko, mkey], writes=["x"])
        cx.close_scope()

    def rwkv_post():
        cx.open_scope()
        mod, mkey = compute_mod(1, 1, (2,))
        ygb = cx.sb("ygb", [128, 16, TOK], BF16)
        slg = cx.slot("ygin")
        if "yg_gath" in io:
            gath_select(io["yg_gath"], 8, [(ygb, "ygb", [(kt // 8, kt % 8) for kt in range(16)])], BF16)
        else:
            yg_d = din("ygT_in", [128, 16, TOK], BF16)
            cx.dma("sp", slg, [(ygb[:, 4 * i:4 * i + 4, :], yg_d[:, 4 * i:4 * i + 4, :]) for i in range(4)], writes=["ygb"])
        outproj("rwkv_w_o", D, ygb, "ygb", KT, mod[:, 2, :], mkey)
        cx.close_scope()

    st_slot = cx.slot("st")
    for stg in stages:
        kind = stg["kind"]
        if kind == "ffn":
            ffn(stg["l"], stg["s"])
        elif kind == "mix0_post":
            mix0_post()
        elif kind == "rwkv_post":
            rwkv_post()
        elif kind == "h_out":
            l = stg["l"]
            cx.open_scope()
            h = cx.sb("h", [128, KT, TOK], BF16)
            mod, mkey = compute_mod(l, 1, (0, 1))
            adaln(l, 1, mod, mkey, h, "h")
            if "hT_out_pairs" in io:
                cx.dma("sp", st_slot, io["hT_out_pairs"](h), reads=["h"])
            else:
                hout = dout("hT_out", [128, KT, TOK], BF16)
                cx.dma("sp", st_slot, [(hout[:, 0:KT // 2, :], h[:, 0:KT // 2, :]), (hout[:, KT // 2:KT, :], h[:, KT // 2:KT, :])],
                       reads=["h"])
            cx.close_scope()
        elif kind == "x_out":
            xout = dout("xT_out", [128, KT, TOK], F32)
            cx.dma("sp", st_slot, [(xout[:, 0:KT // 2, :], x[:, 0:KT // 2, :]), (xout[:, KT // 2:KT, :], x[:, KT // 2:KT, :])],
                   reads=["x"])
        elif kind == "final":
            fg, fkey = load_small("final_gT", [128, KT])
            rms_stats()
            for kt in range(KT):
                cx.op("dve", lambda e, kt=kt: e.scalar_tensor_tensor(
                    out=x[:, kt, :], in0=x[:, kt, :], scalar=fg[:, kt:kt + 1], in1=rstd[:],
                    op0=ALU.mult, op1=ALU.mult),
                    reads=["x", fkey, "rstd"], writes=["x"])
            xout = dout("xT_out", [128, KT, TOK], F32)
            cx.dma("sp", st_slot, [(xout[:, 0:KT // 2, :], x[:, 0:KT // 2, :]), (xout[:, KT // 2:KT, :], x[:, KT // 2:KT, :])],
                   reads=["x"])
    cx.close_scope()
    cx.wait_all("sp")
    return nc


def cast_pairs(dst, src):
    if len(dst.shape) == 2:
        n = dst.shape[1]
        return [(dst[:, c0:min(n, c0 + 1024)], src[:, c0:min(n, c0 + 1024)]) for c0 in range(0, n, 1024)]
    out = []
    for a in range(dst.shape[1]):
        n = dst.shape[2]
        for c0 in range(0, n, 1024):
            out.append((dst[:, a, c0:min(n, c0 + 1024)], src[:, a, c0:min(n, c0 + 1024)]))
    return out


def fm(v):
    v = np.asarray(v)
    return np.ascontiguousarray(v.reshape(-1, 128).T)


def to_xT(x):
    out = []
    for b in range(4):
        for j in range(2):
            xs = x[b, j * TOK:(j + 1) * TOK, :]
            out.append(np.ascontiguousarray(xs.T.reshape(KT, 128, TOK).transpose(1, 0, 2)))
    return out


def from_xT(tiles):
    x = np.empty((4, SEQ, D), np.float32)
    for b in range(4):
        for j in range(2):
            t = tiles[b * 2 + j]
            x[b, j * TOK:(j + 1) * TOK, :] = t.transpose(1, 0, 2).reshape(D, TOK).T
    return x


S5TC = 256
GELU_C = 0.7978845608028654
TWO_PI = 6.283185307179586


CHK_COUNT = 1
DBG_BANKS = [0, 1]
DBG_NODVE = False


class _Stop(Exception):
    pass


def build_M0(do_s5=True, do_ssd=True, stop=None, G=None, io=None):
    nc, cx, dram, banks, bankT = _mk_env(G)
    io = io or {}
    cx.open_scope()

    cnt = [CHK_COUNT]

    def chk(n):
        if stop == n:
            cnt[0] -= 1
            if cnt[0] <= 0:
                raise _Stop()
    try:
        _build_M0_body(nc, cx, dram, banks, bankT, io, do_s5, do_ssd, chk)
    except _Stop:
        pass
    while cx.stacks and stop is not None:
        cx.close_scope()
    if stop is None:
        cx.close_scope()
    cx.wait_all("sp")
    return nc


def _build_M0_body(nc, cx, dram, banks, bankT, io, do_s5, do_ssd, chk):

    def din(name, shape, dtype=F32):
        if name in io:
            return io[name]
        if name not in dram:
            dram[name] = nc.dram_tensor(name, list(shape), dtype, kind="ExternalInput").ap()
        return dram[name]

    def dout(name, shape, dtype=F32):
        if name in io:
            return io[name]
        dram[name] = nc.dram_tensor(name, list(shape), dtype, kind="ExternalOutput").ap()
        return dram[name]

    BF16S = "bf16_from_f32"

    def load(name, shape, dtype=F32, q="sp"):
        sdt, ddt = (BF16, F32) if dtype == BF16S else (dtype, dtype)
        t = cx.sb(name, shape, sdt)
        sl = cx.slot(name)
        pairs = cast_pairs(t[:], din(name, shape, ddt)) if dtype == BF16S else [(t[:], din(name, shape, ddt))]
        cx.dma(q, sl, pairs, writes=[name])
        return t

    w_d = din("w_in_c", [D, 3600])
    hT = cx.sb("hT", [128, KT, SEQ], BF16)
    sl = cx.slot("hT")
    if "hT_pairs" in io:
        cx.dma("sp", sl, io["hT_pairs"](hT, 0), reads=io.get("dep", []), writes=["hT"])
    else:
        hT_d = din("hT", [128, KT, SEQ], BF16)
        cx.dma("sp", sl, [(hT[:, 4 * i:4 * i + 4, :], hT_d[:, 4 * i:4 * i + 4, :]) for i in range(4)], writes=["hT"])
    ws = WStream(cx, nslot=4, elems=4096)
    ident = load("ident", [128, 128])
    identb = cx.sb("identb", [128, 128], BF16)
    cx.op("dve", lambda e: e.tensor_copy(out=identb[:], in_=ident[:]), reads=["ident"], writes=["identb"])
    st_slot = cx.slot("st")
    st_slot2 = cx.slot("st2")

    def proj(blk, col0, ncol_tiles, evac):
        wv = ws.view(blk)
        n = 0
        for ti in range(ncol_tiles):
            for tb in range(4):
                bk = DBG_BANKS[n % len(DBG_BANKS)]
                n += 1
                for kt in range(KT):
                    cx.op("pe", lambda e, kt=kt, ti=ti, tb=tb, bk=bk: e.matmul(
                        banks[bk][:], wv[:, kt, (col0 + ti) * 128:(col0 + ti + 1) * 128],
                        hT[:, kt, tb * 512:(tb + 1) * 512], start=(kt == 0), stop=(kt == KT - 1)),
                        reads=[ws.key(blk), "hT"], writes=[f"bank{bk}"])
                chk(53)
                evac(ti, tb, banks[bk], f"bank{bk}")
                chk(54)

    if do_s5:
        cx.open_scope()
        lre = load("s5_lre", [128, 16])
        lim = load("s5_lim", [128, 16])
        ldt = load("s5_ldt", [128, 16])
        d5 = load("s5_dT", [128, 4])
        cre = load("s5_cre", [128, 16, 128], BF16S, q="pool")
        cimn = load("s5_cim", [128, 16, 128], BF16S, q="pool")
        cx.op("dve", lambda e: e.tensor_scalar(out=cimn[:], in0=cimn[:], scalar1=-1.0, scalar2=None, op0=ALU.mult),
              reads=["s5_cim"], writes=["s5_cim"])
        bre = cx.sb("breT", [128, 16, 128], BF16)
        bim = cx.sb("bimT", [128, 16, 128], BF16)

        sm = {}

        def S(name):
            sm[name] = cx.sb("s5_" + name, [128, 16], F32)
            return sm[name]

        def tt(o, a, b, op, eng="dve"):
            cx.op(eng, lambda e: e.tensor_tensor(out=sm[o][:], in0=sm[a][:], in1=sm[b][:], op=op),
                  reads=["s5sm"], writes=["s5sm"])

        def ts(o, a, s1, op0, s2=None, op1=None):
            if op1 is None:
                cx.op("dve", lambda e: e.tensor_scalar(out=sm[o][:], in0=sm[a][:], scalar1=s1, scalar2=None, op0=op0),
                      reads=["s5sm"], writes=["s5sm"])
            else:
                cx.op("dve", lambda e: e.tensor_scalar(out=sm[o][:], in0=sm[a][:], scalar1=s1, scalar2=s2, op0=op0, op1=op1),
                      reads=["s5sm"], writes=["s5sm"])

        def act(o, a, func, scale=1.0):
            cx.op("act", lambda e: e.activation(out=sm[o][:], in_=sm[a][:], func=func, scale=scale),
                  reads=["s5sm"], writes=["s5sm"])

        chk(1)
        sm["lre"], sm["lim"], sm["ldt"] = lre, lim, ldt
        for n_ in ("lr", "dt", "mag", "ang", "cs", "sn", "t1", "t2", "t3", "den", "nr", "fre", "fim", "lbr", "lbi", "rden"):
            S(n_)
        cx.wait_all("dve")
        cx.wait_all("act")
        ts("lr", "lre", -1e-4, ALU.min)
        act("dt", "ldt", AF.Exp)
        tt("t1", "lr", "dt", ALU.mult)
        act("mag", "t1", AF.Exp)
        tt("ang", "lim", "dt", ALU.mult)

        def sincos(o, a, shift):
            ki = cx.sb("s5_ki", [128, 16], mybir.dt.int32)
            ts("t1", a, 1.0 / TWO_PI, ALU.mult, shift / TWO_PI, ALU.add)
            cx.op("dve", lambda e: e.tensor_copy(out=ki[:], in_=sm["t1"][:]), reads=["s5sm"], writes=["s5ki"])
            cx.op("dve", lambda e: e.tensor_copy(out=sm["t2"][:], in_=ki[:]), reads=["s5ki"], writes=["s5sm"])
            tt("t1", "t1", "t2", ALU.subtract)
            ts("t2", "t1", 0.5, ALU.is_gt)
            tt("t1", "t1", "t2", ALU.subtract)
            ts("t2", "t1", -0.5, ALU.is_lt)
            tt("t1", "t1", "t2", ALU.add)
            act(o, "t1", AF.Sin, scale=TWO_PI)

        sincos("sn", "ang", 0.0)
        sincos("cs", "ang", TWO_PI / 4)
        tt("lbr", "mag", "cs", ALU.mult)
        tt("lbi", "mag", "sn", ALU.mult)
        tt("t1", "lr", "lr", ALU.mult)
        tt("t2", "lim", "lim", ALU.mult)
        tt("den", "t1", "t2", ALU.add)
        cx.op("dve", lambda e: e.reciprocal(out=sm["rden"][:], in_=sm["den"][:]), reads=["s5sm"], writes=["s5sm"])
        ts("nr", "lbr", -1.0, ALU.add)
        tt("t1", "nr", "lr", ALU.mult)
        tt("t2", "lbi", "lim", ALU.mult)
        tt("t1", "t1", "t2", ALU.add)
        tt("fre", "t1", "rden", ALU.mult)
        tt("t1", "lbi", "lr", ALU.mult)
        tt("t2", "nr", "lim", ALU.mult)
        tt("t1", "t1", "t2", ALU.subtract)
        tt("fim", "t1", "rden", ALU.mult)
        S("nfim")
        ts("nfim", "fim", -1.0, ALU.mult)
        S("nsn")
        ts("nsn", "sn", -1.0, ALU.mult)

        chk(2)
        cx.open_scope()
        xbre = load("s5_xbre", [128, 16, 128], q="act")
        xbim = load("s5_xbim", [128, 16, 128], q="act")
        xt = [cx.sb(f"s5xt{i}", [128, 128], F32) for i in range(2)]
        for pr in range(16):
            for part, (A, fa, Bm, fb) in enumerate(((xbre, "fre", xbim, "nfim"), (xbim, "fre", xbre, "fim"))):
                t_ = xt[part]
                cx.op("dve", lambda e, pr=pr, A=A, fa=fa, t_=t_: e.tensor_scalar(
                    out=t_[:], in0=A[:, pr, :], scalar1=sm[fa][:, pr:pr + 1], scalar2=None, op0=ALU.mult),
                    reads=["s5sm", "s5_xbre", "s5_xbim"], writes=[f"s5xt{part}"])
                cx.op("dve", lambda e, pr=pr, Bm=Bm, fb=fb, t_=t_: e.scalar_tensor_tensor(
                    out=t_[:], in0=Bm[:, pr, :], scalar=sm[fb][:, pr:pr + 1], in1=t_[:], op0=ALU.mult, op1=ALU.add),
                    reads=["s5sm", "s5_xbre", "s5_xbim", f"s5xt{part}"], writes=[f"s5xt{part}"])
                cx.op("pe", lambda e, t_=t_, part=part: e.transpose(banks[2 + part][:, 0:128], t_[:], ident[:]),
                      reads=[f"s5xt{part}", "ident"], writes=[f"bank{2 + part}"])
                dst = bre if part == 0 else bim
                cx.op("act", lambda e, dst=dst, pr=pr, part=part: e.activation(
                    out=dst[:, pr, :], in_=banks[2 + part][:, 0:128], func=AF.Copy),
                    reads=[f"bank{2 + part}"], writes=["breT" if part == 0 else "bimT"])

        chk(3)
        cx.close_scope()
        chk(4)
        ctab = cx.sb("ctab", [128, 16, S5TC], F32)
        stab = cx.sb("stab", [128, 16, S5TC], F32)
        rho = cx.sb("rho", [128, 16, S5TC], F32)
        ec = cx.sb("ec", [128, 16], F32)
        es = cx.sb("es", [128, 16], F32)
        et = [cx.sb(f"et{i}", [128, 16], F32) for i in range(3)]
        cx.op("dve", lambda e: e.memset(ctab[:, :, 0:1], 1.0), writes=["tab"])
        cx.op("dve", lambda e: e.memset(stab[:, :, 0:1], 0.0), reads=["tab"], writes=["tab"])
        cx.op("dve", lambda e: e.tensor_copy(out=ec[:], in_=sm["cs"][:]), reads=["s5sm"], writes=["e"])
        cx.op("dve", lambda e: e.tensor_copy(out=es[:], in_=sm["sn"][:]), reads=["s5sm", "e"], writes=["e"])
        L = 1
        while L < S5TC:
            for pr in range(16):
                cx.op("dve", lambda e, pr=pr, L=L: e.tensor_scalar(
                    out=ctab[:, pr, L:2 * L], in0=ctab[:, pr, 0:L], scalar1=ec[:, pr:pr + 1], scalar2=None, op0=ALU.mult),
                    reads=["tab", "e"], writes=["tab"])
                cx.op("dve", lambda e, pr=pr, L=L: e.tensor_scalar(
                    out=stab[:, pr, L:2 * L], in0=ctab[:, pr, 0:L], scalar1=es[:, pr:pr + 1], scalar2=None, op0=ALU.mult),
                    reads=["tab", "e"], writes=["tab"])
            cx.op("dve", lambda e: e.tensor_scalar(out=et[0][:], in0=es[:], scalar1=-1.0, scalar2=None, op0=ALU.mult),
                  reads=["e"], writes=["et"])
            for pr in range(16):
                cx.op("dve", lambda e, pr=pr, L=L: e.scalar_tensor_tensor(
                    out=ctab[:, pr, L:2 * L], in0=stab[:, pr, 0:L], scalar=et[0][:, pr:pr + 1], in1=ctab[:, pr, L:2 * L],
                    op0=ALU.mult, op1=ALU.add), reads=["tab", "et"], writes=["tab"])
                cx.op("dve", lambda e, pr=pr, L=L: e.scalar_tensor_tensor(
                    out=stab[:, pr, L:2 * L], in0=stab[:, pr, 0:L], scalar=ec[:, pr:pr + 1], in1=stab[:, pr, L:2 * L],
                    op0=ALU.mult, op1=ALU.add), reads=["tab", "e"], writes=["tab"])
            cx.op("dve", lambda e: e.tensor_tensor(out=et[1][:], in0=ec[:], in1=ec[:], op=ALU.mult), reads=["e"], writes=["et1"])
            cx.op("dve", lambda e: e.tensor_tensor(out=et[2][:], in0=es[:], in1=es[:], op=ALU.mult), reads=["e"], writes=["et2"])
            cx.op("dve", lambda e: e.scalar_tensor_tensor(out=es[:], in0=es[:], scalar=2.0, in1=ec[:], op0=ALU.mult, op1=ALU.mult),
                  reads=["e"], writes=["e"])
            cx.op("dve", lambda e: e.tensor_tensor(out=ec[:], in0=et[1][:], in1=et[2][:], op=ALU.subtract),
                  reads=["et1", "et2", "e"], writes=["e"])
            L *= 2
        for pr in range(16):
            cx.op("act", lambda e, pr=pr: e.activation(out=rho[:, pr, :], in_=ctab[:, pr, :], func=AF.Identity,
                                                        scale=0.0, bias=sm["mag"][:, pr:pr + 1]),
                  reads=["tab", "s5sm"], writes=["rho"])

        chk(5)
        u32 = cx.sb("u32", [128, SEQ], F32)
        ubf = cx.sb("ubf", [128, SEQ], BF16)
        y5 = cx.sb("y5", [128, SEQ], F32)
        carry = [cx.sb(f"carry{i}", [128, 16], F32) for i in range(2)]
        cx.op("dve", lambda e: e.memset(carry[0][:], 0.0), writes=["carry"])
        cx.op("dve", lambda e: e.memset(carry[1][:], 0.0), reads=["carry"], writes=["carry"])
        chk(51)
        tmp = [[cx.sb(f"s5tmp{j}{i}", [128, S5TC], F32) for i in range(8)] for j in range(2)]
        sbf = [[cx.sb(f"s5sbf{i}{j}", [128, S5TC], BF16) for j in range(2)] for i in range(4)]
        gl = [cx.sb(f"s5gl{i}", [128, S5TC], F32) for i in range(4)]
        y5_d = None if "y_tile" in io else dout("y5T", [128, 4, SEQ])
        NCH = SEQ // S5TC
        for o in range(4):
            blk = ws.add(w_d[:, o * 128:(o + 1) * 128].rearrange("(kt p) c -> p kt c", p=128), KT, 128)
            ws.need(blk)
            chk(52)

            def ev_u(ti, tb, ps, key):
                cx.op("act", lambda e: e.activation(out=u32[:, tb * 512:(tb + 1) * 512], in_=ps[:], func=AF.Copy),
                      reads=[key], writes=["u32"])
                if not DBG_NODVE:
                    cx.op("dve", lambda e: e.tensor_copy(out=ubf[:, tb * 512:(tb + 1) * 512], in_=ps[:]),
                          reads=[key], writes=["ubf"])
            proj(blk, 0, 1, ev_u)
            chk(6)
            for ch in range(NCH):
                c0 = ch * S5TC
                if ch == 1:
                    chk(7)
                for pp in range(4):
                    pr = o * 4 + pp
                    kre, kim = f"bank{2 + (pp % 2) * 2}", f"bank{3 + (pp % 2) * 2}"
                    pre, pim = banks[2 + (pp % 2) * 2], banks[3 + (pp % 2) * 2]
                    cx.op("pe", lambda e: e.matmul(pre[:, 0:S5TC], bre[:, pr, :], ubf[:, c0:c0 + S5TC], start=True, stop=True),
                          reads=["breT", "ubf"], writes=[kre])
                    cx.op("pe", lambda e: e.matmul(pim[:, 0:S5TC], bim[:, pr, :], ubf[:, c0:c0 + S5TC], start=True, stop=True),
                          reads=["bimT", "ubf"], writes=[kim])
                    t = tmp[pp % 2]
                    tq = pp % 2
                    ct, stb = ctab[:, pr, :], stab[:, pr, :]
                    cx.op("dve", lambda e: e.tensor_tensor(out=t[0][:], in0=pre[:, 0:S5TC], in1=ct, op=ALU.mult),
                          reads=[kre, "tab"], writes=[f"t{tq}_0"])
                    cx.op("dve", lambda e: e.tensor_tensor(out=t[1][:], in0=pim[:, 0:S5TC], in1=stb, op=ALU.mult),
                          reads=[kim, "tab"], writes=[f"t{tq}_1"])
                    cx.op("dve", lambda e: e.tensor_tensor(out=t[2][:], in0=pim[:, 0:S5TC], in1=ct, op=ALU.mult),
                          reads=[kim, "tab"], writes=[f"t{tq}_2"])
                    cx.op("dve", lambda e: e.tensor_tensor(out=t[3][:], in0=pre[:, 0:S5TC], in1=stb, op=ALU.mult),
                          reads=[kre, "tab"], writes=[f"t{tq}_3"])
                    cx.op("pool", lambda e: e.tensor_tensor(out=t[0][:], in0=t[0][:], in1=t[1][:], op=ALU.add),
                          reads=[f"t{tq}_0", f"t{tq}_1"], writes=[f"t{tq}_0"])
                    cx.op("pool", lambda e: e.tensor_tensor(out=t[2][:], in0=t[2][:], in1=t[3][:], op=ALU.subtract),
                          reads=[f"t{tq}_2", f"t{tq}_3"], writes=[f"t{tq}_2"])
                    cx.op("dve", lambda e: e.tensor_tensor_scan(out=t[4][:], data0=rho[:, pr, :], data1=t[0][:],
                                                                 initial=carry[0][:, pr:pr + 1], op0=ALU.mult, op1=ALU.add),
                          reads=["rho", f"t{tq}_0", "carry"], writes=[f"t{tq}_4"])
                    cx.op("dve", lambda e: e.tensor_tensor_scan(out=t[5][:], data0=rho[:, pr, :], data1=t[2][:],
                                                                 initial=carry[1][:, pr:pr + 1], op0=ALU.mult, op1=ALU.add),
                          reads=["rho", f"t{tq}_2", "carry"], writes=[f"t{tq}_5"])
                    if ch < NCH - 1:
                        cx.op("dve", lambda e: e.tensor_scalar(out=et[1][:, 0:1], in0=t[5][:, S5TC - 1:S5TC], scalar1=es[:, pr:pr + 1],
                                                                scalar2=None, op0=ALU.mult), reads=[f"t{tq}_5", "e"], writes=["et1"])
                        cx.op("dve", lambda e: e.tensor_scalar(out=et[2][:, 0:1], in0=t[4][:, S5TC - 1:S5TC], scalar1=es[:, pr:pr + 1],
                                                                scalar2=None, op0=ALU.mult), reads=[f"t{tq}_4", "e"], writes=["et2"])
                        cx.op("dve", lambda e: e.scalar_tensor_tensor(out=carry[0][:, pr:pr + 1], in0=t[4][:, S5TC - 1:S5TC],
                                                                       scalar=ec[:, pr:pr + 1], in1=et[1][:, 0:1],
                                                                       op0=ALU.mult, op1=ALU.subtract),
                              reads=[f"t{tq}_4", "e", "et1", "carry"], writes=["carry"])
                        cx.op("dve", lambda e: e.scalar_tensor_tensor(out=carry[1][:, pr:pr + 1], in0=t[5][:, S5TC - 1:S5TC],
                                                                       scalar=ec[:, pr:pr + 1], in1=et[2][:, 0:1],
                                                                       op0=ALU.mult, op1=ALU.add),
                              reads=[f"t{tq}_5", "e", "et2", "carry"], writes=["carry"])
                    cx.op("pool", lambda e: e.tensor_tensor(out=t[6][:], in0=t[4][:], in1=ct, op=ALU.mult),
                          reads=[f"t{tq}_4", "tab"], writes=[f"t{tq}_6"])
                    cx.op("pool", lambda e: e.tensor_tensor(out=t[7][:], in0=t[5][:], in1=stb, op=ALU.mult),
                          reads=[f"t{tq}_5", "tab"], writes=[f"t{tq}_7"])
                    cx.op("pool", lambda e: e.tensor_tensor(out=sbf[pp][0][:], in0=t[6][:], in1=t[7][:], op=ALU.subtract),
                          reads=[f"t{tq}_6", f"t{tq}_7"], writes=[f"sbf{pp}0"])
                    cx.op("pool", lambda e: e.tensor_tensor(out=t[6][:], in0=t[4][:], in1=stb, op=ALU.mult),
                          reads=[f"t{tq}_4", "tab", f"t{tq}_6"], writes=[f"t{tq}_6"])
                    cx.op("pool", lambda e: e.tensor_tensor(out=t[7][:], in0=t[5][:], in1=ct, op=ALU.mult),
                          reads=[f"t{tq}_5", "tab", f"t{tq}_7"], writes=[f"t{tq}_7"])
                    cx.op("pool", lambda e: e.tensor_tensor(out=sbf[pp][1][:], in0=t[6][:], in1=t[7][:], op=ALU.add),
                          reads=[f"t{tq}_6", f"t{tq}_7"], writes=[f"sbf{pp}1"])
                py = banks[6]
                for pp in range(4):
                    pr = o * 4 + pp
                    cx.op("pe", lambda e: e.matmul(py[:, 0:S5TC], cre[:, pr, :], sbf[pp][0][:], start=(pp == 0), stop=False),
                          reads=["s5_cre", f"sbf{pp}0"], writes=["bank6"])
                    cx.op("pe", lambda e: e.matmul(py[:, 0:S5TC], cimn[:, pr, :], sbf[pp][1][:], start=False, stop=(pp == 3)),
                          reads=["s5_cim", f"sbf{pp}1"], writes=["bank6"])
                cx.op("dve", lambda e: e.scalar_tensor_tensor(out=gl[0][:], in0=u32[:, c0:c0 + S5TC], scalar=d5[:, o:o + 1],
                                                               in1=py[:, 0:S5TC], op0=ALU.mult, op1=ALU.add),
                      reads=["u32", "s5_dT", "bank6"], writes=["gl0"])
                cx.op("act", lambda e: e.activation(out=gl[1][:], in_=gl[0][:], func=AF.Square), reads=["gl0"], writes=["gl1"])
                cx.op("dve", lambda e: e.tensor_scalar(out=gl[1][:], in0=gl[1][:], scalar1=GELU_C * 0.044715, scalar2=GELU_C,
                                                        op0=ALU.mult, op1=ALU.add), reads=["gl1"], writes=["gl1"])
                cx.op("dve", lambda e: e.tensor_tensor(out=gl[1][:], in0=gl[1][:], in1=gl[0][:], op=ALU.mult),
                      reads=["gl1", "gl0"], writes=["gl1"])
                cx.op("act", lambda e: e.activation(out=gl[2][:], in_=gl[1][:], func=AF.Tanh), reads=["gl1"], writes=["gl2"])
                cx.op("act", lambda e: e.activation(out=gl[3][:], in_=gl[0][:], func=AF.Copy, scale=0.5), reads=["gl0"], writes=["gl3"])
                cx.op("dve", lambda e: e.scalar_tensor_tensor(out=y5[:, c0:c0 + S5TC], in0=gl[2][:], scalar=1.0, in1=gl[3][:],
                                                               op0=ALU.add, op1=ALU.mult),
                      reads=["gl2", "gl3"], writes=["y5"])
            cx.dma("sp", st_slot if o % 2 == 0 else st_slot2,
                   [(io["y_tile"](o) if "y_tile" in io else y5_d[:, o, :], y5[:])], reads=["y5"])
        cx.close_scope()

    if do_ssd:
        cx.open_scope()
        tri = load("tri", [128, 128])
        ones = load("ones128", [128, 128])
        maskneg = load("maskneg", [128, 128])
        cw = load("conv_wT", [128, 16, 4])
        cb = load("conv_bT", [128, 16])
        dtb = load("dt_bias_bc", [128, 16])
        alog = load("a_log_bc", [128, 16])
        dsk = load("ssd_dT", [128, 8])
        onec = cx.sb("onec", [128, 1], F32)
        cx.op("dve", lambda e: e.memset(onec[:], 1.0), writes=["onec"])
        abc = cx.sb("abc", [128, 16], F32)
        cx.op("act", lambda e: e.activation(out=abc[:], in_=alog[:], func=AF.Exp), reads=["a_log_bc"], writes=["abc"])
        cx.op("dve", lambda e: e.tensor_scalar(out=abc[:], in0=abc[:], scalar1=-1.0, scalar2=None, op0=ALU.mult),
              reads=["abc"], writes=["abc"])
        NC_ = SEQ // 128
        dt_all = cx.sb("dt_all", [128, NC_, 16], F32)
        adt = cx.sb("adt", [128, NC_, 16], F32)
        cum = cx.sb("cum", [128, NC_, 16], F32)
        dte = cx.sb("dte", [128, NC_, 16], F32)
        dectot = cx.sb("dectot", [128, NC_, 16], F32)
        dg = [cx.sb(f"dg{i}", [128, 128], F32) for i in range(2)]
        tsm = [cx.sb(f"tsm{i}", [128, 16], F32) for i in range(2)]
        bdt = ws.add(w_d[:, 3584:3600].rearrange("(kt p) c -> p kt c", p=128), KT, 16)
        ws.need(bdt)
        wdt = ws.view(bdt)
        b2 = banks[2]
        for c in range(NC_):
            for kt in range(KT):
                cx.op("pe", lambda e, kt=kt: e.matmul(b2[:, 0:16], hT[:, kt, c * 128:(c + 1) * 128], wdt[:, kt, :],
                                                       start=(kt == 0), stop=(kt == KT - 1)),
                      reads=[ws.key(bdt), "hT"], writes=["bank2"])
            cx.op("dve", lambda e: e.tensor_tensor(out=tsm[0][:], in0=b2[:, 0:16], in1=dtb[:], op=ALU.add),
                  reads=["bank2", "dt_bias_bc"], writes=["tsm0"])
            cx.op("act", lambda e: e.activation(out=tsm[0][:], in_=tsm[0][:], func=AF.Exp), reads=["tsm0"], writes=["tsm0"])
            cx.op("act", lambda e: e.activation(out=dt_all[:, c, :], in_=tsm[0][:], func=AF.Ln, bias=onec[:], scale=1.0),
                  reads=["tsm0", "onec"], writes=["dt_all"])
            cx.op("dve", lambda e: e.tensor_tensor(out=adt[:, c, :], in0=dt_all[:, c, :], in1=abc[:], op=ALU.mult),
                  reads=["dt_all", "abc"], writes=["adt"])
            cx.op("pe", lambda e: e.matmul(banks[3][:, 0:16], tri[:], adt[:, c, :], start=True, stop=True),
                  reads=["tri", "adt"], writes=["bank3"])
            cx.op("pe", lambda e: e.matmul(banks[4][:, 0:16], ones[:], adt[:, c, :], start=True, stop=True),
                  reads=["ones128", "adt"], writes=["bank4"])
            cx.op("act", lambda e: e.activation(out=cum[:, c, :], in_=banks[3][:, 0:16], func=AF.Copy), reads=["bank3"], writes=["cum"])
            cx.op("act", lambda e: e.activation(out=dectot[:, c, :], in_=banks[4][:, 0:16], func=AF.Exp), reads=["bank4"], writes=["dectot"])
            cx.op("dve", lambda e: e.tensor_tensor(out=tsm[1][:], in0=banks[4][:, 0:16], in1=cum[:, c, :], op=ALU.subtract),
                  reads=["bank4", "cum"], writes=["tsm1"])
            cx.op("act", lambda e: e.activation(out=dte[:, c, :], in_=tsm[1][:], func=AF.Exp), reads=["tsm1"], writes=["dte"])

        raw = cx.sb("raw", [128, 4, 3 + SEQ], F32)
        cx.op("dve", lambda e: e.memset(raw[:, :, 0:3], 0.0), writes=["raw"])
        sz = cx.sb("sz", [128, 2, SEQ], BF16)
        cv = [cx.sb("cv0", [128, SEQ], F32)]
        xs32 = cx.sb("xs32", [128, 2, SEQ], F32)
        xsb = cx.sb("xsb", [128, 2, SEQ], BF16)
        BT = cx.sb("BT", [128, SEQ], BF16)
        CT = cx.sb("CT", [128, SEQ], BF16)
        yout = raw[:, 0:2, 3:3 + SEQ]
        car32 = cx.sb("car32", [128, 4, 64], F32)
        carb = cx.sb("carb", [128, 4, 64], BF16)
        Btok = [cx.sb(f"Btok{i}", [128, 128], BF16) for i in range(2)]
        xq = [cx.sb(f"xq{i}", [128, 256], BF16) for i in range(2)]
        xqd = [cx.sb(f"xqd{i}", [128, 256], BF16) for i in range(2)]
        dm32 = [cx.sb(f"dm32{i}", [128, 128], F32) for i in range(2)]
        dmx = [cx.sb(f"dmx{i}", [128, 128], F32) for i in range(2)]
        MT = [[cx.sb(f"MT{i}{j}", [128, 128], BF16) for j in range(4)] for i in range(2)]
        ebc = [cx.sb(f"ebc{i}", [128, 128], F32) for i in range(2)]
        Cs = [[cx.sb(f"Cs{i}{j}", [128, 128], BF16) for j in range(4)] for i in range(2)]
        ytmp = [cx.sb(f"ytmp{i}", [128, 128], F32) for i in range(2)]
        ys_d = None if "y_tile" in io else dout("ysT", [128, 8, SEQ])
        b3, b4, b5 = banks[3], banks[4], banks[5]
        for gg in range(4):
            base = 512 + gg * 768
            bx = ws.add(w_d[:, base:base + 256].rearrange("(kt p) c -> p kt c", p=128), KT, 256)
            bbc = ws.add(w_d[:, base + 256:base + 512].rearrange("(kt p) c -> p kt c", p=128), KT, 256)
            bz = ws.add(w_d[:, base + 512:base + 768].rearrange("(kt p) c -> p kt c", p=128), KT, 256)

            def ev_raw(off):
                def f(ti, tb, ps, key):
                    cx.op("act", lambda e: e.activation(out=raw[:, off + ti, 3 + tb * 512:3 + (tb + 1) * 512], in_=ps[:], func=AF.Copy),
                          reads=[key], writes=["raw"])
                return f

            def ev_z(ti, tb, ps, key):
                cx.op("act", lambda e: e.activation(out=sz[:, ti, tb * 512:(tb + 1) * 512], in_=ps[:], func=AF.Silu),
                      reads=[key], writes=["sz"])
            ws.need(bx)
            proj(bx, 0, 2, ev_raw(0))
            ws.need(bbc)
            proj(bbc, 0, 2, ev_raw(2))
            ws.need(bz)
            proj(bz, 0, 2, ev_z)
            for ti in range(4):
                tidx = gg * 4 + ti
                cvt = cv[0]
                ck = "cv0"
                cx.op("dve", lambda e: e.tensor_scalar(out=cvt[:], in0=raw[:, ti, 0:SEQ], scalar1=cw[:, tidx, 0:1], scalar2=None,
                                                        op0=ALU.mult), reads=["raw", "conv_wT"], writes=[ck])
                for jj in range(1, 4):
                    cx.op("dve", lambda e, jj=jj: e.scalar_tensor_tensor(out=cvt[:], in0=raw[:, ti, jj:jj + SEQ],
                                                                       scalar=cw[:, tidx, jj:jj + 1], in1=cvt[:],
                                                                       op0=ALU.mult, op1=ALU.add),
                          reads=["raw", "conv_wT", ck], writes=[ck])
                if ti < 2:
                    cx.op("act", lambda e: e.activation(out=xs32[:, ti, :], in_=cvt[:], func=AF.Silu, bias=cb[:, tidx:tidx + 1], scale=1.0),
                          reads=[ck, "conv_bT"], writes=["xs32"])
                    cx.op("pool", lambda e: e.tensor_copy(out=xsb[:, ti, :], in_=xs32[:, ti, :]), reads=["xs32"], writes=["xsb"])
                else:
                    dst, dk = (BT, "BT") if ti == 2 else (CT, "CT")
                    cx.op("act", lambda e: e.activation(out=dst[:], in_=cvt[:], func=AF.Silu, bias=cb[:, tidx:tidx + 1], scale=1.0),
                          reads=[ck, "conv_bT"], writes=[dk])
            cx.op("dve", lambda e: e.memset(car32[:], 0.0), reads=["car32"], writes=["car32"])
            cx.op("dve", lambda e: e.memset(carb[:], 0.0), reads=["carb"], writes=["carb"])
            def partA(c):
                cs_ = slice(c * 128, (c + 1) * 128)
                par = c % 2
                cx.op("pe", lambda e: e.transpose(bankT[:, 0:128], BT[:, cs_], identb[:]), reads=["BT", "identb"], writes=["bankT"])
                cx.op("pe", lambda e: e.transpose(bankT[:, 128:256], xsb[:, 0, cs_], identb[:]), reads=["xsb", "identb"], writes=["bankT"])
                cx.op("pe", lambda e: e.transpose(bankT[:, 256:384], xsb[:, 1, cs_], identb[:]), reads=["xsb", "identb"], writes=["bankT"])
                cx.op("act", lambda e: e.activation(out=Btok[par][:], in_=bankT[:, 0:128], func=AF.Copy),
                      reads=["bankT"], writes=[f"Btok{par}"])
                for hh in range(4):
                    h_ = gg * 4 + hh
                    cx.op("dve", lambda e: e.tensor_scalar(out=xq[par][:, hh * 64:(hh + 1) * 64], in0=bankT[:, 128 + hh * 64:192 + hh * 64],
                                                            scalar1=dt_all[:, c, h_:h_ + 1], scalar2=None, op0=ALU.mult),
                          reads=["bankT", "dt_all"], writes=[f"xq{par}"])
                    cx.op("pool", lambda e: e.tensor_scalar(out=xqd[par][:, hh * 64:(hh + 1) * 64], in0=xq[par][:, hh * 64:(hh + 1) * 64],
                                                             scalar1=dte[:, c, h_:h_ + 1], scalar2=None, op0=ALU.mult),
                          reads=[f"xq{par}", "dte"], writes=[f"xqd{par}"])
                cx.op("pe", lambda e: e.matmul(b3[:, 0:128], BT[:, cs_], CT[:, cs_], start=True, stop=True),
                      reads=["BT", "CT"], writes=["bank3"])
                for hh in range(4):
                    h_ = gg * 4 + hh
                    hp = hh % 2
                    crow = banks[hh % 2][:, 0:128]
                    cx.op("pool", lambda e: e.tensor_scalar(out=dg[hp][:], in0=ident[:], scalar1=cum[:, c, h_:h_ + 1], scalar2=None,
                                                             op0=ALU.mult), reads=["ident", "cum"], writes=[f"dg{hp}"])
                    cx.op("pe", lambda e: e.matmul(crow, ones[:], dg[hp][:], start=True, stop=True),
                          reads=["ones128", f"dg{hp}"], writes=[f"bank{hh % 2}"])
                    cx.op("dve", lambda e: e.scalar_tensor_tensor(out=dm32[hp][:], in0=crow, scalar=cum[:, c, h_:h_ + 1], in1=maskneg[:],
                                                                   op0=ALU.subtract, op1=ALU.add),
                          reads=[f"bank{hh % 2}", "cum", "maskneg"], writes=[f"dm32{hp}"])
                    cx.op("act", lambda e: e.activation(out=dmx[hp][:], in_=dm32[hp][:], func=AF.Exp), reads=[f"dm32{hp}"], writes=[f"dmx{hp}"])
                    cx.op("dve", lambda e: e.tensor_tensor(out=MT[par][hh][:], in0=dmx[hp][:], in1=b3[:, 0:128], op=ALU.mult),
                          reads=[f"dmx{hp}", "bank3"], writes=[f"MT{par}{hh}"])
                    cx.op("act", lambda e: e.activation(out=ebc[hp][:], in_=crow, func=AF.Exp), reads=[f"bank{hh % 2}"], writes=[f"ebc{hp}"])
                    cx.op("pool", lambda e: e.tensor_tensor(out=Cs[par][hh][:], in0=CT[:, cs_], in1=ebc[hp][:], op=ALU.mult),
                          reads=["CT", f"ebc{hp}"], writes=[f"Cs{par}{hh}"])
                cx.op("pe", lambda e: e.matmul(banks[2][:, 0:256], Btok[par][:], xqd[par][:], start=True, stop=True),
                      reads=[f"Btok{par}", f"xqd{par}"], writes=["bank2"])

            def partB(c):
                cs_ = slice(c * 128, (c + 1) * 128)
                par = c % 2
                for hh in range(4):
                    pt, half = hh // 2, hh % 2
                    yo = banks[5 + par][half * 64:(half + 1) * 64, pt * 128:(pt + 1) * 128]
                    cx.op("pe", lambda e: e.matmul(yo, xq[par][:, hh * 64:(hh + 1) * 64], MT[par][hh][:], start=True, stop=False),
                          reads=[f"xq{par}", f"MT{par}{hh}"], writes=[f"bank{5 + par}"])
                    cx.op("pe", lambda e: e.matmul(yo, carb[:, hh, :], Cs[par][hh][:], start=False, stop=True),
                          reads=["carb", f"Cs{par}{hh}"], writes=[f"bank{5 + par}"])
                for hh in range(4):
                    h_ = gg * 4 + hh
                    cx.op("dve", lambda e: e.scalar_tensor_tensor(out=car32[:, hh, :], in0=car32[:, hh, :], scalar=dectot[:, c, h_:h_ + 1],
                                                                   in1=banks[2][:, hh * 64:(hh + 1) * 64], op0=ALU.mult, op1=ALU.add),
                          reads=["car32", "dectot", "bank2"], writes=["car32"])
                cx.op("pool", lambda e: e.tensor_copy(out=carb[:], in_=car32[:]), reads=["car32"], writes=["carb"])
                for pt in range(2):
                    cx.op("dve", lambda e: e.scalar_tensor_tensor(out=ytmp[pt][:], in0=xs32[:, pt, cs_], scalar=dsk[:, gg * 2 + pt:gg * 2 + pt + 1],
                                                                   in1=banks[5 + par][:, pt * 128:(pt + 1) * 128],
                                                                   op0=ALU.mult, op1=ALU.add),
                          reads=["xs32", "ssd_dT", f"bank{5 + par}"], writes=[f"ytmp{pt}"])
                    cx.op("pool", lambda e: e.tensor_tensor(out=yout[:, pt, cs_], in0=ytmp[pt][:], in1=sz[:, pt, cs_], op=ALU.mult),
                          reads=[f"ytmp{pt}", "sz"], writes=["raw"])

            for it_ in cx.record(partA, 0):
                cx.play(it_)
            for c in range(NC_):
                la = cx.record(partB, c)
                lb = cx.record(partA, c + 1) if c + 1 < NC_ else []
                cx.play_interleaved(la, lb)
            cx.dma("sp", st_slot if gg % 2 == 0 else st_slot2,
                   [((io["y_tile"](4 + gg * 2 + pt) if "y_tile" in io else ys_d[:, gg * 2 + pt, :]), yout[:, pt, :])
                    for pt in range(2)], reads=["raw"])
        cx.close_scope()


def prep_M0(inp, b, j, hT_full):
    m = {"hT": hT_full, "ident": np.eye(128, dtype=np.float32)}
    w = inp["hyb_w_in"][0]
    cols = [np.arange(j * 512, (j + 1) * 512)]
    for gg in range(4):
        G = j * 4 + gg
        cols.append(3072 + G * 256 + np.arange(256))
        cols.append(3072 + 2048 + G * 128 + np.arange(128))
        cols.append(3072 + 3072 + G * 128 + np.arange(128))
        cols.append(1024 + G * 256 + np.arange(256))
    cols.append(7168 + 16 * j + np.arange(16))
    cols = np.concatenate(cols)
    m["w_in_c"] = np.ascontiguousarray(w[:, cols])
    g0 = 32 * j
    lre = np.zeros((128, 16), np.float32)
    lim = np.zeros((128, 16), np.float32)
    ldt = np.zeros((128, 16), np.float32)
    xbre = np.zeros((128, 16, 128), np.float32)
    xbim = np.zeros((128, 16, 128), np.float32)
    cre = np.zeros((128, 16, 128), np.float32)
    cim = np.zeros((128, 16, 128), np.float32)
    for pr in range(16):
        pp = pr % 4
        for gi in range(2):
            g = g0 + 2 * pr + gi
            rows = slice(gi * 64, gi * 64 + 64)
            cs = slice(32 * pp + 16 * gi, 32 * pp + 16 * gi + 16)
            lre[rows, pr] = inp["s5_lambda_re"][0, g]
            lim[rows, pr] = inp["s5_lambda_im"][0, g]
            ldt[rows, pr] = inp["s5_log_dt"][0, g]
            xbre[rows, pr, cs] = inp["s5_b_re"][0, g]
            xbim[rows, pr, cs] = inp["s5_b_im"][0, g]
            cre[rows, pr, cs] = inp["s5_c_re"][0, g].T
            cim[rows, pr, cs] = inp["s5_c_im"][0, g].T
    m.update(s5_lre=lre, s5_lim=lim, s5_ldt=ldt, s5_xbre=xbre, s5_xbim=xbim, s5_cre=cre, s5_cim=cim)
    m["s5_dT"] = np.ascontiguousarray(inp["s5_d"][0, j * 512:(j + 1) * 512].reshape(4, 128).T)
    cwT = np.zeros((128, 16, 4), np.float32)
    cbT = np.zeros((128, 16), np.float32)
    dT = np.zeros((128, 8), np.float32)
    cwf, cbf = inp["ssd_conv_w"][0], inp["ssd_conv_b"][0]
    for gg in range(4):
        G = j * 4 + gg
        chans = [G * 256 + np.arange(128), G * 256 + 128 + np.arange(128),
                 2048 + G * 128 + np.arange(128), 3072 + G * 128 + np.arange(128)]
        for ti in range(4):
            cwT[:, gg * 4 + ti, :] = cwf[:, chans[ti]].T
            cbT[:, gg * 4 + ti] = cbf[chans[ti]]
        for pt in range(2):
            heads = (G * 256 + pt * 128 + np.arange(128)) // 64
            dT[:, gg * 2 + pt] = inp["ssd_d"][0][heads]
    hs = slice(16 * j, 16 * j + 16)
    m.update(conv_wT=cwT, conv_bT=cbT, ssd_dT=dT,
             dt_bias_bc=np.ascontiguousarray(np.broadcast_to(inp["ssd_dt_bias"][0, hs], (128, 16))),
             a_log_bc=np.ascontiguousarray(np.broadcast_to(inp["ssd_a_log"][0, hs], (128, 16))))
    tri = np.triu(np.ones((128, 128), np.float32))
    m["tri"] = tri
    m["ones128"] = np.ones((128, 128), np.float32)
    m["maskneg"] = np.where(np.arange(128)[None, :] >= np.arange(128)[:, None], 0.0, -30000.0).astype(np.float32)
    sel = np.zeros((16, 16, 128), np.float32)
    for h_ in range(16):
        sel[h_, h_, :] = 1.0
    m["sel"] = sel.reshape(16, 16 * 128)
    return m


RC = 64
LD_C = 0.6065306597126334
GN_EPS = 64e-5


def build_M1(G=None, io=None):
    nc, cx, dram, banks, bankT = _mk_env(G)
    io = io or {}
    cx.open_scope()

    def din(name, shape, dtype=F32):
        if name in io:
            return io[name]
        if name not in dram:
            dram[name] = nc.dram_tensor(name, list(shape), dtype, kind="ExternalInput").ap()
        return dram[name]

    def dout(name, shape, dtype=F32):
        if name in io:
            return io[name]
        dram[name] = nc.dram_tensor(name, list(shape), dtype, kind="ExternalOutput").ap()
        return dram[name]

    BF16S = "bf16_from_f32"

    def load(name, shape, dtype=F32, q="sp"):
        sdt, ddt = (BF16, F32) if dtype == BF16S else (dtype, dtype)
        t = cx.sb(name, shape, sdt)
        sl = cx.slot(name)
        pairs = cast_pairs(t[:], din(name, shape, ddt)) if dtype == BF16S else [(t[:], din(name, shape, ddt))]
        cx.dma(q, sl, pairs, writes=[name])
        return t

    hb = cx.sb("hbuf", [128, KT, SEQ + 1], BF16)
    cx.op("dve", lambda e: e.memset(hb[:, :, 0:1], 0.0), writes=["hb"])
    sl = cx.slot("hT")
    if "hT_pairs" in io:
        cx.dma("sp", sl, io["hT_pairs"](hb, 1), reads=["hb"] + io.get("dep", []), writes=["hb"])
    else:
        hT_d = din("hT", [128, KT, SEQ], BF16)
        cx.dma("sp", sl, [(hb[:, 4 * i:4 * i + 4, 1:SEQ + 1], hT_d[:, 4 * i:4 * i + 4, :]) for i in range(4)],
               reads=["hb"], writes=["hb"])
    ws = WStream(cx, nslot=2, elems=2048)
    ident = load("ident", [128, 128])
    identb = cx.sb("identb", [128, 128], BF16)
    cx.op("dve", lambda e: e.tensor_copy(out=identb[:], in_=ident[:]), reads=["ident"], writes=["identb"])
    mask3 = load("mask3", [128, 384])
    blockones = load("blockones", [128, 128])
    resetm = load("resetmask", [128, SEQ], BF16S, q="pool")
    muT = load("muT", [128, 6, KT])
    w0T = load("w0T", [128, 8])
    a0T = load("a0T", [128, 8])
    kkT = load("k_kT", [128, 8])
    kaT = load("k_aT", [128, 8])
    rkT = load("r_kT", [128, 8])
    lng = load("lng_stack", [128, 8, 64])
    lnb = load("lnb_stack", [128, 8, 64])
    w2c = load("w2c", [96, 1024], BF16S, q="pool")
    a2c = load("a2c", [96, 1024], BF16S, q="pool")
    g2c = load("g2c", [128, 2, 1024], BF16S, q="pool")
    onesb = cx.sb("onesb", [128, 2], BF16)
    cx.op("dve", lambda e: e.memset(onesb[:], 1.0), writes=["onesb"])
    epsg = cx.sb("epsg", [128, 1], F32)
    cx.op("dve", lambda e: e.memset(epsg[:], GN_EPS), writes=["epsg"])
    st_slots = [cx.slot("st0"), cx.slot("st1")]

    wder = [[cx.sb(f"wd{i}{j}", [128, KT, 128], BF16) for j in range(2)] for i in range(2)]
    nder = [0]

    def derive(blk, mu_i, ncol):
        i = nder[0] % 2
        nder[0] += 1
        wv = ws.view(blk)
        w1_, w2_ = wder[i][0], wder[i][1]
        for kt in range(KT):
            eng = "dve" if kt % 2 == 0 else "pool"
            cx.op(eng, lambda e, kt=kt: e.tensor_scalar(out=w2_[:, kt, 0:ncol], in0=wv[:, kt, :], scalar1=muT[:, mu_i, kt:kt + 1],
                                                        scalar2=None, op0=ALU.mult),
                  reads=[ws.key(blk), "muT"], writes=[f"wd{i}1"])
        cx.op("pool", lambda e: e.tensor_tensor(out=w1_[:, :, 0:ncol], in0=wv[:], in1=w2_[:, :, 0:ncol], op=ALU.subtract),
              reads=[ws.key(blk), f"wd{i}1"], writes=[f"wd{i}0"])
        return w1_, w2_, f"wd{i}0", f"wd{i}1"

    pj = [0]

    def proj2(der, ncol, evac):
        w1_, w2_, k1, k2 = der
        for tb in range(4):
            bk = pj[0] % 2
            pj[0] += 1
            for kt in range(KT):
                cx.op("pe", lambda e, kt=kt: e.matmul(banks[bk][0:ncol, :], w1_[:, kt, 0:ncol], hb[:, kt, 1 + tb * 512:1 + (tb + 1) * 512],
                                                       start=(kt == 0), stop=False),
                      reads=[k1, "hb"], writes=[f"bank{bk}"])
            for kt in range(KT):
                cx.op("pe", lambda e, kt=kt: e.matmul(banks[bk][0:ncol, :], w2_[:, kt, 0:ncol], hb[:, kt, tb * 512:(tb + 1) * 512],
                                                       start=False, stop=(kt == KT - 1)),
                      reads=[k2, "hb"], writes=[f"bank{bk}"])
            evac(tb, banks[bk], f"bank{bk}")

    def wblock(name, shape_cols, c0, ncol):
        src = din(name, [D, shape_cols])
        return ws.add(src[:, c0:c0 + ncol].rearrange("(kt p) c -> p kt c", p=128), KT, ncol)

    tw = cx.sb("tw", [96, SEQ], BF16)
    ta = cx.sb("ta", [96, SEQ], BF16)
    tg = cx.sb("tg", [128, 2, SEQ], BF16)
    b_w1 = wblock("w1", 96, 0, 96)
    b_a1 = wblock("a1", 96, 0, 96)
    b_g1 = [wblock("g1", 256, i * 128, 128) for i in range(2)]
    ws.need(b_w1)
    proj2(derive(b_w1, 1, 96), 96, lambda tb, ps, key: cx.op(
        "act", lambda e: e.activation(out=tw[:, tb * 512:(tb + 1) * 512], in_=ps[0:96, :], func=AF.Tanh), reads=[key], writes=["tw"]))
    ws.need(b_a1)
    proj2(derive(b_a1, 4, 96), 96, lambda tb, ps, key: cx.op(
        "act", lambda e: e.activation(out=ta[:, tb * 512:(tb + 1) * 512], in_=ps[0:96, :], func=AF.Copy), reads=[key], writes=["ta"]))
    for i in range(2):
        ws.need(b_g1[i])
        proj2(derive(b_g1[i], 5, 128), 128, lambda tb, ps, key, i=i: cx.op(
            "act", lambda e: e.activation(out=tg[:, i, tb * 512:(tb + 1) * 512], in_=ps[:], func=AF.Sigmoid), reads=[key], writes=["tg"]))

    r_bf = cx.sb("r_bf", [128, SEQ], BF16)
    k32 = cx.sb("k32", [128, SEQ], F32)
    v_bf = cx.sb("v_bf", [128, SEQ], BF16)
    a32 = cx.sb("a32", [128, SEQ], F32)
    kk32 = cx.sb("kk32", [128, SEQ], F32)
    ld32 = cx.sb("ld32", [128, SEQ], F32)
    cl32 = cx.sb("cl32", [128, SEQ], F32)
    ecl = cx.sb("ecl", [128, SEQ], F32)
    g_bf = cx.sb("g_bf", [128, SEQ], BF16)
    yg = cx.sb("yg", [128, SEQ], BF16)
    sqt = [cx.sb("sqt0", [128, 512], F32)] * 2
    Ear = [cx.sb(f"Ear{i}", [128, 256], BF16) for i in range(3)]
    Eb = [cx.sb(f"Eb{i}", [128, 128], BF16) for i in range(3)]
    Ek = [cx.sb(f"Ek{i}", [128, 128], BF16) for i in range(3)]
    Ez = [cx.sb(f"Ez{i}", [128, 128], BF16) for i in range(3)]
    for i in range(3):
        for t_, k_ in ((Ear[i], f"Ear{i}"), (Eb[i], f"Eb{i}"), (Ek[i], f"Ek{i}"), (Ez[i], f"Ez{i}")):
            cx.op("pool", lambda e, t_=t_: e.memset(t_[:], 0.0), writes=[k_])
    EbkT = [cx.sb(f"EbkT{i}", [128, 256], BF16) for i in range(3)]
    Pm = [[cx.sb(f"Pm{q}{i}", [128, 128], F32) for i in range(2)] for q in range(2)]
    PTm = [[cx.sb(f"PTm{q}{i}", [128, 128], F32) for i in range(2)] for q in range(2)]
    Rm = [cx.sb(f"Rm{q}", [128, 128], F32) for q in range(2)]
    Rb = [cx.sb(f"Rb{i}", [128, 128], BF16) for i in range(3)]
    Arb = [cx.sb(f"Arb{i}", [128, 128], BF16) for i in range(3)]
    Aak_rk = [cx.sb(f"Aakrk{i}", [128, 256], BF16) for i in range(3)]
    Vs = [cx.sb(f"Vs{i}", [128, 64], BF16) for i in range(3)]
    Xb = cx.sb("Xb", [128, 64], BF16)
    Ub = cx.sb("Ub", [128, 64], BF16)
    S32 = cx.sb("S32", [128, 64], F32)
    S0b = cx.sb("S0b", [128, 64], BF16)
    ys = cx.sb("ys", [128, 64], F32)
    ysq = cx.sb("ysq", [128, 64], F32)
    yn = cx.sb("yn", [128, 64], F32)
    yob = cx.sb("yob", [128, 64], BF16)
    stat = cx.sb("stat", [128, 8], F32)
    bon = cx.sb("bon", [128, 2], F32)
    yg_d = None if "yg_tile" in io else dout("ygT", [128, 8, SEQ], BF16)

    for P in range(8):
        c0 = P * 128
        b_r = wblock("wr_c", 1024, c0, 128)
        b_k = wblock("wk_c", 1024, c0, 128)
        b_v = wblock("wv_c", 1024, c0, 128)
        ws.need(b_r)
        proj2(derive(b_r, 0, 128), 128, lambda tb, ps, key: cx.op(
            "act", lambda e: e.activation(out=r_bf[:, tb * 512:(tb + 1) * 512], in_=ps[:], func=AF.Copy), reads=[key], writes=["r_bf"]))
        ws.need(b_k)
        proj2(derive(b_k, 2, 128), 128, lambda tb, ps, key: cx.op(
            "act", lambda e: e.activation(out=k32[:, tb * 512:(tb + 1) * 512], in_=ps[:], func=AF.Copy), reads=[key], writes=["k32"]))
        ws.need(b_v)
        proj2(derive(b_v, 3, 128), 128, lambda tb, ps, key: cx.op(
            "act", lambda e: e.activation(out=v_bf[:, tb * 512:(tb + 1) * 512], in_=ps[:], func=AF.Copy), reads=[key], writes=["v_bf"]))
        for tb in range(4):
            ts_ = slice(tb * 512, (tb + 1) * 512)
            bk = pj[0] % 2
            pj[0] += 1
            cx.op("pe", lambda e: e.matmul(banks[bk][:], w2c[:, c0:c0 + 128], tw[:, ts_], start=True, stop=True),
                  reads=["w2c", "tw"], writes=[f"bank{bk}"])
            cx.op("act", lambda e: e.activation(out=ld32[:, ts_], in_=banks[bk][:], func=AF.Sigmoid, bias=w0T[:, P:P + 1], scale=1.0),
                  reads=[f"bank{bk}", "w0T"], writes=["ld32"])
            bk = pj[0] % 2
            pj[0] += 1
            cx.op("pe", lambda e: e.matmul(banks[bk][:], a2c[:, c0:c0 + 128], ta[:, ts_], start=True, stop=True),
                  reads=["a2c", "ta"], writes=[f"bank{bk}"])
            cx.op("act", lambda e: e.activation(out=a32[:, ts_], in_=banks[bk][:], func=AF.Sigmoid, bias=a0T[:, P:P + 1], scale=1.0),
                  reads=[f"bank{bk}", "a0T"], writes=["a32"])
            bk = pj[0] % 2
            pj[0] += 1
            for i in range(2):
                cx.op("pe", lambda e, i=i: e.matmul(banks[bk][:], g2c[:, i, c0:c0 + 128], tg[:, i, ts_], start=(i == 0), stop=(i == 1)),
                      reads=["g2c", "tg"], writes=[f"bank{bk}"])
            cx.op("act", lambda e: e.activation(out=g_bf[:, ts_], in_=banks[bk][:], func=AF.Copy), reads=[f"bank{bk}"], writes=["g_bf"])
        cx.op("dve", lambda e: e.tensor_scalar(out=ld32[:], in0=ld32[:], scalar1=-LD_C, scalar2=None, op0=ALU.mult),
              reads=["ld32"], writes=["ld32"])
        cx.op("dve", lambda e: e.tensor_scalar(out=kk32[:], in0=k32[:], scalar1=kkT[:, P:P + 1], scalar2=None, op0=ALU.mult),
              reads=["k32", "k_kT"], writes=["kk32"])
        for tb in range(4):
            ts_ = slice(tb * 512, (tb + 1) * 512)
            sq_, sk = sqt[0], "sqt0"
            cx.op("act", lambda e: e.activation(out=sq_[:], in_=kk32[:, ts_], func=AF.Square), reads=["kk32"], writes=[sk])
            bk = pj[0] % 2
            pj[0] += 1
            cx.op("pe", lambda e: e.matmul(banks[bk][:], blockones[:], sq_[:], start=True, stop=True),
                  reads=["blockones", sk], writes=[f"bank{bk}"])
            cx.op("act", lambda e: e.activation(out=sq_[:], in_=banks[bk][:], func=AF.Sqrt), reads=[f"bank{bk}"], writes=[sk])
            cx.op("dve", lambda e: e.tensor_scalar(out=sq_[:], in0=sq_[:], scalar1=1e-12, scalar2=None, op0=ALU.max), reads=[sk], writes=[sk])
            cx.op("dve", lambda e: e.reciprocal(out=sq_[:], in_=sq_[:]), reads=[sk], writes=[sk])
            cx.op("dve", lambda e: e.tensor_tensor(out=kk32[:, ts_], in0=kk32[:, ts_], in1=sq_[:], op=ALU.mult),
                  reads=["kk32", sk], writes=["kk32"])
        cx.op("dve", lambda e: e.tensor_scalar(out=ecl[:], in0=a32[:], scalar1=-1.0, scalar2=kaT[:, P:P + 1], op0=ALU.add, op1=ALU.mult),
              reads=["a32", "k_aT"], writes=["ecl"])
        cx.op("dve", lambda e: e.scalar_tensor_tensor(out=k32[:], in0=ecl[:], scalar=1.0, in1=k32[:], op0=ALU.add, op1=ALU.mult),
              reads=["ecl", "k32"], writes=["k32"])
        cx.op("pool", lambda e: e.tensor_tensor(out=a32[:], in0=a32[:], in1=kk32[:], op=ALU.mult), reads=["a32", "kk32"], writes=["a32"])
        cx.op("dve", lambda e: e.tensor_tensor_scan(out=cl32[:], data0=resetm[:], data1=ld32[:], initial=0.0, op0=ALU.mult, op1=ALU.add),
              reads=["resetmask", "ld32"], writes=["cl32"])
        cx.op("pool", lambda e: e.tensor_tensor(out=ld32[:], in0=cl32[:], in1=ld32[:], op=ALU.subtract), reads=["cl32", "ld32"], writes=["ld32"])
        cx.op("act", lambda e: e.activation(out=ld32[:], in_=ld32[:], func=AF.Exp), reads=["ld32"], writes=["ld32"])
        cx.op("act", lambda e: e.activation(out=ecl[:], in_=cl32[:], func=AF.Exp), reads=["cl32", "ecl"], writes=["ecl"])
        cx.op("act", lambda e: e.activation(out=cl32[:], in_=cl32[:], func=AF.Exp, scale=-1.0), reads=["cl32"], writes=["cl32"])
        eclm, encl, beta, kfin = ld32, cl32, a32, k32
        cx.op("dve", lambda e: e.memset(S32[:], 0.0), reads=["S32"], writes=["S32"])
        cx.op("dve", lambda e: e.memset(S0b[:], 0.0), reads=["S0b"], writes=["S0b"])
        def part1a(c):
            cs = slice(c * RC, (c + 1) * RC)
            par = c % 3
            q2 = c % 2
            for hd in range(2):
                R_ = slice(hd * 64, hd * 64 + 64)
                e1, e2 = ("dve", "pool") if hd == 0 else ("pool", "dve")
                cx.op(e1, lambda e: e.scalar_tensor_tensor(out=Ear[par][R_, hd * 64:hd * 64 + 64], in0=kk32[R_, cs], scalar=-1.0, in1=eclm[R_, cs],
                                                           op0=ALU.mult, op1=ALU.mult) if e1 == "dve" else
                      e.tensor_tensor(out=Ear[par][R_, hd * 64:hd * 64 + 64], in0=kk32[R_, cs], in1=eclm[R_, cs], op=ALU.mult),
                      reads=["kk32", "ld32"], writes=[f"Ear{par}"])
                if e1 != "dve":
                    cx.op("pool", lambda e: e.tensor_scalar(out=Ear[par][R_, hd * 64:hd * 64 + 64], in0=Ear[par][R_, hd * 64:hd * 64 + 64],
                                                             scalar1=-1.0, scalar2=None, op0=ALU.mult),
                          reads=[f"Ear{par}"], writes=[f"Ear{par}"])
                cx.op(e2, lambda e: e.tensor_tensor(out=Ear[par][R_, 128 + hd * 64:128 + hd * 64 + 64], in0=r_bf[R_, cs], in1=ecl[R_, cs], op=ALU.mult),
                      reads=["r_bf", "ecl"], writes=[f"Ear{par}"])
                cx.op(e1, lambda e: e.tensor_tensor(out=Eb[par][R_, hd * 64:hd * 64 + 64], in0=beta[R_, cs], in1=encl[R_, cs], op=ALU.mult),
                      reads=["a32", "cl32"], writes=[f"Eb{par}"])
                cx.op(e2, lambda e: e.tensor_tensor(out=Ek[par][R_, hd * 64:hd * 64 + 64], in0=kfin[R_, cs], in1=encl[R_, cs], op=ALU.mult),
                      reads=["k32", "cl32"], writes=[f"Ek{par}"])
                cx.op("dve", lambda e: e.scalar_tensor_tensor(out=Ez[par][R_, hd * 64:hd * 64 + 64], in0=kfin[R_, cs], scalar=rkT[R_, P:P + 1],
                                                               in1=r_bf[R_, cs], op0=ALU.mult, op1=ALU.mult),
                      reads=["k32", "r_kT", "r_bf"], writes=[f"Ez{par}"])
                cx.op("pe", lambda e: e.transpose(bankT[R_, 0:64], v_bf[R_, cs], identb[R_, hd * 64:hd * 64 + 64]),
                      reads=["v_bf", "identb"], writes=["bankT"])
            cx.op("act", lambda e: e.activation(out=Vs[par][:], in_=bankT[:, 0:64], func=AF.Copy), reads=["bankT"], writes=[f"Vs{par}"])
            cx.op("pe", lambda e: e.matmul(banks[2][:, 0:256], Eb[par][:], Ear[par][:], start=True, stop=True),
                  reads=[f"Eb{par}", f"Ear{par}"], writes=["bank2"])
            cx.op("pe", lambda e: e.matmul(banks[3][:, 0:256], Ek[par][:], Ear[par][:], start=True, stop=True),
                  reads=[f"Ek{par}", f"Ear{par}"], writes=["bank3"])
            cx.op("pe", lambda e: e.matmul(banks[2][:, 256:384], Ear[par][:, 0:128], Eb[par][:], start=True, stop=True),
                  reads=[f"Eb{par}", f"Ear{par}"], writes=["bank2"])
            cx.op("pe", lambda e: e.transpose(bankT[:, 128:256], Eb[par][:], identb[:]), reads=[f"Eb{par}", "identb"], writes=["bankT"])
            cx.op("pe", lambda e: e.transpose(bankT[:, 256:384], Ek[par][:], identb[:]), reads=[f"Ek{par}", "identb"], writes=["bankT"])
            cx.op("dve", lambda e: e.tensor_tensor(out=Pm[q2][0][:], in0=banks[2][:, 0:128], in1=mask3[:, 0:128], op=ALU.mult),
                  reads=["bank2", "mask3"], writes=[f"Pm{q2}0"])
            cx.op("dve", lambda e: e.tensor_tensor(out=Arb[par][:], in0=banks[2][:, 128:256], in1=mask3[:, 128:256], op=ALU.mult),
                  reads=["bank2", "mask3"], writes=[f"Arb{par}"])
            cx.op("dve", lambda e: e.tensor_tensor(out=Aak_rk[par][:], in0=banks[3][:, 0:256], in1=mask3[:, 0:256], op=ALU.mult),
                  reads=["bank3", "mask3"], writes=[f"Aakrk{par}"])
            cx.op("dve", lambda e: e.tensor_tensor(out=PTm[q2][0][:], in0=banks[2][:, 256:384], in1=mask3[:, 256:384], op=ALU.mult),
                  reads=["bank2", "mask3"], writes=[f"PTm{q2}0"])
            cx.op("act", lambda e: e.activation(out=EbkT[par][:], in_=bankT[:, 128:384], func=AF.Copy), reads=["bankT"], writes=[f"EbkT{par}"])
            cx.op("pool", lambda e: e.tensor_tensor(out=Rm[q2][:], in0=Pm[q2][0][:], in1=ident[:], op=ALU.add), reads=[f"Pm{q2}0", "ident"], writes=[f"Rm{q2}"])
        def part1b(c):
            par = c % 3
            q2 = c % 2
            cur = 0
            for lvl in range(1, 6):
                nxt = 1 - cur
                if lvl < 5:
                    cx.op("pe", lambda e: e.matmul(banks[5][:, 0:128], PTm[q2][cur][:], Pm[q2][cur][:], start=True, stop=True),
                          reads=[f"PTm{q2}{cur}", f"Pm{q2}{cur}"], writes=["bank5"])
                cx.op("pe", lambda e: e.matmul(banks[6][:, 0:128], Pm[q2][cur][:], PTm[q2][cur][:], start=True, stop=True),
                      reads=[f"PTm{q2}{cur}", f"Pm{q2}{cur}"], writes=["bank6"])
                if lvl < 5:
                    cx.op("act", lambda e: e.activation(out=Pm[q2][nxt][:], in_=banks[5][:, 0:128], func=AF.Copy),
                          reads=["bank5"], writes=[f"Pm{q2}{nxt}"])
                cx.op("dve", lambda e: e.tensor_copy(out=PTm[q2][nxt][:], in_=banks[6][:, 0:128]), reads=["bank6"], writes=[f"PTm{q2}{nxt}"])
                cx.op("pe", lambda e: e.matmul(banks[4][:, 0:128], PTm[q2][nxt][:], Rm[q2][:], start=True, stop=True),
                      reads=[f"PTm{q2}{nxt}", f"Rm{q2}"], writes=["bank4"])
                cx.op("dve", lambda e: e.tensor_tensor(out=Rm[q2][:], in0=Rm[q2][:], in1=banks[4][:, 0:128], op=ALU.add),
                      reads=[f"Rm{q2}", "bank4"], writes=[f"Rm{q2}"])
                cur = nxt
            cx.op("act", lambda e: e.activation(out=Rb[par][:], in_=Rm[q2][:], func=AF.Copy), reads=[f"Rm{q2}"], writes=[f"Rb{par}"])

        def part2(c):
            cs = slice(c * RC, (c + 1) * RC)
            par = c % 3
            cx.op("pe", lambda e: e.matmul(banks[0][:, 0:64], Ear[par][:, 0:128], S0b[:], start=True, stop=False),
                  reads=[f"Ear{par}", "S0b"], writes=["bank0"])
            cx.op("pe", lambda e: e.matmul(banks[0][:, 0:64], Aak_rk[par][:, 0:128], Vs[par][:], start=False, stop=True),
                  reads=[f"Aakrk{par}", f"Vs{par}"], writes=["bank0"])
            cx.op("act", lambda e: e.activation(out=Xb[:], in_=banks[0][:, 0:64], func=AF.Copy), reads=["bank0"], writes=["Xb"])
            cx.op("pe", lambda e: e.matmul(banks[1][:, 0:64], Rb[par][:], Xb[:], start=True, stop=True), reads=[f"Rb{par}", "Xb"], writes=["bank1"])
            cx.op("act", lambda e: e.activation(out=Ub[:], in_=banks[1][:, 0:64], func=AF.Copy), reads=["bank1"], writes=["Ub"])
            cx.op("pe", lambda e: e.matmul(banks[0][:, 0:64], Ear[par][:, 128:256], S0b[:], start=True, stop=False),
                  reads=[f"Ear{par}", "S0b"], writes=["bank0"])
            cx.op("pe", lambda e: e.matmul(banks[0][:, 0:64], Arb[par][:], Ub[:], start=False, stop=False), reads=[f"Arb{par}", "Ub"], writes=["bank0"])
            cx.op("pe", lambda e: e.matmul(banks[0][:, 0:64], Aak_rk[par][:, 128:256], Vs[par][:], start=False, stop=True),
                  reads=[f"Aakrk{par}", f"Vs{par}"], writes=["bank0"])
            cx.op("pe", lambda e: e.matmul(banks[0][:, 64:66], Ez[par][:], onesb[:], start=True, stop=True),
                  reads=[f"Ez{par}", "onesb"], writes=["bank0"])
            cx.op("pe", lambda e: e.matmul(banks[1][:, 0:64], EbkT[par][:, 0:128], Ub[:], start=True, stop=False), reads=[f"EbkT{par}", "Ub"], writes=["bank1"])
            cx.op("pe", lambda e: e.matmul(banks[1][:, 0:64], EbkT[par][:, 128:256], Vs[par][:], start=False, stop=True), reads=[f"EbkT{par}", f"Vs{par}"], writes=["bank1"])
            cx.op("dve", lambda e: e.tensor_tensor(out=S32[:], in0=S32[:], in1=banks[1][:, 0:64], op=ALU.add), reads=["S32", "bank1"], writes=["S32"])
            wc = ecl[:, c * RC + RC - 1:c * RC + RC]
            cx.op("dve", lambda e: e.tensor_scalar(out=S32[:], in0=S32[:], scalar1=wc, scalar2=None, op0=ALU.mult),
                  reads=["S32", "ecl"], writes=["S32"])
            cx.op("pool", lambda e: e.tensor_copy(out=S0b[:], in_=S32[:]), reads=["S32"], writes=["S0b"])
            cx.op("act", lambda e: e.activation(out=ys[:], in_=banks[0][:, 0:64], func=AF.Copy, accum_out=stat[:, 0:1]),
                  reads=["bank0"], writes=["ys", "stat"])
            cx.op("act", lambda e: e.activation(out=ysq[:], in_=ys[:], func=AF.Square, accum_out=stat[:, 1:2]),
                  reads=["ys"], writes=["ysq", "stat"])
            cx.op("dve", lambda e: e.tensor_scalar(out=stat[:, 2:3], in0=stat[:, 0:1], scalar1=1.0 / 64, scalar2=None, op0=ALU.mult),
                  reads=["stat"], writes=["stat"])
            cx.op("dve", lambda e: e.tensor_tensor(out=stat[:, 3:4], in0=stat[:, 2:3], in1=stat[:, 2:3], op=ALU.mult),
                  reads=["stat"], writes=["stat"])
            cx.op("dve", lambda e: e.scalar_tensor_tensor(out=stat[:, 4:5], in0=stat[:, 1:2], scalar=1.0 / 64, in1=stat[:, 3:4],
                                                           op0=ALU.mult, op1=ALU.subtract), reads=["stat"], writes=["stat"])
            cx.op("act", lambda e: e.activation(out=stat[:, 5:6], in_=stat[:, 4:5], func=AF.Sqrt, bias=epsg[:], scale=1.0),
                  reads=["stat", "epsg"], writes=["stat"])
            cx.op("dve", lambda e: e.reciprocal(out=stat[:, 5:6], in_=stat[:, 5:6]), reads=["stat"], writes=["stat"])
            cx.op("dve", lambda e: e.tensor_scalar(out=yn[:], in0=ys[:], scalar1=stat[:, 2:3], scalar2=stat[:, 5:6],
                                                    op0=ALU.subtract, op1=ALU.mult), reads=["ys", "stat"], writes=["yn"])
            cx.op("pool", lambda e: e.tensor_tensor(out=yn[:], in0=yn[:], in1=lng[:, P, :], op=ALU.mult), reads=["yn", "lng_stack"], writes=["yn"])
            cx.op("pool", lambda e: e.tensor_tensor(out=yn[:], in0=yn[:], in1=lnb[:, P, :], op=ALU.add), reads=["yn", "lnb_stack"], writes=["yn"])
            cx.op("act", lambda e: e.activation(out=bon[:], in_=banks[0][:, 64:66], func=AF.Copy), reads=["bank0"], writes=["bon"])
            cx.op("dve", lambda e: e.scalar_tensor_tensor(out=yob[:], in0=Vs[par][:], scalar=bon[:, 0:1], in1=yn[:], op0=ALU.mult, op1=ALU.add),
                  reads=[f"Vs{par}", "bon", "yn"], writes=["yob"])
            for hd in range(2):
                R_ = slice(hd * 64, hd * 64 + 64)
                cx.op("pe", lambda e: e.transpose(bankT[R_, 512:576], yob[R_, :], identb[R_, hd * 64:hd * 64 + 64]),
                      reads=["yob", "identb"], writes=["bankT"])
            cx.op("dve", lambda e: e.tensor_tensor(out=yg[:, cs], in0=bankT[:, 512:576], in1=g_bf[:, cs], op=ALU.mult),
                  reads=["bankT", "g_bf"], writes=["yg"])
        NCH_ = SEQ // RC
        for it in cx.record(part1a, 0):
            cx.play(it)
        cx.play_interleaved(cx.record(part1b, 0), cx.record(part1a, 1))
        for c in range(NCH_):
            la = cx.record(part2, c)
            lb = cx.record(part1b, c + 1) if c + 1 < NCH_ else []
            lc = cx.record(part1a, c + 2) if c + 2 < NCH_ else []
            cx.play_interleaved3(la, lb, lc)
        cx.dma("sp", st_slots[P % 2], [(io["yg_tile"](P) if "yg_tile" in io else yg_d[:, P, :], yg[:])], reads=["yg"])
    cx.close_scope()
    cx.wait_all("sp")
    return nc


def prep_M1(inp, b, j, hT_full):
    m = {"hT": hT_full, "ident": np.eye(128, dtype=np.float32)}
    cs = slice(j * 1024, (j + 1) * 1024)
    m["wr_c"] = np.ascontiguousarray(inp["rwkv_w_r"][0][:, cs])
    m["wk_c"] = np.ascontiguousarray(inp["rwkv_w_k"][0][:, cs])
    m["wv_c"] = np.ascontiguousarray(inp["rwkv_w_v"][0][:, cs])
    m["w1"] = inp["rwkv_w1"][0]
    m["a1"] = inp["rwkv_a1"][0]
    m["g1"] = inp["rwkv_g1"][0]
    m["w2c"] = np.ascontiguousarray(inp["rwkv_w2"][0][:, cs])
    m["a2c"] = np.ascontiguousarray(inp["rwkv_a2"][0][:, cs])
    m["g2c"] = np.ascontiguousarray(inp["rwkv_g2"][0][:, cs].reshape(2, 128, 1024).transpose(1, 0, 2))
    m["muT"] = np.ascontiguousarray(inp["rwkv_mu"][0].reshape(6, KT, 128).transpose(2, 0, 1))
    for nm, key in (("w0T", "rwkv_w0"), ("a0T", "rwkv_a0"), ("k_kT", "rwkv_k_k"), ("k_aT", "rwkv_k_a")):
        m[nm] = np.ascontiguousarray(inp[key][0][cs].reshape(8, 128).T)
    m["r_kT"] = np.ascontiguousarray(inp["rwkv_r_k"][0].reshape(-1)[cs].reshape(8, 128).T)
    lg = inp["rwkv_ln_g"][0][cs].reshape(8, 2, 64)
    lb = inp["rwkv_ln_b"][0][cs].reshape(8, 2, 64)
    lng = np.zeros((128, 8, 64), np.float32)
    lnb = np.zeros((128, 8, 64), np.float32)
    for hd in range(2):
        lng[hd * 64:(hd + 1) * 64] = lg[None, :, hd, :]
        lnb[hd * 64:(hd + 1) * 64] = lb[None, :, hd, :]
    m["lng_stack"], m["lnb_stack"] = lng, lnb
    s_ = np.arange(64)
    blk = np.kron(np.eye(2, dtype=np.float32), np.ones((64, 64), np.float32))
    mS = np.kron(np.eye(2, dtype=np.float32), (s_[:, None] < s_[None, :]).astype(np.float32))
    mI = np.kron(np.eye(2, dtype=np.float32), (s_[:, None] <= s_[None, :]).astype(np.float32))
    m["mask3"] = np.ascontiguousarray(np.concatenate([mS, mI, mS.T], axis=1))
    m["blockones"] = blk
    rm = np.ones((128, SEQ), np.float32)
    rm[:, ::RC] = 0.0
    m["resetmask"] = rm
    return m


def _T_maps(inp, stages, xT, extra):
    maps = []
    for core in range(NCORES):
        b = core // 2
        m = {"xT": xT[core], "cT": fm(inp["c"][b])}
        for stg in stages:
            kind = stg["kind"]
            if kind == "ffn":
                l, s_ = stg["l"], stg["s"]
                fi = 0 if s_ == 0 else 1
                m[f"w_mod{l}"] = inp["w_mod"][l]
                m[f"b_modT{l}"] = fm(inp["b_mod"][l])
                m[f"norm_gT{l}{s_}"] = fm(inp["norm_g"][l, s_])
                m[f"ffn_w1_{l}{fi}"] = inp["ffn_w1"][l, fi]
                m[f"ffn_w3_{l}{fi}"] = inp["ffn_w3"][l, fi]
                m[f"ffn_w2_{l}{fi}"] = inp["ffn_w2"][l, fi]
            elif kind == "h_out":
                l = stg["l"]
                m[f"w_mod{l}"] = inp["w_mod"][l]
                m[f"b_modT{l}"] = fm(inp["b_mod"][l])
                m[f"norm_gT{l}1"] = fm(inp["norm_g"][l, 1])
            elif kind == "mix0_post":
                m["w_mod0"] = inp["w_mod"][0]
                m["b_modT0"] = fm(inp["b_mod"][0])
                m["glu_bT"] = fm(inp["s5_glu_b"][0])
                m["ssd_norm_gT"] = fm(inp["ssd_norm_g"][0])
                m["s5_glu_w"] = inp["s5_glu_w"][0]
                m["hyb_w_out"] = inp["hyb_w_out"][0]
            elif kind == "rwkv_post":
                m["w_mod1"] = inp["w_mod"][1]
                m["b_modT1"] = fm(inp["b_mod"][1])
                m["rwkv_w_o"] = inp["rwkv_w_o"][0]
            elif kind == "final":
                m["final_gT"] = fm(inp["final_g"])
        m.update(extra[core])
        maps.append(m)
    return maps


def _run(nc, maps):
    return run_bass_kernel_spmd(nc, maps, core_ids=list(range(NCORES))).results


def _only_declared(maps):
    keep = set(LAST_DRAM.keys())
    return [{k: v for k, v in m.items() if k in keep} for m in maps]


def _pair_cat_tokens(tiles, b):
    return np.ascontiguousarray(np.concatenate([tiles[2 * b], tiles[2 * b + 1]], axis=2))


GROUPS = [[0, 1], [2, 3], [4, 5], [6, 7]]
ST0 = [{"kind": "ffn", "l": 0, "s": 0}, {"kind": "h_out", "l": 0}, {"kind": "x_out"}]
ST1 = [{"kind": "mix0_post"}, {"kind": "ffn", "l": 0, "s": 2}, {"kind": "ffn", "l": 1, "s": 0},
       {"kind": "h_out", "l": 1}, {"kind": "x_out"}]
ST2 = [{"kind": "rwkv_post"}, {"kind": "ffn", "l": 1, "s": 2}, {"kind": "final"}]


def build_fused(upto=None):
    nc = bass.Bass("TRN2", target_bir_lowering=False)
    cx = Ctx(nc)
    banks = [cx.ps(f"bank{i}") for i in range(7)]
    cx.uid += 1
    bankT = nc.alloc_psum_tensor(f"bankT_{cx.uid}", [128, 1024], BF16)
    G = {"nc": nc, "cx": cx, "dram": {}, "banks": banks, "bankT": bankT}

    def idram(name, shape, dt):
        return nc.dram_tensor(name, list(shape), dt).ap()

    ncc = [0]

    def allgather(src, dst):
        sl = cx.slot("cc")
        nc.gpsimd.collective_compute("AllGather", ALU.bypass, replica_groups=GROUPS,
                                     ins=[src.opt()], outs=[dst.opt()]).then_inc(sl["sem"])
        sl["count"] += 1
        ncc[0] += 1
        cx._record((sl["sem"], sl["count"], sl["key"]), [], [f"cc{ncc[0]}"])
        return f"cc{ncc[0]}"

    CH = 4096

    def chunks(name, ncol, dt):
        n = ncol // CH
        return ([idram(f"{name}_s{i}", [128, CH], dt) for i in range(n)],
                [idram(f"{name}_g{i}", [256, CH], dt) for i in range(n)])

    def gather_all(snd, rcv):
        return [allgather(a, b) for a, b in zip(snd, rcv)]

    def h_out_pairs(snd):
        return lambda h: [(snd[c].rearrange("p (k t) -> p k t", k=4), h[:, 4 * c:4 * c + 4, :]) for c in range(4)]

    def h_loader(rcv):
        def f(tile, off):
            pairs = []
            for r in range(2):
                for c in range(4):
                    src = rcv[c][r * 128:(r + 1) * 128, :].rearrange("p (k t) -> p k t", k=4)
                    pairs.append((tile[:, 4 * c:4 * c + 4, off + r * TOK:off + (r + 1) * TOK], src))
            return pairs
        return f

    def tile_fn(snd):
        return lambda idx: snd[idx // 2][:, (idx % 2) * SEQ:(idx % 2 + 1) * SEQ]

    def gath_fn(rcv):
        return lambda r, lt, half: rcv[lt // 2][r * 128:(r + 1) * 128, (lt % 2) * SEQ + half * TOK:(lt % 2) * SEQ + (half + 1) * TOK]

    xs1 = idram("xs1", [128, KT, TOK], F32)
    xs2 = idram("xs2", [128, KT, TOK], F32)
    h0s, h0g = chunks("h0", KT * TOK, BF16)
    h1s, h1g = chunks("h1", KT * TOK, BF16)
    y0s, y0g = chunks("y0", 12 * SEQ, F32)
    y1s, y1g = chunks("y1", 8 * SEQ, BF16)

    def dbg(n, src, shape, dt):
        if upto != n:
            return False
        cx.barrier()
        o = nc.dram_tensor("dbg", list(shape), dt, kind="ExternalOutput").ap()
        cx.dma("sp", cx.slot("dbg"), [(o, src)])
        cx.wait_all("sp")
        return True

    global LAST_DRAM
    LAST_DRAM = G["dram"]
    build_T(ST0, G, io={"hT_out_pairs": h_out_pairs(h0s), "xT_out": xs1})
    if dbg(1, xs1, [128, KT, TOK], F32):
        return nc
    cx.new_phase()
    dep = gather_all(h0s, h0g)
    if dbg(2, h0g[3], [256, CH], BF16):
        return nc
    build_M0(G=G, io={"hT_pairs": h_loader(h0g), "y_tile": tile_fn(y0s), "dep": dep})
    if dbg(3, y0s[0], [128, CH], F32):
        return nc
    cx.new_phase()
    dep = gather_all(y0s, y0g)
    if dbg(4, y0g[5], [256, CH], F32):
        return nc
    build_T(ST1, G, io={"xT": xs1, "y_gath": gath_fn(y0g), "hT_out_pairs": h_out_pairs(h1s), "xT_out": xs2, "dep": dep})
    if dbg(5, xs2, [128, KT, TOK], F32):
        return nc
    cx.new_phase()
    dep = gather_all(h1s, h1g)
    build_M1(G=G, io={"hT_pairs": h_loader(h1g), "yg_tile": tile_fn(y1s), "dep": dep})
    if dbg(6, y1s[0], [128, CH], BF16):
        return nc
    cx.new_phase()
    dep = gather_all(y1s, y1g)
    build_T(ST2, G, io={"xT": xs2, "yg_gath": gath_fn(y1g), "dep": dep})
    cx.wait_all("sp")
    return nc


LAST_DRAM = {}


def kernel_unfused(**inputs):
    inp = {k: np.asarray(v) for k, v in inputs.items()}
    xT = to_xT(inp["x"].astype(np.float32, copy=False))
    st0 = [{"kind": "ffn", "l": 0, "s": 0}, {"kind": "h_out", "l": 0}, {"kind": "x_out"}]
    r = _run(build_T(st0), _T_maps(inp, st0, xT, [{}] * NCORES))
    xT = [np.asarray(q["xT_out"]) for q in r]
    hT = [np.asarray(q["hT_out"]) for q in r]
    maps = [prep_M0(inp, c // 2, c % 2, _pair_cat_tokens(hT, c // 2)) for c in range(NCORES)]
    r = _run(build_M0(), maps)
    y5 = [np.asarray(q["y5T"]) for q in r]
    ys = [np.asarray(q["ysT"]) for q in r]
    extra = []
    for c in range(NCORES):
        b, jt = c // 2, c % 2
        ts = slice(jt * TOK, (jt + 1) * TOK)
        extra.append({"y5T_in": np.ascontiguousarray(np.concatenate([y5[2 * b][:, :, ts], y5[2 * b + 1][:, :, ts]], axis=1)),
                      "ysT_in": np.ascontiguousarray(np.concatenate([ys[2 * b][:, :, ts], ys[2 * b + 1][:, :, ts]], axis=1))})
    st1 = [{"kind": "mix0_post"}, {"kind": "ffn", "l": 0, "s": 2}, {"kind": "ffn", "l": 1, "s": 0},
           {"kind": "h_out", "l": 1}, {"kind": "x_out"}]
    r = _run(build_T(st1), _T_maps(inp, st1, xT, extra))
    xT = [np.asarray(q["xT_out"]) for q in r]
    hT = [np.asarray(q["hT_out"]) for q in r]
    maps = [prep_M1(inp, c // 2, c % 2, _pair_cat_tokens(hT, c // 2)) for c in range(NCORES)]
    r = _run(build_M1(), maps)
    yg = [np.asarray(q["ygT"]) for q in r]
    extra = []
    for c in range(NCORES):
        b, jt = c // 2, c % 2
        ts = slice(jt * TOK, (jt + 1) * TOK)
        extra.append({"ygT_in": np.ascontiguousarray(np.concatenate([yg[2 * b][:, :, ts], yg[2 * b + 1][:, :, ts]], axis=1))})
    st2 = [{"kind": "rwkv_post"}, {"kind": "ffn", "l": 1, "s": 2}, {"kind": "final"}]
    r = _run(build_T(st2), _T_maps(inp, st2, xT, extra))
    out = from_xT([np.asarray(q["xT_out"]) for q in r])
    return out.astype(np.float32)


def fused_maps(inp):
    xT = to_xT(inp["x"].astype(np.float32, copy=False))
    maps = []
    for c in range(NCORES):
        b, j = c // 2, c % 2
        m = {}
        for st in (ST0, ST1, ST2):
            m.update(_T_maps(inp, st, xT, [{}] * NCORES)[c])
        m0 = prep_M0(inp, b, j, None)
        m1 = prep_M1(inp, b, j, None)
        m0.pop("hT")
        m1.pop("hT")
        m.update(m0)
        m.update(m1)
        sel = np.zeros((128, 2), np.float32)
        sel[:, j] = 1.0
        m["selT"] = sel
        maps.append(m)
    return maps


def kernel(**inputs):
    inp = {k: np.asarray(v) for k, v in inputs.items()}
    nc = build_fused()
    r = _run(nc, _only_declared(fused_maps(inp)))
    out = from_xT([np.asarray(q["xT_out"]) for q in r])
    return out.astype(np.float32)
```

```python
import numpy as np
import concourse.bass as bass
import concourse.mybir as mybir
from concourse.bass_utils import run_bass_kernel_spmd

F32 = mybir.dt.float32
BF16 = mybir.dt.bfloat16
AF = mybir.ActivationFunctionType
ALU = mybir.AluOpType

D = 2048
KT = 16
FFN = 5632
NCORES = 8
TOK = 1024
SEQ = 2048
EPS = 1e-6


SKIP_SELF = {"pe"}


class _Eng:
    def __init__(self, name, obj, sem):
        self.name, self.obj, self.sem = name, obj, sem
        self.count = 0
        self.seen = {}


class _Rec:
    def __getattr__(self, name):
        def f(*a, **k):
            self.call = (name, a, k)
            return self
        return f


class Ctx:
    def record(self, fn, *args):
        self.rec = []
        fn(*args)
        lst, self.rec = self.rec, None
        return lst

    def play(self, item):
        engname, (name, a, k), reads, writes = item
        return self.op(engname, lambda e: getattr(e, name)(*a, **k), reads, writes)

    def play_interleaved(self, la, lb):
        i = j = 0
        na, nb = len(la), len(lb)
        while i < na or j < nb:
            if i < na and (j >= nb or i * nb <= j * na):
                self.play(la[i])
                i += 1
            else:
                self.play(lb[j])
                j += 1

    def play_interleaved3(self, la, lb, lc):
        lists = [l for l in (la, lb, lc) if l]
        pos = [0] * len(lists)
        while any(p < len(l) for p, l in zip(pos, lists)):
            k = min((i for i in range(len(lists)) if pos[i] < len(lists[i])), key=lambda i: pos[i] / len(lists[i]))
            self.play(lists[k][pos[k]])
            pos[k] += 1

    def __init__(self, nc):
        self.nc = nc
        self.engs = {}
        for name, attr in (("pe", "tensor"), ("act", "scalar"), ("dve", "vector"),
                           ("pool", "gpsimd"), ("sp", "sync")):
            self.engs[name] = _Eng(name, getattr(nc, attr), nc.alloc_semaphore("sem_" + name))
        self.res = {}
        self.nslots = 0
        self.uid = 0
        self.stacks = []
        self.free_slots = []
        self.phase = 0
        self.rec = None

    def sb(self, name, shape, dtype=F32):
        self.uid += 1
        if self.stacks:
            return self.stacks[-1][0].enter_context(self.nc.sbuf_tensor(f"{name}_{self.uid}", list(shape), dtype))
        return self.nc.alloc_sbuf_tensor(f"{name}_{self.uid}", list(shape), dtype)

    def open_scope(self):
        import contextlib
        self.stacks.append((contextlib.ExitStack(), []))

    def close_scope(self):
        self.barrier()
        st, slots = self.stacks.pop()
        st.close()
        self.free_slots.extend(slots)

    def ps(self, name):
        self.uid += 1
        return self.nc.alloc_psum_tensor(f"{name}_{self.uid}", [128, 512], F32)

    def slot(self, name):
        if self.free_slots:
            sl = self.free_slots.pop()
        else:
            self.nslots += 1
            sl = {"sem": self.nc.alloc_semaphore(f"dsem_{name}_{self.nslots}"), "count": 0,
                  "key": f"slot{self.nslots}"}
        if self.stacks:
            self.stacks[-1][1].append(sl)
        return sl

    def _deps(self, reads, writes):
        deps = []
        for r in reads:
            st = self.res.get(r)
            if st and st["w"]:
                deps.append(st["w"])
            if st and r.startswith("bank"):
                deps.extend(st["r"].values())
        for w in writes:
            st = self.res.get(w)
            if st:
                if st["w"]:
                    deps.append(st["w"])
                deps.extend(st["r"].values())
        return deps

    def _wait(self, eng, deps, skip_self):
        for sem, val, key in deps:
            if skip_self and key == eng.name:
                continue
            if eng.seen.get(key, 0) < val:
                eng.obj.wait_ge(sem, val)
                eng.seen[key] = val

    def _record(self, tok, reads, writes):
        for r in reads:
            st = self.res.setdefault(r, {"w": None, "r": {}})
            st["r"][tok[2]] = tok
        for w in writes:
            self.res[w] = {"w": tok, "r": {}}

    def op(self, engname, emit, reads=(), writes=()):
        if self.rec is not None:
            r = _Rec()
            emit(r)
            self.rec.append((engname, r.call, tuple(reads), tuple(writes)))
            return None
        eng = self.engs[engname]
        self._wait(eng, self._deps(reads, writes), skip_self=(engname in SKIP_SELF))
        inst = emit(eng.obj)
        eng.count += 1
        inst.then_inc(eng.sem, 1)
        tok = (eng.sem, eng.count, engname)
        eng.seen[engname] = max(eng.seen.get(engname, 0), 0)
        self._record(tok, reads, writes)
        return tok

    def dma(self, qname, slot, pairs, reads=(), writes=(), **kw):
        eng = self.engs[qname]
        self._wait(eng, self._deps(reads, writes), skip_self=False)
        for out, in_ in pairs:
            eng.obj.dma_start(out=out, in_=in_, **kw).then_inc(slot["sem"], 16)
            slot["count"] += 16
        tok = (slot["sem"], slot["count"], slot["key"])
        self._record(tok, reads, writes)
        return tok

    def wait_all(self, engname):
        eng = self.engs[engname]
        deps = []
        for e in self.engs.values():
            if e.count:
                deps.append((e.sem, e.count, e.name))
        for st in self.res.values():
            if st["w"]:
                deps.append(st["w"])
            deps.extend(st["r"].values())
        self._wait(eng, deps, skip_self=False)

    def barrier(self):
        for n in self.engs:
            self.wait_all(n)

    def new_phase(self):
        self.barrier()
        self.phase += 1
        for e in self.engs.values():
            e.sem = self.nc.alloc_semaphore(f"sem_{e.name}_p{self.phase}")
            e.count = 0
            e.seen = {}
        self.res = {}


class WStream:
    NSLOT = 4
    ELEMS = 8192

    def __init__(self, cx, nslot=4, elems=8192):
        self.cx = cx
        self.NSLOT, self.ELEMS = nslot, elems
        self.tiles = [cx.sb(f"wslot{i}", [128, self.ELEMS], BF16) for i in range(self.NSLOT)]
        self.slots = [cx.slot(f"w{i}") for i in range(self.NSLOT)]
        self.plan = []
        self.issued = 0

    def add(self, src_ap, a, b):
        assert a * b <= self.ELEMS
        self.plan.append((src_ap, a, b))
        return len(self.plan) - 1

    def view(self, i):
        _, a, b = self.plan[i]
        t = self.tiles[i % self.NSLOT]
        return t[:, 0:a * b].rearrange("p (a b) -> p a b", a=a)

    def key(self, i):
        return f"wslot{i % self.NSLOT}"

    def _issue(self, i):
        src, a, b = self.plan[i]
        v = self.view(i)
        pairs = [(v[:, :, c0:min(b, c0 + 1024)], src[:, :, c0:min(b, c0 + 1024)]) for c0 in range(0, b, 1024)]
        self.cx.dma("pool", self.slots[i % self.NSLOT], pairs, writes=[self.key(i)])

    def need(self, i):
        upto = min(len(self.plan), i + self.NSLOT)
        while self.issued < upto:
            self._issue(self.issued)
            self.issued += 1


def _mk_env(G):
    if G is not None:
        return G["nc"], G["cx"], G["dram"], G["banks"], G["bankT"]
    nc = bass.Bass("TRN2", target_bir_lowering=False)
    cx = Ctx(nc)
    banks = [cx.ps(f"bank{i}") for i in range(7)]
    cx.uid += 1
    bankT = nc.alloc_psum_tensor(f"bankT_{cx.uid}", [128, 1024], BF16)
    return nc, cx, {}, banks, bankT


def build_T(stages, G=None, io=None):
    nc, cx, dram, banks, bankT = _mk_env(G)
    io = io or {}
    cx.open_scope()

    def din(name, shape, dtype=F32):
        if name in io:
            return io[name]
        if name not in dram:
            dram[name] = nc.dram_tensor(name, list(shape), dtype, kind="ExternalInput").ap()
        return dram[name]

    def dout(name, shape, dtype=F32):
        if name in io:
            return io[name]
        dram[name] = nc.dram_tensor(name, list(shape), dtype, kind="ExternalOutput").ap()
        return dram[name]

    xT_d = din("xT", [128, KT, TOK])
    cT_d = din("cT", [128, KT])

    x = cx.sb("x", [128, KT, TOK], F32)
    sq = [cx.sb(f"sq{i}", [128, TOK], F32) for i in range(2)]
    rstd = cx.sb("rstd", [128, TOK], F32)
    cact = cx.sb("cact", [128, KT], BF16)
    cin = cx.sb("cin", [128, KT], F32)
    ones = cx.sb("ones", [128, 128], F32)
    ws = WStream(cx)
    ld = cx.slot("ld")
    ld2 = cx.slot("ld2")

    cx.op("dve", lambda e: e.memset(ones[:], 1.0), writes=["ones"])
    cx.dma("sp", ld, [(x[:, 0:KT // 2, :], xT_d[:, 0:KT // 2, :]), (x[:, KT // 2:KT, :], xT_d[:, KT // 2:KT, :])],
           writes=["x"])
    cx.dma("sp", ld2, [(cin[:], cT_d)], writes=["cin"])
    cx.op("act", lambda e: e.activation(out=cact[:], in_=cin[:], func=AF.Silu),
          reads=["cin"], writes=["cact"])

    small_id = [0]

    def load_small(name, shape):
        small_id[0] += 1
        t = cx.sb(name, shape, F32)
        sl = cx.slot(name)
        cx.dma("sp", sl, [(t[:], din(name, shape))], writes=[name + str(small_id[0])])
        return t, name + str(small_id[0])

    def compute_mod(l, s, which):
        wmod = din(f"w_mod{l}", [D, 9 * D])
        bmod, bkey = load_small(f"b_modT{l}", [128, 9 * KT])
        out = cx.sb(f"mod{l}{s}", [128, 3, KT], F32)
        okey = f"mod{l}{s}_" + "".join(str(w) for w in which)
        blocks = []
        for j in which:
            for cb in range(D // 512):
                c0 = s * 3 * D + j * D + cb * 512
                src = wmod[:, c0:c0 + 512].rearrange("(kt p) c -> p kt c", p=128)
                blocks.append((j, cb, ws.add(src, KT, 512)))
        bank = banks[6]
        for (j, cb, bid) in blocks:
            ws.need(bid)
            wv = ws.view(bid)
            for ft in range(4):
                for kt in range(KT):
                    cx.op("pe", lambda e, ft=ft, kt=kt, wv=wv: e.matmul(
                        bank[:, ft:ft + 1], wv[:, kt, ft * 128:(ft + 1) * 128], cact[:, kt:kt + 1],
                        start=(kt == 0), stop=(kt == KT - 1)),
                        reads=[ws.key(bid), "cact"], writes=["bank6"])
            jj = s * 3 + j
            col = jj * KT + cb * 4
            cx.op("dve", lambda e, j=j, cb=cb, col=col: e.tensor_tensor(
                out=out[:, j, cb * 4:cb * 4 + 4], in0=bank[:, 0:4], in1=bmod[:, col:col + 4], op=ALU.add),
                reads=["bank6", bkey], writes=[okey])
        return out, okey

    def rms_stats(xkey="x"):
        for kt in range(KT):
            s_ = sq[kt % 2]
            cx.op("act", lambda e, kt=kt, s_=s_: e.activation(out=s_[:], in_=x[:, kt, :], func=AF.Square),
                  reads=[xkey], writes=[f"sq{kt % 2}"])
            for t in range(2):
                cx.op("pe", lambda e, kt=kt, t=t, s_=s_: e.matmul(
                    banks[4 + t][:], ones[:], s_[:, t * 512:(t + 1) * 512],
                    start=(kt == 0), stop=(kt == KT - 1)),
                    reads=[f"sq{kt % 2}", "ones"], writes=[f"bank{4 + t}"])
        for t in range(2):
            cx.op("act", lambda e, t=t: e.activation(out=rstd[:, t * 512:(t + 1) * 512], in_=banks[4 + t][:],
                                                      func=AF.Sqrt, scale=1.0 / D, bias=epsb[:]),
                  reads=[f"bank{4 + t}", "epsb"], writes=["rstd"])
        cx.op("dve", lambda e: e.reciprocal(out=rstd[:], in_=rstd[:]), reads=["rstd"], writes=["rstd"])

    epsb = cx.sb("epsb", [128, 1], F32)
    cx.op("dve", lambda e: e.memset(epsb[:], EPS), writes=["epsb"])

    def adaln(l, s, mod, mkey, dst, dkey):
        ng, ngkey = load_small(f"norm_gT{l}{s}", [128, KT])
        a = cx.sb(f"a{l}{s}", [128, KT], F32)
        akey = f"a{l}{s}"
        cx.op("dve", lambda e: e.scalar_tensor_tensor(out=a[:], in0=mod[:, 1, :], scalar=1.0, in1=ng[:],
                                                      op0=ALU.add, op1=ALU.mult),
              reads=[mkey, ngkey], writes=[akey])
        rms_stats()
        for kt in range(KT):
            s_ = sq[kt % 2]
            cx.op("dve", lambda e, kt=kt, s_=s_: e.scalar_tensor_tensor(
                out=s_[:], in0=x[:, kt, :], scalar=a[:, kt:kt + 1], in1=rstd[:],
                op0=ALU.mult, op1=ALU.mult),
                reads=["x", akey, "rstd"], writes=[f"sq{kt % 2}"])
            cx.op("act", lambda e, kt=kt, s_=s_: e.activation(
                out=dst[:, kt, :], in_=s_[:], func=AF.Identity, bias=mod[:, 0, kt:kt + 1], scale=1.0),
                reads=[f"sq{kt % 2}", mkey], writes=[dkey])

    def ffn(l, s):
        fi = 0 if s == 0 else 1
        w1 = din(f"ffn_w1_{l}{fi}", [D, FFN])
        w3 = din(f"ffn_w3_{l}{fi}", [D, FFN])
        w2 = din(f"ffn_w2_{l}{fi}", [FFN, D])
        cx.open_scope()
        h = cx.sb("h", [128, KT, TOK], BF16)
        g = [cx.sb(f"g{i}", [128, 4, TOK], BF16) for i in range(2)]
        silu_t = [cx.sb(f"silu{i}", [128, 512], F32) for i in range(2)]
        mod, mkey = MODS[(l, s)]
        adaln(l, s, mod, mkey, h, "h")
        hg = cx.sb(f"hg{l}{s}", [128, KT], F32)
        cx.op("dve", lambda e: e.tensor_scalar(out=hg[:], in0=mod[:, 2, :], scalar1=0.5, scalar2=None,
                                               op0=ALU.mult),
              reads=[mkey], writes=[f"hg{l}{s}"])
        NCH = FFN // 512
        blk = []
        for c in range(NCH):
            b1 = ws.add(w1[:, c * 512:(c + 1) * 512].rearrange("(kt p) c -> p kt c", p=128), KT, 512)
            b3 = ws.add(w3[:, c * 512:(c + 1) * 512].rearrange("(kt p) c -> p kt c", p=128), KT, 512)
            b2 = ws.add(w2[c * 512:(c + 1) * 512, :].rearrange("(kt p) c -> p kt c", p=128), 4, D)
            blk.append((b1, b3, b2))
        ev = 0
        for c in range(NCH):
            b1, b3, b2 = blk[c]
            gb = g[c % 2]
            gkey = f"g{c % 2}"
            ws.need(b1)
            w1v, w3v = ws.view(b1), ws.view(b3)
            for m in range(4):
                for t in range(2):
                    pa, pb = banks[t * 2], banks[t * 2 + 1]
                    ka, kb = f"bank{t * 2}", f"bank{t * 2 + 1}"
                    for kt in range(KT):
                        cx.op("pe", lambda e, kt=kt, m=m, t=t, pa=pa: e.matmul(
                            pa[:], w1v[:, kt, m * 128:(m + 1) * 128], h[:, kt, t * 512:(t + 1) * 512],
                            start=(kt == 0), stop=(kt == KT - 1)),
                            reads=[ws.key(b1), "h"], writes=[ka])
                    for kt in range(KT):
                        cx.op("pe", lambda e, kt=kt, m=m, t=t, pb=pb: e.matmul(
                            pb[:], w3v[:, kt, m * 128:(m + 1) * 128], h[:, kt, t * 512:(t + 1) * 512],
                            start=(kt == 0), stop=(kt == KT - 1)),
                            reads=[ws.key(b3), "h"], writes=[kb])
                    st_ = silu_t[ev % 2]
                    skey = f"silu{ev % 2}"
                    ev += 1
                    cx.op("act", lambda e, pa=pa, st_=st_: e.activation(out=st_[:], in_=pa[:], func=AF.Silu),
                          reads=[ka], writes=[skey])
                    cx.op("dve", lambda e, pb=pb, st_=st_, m=m, t=t, gb=gb: e.tensor_tensor(
                        out=gb[:, m, t * 512:(t + 1) * 512], in0=st_[:], in1=pb[:], op=ALU.mult),
                        reads=[skey, kb], writes=[gkey])
            ws.need(b2)
            w2v = ws.view(b2)
            for j in range(KT):
                for t in range(2):
                    bi = 4 + ((j * 2 + t) % 3)
                    po, ko = banks[bi], f"bank{bi}"
                    for m in range(4):
                        cx.op("pe", lambda e, m=m, j=j, t=t, po=po, gb=gb: e.matmul(
                            po[:], w2v[:, m, j * 128:(j + 1) * 128], gb[:, m, t * 512:(t + 1) * 512],
                            start=(m == 0), stop=(m == 3)),
                            reads=[ws.key(b2), gkey], writes=[ko])
                    cx.op("dve", lambda e, j=j, t=t, po=po: e.scalar_tensor_tensor(
                        out=x[:, j, t * 512:(t + 1) * 512], in0=po[:], scalar=hg[:, j:j + 1],
                        in1=x[:, j, t * 512:(t + 1) * 512], op0=ALU.mult, op1=ALU.add),
                        reads=[ko, f"hg{l}{s}"], writes=["x"])
        cx.close_scope()

    def outproj(wname, krows, src, skey, nkt, gate, gkey):
        wd = din(wname, [krows, D])
        blks = [ws.add(wd[:, j * 128:(j + 1) * 128].rearrange("(kt p) c -> p kt c", p=128), nkt, 128) for j in range(KT)]
        for j in range(KT):
            ws.need(blks[j])
            wv = ws.view(blks[j])
            for t in range(2):
                bi = 4 + ((j * 2 + t) % 3)
                po, ko = banks[bi], f"bank{bi}"
                for kt in range(nkt):
                    cx.op("pe", lambda e, kt=kt: e.matmul(po[:], wv[:, kt, :], src[:, kt, t * 512:(t + 1) * 512],
                                                           start=(kt == 0), stop=(kt == nkt - 1)),
                          reads=[ws.key(blks[j]), skey], writes=[ko])
                cx.op("dve", lambda e: e.scalar_tensor_tensor(
                    out=x[:, j, t * 512:(t + 1) * 512], in0=po[:], scalar=gate[:, j:j + 1],
                    in1=x[:, j, t * 512:(t + 1) * 512], op0=ALU.mult, op1=ALU.add),
                    reads=[ko, gkey], writes=["x"])

    def gath_select(gath, ntile, dests, gdt):
        selT, selk = load_small("selT", [128, 2])
        if gdt == F32:
            stA, kA = sq, ["sq0", "sq1"]
        else:
            stA, kA = [cx.sb(f"gsA{i}", [128, TOK], gdt) for i in range(2)], ["gsA0", "gsA1"]
        stB = [cx.sb(f"gsB{i}", [128, TOK], gdt) for i in range(2)]
        sls = [cx.slot(f"gs{i}") for i in range(2)]
        n = 0
        for dst, dkey, tiles in dests:
            for di, (r, lt) in enumerate(tiles):
                b_ = n % 2
                n += 1
                cx.dma("sp", sls[b_], [(stA[b_][:], gath(r, lt, 0)), (stB[b_][:], gath(r, lt, 1))],
                       reads=io.get("dep", []), writes=[kA[b_], f"gsB{b_}"])
                cx.op("dve", lambda e: e.tensor_scalar(out=stB[b_][:], in0=stB[b_][:], scalar1=selT[:, 1:2], scalar2=None, op0=ALU.mult),
                      reads=[f"gsB{b_}", selk], writes=[f"gsB{b_}"])
                cx.op("dve", lambda e: e.scalar_tensor_tensor(out=dst[:, di, :], in0=stA[b_][:], scalar=selT[:, 0:1], in1=stB[b_][:],
                                                               op0=ALU.mult, op1=ALU.add),
                      reads=[kA[b_], f"gsB{b_}", selk], writes=[dkey])

    def mix0_post():
        cx.open_scope()
        mod, mkey = MODS[(0, 1, "g")]
        y5b = cx.sb("y5b", [128, 8, TOK], BF16)
        ysb = cx.sb("ysb", [128, 16, TOK], BF16)
        sig = cx.sb("sig", [128, 8, 512], BF16)
        sl5, sls = cx.slot("y5in"), cx.slot("ysin")
        if "y_gath" not in io:
            y5_d = din("y5T_in", [128, 8, TOK])
            ys_d = din("ysT_in", [128, 16, TOK])
        if "y_gath" in io:
            gath_select(io["y_gath"], 12, [(y5b, "y5b", [(ft // 4, ft % 4) for ft in range(8)]),
                                           (ysb, "ysb", [(kt // 8, 4 + kt % 8) for kt in range(16)])], F32)
        else:
            cx.dma("pool", sl5, cast_pairs(y5b[:], y5_d), writes=["y5b"])
            cx.dma("pool", sls, cast_pairs(ysb[:], ys_d), writes=["ysb"])
        glub, gbk = load_small("glu_bT", [128, 8])
        sng, sgk = load_small("ssd_norm_gT", [128, 16])
        gw = din("s5_glu_w", [1024, 1024])
        blks = [ws.add(gw[:, j * 128:(j + 1) * 128].rearrange("(kt p) c -> p kt c", p=128), 8, 128) for j in range(8)]
        for t in range(2):
            for j in range(8):
                ws.need(blks[j])
                bi = j % 4
                po, ko = banks[bi], f"bank{bi}"
                wv = ws.view(blks[j])
                for kt in range(8):
                    cx.op("pe", lambda e, kt=kt: e.matmul(po[:], wv[:, kt, :], y5b[:, kt, t * 512:(t + 1) * 512],
                                                           start=(kt == 0), stop=(kt == 7)),
                          reads=[ws.key(blks[j]), "y5b"], writes=[ko])
                cx.op("act", lambda e: e.activation(out=sig[:, j, :], in_=po[:], func=AF.Sigmoid, bias=glub[:, j:j + 1], scale=1.0),
                      reads=[ko, gbk], writes=["sig"])
            if t == 0:
                blks = [ws.add(gw[:, j * 128:(j + 1) * 128].rearrange("(kt p) c -> p kt c", p=128), 8, 128) for j in range(8)]
            for j in range(8):
                cx.op("dve", lambda e: e.tensor_tensor(out=y5b[:, j, t * 512:(t + 1) * 512], in0=y5b[:, j, t * 512:(t + 1) * 512],
                                                        in1=sig[:, j, :], op=ALU.mult), reads=["y5b", "sig"], writes=["y5b"])
        for kt in range(KT):
            s_ = sq[kt % 2]
            cx.op("act", lambda e: e.activation(out=s_[:], in_=ysb[:, kt, :], func=AF.Square), reads=["ysb"], writes=[f"sq{kt % 2}"])
            for t in range(2):
                cx.op("pe", lambda e: e.matmul(banks[4 + t][:], ones[:], s_[:, t * 512:(t + 1) * 512], start=(kt == 0), stop=(kt == KT - 1)),
                      reads=[f"sq{kt % 2}", "ones"], writes=[f"bank{4 + t}"])
        for t in range(2):
            cx.op("act", lambda e: e.activation(out=rstd[:, t * 512:(t + 1) * 512], in_=banks[4 + t][:], func=AF.Sqrt, scale=1.0 / D, bias=epsb[:]),
                  reads=[f"bank{4 + t}", "epsb"], writes=["rstd"])
        cx.op("dve", lambda e: e.reciprocal(out=rstd[:], in_=rstd[:]), reads=["rstd"], writes=["rstd"])
        for kt in range(KT):
            cx.op("dve", lambda e: e.scalar_tensor_tensor(out=ysb[:, kt, :], in0=ysb[:, kt, :], scalar=sng[:, kt:kt + 1], in1=rstd[:],
                                                           op0=ALU.mult, op1=ALU.mult), reads=["ysb", sgk, "rstd"], writes=["ysb"])
        wd = din("hyb_w_out", [3072, D])
        blk5 = [ws.add(wd[0:1024, j * 128:(j + 1) * 128].rearrange("(kt p) c -> p kt c", p=128), 8, 128) for j in range(KT)]
        gate = mod[:, 2, :]
        for j in range(KT):
            blks_ = ws.add(wd[1024:3072, j * 128:(j + 1) * 128].rearrange("(kt p) c -> p kt c", p=128), 16, 128)
            blk5[j] = (blk5[j], blks_)
        for part in range(2):
            for j in range(KT):
                b_ = blk5[j][part]
                ws.need(b_)
                wv = ws.view(b_)
                nk = 8 if part == 0 else 16
                srcb, skey = (y5b, "y5b") if part == 0 else (ysb, "ysb")
                for t in range(2):
                    bi = 4 + ((j * 2 + t) % 3)
                    po, ko = banks[bi], f"bank{bi}"
                    for kt in range(nk):
                        cx.op("pe", lambda e, kt=kt: e.matmul(po[:], wv[:, kt, :], srcb[:, kt, t * 512:(t + 1) * 512],
                                                               start=(kt == 0), stop=(kt == nk - 1)),
                              reads=[ws.key(b_), skey], writes=[ko])
                    cx.op("dve", lambda e: e.scalar_tensor_tensor(
                        out=x[:, j, t * 512:(t + 1) * 512], in0=po[:], scalar=gate[:, j:j + 1],
                        in1=x[:, j, t * 512:(t + 1) * 512], op0=ALU.mult, op1=ALU.add),
                        reads=[ko, mkey], writes=["x"])
        cx.close_scope()

    def rwkv_post():
        cx.open_scope()
        mod, mkey = MODS[(1, 1, "g")]
        ygb = cx.sb("ygb", [128, 16, TOK], BF16)
        slg = cx.slot("ygin")
        if "yg_gath" in io:
            gath_select(io["yg_gath"], 8, [(ygb, "ygb", [(kt // 8, kt % 8) for kt in range(16)])], BF16)
        else:
            yg_d = din("ygT_in", [128, 16, TOK], BF16)
            cx.dma("sp", slg, [(ygb[:, 4 * i:4 * i + 4, :], yg_d[:, 4 * i:4 * i + 4, :]) for i in range(4)], writes=["ygb"])
        outproj("rwkv_w_o", D, ygb, "ygb", KT, mod[:, 2, :], mkey)
        cx.close_scope()

    st_slot = cx.slot("st")
    MODS = {}
    for stg in stages:
        kind = stg["kind"]
        if kind == "ffn":
            MODS[(stg["l"], stg["s"])] = compute_mod(stg["l"], stg["s"], (0, 1, 2))
        elif kind == "h_out":
            MODS[(stg["l"], 1, "h")] = compute_mod(stg["l"], 1, (0, 1))
        elif kind == "mix0_post":
            MODS[(0, 1, "g")] = compute_mod(0, 1, (2,))
        elif kind == "rwkv_post":
            MODS[(1, 1, "g")] = compute_mod(1, 1, (2,))
    for stg in stages:
        kind = stg["kind"]
        if kind == "ffn":
            ffn(stg["l"], stg["s"])
        elif kind == "mix0_post":
            mix0_post()
        elif kind == "rwkv_post":
            rwkv_post()
        elif kind == "h_out":
            l = stg["l"]
            cx.open_scope()
            h = cx.sb("h", [128, KT, TOK], BF16)
            mod, mkey = MODS[(l, 1, "h")]
            adaln(l, 1, mod, mkey, h, "h")
            if "hT_out_pairs" in io:
                cx.dma("sp", st_slot, io["hT_out_pairs"](h), reads=["h"])
            else:
                hout = dout("hT_out", [128, KT, TOK], BF16)
                cx.dma("sp", st_slot, [(hout[:, 0:KT // 2, :], h[:, 0:KT // 2, :]), (hout[:, KT // 2:KT, :], h[:, KT // 2:KT, :])],
                       reads=["h"])
            cx.close_scope()
        elif kind == "x_out":
            xout = dout("xT_out", [128, KT, TOK], F32)
            cx.dma("sp", st_slot, [(xout[:, 0:KT // 2, :], x[:, 0:KT // 2, :]), (xout[:, KT // 2:KT, :], x[:, KT // 2:KT, :])],
                   reads=["x"])
        elif kind == "final":
            fg, fkey = load_small("final_gT", [128, KT])
            rms_stats()
            for kt in range(KT):
                cx.op("dve", lambda e, kt=kt: e.scalar_tensor_tensor(
                    out=x[:, kt, :], in0=x[:, kt, :], scalar=fg[:, kt:kt + 1], in1=rstd[:],
                    op0=ALU.mult, op1=ALU.mult),
                    reads=["x", fkey, "rstd"], writes=["x"])
            xout = dout("xT_out", [128, KT, TOK], F32)
            cx.dma("sp", st_slot, [(xout[:, 0:KT // 2, :], x[:, 0:KT // 2, :]), (xout[:, KT // 2:KT, :], x[:, KT // 2:KT, :])],
                   reads=["x"])
    cx.close_scope()
    cx.wait_all("sp")
    return nc


def cast_pairs(dst, src):
    if len(dst.shape) == 2:
        n = dst.shape[1]
        return [(dst[:, c0:min(n, c0 + 1024)], src[:, c0:min(n, c0 + 1024)]) for c0 in range(0, n, 1024)]
    out = []
    for a in range(dst.shape[1]):
        n = dst.shape[2]
        for c0 in range(0, n, 1024):
            out.append((dst[:, a, c0:min(n, c0 + 1024)], src[:, a, c0:min(n, c0 + 1024)]))
    return out


def fm(v):
    v = np.asarray(v)
    return np.ascontiguousarray(v.reshape(-1, 128).T)


def to_xT(x):
    out = []
    for b in range(4):
        for j in range(2):
            xs = x[b, j * TOK:(j + 1) * TOK, :]
            out.append(np.ascontiguousarray(xs.T.reshape(KT, 128, TOK).transpose(1, 0, 2)))
    return out


def from_xT(tiles):
    x = np.empty((4, SEQ, D), np.float32)
    for b in range(4):
        for j in range(2):
            t = tiles[b * 2 + j]
            x[b, j * TOK:(j + 1) * TOK, :] = t.transpose(1, 0, 2).reshape(D, TOK).T
    return x


S5TC = 256
GELU_C = 0.7978845608028654
TWO_PI = 6.283185307179586


CHK_COUNT = 1
DBG_BANKS = [0, 1]
DBG_NODVE = False


class _Stop(Exception):
    pass


def build_M0(do_s5=True, do_ssd=True, stop=None, G=None, io=None):
    nc, cx, dram, banks, bankT = _mk_env(G)
    io = io or {}
    cx.open_scope()

    cnt = [CHK_COUNT]

    def chk(n):
        if stop == n:
            cnt[0] -= 1
            if cnt[0] <= 0:
                raise _Stop()
    try:
        _build_M0_body(nc, cx, dram, banks, bankT, io, do_s5, do_ssd, chk)
    except _Stop:
        pass
    while cx.stacks and stop is not None:
        cx.close_scope()
    if stop is None:
        cx.close_scope()
    cx.wait_all("sp")
    return nc


def _build_M0_body(nc, cx, dram, banks, bankT, io, do_s5, do_ssd, chk):

    def din(name, shape, dtype=F32):
        if name in io:
            return io[name]
        if name not in dram:
            dram[name] = nc.dram_tensor(name, list(shape), dtype, kind="ExternalInput").ap()
        return dram[name]

    def dout(name, shape, dtype=F32):
        if name in io:
            return io[name]
        dram[name] = nc.dram_tensor(name, list(shape), dtype, kind="ExternalOutput").ap()
        return dram[name]

    BF16S = "bf16_from_f32"

    def load(name, shape, dtype=F32, q="sp"):
        sdt, ddt = (BF16, F32) if dtype == BF16S else (dtype, dtype)
        t = cx.sb(name, shape, sdt)
        sl = cx.slot(name)
        pairs = cast_pairs(t[:], din(name, shape, ddt)) if dtype == BF16S else [(t[:], din(name, shape, ddt))]
        cx.dma(q, sl, pairs, writes=[name])
        return t

    w_d = din("w_in_c", [D, 3600])
    hT = cx.sb("hT", [128, KT, SEQ], BF16)
    sl = cx.slot("hT")
    if "hT_pairs" in io:
        cx.dma("sp", sl, io["hT_pairs"](hT, 0), reads=io.get("dep", []), writes=["hT"])
    else:
        hT_d = din("hT", [128, KT, SEQ], BF16)
        cx.dma("sp", sl, [(hT[:, 4 * i:4 * i + 4, :], hT_d[:, 4 * i:4 * i + 4, :]) for i in range(4)], writes=["hT"])
    ws = WStream(cx, nslot=4, elems=4096)
    ident = load("ident", [128, 128])
    identb = cx.sb("identb", [128, 128], BF16)
    cx.op("dve", lambda e: e.tensor_copy(out=identb[:], in_=ident[:]), reads=["ident"], writes=["identb"])
    st_slot = cx.slot("st")
    st_slot2 = cx.slot("st2")

    def proj(blk, col0, ncol_tiles, evac):
        wv = ws.view(blk)
        n = 0
        for ti in range(ncol_tiles):
            for tb in range(4):
                bk = DBG_BANKS[n % len(DBG_BANKS)]
                n += 1
                for kt in range(KT):
                    cx.op("pe", lambda e, kt=kt, ti=ti, tb=tb, bk=bk: e.matmul(
                        banks[bk][:], wv[:, kt, (col0 + ti) * 128:(col0 + ti + 1) * 128],
                        hT[:, kt, tb * 512:(tb + 1) * 512], start=(kt == 0), stop=(kt == KT - 1)),
                        reads=[ws.key(blk), "hT"], writes=[f"bank{bk}"])
                chk(53)
                evac(ti, tb, banks[bk], f"bank{bk}")
                chk(54)

    if do_s5:
        cx.open_scope()
        lre = load("s5_lre", [128, 16])
        lim = load("s5_lim", [128, 16])
        ldt = load("s5_ldt", [128, 16])
        d5 = load("s5_dT", [128, 4])
        cre = load("s5_cre", [128, 16, 128], BF16S, q="pool")
        cimn = load("s5_cim", [128, 16, 128], BF16S, q="pool")
        cx.op("dve", lambda e: e.tensor_scalar(out=cimn[:], in0=cimn[:], scalar1=-1.0, scalar2=None, op0=ALU.mult),
              reads=["s5_cim"], writes=["s5_cim"])
        bre = cx.sb("breT", [128, 16, 128], BF16)
        bim = cx.sb("bimT", [128, 16, 128], BF16)

        sm = {}

        def S(name):
            sm[name] = cx.sb("s5_" + name, [128, 16], F32)
            return sm[name]

        def tt(o, a, b, op, eng="dve"):
            cx.op(eng, lambda e: e.tensor_tensor(out=sm[o][:], in0=sm[a][:], in1=sm[b][:], op=op),
                  reads=["s5sm"], writes=["s5sm"])

        def ts(o, a, s1, op0, s2=None, op1=None):
            if op1 is None:
                cx.op("dve", lambda e: e.tensor_scalar(out=sm[o][:], in0=sm[a][:], scalar1=s1, scalar2=None, op0=op0),
                      reads=["s5sm"], writes=["s5sm"])
            else:
                cx.op("dve", lambda e: e.tensor_scalar(out=sm[o][:], in0=sm[a][:], scalar1=s1, scalar2=s2, op0=op0, op1=op1),
                      reads=["s5sm"], writes=["s5sm"])

        def act(o, a, func, scale=1.0):
            cx.op("act", lambda e: e.activation(out=sm[o][:], in_=sm[a][:], func=func, scale=scale),
                  reads=["s5sm"], writes=["s5sm"])

        chk(1)
        sm["lre"], sm["lim"], sm["ldt"] = lre, lim, ldt
        for n_ in ("lr", "dt", "mag", "ang", "cs", "sn", "t1", "t2", "t3", "den", "nr", "fre", "fim", "lbr", "lbi", "rden"):
            S(n_)
        cx.wait_all("dve")
        cx.wait_all("act")
        ts("lr", "lre", -1e-4, ALU.min)
        act("dt", "ldt", AF.Exp)
        tt("t1", "lr", "dt", ALU.mult)
        act("mag", "t1", AF.Exp)
        tt("ang", "lim", "dt", ALU.mult)

        def sincos(o, a, shift):
            ki = cx.sb("s5_ki", [128, 16], mybir.dt.int32)
            ts("t1", a, 1.0 / TWO_PI, ALU.mult, shift / TWO_PI, ALU.add)
            cx.op("dve", lambda e: e.tensor_copy(out=ki[:], in_=sm["t1"][:]), reads=["s5sm"], writes=["s5ki"])
            cx.op("dve", lambda e: e.tensor_copy(out=sm["t2"][:], in_=ki[:]), reads=["s5ki"], writes=["s5sm"])
            tt("t1", "t1", "t2", ALU.subtract)
            ts("t2", "t1", 0.5, ALU.is_gt)
            tt("t1", "t1", "t2", ALU.subtract)
            ts("t2", "t1", -0.5, ALU.is_lt)
            tt("t1", "t1", "t2", ALU.add)
            act(o, "t1", AF.Sin, scale=TWO_PI)

        sincos("sn", "ang", 0.0)
        sincos("cs", "ang", TWO_PI / 4)
        tt("lbr", "mag", "cs", ALU.mult)
        tt("lbi", "mag", "sn", ALU.mult)
        tt("t1", "lr", "lr", ALU.mult)
        tt("t2", "lim", "lim", ALU.mult)
        tt("den", "t1", "t2", ALU.add)
        cx.op("dve", lambda e: e.reciprocal(out=sm["rden"][:], in_=sm["den"][:]), reads=["s5sm"], writes=["s5sm"])
        ts("nr", "lbr", -1.0, ALU.add)
        tt("t1", "nr", "lr", ALU.mult)
        tt("t2", "lbi", "lim", ALU.mult)
        tt("t1", "t1", "t2", ALU.add)
        tt("fre", "t1", "rden", ALU.mult)
        tt("t1", "lbi", "lr", ALU.mult)
        tt("t2", "nr", "lim", ALU.mult)
        tt("t1", "t1", "t2", ALU.subtract)
        tt("fim", "t1", "rden", ALU.mult)
        S("nfim")
        ts("nfim", "fim", -1.0, ALU.mult)
        S("nsn")
        ts("nsn", "sn", -1.0, ALU.mult)

        chk(2)
        cx.open_scope()
        xbre = load("s5_xbre", [128, 16, 128], q="act")
        xbim = load("s5_xbim", [128, 16, 128], q="act")
        xt = [cx.sb(f"s5xt{i}", [128, 128], F32) for i in range(2)]
        for pr in range(16):
            for part, (A, fa, Bm, fb) in enumerate(((xbre, "fre", xbim, "nfim"), (xbim, "fre", xbre, "fim"))):
                t_ = xt[part]
                cx.op("dve", lambda e, pr=pr, A=A, fa=fa, t_=t_: e.tensor_scalar(
                    out=t_[:], in0=A[:, pr, :], scalar1=sm[fa][:, pr:pr + 1], scalar2=None, op0=ALU.mult),
                    reads=["s5sm", "s5_xbre", "s5_xbim"], writes=[f"s5xt{part}"])
                cx.op("dve", lambda e, pr=pr, Bm=Bm, fb=fb, t_=t_: e.scalar_tensor_tensor(
                    out=t_[:], in0=Bm[:, pr, :], scalar=sm[fb][:, pr:pr + 1], in1=t_[:], op0=ALU.mult, op1=ALU.add),
                    reads=["s5sm", "s5_xbre", "s5_xbim", f"s5xt{part}"], writes=[f"s5xt{part}"])
                cx.op("pe", lambda e, t_=t_, part=part: e.transpose(banks[2 + part][:, 0:128], t_[:], ident[:]),
                      reads=[f"s5xt{part}", "ident"], writes=[f"bank{2 + part}"])
                dst = bre if part == 0 else bim
                cx.op("act", lambda e, dst=dst, pr=pr, part=part: e.activation(
                    out=dst[:, pr, :], in_=banks[2 + part][:, 0:128], func=AF.Copy),
                    reads=[f"bank{2 + part}"], writes=["breT" if part == 0 else "bimT"])

        chk(3)
        cx.close_scope()
        chk(4)
        ctab = cx.sb("ctab", [128, 16, S5TC], F32)
        stab = cx.sb("stab", [128, 16, S5TC], F32)
        rho = cx.sb("rho", [128, 16, S5TC], F32)
        ec = cx.sb("ec", [128, 16], F32)
        es = cx.sb("es", [128, 16], F32)
        et = [cx.sb(f"et{i}", [128, 16], F32) for i in range(3)]
        cx.op("dve", lambda e: e.memset(ctab[:, :, 0:1], 1.0), writes=["tab"])
        cx.op("dve", lambda e: e.memset(stab[:, :, 0:1], 0.0), reads=["tab"], writes=["tab"])
        cx.op("dve", lambda e: e.tensor_copy(out=ec[:], in_=sm["cs"][:]), reads=["s5sm"], writes=["e"])
        cx.op("dve", lambda e: e.tensor_copy(out=es[:], in_=sm["sn"][:]), reads=["s5sm", "e"], writes=["e"])
        L = 1
        while L < S5TC:
            for pr in range(16):
                cx.op("dve", lambda e, pr=pr, L=L: e.tensor_scalar(
                    out=ctab[:, pr, L:2 * L], in0=ctab[:, pr, 0:L], scalar1=ec[:, pr:pr + 1], scalar2=None, op0=ALU.mult),
                    reads=["tab", "e"], writes=["tab"])
                cx.op("dve", lambda e, pr=pr, L=L: e.tensor_scalar(
                    out=stab[:, pr, L:2 * L], in0=ctab[:, pr, 0:L], scalar1=es[:, pr:pr + 1], scalar2=None, op0=ALU.mult),
                    reads=["tab", "e"], writes=["tab"])
            cx.op("dve", lambda e: e.tensor_scalar(out=et[0][:], in0=es[:], scalar1=-1.0, scalar2=None, op0=ALU.mult),
                  reads=["e"], writes=["et"])
            for pr in range(16):
                cx.op("dve", lambda e, pr=pr, L=L: e.scalar_tensor_tensor(
                    out=ctab[:, pr, L:2 * L], in0=stab[:, pr, 0:L], scalar=et[0][:, pr:pr + 1], in1=ctab[:, pr, L:2 * L],
                    op0=ALU.mult, op1=ALU.add), reads=["tab", "et"], writes=["tab"])
                cx.op("dve", lambda e, pr=pr, L=L: e.scalar_tensor_tensor(
                    out=stab[:, pr, L:2 * L], in0=stab[:, pr, 0:L], scalar=ec[:, pr:pr + 1], in1=stab[:, pr, L:2 * L],
                    op0=ALU.mult, op1=ALU.add), reads=["tab", "e"], writes=["tab"])
            cx.op("dve", lambda e: e.tensor_tensor(out=et[1][:], in0=ec[:], in1=ec[:], op=ALU.mult), reads=["e"], writes=["et1"])
            cx.op("dve", lambda e: e.tensor_tensor(out=et[2][:], in0=es[:], in1=es[:], op=ALU.mult), reads=["e"], writes=["et2"])
            cx.op("dve", lambda e: e.scalar_tensor_tensor(out=es[:], in0=es[:], scalar=2.0, in1=ec[:], op0=ALU.mult, op1=ALU.mult),
                  reads=["e"], writes=["e"])
            cx.op("dve", lambda e: e.tensor_tensor(out=ec[:], in0=et[1][:], in1=et[2][:], op=ALU.subtract),
                  reads=["et1", "et2", "e"], writes=["e"])
            L *= 2
        for pr in range(16):
            cx.op("act", lambda e, pr=pr: e.activation(out=rho[:, pr, :], in_=ctab[:, pr, :], func=AF.Identity,
                                                        scale=0.0, bias=sm["mag"][:, pr:pr + 1]),
                  reads=["tab", "s5sm"], writes=["rho"])

        chk(5)
        u32 = cx.sb("u32", [128, SEQ], F32)
        ubf = cx.sb("ubf", [128, SEQ], BF16)
        y5 = cx.sb("y5", [128, SEQ], F32)
        carry = [cx.sb(f"carry{i}", [128, 16], F32) for i in range(2)]
        cx.op("dve", lambda e: e.memset(carry[0][:], 0.0), writes=["carry"])
        cx.op("dve", lambda e: e.memset(carry[1][:], 0.0), reads=["carry"], writes=["carry"])
        chk(51)
        tmp = [[cx.sb(f"s5tmp{j}{i}", [128, S5TC], F32) for i in range(8)] for j in range(2)]
        sbf = [[cx.sb(f"s5sbf{i}{j}", [128, S5TC], BF16) for j in range(2)] for i in range(4)]
        gl = [cx.sb(f"s5gl{i}", [128, S5TC], F32) for i in range(4)]
        y5_d = None if "y_tile" in io else dout("y5T", [128, 4, SEQ])
        NCH = SEQ // S5TC
        for o in range(4):
            blk = ws.add(w_d[:, o * 128:(o + 1) * 128].rearrange("(kt p) c -> p kt c", p=128), KT, 128)
            ws.need(blk)
            chk(52)

            def ev_u(ti, tb, ps, key):
                cx.op("act", lambda e: e.activation(out=u32[:, tb * 512:(tb + 1) * 512], in_=ps[:], func=AF.Copy),
                      reads=[key], writes=["u32"])
                if not DBG_NODVE:
                    cx.op("dve", lambda e: e.tensor_copy(out=ubf[:, tb * 512:(tb + 1) * 512], in_=ps[:]),
                          reads=[key], writes=["ubf"])
            proj(blk, 0, 1, ev_u)
            chk(6)
            for ch in range(NCH):
                c0 = ch * S5TC
                if ch == 1:
                    chk(7)
                for pp in range(4):
                    pr = o * 4 + pp
                    kre, kim = f"bank{2 + (pp % 2) * 2}", f"bank{3 + (pp % 2) * 2}"
                    pre, pim = banks[2 + (pp % 2) * 2], banks[3 + (pp % 2) * 2]
                    cx.op("pe", lambda e: e.matmul(pre[:, 0:S5TC], bre[:, pr, :], ubf[:, c0:c0 + S5TC], start=True, stop=True),
                          reads=["breT", "ubf"], writes=[kre])
                    cx.op("pe", lambda e: e.matmul(pim[:, 0:S5TC], bim[:, pr, :], ubf[:, c0:c0 + S5TC], start=True, stop=True),
                          reads=["bimT", "ubf"], writes=[kim])
                    t = tmp[pp % 2]
                    tq = pp % 2
                    ct, stb = ctab[:, pr, :], stab[:, pr, :]
                    cx.op("dve", lambda e: e.tensor_tensor(out=t[0][:], in0=pre[:, 0:S5TC], in1=ct, op=ALU.mult),
                          reads=[kre, "tab"], writes=[f"t{tq}_0"])
                    cx.op("dve", lambda e: e.tensor_tensor(out=t[1][:], in0=pim[:, 0:S5TC], in1=stb, op=ALU.mult),
                          reads=[kim, "tab"], writes=[f"t{tq}_1"])
                    cx.op("dve", lambda e: e.tensor_tensor(out=t[2][:], in0=pim[:, 0:S5TC], in1=ct, op=ALU.mult),
                          reads=[kim, "tab"], writes=[f"t{tq}_2"])
                    cx.op("dve", lambda e: e.tensor_tensor(out=t[3][:], in0=pre[:, 0:S5TC], in1=stb, op=ALU.mult),
                          reads=[kre, "tab"], writes=[f"t{tq}_3"])
                    cx.op("pool", lambda e: e.tensor_tensor(out=t[0][:], in0=t[0][:], in1=t[1][:], op=ALU.add),
                          reads=[f"t{tq}_0", f"t{tq}_1"], writes=[f"t{tq}_0"])
                    cx.op("pool", lambda e: e.tensor_tensor(out=t[2][:], in0=t[2][:], in1=t[3][:], op=ALU.subtract),
                          reads=[f"t{tq}_2", f"t{tq}_3"], writes=[f"t{tq}_2"])
                    cx.op("dve", lambda e: e.tensor_tensor_scan(out=t[4][:], data0=rho[:, pr, :], data1=t[0][:],
                                                                 initial=carry[0][:, pr:pr + 1], op0=ALU.mult, op1=ALU.add),
                          reads=["rho", f"t{tq}_0", "carry"], writes=[f"t{tq}_4"])
                    cx.op("dve", lambda e: e.tensor_tensor_scan(out=t[5][:], data0=rho[:, pr, :], data1=t[2][:],
                                                                 initial=carry[1][:, pr:pr + 1], op0=ALU.mult, op1=ALU.add),
                          reads=["rho", f"t{tq}_2", "carry"], writes=[f"t{tq}_5"])
                    if ch < NCH - 1:
                        cx.op("dve", lambda e: e.tensor_scalar(out=et[1][:, 0:1], in0=t[5][:, S5TC - 1:S5TC], scalar1=es[:, pr:pr + 1],
                                                                scalar2=None, op0=ALU.mult), reads=[f"t{tq}_5", "e"], writes=["et1"])
                        cx.op("dve", lambda e: e.tensor_scalar(out=et[2][:, 0:1], in0=t[4][:, S5TC - 1:S5TC], scalar1=es[:, pr:pr + 1],
                                                                scalar2=None, op0=ALU.mult), reads=[f"t{tq}_4", "e"], writes=["et2"])
                        cx.op("dve", lambda e: e.scalar_tensor_tensor(out=carry[0][:, pr:pr + 1], in0=t[4][:, S5TC - 1:S5TC],
                                                                       scalar=ec[:, pr:pr + 1], in1=et[1][:, 0:1],
                                                                       op0=ALU.mult, op1=ALU.subtract),
                              reads=[f"t{tq}_4", "e", "et1", "carry"], writes=["carry"])
                        cx.op("dve", lambda e: e.scalar_tensor_tensor(out=carry[1][:, pr:pr + 1], in0=t[5][:, S5TC - 1:S5TC],
                                                                       scalar=ec[:, pr:pr + 1], in1=et[2][:, 0:1],
                                                                       op0=ALU.mult, op1=ALU.add),
                              reads=[f"t{tq}_5", "e", "et2", "carry"], writes=["carry"])
                    cx.op("pool", lambda e: e.tensor_tensor(out=t[6][:], in0=t[4][:], in1=ct, op=ALU.mult),
                          reads=[f"t{tq}_4", "tab"], writes=[f"t{tq}_6"])
                    cx.op("pool", lambda e: e.tensor_tensor(out=t[7][:], in0=t[5][:], in1=stb, op=ALU.mult),
                          reads=[f"t{tq}_5", "tab"], writes=[f"t{tq}_7"])
                    cx.op("pool", lambda e: e.tensor_tensor(out=sbf[pp][0][:], in0=t[6][:], in1=t[7][:], op=ALU.subtract),
                          reads=[f"t{tq}_6", f"t{tq}_7"], writes=[f"sbf{pp}0"])
                    cx.op("pool", lambda e: e.tensor_tensor(out=t[6][:], in0=t[4][:], in1=stb, op=ALU.mult),
                          reads=[f"t{tq}_4", "tab", f"t{tq}_6"], writes=[f"t{tq}_6"])
                    cx.op("pool", lambda e: e.tensor_tensor(out=t[7][:], in0=t[5][:], in1=ct, op=ALU.mult),
                          reads=[f"t{tq}_5", "tab", f"t{tq}_7"], writes=[f"t{tq}_7"])
                    cx.op("pool", lambda e: e.tensor_tensor(out=sbf[pp][1][:], in0=t[6][:], in1=t[7][:], op=ALU.add),
                          reads=[f"t{tq}_6", f"t{tq}_7"], writes=[f"sbf{pp}1"])
                py = banks[6]
                for pp in range(4):
                    pr = o * 4 + pp
                    cx.op("pe", lambda e: e.matmul(py[:, 0:S5TC], cre[:, pr, :], sbf[pp][0][:], start=(pp == 0), stop=False),
                          reads=["s5_cre", f"sbf{pp}0"], writes=["bank6"])
                    cx.op("pe", lambda e: e.matmul(py[:, 0:S5TC], cimn[:, pr, :], sbf[pp][1][:], start=False, stop=(pp == 3)),
                          reads=["s5_cim", f"sbf{pp}1"], writes=["bank6"])
                cx.op("dve", lambda e: e.scalar_tensor_tensor(out=gl[0][:], in0=u32[:, c0:c0 + S5TC], scalar=d5[:, o:o + 1],
                                                               in1=py[:, 0:S5TC], op0=ALU.mult, op1=ALU.add),
                      reads=["u32", "s5_dT", "bank6"], writes=["gl0"])
                cx.op("act", lambda e: e.activation(out=gl[1][:], in_=gl[0][:], func=AF.Square), reads=["gl0"], writes=["gl1"])
                cx.op("dve", lambda e: e.tensor_scalar(out=gl[1][:], in0=gl[1][:], scalar1=GELU_C * 0.044715, scalar2=GELU_C,
                                                        op0=ALU.mult, op1=ALU.add), reads=["gl1"], writes=["gl1"])
                cx.op("dve", lambda e: e.tensor_tensor(out=gl[1][:], in0=gl[1][:], in1=gl[0][:], op=ALU.mult),
                      reads=["gl1", "gl0"], writes=["gl1"])
                cx.op("act", lambda e: e.activation(out=gl[2][:], in_=gl[1][:], func=AF.Tanh), reads=["gl1"], writes=["gl2"])
                cx.op("act", lambda e: e.activation(out=gl[3][:], in_=gl[0][:], func=AF.Copy, scale=0.5), reads=["gl0"], writes=["gl3"])
                cx.op("dve", lambda e: e.scalar_tensor_tensor(out=y5[:, c0:c0 + S5TC], in0=gl[2][:], scalar=1.0, in1=gl[3][:],
                                                               op0=ALU.add, op1=ALU.mult),
                      reads=["gl2", "gl3"], writes=["y5"])
            cx.dma("sp", st_slot if o % 2 == 0 else st_slot2,
                   [(io["y_tile"](o) if "y_tile" in io else y5_d[:, o, :], y5[:])], reads=["y5"])
        cx.close_scope()

    if do_ssd:
        cx.open_scope()
        tri = load("tri", [128, 128])
        ones = load("ones128", [128, 128])
        maskneg = load("maskneg", [128, 128])
        cw = load("conv_wT", [128, 16, 4])
        cb = load("conv_bT", [128, 16])
        dtb = load("dt_bias_bc", [128, 16])
        alog = load("a_log_bc", [128, 16])
        dsk = load("ssd_dT", [128, 8])
        onec = cx.sb("onec", [128, 1], F32)
        cx.op("dve", lambda e: e.memset(onec[:], 1.0), writes=["onec"])
        abc = cx.sb("abc", [128, 16], F32)
        cx.op("act", lambda e: e.activation(out=abc[:], in_=alog[:], func=AF.Exp), reads=["a_log_bc"], writes=["abc"])
        cx.op("dve", lambda e: e.tensor_scalar(out=abc[:], in0=abc[:], scalar1=-1.0, scalar2=None, op0=ALU.mult),
              reads=["abc"], writes=["abc"])
        NC_ = SEQ // 128
        dt_all = cx.sb("dt_all", [128, NC_, 16], F32)
        adt = cx.sb("adt", [128, NC_, 16], F32)
        cum = cx.sb("cum", [128, NC_, 16], F32)
        dte = cx.sb("dte", [128, NC_, 16], F32)
        dectot = cx.sb("dectot", [128, NC_, 16], F32)
        dg = [cx.sb(f"dg{i}", [128, 128], F32) for i in range(2)]
        tsm = [cx.sb(f"tsm{i}", [128, 16], F32) for i in range(2)]
        bdt = ws.add(w_d[:, 3584:3600].rearrange("(kt p) c -> p kt c", p=128), KT, 16)
        ws.need(bdt)
        wdt = ws.view(bdt)
        b2 = banks[2]
        for c in range(NC_):
            for kt in range(KT):
                cx.op("pe", lambda e, kt=kt: e.matmul(b2[:, 0:16], hT[:, kt, c * 128:(c + 1) * 128], wdt[:, kt, :],
                                                       start=(kt == 0), stop=(kt == KT - 1)),
                      reads=[ws.key(bdt), "hT"], writes=["bank2"])
            cx.op("dve", lambda e: e.tensor_tensor(out=tsm[0][:], in0=b2[:, 0:16], in1=dtb[:], op=ALU.add),
                  reads=["bank2", "dt_bias_bc"], writes=["tsm0"])
            cx.op("act", lambda e: e.activation(out=tsm[0][:], in_=tsm[0][:], func=AF.Exp), reads=["tsm0"], writes=["tsm0"])
            cx.op("act", lambda e: e.activation(out=dt_all[:, c, :], in_=tsm[0][:], func=AF.Ln, bias=onec[:], scale=1.0),
                  reads=["tsm0", "onec"], writes=["dt_all"])
            cx.op("dve", lambda e: e.tensor_tensor(out=adt[:, c, :], in0=dt_all[:, c, :], in1=abc[:], op=ALU.mult),
                  reads=["dt_all", "abc"], writes=["adt"])
            cx.op("pe", lambda e: e.matmul(banks[3][:, 0:16], tri[:], adt[:, c, :], start=True, stop=True),
                  reads=["tri", "adt"], writes=["bank3"])
            cx.op("pe", lambda e: e.matmul(banks[4][:, 0:16], ones[:], adt[:, c, :], start=True, stop=True),
                  reads=["ones128", "adt"], writes=["bank4"])
            cx.op("act", lambda e: e.activation(out=cum[:, c, :], in_=banks[3][:, 0:16], func=AF.Copy), reads=["bank3"], writes=["cum"])
            cx.op("act", lambda e: e.activation(out=dectot[:, c, :], in_=banks[4][:, 0:16], func=AF.Exp), reads=["bank4"], writes=["dectot"])
            cx.op("dve", lambda e: e.tensor_tensor(out=tsm[1][:], in0=banks[4][:, 0:16], in1=cum[:, c, :], op=ALU.subtract),
                  reads=["bank4", "cum"], writes=["tsm1"])
            cx.op("act", lambda e: e.activation(out=dte[:, c, :], in_=tsm[1][:], func=AF.Exp), reads=["tsm1"], writes=["dte"])

        raw = cx.sb("raw", [128, 4, 3 + SEQ], F32)
        cx.op("dve", lambda e: e.memset(raw[:, :, 0:3], 0.0), writes=["raw"])
        sz = cx.sb("sz", [128, 2, SEQ], BF16)
        cv = [cx.sb("cv0", [128, SEQ], F32)]
        xs32 = cx.sb("xs32", [128, 2, SEQ], F32)
        xsb = cx.sb("xsb", [128, 2, SEQ], BF16)
        BT = cx.sb("BT", [128, SEQ], BF16)
        CT = cx.sb("CT", [128, SEQ], BF16)
        yout = raw[:, 0:2, 3:3 + SEQ]
        car32 = cx.sb("car32", [128, 4, 64], F32)
        carb = cx.sb("carb", [128, 4, 64], BF16)
        Btok = [cx.sb(f"Btok{i}", [128, 128], BF16) for i in range(3)]
        CBs = [cx.sb(f"CBs{i}", [128, 128], F32) for i in range(3)]
        xq = [cx.sb(f"xq{i}", [128, 256], BF16) for i in range(3)]
        xqd = [cx.sb(f"xqd{i}", [128, 256], BF16) for i in range(3)]
        dm32 = [cx.sb(f"dm32{i}", [128, 128], F32) for i in range(2)]
        dmx = [cx.sb(f"dmx{i}", [128, 128], F32) for i in range(2)]
        MT = [[cx.sb(f"MT{i}{j}", [128, 128], BF16) for j in range(4)] for i in range(3)]
        ebc = [cx.sb(f"ebc{i}", [128, 128], F32) for i in range(2)]
        Cs = [[cx.sb(f"Cs{i}{j}", [128, 128], BF16) for j in range(4)] for i in range(3)]
        ytmp = [cx.sb(f"ytmp{i}", [128, 128], F32) for i in range(2)]
        ys_d = None if "y_tile" in io else dout("ysT", [128, 8, SEQ])
        b3, b4, b5 = banks[3], banks[4], banks[5]
        for gg in range(4):
            base = 512 + gg * 768
            bx = ws.add(w_d[:, base:base + 256].rearrange("(kt p) c -> p kt c", p=128), KT, 256)
            bbc = ws.add(w_d[:, base + 256:base + 512].rearrange("(kt p) c -> p kt c", p=128), KT, 256)
            bz = ws.add(w_d[:, base + 512:base + 768].rearrange("(kt p) c -> p kt c", p=128), KT, 256)

            def ev_raw(off):
                def f(ti, tb, ps, key):
                    cx.op("act", lambda e: e.activation(out=raw[:, off + ti, 3 + tb * 512:3 + (tb + 1) * 512], in_=ps[:], func=AF.Copy),
                          reads=[key], writes=["raw"])
                return f

            def ev_z(ti, tb, ps, key):
                cx.op("act", lambda e: e.activation(out=sz[:, ti, tb * 512:(tb + 1) * 512], in_=ps[:], func=AF.Silu),
                      reads=[key], writes=["sz"])
            ws.need(bx)
            proj(bx, 0, 2, ev_raw(0))
            ws.need(bbc)
            proj(bbc, 0, 2, ev_raw(2))
            ws.need(bz)
            proj(bz, 0, 2, ev_z)
            for ti in range(4):
                tidx = gg * 4 + ti
                cvt = cv[0]
                ck = "cv0"
                cx.op("dve", lambda e: e.tensor_scalar(out=cvt[:], in0=raw[:, ti, 0:SEQ], scalar1=cw[:, tidx, 0:1], scalar2=None,
                                                        op0=ALU.mult), reads=["raw", "conv_wT"], writes=[ck])
                for jj in range(1, 4):
                    cx.op("dve", lambda e, jj=jj: e.scalar_tensor_tensor(out=cvt[:], in0=raw[:, ti, jj:jj + SEQ],
                                                                       scalar=cw[:, tidx, jj:jj + 1], in1=cvt[:],
                                                                       op0=ALU.mult, op1=ALU.add),
                          reads=["raw", "conv_wT", ck], writes=[ck])
                if ti < 2:
                    cx.op("act", lambda e: e.activation(out=xs32[:, ti, :], in_=cvt[:], func=AF.Silu, bias=cb[:, tidx:tidx + 1], scale=1.0),
                          reads=[ck, "conv_bT"], writes=["xs32"])
                    cx.op("pool", lambda e: e.tensor_copy(out=xsb[:, ti, :], in_=xs32[:, ti, :]), reads=["xs32"], writes=["xsb"])
                else:
                    dst, dk = (BT, "BT") if ti == 2 else (CT, "CT")
                    cx.op("act", lambda e: e.activation(out=dst[:], in_=cvt[:], func=AF.Silu, bias=cb[:, tidx:tidx + 1], scale=1.0),
                          reads=[ck, "conv_bT"], writes=[dk])
            cx.op("dve", lambda e: e.memset(car32[:], 0.0), reads=["car32"], writes=["car32"])
            cx.op("dve", lambda e: e.memset(carb[:], 0.0), reads=["carb"], writes=["carb"])
            def partA1(c):
                cs_ = slice(c * 128, (c + 1) * 128)
                par = c % 3
                cx.op("pe", lambda e: e.transpose(bankT[:, 0:128], BT[:, cs_], identb[:]), reads=["BT", "identb"], writes=["bankT"])
                cx.op("pe", lambda e: e.transpose(bankT[:, 128:256], xsb[:, 0, cs_], identb[:]), reads=["xsb", "identb"], writes=["bankT"])
                cx.op("pe", lambda e: e.transpose(bankT[:, 256:384], xsb[:, 1, cs_], identb[:]), reads=["xsb", "identb"], writes=["bankT"])
                cx.op("act", lambda e: e.activation(out=Btok[par][:], in_=bankT[:, 0:128], func=AF.Copy),
                      reads=["bankT"], writes=[f"Btok{par}"])
                for hh in range(4):
                    h_ = gg * 4 + hh
                    cx.op("dve", lambda e: e.tensor_scalar(out=xq[par][:, hh * 64:(hh + 1) * 64], in0=bankT[:, 128 + hh * 64:192 + hh * 64],
                                                            scalar1=dt_all[:, c, h_:h_ + 1], scalar2=None, op0=ALU.mult),
                          reads=["bankT", "dt_all"], writes=[f"xq{par}"])
                    cx.op("pool", lambda e: e.tensor_scalar(out=xqd[par][:, hh * 64:(hh + 1) * 64], in0=xq[par][:, hh * 64:(hh + 1) * 64],
                                                             scalar1=dte[:, c, h_:h_ + 1], scalar2=None, op0=ALU.mult),
                          reads=[f"xq{par}", "dte"], writes=[f"xqd{par}"])
                cx.op("pe", lambda e: e.matmul(b3[:, 0:128], BT[:, cs_], CT[:, cs_], start=True, stop=True),
                      reads=["BT", "CT"], writes=["bank3"])
                cx.op("act", lambda e: e.activation(out=CBs[par][:], in_=b3[:, 0:128], func=AF.Copy), reads=["bank3"], writes=[f"CBs{par}"])

            def partA2(c):
                cs_ = slice(c * 128, (c + 1) * 128)
                par = c % 3
                for hh in range(4):
                    h_ = gg * 4 + hh
                    hp = hh % 2
                    crow = banks[hh % 2][:, 0:128]
                    cx.op("pool", lambda e: e.tensor_scalar(out=dg[hp][:], in0=ident[:], scalar1=cum[:, c, h_:h_ + 1], scalar2=None,
                                                             op0=ALU.mult), reads=["ident", "cum"], writes=[f"dg{hp}"])
                    cx.op("pe", lambda e: e.matmul(crow, ones[:], dg[hp][:], start=True, stop=True),
                          reads=["ones128", f"dg{hp}"], writes=[f"bank{hh % 2}"])
                    cx.op("dve", lambda e: e.scalar_tensor_tensor(out=dm32[hp][:], in0=crow, scalar=cum[:, c, h_:h_ + 1], in1=maskneg[:],
                                                                   op0=ALU.subtract, op1=ALU.add),
                          reads=[f"bank{hh % 2}", "cum", "maskneg"], writes=[f"dm32{hp}"])
                    cx.op("act", lambda e: e.activation(out=dmx[hp][:], in_=dm32[hp][:], func=AF.Exp), reads=[f"dm32{hp}"], writes=[f"dmx{hp}"])
                    cx.op("dve", lambda e: e.tensor_tensor(out=MT[par][hh][:], in0=dmx[hp][:], in1=CBs[par][:], op=ALU.mult),
                          reads=[f"dmx{hp}", f"CBs{par}"], writes=[f"MT{par}{hh}"])
                    cx.op("act", lambda e: e.activation(out=ebc[hp][:], in_=crow, func=AF.Exp), reads=[f"bank{hh % 2}"], writes=[f"ebc{hp}"])
                    cx.op("pool", lambda e: e.tensor_tensor(out=Cs[par][hh][:], in0=CT[:, cs_], in1=ebc[hp][:], op=ALU.mult),
                          reads=["CT", f"ebc{hp}"], writes=[f"Cs{par}{hh}"])

            def partB(c):
                cs_ = slice(c * 128, (c + 1) * 128)
                par = c % 3
                q2 = c % 2
                for hh in range(4):
                    pt, half = hh // 2, hh % 2
                    yo = banks[5 + q2][half * 64:(half + 1) * 64, pt * 128:(pt + 1) * 128]
                    cx.op("pe", lambda e: e.matmul(yo, xq[par][:, hh * 64:(hh + 1) * 64], MT[par][hh][:], start=True, stop=False),
                          reads=[f"xq{par}", f"MT{par}{hh}"], writes=[f"bank{5 + q2}"])
                    cx.op("pe", lambda e: e.matmul(yo, carb[:, hh, :], Cs[par][hh][:], start=False, stop=True),
                          reads=["carb", f"Cs{par}{hh}"], writes=[f"bank{5 + q2}"])
                cx.op("pe", lambda e: e.matmul(banks[2][:, 0:256], Btok[par][:], xqd[par][:], start=True, stop=True),
                      reads=[f"Btok{par}", f"xqd{par}"], writes=["bank2"])
                for hh in range(4):
                    h_ = gg * 4 + hh
                    cx.op("dve", lambda e: e.scalar_tensor_tensor(out=car32[:, hh, :], in0=car32[:, hh, :], scalar=dectot[:, c, h_:h_ + 1],
                                                                   in1=banks[2][:, hh * 64:(hh + 1) * 64], op0=ALU.mult, op1=ALU.add),
                          reads=["car32", "dectot", "bank2"], writes=["car32"])
                cx.op("pool", lambda e: e.tensor_copy(out=carb[:], in_=car32[:]), reads=["car32"], writes=["carb"])
                for pt in range(2):
                    cx.op("dve", lambda e: e.scalar_tensor_tensor(out=ytmp[pt][:], in0=xs32[:, pt, cs_], scalar=dsk[:, gg * 2 + pt:gg * 2 + pt + 1],
                                                                   in1=banks[5 + q2][:, pt * 128:(pt + 1) * 128],
                                                                   op0=ALU.mult, op1=ALU.add),
                          reads=["xs32", "ssd_dT", f"bank{5 + q2}"], writes=[f"ytmp{pt}"])
                    cx.op("pool", lambda e: e.tensor_tensor(out=yout[:, pt, cs_], in0=ytmp[pt][:], in1=sz[:, pt, cs_], op=ALU.mult),
                          reads=[f"ytmp{pt}", "sz"], writes=["raw"])

            for it_ in cx.record(partA1, 0):
                cx.play(it_)
            cx.play_interleaved(cx.record(partA2, 0), cx.record(partA1, 1))
            for c in range(NC_):
                la = cx.record(partB, c)
                lb = cx.record(partA2, c + 1) if c + 1 < NC_ else []
                lc = cx.record(partA1, c + 2) if c + 2 < NC_ else []
                cx.play_interleaved3(la, lb, lc)
            cx.dma("sp", st_slot if gg % 2 == 0 else st_slot2,
                   [((io["y_tile"](4 + gg * 2 + pt) if "y_tile" in io else ys_d[:, gg * 2 + pt, :]), yout[:, pt, :])
                    for pt in range(2)], reads=["raw"])
        cx.close_scope()


def prep_M0(inp, b, j, hT_full):
    m = {"hT": hT_full, "ident": np.eye(128, dtype=np.float32)}
    w = inp["hyb_w_in"][0]
    cols = [np.arange(j * 512, (j + 1) * 512)]
    for gg in range(4):
        G = j * 4 + gg
        cols.append(3072 + G * 256 + np.arange(256))
        cols.append(3072 + 2048 + G * 128 + np.arange(128))
        cols.append(3072 + 3072 + G * 128 + np.arange(128))
        cols.append(1024 + G * 256 + np.arange(256))
    cols.append(7168 + 16 * j + np.arange(16))
    cols = np.concatenate(cols)
    m["w_in_c"] = np.ascontiguousarray(w[:, cols])
    g0 = 32 * j
    lre = np.zeros((128, 16), np.float32)
    lim = np.zeros((128, 16), np.float32)
    ldt = np.zeros((128, 16), np.float32)
    xbre = np.zeros((128, 16, 128), np.float32)
    xbim = np.zeros((128, 16, 128), np.float32)
    cre = np.zeros((128, 16, 128), np.float32)
    cim = np.zeros((128, 16, 128), np.float32)
    for pr in range(16):
        pp = pr % 4
        for gi in range(2):
            g = g0 + 2 * pr + gi
            rows = slice(gi * 64, gi * 64 + 64)
            cs = slice(32 * pp + 16 * gi, 32 * pp + 16 * gi + 16)
            lre[rows, pr] = inp["s5_lambda_re"][0, g]
            lim[rows, pr] = inp["s5_lambda_im"][0, g]
            ldt[rows, pr] = inp["s5_log_dt"][0, g]
            xbre[rows, pr, cs] = inp["s5_b_re"][0, g]
            xbim[rows, pr, cs] = inp["s5_b_im"][0, g]
            cre[rows, pr, cs] = inp["s5_c_re"][0, g].T
            cim[rows, pr, cs] = inp["s5_c_im"][0, g].T
    m.update(s5_lre=lre, s5_lim=lim, s5_ldt=ldt, s5_xbre=xbre, s5_xbim=xbim, s5_cre=cre, s5_cim=cim)
    m["s5_dT"] = np.ascontiguousarray(inp["s5_d"][0, j * 512:(j + 1) * 512].reshape(4, 128).T)
    cwT = np.zeros((128, 16, 4), np.float32)
    cbT = np.zeros((128, 16), np.float32)
    dT = np.zeros((128, 8), np.float32)
    cwf, cbf = inp["ssd_conv_w"][0], inp["ssd_conv_b"][0]
    for gg in range(4):
        G = j * 4 + gg
        chans = [G * 256 + np.arange(128), G * 256 + 128 + np.arange(128),
                 2048 + G * 128 + np.arange(128), 3072 + G * 128 + np.arange(128)]
        for ti in range(4):
            cwT[:, gg * 4 + ti, :] = cwf[:, chans[ti]].T
            cbT[:, gg * 4 + ti] = cbf[chans[ti]]
        for pt in range(2):
            heads = (G * 256 + pt * 128 + np.arange(128)) // 64
            dT[:, gg * 2 + pt] = inp["ssd_d"][0][heads]
    hs = slice(16 * j, 16 * j + 16)
    m.update(conv_wT=cwT, conv_bT=cbT, ssd_dT=dT,
             dt_bias_bc=np.ascontiguousarray(np.broadcast_to(inp["ssd_dt_bias"][0, hs], (128, 16))),
             a_log_bc=np.ascontiguousarray(np.broadcast_to(inp["ssd_a_log"][0, hs], (128, 16))))
    tri = np.triu(np.ones((128, 128), np.float32))
    m["tri"] = tri
    m["ones128"] = np.ones((128, 128), np.float32)
    m["maskneg"] = np.where(np.arange(128)[None, :] >= np.arange(128)[:, None], 0.0, -30000.0).astype(np.float32)
    sel = np.zeros((16, 16, 128), np.float32)
    for h_ in range(16):
        sel[h_, h_, :] = 1.0
    m["sel"] = sel.reshape(16, 16 * 128)
    return m


RC = 64
LD_C = 0.6065306597126334
GN_EPS = 64e-5


def build_M1(G=None, io=None):
    nc, cx, dram, banks, bankT = _mk_env(G)
    io = io or {}
    cx.open_scope()

    def din(name, shape, dtype=F32):
        if name in io:
            return io[name]
        if name not in dram:
            dram[name] = nc.dram_tensor(name, list(shape), dtype, kind="ExternalInput").ap()
        return dram[name]

    def dout(name, shape, dtype=F32):
        if name in io:
            return io[name]
        dram[name] = nc.dram_tensor(name, list(shape), dtype, kind="ExternalOutput").ap()
        return dram[name]

    BF16S = "bf16_from_f32"

    def load(name, shape, dtype=F32, q="sp"):
        sdt, ddt = (BF16, F32) if dtype == BF16S else (dtype, dtype)
        t = cx.sb(name, shape, sdt)
        sl = cx.slot(name)
        pairs = cast_pairs(t[:], din(name, shape, ddt)) if dtype == BF16S else [(t[:], din(name, shape, ddt))]
        cx.dma(q, sl, pairs, writes=[name])
        return t

    hb = cx.sb("hbuf", [128, KT, SEQ + 1], BF16)
    cx.op("dve", lambda e: e.memset(hb[:, :, 0:1], 0.0), writes=["hb"])
    sl = cx.slot("hT")
    if "hT_pairs" in io:
        cx.dma("sp", sl, io["hT_pairs"](hb, 1), reads=["hb"] + io.get("dep", []), writes=["hb"])
    else:
        hT_d = din("hT", [128, KT, SEQ], BF16)
        cx.dma("sp", sl, [(hb[:, 4 * i:4 * i + 4, 1:SEQ + 1], hT_d[:, 4 * i:4 * i + 4, :]) for i in range(4)],
               reads=["hb"], writes=["hb"])
    ws = WStream(cx, nslot=2, elems=2048)
    ident = load("ident", [128, 128])
    identb = cx.sb("identb", [128, 128], BF16)
    cx.op("dve", lambda e: e.tensor_copy(out=identb[:], in_=ident[:]), reads=["ident"], writes=["identb"])
    mask3 = load("mask3", [128, 384])
    blockones = load("blockones", [128, 128])
    resetm = load("resetmask", [128, SEQ], BF16S, q="pool")
    muT = load("muT", [128, 6, KT])
    w0T = load("w0T", [128, 8])
    a0T = load("a0T", [128, 8])
    kkT = load("k_kT", [128, 8])
    kaT = load("k_aT", [128, 8])
    rkT = load("r_kT", [128, 8])
    lng = load("lng_stack", [128, 8, 64])
    lnb = load("lnb_stack", [128, 8, 64])
    w2c = load("w2c", [96, 1024], BF16S, q="pool")
    a2c = load("a2c", [96, 1024], BF16S, q="pool")
    g2c = load("g2c", [128, 2, 1024], BF16S, q="pool")
    onesb = cx.sb("onesb", [128, 2], BF16)
    cx.op("dve", lambda e: e.memset(onesb[:], 1.0), writes=["onesb"])
    epsg = cx.sb("epsg", [128, 1], F32)
    cx.op("dve", lambda e: e.memset(epsg[:], GN_EPS), writes=["epsg"])
    st_slots = [cx.slot("st0"), cx.slot("st1")]

    wder = [[cx.sb(f"wd{i}{j}", [128, KT, 128], BF16) for j in range(2)] for i in range(2)]
    nder = [0]

    def derive(blk, mu_i, ncol):
        i = nder[0] % 2
        nder[0] += 1
        wv = ws.view(blk)
        w1_, w2_ = wder[i][0], wder[i][1]
        for kt in range(KT):
            eng = "dve" if kt % 2 == 0 else "pool"
            cx.op(eng, lambda e, kt=kt: e.tensor_scalar(out=w2_[:, kt, 0:ncol], in0=wv[:, kt, :], scalar1=muT[:, mu_i, kt:kt + 1],
                                                        scalar2=None, op0=ALU.mult),
                  reads=[ws.key(blk), "muT"], writes=[f"wd{i}1"])
        cx.op("pool", lambda e: e.tensor_tensor(out=w1_[:, :, 0:ncol], in0=wv[:], in1=w2_[:, :, 0:ncol], op=ALU.subtract),
              reads=[ws.key(blk), f"wd{i}1"], writes=[f"wd{i}0"])
        return w1_, w2_, f"wd{i}0", f"wd{i}1"

    pj = [0]

    def proj2(der, ncol, evac):
        w1_, w2_, k1, k2 = der
        for tb in range(4):
            bk = pj[0] % 2
            pj[0] += 1
            for kt in range(KT):
                cx.op("pe", lambda e, kt=kt: e.matmul(banks[bk][0:ncol, :], w1_[:, kt, 0:ncol], hb[:, kt, 1 + tb * 512:1 + (tb + 1) * 512],
                                                       start=(kt == 0), stop=False),
                      reads=[k1, "hb"], writes=[f"bank{bk}"])
            for kt in range(KT):
                cx.op("pe", lambda e, kt=kt: e.matmul(banks[bk][0:ncol, :], w2_[:, kt, 0:ncol], hb[:, kt, tb * 512:(tb + 1) * 512],
                                                       start=False, stop=(kt == KT - 1)),
                      reads=[k2, "hb"], writes=[f"bank{bk}"])
            evac(tb, banks[bk], f"bank{bk}")

    def wblock(name, shape_cols, c0, ncol):
        src = din(name, [D, shape_cols])
        return ws.add(src[:, c0:c0 + ncol].rearrange("(kt p) c -> p kt c", p=128), KT, ncol)

    tw = cx.sb("tw", [96, SEQ], BF16)
    ta = cx.sb("ta", [96, SEQ], BF16)
    tg = cx.sb("tg", [128, 2, SEQ], BF16)
    b_w1 = wblock("w1", 96, 0, 96)
    b_a1 = wblock("a1", 96, 0, 96)
    b_g1 = [wblock("g1", 256, i * 128, 128) for i in range(2)]
    ws.need(b_w1)
    proj2(derive(b_w1, 1, 96), 96, lambda tb, ps, key: cx.op(
        "act", lambda e: e.activation(out=tw[:, tb * 512:(tb + 1) * 512], in_=ps[0:96, :], func=AF.Tanh), reads=[key], writes=["tw"]))
    ws.need(b_a1)
    proj2(derive(b_a1, 4, 96), 96, lambda tb, ps, key: cx.op(
        "act", lambda e: e.activation(out=ta[:, tb * 512:(tb + 1) * 512], in_=ps[0:96, :], func=AF.Copy), reads=[key], writes=["ta"]))
    for i in range(2):
        ws.need(b_g1[i])
        proj2(derive(b_g1[i], 5, 128), 128, lambda tb, ps, key, i=i: cx.op(
            "act", lambda e: e.activation(out=tg[:, i, tb * 512:(tb + 1) * 512], in_=ps[:], func=AF.Sigmoid), reads=[key], writes=["tg"]))

    r_bf = cx.sb("r_bf", [128, SEQ], BF16)
    k32 = cx.sb("k32", [128, SEQ], F32)
    v_bf = cx.sb("v_bf", [128, SEQ], BF16)
    a32 = cx.sb("a32", [128, SEQ], F32)
    kk32 = cx.sb("kk32", [128, SEQ], F32)
    ld32 = cx.sb("ld32", [128, SEQ], F32)
    cl32 = cx.sb("cl32", [128, SEQ], F32)
    ecl = cx.sb("ecl", [128, SEQ], F32)
    g_bf = cx.sb("g_bf", [128, SEQ], BF16)
    yg = cx.sb("yg", [128, SEQ], BF16)
    sqt = [cx.sb("sqt0", [128, 512], F32)] * 2
    Ear = [cx.sb(f"Ear{i}", [128, 256], BF16) for i in range(3)]
    Eb = [cx.sb(f"Eb{i}", [128, 128], BF16) for i in range(3)]
    Ek = [cx.sb(f"Ek{i}", [128, 128], BF16) for i in range(3)]
    Ez = [cx.sb(f"Ez{i}", [128, 128], BF16) for i in range(3)]
    for i in range(3):
        for t_, k_ in ((Ear[i], f"Ear{i}"), (Eb[i], f"Eb{i}"), (Ek[i], f"Ek{i}"), (Ez[i], f"Ez{i}")):
            cx.op("pool", lambda e, t_=t_: e.memset(t_[:], 0.0), writes=[k_])
    EbkT = [cx.sb(f"EbkT{i}", [128, 256], BF16) for i in range(3)]
    Pm = [[cx.sb(f"Pm{q}{i}", [128, 128], F32) for i in range(2)] for q in range(2)]
    PTm = [[cx.sb(f"PTm{q}{i}", [128, 128], F32) for i in range(2)] for q in range(2)]
    Rm = [cx.sb(f"Rm{q}", [128, 128], F32) for q in range(2)]
    Rb = [cx.sb(f"Rb{i}", [128, 128], BF16) for i in range(3)]
    Arb = [cx.sb(f"Arb{i}", [128, 128], BF16) for i in range(3)]
    Aak_rk = [cx.sb(f"Aakrk{i}", [128, 256], BF16) for i in range(3)]
    Vs = [cx.sb(f"Vs{i}", [128, 64], BF16) for i in range(3)]
    Xb = cx.sb("Xb", [128, 64], BF16)
    Ub = cx.sb("Ub", [128, 64], BF16)
    S32 = cx.sb("S32", [128, 64], F32)
    S0b = cx.sb("S0b", [128, 64], BF16)
    ys = cx.sb("ys", [128, 64], F32)
    ysq = cx.sb("ysq", [128, 64], F32)
    yn = cx.sb("yn", [128, 64], F32)
    yob = cx.sb("yob", [128, 64], BF16)
    stat = cx.sb("stat", [128, 8], F32)
    bon = cx.sb("bon", [128, 2], F32)
    yg_d = None if "yg_tile" in io else dout("ygT", [128, 8, SEQ], BF16)

    for P in range(8):
        c0 = P * 128
        b_r = wblock("wr_c", 1024, c0, 128)
        b_k = wblock("wk_c", 1024, c0, 128)
        b_v = wblock("wv_c", 1024, c0, 128)
        ws.need(b_r)
        proj2(derive(b_r, 0, 128), 128, lambda tb, ps, key: cx.op(
            "act", lambda e: e.activation(out=r_bf[:, tb * 512:(tb + 1) * 512], in_=ps[:], func=AF.Copy), reads=[key], writes=["r_bf"]))
        ws.need(b_k)
        proj2(derive(b_k, 2, 128), 128, lambda tb, ps, key: cx.op(
            "act", lambda e: e.activation(out=k32[:, tb * 512:(tb + 1) * 512], in_=ps[:], func=AF.Copy), reads=[key], writes=["k32"]))
        ws.need(b_v)
        proj2(derive(b_v, 3, 128), 128, lambda tb, ps, key: cx.op(
            "act", lambda e: e.activation(out=v_bf[:, tb * 512:(tb + 1) * 512], in_=ps[:], func=AF.Copy), reads=[key], writes=["v_bf"]))
        for tb in range(4):
            ts_ = slice(tb * 512, (tb + 1) * 512)
            bk = pj[0] % 2
            pj[0] += 1
            cx.op("pe", lambda e: e.matmul(banks[bk][:], w2c[:, c0:c0 + 128], tw[:, ts_], start=True, stop=True),
                  reads=["w2c", "tw"], writes=[f"bank{bk}"])
            cx.op("act", lambda e: e.activation(out=ld32[:, ts_], in_=banks[bk][:], func=AF.Sigmoid, bias=w0T[:, P:P + 1], scale=1.0),
                  reads=[f"bank{bk}", "w0T"], writes=["ld32"])
            bk = pj[0] % 2
            pj[0] += 1
            cx.op("pe", lambda e: e.matmul(banks[bk][:], a2c[:, c0:c0 + 128], ta[:, ts_], start=True, stop=True),
                  reads=["a2c", "ta"], writes=[f"bank{bk}"])
            cx.op("act", lambda e: e.activation(out=a32[:, ts_], in_=banks[bk][:], func=AF.Sigmoid, bias=a0T[:, P:P + 1], scale=1.0),
                  reads=[f"bank{bk}", "a0T"], writes=["a32"])
            bk = pj[0] % 2
            pj[0] += 1
            for i in range(2):
                cx.op("pe", lambda e, i=i: e.matmul(banks[bk][:], g2c[:, i, c0:c0 + 128], tg[:, i, ts_], start=(i == 0), stop=(i == 1)),
                      reads=["g2c", "tg"], writes=[f"bank{bk}"])
            cx.op("act", lambda e: e.activation(out=g_bf[:, ts_], in_=banks[bk][:], func=AF.Copy), reads=[f"bank{bk}"], writes=["g_bf"])
        cx.op("dve", lambda e: e.tensor_scalar(out=ld32[:], in0=ld32[:], scalar1=-LD_C, scalar2=None, op0=ALU.mult),
              reads=["ld32"], writes=["ld32"])
        cx.op("dve", lambda e: e.tensor_scalar(out=kk32[:], in0=k32[:], scalar1=kkT[:, P:P + 1], scalar2=None, op0=ALU.mult),
              reads=["k32", "k_kT"], writes=["kk32"])
        for tb in range(4):
            ts_ = slice(tb * 512, (tb + 1) * 512)
            sq_, sk = sqt[0], "sqt0"
            cx.op("act", lambda e: e.activation(out=sq_[:], in_=kk32[:, ts_], func=AF.Square), reads=["kk32"], writes=[sk])
            bk = pj[0] % 2
            pj[0] += 1
            cx.op("pe", lambda e: e.matmul(banks[bk][:], blockones[:], sq_[:], start=True, stop=True),
                  reads=["blockones", sk], writes=[f"bank{bk}"])
            cx.op("act", lambda e: e.activation(out=sq_[:], in_=banks[bk][:], func=AF.Sqrt), reads=[f"bank{bk}"], writes=[sk])
            cx.op("dve", lambda e: e.tensor_scalar(out=sq_[:], in0=sq_[:], scalar1=1e-12, scalar2=None, op0=ALU.max), reads=[sk], writes=[sk])
            cx.op("dve", lambda e: e.reciprocal(out=sq_[:], in_=sq_[:]), reads=[sk], writes=[sk])
            cx.op("dve", lambda e: e.tensor_tensor(out=kk32[:, ts_], in0=kk32[:, ts_], in1=sq_[:], op=ALU.mult),
                  reads=["kk32", sk], writes=["kk32"])
        cx.op("dve", lambda e: e.tensor_scalar(out=ecl[:], in0=a32[:], scalar1=-1.0, scalar2=kaT[:, P:P + 1], op0=ALU.add, op1=ALU.mult),
              reads=["a32", "k_aT"], writes=["ecl"])
        cx.op("dve", lambda e: e.scalar_tensor_tensor(out=k32[:], in0=ecl[:], scalar=1.0, in1=k32[:], op0=ALU.add, op1=ALU.mult),
              reads=["ecl", "k32"], writes=["k32"])
        cx.op("pool", lambda e: e.tensor_tensor(out=a32[:], in0=a32[:], in1=kk32[:], op=ALU.mult), reads=["a32", "kk32"], writes=["a32"])
        cx.op("dve", lambda e: e.tensor_tensor_scan(out=cl32[:], data0=resetm[:], data1=ld32[:], initial=0.0, op0=ALU.mult, op1=ALU.add),
              reads=["resetmask", "ld32"], writes=["cl32"])
        cx.op("pool", lambda e: e.tensor_tensor(out=ld32[:], in0=cl32[:], in1=ld32[:], op=ALU.subtract), reads=["cl32", "ld32"], writes=["ld32"])
        cx.op("act", lambda e: e.activation(out=ld32[:], in_=ld32[:], func=AF.Exp), reads=["ld32"], writes=["ld32"])
        cx.op("act", lambda e: e.activation(out=ecl[:], in_=cl32[:], func=AF.Exp), reads=["cl32", "ecl"], writes=["ecl"])
        cx.op("act", lambda e: e.activation(out=cl32[:], in_=cl32[:], func=AF.Exp, scale=-1.0), reads=["cl32"], writes=["cl32"])
        eclm, encl, beta, kfin = ld32, cl32, a32, k32
        cx.op("dve", lambda e: e.memset(S32[:], 0.0), reads=["S32"], writes=["S32"])
        cx.op("dve", lambda e: e.memset(S0b[:], 0.0), reads=["S0b"], writes=["S0b"])
        def part1a(c):
            cs = slice(c * RC, (c + 1) * RC)
            par = c % 3
            q2 = c % 2
            for hd in range(2):
                R_ = slice(hd * 64, hd * 64 + 64)
                e1, e2 = ("dve", "pool") if hd == 0 else ("pool", "dve")
                cx.op(e1, lambda e: e.scalar_tensor_tensor(out=Ear[par][R_, hd * 64:hd * 64 + 64], in0=kk32[R_, cs], scalar=-1.0, in1=eclm[R_, cs],
                                                           op0=ALU.mult, op1=ALU.mult) if e1 == "dve" else
                      e.tensor_tensor(out=Ear[par][R_, hd * 64:hd * 64 + 64], in0=kk32[R_, cs], in1=eclm[R_, cs], op=ALU.mult),
                      reads=["kk32", "ld32"], writes=[f"Ear{par}"])
                if e1 != "dve":
                    cx.op("pool", lambda e: e.tensor_scalar(out=Ear[par][R_, hd * 64:hd * 64 + 64], in0=Ear[par][R_, hd * 64:hd * 64 + 64],
                                                             scalar1=-1.0, scalar2=None, op0=ALU.mult),
                          reads=[f"Ear{par}"], writes=[f"Ear{par}"])
                cx.op(e2, lambda e: e.tensor_tensor(out=Ear[par][R_, 128 + hd * 64:128 + hd * 64 + 64], in0=r_bf[R_, cs], in1=ecl[R_, cs], op=ALU.mult),
                      reads=["r_bf", "ecl"], writes=[f"Ear{par}"])
                cx.op(e1, lambda e: e.tensor_tensor(out=Eb[par][R_, hd * 64:hd * 64 + 64], in0=beta[R_, cs], in1=encl[R_, cs], op=ALU.mult),
                      reads=["a32", "cl32"], writes=[f"Eb{par}"])
                cx.op(e2, lambda e: e.tensor_tensor(out=Ek[par][R_, hd * 64:hd * 64 + 64], in0=kfin[R_, cs], in1=encl[R_, cs], op=ALU.mult),
                      reads=["k32", "cl32"], writes=[f"Ek{par}"])
                cx.op("dve", lambda e: e.scalar_tensor_tensor(out=Ez[par][R_, hd * 64:hd * 64 + 64], in0=kfin[R_, cs], scalar=rkT[R_, P:P + 1],
                                                               in1=r_bf[R_, cs], op0=ALU.mult, op1=ALU.mult),
                      reads=["k32", "r_kT", "r_bf"], writes=[f"Ez{par}"])
                cx.op("pe", lambda e: e.transpose(bankT[R_, 0:64], v_bf[R_, cs], identb[R_, hd * 64:hd * 64 + 64]),
                      reads=["v_bf", "identb"], writes=["bankT"])
            cx.op("act", lambda e: e.activation(out=Vs[par][:], in_=bankT[:, 0:64], func=AF.Copy), reads=["bankT"], writes=[f"Vs{par}"])
            cx.op("pe", lambda e: e.matmul(banks[2][:, 0:256], Eb[par][:], Ear[par][:], start=True, stop=True),
                  reads=[f"Eb{par}", f"Ear{par}"], writes=["bank2"])
            cx.op("pe", lambda e: e.matmul(banks[3][:, 0:256], Ek[par][:], Ear[par][:], start=True, stop=True),
                  reads=[f"Ek{par}", f"Ear{par}"], writes=["bank3"])
            cx.op("pe", lambda e: e.matmul(banks[2][:, 256:384], Ear[par][:, 0:128], Eb[par][:], start=True, stop=True),
                  reads=[f"Eb{par}", f"Ear{par}"], writes=["bank2"])
            cx.op("pe", lambda e: e.transpose(bankT[:, 128:256], Eb[par][:], identb[:]), reads=[f"Eb{par}", "identb"], writes=["bankT"])
            cx.op("pe", lambda e: e.transpose(bankT[:, 256:384], Ek[par][:], identb[:]), reads=[f"Ek{par}", "identb"], writes=["bankT"])
            cx.op("dve", lambda e: e.tensor_tensor(out=Pm[q2][0][:], in0=banks[2][:, 0:128], in1=mask3[:, 0:128], op=ALU.mult),
                  reads=["bank2", "mask3"], writes=[f"Pm{q2}0"])
            cx.op("dve", lambda e: e.tensor_tensor(out=Arb[par][:], in0=banks[2][:, 128:256], in1=mask3[:, 128:256], op=ALU.mult),
                  reads=["bank2", "mask3"], writes=[f"Arb{par}"])
            cx.op("dve", lambda e: e.tensor_tensor(out=Aak_rk[par][:], in0=banks[3][:, 0:256], in1=mask3[:, 0:256], op=ALU.mult),
                  reads=["bank3", "mask3"], writes=[f"Aakrk{par}"])
            cx.op("dve", lambda e: e.tensor_tensor(out=PTm[q2][0][:], in0=banks[2][:, 256:384], in1=mask3[:, 256:384], op=ALU.mult),
                  reads=["bank2", "mask3"], writes=[f"PTm{q2}0"])
            cx.op("act", lambda e: e.activation(out=EbkT[par][:], in_=bankT[:, 128:384], func=AF.Copy), reads=["bankT"], writes=[f"EbkT{par}"])
            cx.op("pool", lambda e: e.tensor_tensor(out=Rm[q2][:], in0=Pm[q2][0][:], in1=ident[:], op=ALU.add), reads=[f"Pm{q2}0", "ident"], writes=[f"Rm{q2}"])
        def part1b(c):
            par = c % 3
            q2 = c % 2
            cur = 0
            for lvl in range(1, 6):
                nxt = 1 - cur
                if lvl < 5:
                    cx.op("pe", lambda e: e.matmul(banks[5][:, 0:128], PTm[q2][cur][:], Pm[q2][cur][:], start=True, stop=True),
                          reads=[f"PTm{q2}{cur}", f"Pm{q2}{cur}"], writes=["bank5"])
                cx.op("pe", lambda e: e.matmul(banks[6][:, 0:128], Pm[q2][cur][:], PTm[q2][cur][:], start=True, stop=True),
                      reads=[f"PTm{q2}{cur}", f"Pm{q2}{cur}"], writes=["bank6"])
                if lvl < 5:
                    cx.op("act", lambda e: e.activation(out=Pm[q2][nxt][:], in_=banks[5][:, 0:128], func=AF.Copy),
                          reads=["bank5"], writes=[f"Pm{q2}{nxt}"])
                cx.op("dve", lambda e: e.tensor_copy(out=PTm[q2][nxt][:], in_=banks[6][:, 0:128]), reads=["bank6"], writes=[f"PTm{q2}{nxt}"])
                cx.op("pe", lambda e: e.matmul(banks[4][:, 0:128], PTm[q2][nxt][:], Rm[q2][:], start=True, stop=True),
                      reads=[f"PTm{q2}{nxt}", f"Rm{q2}"], writes=["bank4"])
                cx.op("dve", lambda e: e.tensor_tensor(out=Rm[q2][:], in0=Rm[q2][:], in1=banks[4][:, 0:128], op=ALU.add),
                      reads=[f"Rm{q2}", "bank4"], writes=[f"Rm{q2}"])
                cur = nxt
            cx.op("act", lambda e: e.activation(out=Rb[par][:], in_=Rm[q2][:], func=AF.Copy), reads=[f"Rm{q2}"], writes=[f"Rb{par}"])

        def part2(c):
            cs = slice(c * RC, (c + 1) * RC)
            par = c % 3
            cx.op("pe", lambda e: e.matmul(banks[0][:, 0:64], Ear[par][:, 0:128], S0b[:], start=True, stop=False),
                  reads=[f"Ear{par}", "S0b"], writes=["bank0"])
            cx.op("pe", lambda e: e.matmul(banks[0][:, 0:64], Aak_rk[par][:, 0:128], Vs[par][:], start=False, stop=True),
                  reads=[f"Aakrk{par}", f"Vs{par}"], writes=["bank0"])
            cx.op("act", lambda e: e.activation(out=Xb[:], in_=banks[0][:, 0:64], func=AF.Copy), reads=["bank0"], writes=["Xb"])
            cx.op("pe", lambda e: e.matmul(banks[1][:, 0:64], Rb[par][:], Xb[:], start=True, stop=True), reads=[f"Rb{par}", "Xb"], writes=["bank1"])
            cx.op("act", lambda e: e.activation(out=Ub[:], in_=banks[1][:, 0:64], func=AF.Copy), reads=["bank1"], writes=["Ub"])
            cx.op("pe", lambda e: e.matmul(banks[0][:, 0:64], Ear[par][:, 128:256], S0b[:], start=True, stop=False),
                  reads=[f"Ear{par}", "S0b"], writes=["bank0"])
            cx.op("pe", lambda e: e.matmul(banks[0][:, 0:64], Arb[par][:], Ub[:], start=False, stop=False), reads=[f"Arb{par}", "Ub"], writes=["bank0"])
            cx.op("pe", lambda e: e.matmul(banks[0][:, 0:64], Aak_rk[par][:, 128:256], Vs[par][:], start=False, stop=True),
                  reads=[f"Aakrk{par}", f"Vs{par}"], writes=["bank0"])
            cx.op("pe", lambda e: e.matmul(banks[0][:, 64:66], Ez[par][:], onesb[:], start=True, stop=True),
                  reads=[f"Ez{par}", "onesb"], writes=["bank0"])
            cx.op("pe", lambda e: e.matmul(banks[1][:, 0:64], EbkT[par][:, 0:128], Ub[:], start=True, stop=False), reads=[f"EbkT{par}", "Ub"], writes=["bank1"])
            cx.op("pe", lambda e: e.matmul(banks[1][:, 0:64], EbkT[par][:, 128:256], Vs[par][:], start=False, stop=True), reads=[f"EbkT{par}", f"Vs{par}"], writes=["bank1"])
            cx.op("dve", lambda e: e.tensor_tensor(out=S32[:], in0=S32[:], in1=banks[1][:, 0:64], op=ALU.add), reads=["S32", "bank1"], writes=["S32"])
            wc = ecl[:, c * RC + RC - 1:c * RC + RC]
            cx.op("dve", lambda e: e.tensor_scalar(out=S32[:], in0=S32[:], scalar1=wc, scalar2=None, op0=ALU.mult),
                  reads=["S32", "ecl"], writes=["S32"])
            cx.op("pool", lambda e: e.tensor_copy(out=S0b[:], in_=S32[:]), reads=["S32"], writes=["S0b"])
            cx.op("act", lambda e: e.activation(out=ys[:], in_=banks[0][:, 0:64], func=AF.Copy, accum_out=stat[:, 0:1]),
                  reads=["bank0"], writes=["ys", "stat"])
            cx.op("act", lambda e: e.activation(out=ysq[:], in_=ys[:], func=AF.Square, accum_out=stat[:, 1:2]),
                  reads=["ys"], writes=["ysq", "stat"])
            cx.op("dve", lambda e: e.tensor_scalar(out=stat[:, 2:3], in0=stat[:, 0:1], scalar1=1.0 / 64, scalar2=None, op0=ALU.mult),
                  reads=["stat"], writes=["stat"])
            cx.op("dve", lambda e: e.tensor_tensor(out=stat[:, 3:4], in0=stat[:, 2:3], in1=stat[:, 2:3], op=ALU.mult),
                  reads=["stat"], writes=["stat"])
            cx.op("dve", lambda e: e.scalar_tensor_tensor(out=stat[:, 4:5], in0=stat[:, 1:2], scalar=1.0 / 64, in1=stat[:, 3:4],
                                                           op0=ALU.mult, op1=ALU.subtract), reads=["stat"], writes=["stat"])
            cx.op("act", lambda e: e.activation(out=stat[:, 5:6], in_=stat[:, 4:5], func=AF.Sqrt, bias=epsg[:], scale=1.0),
                  reads=["stat", "epsg"], writes=["stat"])
            cx.op("dve", lambda e: e.reciprocal(out=stat[:, 5:6], in_=stat[:, 5:6]), reads=["stat"], writes=["stat"])
            cx.op("dve", lambda e: e.tensor_scalar(out=yn[:], in0=ys[:], scalar1=stat[:, 2:3], scalar2=stat[:, 5:6],
                                                    op0=ALU.subtract, op1=ALU.mult), reads=["ys", "stat"], writes=["yn"])
            cx.op("pool", lambda e: e.tensor_tensor(out=yn[:], in0=yn[:], in1=lng[:, P, :], op=ALU.mult), reads=["yn", "lng_stack"], writes=["yn"])
            cx.op("pool", lambda e: e.tensor_tensor(out=yn[:], in0=yn[:], in1=lnb[:, P, :], op=ALU.add), reads=["yn", "lnb_stack"], writes=["yn"])
            cx.op("act", lambda e: e.activation(out=bon[:], in_=banks[0][:, 64:66], func=AF.Copy), reads=["bank0"], writes=["bon"])
            cx.op("dve", lambda e: e.scalar_tensor_tensor(out=yob[:], in0=Vs[par][:], scalar=bon[:, 0:1], in1=yn[:], op0=ALU.mult, op1=ALU.add),
                  reads=[f"Vs{par}", "bon", "yn"], writes=["yob"])
            for hd in range(2):
                R_ = slice(hd * 64, hd * 64 + 64)
                cx.op("pe", lambda e: e.transpose(bankT[R_, 512:576], yob[R_, :], identb[R_, hd * 64:hd * 64 + 64]),
                      reads=["yob", "identb"], writes=["bankT"])
            cx.op("dve", lambda e: e.tensor_tensor(out=yg[:, cs], in0=bankT[:, 512:576], in1=g_bf[:, cs], op=ALU.mult),
                  reads=["bankT", "g_bf"], writes=["yg"])
        NCH_ = SEQ // RC
        for it in cx.record(part1a, 0):
            cx.play(it)
        cx.play_interleaved(cx.record(part1b, 0), cx.record(part1a, 1))
        for c in range(NCH_):
            la = cx.record(part2, c)
            lb = cx.record(part1b, c + 1) if c + 1 < NCH_ else []
            lc = cx.record(part1a, c + 2) if c + 2 < NCH_ else []
            cx.play_interleaved3(la, lb, lc)
        cx.dma("sp", st_slots[P % 2], [(io["yg_tile"](P) if "yg_tile" in io else yg_d[:, P, :], yg[:])], reads=["yg"])
    cx.close_scope()
    cx.wait_all("sp")
    return nc


def prep_M1(inp, b, j, hT_full):
    m = {"hT": hT_full, "ident": np.eye(128, dtype=np.float32)}
    cs = slice(j * 1024, (j + 1) * 1024)
    m["wr_c"] = np.ascontiguousarray(inp["rwkv_w_r"][0][:, cs])
    m["wk_c"] = np.ascontiguousarray(inp["rwkv_w_k"][0][:, cs])
    m["wv_c"] = np.ascontiguousarray(inp["rwkv_w_v"][0][:, cs])
    m["w1"] = inp["rwkv_w1"][0]
    m["a1"] = inp["rwkv_a1"][0]
    m["g1"] = inp["rwkv_g1"][0]
    m["w2c"] = np.ascontiguousarray(inp["rwkv_w2"][0][:, cs])
    m["a2c"] = np.ascontiguousarray(inp["rwkv_a2"][0][:, cs])
    m["g2c"] = np.ascontiguousarray(inp["rwkv_g2"][0][:, cs].reshape(2, 128, 1024).transpose(1, 0, 2))
    m["muT"] = np.ascontiguousarray(inp["rwkv_mu"][0].reshape(6, KT, 128).transpose(2, 0, 1))
    for nm, key in (("w0T", "rwkv_w0"), ("a0T", "rwkv_a0"), ("k_kT", "rwkv_k_k"), ("k_aT", "rwkv_k_a")):
        m[nm] = np.ascontiguousarray(inp[key][0][cs].reshape(8, 128).T)
    m["r_kT"] = np.ascontiguousarray(inp["rwkv_r_k"][0].reshape(-1)[cs].reshape(8, 128).T)
    lg = inp["rwkv_ln_g"][0][cs].reshape(8, 2, 64)
    lb = inp["rwkv_ln_b"][0][cs].reshape(8, 2, 64)
    lng = np.zeros((128, 8, 64), np.float32)
    lnb = np.zeros((128, 8, 64), np.float32)
    for hd in range(2):
        lng[hd * 64:(hd + 1) * 64] = lg[None, :, hd, :]
        lnb[hd * 64:(hd + 1) * 64] = lb[None, :, hd, :]
    m["lng_stack"], m["lnb_stack"] = lng, lnb
    s_ = np.arange(64)
    blk = np.kron(np.eye(2, dtype=np.float32), np.ones((64, 64), np.float32))
    mS = np.kron(np.eye(2, dtype=np.float32), (s_[:, None] < s_[None, :]).astype(np.float32))
    mI = np.kron(np.eye(2, dtype=np.float32), (s_[:, None] <= s_[None, :]).astype(np.float32))
    m["mask3"] = np.ascontiguousarray(np.concatenate([mS, mI, mS.T], axis=1))
    m["blockones"] = blk
    rm = np.ones((128, SEQ), np.float32)
    rm[:, ::RC] = 0.0
    m["resetmask"] = rm
    return m


def _T_maps(inp, stages, xT, extra):
    maps = []
    for core in range(NCORES):
        b = core // 2
        m = {"xT": xT[core], "cT": fm(inp["c"][b])}
        for stg in stages:
            kind = stg["kind"]
            if kind == "ffn":
                l, s_ = stg["l"], stg["s"]
                fi = 0 if s_ == 0 else 1
                m[f"w_mod{l}"] = inp["w_mod"][l]
                m[f"b_modT{l}"] = fm(inp["b_mod"][l])
                m[f"norm_gT{l}{s_}"] = fm(inp["norm_g"][l, s_])
                m[f"ffn_w1_{l}{fi}"] = inp["ffn_w1"][l, fi]
                m[f"ffn_w3_{l}{fi}"] = inp["ffn_w3"][l, fi]
                m[f"ffn_w2_{l}{fi}"] = inp["ffn_w2"][l, fi]
            elif kind == "h_out":
                l = stg["l"]
                m[f"w_mod{l}"] = inp["w_mod"][l]
                m[f"b_modT{l}"] = fm(inp["b_mod"][l])
                m[f"norm_gT{l}1"] = fm(inp["norm_g"][l, 1])
            elif kind == "mix0_post":
                m["w_mod0"] = inp["w_mod"][0]
                m["b_modT0"] = fm(inp["b_mod"][0])
                m["glu_bT"] = fm(inp["s5_glu_b"][0])
                m["ssd_norm_gT"] = fm(inp["ssd_norm_g"][0])
                m["s5_glu_w"] = inp["s5_glu_w"][0]
                m["hyb_w_out"] = inp["hyb_w_out"][0]
            elif kind == "rwkv_post":
                m["w_mod1"] = inp["w_mod"][1]
                m["b_modT1"] = fm(inp["b_mod"][1])
                m["rwkv_w_o"] = inp["rwkv_w_o"][0]
            elif kind == "final":
                m["final_gT"] = fm(inp["final_g"])
        m.update(extra[core])
        maps.append(m)
    return maps


def _run(nc, maps):
    return run_bass_kernel_spmd(nc, maps, core_ids=list(range(NCORES))).results


def _only_declared(maps):
    keep = set(LAST_DRAM.keys())
    return [{k: v for k, v in m.items() if k in keep} for m in maps]


def _pair_cat_tokens(tiles, b):
    return np.ascontiguousarray(np.concatenate([tiles[2 * b], tiles[2 * b + 1]], axis=2))


GROUPS = [[0, 1], [2, 3], [4, 5], [6, 7]]
ST0 = [{"kind": "ffn", "l": 0, "s": 0}, {"kind": "h_out", "l": 0}, {"kind": "x_out"}]
ST1 = [{"kind": "mix0_post"}, {"kind": "ffn", "l": 0, "s": 2}, {"kind": "ffn", "l": 1, "s": 0},
       {"kind": "h_out", "l": 1}, {"kind": "x_out"}]
ST2 = [{"kind": "rwkv_post"}, {"kind": "ffn", "l": 1, "s": 2}, {"kind": "final"}]


def build_fused(upto=None):
    nc = bass.Bass("TRN2", target_bir_lowering=False)
    cx = Ctx(nc)
    banks = [cx.ps(f"bank{i}") for i in range(7)]
    cx.uid += 1
    bankT = nc.alloc_psum_tensor(f"bankT_{cx.uid}", [128, 1024], BF16)
    G = {"nc": nc, "cx": cx, "dram": {}, "banks": banks, "bankT": bankT}

    def idram(name, shape, dt):
        return nc.dram_tensor(name, list(shape), dt).ap()

    ncc = [0]

    def allgather(src, dst):
        sl = cx.slot("cc")
        nc.gpsimd.collective_compute("AllGather", ALU.bypass, replica_groups=GROUPS,
                                     ins=[src.opt()], outs=[dst.opt()]).then_inc(sl["sem"])
        sl["count"] += 1
        ncc[0] += 1
        cx._record((sl["sem"], sl["count"], sl["key"]), [], [f"cc{ncc[0]}"])
        return f"cc{ncc[0]}"

    CH = 4096

    def chunks(name, ncol, dt):
        n = ncol // CH
        return ([idram(f"{name}_s{i}", [128, CH], dt) for i in range(n)],
                [idram(f"{name}_g{i}", [256, CH], dt) for i in range(n)])

    def gather_all(snd, rcv):
        return [allgather(a, b) for a, b in zip(snd, rcv)]

    def h_out_pairs(snd):
        return lambda h: [(snd[c].rearrange("p (k t) -> p k t", k=4), h[:, 4 * c:4 * c + 4, :]) for c in range(4)]

    def h_loader(rcv):
        def f(tile, off):
            pairs = []
            for r in range(2):
                for c in range(4):
                    src = rcv[c][r * 128:(r + 1) * 128, :].rearrange("p (k t) -> p k t", k=4)
                    pairs.append((tile[:, 4 * c:4 * c + 4, off + r * TOK:off + (r + 1) * TOK], src))
            return pairs
        return f

    def tile_fn(snd):
        return lambda idx: snd[idx // 2][:, (idx % 2) * SEQ:(idx % 2 + 1) * SEQ]

    def gath_fn(rcv):
        return lambda r, lt, half: rcv[lt // 2][r * 128:(r + 1) * 128, (lt % 2) * SEQ + half * TOK:(lt % 2) * SEQ + (half + 1) * TOK]

    xs1 = idram("xs1", [128, KT, TOK], F32)
    xs2 = idram("xs2", [128, KT, TOK], F32)
    h0s, h0g = chunks("h0", KT * TOK, BF16)
    h1s, h1g = chunks("h1", KT * TOK, BF16)
    y0s, y0g = chunks("y0", 12 * SEQ, F32)
    y1s, y1g = chunks("y1", 8 * SEQ, BF16)

    def dbg(n, src, shape, dt):
        if upto != n:
            return False
        cx.barrier()
        o = nc.dram_tensor("dbg", list(shape), dt, kind="ExternalOutput").ap()
        cx.dma("sp", cx.slot("dbg"), [(o, src)])
        cx.wait_all("sp")
        return True

    global LAST_DRAM
    LAST_DRAM = G["dram"]
    build_T(ST0, G, io={"hT_out_pairs": h_out_pairs(h0s), "xT_out": xs1})
    if dbg(1, xs1, [128, KT, TOK], F32):
        return nc
    cx.new_phase()
    dep = gather_all(h0s, h0g)
    if dbg(2, h0g[3], [256, CH], BF16):
        return nc
    build_M0(G=G, io={"hT_pairs": h_loader(h0g), "y_tile": tile_fn(y0s), "dep": dep})
    if dbg(3, y0s[0], [128, CH], F32):
        return nc
    cx.new_phase()
    dep = gather_all(y0s, y0g)
    if dbg(4, y0g[5], [256, CH], F32):
        return nc
    build_T(ST1, G, io={"xT": xs1, "y_gath": gath_fn(y0g), "hT_out_pairs": h_out_pairs(h1s), "xT_out": xs2, "dep": dep})
    if dbg(5, xs2, [128, KT, TOK], F32):
        return nc
    cx.new_phase()
    dep = gather_all(h1s, h1g)
    build_M1(G=G, io={"hT_pairs": h_loader(h1g), "yg_tile": tile_fn(y1s), "dep": dep})
    if dbg(6, y1s[0], [128, CH], BF16):
        return nc
    cx.new_phase()
    dep = gather_all(y1s, y1g)
    build_T(ST2, G, io={"xT": xs2, "yg_gath": gath_fn(y1g), "dep": dep})
    cx.wait_all("sp")
    return nc


LAST_DRAM = {}


def kernel_unfused(**inputs):
    inp = {k: np.asarray(v) for k, v in inputs.items()}
    xT = to_xT(inp["x"].astype(np.float32, copy=False))
    st0 = [{"kind": "ffn", "l": 0, "s": 0}, {"kind": "h_out", "l": 0}, {"kind": "x_out"}]
    r = _run(build_T(st0), _T_maps(inp, st0, xT, [{}] * NCORES))
    xT = [np.asarray(q["xT_out"]) for q in r]
    hT = [np.asarray(q["hT_out"]) for q in r]
    maps = [prep_M0(inp, c // 2, c % 2, _pair_cat_tokens(hT, c // 2)) for c in range(NCORES)]
    r = _run(build_M0(), maps)
    y5 = [np.asarray(q["y5T"]) for q in r]
    ys = [np.asarray(q["ysT"]) for q in r]
    extra = []
    for c in range(NCORES):
        b, jt = c // 2, c % 2
        ts = slice(jt * TOK, (jt + 1) * TOK)
        extra.append({"y5T_in": np.ascontiguousarray(np.concatenate([y5[2 * b][:, :, ts], y5[2 * b + 1][:, :, ts]], axis=1)),
                      "ysT_in": np.ascontiguousarray(np.concatenate([ys[2 * b][:, :, ts], ys[2 * b + 1][:, :, ts]], axis=1))})
    st1 = [{"kind": "mix0_post"}, {"kind": "ffn", "l": 0, "s": 2}, {"kind": "ffn", "l": 1, "s": 0},
           {"kind": "h_out", "l": 1}, {"kind": "x_out"}]
    r = _run(build_T(st1), _T_maps(inp, st1, xT, extra))
    xT = [np.asarray(q["xT_out"]) for q in r]
    hT = [np.asarray(q["hT_out"]) for q in r]
    maps = [prep_M1(inp, c // 2, c % 2, _pair_cat_tokens(hT, c // 2)) for c in range(NCORES)]
    r = _run(build_M1(), maps)
    yg = [np.asarray(q["ygT"]) for q in r]
    extra = []
    for c in range(NCORES):
        b, jt = c // 2, c % 2
        ts = slice(jt * TOK, (jt + 1) * TOK)
        extra.append({"ygT_in": np.ascontiguousarray(np.concatenate([yg[2 * b][:, :, ts], yg[2 * b + 1][:, :, ts]], axis=1))})
    st2 = [{"kind": "rwkv_post"}, {"kind": "ffn", "l": 1, "s": 2}, {"kind": "final"}]
    r = _run(build_T(st2), _T_maps(inp, st2, xT, extra))
    out = from_xT([np.asarray(q["xT_out"]) for q in r])
    return out.astype(np.float32)


def fused_maps(inp):
    xT = to_xT(inp["x"].astype(np.float32, copy=False))
    maps = []
    for c in range(NCORES):
        b, j = c // 2, c % 2
        m = {}
        for st in (ST0, ST1, ST2):
            m.update(_T_maps(inp, st, xT, [{}] * NCORES)[c])
        m0 = prep_M0(inp, b, j, None)
        m1 = prep_M1(inp, b, j, None)
        m0.pop("hT")
        m1.pop("hT")
        m.update(m0)
        m.update(m1)
        sel = np.zeros((128, 2), np.float32)
        sel[:, j] = 1.0
        m["selT"] = sel
        maps.append(m)
    return maps


def kernel(**inputs):
    inp = {k: np.asarray(v) for k, v in inputs.items()}
    nc = build_fused()
    r = _run(nc, _only_declared(fused_maps(inp)))
    out = from_xT([np.asarray(q["xT_out"]) for q in r])
    return out.astype(np.float32)
```

```python
import numpy as np
import concourse.bass as bass
import concourse.mybir as mybir
from concourse.bass_utils import run_bass_kernel_spmd

F32 = mybir.dt.float32
BF16 = mybir.dt.bfloat16
AF = mybir.ActivationFunctionType
ALU = mybir.AluOpType

D = 2048
KT = 16
FFN = 5632
NCORES = 8
TOK = 1024
SEQ = 2048
EPS = 1e-6


SKIP_SELF = {"pe"}


class _Eng:
    def __init__(self, name, obj, sem):
        self.name, self.obj, self.sem = name, obj, sem
        self.count = 0
        self.seen = {}


class _Rec:
    def __getattr__(self, name):
        def f(*a, **k):
            self.call = (name, a, k)
            return self
        return f


class Ctx:
    def record(self, fn, *args):
        self.rec = []
        fn(*args)
        lst, self.rec = self.rec, None
        return lst

    def play(self, item):
        engname, (name, a, k), reads, writes = item
        return self.op(engname, lambda e: getattr(e, name)(*a, **k), reads, writes)

    def play_interleaved(self, la, lb):
        i = j = 0
        na, nb = len(la), len(lb)
        while i < na or j < nb:
            if i < na and (j >= nb or i * nb <= j * na):
                self.play(la[i])
                i += 1
            else:
                self.play(lb[j])
                j += 1

    def play_interleaved3(self, la, lb, lc):
        lists = [l for l in (la, lb, lc) if l]
        pos = [0] * len(lists)
        while any(p < len(l) for p, l in zip(pos, lists)):
            k = min((i for i in range(len(lists)) if pos[i] < len(lists[i])), key=lambda i: pos[i] / len(lists[i]))
            self.play(lists[k][pos[k]])
            pos[k] += 1

    def __init__(self, nc):
        self.nc = nc
        self.engs = {}
        for name, attr in (("pe", "tensor"), ("act", "scalar"), ("dve", "vector"),
                           ("pool", "gpsimd"), ("sp", "sync")):
            self.engs[name] = _Eng(name, getattr(nc, attr), nc.alloc_semaphore("sem_" + name))
        self.res = {}
        self.nslots = 0
        self.uid = 0
        self.stacks = []
        self.free_slots = []
        self.phase = 0
        self.rec = None

    def sb(self, name, shape, dtype=F32):
        self.uid += 1
        if self.stacks:
            return self.stacks[-1][0].enter_context(self.nc.sbuf_tensor(f"{name}_{self.uid}", list(shape), dtype))
        return self.nc.alloc_sbuf_tensor(f"{name}_{self.uid}", list(shape), dtype)

    def open_scope(self):
        import contextlib
        self.stacks.append((contextlib.ExitStack(), []))

    def close_scope(self):
        self.barrier()
        st, slots = self.stacks.pop()
        st.close()
        self.free_slots.extend(slots)

    def ps(self, name):
        self.uid += 1
        return self.nc.alloc_psum_tensor(f"{name}_{self.uid}", [128, 512], F32)

    def slot(self, name):
        if self.free_slots:
            sl = self.free_slots.pop()
        else:
            self.nslots += 1
            sl = {"sem": self.nc.alloc_semaphore(f"dsem_{name}_{self.nslots}"), "count": 0,
                  "key": f"slot{self.nslots}"}
        if self.stacks:
            self.stacks[-1][1].append(sl)
        return sl

    def _deps(self, reads, writes):
        deps = []
        for r in reads:
            st = self.res.get(r)
            if st and st["w"]:
                deps.append(st["w"])
            if st and r.startswith("bank"):
                deps.extend(st["r"].values())
        for w in writes:
            st = self.res.get(w)
            if st:
                if st["w"]:
                    deps.append(st["w"])
                deps.extend(st["r"].values())
        return deps

    def _wait(self, eng, deps, skip_self):
        for sem, val, key in deps:
            if skip_self and key == eng.name:
                continue
            if eng.seen.get(key, 0) < val:
                eng.obj.wait_ge(sem, val)
                eng.seen[key] = val

    def _record(self, tok, reads, writes):
        for r in reads:
            st = self.res.setdefault(r, {"w": None, "r": {}})
            st["r"][tok[2]] = tok
        for w in writes:
            self.res[w] = {"w": tok, "r": {}}

    def op(self, engname, emit, reads=(), writes=()):
        if self.rec is not None:
            r = _Rec()
            emit(r)
            self.rec.append((engname, r.call, tuple(reads), tuple(writes)))
            return None
        eng = self.engs[engname]
        self._wait(eng, self._deps(reads, writes), skip_self=(engname in SKIP_SELF))
        inst = emit(eng.obj)
        eng.count += 1
        inst.then_inc(eng.sem, 1)
        tok = (eng.sem, eng.count, engname)
        eng.seen[engname] = max(eng.seen.get(engname, 0), 0)
        self._record(tok, reads, writes)
        return tok

    def dma(self, qname, slot, pairs, reads=(), writes=(), **kw):
        eng = self.engs[qname]
        self._wait(eng, self._deps(reads, writes), skip_self=False)
        for out, in_ in pairs:
            eng.obj.dma_start(out=out, in_=in_, **kw).then_inc(slot["sem"], 16)
            slot["count"] += 16
        tok = (slot["sem"], slot["count"], slot["key"])
        self._record(tok, reads, writes)
        return tok

    def wait_all(self, engname):
        eng = self.engs[engname]
        deps = []
        for e in self.engs.values():
            if e.count:
                deps.append((e.sem, e.count, e.name))
        for st in self.res.values():
            if st["w"]:
                deps.append(st["w"])
            deps.extend(st["r"].values())
        self._wait(eng, deps, skip_self=False)

    def barrier(self):
        for n in self.engs:
            self.wait_all(n)

    def new_phase(self):
        self.barrier()
        self.phase += 1
        for e in self.engs.values():
            e.sem = self.nc.alloc_semaphore(f"sem_{e.name}_p{self.phase}")
            e.count = 0
            e.seen = {}
        self.res = {}


class WStream:
    NSLOT = 4
    ELEMS = 8192

    def __init__(self, cx, nslot=4, elems=8192):
        self.cx = cx
        self.NSLOT, self.ELEMS = nslot, elems
        self.tiles = [cx.sb(f"wslot{i}", [128, self.ELEMS], BF16) for i in range(self.NSLOT)]
        self.slots = [cx.slot(f"w{i}") for i in range(self.NSLOT)]
        self.plan = []
        self.issued = 0

    def add(self, src_ap, a, b):
        assert a * b <= self.ELEMS
        self.plan.append((src_ap, a, b))
        return len(self.plan) - 1

    def view(self, i):
        _, a, b = self.plan[i]
        t = self.tiles[i % self.NSLOT]
        return t[:, 0:a * b].rearrange("p (a b) -> p a b", a=a)

    def key(self, i):
        return f"wslot{i % self.NSLOT}"

    def _issue(self, i):
        src, a, b = self.plan[i]
        v = self.view(i)
        pairs = [(v[:, :, c0:min(b, c0 + 1024)], src[:, :, c0:min(b, c0 + 1024)]) for c0 in range(0, b, 1024)]
        self.cx.dma("pool", self.slots[i % self.NSLOT], pairs, writes=[self.key(i)])

    def need(self, i):
        upto = min(len(self.plan), i + self.NSLOT)
        while self.issued < upto:
            self._issue(self.issued)
            self.issued += 1


def _mk_env(G):
    if G is not None:
        return G["nc"], G["cx"], G["dram"], G["banks"], G["bankT"]
    nc = bass.Bass("TRN2", target_bir_lowering=False)
    cx = Ctx(nc)
    banks = [cx.ps(f"bank{i}") for i in range(7)]
    cx.uid += 1
    bankT = nc.alloc_psum_tensor(f"bankT_{cx.uid}", [128, 1024], BF16)
    return nc, cx, {}, banks, bankT


def build_T(stages, G=None, io=None):
    nc, cx, dram, banks, bankT = _mk_env(G)
    io = io or {}
    cx.open_scope()

    def din(name, shape, dtype=F32):
        if name in io:
            return io[name]
        if name not in dram:
            dram[name] = nc.dram_tensor(name, list(shape), dtype, kind="ExternalInput").ap()
        return dram[name]

    def dout(name, shape, dtype=F32):
        if name in io:
            return io[name]
        dram[name] = nc.dram_tensor(name, list(shape), dtype, kind="ExternalOutput").ap()
        return dram[name]

    xT_d = din("xT", [128, KT, TOK])
    cT_d = din("cT", [128, KT])

    x = cx.sb("x", [128, KT, TOK], F32)
    sq = [cx.sb(f"sq{i}", [128, TOK], F32) for i in range(2)]
    rstd = cx.sb("rstd", [128, TOK], F32)
    cact = cx.sb("cact", [128, KT], BF16)
    cin = cx.sb("cin", [128, KT], F32)
    ones = cx.sb("ones", [128, 128], F32)
    ws = WStream(cx)
    ld = cx.slot("ld")
    ld2 = cx.slot("ld2")

    cx.op("dve", lambda e: e.memset(ones[:], 1.0), writes=["ones"])
    cx.dma("sp", ld, [(x[:, 0:KT // 2, :], xT_d[:, 0:KT // 2, :]), (x[:, KT // 2:KT, :], xT_d[:, KT // 2:KT, :])],
           writes=["x"])
    cx.dma("sp", ld2, [(cin[:], cT_d)], writes=["cin"])
    cx.op("act", lambda e: e.activation(out=cact[:], in_=cin[:], func=AF.Silu),
          reads=["cin"], writes=["cact"])

    small_id = [0]

    def load_small(name, shape):
        small_id[0] += 1
        t = cx.sb(name, shape, F32)
        sl = cx.slot(name)
        cx.dma("sp", sl, [(t[:], din(name, shape))], writes=[name + str(small_id[0])])
        return t, name + str(small_id[0])

    def compute_mod(l, s, which):
        wmod = din(f"w_mod{l}", [D, 9 * D])
        bmod, bkey = load_small(f"b_modT{l}", [128, 9 * KT])
        out = cx.sb(f"mod{l}{s}", [128, 3, KT], F32)
        okey = f"mod{l}{s}_" + "".join(str(w) for w in which)
        blocks = []
        for j in which:
            for cb in range(D // 512):
                c0 = s * 3 * D + j * D + cb * 512
                src = wmod[:, c0:c0 + 512].rearrange("(kt p) c -> p kt c", p=128)
                blocks.append((j, cb, ws.add(src, KT, 512)))
        bank = banks[6]
        for (j, cb, bid) in blocks:
            ws.need(bid)
            wv = ws.view(bid)
            for ft in range(4):
                for kt in range(KT):
                    cx.op("pe", lambda e, ft=ft, kt=kt, wv=wv: e.matmul(
                        bank[:, ft:ft + 1], wv[:, kt, ft * 128:(ft + 1) * 128], cact[:, kt:kt + 1],
                        start=(kt == 0), stop=(kt == KT - 1)),
                        reads=[ws.key(bid), "cact"], writes=["bank6"])
            jj = s * 3 + j
            col = jj * KT + cb * 4
            cx.op("dve", lambda e, j=j, cb=cb, col=col: e.tensor_tensor(
                out=out[:, j, cb * 4:cb * 4 + 4], in0=bank[:, 0:4], in1=bmod[:, col:col + 4], op=ALU.add),
                reads=["bank6", bkey], writes=[okey])
        return out, okey

    def rms_stats(xkey="x"):
        for kt in range(KT):
            s_ = sq[kt % 2]
            cx.op("act", lambda e, kt=kt, s_=s_: e.activation(out=s_[:], in_=x[:, kt, :], func=AF.Square),
                  reads=[xkey], writes=[f"sq{kt % 2}"])
            for t in range(2):
                cx.op("pe", lambda e, kt=kt, t=t, s_=s_: e.matmul(
                    banks[4 + t][:], ones[:], s_[:, t * 512:(t + 1) * 512],
                    start=(kt == 0), stop=(kt == KT - 1)),
                    reads=[f"sq{kt % 2}", "ones"], writes=[f"bank{4 + t}"])
        for t in range(2):
            cx.op("act", lambda e, t=t: e.activation(out=rstd[:, t * 512:(t + 1) * 512], in_=banks[4 + t][:],
                                                      func=AF.Sqrt, scale=1.0 / D, bias=epsb[:]),
                  reads=[f"bank{4 + t}", "epsb"], writes=["rstd"])
        cx.op("dve", lambda e: e.reciprocal(out=rstd[:], in_=rstd[:]), reads=["rstd"], writes=["rstd"])

    epsb = cx.sb("epsb", [128, 1], F32)
    cx.op("dve", lambda e: e.memset(epsb[:], EPS), writes=["epsb"])

    def adaln(l, s, mod, mkey, dst, dkey):
        ng, ngkey = load_small(f"norm_gT{l}{s}", [128, KT])
        a = cx.sb(f"a{l}{s}", [128, KT], F32)
        akey = f"a{l}{s}"
        cx.op("dve", lambda e: e.scalar_tensor_tensor(out=a[:], in0=mod[:, 1, :], scalar=1.0, in1=ng[:],
                                                      op0=ALU.add, op1=ALU.mult),
              reads=[mkey, ngkey], writes=[akey])
        rms_stats()
        for kt in range(KT):
            s_ = sq[kt % 2]
            cx.op("dve", lambda e, kt=kt, s_=s_: e.scalar_tensor_tensor(
                out=s_[:], in0=x[:, kt, :], scalar=a[:, kt:kt + 1], in1=rstd[:],
                op0=ALU.mult, op1=ALU.mult),
                reads=["x", akey, "rstd"], writes=[f"sq{kt % 2}"])
            cx.op("act", lambda e, kt=kt, s_=s_: e.activation(
                out=dst[:, kt, :], in_=s_[:], func=AF.Identity, bias=mod[:, 0, kt:kt + 1], scale=1.0),
                reads=[f"sq{kt % 2}", mkey], writes=[dkey])

    def ffn(l, s):
        fi = 0 if s == 0 else 1
        w1 = din(f"ffn_w1_{l}{fi}", [D, FFN])
        w3 = din(f"ffn_w3_{l}{fi}", [D, FFN])
        w2 = din(f"ffn_w2_{l}{fi}", [FFN, D])
        cx.open_scope()
        h = cx.sb("h", [128, KT, TOK], BF16)
        g = [cx.sb(f"g{i}", [128, 4, TOK], BF16) for i in range(2)]
        silu_t = [cx.sb(f"silu{i}", [128, 512], F32) for i in range(2)]
        mod, mkey = MODS[(l, s)]
        adaln(l, s, mod, mkey, h, "h")
        hg = cx.sb(f"hg{l}{s}", [128, KT], F32)
        cx.op("dve", lambda e: e.tensor_scalar(out=hg[:], in0=mod[:, 2, :], scalar1=0.5, scalar2=None,
                                               op0=ALU.mult),
              reads=[mkey], writes=[f"hg{l}{s}"])
        NCH = FFN // 512
        blk = []
        for c in range(NCH):
            b1 = ws.add(w1[:, c * 512:(c + 1) * 512].rearrange("(kt p) c -> p kt c", p=128), KT, 512)
            b3 = ws.add(w3[:, c * 512:(c + 1) * 512].rearrange("(kt p) c -> p kt c", p=128), KT, 512)
            b2 = ws.add(w2[c * 512:(c + 1) * 512, :].rearrange("(kt p) c -> p kt c", p=128), 4, D)
            blk.append((b1, b3, b2))
        ev = 0
        for c in range(NCH):
            b1, b3, b2 = blk[c]
            gb = g[c % 2]
            gkey = f"g{c % 2}"
            ws.need(b1)
            w1v, w3v = ws.view(b1), ws.view(b3)
            for m in range(4):
                for t in range(2):
                    pa, pb = banks[t * 2], banks[t * 2 + 1]
                    ka, kb = f"bank{t * 2}", f"bank{t * 2 + 1}"
                    for kt in range(KT):
                        cx.op("pe", lambda e, kt=kt, m=m, t=t, pa=pa: e.matmul(
                            pa[:], w1v[:, kt, m * 128:(m + 1) * 128], h[:, kt, t * 512:(t + 1) * 512],
                            start=(kt == 0), stop=(kt == KT - 1)),
                            reads=[ws.key(b1), "h"], writes=[ka])
                    for kt in range(KT):
                        cx.op("pe", lambda e, kt=kt, m=m, t=t, pb=pb: e.matmul(
                            pb[:], w3v[:, kt, m * 128:(m + 1) * 128], h[:, kt, t * 512:(t + 1) * 512],
                            start=(kt == 0), stop=(kt == KT - 1)),
                            reads=[ws.key(b3), "h"], writes=[kb])
                    st_ = silu_t[ev % 2]
                    skey = f"silu{ev % 2}"
                    ev += 1
                    cx.op("act", lambda e, pa=pa, st_=st_: e.activation(out=st_[:], in_=pa[:], func=AF.Silu),
                          reads=[ka], writes=[skey])
                    cx.op("dve", lambda e, pb=pb, st_=st_, m=m, t=t, gb=gb: e.tensor_tensor(
                        out=gb[:, m, t * 512:(t + 1) * 512], in0=st_[:], in1=pb[:], op=ALU.mult),
                        reads=[skey, kb], writes=[gkey])
            ws.need(b2)
            w2v = ws.view(b2)
            for j in range(KT):
                for t in range(2):
                    bi = 4 + ((j * 2 + t) % 3)
                    po, ko = banks[bi], f"bank{bi}"
                    for m in range(4):
                        cx.op("pe", lambda e, m=m, j=j, t=t, po=po, gb=gb: e.matmul(
                            po[:], w2v[:, m, j * 128:(j + 1) * 128], gb[:, m, t * 512:(t + 1) * 512],
                            start=(m == 0), stop=(m == 3)),
                            reads=[ws.key(b2), gkey], writes=[ko])
                    cx.op("dve", lambda e, j=j, t=t, po=po: e.scalar_tensor_tensor(
                        out=x[:, j, t * 512:(t + 1) * 512], in0=po[:], scalar=hg[:, j:j + 1],
                        in1=x[:, j, t * 512:(t + 1) * 512], op0=ALU.mult, op1=ALU.add),
                        reads=[ko, f"hg{l}{s}"], writes=["x"])
        cx.close_scope()

    def outproj(wname, krows, src, skey, nkt, gate, gkey):
        wd = din(wname, [krows, D])
        blks = [ws.add(wd[:, j * 128:(j + 1) * 128].rearrange("(kt p) c -> p kt c", p=128), nkt, 128) for j in range(KT)]
        for j in range(KT):
            ws.need(blks[j])
            wv = ws.view(blks[j])
            for t in range(2):
                bi = 4 + ((j * 2 + t) % 3)
                po, ko = banks[bi], f"bank{bi}"
                for kt in range(nkt):
                    cx.op("pe", lambda e, kt=kt: e.matmul(po[:], wv[:, kt, :], src[:, kt, t * 512:(t + 1) * 512],
                                                           start=(kt == 0), stop=(kt == nkt - 1)),
                          reads=[ws.key(blks[j]), skey], writes=[ko])
                cx.op("dve", lambda e: e.scalar_tensor_tensor(
                    out=x[:, j, t * 512:(t + 1) * 512], in0=po[:], scalar=gate[:, j:j + 1],
                    in1=x[:, j, t * 512:(t + 1) * 512], op0=ALU.mult, op1=ALU.add),
                    reads=[ko, gkey], writes=["x"])

    def gath_select(gath, ntile, dests, gdt):
        selT, selk = load_small("selT", [128, 2])
        if gdt == F32:
            stA, kA = sq, ["sq0", "sq1"]
        else:
            stA, kA = [cx.sb(f"gsA{i}", [128, TOK], gdt) for i in range(2)], ["gsA0", "gsA1"]
        stB = [cx.sb(f"gsB{i}", [128, TOK], gdt) for i in range(2)]
        sls = [cx.slot(f"gs{i}") for i in range(2)]
        n = 0
        for dst, dkey, tiles in dests:
            for di, (r, lt) in enumerate(tiles):
                b_ = n % 2
                n += 1
                cx.dma("sp", sls[b_], [(stA[b_][:], gath(r, lt, 0)), (stB[b_][:], gath(r, lt, 1))],
                       reads=io.get("dep", []), writes=[kA[b_], f"gsB{b_}"])
                cx.op("dve", lambda e: e.tensor_scalar(out=stB[b_][:], in0=stB[b_][:], scalar1=selT[:, 1:2], scalar2=None, op0=ALU.mult),
                      reads=[f"gsB{b_}", selk], writes=[f"gsB{b_}"])
                cx.op("dve", lambda e: e.scalar_tensor_tensor(out=dst[:, di, :], in0=stA[b_][:], scalar=selT[:, 0:1], in1=stB[b_][:],
                                                               op0=ALU.mult, op1=ALU.add),
                      reads=[kA[b_], f"gsB{b_}", selk], writes=[dkey])

    def mix0_post():
        cx.open_scope()
        mod, mkey = MODS[(0, 1, "g")]
        y5b = cx.sb("y5b", [128, 8, TOK], BF16)
        ysb = cx.sb("ysb", [128, 16, TOK], BF16)
        sig = cx.sb("sig", [128, 8, 512], BF16)
        sl5, sls = cx.slot("y5in"), cx.slot("ysin")
        if "y_gath" not in io:
            y5_d = din("y5T_in", [128, 8, TOK])
            ys_d = din("ysT_in", [128, 16, TOK])
        if "y_gath" in io:
            gath_select(io["y_gath"], 12, [(y5b, "y5b", [(ft // 4, ft % 4) for ft in range(8)]),
                                           (ysb, "ysb", [(kt // 8, 4 + kt % 8) for kt in range(16)])], F32)
        else:
            cx.dma("pool", sl5, cast_pairs(y5b[:], y5_d), writes=["y5b"])
            cx.dma("pool", sls, cast_pairs(ysb[:], ys_d), writes=["ysb"])
        glub, gbk = load_small("glu_bT", [128, 8])
        sng, sgk = load_small("ssd_norm_gT", [128, 16])
        gw = din("s5_glu_w", [1024, 1024])
        blks = [ws.add(gw[:, j * 128:(j + 1) * 128].rearrange("(kt p) c -> p kt c", p=128), 8, 128) for j in range(8)]
        for t in range(2):
            for j in range(8):
                ws.need(blks[j])
                bi = j % 4
                po, ko = banks[bi], f"bank{bi}"
                wv = ws.view(blks[j])
                for kt in range(8):
                    cx.op("pe", lambda e, kt=kt: e.matmul(po[:], wv[:, kt, :], y5b[:, kt, t * 512:(t + 1) * 512],
                                                           start=(kt == 0), stop=(kt == 7)),
                          reads=[ws.key(blks[j]), "y5b"], writes=[ko])
                cx.op("act", lambda e: e.activation(out=sig[:, j, :], in_=po[:], func=AF.Sigmoid, bias=glub[:, j:j + 1], scale=1.0),
                      reads=[ko, gbk], writes=["sig"])
            if t == 0:
                blks = [ws.add(gw[:, j * 128:(j + 1) * 128].rearrange("(kt p) c -> p kt c", p=128), 8, 128) for j in range(8)]
            for j in range(8):
                cx.op("dve", lambda e: e.tensor_tensor(out=y5b[:, j, t * 512:(t + 1) * 512], in0=y5b[:, j, t * 512:(t + 1) * 512],
                                                        in1=sig[:, j, :], op=ALU.mult), reads=["y5b", "sig"], writes=["y5b"])
        for kt in range(KT):
            s_ = sq[kt % 2]
            cx.op("act", lambda e: e.activation(out=s_[:], in_=ysb[:, kt, :], func=AF.Square), reads=["ysb"], writes=[f"sq{kt % 2}"])
            for t in range(2):
                cx.op("pe", lambda e: e.matmul(banks[4 + t][:], ones[:], s_[:, t * 512:(t + 1) * 512], start=(kt == 0), stop=(kt == KT - 1)),
                      reads=[f"sq{kt % 2}", "ones"], writes=[f"bank{4 + t}"])
        for t in range(2):
            cx.op("act", lambda e: e.activation(out=rstd[:, t * 512:(t + 1) * 512], in_=banks[4 + t][:], func=AF.Sqrt, scale=1.0 / D, bias=epsb[:]),
                  reads=[f"bank{4 + t}", "epsb"], writes=["rstd"])
        cx.op("dve", lambda e: e.reciprocal(out=rstd[:], in_=rstd[:]), reads=["rstd"], writes=["rstd"])
        for kt in range(KT):
            cx.op("dve", lambda e: e.scalar_tensor_tensor(out=ysb[:, kt, :], in0=ysb[:, kt, :], scalar=sng[:, kt:kt + 1], in1=rstd[:],
                                                           op0=ALU.mult, op1=ALU.mult), reads=["ysb", sgk, "rstd"], writes=["ysb"])
        wd = din("hyb_w_out", [3072, D])
        blk5 = [ws.add(wd[0:1024, j * 128:(j + 1) * 128].rearrange("(kt p) c -> p kt c", p=128), 8, 128) for j in range(KT)]
        gate = mod[:, 2, :]
        for j in range(KT):
            blks_ = ws.add(wd[1024:3072, j * 128:(j + 1) * 128].rearrange("(kt p) c -> p kt c", p=128), 16, 128)
            blk5[j] = (blk5[j], blks_)
        for part in range(2):
            for j in range(KT):
                b_ = blk5[j][part]
                ws.need(b_)
                wv = ws.view(b_)
                nk = 8 if part == 0 else 16
                srcb, skey = (y5b, "y5b") if part == 0 else (ysb, "ysb")
                for t in range(2):
                    bi = 4 + ((j * 2 + t) % 3)
                    po, ko = banks[bi], f"bank{bi}"
                    for kt in range(nk):
                        cx.op("pe", lambda e, kt=kt: e.matmul(po[:], wv[:, kt, :], srcb[:, kt, t * 512:(t + 1) * 512],
                                                               start=(kt == 0), stop=(kt == nk - 1)),
                              reads=[ws.key(b_), skey], writes=[ko])
                    cx.op("dve", lambda e: e.scalar_tensor_tensor(
                        out=x[:, j, t * 512:(t + 1) * 512], in0=po[:], scalar=gate[:, j:j + 1],
                        in1=x[:, j, t * 512:(t + 1) * 512], op0=ALU.mult, op1=ALU.add),
                        reads=[ko, mkey], writes=["x"])
        cx.close_scope()

    def rwkv_post():
        cx.open_scope()
        mod, mkey = MODS[(1, 1, "g")]
        ygb = cx.sb("ygb", [128, 16, TOK], BF16)
        slg = cx.slot("ygin")
        if "yg_gath" in io:
            gath_select(io["yg_gath"], 8, [(ygb, "ygb", [(kt // 8, kt % 8) for kt in range(16)])], BF16)
        else:
            yg_d = din("ygT_in", [128, 16, TOK], BF16)
            cx.dma("sp", slg, [(ygb[:, 4 * i:4 * i + 4, :], yg_d[:, 4 * i:4 * i + 4, :]) for i in range(4)], writes=["ygb"])
        outproj("rwkv_w_o", D, ygb, "ygb", KT, mod[:, 2, :], mkey)
        cx.close_scope()

    st_slot = cx.slot("st")
    MODS = {}
    for stg in stages:
        kind = stg["kind"]
        if kind == "ffn":
            MODS[(stg["l"], stg["s"])] = compute_mod(stg["l"], stg["s"], (0, 1, 2))
        elif kind == "h_out":
            MODS[(stg["l"], 1, "h")] = compute_mod(stg["l"], 1, (0, 1))
        elif kind == "mix0_post":
            MODS[(0, 1, "g")] = compute_mod(0, 1, (2,))
        elif kind == "rwkv_post":
            MODS[(1, 1, "g")] = compute_mod(1, 1, (2,))
    for stg in stages:
        kind = stg["kind"]
        if kind == "ffn":
            ffn(stg["l"], stg["s"])
        elif kind == "mix0_post":
            mix0_post()
        elif kind == "rwkv_post":
            rwkv_post()
        elif kind == "h_out":
            l = stg["l"]
            cx.open_scope()
            h = cx.sb("h", [128, KT, TOK], BF16)
            mod, mkey = MODS[(l, 1, "h")]
            adaln(l, 1, mod, mkey, h, "h")
            if "hT_out_pairs" in io:
                cx.dma("sp", st_slot, io["hT_out_pairs"](h), reads=["h"])
            else:
                hout = dout("hT_out", [128, KT, TOK], BF16)
                cx.dma("sp", st_slot, [(hout[:, 0:KT // 2, :], h[:, 0:KT // 2, :]), (hout[:, KT // 2:KT, :], h[:, KT // 2:KT, :])],
                       reads=["h"])
            cx.close_scope()
        elif kind == "x_out":
            xout = dout("xT_out", [128, KT, TOK], F32)
            cx.dma("sp", st_slot, [(xout[:, 0:KT // 2, :], x[:, 0:KT // 2, :]), (xout[:, KT // 2:KT, :], x[:, KT // 2:KT, :])],
                   reads=["x"])
        elif kind == "final":
            fg, fkey = load_small("final_gT", [128, KT])
            rms_stats()
            for kt in range(KT):
                cx.op("dve", lambda e, kt=kt: e.scalar_tensor_tensor(
                    out=x[:, kt, :], in0=x[:, kt, :], scalar=fg[:, kt:kt + 1], in1=rstd[:],
                    op0=ALU.mult, op1=ALU.mult),
                    reads=["x", fkey, "rstd"], writes=["x"])
            xout = dout("xT_out", [128, KT, TOK], F32)
            cx.dma("sp", st_slot, [(xout[:, 0:KT // 2, :], x[:, 0:KT // 2, :]), (xout[:, KT // 2:KT, :], x[:, KT // 2:KT, :])],
                   reads=["x"])
    cx.close_scope()
    cx.wait_all("sp")
    return nc


def cast_pairs(dst, src):
    if len(dst.shape) == 2:
        n = dst.shape[1]
        return [(dst[:, c0:min(n, c0 + 1024)], src[:, c0:min(n, c0 + 1024)]) for c0 in range(0, n, 1024)]
    out = []
    for a in range(dst.shape[1]):
        n = dst.shape[2]
        for c0 in range(0, n, 1024):
            out.append((dst[:, a, c0:min(n, c0 + 1024)], src[:, a, c0:min(n, c0 + 1024)]))
    return out


def fm(v):
    v = np.asarray(v)
    return np.ascontiguousarray(v.reshape(-1, 128).T)


def to_xT(x):
    out = []
    for b in range(4):
        for j in range(2):
            xs = x[b, j * TOK:(j + 1) * TOK, :]
            out.append(np.ascontiguousarray(xs.T.reshape(KT, 128, TOK).transpose(1, 0, 2)))
    return out


def from_xT(tiles):
    x = np.empty((4, SEQ, D), np.float32)
    for b in range(4):
        for j in range(2):
            t = tiles[b * 2 + j]
            x[b, j * TOK:(j + 1) * TOK, :] = t.transpose(1, 0, 2).reshape(D, TOK).T
    return x


S5TC = 256
GELU_C = 0.7978845608028654
TWO_PI = 6.283185307179586


CHK_COUNT = 1
DBG_BANKS = [0, 1]
DBG_NODVE = False


class _Stop(Exception):
    pass


def build_M0(do_s5=True, do_ssd=True, stop=None, G=None, io=None):
    nc, cx, dram, banks, bankT = _mk_env(G)
    io = io or {}
    cx.open_scope()

    cnt = [CHK_COUNT]

    def chk(n):
        if stop == n:
            cnt[0] -= 1
            if cnt[0] <= 0:
                raise _Stop()
    try:
        _build_M0_body(nc, cx, dram, banks, bankT, io, do_s5, do_ssd, chk)
    except _Stop:
        pass
    while cx.stacks and stop is not None:
        cx.close_scope()
    if stop is None:
        cx.close_scope()
    cx.wait_all("sp")
    return nc


def _build_M0_body(nc, cx, dram, banks, bankT, io, do_s5, do_ssd, chk):

    def din(name, shape, dtype=F32):
        if name in io:
            return io[name]
        if name not in dram:
            dram[name] = nc.dram_tensor(name, list(shape), dtype, kind="ExternalInput").ap()
        return dram[name]

    def dout(name, shape, dtype=F32):
        if name in io:
            return io[name]
        dram[name] = nc.dram_tensor(name, list(shape), dtype, kind="ExternalOutput").ap()
        return dram[name]

    BF16S = "bf16_from_f32"

    def load(name, shape, dtype=F32, q="sp"):
        sdt, ddt = (BF16, F32) if dtype == BF16S else (dtype, dtype)
        t = cx.sb(name, shape, sdt)
        sl = cx.slot(name)
        pairs = cast_pairs(t[:], din(name, shape, ddt)) if dtype == BF16S else [(t[:], din(name, shape, ddt))]
        cx.dma(q, sl, pairs, writes=[name])
        return t

    w_d = din("w_in_c", [D, 3600])
    hT = cx.sb("hT", [128, KT, SEQ], BF16)
    sl = cx.slot("hT")
    if "hT_pairs" in io:
        cx.dma("sp", sl, io["hT_pairs"](hT, 0), reads=io.get("dep", []), writes=["hT"])
    else:
        hT_d = din("hT", [128, KT, SEQ], BF16)
        cx.dma("sp", sl, [(hT[:, 4 * i:4 * i + 4, :], hT_d[:, 4 * i:4 * i + 4, :]) for i in range(4)], writes=["hT"])
    ws = WStream(cx, nslot=4, elems=4096)
    ident = load("ident", [128, 128])
    identb = cx.sb("identb", [128, 128], BF16)
    cx.op("dve", lambda e: e.tensor_copy(out=identb[:], in_=ident[:]), reads=["ident"], writes=["identb"])
    st_slot = cx.slot("st")
    st_slot2 = cx.slot("st2")

    def proj(blk, col0, ncol_tiles, evac):
        wv = ws.view(blk)
        n = 0
        for ti in range(ncol_tiles):
            for tb in range(4):
                bk = DBG_BANKS[n % len(DBG_BANKS)]
                n += 1
                for kt in range(KT):
                    cx.op("pe", lambda e, kt=kt, ti=ti, tb=tb, bk=bk: e.matmul(
                        banks[bk][:], wv[:, kt, (col0 + ti) * 128:(col0 + ti + 1) * 128],
                        hT[:, kt, tb * 512:(tb + 1) * 512], start=(kt == 0), stop=(kt == KT - 1)),
                        reads=[ws.key(blk), "hT"], writes=[f"bank{bk}"])
                chk(53)
                evac(ti, tb, banks[bk], f"bank{bk}")
                chk(54)

    if do_s5:
        cx.open_scope()
        lre = load("s5_lre", [128, 16])
        lim = load("s5_lim", [128, 16])
        ldt = load("s5_ldt", [128, 16])
        d5 = load("s5_dT", [128, 4])
        cre = load("s5_cre", [128, 16, 128], BF16S, q="pool")
        cimn = load("s5_cim", [128, 16, 128], BF16S, q="pool")
        cx.op("dve", lambda e: e.tensor_scalar(out=cimn[:], in0=cimn[:], scalar1=-1.0, scalar2=None, op0=ALU.mult),
              reads=["s5_cim"], writes=["s5_cim"])
        bre = cx.sb("breT", [128, 16, 128], BF16)
        bim = cx.sb("bimT", [128, 16, 128], BF16)

        sm = {}

        def S(name):
            sm[name] = cx.sb("s5_" + name, [128, 16], F32)
            return sm[name]

        def tt(o, a, b, op, eng="dve"):
            cx.op(eng, lambda e: e.tensor_tensor(out=sm[o][:], in0=sm[a][:], in1=sm[b][:], op=op),
                  reads=["s5sm"], writes=["s5sm"])

        def ts(o, a, s1, op0, s2=None, op1=None):
            if op1 is None:
                cx.op("dve", lambda e: e.tensor_scalar(out=sm[o][:], in0=sm[a][:], scalar1=s1, scalar2=None, op0=op0),
                      reads=["s5sm"], writes=["s5sm"])
            else:
                cx.op("dve", lambda e: e.tensor_scalar(out=sm[o][:], in0=sm[a][:], scalar1=s1, scalar2=s2, op0=op0, op1=op1),
                      reads=["s5sm"], writes=["s5sm"])

        def act(o, a, func, scale=1.0):
            cx.op("act", lambda e: e.activation(out=sm[o][:], in_=sm[a][:], func=func, scale=scale),
                  reads=["s5sm"], writes=["s5sm"])

        chk(1)
        sm["lre"], sm["lim"], sm["ldt"] = lre, lim, ldt
        for n_ in ("lr", "dt", "mag", "ang", "cs", "sn", "t1", "t2", "t3", "den", "nr", "fre", "fim", "lbr", "lbi", "rden"):
            S(n_)
        cx.wait_all("dve")
        cx.wait_all("act")
        ts("lr", "lre", -1e-4, ALU.min)
        act("dt", "ldt", AF.Exp)
        tt("t1", "lr", "dt", ALU.mult)
        act("mag", "t1", AF.Exp)
        tt("ang", "lim", "dt", ALU.mult)

        def sincos(o, a, shift):
            ki = cx.sb("s5_ki", [128, 16], mybir.dt.int32)
            ts("t1", a, 1.0 / TWO_PI, ALU.mult, shift / TWO_PI, ALU.add)
            cx.op("dve", lambda e: e.tensor_copy(out=ki[:], in_=sm["t1"][:]), reads=["s5sm"], writes=["s5ki"])
            cx.op("dve", lambda e: e.tensor_copy(out=sm["t2"][:], in_=ki[:]), reads=["s5ki"], writes=["s5sm"])
            tt("t1", "t1", "t2", ALU.subtract)
            ts("t2", "t1", 0.5, ALU.is_gt)
            tt("t1", "t1", "t2", ALU.subtract)
            ts("t2", "t1", -0.5, ALU.is_lt)
            tt("t1", "t1", "t2", ALU.add)
            act(o, "t1", AF.Sin, scale=TWO_PI)

        sincos("sn", "ang", 0.0)
        sincos("cs", "ang", TWO_PI / 4)
        tt("lbr", "mag", "cs", ALU.mult)
        tt("lbi", "mag", "sn", ALU.mult)
        tt("t1", "lr", "lr", ALU.mult)
        tt("t2", "lim", "lim", ALU.mult)
        tt("den", "t1", "t2", ALU.add)
        cx.op("dve", lambda e: e.reciprocal(out=sm["rden"][:], in_=sm["den"][:]), reads=["s5sm"], writes=["s5sm"])
        ts("nr", "lbr", -1.0, ALU.add)
        tt("t1", "nr", "lr", ALU.mult)
        tt("t2", "lbi", "lim", ALU.mult)
        tt("t1", "t1", "t2", ALU.add)
        tt("fre", "t1", "rden", ALU.mult)
        tt("t1", "lbi", "lr", ALU.mult)
        tt("t2", "nr", "lim", ALU.mult)
        tt("t1", "t1", "t2", ALU.subtract)
        tt("fim", "t1", "rden", ALU.mult)
        S("nfim")
        ts("nfim", "fim", -1.0, ALU.mult)
        S("nsn")
        ts("nsn", "sn", -1.0, ALU.mult)

        chk(2)
        cx.open_scope()
        xbre = load("s5_xbre", [128, 16, 128], q="act")
        xbim = load("s5_xbim", [128, 16, 128], q="act")
        xt = [cx.sb(f"s5xt{i}", [128, 128], F32) for i in range(2)]
        for pr in range(16):
            for part, (A, fa, Bm, fb) in enumerate(((xbre, "fre", xbim, "nfim"), (xbim, "fre", xbre, "fim"))):
                t_ = xt[part]
                cx.op("dve", lambda e, pr=pr, A=A, fa=fa, t_=t_: e.tensor_scalar(
                    out=t_[:], in0=A[:, pr, :], scalar1=sm[fa][:, pr:pr + 1], scalar2=None, op0=ALU.mult),
                    reads=["s5sm", "s5_xbre", "s5_xbim"], writes=[f"s5xt{part}"])
                cx.op("dve", lambda e, pr=pr, Bm=Bm, fb=fb, t_=t_: e.scalar_tensor_tensor(
                    out=t_[:], in0=Bm[:, pr, :], scalar=sm[fb][:, pr:pr + 1], in1=t_[:], op0=ALU.mult, op1=ALU.add),
                    reads=["s5sm", "s5_xbre", "s5_xbim", f"s5xt{part}"], writes=[f"s5xt{part}"])
                cx.op("pe", lambda e, t_=t_, part=part: e.transpose(banks[2 + part][:, 0:128], t_[:], ident[:]),
                      reads=[f"s5xt{part}", "ident"], writes=[f"bank{2 + part}"])
                dst = bre if part == 0 else bim
                cx.op("act", lambda e, dst=dst, pr=pr, part=part: e.activation(
                    out=dst[:, pr, :], in_=banks[2 + part][:, 0:128], func=AF.Copy),
                    reads=[f"bank{2 + part}"], writes=["breT" if part == 0 else "bimT"])

        chk(3)
        cx.close_scope()
        chk(4)
        ctab = cx.sb("ctab", [128, 16, S5TC], F32)
        stab = cx.sb("stab", [128, 16, S5TC], F32)
        rho = cx.sb("rho", [128, 16, S5TC], F32)
        ec = cx.sb("ec", [128, 16], F32)
        es = cx.sb("es", [128, 16], F32)
        et = [cx.sb(f"et{i}", [128, 16], F32) for i in range(3)]
        cx.op("dve", lambda e: e.memset(ctab[:, :, 0:1], 1.0), writes=["tab"])
        cx.op("dve", lambda e: e.memset(stab[:, :, 0:1], 0.0), reads=["tab"], writes=["tab"])
        cx.op("dve", lambda e: e.tensor_copy(out=ec[:], in_=sm["cs"][:]), reads=["s5sm"], writes=["e"])
        cx.op("dve", lambda e: e.tensor_copy(out=es[:], in_=sm["sn"][:]), reads=["s5sm", "e"], writes=["e"])
        L = 1
        while L < S5TC:
            for pr in range(16):
                cx.op("dve", lambda e, pr=pr, L=L: e.tensor_scalar(
                    out=ctab[:, pr, L:2 * L], in0=ctab[:, pr, 0:L], scalar1=ec[:, pr:pr + 1], scalar2=None, op0=ALU.mult),
                    reads=["tab", "e"], writes=["tab"])
                cx.op("dve", lambda e, pr=pr, L=L: e.tensor_scalar(
                    out=stab[:, pr, L:2 * L], in0=ctab[:, pr, 0:L], scalar1=es[:, pr:pr + 1], scalar2=None, op0=ALU.mult),
                    reads=["tab", "e"], writes=["tab"])
            cx.op("dve", lambda e: e.tensor_scalar(out=et[0][:], in0=es[:], scalar1=-1.0, scalar2=None, op0=ALU.mult),
                  reads=["e"], writes=["et"])
            for pr in range(16):
                cx.op("dve", lambda e, pr=pr, L=L: e.scalar_tensor_tensor(
                    out=ctab[:, pr, L:2 * L], in0=stab[:, pr, 0:L], scalar=et[0][:, pr:pr + 1], in1=ctab[:, pr, L:2 * L],
                    op0=ALU.mult, op1=ALU.add), reads=["tab", "et"], writes=["tab"])
                cx.op("dve", lambda e, pr=pr, L=L: e.scalar_tensor_tensor(
                    out=stab[:, pr, L:2 * L], in0=stab[:, pr, 0:L], scalar=ec[:, pr:pr + 1], in1=stab[:, pr, L:2 * L],
                    op0=ALU.mult, op1=ALU.add), reads=["tab", "e"], writes=["tab"])
            cx.op("dve", lambda e: e.tensor_tensor(out=et[1][:], in0=ec[:], in1=ec[:], op=ALU.mult), reads=["e"], writes=["et1"])
            cx.op("dve", lambda e: e.tensor_tensor(out=et[2][:], in0=es[:], in1=es[:], op=ALU.mult), reads=["e"], writes=["et2"])
            cx.op("dve", lambda e: e.scalar_tensor_tensor(out=es[:], in0=es[:], scalar=2.0, in1=ec[:], op0=ALU.mult, op1=ALU.mult),
                  reads=["e"], writes=["e"])
            cx.op("dve", lambda e: e.tensor_tensor(out=ec[:], in0=et[1][:], in1=et[2][:], op=ALU.subtract),
                  reads=["et1", "et2", "e"], writes=["e"])
            L *= 2
        for pr in range(16):
            cx.op("act", lambda e, pr=pr: e.activation(out=rho[:, pr, :], in_=ctab[:, pr, :], func=AF.Identity,
                                                        scale=0.0, bias=sm["mag"][:, pr:pr + 1]),
                  reads=["tab", "s5sm"], writes=["rho"])

        chk(5)
        u32 = cx.sb("u32", [128, SEQ], F32)
        ubf = cx.sb("ubf", [128, SEQ], BF16)
        y5 = cx.sb("y5", [128, SEQ], F32)
        carry = [cx.sb(f"carry{i}", [128, 16], F32) for i in range(2)]
        cx.op("dve", lambda e: e.memset(carry[0][:], 0.0), writes=["carry"])
        cx.op("dve", lambda e: e.memset(carry[1][:], 0.0), reads=["carry"], writes=["carry"])
        chk(51)
        tmp = [[cx.sb(f"s5tmp{j}{i}", [128, S5TC], F32) for i in range(8)] for j in range(2)]
        sbf = [[cx.sb(f"s5sbf{i}{j}", [128, S5TC], BF16) for j in range(2)] for i in range(4)]
        gl = [cx.sb(f"s5gl{i}", [128, S5TC], F32) for i in range(4)]
        y5_d = None if "y_tile" in io else dout("y5T", [128, 4, SEQ])
        NCH = SEQ // S5TC
        for o in range(4):
            blk = ws.add(w_d[:, o * 128:(o + 1) * 128].rearrange("(kt p) c -> p kt c", p=128), KT, 128)
            ws.need(blk)
            chk(52)

            def ev_u(ti, tb, ps, key):
                cx.op("act", lambda e: e.activation(out=u32[:, tb * 512:(tb + 1) * 512], in_=ps[:], func=AF.Copy),
                      reads=[key], writes=["u32"])
                if not DBG_NODVE:
                    cx.op("dve", lambda e: e.tensor_copy(out=ubf[:, tb * 512:(tb + 1) * 512], in_=ps[:]),
                          reads=[key], writes=["ubf"])
            proj(blk, 0, 1, ev_u)
            chk(6)
            for ch in range(NCH):
                c0 = ch * S5TC
                if ch == 1:
                    chk(7)
                for pp in range(4):
                    pr = o * 4 + pp
                    kre, kim = f"bank{2 + (pp % 2) * 2}", f"bank{3 + (pp % 2) * 2}"
                    pre, pim = banks[2 + (pp % 2) * 2], banks[3 + (pp % 2) * 2]
                    cx.op("pe", lambda e: e.matmul(pre[:, 0:S5TC], bre[:, pr, :], ubf[:, c0:c0 + S5TC], start=True, stop=True),
                          reads=["breT", "ubf"], writes=[kre])
                    cx.op("pe", lambda e: e.matmul(pim[:, 0:S5TC], bim[:, pr, :], ubf[:, c0:c0 + S5TC], start=True, stop=True),
                          reads=["bimT", "ubf"], writes=[kim])
                    t = tmp[pp % 2]
                    tq = pp % 2
                    ct, stb = ctab[:, pr, :], stab[:, pr, :]
                    cx.op("dve", lambda e: e.tensor_tensor(out=t[0][:], in0=pre[:, 0:S5TC], in1=ct, op=ALU.mult),
                          reads=[kre, "tab"], writes=[f"t{tq}_0"])
                    cx.op("dve", lambda e: e.tensor_tensor(out=t[1][:], in0=pim[:, 0:S5TC], in1=stb, op=ALU.mult),
                          reads=[kim, "tab"], writes=[f"t{tq}_1"])
                    cx.op("dve", lambda e: e.tensor_tensor(out=t[2][:], in0=pim[:, 0:S5TC], in1=ct, op=ALU.mult),
                          reads=[kim, "tab"], writes=[f"t{tq}_2"])
                    cx.op("dve", lambda e: e.tensor_tensor(out=t[3][:], in0=pre[:, 0:S5TC], in1=stb, op=ALU.mult),
                          reads=[kre, "tab"], writes=[f"t{tq}_3"])
                    cx.op("dve", lambda e: e.tensor_tensor(out=t[0][:], in0=t[0][:], in1=t[1][:], op=ALU.add),
                          reads=[f"t{tq}_0", f"t{tq}_1"], writes=[f"t{tq}_0"])
                    cx.op("dve", lambda e: e.tensor_tensor(out=t[2][:], in0=t[2][:], in1=t[3][:], op=ALU.subtract),
                          reads=[f"t{tq}_2", f"t{tq}_3"], writes=[f"t{tq}_2"])
                    cx.op("dve", lambda e: e.tensor_tensor_scan(out=t[4][:], data0=rho[:, pr, :], data1=t[0][:],
                                                                 initial=carry[0][:, pr:pr + 1], op0=ALU.mult, op1=ALU.add),
                          reads=["rho", f"t{tq}_0", "carry"], writes=[f"t{tq}_4"])
                    cx.op("dve", lambda e: e.tensor_tensor_scan(out=t[5][:], data0=rho[:, pr, :], data1=t[2][:],
                                                                 initial=carry[1][:, pr:pr + 1], op0=ALU.mult, op1=ALU.add),
                          reads=["rho", f"t{tq}_2", "carry"], writes=[f"t{tq}_5"])
                    if ch < NCH - 1:
                        cx.op("dve", lambda e: e.tensor_scalar(out=et[1][:, 0:1], in0=t[5][:, S5TC - 1:S5TC], scalar1=es[:, pr:pr + 1],
                                                                scalar2=None, op0=ALU.mult), reads=[f"t{tq}_5", "e"], writes=["et1"])
                        cx.op("dve", lambda e: e.tensor_scalar(out=et[2][:, 0:1], in0=t[4][:, S5TC - 1:S5TC], scalar1=es[:, pr:pr + 1],
                                                                scalar2=None, op0=ALU.mult), reads=[f"t{tq}_4", "e"], writes=["et2"])
                        cx.op("dve", lambda e: e.scalar_tensor_tensor(out=carry[0][:, pr:pr + 1], in0=t[4][:, S5TC - 1:S5TC],
                                                                       scalar=ec[:, pr:pr + 1], in1=et[1][:, 0:1],
                                                                       op0=ALU.mult, op1=ALU.subtract),
                              reads=[f"t{tq}_4", "e", "et1", "carry"], writes=["carry"])
                        cx.op("dve", lambda e: e.scalar_tensor_tensor(out=carry[1][:, pr:pr + 1], in0=t[5][:, S5TC - 1:S5TC],
                                                                       scalar=ec[:, pr:pr + 1], in1=et[2][:, 0:1],
                                                                       op0=ALU.mult, op1=ALU.add),
                              reads=[f"t{tq}_5", "e", "et2", "carry"], writes=["carry"])
                    cx.op("pool", lambda e: e.tensor_tensor(out=t[6][:], in0=t[4][:], in1=ct, op=ALU.mult),
                          reads=[f"t{tq}_4", "tab"], writes=[f"t{tq}_6"])
                    cx.op("pool", lambda e: e.tensor_tensor(out=t[7][:], in0=t[5][:], in1=stb, op=ALU.mult),
                          reads=[f"t{tq}_5", "tab"], writes=[f"t{tq}_7"])
                    cx.op("pool", lambda e: e.tensor_tensor(out=sbf[pp][0][:], in0=t[6][:], in1=t[7][:], op=ALU.subtract),
                          reads=[f"t{tq}_6", f"t{tq}_7"], writes=[f"sbf{pp}0"])
                    cx.op("pool", lambda e: e.tensor_tensor(out=t[6][:], in0=t[4][:], in1=stb, op=ALU.mult),
                          reads=[f"t{tq}_4", "tab", f"t{tq}_6"], writes=[f"t{tq}_6"])
                    cx.op("pool", lambda e: e.tensor_tensor(out=t[7][:], in0=t[5][:], in1=ct, op=ALU.mult),
                          reads=[f"t{tq}_5", "tab", f"t{tq}_7"], writes=[f"t{tq}_7"])
                    cx.op("pool", lambda e: e.tensor_tensor(out=sbf[pp][1][:], in0=t[6][:], in1=t[7][:], op=ALU.add),
                          reads=[f"t{tq}_6", f"t{tq}_7"], writes=[f"sbf{pp}1"])
                py = banks[6]
                for pp in range(4):
                    pr = o * 4 + pp
                    cx.op("pe", lambda e: e.matmul(py[:, 0:S5TC], cre[:, pr, :], sbf[pp][0][:], start=(pp == 0), stop=False),
                          reads=["s5_cre", f"sbf{pp}0"], writes=["bank6"])
                    cx.op("pe", lambda e: e.matmul(py[:, 0:S5TC], cimn[:, pr, :], sbf[pp][1][:], start=False, stop=(pp == 3)),
                          reads=["s5_cim", f"sbf{pp}1"], writes=["bank6"])
                cx.op("dve", lambda e: e.scalar_tensor_tensor(out=gl[0][:], in0=u32[:, c0:c0 + S5TC], scalar=d5[:, o:o + 1],
                                                               in1=py[:, 0:S5TC], op0=ALU.mult, op1=ALU.add),
                      reads=["u32", "s5_dT", "bank6"], writes=["gl0"])
                cx.op("act", lambda e: e.activation(out=gl[1][:], in_=gl[0][:], func=AF.Square), reads=["gl0"], writes=["gl1"])
                cx.op("dve", lambda e: e.tensor_scalar(out=gl[1][:], in0=gl[1][:], scalar1=GELU_C * 0.044715, scalar2=GELU_C,
                                                        op0=ALU.mult, op1=ALU.add), reads=["gl1"], writes=["gl1"])
                cx.op("dve", lambda e: e.tensor_tensor(out=gl[1][:], in0=gl[1][:], in1=gl[0][:], op=ALU.mult),
                      reads=["gl1", "gl0"], writes=["gl1"])
                cx.op("act", lambda e: e.activation(out=gl[2][:], in_=gl[1][:], func=AF.Tanh), reads=["gl1"], writes=["gl2"])
                cx.op("act", lambda e: e.activation(out=gl[3][:], in_=gl[0][:], func=AF.Copy, scale=0.5), reads=["gl0"], writes=["gl3"])
                cx.op("dve", lambda e: e.scalar_tensor_tensor(out=y5[:, c0:c0 + S5TC], in0=gl[2][:], scalar=1.0, in1=gl[3][:],
                                                               op0=ALU.add, op1=ALU.mult),
                      reads=["gl2", "gl3"], writes=["y5"])
            cx.dma("sp", st_slot if o % 2 == 0 else st_slot2,
                   [(io["y_tile"](o) if "y_tile" in io else y5_d[:, o, :], y5[:])], reads=["y5"])
        cx.close_scope()

    if do_ssd:
        cx.open_scope()
        tri = load("tri", [128, 128])
        ones = load("ones128", [128, 128])
        maskneg = load("maskneg", [128, 128])
        cw = load("conv_wT", [128, 16, 4])
        cb = load("conv_bT", [128, 16])
        dtb = load("dt_bias_bc", [128, 16])
        alog = load("a_log_bc", [128, 16])
        dsk = load("ssd_dT", [128, 8])
        onec = cx.sb("onec", [128, 1], F32)
        cx.op("dve", lambda e: e.memset(onec[:], 1.0), writes=["onec"])
        abc = cx.sb("abc", [128, 16], F32)
        cx.op("act", lambda e: e.activation(out=abc[:], in_=alog[:], func=AF.Exp), reads=["a_log_bc"], writes=["abc"])
        cx.op("dve", lambda e: e.tensor_scalar(out=abc[:], in0=abc[:], scalar1=-1.0, scalar2=None, op0=ALU.mult),
              reads=["abc"], writes=["abc"])
        NC_ = SEQ // 128
        dt_all = cx.sb("dt_all", [128, NC_, 16], F32)
        adt = cx.sb("adt", [128, NC_, 16], F32)
        cum = cx.sb("cum", [128, NC_, 16], F32)
        dte = cx.sb("dte", [128, NC_, 16], F32)
        dectot = cx.sb("dectot", [128, NC_, 16], F32)
        dg = [cx.sb(f"dg{i}", [128, 128], F32) for i in range(2)]
        tsm = [cx.sb(f"tsm{i}", [128, 16], F32) for i in range(2)]
        bdt = ws.add(w_d[:, 3584:3600].rearrange("(kt p) c -> p kt c", p=128), KT, 16)
        ws.need(bdt)
        wdt = ws.view(bdt)
        b2 = banks[2]
        for c in range(NC_):
            for kt in range(KT):
                cx.op("pe", lambda e, kt=kt: e.matmul(b2[:, 0:16], hT[:, kt, c * 128:(c + 1) * 128], wdt[:, kt, :],
                                                       start=(kt == 0), stop=(kt == KT - 1)),
                      reads=[ws.key(bdt), "hT"], writes=["bank2"])
            cx.op("dve", lambda e: e.tensor_tensor(out=tsm[0][:], in0=b2[:, 0:16], in1=dtb[:], op=ALU.add),
                  reads=["bank2", "dt_bias_bc"], writes=["tsm0"])
            cx.op("act", lambda e: e.activation(out=tsm[0][:], in_=tsm[0][:], func=AF.Exp), reads=["tsm0"], writes=["tsm0"])
            cx.op("act", lambda e: e.activation(out=dt_all[:, c, :], in_=tsm[0][:], func=AF.Ln, bias=onec[:], scale=1.0),
                  reads=["tsm0", "onec"], writes=["dt_all"])
            cx.op("dve", lambda e: e.tensor_tensor(out=adt[:, c, :], in0=dt_all[:, c, :], in1=abc[:], op=ALU.mult),
                  reads=["dt_all", "abc"], writes=["adt"])
            cx.op("pe", lambda e: e.matmul(banks[3][:, 0:16], tri[:], adt[:, c, :], start=True, stop=True),
                  reads=["tri", "adt"], writes=["bank3"])
            cx.op("pe", lambda e: e.matmul(banks[4][:, 0:16], ones[:], adt[:, c, :], start=True, stop=True),
                  reads=["ones128", "adt"], writes=["bank4"])
            cx.op("act", lambda e: e.activation(out=cum[:, c, :], in_=banks[3][:, 0:16], func=AF.Copy), reads=["bank3"], writes=["cum"])
            cx.op("act", lambda e: e.activation(out=dectot[:, c, :], in_=banks[4][:, 0:16], func=AF.Exp), reads=["bank4"], writes=["dectot"])
            cx.op("dve", lambda e: e.tensor_tensor(out=tsm[1][:], in0=banks[4][:, 0:16], in1=cum[:, c, :], op=ALU.subtract),
                  reads=["bank4", "cum"], writes=["tsm1"])
            cx.op("act", lambda e: e.activation(out=dte[:, c, :], in_=tsm[1][:], func=AF.Exp), reads=["tsm1"], writes=["dte"])

        raw = cx.sb("raw", [128, 4, 3 + SEQ], F32)
        cx.op("dve", lambda e: e.memset(raw[:, :, 0:3], 0.0), writes=["raw"])
        sz = cx.sb("sz", [128, 2, SEQ], BF16)
        cv = [cx.sb("cv0", [128, SEQ], F32)]
        xs32 = cx.sb("xs32", [128, 2, SEQ], F32)
        xsb = cx.sb("xsb", [128, 2, SEQ], BF16)
        BT = cx.sb("BT", [128, SEQ], BF16)
        CT = cx.sb("CT", [128, SEQ], BF16)
        yout = raw[:, 0:2, 3:3 + SEQ]
        car32 = cx.sb("car32", [128, 4, 64], F32)
        carb = cx.sb("carb", [128, 4, 64], BF16)
        Btok = [cx.sb(f"Btok{i}", [128, 128], BF16) for i in range(3)]
        CBs = [cx.sb(f"CBs{i}", [128, 128], F32) for i in range(3)]
        xq = [cx.sb(f"xq{i}", [128, 256], BF16) for i in range(3)]
        xqd = [cx.sb(f"xqd{i}", [128, 256], BF16) for i in range(3)]
        dm32 = [cx.sb(f"dm32{i}", [128, 128], F32) for i in range(2)]
        dmx = [cx.sb(f"dmx{i}", [128, 128], F32) for i in range(2)]
        MT = [[cx.sb(f"MT{i}{j}", [128, 128], BF16) for j in range(4)] for i in range(3)]
        ebc = [cx.sb(f"ebc{i}", [128, 128], F32) for i in range(2)]
        Cs = [[cx.sb(f"Cs{i}{j}", [128, 128], BF16) for j in range(4)] for i in range(3)]
        ytmp = [cx.sb(f"ytmp{i}", [128, 128], F32) for i in range(2)]
        ys_d = None if "y_tile" in io else dout("ysT", [128, 8, SEQ])
        b3, b4, b5 = banks[3], banks[4], banks[5]
        for gg in range(4):
            base = 512 + gg * 768
            bx = ws.add(w_d[:, base:base + 256].rearrange("(kt p) c -> p kt c", p=128), KT, 256)
            bbc = ws.add(w_d[:, base + 256:base + 512].rearrange("(kt p) c -> p kt c", p=128), KT, 256)
            bz = ws.add(w_d[:, base + 512:base + 768].rearrange("(kt p) c -> p kt c", p=128), KT, 256)

            def ev_raw(off):
                def f(ti, tb, ps, key):
                    cx.op("act", lambda e: e.activation(out=raw[:, off + ti, 3 + tb * 512:3 + (tb + 1) * 512], in_=ps[:], func=AF.Copy),
                          reads=[key], writes=["raw"])
                return f

            def ev_z(ti, tb, ps, key):
                cx.op("act", lambda e: e.activation(out=sz[:, ti, tb * 512:(tb + 1) * 512], in_=ps[:], func=AF.Silu),
                      reads=[key], writes=["sz"])
            ws.need(bx)
            proj(bx, 0, 2, ev_raw(0))
            ws.need(bbc)
            proj(bbc, 0, 2, ev_raw(2))
            ws.need(bz)
            proj(bz, 0, 2, ev_z)
            for ti in range(4):
                tidx = gg * 4 + ti
                cvt = cv[0]
                ck = "cv0"
                cx.op("dve", lambda e: e.tensor_scalar(out=cvt[:], in0=raw[:, ti, 0:SEQ], scalar1=cw[:, tidx, 0:1], scalar2=None,
                                                        op0=ALU.mult), reads=["raw", "conv_wT"], writes=[ck])
                for jj in range(1, 4):
                    cx.op("dve", lambda e, jj=jj: e.scalar_tensor_tensor(out=cvt[:], in0=raw[:, ti, jj:jj + SEQ],
                                                                       scalar=cw[:, tidx, jj:jj + 1], in1=cvt[:],
                                                                       op0=ALU.mult, op1=ALU.add),
                          reads=["raw", "conv_wT", ck], writes=[ck])
                if ti < 2:
                    cx.op("act", lambda e: e.activation(out=xs32[:, ti, :], in_=cvt[:], func=AF.Silu, bias=cb[:, tidx:tidx + 1], scale=1.0),
                          reads=[ck, "conv_bT"], writes=["xs32"])
                    cx.op("pool", lambda e: e.tensor_copy(out=xsb[:, ti, :], in_=xs32[:, ti, :]), reads=["xs32"], writes=["xsb"])
                else:
                    dst, dk = (BT, "BT") if ti == 2 else (CT, "CT")
                    cx.op("act", lambda e: e.activation(out=dst[:], in_=cvt[:], func=AF.Silu, bias=cb[:, tidx:tidx + 1], scale=1.0),
                          reads=[ck, "conv_bT"], writes=[dk])
            cx.op("dve", lambda e: e.memset(car32[:], 0.0), reads=["car32"], writes=["car32"])
            cx.op("dve", lambda e: e.memset(carb[:], 0.0), reads=["carb"], writes=["carb"])
            def partA1(c):
                cs_ = slice(c * 128, (c + 1) * 128)
                par = c % 3
                cx.op("pe", lambda e: e.transpose(bankT[:, 0:128], BT[:, cs_], identb[:]), reads=["BT", "identb"], writes=["bankT"])
                cx.op("pe", lambda e: e.transpose(bankT[:, 128:256], xsb[:, 0, cs_], identb[:]), reads=["xsb", "identb"], writes=["bankT"])
                cx.op("pe", lambda e: e.transpose(bankT[:, 256:384], xsb[:, 1, cs_], identb[:]), reads=["xsb", "identb"], writes=["bankT"])
                cx.op("act", lambda e: e.activation(out=Btok[par][:], in_=bankT[:, 0:128], func=AF.Copy),
                      reads=["bankT"], writes=[f"Btok{par}"])
                for hh in range(4):
                    h_ = gg * 4 + hh
                    cx.op("dve", lambda e: e.tensor_scalar(out=xq[par][:, hh * 64:(hh + 1) * 64], in0=bankT[:, 128 + hh * 64:192 + hh * 64],
                                                            scalar1=dt_all[:, c, h_:h_ + 1], scalar2=None, op0=ALU.mult),
                          reads=["bankT", "dt_all"], writes=[f"xq{par}"])
                    cx.op("pool", lambda e: e.tensor_scalar(out=xqd[par][:, hh * 64:(hh + 1) * 64], in0=xq[par][:, hh * 64:(hh + 1) * 64],
                                                             scalar1=dte[:, c, h_:h_ + 1], scalar2=None, op0=ALU.mult),
                          reads=[f"xq{par}", "dte"], writes=[f"xqd{par}"])
                cx.op("pe", lambda e: e.matmul(b3[:, 0:128], BT[:, cs_], CT[:, cs_], start=True, stop=True),
                      reads=["BT", "CT"], writes=["bank3"])
                cx.op("act", lambda e: e.activation(out=CBs[par][:], in_=b3[:, 0:128], func=AF.Copy), reads=["bank3"], writes=[f"CBs{par}"])

            def partA2(c):
                cs_ = slice(c * 128, (c + 1) * 128)
                par = c % 3
                for hh in range(4):
                    h_ = gg * 4 + hh
                    hp = hh % 2
                    crow = banks[hh % 2][:, 0:128]
                    cx.op("pool", lambda e: e.tensor_scalar(out=dg[hp][:], in0=ident[:], scalar1=cum[:, c, h_:h_ + 1], scalar2=None,
                                                             op0=ALU.mult), reads=["ident", "cum"], writes=[f"dg{hp}"])
                    cx.op("pe", lambda e: e.matmul(crow, ones[:], dg[hp][:], start=True, stop=True),
                          reads=["ones128", f"dg{hp}"], writes=[f"bank{hh % 2}"])
                    cx.op("dve", lambda e: e.scalar_tensor_tensor(out=dm32[hp][:], in0=crow, scalar=cum[:, c, h_:h_ + 1], in1=maskneg[:],
                                                                   op0=ALU.subtract, op1=ALU.add),
                          reads=[f"bank{hh % 2}", "cum", "maskneg"], writes=[f"dm32{hp}"])
                    cx.op("act", lambda e: e.activation(out=dmx[hp][:], in_=dm32[hp][:], func=AF.Exp), reads=[f"dm32{hp}"], writes=[f"dmx{hp}"])
                    cx.op("dve", lambda e: e.tensor_tensor(out=MT[par][hh][:], in0=dmx[hp][:], in1=CBs[par][:], op=ALU.mult),
                          reads=[f"dmx{hp}", f"CBs{par}"], writes=[f"MT{par}{hh}"])
                    cx.op("act", lambda e: e.activation(out=ebc[hp][:], in_=crow, func=AF.Exp), reads=[f"bank{hh % 2}"], writes=[f"ebc{hp}"])
                    cx.op("pool", lambda e: e.tensor_tensor(out=Cs[par][hh][:], in0=CT[:, cs_], in1=ebc[hp][:], op=ALU.mult),
                          reads=["CT", f"ebc{hp}"], writes=[f"Cs{par}{hh}"])

            def partB(c):
                cs_ = slice(c * 128, (c + 1) * 128)
                par = c % 3
                q2 = c % 2
                for hh in range(4):
                    pt, half = hh // 2, hh % 2
                    yo = banks[5 + q2][half * 64:(half + 1) * 64, pt * 128:(pt + 1) * 128]
                    cx.op("pe", lambda e: e.matmul(yo, xq[par][:, hh * 64:(hh + 1) * 64], MT[par][hh][:], start=True, stop=False),
                          reads=[f"xq{par}", f"MT{par}{hh}"], writes=[f"bank{5 + q2}"])
                    cx.op("pe", lambda e: e.matmul(yo, carb[:, hh, :], Cs[par][hh][:], start=False, stop=True),
                          reads=["carb", f"Cs{par}{hh}"], writes=[f"bank{5 + q2}"])
                cx.op("pe", lambda e: e.matmul(banks[2][:, 0:256], Btok[par][:], xqd[par][:], start=True, stop=True),
                      reads=[f"Btok{par}", f"xqd{par}"], writes=["bank2"])
                for hh in range(4):
                    h_ = gg * 4 + hh
                    cx.op("dve", lambda e: e.scalar_tensor_tensor(out=car32[:, hh, :], in0=car32[:, hh, :], scalar=dectot[:, c, h_:h_ + 1],
                                                                   in1=banks[2][:, hh * 64:(hh + 1) * 64], op0=ALU.mult, op1=ALU.add),
                          reads=["car32", "dectot", "bank2"], writes=["car32"])
                cx.op("pool", lambda e: e.tensor_copy(out=carb[:], in_=car32[:]), reads=["car32"], writes=["carb"])
                for pt in range(2):
                    cx.op("dve", lambda e: e.scalar_tensor_tensor(out=ytmp[pt][:], in0=xs32[:, pt, cs_], scalar=dsk[:, gg * 2 + pt:gg * 2 + pt + 1],
                                                                   in1=banks[5 + q2][:, pt * 128:(pt + 1) * 128],
                                                                   op0=ALU.mult, op1=ALU.add),
                          reads=["xs32", "ssd_dT", f"bank{5 + q2}"], writes=[f"ytmp{pt}"])
                    cx.op("pool", lambda e: e.tensor_tensor(out=yout[:, pt, cs_], in0=ytmp[pt][:], in1=sz[:, pt, cs_], op=ALU.mult),
                          reads=[f"ytmp{pt}", "sz"], writes=["raw"])

            for it_ in cx.record(partA1, 0):
                cx.play(it_)
            cx.play_interleaved(cx.record(partA2, 0), cx.record(partA1, 1))
            for c in range(NC_):
                la = cx.record(partB, c)
                lb = cx.record(partA2, c + 1) if c + 1 < NC_ else []
                lc = cx.record(partA1, c + 2) if c + 2 < NC_ else []
                cx.play_interleaved3(la, lb, lc)
            cx.dma("sp", st_slot if gg % 2 == 0 else st_slot2,
                   [((io["y_tile"](4 + gg * 2 + pt) if "y_tile" in io else ys_d[:, gg * 2 + pt, :]), yout[:, pt, :])
                    for pt in range(2)], reads=["raw"])
        cx.close_scope()


def prep_M0(inp, b, j, hT_full):
    m = {"hT": hT_full, "ident": np.eye(128, dtype=np.float32)}
    w = inp["hyb_w_in"][0]
    cols = [np.arange(j * 512, (j + 1) * 512)]
    for gg in range(4):
        G = j * 4 + gg
        cols.append(3072 + G * 256 + np.arange(256))
        cols.append(3072 + 2048 + G * 128 + np.arange(128))
        cols.append(3072 + 3072 + G * 128 + np.arange(128))
        cols.append(1024 + G * 256 + np.arange(256))
    cols.append(7168 + 16 * j + np.arange(16))
    cols = np.concatenate(cols)
    m["w_in_c"] = np.ascontiguousarray(w[:, cols])
    g0 = 32 * j
    lre = np.zeros((128, 16), np.float32)
    lim = np.zeros((128, 16), np.float32)
    ldt = np.zeros((128, 16), np.float32)
    xbre = np.zeros((128, 16, 128), np.float32)
    xbim = np.zeros((128, 16, 128), np.float32)
    cre = np.zeros((128, 16, 128), np.float32)
    cim = np.zeros((128, 16, 128), np.float32)
    for pr in range(16):
        pp = pr % 4
        for gi in range(2):
            g = g0 + 2 * pr + gi
            rows = slice(gi * 64, gi * 64 + 64)
            cs = slice(32 * pp + 16 * gi, 32 * pp + 16 * gi + 16)
            lre[rows, pr] = inp["s5_lambda_re"][0, g]
            lim[rows, pr] = inp["s5_lambda_im"][0, g]
            ldt[rows, pr] = inp["s5_log_dt"][0, g]
            xbre[rows, pr, cs] = inp["s5_b_re"][0, g]
            xbim[rows, pr, cs] = inp["s5_b_im"][0, g]
            cre[rows, pr, cs] = inp["s5_c_re"][0, g].T
            cim[rows, pr, cs] = inp["s5_c_im"][0, g].T
    m.update(s5_lre=lre, s5_lim=lim, s5_ldt=ldt, s5_xbre=xbre, s5_xbim=xbim, s5_cre=cre, s5_cim=cim)
    m["s5_dT"] = np.ascontiguousarray(inp["s5_d"][0, j * 512:(j + 1) * 512].reshape(4, 128).T)
    cwT = np.zeros((128, 16, 4), np.float32)
    cbT = np.zeros((128, 16), np.float32)
    dT = np.zeros((128, 8), np.float32)
    cwf, cbf = inp["ssd_conv_w"][0], inp["ssd_conv_b"][0]
    for gg in range(4):
        G = j * 4 + gg
        chans = [G * 256 + np.arange(128), G * 256 + 128 + np.arange(128),
                 2048 + G * 128 + np.arange(128), 3072 + G * 128 + np.arange(128)]
        for ti in range(4):
            cwT[:, gg * 4 + ti, :] = cwf[:, chans[ti]].T
            cbT[:, gg * 4 + ti] = cbf[chans[ti]]
        for pt in range(2):
            heads = (G * 256 + pt * 128 + np.arange(128)) // 64
            dT[:, gg * 2 + pt] = inp["ssd_d"][0][heads]
    hs = slice(16 * j, 16 * j + 16)
    m.update(conv_wT=cwT, conv_bT=cbT, ssd_dT=dT,
             dt_bias_bc=np.ascontiguousarray(np.broadcast_to(inp["ssd_dt_bias"][0, hs], (128, 16))),
             a_log_bc=np.ascontiguousarray(np.broadcast_to(inp["ssd_a_log"][0, hs], (128, 16))))
    tri = np.triu(np.ones((128, 128), np.float32))
    m["tri"] = tri
    m["ones128"] = np.ones((128, 128), np.float32)
    m["maskneg"] = np.where(np.arange(128)[None, :] >= np.arange(128)[:, None], 0.0, -30000.0).astype(np.float32)
    sel = np.zeros((16, 16, 128), np.float32)
    for h_ in range(16):
        sel[h_, h_, :] = 1.0
    m["sel"] = sel.reshape(16, 16 * 128)
    return m


RC = 64
LD_C = 0.6065306597126334
GN_EPS = 64e-5


def build_M1(G=None, io=None):
    nc, cx, dram, banks, bankT = _mk_env(G)
    io = io or {}
    cx.open_scope()

    def din(name, shape, dtype=F32):
        if name in io:
            return io[name]
        if name not in dram:
            dram[name] = nc.dram_tensor(name, list(shape), dtype, kind="ExternalInput").ap()
        return dram[name]

    def dout(name, shape, dtype=F32):
        if name in io:
            return io[name]
        dram[name] = nc.dram_tensor(name, list(shape), dtype, kind="ExternalOutput").ap()
        return dram[name]

    BF16S = "bf16_from_f32"

    def load(name, shape, dtype=F32, q="sp"):
        sdt, ddt = (BF16, F32) if dtype == BF16S else (dtype, dtype)
        t = cx.sb(name, shape, sdt)
        sl = cx.slot(name)
        pairs = cast_pairs(t[:], din(name, shape, ddt)) if dtype == BF16S else [(t[:], din(name, shape, ddt))]
        cx.dma(q, sl, pairs, writes=[name])
        return t

    hb = cx.sb("hbuf", [128, KT, SEQ + 1], BF16)
    cx.op("dve", lambda e: e.memset(hb[:, :, 0:1], 0.0), writes=["hb"])
    sl = cx.slot("hT")
    if "hT_pairs" in io:
        cx.dma("sp", sl, io["hT_pairs"](hb, 1), reads=["hb"] + io.get("dep", []), writes=["hb"])
    else:
        hT_d = din("hT", [128, KT, SEQ], BF16)
        cx.dma("sp", sl, [(hb[:, 4 * i:4 * i + 4, 1:SEQ + 1], hT_d[:, 4 * i:4 * i + 4, :]) for i in range(4)],
               reads=["hb"], writes=["hb"])
    ws = WStream(cx, nslot=2, elems=2048)
    ident = load("ident", [128, 128])
    identb = cx.sb("identb", [128, 128], BF16)
    cx.op("dve", lambda e: e.tensor_copy(out=identb[:], in_=ident[:]), reads=["ident"], writes=["identb"])
    mask3 = load("mask3", [128, 384])
    blockones = load("blockones", [128, 128])
    resetm = load("resetmask", [128, SEQ], BF16S, q="pool")
    muT = load("muT", [128, 6, KT])
    w0T = load("w0T", [128, 8])
    a0T = load("a0T", [128, 8])
    kkT = load("k_kT", [128, 8])
    kaT = load("k_aT", [128, 8])
    rkT = load("r_kT", [128, 8])
    lng = load("lng_stack", [128, 8, 64])
    lnb = load("lnb_stack", [128, 8, 64])
    w2c = load("w2c", [96, 1024], BF16S, q="pool")
    a2c = load("a2c", [96, 1024], BF16S, q="pool")
    g2c = load("g2c", [128, 2, 1024], BF16S, q="pool")
    onesb = cx.sb("onesb", [128, 2], BF16)
    cx.op("dve", lambda e: e.memset(onesb[:], 1.0), writes=["onesb"])
    epsg = cx.sb("epsg", [128, 1], F32)
    cx.op("dve", lambda e: e.memset(epsg[:], GN_EPS), writes=["epsg"])
    st_slots = [cx.slot("st0"), cx.slot("st1")]

    wder = [[cx.sb(f"wd{i}{j}", [128, KT, 128], BF16) for j in range(2)] for i in range(2)]
    nder = [0]

    def derive(blk, mu_i, ncol):
        i = nder[0] % 2
        nder[0] += 1
        wv = ws.view(blk)
        w1_, w2_ = wder[i][0], wder[i][1]
        for kt in range(KT):
            eng = "dve" if kt % 2 == 0 else "pool"
            cx.op(eng, lambda e, kt=kt: e.tensor_scalar(out=w2_[:, kt, 0:ncol], in0=wv[:, kt, :], scalar1=muT[:, mu_i, kt:kt + 1],
                                                        scalar2=None, op0=ALU.mult),
                  reads=[ws.key(blk), "muT"], writes=[f"wd{i}1"])
        cx.op("pool", lambda e: e.tensor_tensor(out=w1_[:, :, 0:ncol], in0=wv[:], in1=w2_[:, :, 0:ncol], op=ALU.subtract),
              reads=[ws.key(blk), f"wd{i}1"], writes=[f"wd{i}0"])
        return w1_, w2_, f"wd{i}0", f"wd{i}1"

    pj = [0]

    def proj2(der, ncol, evac):
        w1_, w2_, k1, k2 = der
        for tb in range(4):
            bk = pj[0] % 2
            pj[0] += 1
            for kt in range(KT):
                cx.op("pe", lambda e, kt=kt: e.matmul(banks[bk][0:ncol, :], w1_[:, kt, 0:ncol], hb[:, kt, 1 + tb * 512:1 + (tb + 1) * 512],
                                                       start=(kt == 0), stop=False),
                      reads=[k1, "hb"], writes=[f"bank{bk}"])
            for kt in range(KT):
                cx.op("pe", lambda e, kt=kt: e.matmul(banks[bk][0:ncol, :], w2_[:, kt, 0:ncol], hb[:, kt, tb * 512:(tb + 1) * 512],
                                                       start=False, stop=(kt == KT - 1)),
                      reads=[k2, "hb"], writes=[f"bank{bk}"])
            evac(tb, banks[bk], f"bank{bk}")

    def wblock(name, shape_cols, c0, ncol):
        src = din(name, [D, shape_cols])
        return ws.add(src[:, c0:c0 + ncol].rearrange("(kt p) c -> p kt c", p=128), KT, ncol)

    tw = cx.sb("tw", [96, SEQ], BF16)
    ta = cx.sb("ta", [96, SEQ], BF16)
    tg = cx.sb("tg", [128, 2, SEQ], BF16)
    b_w1 = wblock("w1", 96, 0, 96)
    b_a1 = wblock("a1", 96, 0, 96)
    b_g1 = [wblock("g1", 256, i * 128, 128) for i in range(2)]
    ws.need(b_w1)
    proj2(derive(b_w1, 1, 96), 96, lambda tb, ps, key: cx.op(
        "act", lambda e: e.activation(out=tw[:, tb * 512:(tb + 1) * 512], in_=ps[0:96, :], func=AF.Tanh), reads=[key], writes=["tw"]))
    ws.need(b_a1)
    proj2(derive(b_a1, 4, 96), 96, lambda tb, ps, key: cx.op(
        "act", lambda e: e.activation(out=ta[:, tb * 512:(tb + 1) * 512], in_=ps[0:96, :], func=AF.Copy), reads=[key], writes=["ta"]))
    for i in range(2):
        ws.need(b_g1[i])
        proj2(derive(b_g1[i], 5, 128), 128, lambda tb, ps, key, i=i: cx.op(
            "act", lambda e: e.activation(out=tg[:, i, tb * 512:(tb + 1) * 512], in_=ps[:], func=AF.Sigmoid), reads=[key], writes=["tg"]))

    r_bf = cx.sb("r_bf", [128, SEQ], BF16)
    k32 = cx.sb("k32", [128, SEQ], F32)
    v_bf = cx.sb("v_bf", [128, SEQ], BF16)
    a32 = cx.sb("a32", [128, SEQ], F32)
    kk32 = cx.sb("kk32", [128, SEQ], F32)
    ld32 = cx.sb("ld32", [128, SEQ], F32)
    cl32 = cx.sb("cl32", [128, SEQ], F32)
    ecl = cx.sb("ecl", [128, SEQ], F32)
    g_bf = cx.sb("g_bf", [128, SEQ], BF16)
    yg = cx.sb("yg", [128, SEQ], BF16)
    sqt = [cx.sb("sqt0", [128, 512], F32)] * 2
    Ear = [cx.sb(f"Ear{i}", [128, 256], BF16) for i in range(3)]
    Eb = [cx.sb(f"Eb{i}", [128, 128], BF16) for i in range(3)]
    Ek = [cx.sb(f"Ek{i}", [128, 128], BF16) for i in range(3)]
    Ez = [cx.sb(f"Ez{i}", [128, 128], BF16) for i in range(3)]
    for i in range(3):
        for t_, k_ in ((Ear[i], f"Ear{i}"), (Eb[i], f"Eb{i}"), (Ek[i], f"Ek{i}"), (Ez[i], f"Ez{i}")):
            cx.op("pool", lambda e, t_=t_: e.memset(t_[:], 0.0), writes=[k_])
    EbkT = [cx.sb(f"EbkT{i}", [128, 256], BF16) for i in range(3)]
    Pm = [[cx.sb(f"Pm{q}{i}", [128, 128], F32) for i in range(2)] for q in range(2)]
    PTm = [[cx.sb(f"PTm{q}{i}", [128, 128], F32) for i in range(2)] for q in range(2)]
    Rm = [cx.sb(f"Rm{q}", [128, 128], F32) for q in range(2)]
    Rb = [cx.sb(f"Rb{i}", [128, 128], BF16) for i in range(3)]
    Arb = [cx.sb(f"Arb{i}", [128, 128], BF16) for i in range(3)]
    Aak_rk = [cx.sb(f"Aakrk{i}", [128, 256], BF16) for i in range(3)]
    Vs = [cx.sb(f"Vs{i}", [128, 64], BF16) for i in range(3)]
    Xb = cx.sb("Xb", [128, 64], BF16)
    Ub = cx.sb("Ub", [128, 64], BF16)
    S32 = cx.sb("S32", [128, 64], F32)
    S0b = cx.sb("S0b", [128, 64], BF16)
    ys = cx.sb("ys", [128, 64], F32)
    ysq = cx.sb("ysq", [128, 64], F32)
    yn = cx.sb("yn", [128, 64], F32)
    yob = cx.sb("yob", [128, 64], BF16)
    stat = cx.sb("stat", [128, 8], F32)
    bon = cx.sb("bon", [128, 2], F32)
    yg_d = None if "yg_tile" in io else dout("ygT", [128, 8, SEQ], BF16)

    for P in range(8):
        c0 = P * 128
        b_r = wblock("wr_c", 1024, c0, 128)
        b_k = wblock("wk_c", 1024, c0, 128)
        b_v = wblock("wv_c", 1024, c0, 128)
        ws.need(b_r)
        proj2(derive(b_r, 0, 128), 128, lambda tb, ps, key: cx.op(
            "act", lambda e: e.activation(out=r_bf[:, tb * 512:(tb + 1) * 512], in_=ps[:], func=AF.Copy), reads=[key], writes=["r_bf"]))
        ws.need(b_k)
        proj2(derive(b_k, 2, 128), 128, lambda tb, ps, key: cx.op(
            "act", lambda e: e.activation(out=k32[:, tb * 512:(tb + 1) * 512], in_=ps[:], func=AF.Copy), reads=[key], writes=["k32"]))
        ws.need(b_v)
        proj2(derive(b_v, 3, 128), 128, lambda tb, ps, key: cx.op(
            "act", lambda e: e.activation(out=v_bf[:, tb * 512:(tb + 1) * 512], in_=ps[:], func=AF.Copy), reads=[key], writes=["v_bf"]))
        for tb in range(4):
            ts_ = slice(tb * 512, (tb + 1) * 512)
            bk = pj[0] % 2
            pj[0] += 1
            cx.op("pe", lambda e: e.matmul(banks[bk][:], w2c[:, c0:c0 + 128], tw[:, ts_], start=True, stop=True),
                  reads=["w2c", "tw"], writes=[f"bank{bk}"])
            cx.op("act", lambda e: e.activation(out=ld32[:, ts_], in_=banks[bk][:], func=AF.Sigmoid, bias=w0T[:, P:P + 1], scale=1.0),
                  reads=[f"bank{bk}", "w0T"], writes=["ld32"])
            bk = pj[0] % 2
            pj[0] += 1
            cx.op("pe", lambda e: e.matmul(banks[bk][:], a2c[:, c0:c0 + 128], ta[:, ts_], start=True, stop=True),
                  reads=["a2c", "ta"], writes=[f"bank{bk}"])
            cx.op("act", lambda e: e.activation(out=a32[:, ts_], in_=banks[bk][:], func=AF.Sigmoid, bias=a0T[:, P:P + 1], scale=1.0),
                  reads=[f"bank{bk}", "a0T"], writes=["a32"])
            bk = pj[0] % 2
            pj[0] += 1
            for i in range(2):
                cx.op("pe", lambda e, i=i: e.matmul(banks[bk][:], g2c[:, i, c0:c0 + 128], tg[:, i, ts_], start=(i == 0), stop=(i == 1)),
                      reads=["g2c", "tg"], writes=[f"bank{bk}"])
            cx.op("act", lambda e: e.activation(out=g_bf[:, ts_], in_=banks[bk][:], func=AF.Copy), reads=[f"bank{bk}"], writes=["g_bf"])
        cx.op("dve", lambda e: e.tensor_scalar(out=ld32[:], in0=ld32[:], scalar1=-LD_C, scalar2=None, op0=ALU.mult),
              reads=["ld32"], writes=["ld32"])
        cx.op("dve", lambda e: e.tensor_scalar(out=kk32[:], in0=k32[:], scalar1=kkT[:, P:P + 1], scalar2=None, op0=ALU.mult),
              reads=["k32", "k_kT"], writes=["kk32"])
        for tb in range(4):
            ts_ = slice(tb * 512, (tb + 1) * 512)
            sq_, sk = sqt[0], "sqt0"
            cx.op("act", lambda e: e.activation(out=sq_[:], in_=kk32[:, ts_], func=AF.Square), reads=["kk32"], writes=[sk])
            bk = pj[0] % 2
            pj[0] += 1
            cx.op("pe", lambda e: e.matmul(banks[bk][:], blockones[:], sq_[:], start=True, stop=True),
                  reads=["blockones", sk], writes=[f"bank{bk}"])
            cx.op("act", lambda e: e.activation(out=sq_[:], in_=banks[bk][:], func=AF.Sqrt), reads=[f"bank{bk}"], writes=[sk])
            cx.op("dve", lambda e: e.tensor_scalar(out=sq_[:], in0=sq_[:], scalar1=1e-12, scalar2=None, op0=ALU.max), reads=[sk], writes=[sk])
            cx.op("dve", lambda e: e.reciprocal(out=sq_[:], in_=sq_[:]), reads=[sk], writes=[sk])
            cx.op("dve", lambda e: e.tensor_tensor(out=kk32[:, ts_], in0=kk32[:, ts_], in1=sq_[:], op=ALU.mult),
                  reads=["kk32", sk], writes=["kk32"])
        cx.op("dve", lambda e: e.tensor_scalar(out=ecl[:], in0=a32[:], scalar1=-1.0, scalar2=kaT[:, P:P + 1], op0=ALU.add, op1=ALU.mult),
              reads=["a32", "k_aT"], writes=["ecl"])
        cx.op("dve", lambda e: e.scalar_tensor_tensor(out=k32[:], in0=ecl[:], scalar=1.0, in1=k32[:], op0=ALU.add, op1=ALU.mult),
              reads=["ecl", "k32"], writes=["k32"])
        cx.op("pool", lambda e: e.tensor_tensor(out=a32[:], in0=a32[:], in1=kk32[:], op=ALU.mult), reads=["a32", "kk32"], writes=["a32"])
        cx.op("dve", lambda e: e.tensor_tensor_scan(out=cl32[:], data0=resetm[:], data1=ld32[:], initial=0.0, op0=ALU.mult, op1=ALU.add),
              reads=["resetmask", "ld32"], writes=["cl32"])
        cx.op("pool", lambda e: e.tensor_tensor(out=ld32[:], in0=cl32[:], in1=ld32[:], op=ALU.subtract), reads=["cl32", "ld32"], writes=["ld32"])
        cx.op("act", lambda e: e.activation(out=ld32[:], in_=ld32[:], func=AF.Exp), reads=["ld32"], writes=["ld32"])
        cx.op("act", lambda e: e.activation(out=ecl[:], in_=cl32[:], func=AF.Exp), reads=["cl32", "ecl"], writes=["ecl"])
        cx.op("act", lambda e: e.activation(out=cl32[:], in_=cl32[:], func=AF.Exp, scale=-1.0), reads=["cl32"], writes=["cl32"])
        eclm, encl, beta, kfin = ld32, cl32, a32, k32
        cx.op("dve", lambda e: e.memset(S32[:], 0.0), reads=["S32"], writes=["S32"])
        cx.op("dve", lambda e: e.memset(S0b[:], 0.0), reads=["S0b"], writes=["S0b"])
        def part1a(c):
            cs = slice(c * RC, (c + 1) * RC)
            par = c % 3
            q2 = c % 2
            for hd in range(2):
                R_ = slice(hd * 64, hd * 64 + 64)
                e1, e2 = ("dve", "pool") if hd == 0 else ("pool", "dve")
                cx.op(e1, lambda e: e.scalar_tensor_tensor(out=Ear[par][R_, hd * 64:hd * 64 + 64], in0=kk32[R_, cs], scalar=-1.0, in1=eclm[R_, cs],
                                                           op0=ALU.mult, op1=ALU.mult) if e1 == "dve" else
                      e.tensor_tensor(out=Ear[par][R_, hd * 64:hd * 64 + 64], in0=kk32[R_, cs], in1=eclm[R_, cs], op=ALU.mult),
                      reads=["kk32", "ld32"], writes=[f"Ear{par}"])
                if e1 != "dve":
                    cx.op("pool", lambda e: e.tensor_scalar(out=Ear[par][R_, hd * 64:hd * 64 + 64], in0=Ear[par][R_, hd * 64:hd * 64 + 64],
                                                             scalar1=-1.0, scalar2=None, op0=ALU.mult),
                          reads=[f"Ear{par}"], writes=[f"Ear{par}"])
                cx.op(e2, lambda e: e.tensor_tensor(out=Ear[par][R_, 128 + hd * 64:128 + hd * 64 + 64], in0=r_bf[R_, cs], in1=ecl[R_, cs], op=ALU.mult),
                      reads=["r_bf", "ecl"], writes=[f"Ear{par}"])
                cx.op(e1, lambda e: e.tensor_tensor(out=Eb[par][R_, hd * 64:hd * 64 + 64], in0=beta[R_, cs], in1=encl[R_, cs], op=ALU.mult),
                      reads=["a32", "cl32"], writes=[f"Eb{par}"])
                cx.op(e2, lambda e: e.tensor_tensor(out=Ek[par][R_, hd * 64:hd * 64 + 64], in0=kfin[R_, cs], in1=encl[R_, cs], op=ALU.mult),
                      reads=["k32", "cl32"], writes=[f"Ek{par}"])
                cx.op("dve", lambda e: e.scalar_tensor_tensor(out=Ez[par][R_, hd * 64:hd * 64 + 64], in0=kfin[R_, cs], scalar=rkT[R_, P:P + 1],
                                                               in1=r_bf[R_, cs], op0=ALU.mult, op1=ALU.mult),
                      reads=["k32", "r_kT", "r_bf"], writes=[f"Ez{par}"])
                cx.op("pe", lambda e: e.transpose(bankT[R_, 0:64], v_bf[R_, cs], identb[R_, hd * 64:hd * 64 + 64]),
                      reads=["v_bf", "identb"], writes=["bankT"])
            cx.op("act", lambda e: e.activation(out=Vs[par][:], in_=bankT[:, 0:64], func=AF.Copy), reads=["bankT"], writes=[f"Vs{par}"])
            cx.op("pe", lambda e: e.matmul(banks[2][:, 0:256], Eb[par][:], Ear[par][:], start=True, stop=True),
                  reads=[f"Eb{par}", f"Ear{par}"], writes=["bank2"])
            cx.op("pe", lambda e: e.matmul(banks[3][:, 0:256], Ek[par][:], Ear[par][:], start=True, stop=True),
                  reads=[f"Ek{par}", f"Ear{par}"], writes=["bank3"])
            cx.op("pe", lambda e: e.matmul(banks[2][:, 256:384], Ear[par][:, 0:128], Eb[par][:], start=True, stop=True),
                  reads=[f"Eb{par}", f"Ear{par}"], writes=["bank2"])
            cx.op("pe", lambda e: e.transpose(bankT[:, 128:256], Eb[par][:], identb[:]), reads=[f"Eb{par}", "identb"], writes=["bankT"])
            cx.op("pe", lambda e: e.transpose(bankT[:, 256:384], Ek[par][:], identb[:]), reads=[f"Ek{par}", "identb"], writes=["bankT"])
            cx.op("dve", lambda e: e.tensor_tensor(out=Pm[q2][0][:], in0=banks[2][:, 0:128], in1=mask3[:, 0:128], op=ALU.mult),
                  reads=["bank2", "mask3"], writes=[f"Pm{q2}0"])
            cx.op("dve", lambda e: e.tensor_tensor(out=Arb[par][:], in0=banks[2][:, 128:256], in1=mask3[:, 128:256], op=ALU.mult),
                  reads=["bank2", "mask3"], writes=[f"Arb{par}"])
            cx.op("dve", lambda e: e.tensor_tensor(out=Aak_rk[par][:], in0=banks[3][:, 0:256], in1=mask3[:, 0:256], op=ALU.mult),
                  reads=["bank3", "mask3"], writes=[f"Aakrk{par}"])
            cx.op("dve", lambda e: e.tensor_tensor(out=PTm[q2][0][:], in0=banks[2][:, 256:384], in1=mask3[:, 256:384], op=ALU.mult),
                  reads=["bank2", "mask3"], writes=[f"PTm{q2}0"])
            cx.op("act", lambda e: e.activation(out=EbkT[par][:], in_=bankT[:, 128:384], func=AF.Copy), reads=["bankT"], writes=[f"EbkT{par}"])
            cx.op("pool", lambda e: e.tensor_tensor(out=Rm[q2][:], in0=Pm[q2][0][:], in1=ident[:], op=ALU.add), reads=[f"Pm{q2}0", "ident"], writes=[f"Rm{q2}"])
        def part1b(c):
            par = c % 3
            q2 = c % 2
            cur = 0
            for lvl in range(1, 6):
                nxt = 1 - cur
                if lvl < 5:
                    cx.op("pe", lambda e: e.matmul(banks[5][:, 0:128], PTm[q2][cur][:], Pm[q2][cur][:], start=True, stop=True),
                          reads=[f"PTm{q2}{cur}", f"Pm{q2}{cur}"], writes=["bank5"])
                cx.op("pe", lambda e: e.matmul(banks[6][:, 0:128], Pm[q2][cur][:], PTm[q2][cur][:], start=True, stop=True),
                      reads=[f"PTm{q2}{cur}", f"Pm{q2}{cur}"], writes=["bank6"])
                if lvl < 5:
                    cx.op("act", lambda e: e.activation(out=Pm[q2][nxt][:], in_=banks[5][:, 0:128], func=AF.Copy),
                          reads=["bank5"], writes=[f"Pm{q2}{nxt}"])
                cx.op("dve", lambda e: e.tensor_copy(out=PTm[q2][nxt][:], in_=banks[6][:, 0:128]), reads=["bank6"], writes=[f"PTm{q2}{nxt}"])
                cx.op("pe", lambda e: e.matmul(banks[4][:, 0:128], PTm[q2][nxt][:], Rm[q2][:], start=True, stop=True),
                      reads=[f"PTm{q2}{nxt}", f"Rm{q2}"], writes=["bank4"])
                cx.op("dve", lambda e: e.tensor_tensor(out=Rm[q2][:], in0=Rm[q2][:], in1=banks[4][:, 0:128], op=ALU.add),
                      reads=[f"Rm{q2}", "bank4"], writes=[f"Rm{q2}"])
                cur = nxt
            cx.op("act", lambda e: e.activation(out=Rb[par][:], in_=Rm[q2][:], func=AF.Copy), reads=[f"Rm{q2}"], writes=[f"Rb{par}"])

        def part2(c):
            cs = slice(c * RC, (c + 1) * RC)
            par = c % 3
            cx.op("pe", lambda e: e.matmul(banks[0][:, 0:64], Ear[par][:, 0:128], S0b[:], start=True, stop=False),
                  reads=[f"Ear{par}", "S0b"], writes=["bank0"])
            cx.op("pe", lambda e: e.matmul(banks[0][:, 0:64], Aak_rk[par][:, 0:128], Vs[par][:], start=False, stop=True),
                  reads=[f"Aakrk{par}", f"Vs{par}"], writes=["bank0"])
            cx.op("act", lambda e: e.activation(out=Xb[:], in_=banks[0][:, 0:64], func=AF.Copy), reads=["bank0"], writes=["Xb"])
            cx.op("pe", lambda e: e.matmul(banks[1][:, 0:64], Rb[par][:], Xb[:], start=True, stop=True), reads=[f"Rb{par}", "Xb"], writes=["bank1"])
            cx.op("act", lambda e: e.activation(out=Ub[:], in_=banks[1][:, 0:64], func=AF.Copy), reads=["bank1"], writes=["Ub"])
            cx.op("pe", lambda e: e.matmul(banks[0][:, 0:64], Ear[par][:, 128:256], S0b[:], start=True, stop=False),
                  reads=[f"Ear{par}", "S0b"], writes=["bank0"])
            cx.op("pe", lambda e: e.matmul(banks[0][:, 0:64], Arb[par][:], Ub[:], start=False, stop=False), reads=[f"Arb{par}", "Ub"], writes=["bank0"])
            cx.op("pe", lambda e: e.matmul(banks[0][:, 0:64], Aak_rk[par][:, 128:256], Vs[par][:], start=False, stop=True),
                  reads=[f"Aakrk{par}", f"Vs{par}"], writes=["bank0"])
            cx.op("pe", lambda e: e.matmul(banks[0][:, 64:66], Ez[par][:], onesb[:], start=True, stop=True),
                  reads=[f"Ez{par}", "onesb"], writes=["bank0"])
            cx.op("pe", lambda e: e.matmul(banks[1][:, 0:64], EbkT[par][:, 0:128], Ub[:], start=True, stop=False), reads=[f"EbkT{par}", "Ub"], writes=["bank1"])
            cx.op("pe", lambda e: e.matmul(banks[1][:, 0:64], EbkT[par][:, 128:256], Vs[par][:], start=False, stop=True), reads=[f"EbkT{par}", f"Vs{par}"], writes=["bank1"])
            cx.op("dve", lambda e: e.tensor_tensor(out=S32[:], in0=S32[:], in1=banks[1][:, 0:64], op=ALU.add), reads=["S32", "bank1"], writes=["S32"])
            wc = ecl[:, c * RC + RC - 1:c * RC + RC]
            cx.op("dve", lambda e: e.tensor_scalar(out=S32[:], in0=S32[:], scalar1=wc, scalar2=None, op0=ALU.mult),
                  reads=["S32", "ecl"], writes=["S32"])
            cx.op("pool", lambda e: e.tensor_copy(out=S0b[:], in_=S32[:]), reads=["S32"], writes=["S0b"])
            cx.op("act", lambda e: e.activation(out=ys[:], in_=banks[0][:, 0:64], func=AF.Copy, accum_out=stat[:, 0:1]),
                  reads=["bank0"], writes=["ys", "stat"])
            cx.op("act", lambda e: e.activation(out=ysq[:], in_=ys[:], func=AF.Square, accum_out=stat[:, 1:2]),
                  reads=["ys"], writes=["ysq", "stat"])
            cx.op("dve", lambda e: e.tensor_scalar(out=stat[:, 2:3], in0=stat[:, 0:1], scalar1=1.0 / 64, scalar2=None, op0=ALU.mult),
                  reads=["stat"], writes=["stat"])
            cx.op("dve", lambda e: e.tensor_tensor(out=stat[:, 3:4], in0=stat[:, 2:3], in1=stat[:, 2:3], op=ALU.mult),
                  reads=["stat"], writes=["stat"])
            cx.op("dve", lambda e: e.scalar_tensor_tensor(out=stat[:, 4:5], in0=stat[:, 1:2], scalar=1.0 / 64, in1=stat[:, 3:4],
                                                           op0=ALU.mult, op1=ALU.subtract), reads=["stat"], writes=["stat"])
            cx.op("act", lambda e: e.activation(out=stat[:, 5:6], in_=stat[:, 4:5], func=AF.Sqrt, bias=epsg[:], scale=1.0),
                  reads=["stat", "epsg"], writes=["stat"])
            cx.op("dve", lambda e: e.reciprocal(out=stat[:, 5:6], in_=stat[:, 5:6]), reads=["stat"], writes=["stat"])
            cx.op("dve", lambda e: e.tensor_scalar(out=yn[:], in0=ys[:], scalar1=stat[:, 2:3], scalar2=stat[:, 5:6],
                                                    op0=ALU.subtract, op1=ALU.mult), reads=["ys", "stat"], writes=["yn"])
            cx.op("pool", lambda e: e.tensor_tensor(out=yn[:], in0=yn[:], in1=lng[:, P, :], op=ALU.mult), reads=["yn", "lng_stack"], writes=["yn"])
            cx.op("pool", lambda e: e.tensor_tensor(out=yn[:], in0=yn[:], in1=lnb[:, P, :], op=ALU.add), reads=["yn", "lnb_stack"], writes=["yn"])
            cx.op("act", lambda e: e.activation(out=bon[:], in_=banks[0][:, 64:66], func=AF.Copy), reads=["bank0"], writes=["bon"])
            cx.op("dve", lambda e: e.scalar_tensor_tensor(out=yob[:], in0=Vs[par][:], scalar=bon[:, 0:1], in1=yn[:], op0=ALU.mult, op1=ALU.add),
                  reads=[f"Vs{par}", "bon", "yn"], writes=["yob"])
            for hd in range(2):
                R_ = slice(hd * 64, hd * 64 + 64)
                cx.op("pe", lambda e: e.transpose(bankT[R_, 512:576], yob[R_, :], identb[R_, hd * 64:hd * 64 + 64]),
                      reads=["yob", "identb"], writes=["bankT"])
            cx.op("dve", lambda e: e.tensor_tensor(out=yg[:, cs], in0=bankT[:, 512:576], in1=g_bf[:, cs], op=ALU.mult),
                  reads=["bankT", "g_bf"], writes=["yg"])
        NCH_ = SEQ // RC
        for it in cx.record(part1a, 0):
            cx.play(it)
        cx.play_interleaved(cx.record(part1b, 0), cx.record(part1a, 1))
        for c in range(NCH_):
            la = cx.record(part2, c)
            lb = cx.record(part1b, c + 1) if c + 1 < NCH_ else []
            lc = cx.record(part1a, c + 2) if c + 2 < NCH_ else []
            cx.play_interleaved3(la, lb, lc)
        cx.dma("sp", st_slots[P % 2], [(io["yg_tile"](P) if "yg_tile" in io else yg_d[:, P, :], yg[:])], reads=["yg"])
    cx.close_scope()
    cx.wait_all("sp")
    return nc


def prep_M1(inp, b, j, hT_full):
    m = {"hT": hT_full, "ident": np.eye(128, dtype=np.float32)}
    cs = slice(j * 1024, (j + 1) * 1024)
    m["wr_c"] = np.ascontiguousarray(inp["rwkv_w_r"][0][:, cs])
    m["wk_c"] = np.ascontiguousarray(inp["rwkv_w_k"][0][:, cs])
    m["wv_c"] = np.ascontiguousarray(inp["rwkv_w_v"][0][:, cs])
    m["w1"] = inp["rwkv_w1"][0]
    m["a1"] = inp["rwkv_a1"][0]
    m["g1"] = inp["rwkv_g1"][0]
    m["w2c"] = np.ascontiguousarray(inp["rwkv_w2"][0][:, cs])
    m["a2c"] = np.ascontiguousarray(inp["rwkv_a2"][0][:, cs])
    m["g2c"] = np.ascontiguousarray(inp["rwkv_g2"][0][:, cs].reshape(2, 128, 1024).transpose(1, 0, 2))
    m["muT"] = np.ascontiguousarray(inp["rwkv_mu"][0].reshape(6, KT, 128).transpose(2, 0, 1))
    for nm, key in (("w0T", "rwkv_w0"), ("a0T", "rwkv_a0"), ("k_kT", "rwkv_k_k"), ("k_aT", "rwkv_k_a")):
        m[nm] = np.ascontiguousarray(inp[key][0][cs].reshape(8, 128).T)
    m["r_kT"] = np.ascontiguousarray(inp["rwkv_r_k"][0].reshape(-1)[cs].reshape(8, 128).T)
    lg = inp["rwkv_ln_g"][0][cs].reshape(8, 2, 64)
    lb = inp["rwkv_ln_b"][0][cs].reshape(8, 2, 64)
    lng = np.zeros((128, 8, 64), np.float32)
    lnb = np.zeros((128, 8, 64), np.float32)
    for hd in range(2):
        lng[hd * 64:(hd + 1) * 64] = lg[None, :, hd, :]
        lnb[hd * 64:(hd + 1) * 64] = lb[None, :, hd, :]
    m["lng_stack"], m["lnb_stack"] = lng, lnb
    s_ = np.arange(64)
    blk = np.kron(np.eye(2, dtype=np.float32), np.ones((64, 64), np.float32))
    mS = np.kron(np.eye(2, dtype=np.float32), (s_[:, None] < s_[None, :]).astype(np.float32))
    mI = np.kron(np.eye(2, dtype=np.float32), (s_[:, None] <= s_[None, :]).astype(np.float32))
    m["mask3"] = np.ascontiguousarray(np.concatenate([mS, mI, mS.T], axis=1))
    m["blockones"] = blk
    rm = np.ones((128, SEQ), np.float32)
    rm[:, ::RC] = 0.0
    m["resetmask"] = rm
    return m


def _T_maps(inp, stages, xT, extra):
    maps = []
    for core in range(NCORES):
        b = core // 2
        m = {"xT": xT[core], "cT": fm(inp["c"][b])}
        for stg in stages:
            kind = stg["kind"]
            if kind == "ffn":
                l, s_ = stg["l"], stg["s"]
                fi = 0 if s_ == 0 else 1
                m[f"w_mod{l}"] = inp["w_mod"][l]
                m[f"b_modT{l}"] = fm(inp["b_mod"][l])
                m[f"norm_gT{l}{s_}"] = fm(inp["norm_g"][l, s_])
                m[f"ffn_w1_{l}{fi}"] = inp["ffn_w1"][l, fi]
                m[f"ffn_w3_{l}{fi}"] = inp["ffn_w3"][l, fi]
                m[f"ffn_w2_{l}{fi}"] = inp["ffn_w2"][l, fi]
            elif kind == "h_out":
                l = stg["l"]
                m[f"w_mod{l}"] = inp["w_mod"][l]
                m[f"b_modT{l}"] = fm(inp["b_mod"][l])
                m[f"norm_gT{l}1"] = fm(inp["norm_g"][l, 1])
            elif kind == "mix0_post":
                m["w_mod0"] = inp["w_mod"][0]
                m["b_modT0"] = fm(inp["b_mod"][0])
                m["glu_bT"] = fm(inp["s5_glu_b"][0])
                m["ssd_norm_gT"] = fm(inp["ssd_norm_g"][0])
                m["s5_glu_w"] = inp["s5_glu_w"][0]
                m["hyb_w_out"] = inp["hyb_w_out"][0]
            elif kind == "rwkv_post":
                m["w_mod1"] = inp["w_mod"][1]
                m["b_modT1"] = fm(inp["b_mod"][1])
                m["rwkv_w_o"] = inp["rwkv_w_o"][0]
            elif kind == "final":
                m["final_gT"] = fm(inp["final_g"])
        m.update(extra[core])
        maps.append(m)
    return maps


def _run(nc, maps):
    return run_bass_kernel_spmd(nc, maps, core_ids=list(range(NCORES))).results


def _only_declared(maps):
    keep = set(LAST_DRAM.keys())
    return [{k: v for k, v in m.items() if k in keep} for m in maps]


def _pair_cat_tokens(tiles, b):
    return np.ascontiguousarray(np.concatenate([tiles[2 * b], tiles[2 * b + 1]], axis=2))


GROUPS = [[0, 1], [2, 3], [4, 5], [6, 7]]
ST0 = [{"kind": "ffn", "l": 0, "s": 0}, {"kind": "h_out", "l": 0}, {"kind": "x_out"}]
ST1 = [{"kind": "mix0_post"}, {"kind": "ffn", "l": 0, "s": 2}, {"kind": "ffn", "l": 1, "s": 0},
       {"kind": "h_out", "l": 1}, {"kind": "x_out"}]
ST2 = [{"kind": "rwkv_post"}, {"kind": "ffn", "l": 1, "s": 2}, {"kind": "final"}]


def build_fused(upto=None):
    nc = bass.Bass("TRN2", target_bir_lowering=False)
    cx = Ctx(nc)
    banks = [cx.ps(f"bank{i}") for i in range(7)]
    cx.uid += 1
    bankT = nc.alloc_psum_tensor(f"bankT_{cx.uid}", [128, 1024], BF16)
    G = {"nc": nc, "cx": cx, "dram": {}, "banks": banks, "bankT": bankT}

    def idram(name, shape, dt):
        return nc.dram_tensor(name, list(shape), dt).ap()

    ncc = [0]

    def allgather(src, dst):
        sl = cx.slot("cc")
        nc.gpsimd.collective_compute("AllGather", ALU.bypass, replica_groups=GROUPS,
                                     ins=[src.opt()], outs=[dst.opt()]).then_inc(sl["sem"])
        sl["count"] += 1
        ncc[0] += 1
        cx._record((sl["sem"], sl["count"], sl["key"]), [], [f"cc{ncc[0]}"])
        return f"cc{ncc[0]}"

    CH = 4096

    def chunks(name, ncol, dt):
        n = ncol // CH
        return ([idram(f"{name}_s{i}", [128, CH], dt) for i in range(n)],
                [idram(f"{name}_g{i}", [256, CH], dt) for i in range(n)])

    def gather_all(snd, rcv):
        return [allgather(a, b) for a, b in zip(snd, rcv)]

    def h_out_pairs(snd):
        return lambda h: [(snd[c].rearrange("p (k t) -> p k t", k=4), h[:, 4 * c:4 * c + 4, :]) for c in range(4)]

    def h_loader(rcv):
        def f(tile, off):
            pairs = []
            for r in range(2):
                for c in range(4):
                    src = rcv[c][r * 128:(r + 1) * 128, :].rearrange("p (k t) -> p k t", k=4)
                    pairs.append((tile[:, 4 * c:4 * c + 4, off + r * TOK:off + (r + 1) * TOK], src))
            return pairs
        return f

    def tile_fn(snd):
        return lambda idx: snd[idx // 2][:, (idx % 2) * SEQ:(idx % 2 + 1) * SEQ]

    def gath_fn(rcv):
        return lambda r, lt, half: rcv[lt // 2][r * 128:(r + 1) * 128, (lt % 2) * SEQ + half * TOK:(lt % 2) * SEQ + (half + 1) * TOK]

    xs1 = idram("xs1", [128, KT, TOK], F32)
    xs2 = idram("xs2", [128, KT, TOK], F32)
    h0s, h0g = chunks("h0", KT * TOK, BF16)
    h1s, h1g = chunks("h1", KT * TOK, BF16)
    y0s, y0g = chunks("y0", 12 * SEQ, F32)
    y1s, y1g = chunks("y1", 8 * SEQ, BF16)

    def dbg(n, src, shape, dt):
        if upto != n:
            return False
        cx.barrier()
        o = nc.dram_tensor("dbg", list(shape), dt, kind="ExternalOutput").ap()
        cx.dma("sp", cx.slot("dbg"), [(o, src)])
        cx.wait_all("sp")
        return True

    global LAST_DRAM
    LAST_DRAM = G["dram"]
    build_T(ST0, G, io={"hT_out_pairs": h_out_pairs(h0s), "xT_out": xs1})
    if dbg(1, xs1, [128, KT, TOK], F32):
        return nc
    cx.new_phase()
    dep = gather_all(h0s, h0g)
    if dbg(2, h0g[3], [256, CH], BF16):
        return nc
    build_M0(G=G, io={"hT_pairs": h_loader(h0g), "y_tile": tile_fn(y0s), "dep": dep})
    if dbg(3, y0s[0], [128, CH], F32):
        return nc
    cx.new_phase()
    dep = gather_all(y0s, y0g)
    if dbg(4, y0g[5], [256, CH], F32):
        return nc
    build_T(ST1, G, io={"xT": xs1, "y_gath": gath_fn(y0g), "hT_out_pairs": h_out_pairs(h1s), "xT_out": xs2, "dep": dep})
    if dbg(5, xs2, [128, KT, TOK], F32):
        return nc
    cx.new_phase()
    dep = gather_all(h1s, h1g)
    build_M1(G=G, io={"hT_pairs": h_loader(h1g), "yg_tile": tile_fn(y1s), "dep": dep})
    if dbg(6, y1s[0], [128, CH], BF16):
        return nc
    cx.new_phase()
    dep = gather_all(y1s, y1g)
    build_T(ST2, G, io={"xT": xs2, "yg_gath": gath_fn(y1g), "dep": dep})
    cx.wait_all("sp")
    return nc


LAST_DRAM = {}


def kernel_unfused(**inputs):
    inp = {k: np.asarray(v) for k, v in inputs.items()}
    xT = to_xT(inp["x"].astype(np.float32, copy=False))
    st0 = [{"kind": "ffn", "l": 0, "s": 0}, {"kind": "h_out", "l": 0}, {"kind": "x_out"}]
    r = _run(build_T(st0), _T_maps(inp, st0, xT, [{}] * NCORES))
    xT = [np.asarray(q["xT_out"]) for q in r]
    hT = [np.asarray(q["hT_out"]) for q in r]
    maps = [prep_M0(inp, c // 2, c % 2, _pair_cat_tokens(hT, c // 2)) for c in range(NCORES)]
    r = _run(build_M0(), maps)
    y5 = [np.asarray(q["y5T"]) for q in r]
    ys = [np.asarray(q["ysT"]) for q in r]
    extra = []
    for c in range(NCORES):
        b, jt = c // 2, c % 2
        ts = slice(jt * TOK, (jt + 1) * TOK)
        extra.append({"y5T_in": np.ascontiguousarray(np.concatenate([y5[2 * b][:, :, ts], y5[2 * b + 1][:, :, ts]], axis=1)),
                      "ysT_in": np.ascontiguousarray(np.concatenate([ys[2 * b][:, :, ts], ys[2 * b + 1][:, :, ts]], axis=1))})
    st1 = [{"kind": "mix0_post"}, {"kind": "ffn", "l": 0, "s": 2}, {"kind": "ffn", "l": 1, "s": 0},
           {"kind": "h_out", "l": 1}, {"kind": "x_out"}]
    r = _run(build_T(st1), _T_maps(inp, st1, xT, extra))
    xT = [np.asarray(q["xT_out"]) for q in r]
    hT = [np.asarray(q["hT_out"]) for q in r]
    maps = [prep_M1(inp, c // 2, c % 2, _pair_cat_tokens(hT, c // 2)) for c in range(NCORES)]
    r = _run(build_M1(), maps)
    yg = [np.asarray(q["ygT"]) for q in r]
    extra = []
    for c in range(NCORES):
        b, jt = c // 2, c % 2
        ts = slice(jt * TOK, (jt + 1) * TOK)
        extra.append({"ygT_in": np.ascontiguousarray(np.concatenate([yg[2 * b][:, :, ts], yg[2 * b + 1][:, :, ts]], axis=1))})
    st2 = [{"kind": "rwkv_post"}, {"kind": "ffn", "l": 1, "s": 2}, {"kind": "final"}]
    r = _run(build_T(st2), _T_maps(inp, st2, xT, extra))
    out = from_xT([np.asarray(q["xT_out"]) for q in r])
    return out.astype(np.float32)


def fused_maps(inp):
    xT = to_xT(inp["x"].astype(np.float32, copy=False))
    maps = []
    for c in range(NCORES):
        b, j = c // 2, c % 2
        m = {}
        for st in (ST0, ST1, ST2):
            m.update(_T_maps(inp, st, xT, [{}] * NCORES)[c])
        m0 = prep_M0(inp, b, j, None)
        m1 = prep_M1(inp, b, j, None)
        m0.pop("hT")
        m1.pop("hT")
        m.update(m0)
        m.update(m1)
        sel = np.zeros((128, 2), np.float32)
        sel[:, j] = 1.0
        m["selT"] = sel
        maps.append(m)
    return maps


def kernel(**inputs):
    inp = {k: np.asarray(v) for k, v in inputs.items()}
    nc = build_fused()
    r = _run(nc, _only_declared(fused_maps(inp)))
    out = from_xT([np.asarray(q["xT_out"]) for q in r])
    return out.astype(np.float32)
```

```python
import numpy as np
import concourse.bass as bass
import concourse.mybir as mybir
from concourse.bass_utils import run_bass_kernel_spmd

F32 = mybir.dt.float32
BF16 = mybir.dt.bfloat16
AF = mybir.ActivationFunctionType
ALU = mybir.AluOpType

D = 2048
KT = 16
FFN = 5632
NCORES = 8
TOK = 1024
SEQ = 2048
EPS = 1e-6


SKIP_SELF = {"pe"}


class _Eng:
    def __init__(self, name, obj, sem):
        self.name, self.obj, self.sem = name, obj, sem
        self.count = 0
        self.seen = {}


class _Rec:
    def __getattr__(self, name):
        def f(*a, **k):
            self.call = (name, a, k)
            return self
        return f


class Ctx:
    def record(self, fn, *args):
        self.rec = []
        fn(*args)
        lst, self.rec = self.rec, None
        return lst

    def play(self, item):
        engname, (name, a, k), reads, writes = item
        return self.op(engname, lambda e: getattr(e, name)(*a, **k), reads, writes)

    def play_interleaved(self, la, lb):
        i = j = 0
        na, nb = len(la), len(lb)
        while i < na or j < nb:
            if i < na and (j >= nb or i * nb <= j * na):
                self.play(la[i])
                i += 1
            else:
                self.play(lb[j])
                j += 1

    def play_interleaved3(self, la, lb, lc):
        lists = [l for l in (la, lb, lc) if l]
        pos = [0] * len(lists)
        while any(p < len(l) for p, l in zip(pos, lists)):
            k = min((i for i in range(len(lists)) if pos[i] < len(lists[i])), key=lambda i: pos[i] / len(lists[i]))
            self.play(lists[k][pos[k]])
            pos[k] += 1

    def __init__(self, nc):
        self.nc = nc
        self.engs = {}
        for name, attr in (("pe", "tensor"), ("act", "scalar"), ("dve", "vector"),
                           ("pool", "gpsimd"), ("sp", "sync")):
            self.engs[name] = _Eng(name, getattr(nc, attr), nc.alloc_semaphore("sem_" + name))
        self.res = {}
        self.nslots = 0
        self.uid = 0
        self.stacks = []
        self.free_slots = []
        self.phase = 0
        self.rec = None

    def sb(self, name, shape, dtype=F32):
        self.uid += 1
        if self.stacks:
            return self.stacks[-1][0].enter_context(self.nc.sbuf_tensor(f"{name}_{self.uid}", list(shape), dtype))
        return self.nc.alloc_sbuf_tensor(f"{name}_{self.uid}", list(shape), dtype)

    def open_scope(self):
        import contextlib
        self.stacks.append((contextlib.ExitStack(), []))

    def close_scope(self):
        self.barrier()
        st, slots = self.stacks.pop()
        st.close()
        self.free_slots.extend(slots)

    def ps(self, name):
        self.uid += 1
        return self.nc.alloc_psum_tensor(f"{name}_{self.uid}", [128, 512], F32)

    def slot(self, name):
        if self.free_slots:
            sl = self.free_slots.pop()
        else:
            self.nslots += 1
            sl = {"sem": self.nc.alloc_semaphore(f"dsem_{name}_{self.nslots}"), "count": 0,
                  "key": f"slot{self.nslots}"}
        if self.stacks:
            self.stacks[-1][1].append(sl)
        return sl

    def _deps(self, reads, writes):
        deps = []
        for r in reads:
            st = self.res.get(r)
            if st and st["w"]:
                deps.append(st["w"])
            if st and r.startswith("bank"):
                deps.extend(st["r"].values())
        for w in writes:
            st = self.res.get(w)
            if st:
                if st["w"]:
                    deps.append(st["w"])
                deps.extend(st["r"].values())
        return deps

    def _wait(self, eng, deps, skip_self):
        for sem, val, key in deps:
            if skip_self and key == eng.name:
                continue
            if eng.seen.get(key, 0) < val:
                eng.obj.wait_ge(sem, val)
                eng.seen[key] = val

    def _record(self, tok, reads, writes):
        for r in reads:
            st = self.res.setdefault(r, {"w": None, "r": {}})
            st["r"][tok[2]] = tok
        for w in writes:
            self.res[w] = {"w": tok, "r": {}}

    def op(self, engname, emit, reads=(), writes=()):
        if self.rec is not None:
            r = _Rec()
            emit(r)
            self.rec.append((engname, r.call, tuple(reads), tuple(writes)))
            return None
        eng = self.engs[engname]
        self._wait(eng, self._deps(reads, writes), skip_self=(engname in SKIP_SELF))
        inst = emit(eng.obj)
        eng.count += 1
        inst.then_inc(eng.sem, 1)
        tok = (eng.sem, eng.count, engname)
        eng.seen[engname] = max(eng.seen.get(engname, 0), 0)
        self._record(tok, reads, writes)
        return tok

    def dma(self, qname, slot, pairs, reads=(), writes=(), **kw):
        eng = self.engs[qname]
        self._wait(eng, self._deps(reads, writes), skip_self=False)
        for out, in_ in pairs:
            eng.obj.dma_start(out=out, in_=in_, **kw).then_inc(slot["sem"], 16)
            slot["count"] += 16
        tok = (slot["sem"], slot["count"], slot["key"])
        self._record(tok, reads, writes)
        return tok

    def wait_all(self, engname):
        eng = self.engs[engname]
        deps = []
        for e in self.engs.values():
            if e.count:
                deps.append((e.sem, e.count, e.name))
        for st in self.res.values():
            if st["w"]:
                deps.append(st["w"])
            deps.extend(st["r"].values())
        self._wait(eng, deps, skip_self=False)

    def barrier(self):
        for n in self.engs:
            self.wait_all(n)

    def new_phase(self):
        self.barrier()
        self.phase += 1
        for e in self.engs.values():
            e.sem = self.nc.alloc_semaphore(f"sem_{e.name}_p{self.phase}")
            e.count = 0
            e.seen = {}
        self.res = {}


class WStream:
    NSLOT = 4
    ELEMS = 8192

    def __init__(self, cx, nslot=4, elems=8192):
        self.cx = cx
        self.NSLOT, self.ELEMS = nslot, elems
        self.tiles = [cx.sb(f"wslot{i}", [128, self.ELEMS], BF16) for i in range(self.NSLOT)]
        self.slots = [cx.slot(f"w{i}") for i in range(self.NSLOT)]
        self.plan = []
        self.issued = 0

    def add(self, src_ap, a, b):
        assert a * b <= self.ELEMS
        self.plan.append((src_ap, a, b))
        return len(self.plan) - 1

    def view(self, i):
        _, a, b = self.plan[i]
        t = self.tiles[i % self.NSLOT]
        return t[:, 0:a * b].rearrange("p (a b) -> p a b", a=a)

    def key(self, i):
        return f"wslot{i % self.NSLOT}"

    def _issue(self, i):
        src, a, b = self.plan[i]
        v = self.view(i)
        pairs = [(v[:, :, c0:min(b, c0 + 1024)], src[:, :, c0:min(b, c0 + 1024)]) for c0 in range(0, b, 1024)]
        self.cx.dma("pool", self.slots[i % self.NSLOT], pairs, writes=[self.key(i)])

    def need(self, i):
        upto = min(len(self.plan), i + self.NSLOT)
        while self.issued < upto:
            self._issue(self.issued)
            self.issued += 1


def _mk_env(G):
    if G is not None:
        return G["nc"], G["cx"], G["dram"], G["banks"], G["bankT"]
    nc = bass.Bass("TRN2", target_bir_lowering=False)
    cx = Ctx(nc)
    banks = [cx.ps(f"bank{i}") for i in range(7)]
    cx.uid += 1
    bankT = nc.alloc_psum_tensor(f"bankT_{cx.uid}", [128, 1024], BF16)
    return nc, cx, {}, banks, bankT


def build_T(stages, G=None, io=None):
    nc, cx, dram, banks, bankT = _mk_env(G)
    io = io or {}
    cx.open_scope()

    def din(name, shape, dtype=F32):
        if name in io:
            return io[name]
        if name not in dram:
            dram[name] = nc.dram_tensor(name, list(shape), dtype, kind="ExternalInput").ap()
        return dram[name]

    def dout(name, shape, dtype=F32):
        if name in io:
            return io[name]
        dram[name] = nc.dram_tensor(name, list(shape), dtype, kind="ExternalOutput").ap()
        return dram[name]

    xT_d = din("xT", [128, KT, TOK])
    cT_d = din("cT", [128, KT])

    x = cx.sb("x", [128, KT, TOK], F32)
    sq = [cx.sb(f"sq{i}", [128, TOK], F32) for i in range(2)]
    rstd = cx.sb("rstd", [128, TOK], F32)
    cact = cx.sb("cact", [128, KT], BF16)
    cin = cx.sb("cin", [128, KT], F32)
    ones = cx.sb("ones", [128, 128], F32)
    ws = WStream(cx)
    ld = cx.slot("ld")
    ld2 = cx.slot("ld2")

    cx.op("dve", lambda e: e.memset(ones[:], 1.0), writes=["ones"])
    cx.dma("sp", ld, [(x[:, 0:KT // 2, :], xT_d[:, 0:KT // 2, :]), (x[:, KT // 2:KT, :], xT_d[:, KT // 2:KT, :])],
           writes=["x"])
    cx.dma("sp", ld2, [(cin[:], cT_d)], writes=["cin"])
    cx.op("act", lambda e: e.activation(out=cact[:], in_=cin[:], func=AF.Silu),
          reads=["cin"], writes=["cact"])

    small_id = [0]

    def load_small(name, shape):
        small_id[0] += 1
        t = cx.sb(name, shape, F32)
        sl = cx.slot(name)
        cx.dma("sp", sl, [(t[:], din(name, shape))], writes=[name + str(small_id[0])])
        return t, name + str(small_id[0])

    def compute_mod(l, s, which):
        wmod = din(f"w_mod{l}", [D, 9 * D])
        bmod, bkey = load_small(f"b_modT{l}", [128, 9 * KT])
        out = cx.sb(f"mod{l}{s}", [128, 3, KT], F32)
        okey = f"mod{l}{s}_" + "".join(str(w) for w in which)
        blocks = []
        for j in which:
            for cb in range(D // 512):
                c0 = s * 3 * D + j * D + cb * 512
                src = wmod[:, c0:c0 + 512].rearrange("(kt p) c -> p kt c", p=128)
                blocks.append((j, cb, ws.add(src, KT, 512)))
        bank = banks[6]
        for (j, cb, bid) in blocks:
            ws.need(bid)
            wv = ws.view(bid)
            for ft in range(4):
                for kt in range(KT):
                    cx.op("pe", lambda e, ft=ft, kt=kt, wv=wv: e.matmul(
                        bank[:, ft:ft + 1], wv[:, kt, ft * 128:(ft + 1) * 128], cact[:, kt:kt + 1],
                        start=(kt == 0), stop=(kt == KT - 1)),
                        reads=[ws.key(bid), "cact"], writes=["bank6"])
            jj = s * 3 + j
            col = jj * KT + cb * 4
            cx.op("dve", lambda e, j=j, cb=cb, col=col: e.tensor_tensor(
                out=out[:, j, cb * 4:cb * 4 + 4], in0=bank[:, 0:4], in1=bmod[:, col:col + 4], op=ALU.add),
                reads=["bank6", bkey], writes=[okey])
        return out, okey

    def rms_stats(xkey="x"):
        for kt in range(KT):
            s_ = sq[kt % 2]
            cx.op("act", lambda e, kt=kt, s_=s_: e.activation(out=s_[:], in_=x[:, kt, :], func=AF.Square),
                  reads=[xkey], writes=[f"sq{kt % 2}"])
            for t in range(2):
                cx.op("pe", lambda e, kt=kt, t=t, s_=s_: e.matmul(
                    banks[4 + t][:], ones[:], s_[:, t * 512:(t + 1) * 512],
                    start=(kt == 0), stop=(kt == KT - 1)),
                    reads=[f"sq{kt % 2}", "ones"], writes=[f"bank{4 + t}"])
        for t in range(2):
            cx.op("act", lambda e, t=t: e.activation(out=rstd[:, t * 512:(t + 1) * 512], in_=banks[4 + t][:],
                                                      func=AF.Sqrt, scale=1.0 / D, bias=epsb[:]),
                  reads=[f"bank{4 + t}", "epsb"], writes=["rstd"])
        cx.op("dve", lambda e: e.reciprocal(out=rstd[:], in_=rstd[:]), reads=["rstd"], writes=["rstd"])

    epsb = cx.sb("epsb", [128, 1], F32)
    cx.op("dve", lambda e: e.memset(epsb[:], EPS), writes=["epsb"])

    def adaln(l, s, mod, mkey, dst, dkey):
        ng, ngkey = load_small(f"norm_gT{l}{s}", [128, KT])
        a = cx.sb(f"a{l}{s}", [128, KT], F32)
        akey = f"a{l}{s}"
        cx.op("dve", lambda e: e.scalar_tensor_tensor(out=a[:], in0=mod[:, 1, :], scalar=1.0, in1=ng[:],
                                                      op0=ALU.add, op1=ALU.mult),
              reads=[mkey, ngkey], writes=[akey])
        rms_stats()
        for kt in range(KT):
            s_ = sq[kt % 2]
            cx.op("dve", lambda e, kt=kt, s_=s_: e.scalar_tensor_tensor(
                out=s_[:], in0=x[:, kt, :], scalar=a[:, kt:kt + 1], in1=rstd[:],
                op0=ALU.mult, op1=ALU.mult),
                reads=["x", akey, "rstd"], writes=[f"sq{kt % 2}"])
            cx.op("act", lambda e, kt=kt, s_=s_: e.activation(
                out=dst[:, kt, :], in_=s_[:], func=AF.Identity, bias=mod[:, 0, kt:kt + 1], scale=1.0),
                reads=[f"sq{kt % 2}", mkey], writes=[dkey])

    def ffn(l, s):
        fi = 0 if s == 0 else 1
        w1 = din(f"ffn_w1_{l}{fi}", [D, FFN])
        w3 = din(f"ffn_w3_{l}{fi}", [D, FFN])
        w2 = din(f"ffn_w2_{l}{fi}", [FFN, D])
        cx.open_scope()
        h = cx.sb("h", [128, KT, TOK], BF16)
        g = [cx.sb(f"g{i}", [128, 4, TOK], BF16) for i in range(2)]
        silu_t = [cx.sb(f"silu{i}", [128, 512], F32) for i in range(2)]
        mod, mkey = MODS[(l, s)]
        adaln(l, s, mod, mkey, h, "h")
        hg = cx.sb(f"hg{l}{s}", [128, KT], F32)
        cx.op("dve", lambda e: e.tensor_scalar(out=hg[:], in0=mod[:, 2, :], scalar1=0.5, scalar2=None,
                                               op0=ALU.mult),
              reads=[mkey], writes=[f"hg{l}{s}"])
        NCH = FFN // 512
        blk = []
        for c in range(NCH):
            b1 = ws.add(w1[:, c * 512:(c + 1) * 512].rearrange("(kt p) c -> p kt c", p=128), KT, 512)
            b3 = ws.add(w3[:, c * 512:(c + 1) * 512].rearrange("(kt p) c -> p kt c", p=128), KT, 512)
            b2 = ws.add(w2[c * 512:(c + 1) * 512, :].rearrange("(kt p) c -> p kt c", p=128), 4, D)
            blk.append((b1, b3, b2))
        ev = 0
        for c in range(NCH):
            b1, b3, b2 = blk[c]
            gb = g[c % 2]
            gkey = f"g{c % 2}"
            ws.need(b1)
            w1v, w3v = ws.view(b1), ws.view(b3)
            for m in range(4):
                for t in range(2):
                    pa, pb = banks[t * 2], banks[t * 2 + 1]
                    ka, kb = f"bank{t * 2}", f"bank{t * 2 + 1}"
                    for kt in range(KT):
                        cx.op("pe", lambda e, kt=kt, m=m, t=t, pa=pa: e.matmul(
                            pa[:], w1v[:, kt, m * 128:(m + 1) * 128], h[:, kt, t * 512:(t + 1) * 512],
                            start=(kt == 0), stop=(kt == KT - 1)),
                            reads=[ws.key(b1), "h"], writes=[ka])
                    for kt in range(KT):
                        cx.op("pe", lambda e, kt=kt, m=m, t=t, pb=pb: e.matmul(
                            pb[:], w3v[:, kt, m * 128:(m + 1) * 128], h[:, kt, t * 512:(t + 1) * 512],
                            start=(kt == 0), stop=(kt == KT - 1)),
                            reads=[ws.key(b3), "h"], writes=[kb])
                    st_ = silu_t[ev % 2]
                    skey = f"silu{ev % 2}"
                    ev += 1
                    cx.op("act", lambda e, pa=pa, st_=st_: e.activation(out=st_[:], in_=pa[:], func=AF.Silu),
                          reads=[ka], writes=[skey])
                    cx.op("dve", lambda e, pb=pb, st_=st_, m=m, t=t, gb=gb: e.tensor_tensor(
                        out=gb[:, m, t * 512:(t + 1) * 512], in0=st_[:], in1=pb[:], op=ALU.mult),
                        reads=[skey, kb], writes=[gkey])
            ws.need(b2)
            w2v = ws.view(b2)
            for j in range(KT):
                for t in range(2):
                    bi = 4 + ((j * 2 + t) % 3)
                    po, ko = banks[bi], f"bank{bi}"
                    for m in range(4):
                        cx.op("pe", lambda e, m=m, j=j, t=t, po=po, gb=gb: e.matmul(
                            po[:], w2v[:, m, j * 128:(j + 1) * 128], gb[:, m, t * 512:(t + 1) * 512],
                            start=(m == 0), stop=(m == 3)),
                            reads=[ws.key(b2), gkey], writes=[ko])
                    cx.op("dve", lambda e, j=j, t=t, po=po: e.scalar_tensor_tensor(
                        out=x[:, j, t * 512:(t + 1) * 512], in0=po[:], scalar=hg[:, j:j + 1],
                        in1=x[:, j, t * 512:(t + 1) * 512], op0=ALU.mult, op1=ALU.add),
                        reads=[ko, f"hg{l}{s}"], writes=["x"])
        cx.close_scope()

    def outproj(wname, krows, src, skey, nkt, gate, gkey):
        wd = din(wname, [krows, D])
        blks = [ws.add(wd[:, j * 128:(j + 1) * 128].rearrange("(kt p) c -> p kt c", p=128), nkt, 128) for j in range(KT)]
        for j in range(KT):
            ws.need(blks[j])
            wv = ws.view(blks[j])
            for t in range(2):
                bi = 4 + ((j * 2 + t) % 3)
                po, ko = banks[bi], f"bank{bi}"
                for kt in range(nkt):
                    cx.op("pe", lambda e, kt=kt: e.matmul(po[:], wv[:, kt, :], src[:, kt, t * 512:(t + 1) * 512],
                                                           start=(kt == 0), stop=(kt == nkt - 1)),
                          reads=[ws.key(blks[j]), skey], writes=[ko])
                cx.op("dve", lambda e: e.scalar_tensor_tensor(
                    out=x[:, j, t * 512:(t + 1) * 512], in0=po[:], scalar=gate[:, j:j + 1],
                    in1=x[:, j, t * 512:(t + 1) * 512], op0=ALU.mult, op1=ALU.add),
                    reads=[ko, gkey], writes=["x"])

    def gath_select(gath, ntile, dests, gdt):
        selT, selk = load_small("selT", [128, 2])
        if gdt == F32:
            stA, kA = sq, ["sq0", "sq1"]
        else:
            stA, kA = [cx.sb(f"gsA{i}", [128, TOK], gdt) for i in range(2)], ["gsA0", "gsA1"]
        stB = [cx.sb(f"gsB{i}", [128, TOK], gdt) for i in range(2)]
        sls = [cx.slot(f"gs{i}") for i in range(2)]
        n = 0
        for dst, dkey, tiles in dests:
            for di, (r, lt) in enumerate(tiles):
                b_ = n % 2
                n += 1
                cx.dma("sp", sls[b_], [(stA[b_][:], gath(r, lt, 0)), (stB[b_][:], gath(r, lt, 1))],
                       reads=io.get("dep", []), writes=[kA[b_], f"gsB{b_}"])
                cx.op("dve", lambda e: e.tensor_scalar(out=stB[b_][:], in0=stB[b_][:], scalar1=selT[:, 1:2], scalar2=None, op0=ALU.mult),
                      reads=[f"gsB{b_}", selk], writes=[f"gsB{b_}"])
                cx.op("dve", lambda e: e.scalar_tensor_tensor(out=dst[:, di, :], in0=stA[b_][:], scalar=selT[:, 0:1], in1=stB[b_][:],
                                                               op0=ALU.mult, op1=ALU.add),
                      reads=[kA[b_], f"gsB{b_}", selk], writes=[dkey])

    def mix0_post():
        cx.open_scope()
        mod, mkey = MODS[(0, 1, "g")]
        y5b = cx.sb("y5b", [128, 8, TOK], BF16)
        ysb = cx.sb("ysb", [128, 16, TOK], BF16)
        sig = cx.sb("sig", [128, 8, 512], BF16)
        sl5, sls = cx.slot("y5in"), cx.slot("ysin")
        if "y_gath" not in io:
            y5_d = din("y5T_in", [128, 8, TOK])
            ys_d = din("ysT_in", [128, 16, TOK])
        if "y_gath" in io:
            gath_select(io["y_gath"], 12, [(y5b, "y5b", [(ft // 4, ft % 4) for ft in range(8)]),
                                           (ysb, "ysb", [(kt // 8, 4 + kt % 8) for kt in range(16)])], F32)
        else:
            cx.dma("pool", sl5, cast_pairs(y5b[:], y5_d), writes=["y5b"])
            cx.dma("pool", sls, cast_pairs(ysb[:], ys_d), writes=["ysb"])
        glub, gbk = load_small("glu_bT", [128, 8])
        sng, sgk = load_small("ssd_norm_gT", [128, 16])
        gw = din("s5_glu_w", [1024, 1024])
        blks = [ws.add(gw[:, j * 128:(j + 1) * 128].rearrange("(kt p) c -> p kt c", p=128), 8, 128) for j in range(8)]
        for t in range(2):
            for j in range(8):
                ws.need(blks[j])
                bi = j % 4
                po, ko = banks[bi], f"bank{bi}"
                wv = ws.view(blks[j])
                for kt in range(8):
                    cx.op("pe", lambda e, kt=kt: e.matmul(po[:], wv[:, kt, :], y5b[:, kt, t * 512:(t + 1) * 512],
                                                           start=(kt == 0), stop=(kt == 7)),
                          reads=[ws.key(blks[j]), "y5b"], writes=[ko])
                cx.op("act", lambda e: e.activation(out=sig[:, j, :], in_=po[:], func=AF.Sigmoid, bias=glub[:, j:j + 1], scale=1.0),
                      reads=[ko, gbk], writes=["sig"])
            if t == 0:
                blks = [ws.add(gw[:, j * 128:(j + 1) * 128].rearrange("(kt p) c -> p kt c", p=128), 8, 128) for j in range(8)]
            for j in range(8):
                cx.op("dve", lambda e: e.tensor_tensor(out=y5b[:, j, t * 512:(t + 1) * 512], in0=y5b[:, j, t * 512:(t + 1) * 512],
                                                        in1=sig[:, j, :], op=ALU.mult), reads=["y5b", "sig"], writes=["y5b"])
        for kt in range(KT):
            s_ = sq[kt % 2]
            cx.op("act", lambda e: e.activation(out=s_[:], in_=ysb[:, kt, :], func=AF.Square), reads=["ysb"], writes=[f"sq{kt % 2}"])
            for t in range(2):
                cx.op("pe", lambda e: e.matmul(banks[4 + t][:], ones[:], s_[:, t * 512:(t + 1) * 512], start=(kt == 0), stop=(kt == KT - 1)),
                      reads=[f"sq{kt % 2}", "ones"], writes=[f"bank{4 + t}"])
        for t in range(2):
            cx.op("act", lambda e: e.activation(out=rstd[:, t * 512:(t + 1) * 512], in_=banks[4 + t][:], func=AF.Sqrt, scale=1.0 / D, bias=epsb[:]),
                  reads=[f"bank{4 + t}", "epsb"], writes=["rstd"])
        cx.op("dve", lambda e: e.reciprocal(out=rstd[:], in_=rstd[:]), reads=["rstd"], writes=["rstd"])
        for kt in range(KT):
            cx.op("dve", lambda e: e.scalar_tensor_tensor(out=ysb[:, kt, :], in0=ysb[:, kt, :], scalar=sng[:, kt:kt + 1], in1=rstd[:],
                                                           op0=ALU.mult, op1=ALU.mult), reads=["ysb", sgk, "rstd"], writes=["ysb"])
        wd = din("hyb_w_out", [3072, D])
        blk5 = [ws.add(wd[0:1024, j * 128:(j + 1) * 128].rearrange("(kt p) c -> p kt c", p=128), 8, 128) for j in range(KT)]
        gate = mod[:, 2, :]
        for j in range(KT):
            blks_ = ws.add(wd[1024:3072, j * 128:(j + 1) * 128].rearrange("(kt p) c -> p kt c", p=128), 16, 128)
            blk5[j] = (blk5[j], blks_)
        for part in range(2):
            for j in range(KT):
                b_ = blk5[j][part]
                ws.need(b_)
                wv = ws.view(b_)
                nk = 8 if part == 0 else 16
                srcb, skey = (y5b, "y5b") if part == 0 else (ysb, "ysb")
                for t in range(2):
                    bi = 4 + ((j * 2 + t) % 3)
                    po, ko = banks[bi], f"bank{bi}"
                    for kt in range(nk):
                        cx.op("pe", lambda e, kt=kt: e.matmul(po[:], wv[:, kt, :], srcb[:, kt, t * 512:(t + 1) * 512],
                                                               start=(kt == 0), stop=(kt == nk - 1)),
                              reads=[ws.key(b_), skey], writes=[ko])
                    cx.op("dve", lambda e: e.scalar_tensor_tensor(
                        out=x[:, j, t * 512:(t + 1) * 512], in0=po[:], scalar=gate[:, j:j + 1],
                        in1=x[:, j, t * 512:(t + 1) * 512], op0=ALU.mult, op1=ALU.add),
                        reads=[ko, mkey], writes=["x"])
        cx.close_scope()

    def rwkv_post():
        cx.open_scope()
        mod, mkey = MODS[(1, 1, "g")]
        ygb = cx.sb("ygb", [128, 16, TOK], BF16)
        slg = cx.slot("ygin")
        if "yg_gath" in io:
            gath_select(io["yg_gath"], 8, [(ygb, "ygb", [(kt // 8, kt % 8) for kt in range(16)])], BF16)
        else:
            yg_d = din("ygT_in", [128, 16, TOK], BF16)
            cx.dma("sp", slg, [(ygb[:, 4 * i:4 * i + 4, :], yg_d[:, 4 * i:4 * i + 4, :]) for i in range(4)], writes=["ygb"])
        outproj("rwkv_w_o", D, ygb, "ygb", KT, mod[:, 2, :], mkey)
        cx.close_scope()

    st_slot = cx.slot("st")
    MODS = {}
    for stg in stages:
        kind = stg["kind"]
        if kind == "ffn":
            MODS[(stg["l"], stg["s"])] = compute_mod(stg["l"], stg["s"], (0, 1, 2))
        elif kind == "h_out":
            MODS[(stg["l"], 1, "h")] = compute_mod(stg["l"], 1, (0, 1))
        elif kind == "mix0_post":
            MODS[(0, 1, "g")] = compute_mod(0, 1, (2,))
        elif kind == "rwkv_post":
            MODS[(1, 1, "g")] = compute_mod(1, 1, (2,))
    for stg in stages:
        kind = stg["kind"]
        if kind == "ffn":
            ffn(stg["l"], stg["s"])
        elif kind == "mix0_post":
            mix0_post()
        elif kind == "rwkv_post":
            rwkv_post()
        elif kind == "h_out":
            l = stg["l"]
            cx.open_scope()
            h = cx.sb("h", [128, KT, TOK], BF16)
            mod, mkey = MODS[(l, 1, "h")]
            adaln(l, 1, mod, mkey, h, "h")
            if "hT_out_pairs" in io:
                cx.dma("sp", st_slot, io["hT_out_pairs"](h), reads=["h"])
            else:
                hout = dout("hT_out", [128, KT, TOK], BF16)
                cx.dma("sp", st_slot, [(hout[:, 0:KT // 2, :], h[:, 0:KT // 2, :]), (hout[:, KT // 2:KT, :], h[:, KT // 2:KT, :])],
                       reads=["h"])
            cx.close_scope()
        elif kind == "x_out":
            xout = dout("xT_out", [128, KT, TOK], F32)
            cx.dma("sp", st_slot, [(xout[:, 0:KT // 2, :], x[:, 0:KT // 2, :]), (xout[:, KT // 2:KT, :], x[:, KT // 2:KT, :])],
                   reads=["x"])
        elif kind == "final":
            fg, fkey = load_small("final_gT", [128, KT])
            rms_stats()
            for kt in range(KT):
                cx.op("dve", lambda e, kt=kt: e.scalar_tensor_tensor(
                    out=x[:, kt, :], in0=x[:, kt, :], scalar=fg[:, kt:kt + 1], in1=rstd[:],
                    op0=ALU.mult, op1=ALU.mult),
                    reads=["x", fkey, "rstd"], writes=["x"])
            xout = dout("xT_out", [128, KT, TOK], F32)
            cx.dma("sp", st_slot, [(xout[:, 0:KT // 2, :], x[:, 0:KT // 2, :]), (xout[:, KT // 2:KT, :], x[:, KT // 2:KT, :])],
                   reads=["x"])
    cx.close_scope()
    cx.wait_all("sp")
    return nc


def cast_pairs(dst, src):
    if len(dst.shape) == 2:
        n = dst.shape[1]
        return [(dst[:, c0:min(n, c0 + 1024)], src[:, c0:min(n, c0 + 1024)]) for c0 in range(0, n, 1024)]
    out = []
    for a in range(dst.shape[1]):
        n = dst.shape[2]
        for c0 in range(0, n, 1024):
            out.append((dst[:, a, c0:min(n, c0 + 1024)], src[:, a, c0:min(n, c0 + 1024)]))
    return out


def fm(v):
    v = np.asarray(v)
    return np.ascontiguousarray(v.reshape(-1, 128).T)


def to_xT(x):
    out = []
    for b in range(4):
        for j in range(2):
            xs = x[b, j * TOK:(j + 1) * TOK, :]
            out.append(np.ascontiguousarray(xs.T.reshape(KT, 128, TOK).transpose(1, 0, 2)))
    return out


def from_xT(tiles):
    x = np.empty((4, SEQ, D), np.float32)
    for b in range(4):
        for j in range(2):
            t = tiles[b * 2 + j]
            x[b, j * TOK:(j + 1) * TOK, :] = t.transpose(1, 0, 2).reshape(D, TOK).T
    return x


S5TC = 256
GELU_C = 0.7978845608028654
TWO_PI = 6.283185307179586


CHK_COUNT = 1
DBG_BANKS = [0, 1]
DBG_NODVE = False


class _Stop(Exception):
    pass


def build_M0(do_s5=True, do_ssd=True, stop=None, G=None, io=None):
    nc, cx, dram, banks, bankT = _mk_env(G)
    io = io or {}
    cx.open_scope()

    cnt = [CHK_COUNT]

    def chk(n):
        if stop == n:
            cnt[0] -= 1
            if cnt[0] <= 0:
                raise _Stop()
    try:
        _build_M0_body(nc, cx, dram, banks, bankT, io, do_s5, do_ssd, chk)
    except _Stop:
        pass
    while cx.stacks and stop is not None:
        cx.close_scope()
    if stop is None:
        cx.close_scope()
    cx.wait_all("sp")
    return nc


def _build_M0_body(nc, cx, dram, banks, bankT, io, do_s5, do_ssd, chk):

    def din(name, shape, dtype=F32):
        if name in io:
            return io[name]
        if name not in dram:
            dram[name] = nc.dram_tensor(name, list(shape), dtype, kind="ExternalInput").ap()
        return dram[name]

    def dout(name, shape, dtype=F32):
        if name in io:
            return io[name]
        dram[name] = nc.dram_tensor(name, list(shape), dtype, kind="ExternalOutput").ap()
        return dram[name]

    BF16S = "bf16_from_f32"

    def load(name, shape, dtype=F32, q="sp"):
        sdt, ddt = (BF16, F32) if dtype == BF16S else (dtype, dtype)
        t = cx.sb(name, shape, sdt)
        sl = cx.slot(name)
        pairs = cast_pairs(t[:], din(name, shape, ddt)) if dtype == BF16S else [(t[:], din(name, shape, ddt))]
        cx.dma(q, sl, pairs, writes=[name])
        return t

    w_d = din("w_in_c", [D, 3600])
    hT = cx.sb("hT", [128, KT, SEQ], BF16)
    sl = cx.slot("hT")
    if "hT_pairs" in io:
        cx.dma("sp", sl, io["hT_pairs"](hT, 0), reads=io.get("dep", []), writes=["hT"])
    else:
        hT_d = din("hT", [128, KT, SEQ], BF16)
        cx.dma("sp", sl, [(hT[:, 4 * i:4 * i + 4, :], hT_d[:, 4 * i:4 * i + 4, :]) for i in range(4)], writes=["hT"])
    ws = WStream(cx, nslot=4, elems=4096)
    ident = load("ident", [128, 128])
    identb = cx.sb("identb", [128, 128], BF16)
    cx.op("dve", lambda e: e.tensor_copy(out=identb[:], in_=ident[:]), reads=["ident"], writes=["identb"])
    st_slot = cx.slot("st")
    st_slot2 = cx.slot("st2")

    def proj(blk, col0, ncol_tiles, evac):
        wv = ws.view(blk)
        n = 0
        for ti in range(ncol_tiles):
            for tb in range(4):
                bk = DBG_BANKS[n % len(DBG_BANKS)]
                n += 1
                for kt in range(KT):
                    cx.op("pe", lambda e, kt=kt, ti=ti, tb=tb, bk=bk: e.matmul(
                        banks[bk][:], wv[:, kt, (col0 + ti) * 128:(col0 + ti + 1) * 128],
                        hT[:, kt, tb * 512:(tb + 1) * 512], start=(kt == 0), stop=(kt == KT - 1)),
                        reads=[ws.key(blk), "hT"], writes=[f"bank{bk}"])
                chk(53)
                evac(ti, tb, banks[bk], f"bank{bk}")
                chk(54)

    if do_s5:
        cx.open_scope()
        lre = load("s5_lre", [128, 16])
        lim = load("s5_lim", [128, 16])
        ldt = load("s5_ldt", [128, 16])
        d5 = load("s5_dT", [128, 4])
        cre = load("s5_cre", [128, 16, 128], BF16S, q="pool")
        cimn = load("s5_cim", [128, 16, 128], BF16S, q="pool")
        cx.op("dve", lambda e: e.tensor_scalar(out=cimn[:], in0=cimn[:], scalar1=-1.0, scalar2=None, op0=ALU.mult),
              reads=["s5_cim"], writes=["s5_cim"])
        bre = cx.sb("breT", [128, 16, 128], BF16)
        bim = cx.sb("bimT", [128, 16, 128], BF16)

        sm = {}

        def S(name):
            sm[name] = cx.sb("s5_" + name, [128, 16], F32)
            return sm[name]

        def tt(o, a, b, op, eng="dve"):
            cx.op(eng, lambda e: e.tensor_tensor(out=sm[o][:], in0=sm[a][:], in1=sm[b][:], op=op),
                  reads=["s5sm"], writes=["s5sm"])

        def ts(o, a, s1, op0, s2=None, op1=None):
            if op1 is None:
                cx.op("dve", lambda e: e.tensor_scalar(out=sm[o][:], in0=sm[a][:], scalar1=s1, scalar2=None, op0=op0),
                      reads=["s5sm"], writes=["s5sm"])
            else:
                cx.op("dve", lambda e: e.tensor_scalar(out=sm[o][:], in0=sm[a][:], scalar1=s1, scalar2=s2, op0=op0, op1=op1),
                      reads=["s5sm"], writes=["s5sm"])

        def act(o, a, func, scale=1.0):
            cx.op("act", lambda e: e.activation(out=sm[o][:], in_=sm[a][:], func=func, scale=scale),
                  reads=["s5sm"], writes=["s5sm"])

        chk(1)
        sm["lre"], sm["lim"], sm["ldt"] = lre, lim, ldt
        for n_ in ("lr", "dt", "mag", "ang", "cs", "sn", "t1", "t2", "t3", "den", "nr", "fre", "fim", "lbr", "lbi", "rden"):
            S(n_)
        cx.wait_all("dve")
        cx.wait_all("act")
        ts("lr", "lre", -1e-4, ALU.min)
        act("dt", "ldt", AF.Exp)
        tt("t1", "lr", "dt", ALU.mult)
        act("mag", "t1", AF.Exp)
        tt("ang", "lim", "dt", ALU.mult)

        def sincos(o, a, shift):
            ki = cx.sb("s5_ki", [128, 16], mybir.dt.int32)
            ts("t1", a, 1.0 / TWO_PI, ALU.mult, shift / TWO_PI, ALU.add)
            cx.op("dve", lambda e: e.tensor_copy(out=ki[:], in_=sm["t1"][:]), reads=["s5sm"], writes=["s5ki"])
            cx.op("dve", lambda e: e.tensor_copy(out=sm["t2"][:], in_=ki[:]), reads=["s5ki"], writes=["s5sm"])
            tt("t1", "t1", "t2", ALU.subtract)
            ts("t2", "t1", 0.5, ALU.is_gt)
            tt("t1", "t1", "t2", ALU.subtract)
            ts("t2", "t1", -0.5, ALU.is_lt)
            tt("t1", "t1", "t2", ALU.add)
            act(o, "t1", AF.Sin, scale=TWO_PI)

        sincos("sn", "ang", 0.0)
        sincos("cs", "ang", TWO_PI / 4)
        tt("lbr", "mag", "cs", ALU.mult)
        tt("lbi", "mag", "sn", ALU.mult)
        tt("t1", "lr", "lr", ALU.mult)
        tt("t2", "lim", "lim", ALU.mult)
        tt("den", "t1", "t2", ALU.add)
        cx.op("dve", lambda e: e.reciprocal(out=sm["rden"][:], in_=sm["den"][:]), reads=["s5sm"], writes=["s5sm"])
        ts("nr", "lbr", -1.0, ALU.add)
        tt("t1", "nr", "lr", ALU.mult)
        tt("t2", "lbi", "lim", ALU.mult)
        tt("t1", "t1", "t2", ALU.add)
        tt("fre", "t1", "rden", ALU.mult)
        tt("t1", "lbi", "lr", ALU.mult)
        tt("t2", "nr", "lim", ALU.mult)
        tt("t1", "t1", "t2", ALU.subtract)
        tt("fim", "t1", "rden", ALU.mult)
        S("nfim")
        ts("nfim", "fim", -1.0, ALU.mult)
        S("nsn")
        ts("nsn", "sn", -1.0, ALU.mult)

        chk(2)
        cx.open_scope()
        xbre = load("s5_xbre", [128, 16, 128], q="act")
        xbim = load("s5_xbim", [128, 16, 128], q="act")
        xt = [cx.sb(f"s5xt{i}", [128, 128], F32) for i in range(2)]
        for pr in range(16):
            for part, (A, fa, Bm, fb) in enumerate(((xbre, "fre", xbim, "nfim"), (xbim, "fre", xbre, "fim"))):
                t_ = xt[part]
                cx.op("dve", lambda e, pr=pr, A=A, fa=fa, t_=t_: e.tensor_scalar(
                    out=t_[:], in0=A[:, pr, :], scalar1=sm[fa][:, pr:pr + 1], scalar2=None, op0=ALU.mult),
                    reads=["s5sm", "s5_xbre", "s5_xbim"], writes=[f"s5xt{part}"])
                cx.op("dve", lambda e, pr=pr, Bm=Bm, fb=fb, t_=t_: e.scalar_tensor_tensor(
                    out=t_[:], in0=Bm[:, pr, :], scalar=sm[fb][:, pr:pr + 1], in1=t_[:], op0=ALU.mult, op1=ALU.add),
                    reads=["s5sm", "s5_xbre", "s5_xbim", f"s5xt{part}"], writes=[f"s5xt{part}"])
                cx.op("pe", lambda e, t_=t_, part=part: e.transpose(banks[2 + part][:, 0:128], t_[:], ident[:]),
                      reads=[f"s5xt{part}", "ident"], writes=[f"bank{2 + part}"])
                dst = bre if part == 0 else bim
                cx.op("act", lambda e, dst=dst, pr=pr, part=part: e.activation(
                    out=dst[:, pr, :], in_=banks[2 + part][:, 0:128], func=AF.Copy),
                    reads=[f"bank{2 + part}"], writes=["breT" if part == 0 else "bimT"])

        chk(3)
        cx.close_scope()
        chk(4)
        ctab = cx.sb("ctab", [128, 16, S5TC], F32)
        stab = cx.sb("stab", [128, 16, S5TC], F32)
        rho = cx.sb("rho", [128, 16, S5TC], F32)
        ec = cx.sb("ec", [128, 16], F32)
        es = cx.sb("es", [128, 16], F32)
        et = [cx.sb(f"et{i}", [128, 16], F32) for i in range(3)]
        cx.op("dve", lambda e: e.memset(ctab[:, :, 0:1], 1.0), writes=["tab"])
        cx.op("dve", lambda e: e.memset(stab[:, :, 0:1], 0.0), reads=["tab"], writes=["tab"])
        cx.op("dve", lambda e: e.tensor_copy(out=ec[:], in_=sm["cs"][:]), reads=["s5sm"], writes=["e"])
        cx.op("dve", lambda e: e.tensor_copy(out=es[:], in_=sm["sn"][:]), reads=["s5sm", "e"], writes=["e"])
        L = 1
        while L < S5TC:
            for pr in range(16):
                cx.op("dve", lambda e, pr=pr, L=L: e.tensor_scalar(
                    out=ctab[:, pr, L:2 * L], in0=ctab[:, pr, 0:L], scalar1=ec[:, pr:pr + 1], scalar2=None, op0=ALU.mult),
                    reads=["tab", "e"], writes=["tab"])
                cx.op("dve", lambda e, pr=pr, L=L: e.tensor_scalar(
                    out=stab[:, pr, L:2 * L], in0=ctab[:, pr, 0:L], scalar1=es[:, pr:pr + 1], scalar2=None, op0=ALU.mult),
                    reads=["tab", "e"], writes=["tab"])
            cx.op("dve", lambda e: e.tensor_scalar(out=et[0][:], in0=es[:], scalar1=-1.0, scalar2=None, op0=ALU.mult),
                  reads=["e"], writes=["et"])
            for pr in range(16):
                cx.op("dve", lambda e, pr=pr, L=L: e.scalar_tensor_tensor(
                    out=ctab[:, pr, L:2 * L], in0=stab[:, pr, 0:L], scalar=et[0][:, pr:pr + 1], in1=ctab[:, pr, L:2 * L],
                    op0=ALU.mult, op1=ALU.add), reads=["tab", "et"], writes=["tab"])
                cx.op("dve", lambda e, pr=pr, L=L: e.scalar_tensor_tensor(
                    out=stab[:, pr, L:2 * L], in0=stab[:, pr, 0:L], scalar=ec[:, pr:pr + 1], in1=stab[:, pr, L:2 * L],
                    op0=ALU.mult, op1=ALU.add), reads=["tab", "e"], writes=["tab"])
            cx.op("dve", lambda e: e.tensor_tensor(out=et[1][:], in0=ec[:], in1=ec[:], op=ALU.mult), reads=["e"], writes=["et1"])
            cx.op("dve", lambda e: e.tensor_tensor(out=et[2][:], in0=es[:], in1=es[:], op=ALU.mult), reads=["e"], writes=["et2"])
            cx.op("dve", lambda e: e.scalar_tensor_tensor(out=es[:], in0=es[:], scalar=2.0, in1=ec[:], op0=ALU.mult, op1=ALU.mult),
                  reads=["e"], writes=["e"])
            cx.op("dve", lambda e: e.tensor_tensor(out=ec[:], in0=et[1][:], in1=et[2][:], op=ALU.subtract),
                  reads=["et1", "et2", "e"], writes=["e"])
            L *= 2
        for pr in range(16):
            cx.op("act", lambda e, pr=pr: e.activation(out=rho[:, pr, :], in_=ctab[:, pr, :], func=AF.Identity,
                                                        scale=0.0, bias=sm["mag"][:, pr:pr + 1]),
                  reads=["tab", "s5sm"], writes=["rho"])

        chk(5)
        u32 = cx.sb("u32", [128, SEQ], F32)
        ubf = cx.sb("ubf", [128, SEQ], BF16)
        y5 = cx.sb("y5", [128, SEQ], F32)
        carry = [cx.sb(f"carry{i}", [128, 16], F32) for i in range(2)]
        cx.op("dve", lambda e: e.memset(carry[0][:], 0.0), writes=["carry"])
        cx.op("dve", lambda e: e.memset(carry[1][:], 0.0), reads=["carry"], writes=["carry"])
        chk(51)
        tmp = [[cx.sb(f"s5tmp{j}{i}", [128, S5TC], F32) for i in range(8)] for j in range(2)]
        sbf = [[cx.sb(f"s5sbf{i}{j}", [128, S5TC], BF16) for j in range(2)] for i in range(4)]
        gl = [cx.sb(f"s5gl{i}", [128, S5TC], F32) for i in range(4)]
        y5_d = None if "y_tile" in io else dout("y5T", [128, 4, SEQ])
        NCH = SEQ // S5TC
        for o in range(4):
            blk = ws.add(w_d[:, o * 128:(o + 1) * 128].rearrange("(kt p) c -> p kt c", p=128), KT, 128)
            ws.need(blk)
            chk(52)

            def ev_u(ti, tb, ps, key):
                cx.op("act", lambda e: e.activation(out=u32[:, tb * 512:(tb + 1) * 512], in_=ps[:], func=AF.Copy),
                      reads=[key], writes=["u32"])
                if not DBG_NODVE:
                    cx.op("dve", lambda e: e.tensor_copy(out=ubf[:, tb * 512:(tb + 1) * 512], in_=ps[:]),
                          reads=[key], writes=["ubf"])
            proj(blk, 0, 1, ev_u)
            chk(6)
            for ch in range(NCH):
                c0 = ch * S5TC
                if ch == 1:
                    chk(7)
                for pp in range(4):
                    pr = o * 4 + pp
                    kre, kim = f"bank{2 + (pp % 2) * 2}", f"bank{3 + (pp % 2) * 2}"
                    pre, pim = banks[2 + (pp % 2) * 2], banks[3 + (pp % 2) * 2]
                    cx.op("pe", lambda e: e.matmul(pre[:, 0:S5TC], bre[:, pr, :], ubf[:, c0:c0 + S5TC], start=True, stop=True),
                          reads=["breT", "ubf"], writes=[kre])
                    cx.op("pe", lambda e: e.matmul(pim[:, 0:S5TC], bim[:, pr, :], ubf[:, c0:c0 + S5TC], start=True, stop=True),
                          reads=["bimT", "ubf"], writes=[kim])
                    t = tmp[pp % 2]
                    tq = pp % 2
                    ct, stb = ctab[:, pr, :], stab[:, pr, :]
                    cx.op("dve", lambda e: e.tensor_tensor(out=t[0][:], in0=pre[:, 0:S5TC], in1=ct, op=ALU.mult),
                          reads=[kre, "tab"], writes=[f"t{tq}_0"])
                    cx.op("dve", lambda e: e.tensor_tensor(out=t[1][:], in0=pim[:, 0:S5TC], in1=stb, op=ALU.mult),
                          reads=[kim, "tab"], writes=[f"t{tq}_1"])
                    cx.op("dve", lambda e: e.tensor_tensor(out=t[2][:], in0=pim[:, 0:S5TC], in1=ct, op=ALU.mult),
                          reads=[kim, "tab"], writes=[f"t{tq}_2"])
                    cx.op("dve", lambda e: e.tensor_tensor(out=t[3][:], in0=pre[:, 0:S5TC], in1=stb, op=ALU.mult),
                          reads=[kre, "tab"], writes=[f"t{tq}_3"])
                    cx.op("dve", lambda e: e.tensor_tensor(out=t[0][:], in0=t[0][:], in1=t[1][:], op=ALU.add),
                          reads=[f"t{tq}_0", f"t{tq}_1"], writes=[f"t{tq}_0"])
                    cx.op("dve", lambda e: e.tensor_tensor(out=t[2][:], in0=t[2][:], in1=t[3][:], op=ALU.subtract),
                          reads=[f"t{tq}_2", f"t{tq}_3"], writes=[f"t{tq}_2"])
                    cx.op("dve", lambda e: e.tensor_tensor_scan(out=t[4][:], data0=rho[:, pr, :], data1=t[0][:],
                                                                 initial=carry[0][:, pr:pr + 1], op0=ALU.mult, op1=ALU.add),
                          reads=["rho", f"t{tq}_0", "carry"], writes=[f"t{tq}_4"])
                    cx.op("dve", lambda e: e.tensor_tensor_scan(out=t[5][:], data0=rho[:, pr, :], data1=t[2][:],
                                                                 initial=carry[1][:, pr:pr + 1], op0=ALU.mult, op1=ALU.add),
                          reads=["rho", f"t{tq}_2", "carry"], writes=[f"t{tq}_5"])
                    if ch < NCH - 1:
                        cx.op("dve", lambda e: e.tensor_scalar(out=et[1][:, 0:1], in0=t[5][:, S5TC - 1:S5TC], scalar1=es[:, pr:pr + 1],
                                                                scalar2=None, op0=ALU.mult), reads=[f"t{tq}_5", "e"], writes=["et1"])
                        cx.op("dve", lambda e: e.tensor_scalar(out=et[2][:, 0:1], in0=t[4][:, S5TC - 1:S5TC], scalar1=es[:, pr:pr + 1],
                                                                scalar2=None, op0=ALU.mult), reads=[f"t{tq}_4", "e"], writes=["et2"])
                        cx.op("dve", lambda e: e.scalar_tensor_tensor(out=carry[0][:, pr:pr + 1], in0=t[4][:, S5TC - 1:S5TC],
                                                                       scalar=ec[:, pr:pr + 1], in1=et[1][:, 0:1],
                                                                       op0=ALU.mult, op1=ALU.subtract),
                              reads=[f"t{tq}_4", "e", "et1", "carry"], writes=["carry"])
                        cx.op("dve", lambda e: e.scalar_tensor_tensor(out=carry[1][:, pr:pr + 1], in0=t[5][:, S5TC - 1:S5TC],
                                                                       scalar=ec[:, pr:pr + 1], in1=et[2][:, 0:1],
                                                                       op0=ALU.mult, op1=ALU.add),
                              reads=[f"t{tq}_5", "e", "et2", "carry"], writes=["carry"])
                    cx.op("pool", lambda e: e.tensor_tensor(out=t[6][:], in0=t[4][:], in1=ct, op=ALU.mult),
                          reads=[f"t{tq}_4", "tab"], writes=[f"t{tq}_6"])
                    cx.op("pool", lambda e: e.tensor_tensor(out=t[7][:], in0=t[5][:], in1=stb, op=ALU.mult),
                          reads=[f"t{tq}_5", "tab"], writes=[f"t{tq}_7"])
                    cx.op("pool", lambda e: e.tensor_tensor(out=sbf[pp][0][:], in0=t[6][:], in1=t[7][:], op=ALU.subtract),
                          reads=[f"t{tq}_6", f"t{tq}_7"], writes=[f"sbf{pp}0"])
                    cx.op("pool", lambda e: e.tensor_tensor(out=t[6][:], in0=t[4][:], in1=stb, op=ALU.mult),
                          reads=[f"t{tq}_4", "tab", f"t{tq}_6"], writes=[f"t{tq}_6"])
                    cx.op("pool", lambda e: e.tensor_tensor(out=t[7][:], in0=t[5][:], in1=ct, op=ALU.mult),
                          reads=[f"t{tq}_5", "tab", f"t{tq}_7"], writes=[f"t{tq}_7"])
                    cx.op("pool", lambda e: e.tensor_tensor(out=sbf[pp][1][:], in0=t[6][:], in1=t[7][:], op=ALU.add),
                          reads=[f"t{tq}_6", f"t{tq}_7"], writes=[f"sbf{pp}1"])
                py = banks[6]
                for pp in range(4):
                    pr = o * 4 + pp
                    cx.op("pe", lambda e: e.matmul(py[:, 0:S5TC], cre[:, pr, :], sbf[pp][0][:], start=(pp == 0), stop=False),
                          reads=["s5_cre", f"sbf{pp}0"], writes=["bank6"])
                    cx.op("pe", lambda e: e.matmul(py[:, 0:S5TC], cimn[:, pr, :], sbf[pp][1][:], start=False, stop=(pp == 3)),
                          reads=["s5_cim", f"sbf{pp}1"], writes=["bank6"])
                cx.op("dve", lambda e: e.scalar_tensor_tensor(out=gl[0][:], in0=u32[:, c0:c0 + S5TC], scalar=d5[:, o:o + 1],
                                                               in1=py[:, 0:S5TC], op0=ALU.mult, op1=ALU.add),
                      reads=["u32", "s5_dT", "bank6"], writes=["gl0"])
                cx.op("act", lambda e: e.activation(out=gl[1][:], in_=gl[0][:], func=AF.Square), reads=["gl0"], writes=["gl1"])
                cx.op("dve", lambda e: e.tensor_scalar(out=gl[1][:], in0=gl[1][:], scalar1=GELU_C * 0.044715, scalar2=GELU_C,
                                                        op0=ALU.mult, op1=ALU.add), reads=["gl1"], writes=["gl1"])
                cx.op("dve", lambda e: e.tensor_tensor(out=gl[1][:], in0=gl[1][:], in1=gl[0][:], op=ALU.mult),
                      reads=["gl1", "gl0"], writes=["gl1"])
                cx.op("act", lambda e: e.activation(out=gl[2][:], in_=gl[1][:], func=AF.Tanh), reads=["gl1"], writes=["gl2"])
                cx.op("act", lambda e: e.activation(out=gl[3][:], in_=gl[0][:], func=AF.Copy, scale=0.5), reads=["gl0"], writes=["gl3"])
                cx.op("dve", lambda e: e.scalar_tensor_tensor(out=y5[:, c0:c0 + S5TC], in0=gl[2][:], scalar=1.0, in1=gl[3][:],
                                                               op0=ALU.add, op1=ALU.mult),
                      reads=["gl2", "gl3"], writes=["y5"])
            cx.dma("sp", st_slot if o % 2 == 0 else st_slot2,
                   [(io["y_tile"](o) if "y_tile" in io else y5_d[:, o, :], y5[:])], reads=["y5"])
        cx.close_scope()

    if do_ssd:
        cx.open_scope()
        tri = load("tri", [128, 128])
        ones = load("ones128", [128, 128])
        maskneg = load("maskneg", [128, 128])
        cw = load("conv_wT", [128, 16, 4])
        cb = load("conv_bT", [128, 16])
        dtb = load("dt_bias_bc", [128, 16])
        alog = load("a_log_bc", [128, 16])
        dsk = load("ssd_dT", [128, 8])
        onec = cx.sb("onec", [128, 1], F32)
        cx.op("dve", lambda e: e.memset(onec[:], 1.0), writes=["onec"])
        abc = cx.sb("abc", [128, 16], F32)
        cx.op("act", lambda e: e.activation(out=abc[:], in_=alog[:], func=AF.Exp), reads=["a_log_bc"], writes=["abc"])
        cx.op("dve", lambda e: e.tensor_scalar(out=abc[:], in0=abc[:], scalar1=-1.0, scalar2=None, op0=ALU.mult),
              reads=["abc"], writes=["abc"])
        NC_ = SEQ // 128
        dt_all = cx.sb("dt_all", [128, NC_, 16], F32)
        adt = cx.sb("adt", [128, NC_, 16], F32)
        cum = cx.sb("cum", [128, NC_, 16], F32)
        dte = cx.sb("dte", [128, NC_, 16], F32)
        dectot = cx.sb("dectot", [128, NC_, 16], F32)
        dg = [cx.sb(f"dg{i}", [128, 128], F32) for i in range(2)]
        tsm = [cx.sb(f"tsm{i}", [128, 16], F32) for i in range(2)]
        bdt = ws.add(w_d[:, 3584:3600].rearrange("(kt p) c -> p kt c", p=128), KT, 16)
        ws.need(bdt)
        wdt = ws.view(bdt)
        b2 = banks[2]
        for c in range(NC_):
            for kt in range(KT):
                cx.op("pe", lambda e, kt=kt: e.matmul(b2[:, 0:16], hT[:, kt, c * 128:(c + 1) * 128], wdt[:, kt, :],
                                                       start=(kt == 0), stop=(kt == KT - 1)),
                      reads=[ws.key(bdt), "hT"], writes=["bank2"])
            cx.op("dve", lambda e: e.tensor_tensor(out=tsm[0][:], in0=b2[:, 0:16], in1=dtb[:], op=ALU.add),
                  reads=["bank2", "dt_bias_bc"], writes=["tsm0"])
            cx.op("act", lambda e: e.activation(out=tsm[0][:], in_=tsm[0][:], func=AF.Exp), reads=["tsm0"], writes=["tsm0"])
            cx.op("act", lambda e: e.activation(out=dt_all[:, c, :], in_=tsm[0][:], func=AF.Ln, bias=onec[:], scale=1.0),
                  reads=["tsm0", "onec"], writes=["dt_all"])
            cx.op("dve", lambda e: e.tensor_tensor(out=adt[:, c, :], in0=dt_all[:, c, :], in1=abc[:], op=ALU.mult),
                  reads=["dt_all", "abc"], writes=["adt"])
            cx.op("pe", lambda e: e.matmul(banks[3][:, 0:16], tri[:], adt[:, c, :], start=True, stop=True),
                  reads=["tri", "adt"], writes=["bank3"])
            cx.op("pe", lambda e: e.matmul(banks[4][:, 0:16], ones[:], adt[:, c, :], start=True, stop=True),
                  reads=["ones128", "adt"], writes=["bank4"])
            cx.op("act", lambda e: e.activation(out=cum[:, c, :], in_=banks[3][:, 0:16], func=AF.Copy), reads=["bank3"], writes=["cum"])
            cx.op("act", lambda e: e.activation(out=dectot[:, c, :], in_=banks[4][:, 0:16], func=AF.Exp), reads=["bank4"], writes=["dectot"])
            cx.op("dve", lambda e: e.tensor_tensor(out=tsm[1][:], in0=banks[4][:, 0:16], in1=cum[:, c, :], op=ALU.subtract),
                  reads=["bank4", "cum"], writes=["tsm1"])
            cx.op("act", lambda e: e.activation(out=dte[:, c, :], in_=tsm[1][:], func=AF.Exp), reads=["tsm1"], writes=["dte"])

        raw = cx.sb("raw", [128, 4, 3 + SEQ], F32)
        cx.op("dve", lambda e: e.memset(raw[:, :, 0:3], 0.0), writes=["raw"])
        sz = cx.sb("sz", [128, 2, SEQ], BF16)
        cv = [cx.sb("cv0", [128, SEQ], F32)]
        xs32 = cx.sb("xs32", [128, 2, SEQ], F32)
        xsb = cx.sb("xsb", [128, 2, SEQ], BF16)
        BT = cx.sb("BT", [128, SEQ], BF16)
        CT = cx.sb("CT", [128, SEQ], BF16)
        yout = raw[:, 0:2, 3:3 + SEQ]
        car32 = cx.sb("car32", [128, 4, 64], F32)
        carb = cx.sb("carb", [128, 4, 64], BF16)
        Btok = [cx.sb(f"Btok{i}", [128, 128], BF16) for i in range(3)]
        CBs = [cx.sb(f"CBs{i}", [128, 128], F32) for i in range(3)]
        xq = [cx.sb(f"xq{i}", [128, 256], BF16) for i in range(3)]
        xqd = [cx.sb(f"xqd{i}", [128, 256], BF16) for i in range(3)]
        dm32 = [cx.sb(f"dm32{i}", [128, 128], F32) for i in range(2)]
        dmx = [cx.sb(f"dmx{i}", [128, 128], F32) for i in range(2)]
        MT = [[cx.sb(f"MT{i}{j}", [128, 128], BF16) for j in range(4)] for i in range(3)]
        ebc = [cx.sb(f"ebc{i}", [128, 128], F32) for i in range(2)]
        Cs = [[cx.sb(f"Cs{i}{j}", [128, 128], BF16) for j in range(4)] for i in range(3)]
        ytmp = [cx.sb(f"ytmp{i}", [128, 128], F32) for i in range(2)]
        ys_d = None if "y_tile" in io else dout("ysT", [128, 8, SEQ])
        b3, b4, b5 = banks[3], banks[4], banks[5]
        for gg in range(4):
            base = 512 + gg * 768
            bx = ws.add(w_d[:, base:base + 256].rearrange("(kt p) c -> p kt c", p=128), KT, 256)
            bbc = ws.add(w_d[:, base + 256:base + 512].rearrange("(kt p) c -> p kt c", p=128), KT, 256)
            bz = ws.add(w_d[:, base + 512:base + 768].rearrange("(kt p) c -> p kt c", p=128), KT, 256)

            def ev_raw(off):
                def f(ti, tb, ps, key):
                    cx.op("act", lambda e: e.activation(out=raw[:, off + ti, 3 + tb * 512:3 + (tb + 1) * 512], in_=ps[:], func=AF.Copy),
                          reads=[key], writes=["raw"])
                return f

            def ev_z(ti, tb, ps, key):
                cx.op("act", lambda e: e.activation(out=sz[:, ti, tb * 512:(tb + 1) * 512], in_=ps[:], func=AF.Silu),
                      reads=[key], writes=["sz"])
            ws.need(bx)
            proj(bx, 0, 2, ev_raw(0))
            ws.need(bbc)
            proj(bbc, 0, 2, ev_raw(2))
            ws.need(bz)
            proj(bz, 0, 2, ev_z)
            for ti in range(4):
                tidx = gg * 4 + ti
                cvt = cv[0]
                ck = "cv0"
                cx.op("dve", lambda e: e.tensor_scalar(out=cvt[:], in0=raw[:, ti, 0:SEQ], scalar1=cw[:, tidx, 0:1], scalar2=None,
                                                        op0=ALU.mult), reads=["raw", "conv_wT"], writes=[ck])
                for jj in range(1, 4):
                    cx.op("dve", lambda e, jj=jj: e.scalar_tensor_tensor(out=cvt[:], in0=raw[:, ti, jj:jj + SEQ],
                                                                       scalar=cw[:, tidx, jj:jj + 1], in1=cvt[:],
                                                                       op0=ALU.mult, op1=ALU.add),
                          reads=["raw", "conv_wT", ck], writes=[ck])
                if ti < 2:
                    cx.op("act", lambda e: e.activation(out=xs32[:, ti, :], in_=cvt[:], func=AF.Silu, bias=cb[:, tidx:tidx + 1], scale=1.0),
                          reads=[ck, "conv_bT"], writes=["xs32"])
                    cx.op("pool", lambda e: e.tensor_copy(out=xsb[:, ti, :], in_=xs32[:, ti, :]), reads=["xs32"], writes=["xsb"])
                else:
                    dst, dk = (BT, "BT") if ti == 2 else (CT, "CT")
                    cx.op("act", lambda e: e.activation(out=dst[:], in_=cvt[:], func=AF.Silu, bias=cb[:, tidx:tidx + 1], scale=1.0),
                          reads=[ck, "conv_bT"], writes=[dk])
            cx.op("dve", lambda e: e.memset(car32[:], 0.0), reads=["car32"], writes=["car32"])
            cx.op("dve", lambda e: e.memset(carb[:], 0.0), reads=["carb"], writes=["carb"])
            def partA1(c):
                cs_ = slice(c * 128, (c + 1) * 128)
                par = c % 3
                cx.op("pe", lambda e: e.transpose(bankT[:, 0:128], BT[:, cs_], identb[:]), reads=["BT", "identb"], writes=["bankT"])
                cx.op("pe", lambda e: e.transpose(bankT[:, 128:256], xsb[:, 0, cs_], identb[:]), reads=["xsb", "identb"], writes=["bankT"])
                cx.op("pe", lambda e: e.transpose(bankT[:, 256:384], xsb[:, 1, cs_], identb[:]), reads=["xsb", "identb"], writes=["bankT"])
                cx.op("act", lambda e: e.activation(out=Btok[par][:], in_=bankT[:, 0:128], func=AF.Copy),
                      reads=["bankT"], writes=[f"Btok{par}"])
                for hh in range(4):
                    h_ = gg * 4 + hh
                    cx.op("dve", lambda e: e.tensor_scalar(out=xq[par][:, hh * 64:(hh + 1) * 64], in0=bankT[:, 128 + hh * 64:192 + hh * 64],
                                                            scalar1=dt_all[:, c, h_:h_ + 1], scalar2=None, op0=ALU.mult),
                          reads=["bankT", "dt_all"], writes=[f"xq{par}"])
                    cx.op("pool", lambda e: e.tensor_scalar(out=xqd[par][:, hh * 64:(hh + 1) * 64], in0=xq[par][:, hh * 64:(hh + 1) * 64],
                                                             scalar1=dte[:, c, h_:h_ + 1], scalar2=None, op0=ALU.mult),
                          reads=[f"xq{par}", "dte"], writes=[f"xqd{par}"])
                cx.op("pe", lambda e: e.matmul(b3[:, 0:128], BT[:, cs_], CT[:, cs_], start=True, stop=True),
                      reads=["BT", "CT"], writes=["bank3"])
                cx.op("act", lambda e: e.activation(out=CBs[par][:], in_=b3[:, 0:128], func=AF.Copy), reads=["bank3"], writes=[f"CBs{par}"])

            def partA2(c):
                cs_ = slice(c * 128, (c + 1) * 128)
                par = c % 3
                for hh in range(4):
                    h_ = gg * 4 + hh
                    hp = hh % 2
                    crow = banks[hh % 2][:, 0:128]
                    cx.op("pool", lambda e: e.tensor_scalar(out=dg[hp][:], in0=ident[:], scalar1=cum[:, c, h_:h_ + 1], scalar2=None,
                                                             op0=ALU.mult), reads=["ident", "cum"], writes=[f"dg{hp}"])
                    cx.op("pe", lambda e: e.matmul(crow, ones[:], dg[hp][:], start=True, stop=True),
                          reads=["ones128", f"dg{hp}"], writes=[f"bank{hh % 2}"])
                    cx.op("dve", lambda e: e.scalar_tensor_tensor(out=dm32[hp][:], in0=crow, scalar=cum[:, c, h_:h_ + 1], in1=maskneg[:],
                                                                   op0=ALU.subtract, op1=ALU.add),
                          reads=[f"bank{hh % 2}", "cum", "maskneg"], writes=[f"dm32{hp}"])
                    cx.op("act", lambda e: e.activation(out=dmx[hp][:], in_=dm32[hp][:], func=AF.Exp), reads=[f"dm32{hp}"], writes=[f"dmx{hp}"])
                    cx.op("dve", lambda e: e.tensor_tensor(out=MT[par][hh][:], in0=dmx[hp][:], in1=CBs[par][:], op=ALU.mult),
                          reads=[f"dmx{hp}", f"CBs{par}"], writes=[f"MT{par}{hh}"])
                    cx.op("act", lambda e: e.activation(out=ebc[hp][:], in_=crow, func=AF.Exp), reads=[f"bank{hh % 2}"], writes=[f"ebc{hp}"])
                    cx.op("pool", lambda e: e.tensor_tensor(out=Cs[par][hh][:], in0=CT[:, cs_], in1=ebc[hp][:], op=ALU.mult),
                          reads=["CT", f"ebc{hp}"], writes=[f"Cs{par}{hh}"])

            def partB(c):
                cs_ = slice(c * 128, (c + 1) * 128)
                par = c % 3
                q2 = c % 2
                for hh in range(4):
                    pt, half = hh // 2, hh % 2
                    yo = banks[5 + q2][half * 64:(half + 1) * 64, pt * 128:(pt + 1) * 128]
                    cx.op("pe", lambda e: e.matmul(yo, xq[par][:, hh * 64:(hh + 1) * 64], MT[par][hh][:], start=True, stop=False),
                          reads=[f"xq{par}", f"MT{par}{hh}"], writes=[f"bank{5 + q2}"])
                    cx.op("pe", lambda e: e.matmul(yo, carb[:, hh, :], Cs[par][hh][:], start=False, stop=True),
                          reads=["carb", f"Cs{par}{hh}"], writes=[f"bank{5 + q2}"])
                cx.op("pe", lambda e: e.matmul(banks[2][:, 0:256], Btok[par][:], xqd[par][:], start=True, stop=True),
                      reads=[f"Btok{par}", f"xqd{par}"], writes=["bank2"])
                for hh in range(4):
                    h_ = gg * 4 + hh
                    cx.op("dve", lambda e: e.scalar_tensor_tensor(out=car32[:, hh, :], in0=car32[:, hh, :], scalar=dectot[:, c, h_:h_ + 1],
                                                                   in1=banks[2][:, hh * 64:(hh + 1) * 64], op0=ALU.mult, op1=ALU.add),
                          reads=["car32", "dectot", "bank2"], writes=["car32"])
                cx.op("pool", lambda e: e.tensor_copy(out=carb[:], in_=car32[:]), reads=["car32"], writes=["carb"])
                for pt in range(2):
                    cx.op("dve", lambda e: e.scalar_tensor_tensor(out=ytmp[pt][:], in0=xs32[:, pt, cs_], scalar=dsk[:, gg * 2 + pt:gg * 2 + pt + 1],
                                                                   in1=banks[5 + q2][:, pt * 128:(pt + 1) * 128],
                                                                   op0=ALU.mult, op1=ALU.add),
                          reads=["xs32", "ssd_dT", f"bank{5 + q2}"], writes=[f"ytmp{pt}"])
                    cx.op("pool", lambda e: e.tensor_tensor(out=yout[:, pt, cs_], in0=ytmp[pt][:], in1=sz[:, pt, cs_], op=ALU.mult),
                          reads=[f"ytmp{pt}", "sz"], writes=["raw"])

            for it_ in cx.record(partA1, 0):
                cx.play(it_)
            cx.play_interleaved(cx.record(partA2, 0), cx.record(partA1, 1))
            for c in range(NC_):
                la = cx.record(partB, c)
                lb = cx.record(partA2, c + 1) if c + 1 < NC_ else []
                lc = cx.record(partA1, c + 2) if c + 2 < NC_ else []
                cx.play_interleaved3(la, lb, lc)
            cx.dma("sp", st_slot if gg % 2 == 0 else st_slot2,
                   [((io["y_tile"](4 + gg * 2 + pt) if "y_tile" in io else ys_d[:, gg * 2 + pt, :]), yout[:, pt, :])
                    for pt in range(2)], reads=["raw"])
        cx.close_scope()


def prep_M0(inp, b, j, hT_full):
    m = {"hT": hT_full, "ident": np.eye(128, dtype=np.float32)}
    w = inp["hyb_w_in"][0]
    cols = [np.arange(j * 512, (j + 1) * 512)]
    for gg in range(4):
        G = j * 4 + gg
        cols.append(3072 + G * 256 + np.arange(256))
        cols.append(3072 + 2048 + G * 128 + np.arange(128))
        cols.append(3072 + 3072 + G * 128 + np.arange(128))
        cols.append(1024 + G * 256 + np.arange(256))
    cols.append(7168 + 16 * j + np.arange(16))
    cols = np.concatenate(cols)
    m["w_in_c"] = np.ascontiguousarray(w[:, cols])
    g0 = 32 * j
    lre = np.zeros((128, 16), np.float32)
    lim = np.zeros((128, 16), np.float32)
    ldt = np.zeros((128, 16), np.float32)
    xbre = np.zeros((128, 16, 128), np.float32)
    xbim = np.zeros((128, 16, 128), np.float32)
    cre = np.zeros((128, 16, 128), np.float32)
    cim = np.zeros((128, 16, 128), np.float32)
    for pr in range(16):
        pp = pr % 4
        for gi in range(2):
            g = g0 + 2 * pr + gi
            rows = slice(gi * 64, gi * 64 + 64)
            cs = slice(32 * pp + 16 * gi, 32 * pp + 16 * gi + 16)
            lre[rows, pr] = inp["s5_lambda_re"][0, g]
            lim[rows, pr] = inp["s5_lambda_im"][0, g]
            ldt[rows, pr] = inp["s5_log_dt"][0, g]
            xbre[rows, pr, cs] = inp["s5_b_re"][0, g]
            xbim[rows, pr, cs] = inp["s5_b_im"][0, g]
            cre[rows, pr, cs] = inp["s5_c_re"][0, g].T
            cim[rows, pr, cs] = inp["s5_c_im"][0, g].T
    m.update(s5_lre=lre, s5_lim=lim, s5_ldt=ldt, s5_xbre=xbre, s5_xbim=xbim, s5_cre=cre, s5_cim=cim)
    m["s5_dT"] = np.ascontiguousarray(inp["s5_d"][0, j * 512:(j + 1) * 512].reshape(4, 128).T)
    cwT = np.zeros((128, 16, 4), np.float32)
    cbT = np.zeros((128, 16), np.float32)
    dT = np.zeros((128, 8), np.float32)
    cwf, cbf = inp["ssd_conv_w"][0], inp["ssd_conv_b"][0]
    for gg in range(4):
        G = j * 4 + gg
        chans = [G * 256 + np.arange(128), G * 256 + 128 + np.arange(128),
                 2048 + G * 128 + np.arange(128), 3072 + G * 128 + np.arange(128)]
        for ti in range(4):
            cwT[:, gg * 4 + ti, :] = cwf[:, chans[ti]].T
            cbT[:, gg * 4 + ti] = cbf[chans[ti]]
        for pt in range(2):
            heads = (G * 256 + pt * 128 + np.arange(128)) // 64
            dT[:, gg * 2 + pt] = inp["ssd_d"][0][heads]
    hs = slice(16 * j, 16 * j + 16)
    m.update(conv_wT=cwT, conv_bT=cbT, ssd_dT=dT,
             dt_bias_bc=np.ascontiguousarray(np.broadcast_to(inp["ssd_dt_bias"][0, hs], (128, 16))),
             a_log_bc=np.ascontiguousarray(np.broadcast_to(inp["ssd_a_log"][0, hs], (128, 16))))
    tri = np.triu(np.ones((128, 128), np.float32))
    m["tri"] = tri
    m["ones128"] = np.ones((128, 128), np.float32)
    m["maskneg"] = np.where(np.arange(128)[None, :] >= np.arange(128)[:, None], 0.0, -30000.0).astype(np.float32)
    sel = np.zeros((16, 16, 128), np.float32)
    for h_ in range(16):
        sel[h_, h_, :] = 1.0
    m["sel"] = sel.reshape(16, 16 * 128)
    return m


RC = 64
LD_C = 0.6065306597126334
GN_EPS = 64e-5


def build_M1(G=None, io=None):
    nc, cx, dram, banks, bankT = _mk_env(G)
    io = io or {}
    cx.open_scope()

    def din(name, shape, dtype=F32):
        if name in io:
            return io[name]
        if name not in dram:
            dram[name] = nc.dram_tensor(name, list(shape), dtype, kind="ExternalInput").ap()
        return dram[name]

    def dout(name, shape, dtype=F32):
        if name in io:
            return io[name]
        dram[name] = nc.dram_tensor(name, list(shape), dtype, kind="ExternalOutput").ap()
        return dram[name]

    BF16S = "bf16_from_f32"

    def load(name, shape, dtype=F32, q="sp"):
        sdt, ddt = (BF16, F32) if dtype == BF16S else (dtype, dtype)
        t = cx.sb(name, shape, sdt)
        sl = cx.slot(name)
        pairs = cast_pairs(t[:], din(name, shape, ddt)) if dtype == BF16S else [(t[:], din(name, shape, ddt))]
        cx.dma(q, sl, pairs, writes=[name])
        return t

    hb = cx.sb("hbuf", [128, KT, SEQ + 1], BF16)
    cx.op("dve", lambda e: e.memset(hb[:, :, 0:1], 0.0), writes=["hb"])
    sl = cx.slot("hT")
    if "hT_pairs" in io:
        cx.dma("sp", sl, io["hT_pairs"](hb, 1), reads=["hb"] + io.get("dep", []), writes=["hb"])
    else:
        hT_d = din("hT", [128, KT, SEQ], BF16)
        cx.dma("sp", sl, [(hb[:, 4 * i:4 * i + 4, 1:SEQ + 1], hT_d[:, 4 * i:4 * i + 4, :]) for i in range(4)],
               reads=["hb"], writes=["hb"])
    ws = WStream(cx, nslot=2, elems=2048)
    ident = load("ident", [128, 128])
    identb = cx.sb("identb", [128, 128], BF16)
    cx.op("dve", lambda e: e.tensor_copy(out=identb[:], in_=ident[:]), reads=["ident"], writes=["identb"])
    mask3 = load("mask3", [128, 384])
    blockones = load("blockones", [128, 128])
    resetm = load("resetmask", [128, SEQ], BF16S, q="pool")
    muT = load("muT", [128, 6, KT])
    w0T = load("w0T", [128, 8])
    a0T = load("a0T", [128, 8])
    kkT = load("k_kT", [128, 8])
    kaT = load("k_aT", [128, 8])
    rkT = load("r_kT", [128, 8])
    lng = load("lng_stack", [128, 8, 64])
    lnb = load("lnb_stack", [128, 8, 64])
    w2c = load("w2c", [96, 1024], BF16S, q="pool")
    a2c = load("a2c", [96, 1024], BF16S, q="pool")
    g2c = load("g2c", [128, 2, 1024], BF16S, q="pool")
    onesb = cx.sb("onesb", [128, 2], BF16)
    cx.op("dve", lambda e: e.memset(onesb[:], 1.0), writes=["onesb"])
    epsg = cx.sb("epsg", [128, 1], F32)
    cx.op("dve", lambda e: e.memset(epsg[:], GN_EPS), writes=["epsg"])
    st_slots = [cx.slot("st0"), cx.slot("st1")]

    wder = [[cx.sb(f"wd{i}{j}", [128, KT, 128], BF16) for j in range(2)] for i in range(2)]
    nder = [0]

    def derive(blk, mu_i, ncol):
        i = nder[0] % 2
        nder[0] += 1
        wv = ws.view(blk)
        w1_, w2_ = wder[i][0], wder[i][1]
        for kt in range(KT):
            eng = "dve" if kt % 2 == 0 else "pool"
            cx.op(eng, lambda e, kt=kt: e.tensor_scalar(out=w2_[:, kt, 0:ncol], in0=wv[:, kt, :], scalar1=muT[:, mu_i, kt:kt + 1],
                                                        scalar2=None, op0=ALU.mult),
                  reads=[ws.key(blk), "muT"], writes=[f"wd{i}1"])
        cx.op("pool", lambda e: e.tensor_tensor(out=w1_[:, :, 0:ncol], in0=wv[:], in1=w2_[:, :, 0:ncol], op=ALU.subtract),
              reads=[ws.key(blk), f"wd{i}1"], writes=[f"wd{i}0"])
        return w1_, w2_, f"wd{i}0", f"wd{i}1"

    pj = [0]

    def proj2(der, ncol, evac):
        w1_, w2_, k1, k2 = der
        for tb in range(4):
            bk = pj[0] % 2
            pj[0] += 1
            for kt in range(KT):
                cx.op("pe", lambda e, kt=kt: e.matmul(banks[bk][0:ncol, :], w1_[:, kt, 0:ncol], hb[:, kt, 1 + tb * 512:1 + (tb + 1) * 512],
                                                       start=(kt == 0), stop=False),
                      reads=[k1, "hb"], writes=[f"bank{bk}"])
            for kt in range(KT):
                cx.op("pe", lambda e, kt=kt: e.matmul(banks[bk][0:ncol, :], w2_[:, kt, 0:ncol], hb[:, kt, tb * 512:(tb + 1) * 512],
                                                       start=False, stop=(kt == KT - 1)),
                      reads=[k2, "hb"], writes=[f"bank{bk}"])
            evac(tb, banks[bk], f"bank{bk}")

    def wblock(name, shape_cols, c0, ncol):
        src = din(name, [D, shape_cols])
        return ws.add(src[:, c0:c0 + ncol].rearrange("(kt p) c -> p kt c", p=128), KT, ncol)

    tw = cx.sb("tw", [96, SEQ], BF16)
    ta = cx.sb("ta", [96, SEQ], BF16)
    tg = cx.sb("tg", [128, 2, SEQ], BF16)
    b_w1 = wblock("w1", 96, 0, 96)
    b_a1 = wblock("a1", 96, 0, 96)
    b_g1 = [wblock("g1", 256, i * 128, 128) for i in range(2)]
    ws.need(b_w1)
    proj2(derive(b_w1, 1, 96), 96, lambda tb, ps, key: cx.op(
        "act", lambda e: e.activation(out=tw[:, tb * 512:(tb + 1) * 512], in_=ps[0:96, :], func=AF.Tanh), reads=[key], writes=["tw"]))
    ws.need(b_a1)
    proj2(derive(b_a1, 4, 96), 96, lambda tb, ps, key: cx.op(
        "act", lambda e: e.activation(out=ta[:, tb * 512:(tb + 1) * 512], in_=ps[0:96, :], func=AF.Copy), reads=[key], writes=["ta"]))
    for i in range(2):
        ws.need(b_g1[i])
        proj2(derive(b_g1[i], 5, 128), 128, lambda tb, ps, key, i=i: cx.op(
            "act", lambda e: e.activation(out=tg[:, i, tb * 512:(tb + 1) * 512], in_=ps[:], func=AF.Sigmoid), reads=[key], writes=["tg"]))

    r_bf = cx.sb("r_bf", [128, SEQ], BF16)
    k32 = cx.sb("k32", [128, SEQ], F32)
    v_bf = cx.sb("v_bf", [128, SEQ], BF16)
    a32 = cx.sb("a32", [128, SEQ], F32)
    kk32 = cx.sb("kk32", [128, SEQ], F32)
    ld32 = cx.sb("ld32", [128, SEQ], F32)
    cl32 = cx.sb("cl32", [128, SEQ], F32)
    ecl = cx.sb("ecl", [128, SEQ], F32)
    g_bf = cx.sb("g_bf", [128, SEQ], BF16)
    yg = cx.sb("yg", [128, SEQ], BF16)
    sqt = [cx.sb("sqt0", [128, 512], F32)] * 2
    Ear = [cx.sb(f"Ear{i}", [128, 256], BF16) for i in range(3)]
    Eb = [cx.sb(f"Eb{i}", [128, 128], BF16) for i in range(3)]
    Ek = [cx.sb(f"Ek{i}", [128, 128], BF16) for i in range(3)]
    Ez = [cx.sb(f"Ez{i}", [128, 128], BF16) for i in range(3)]
    for i in range(3):
        for t_, k_ in ((Ear[i], f"Ear{i}"), (Eb[i], f"Eb{i}"), (Ek[i], f"Ek{i}"), (Ez[i], f"Ez{i}")):
            cx.op("pool", lambda e, t_=t_: e.memset(t_[:], 0.0), writes=[k_])
    EbkT = [cx.sb(f"EbkT{i}", [128, 256], BF16) for i in range(3)]
    Pm = [[cx.sb(f"Pm{q}{i}", [128, 128], F32) for i in range(2)] for q in range(2)]
    PTm = [[cx.sb(f"PTm{q}{i}", [128, 128], F32) for i in range(2)] for q in range(2)]
    Rm = [cx.sb(f"Rm{q}", [128, 128], F32) for q in range(2)]
    Rb = [cx.sb(f"Rb{i}", [128, 128], BF16) for i in range(3)]
    Arb = [cx.sb(f"Arb{i}", [128, 128], BF16) for i in range(3)]
    Aak_rk = [cx.sb(f"Aakrk{i}", [128, 256], BF16) for i in range(3)]
    Vs = [cx.sb(f"Vs{i}", [128, 64], BF16) for i in range(3)]
    Xb = cx.sb("Xb", [128, 64], BF16)
    Ub = cx.sb("Ub", [128, 64], BF16)
    S32 = cx.sb("S32", [128, 64], F32)
    S0b = cx.sb("S0b", [128, 64], BF16)
    ys = cx.sb("ys", [128, 64], F32)
    ysq = cx.sb("ysq", [128, 64], F32)
    yn = cx.sb("yn", [128, 64], F32)
    yob = cx.sb("yob", [128, 64], BF16)
    stat = cx.sb("stat", [128, 8], F32)
    bon = cx.sb("bon", [128, 2], F32)
    yg_d = None if "yg_tile" in io else dout("ygT", [128, 8, SEQ], BF16)

    for P in range(8):
        c0 = P * 128
        b_r = wblock("wr_c", 1024, c0, 128)
        b_k = wblock("wk_c", 1024, c0, 128)
        b_v = wblock("wv_c", 1024, c0, 128)
        ws.need(b_r)
        proj2(derive(b_r, 0, 128), 128, lambda tb, ps, key: cx.op(
            "act", lambda e: e.activation(out=r_bf[:, tb * 512:(tb + 1) * 512], in_=ps[:], func=AF.Copy), reads=[key], writes=["r_bf"]))
        ws.need(b_k)
        proj2(derive(b_k, 2, 128), 128, lambda tb, ps, key: cx.op(
            "act", lambda e: e.activation(out=k32[:, tb * 512:(tb + 1) * 512], in_=ps[:], func=AF.Copy), reads=[key], writes=["k32"]))
        ws.need(b_v)
        proj2(derive(b_v, 3, 128), 128, lambda tb, ps, key: cx.op(
            "act", lambda e: e.activation(out=v_bf[:, tb * 512:(tb + 1) * 512], in_=ps[:], func=AF.Copy), reads=[key], writes=["v_bf"]))
        for tb in range(4):
            ts_ = slice(tb * 512, (tb + 1) * 512)
            bk = pj[0] % 2
            pj[0] += 1
            cx.op("pe", lambda e: e.matmul(banks[bk][:], w2c[:, c0:c0 + 128], tw[:, ts_], start=True, stop=True),
                  reads=["w2c", "tw"], writes=[f"bank{bk}"])
            cx.op("act", lambda e: e.activation(out=ld32[:, ts_], in_=banks[bk][:], func=AF.Sigmoid, bias=w0T[:, P:P + 1], scale=1.0),
                  reads=[f"bank{bk}", "w0T"], writes=["ld32"])
            bk = pj[0] % 2
            pj[0] += 1
            cx.op("pe", lambda e: e.matmul(banks[bk][:], a2c[:, c0:c0 + 128], ta[:, ts_], start=True, stop=True),
                  reads=["a2c", "ta"], writes=[f"bank{bk}"])
            cx.op("act", lambda e: e.activation(out=a32[:, ts_], in_=banks[bk][:], func=AF.Sigmoid, bias=a0T[:, P:P + 1], scale=1.0),
                  reads=[f"bank{bk}", "a0T"], writes=["a32"])
            bk = pj[0] % 2
            pj[0] += 1
            for i in range(2):
                cx.op("pe", lambda e, i=i: e.matmul(banks[bk][:], g2c[:, i, c0:c0 + 128], tg[:, i, ts_], start=(i == 0), stop=(i == 1)),
                      reads=["g2c", "tg"], writes=[f"bank{bk}"])
            cx.op("act", lambda e: e.activation(out=g_bf[:, ts_], in_=banks[bk][:], func=AF.Copy), reads=[f"bank{bk}"], writes=["g_bf"])
        cx.op("dve", lambda e: e.tensor_scalar(out=ld32[:], in0=ld32[:], scalar1=-LD_C, scalar2=None, op0=ALU.mult),
              reads=["ld32"], writes=["ld32"])
        cx.op("dve", lambda e: e.tensor_scalar(out=kk32[:], in0=k32[:], scalar1=kkT[:, P:P + 1], scalar2=None, op0=ALU.mult),
              reads=["k32", "k_kT"], writes=["kk32"])
        for tb in range(4):
            ts_ = slice(tb * 512, (tb + 1) * 512)
            sq_, sk = sqt[0], "sqt0"
            cx.op("act", lambda e: e.activation(out=sq_[:], in_=kk32[:, ts_], func=AF.Square), reads=["kk32"], writes=[sk])
            bk = pj[0] % 2
            pj[0] += 1
            cx.op("pe", lambda e: e.matmul(banks[bk][:], blockones[:], sq_[:], start=True, stop=True),
                  reads=["blockones", sk], writes=[f"bank{bk}"])
            cx.op("act", lambda e: e.activation(out=sq_[:], in_=banks[bk][:], func=AF.Sqrt), reads=[f"bank{bk}"], writes=[sk])
            cx.op("dve", lambda e: e.tensor_scalar(out=sq_[:], in0=sq_[:], scalar1=1e-12, scalar2=None, op0=ALU.max), reads=[sk], writes=[sk])
            cx.op("dve", lambda e: e.reciprocal(out=sq_[:], in_=sq_[:]), reads=[sk], writes=[sk])
            cx.op("dve", lambda e: e.tensor_tensor(out=kk32[:, ts_], in0=kk32[:, ts_], in1=sq_[:], op=ALU.mult),
                  reads=["kk32", sk], writes=["kk32"])
        cx.op("dve", lambda e: e.tensor_scalar(out=ecl[:], in0=a32[:], scalar1=-1.0, scalar2=kaT[:, P:P + 1], op0=ALU.add, op1=ALU.mult),
              reads=["a32", "k_aT"], writes=["ecl"])
        cx.op("dve", lambda e: e.scalar_tensor_tensor(out=k32[:], in0=ecl[:], scalar=1.0, in1=k32[:], op0=ALU.add, op1=ALU.mult),
              reads=["ecl", "k32"], writes=["k32"])
        cx.op("pool", lambda e: e.tensor_tensor(out=a32[:], in0=a32[:], in1=kk32[:], op=ALU.mult), reads=["a32", "kk32"], writes=["a32"])
        cx.op("dve", lambda e: e.tensor_tensor_scan(out=cl32[:], data0=resetm[:], data1=ld32[:], initial=0.0, op0=ALU.mult, op1=ALU.add),
              reads=["resetmask", "ld32"], writes=["cl32"])
        cx.op("pool", lambda e: e.tensor_tensor(out=ld32[:], in0=cl32[:], in1=ld32[:], op=ALU.subtract), reads=["cl32", "ld32"], writes=["ld32"])
        cx.op("act", lambda e: e.activation(out=ld32[:], in_=ld32[:], func=AF.Exp), reads=["ld32"], writes=["ld32"])
        cx.op("act", lambda e: e.activation(out=ecl[:], in_=cl32[:], func=AF.Exp), reads=["cl32", "ecl"], writes=["ecl"])
        cx.op("act", lambda e: e.activation(out=cl32[:], in_=cl32[:], func=AF.Exp, scale=-1.0), reads=["cl32"], writes=["cl32"])
        eclm, encl, beta, kfin = ld32, cl32, a32, k32
        cx.op("dve", lambda e: e.memset(S32[:], 0.0), reads=["S32"], writes=["S32"])
        cx.op("dve", lambda e: e.memset(S0b[:], 0.0), reads=["S0b"], writes=["S0b"])
        def part1a(c):
            cs = slice(c * RC, (c + 1) * RC)
            par = c % 3
            q2 = c % 2
            for hd in range(2):
                R_ = slice(hd * 64, hd * 64 + 64)
                e1, e2 = ("dve", "pool") if hd == 0 else ("pool", "dve")
                cx.op("dve", lambda e: e.scalar_tensor_tensor(out=Ear[par][R_, hd * 64:hd * 64 + 64], in0=kk32[R_, cs], scalar=-1.0, in1=eclm[R_, cs],
                                                               op0=ALU.mult, op1=ALU.mult),
                      reads=["kk32", "ld32"], writes=[f"Ear{par}"])
                cx.op(e2, lambda e: e.tensor_tensor(out=Ear[par][R_, 128 + hd * 64:128 + hd * 64 + 64], in0=r_bf[R_, cs], in1=ecl[R_, cs], op=ALU.mult),
                      reads=["r_bf", "ecl"], writes=[f"Ear{par}"])
                cx.op(e1, lambda e: e.tensor_tensor(out=Eb[par][R_, hd * 64:hd * 64 + 64], in0=beta[R_, cs], in1=encl[R_, cs], op=ALU.mult),
                      reads=["a32", "cl32"], writes=[f"Eb{par}"])
                cx.op(e2, lambda e: e.tensor_tensor(out=Ek[par][R_, hd * 64:hd * 64 + 64], in0=kfin[R_, cs], in1=encl[R_, cs], op=ALU.mult),
                      reads=["k32", "cl32"], writes=[f"Ek{par}"])
                cx.op("dve", lambda e: e.scalar_tensor_tensor(out=Ez[par][R_, hd * 64:hd * 64 + 64], in0=kfin[R_, cs], scalar=rkT[R_, P:P + 1],
                                                               in1=r_bf[R_, cs], op0=ALU.mult, op1=ALU.mult),
                      reads=["k32", "r_kT", "r_bf"], writes=[f"Ez{par}"])
                cx.op("pe", lambda e: e.transpose(bankT[R_, 0:64], v_bf[R_, cs], identb[R_, hd * 64:hd * 64 + 64]),
                      reads=["v_bf", "identb"], writes=["bankT"])
            cx.op("act", lambda e: e.activation(out=Vs[par][:], in_=bankT[:, 0:64], func=AF.Copy), reads=["bankT"], writes=[f"Vs{par}"])
            cx.op("pe", lambda e: e.matmul(banks[2][:, 0:256], Eb[par][:], Ear[par][:], start=True, stop=True),
                  reads=[f"Eb{par}", f"Ear{par}"], writes=["bank2"])
            cx.op("pe", lambda e: e.matmul(banks[3][:, 0:256], Ek[par][:], Ear[par][:], start=True, stop=True),
                  reads=[f"Ek{par}", f"Ear{par}"], writes=["bank3"])
            cx.op("pe", lambda e: e.matmul(banks[2][:, 256:384], Ear[par][:, 0:128], Eb[par][:], start=True, stop=True),
                  reads=[f"Eb{par}", f"Ear{par}"], writes=["bank2"])
            cx.op("pe", lambda e: e.transpose(bankT[:, 128:256], Eb[par][:], identb[:]), reads=[f"Eb{par}", "identb"], writes=["bankT"])
            cx.op("pe", lambda e: e.transpose(bankT[:, 256:384], Ek[par][:], identb[:]), reads=[f"Ek{par}", "identb"], writes=["bankT"])
            cx.op("dve", lambda e: e.tensor_tensor(out=Pm[q2][0][:], in0=banks[2][:, 0:128], in1=mask3[:, 0:128], op=ALU.mult),
                  reads=["bank2", "mask3"], writes=[f"Pm{q2}0"])
            cx.op("dve", lambda e: e.tensor_tensor(out=Arb[par][:], in0=banks[2][:, 128:256], in1=mask3[:, 128:256], op=ALU.mult),
                  reads=["bank2", "mask3"], writes=[f"Arb{par}"])
            cx.op("dve", lambda e: e.tensor_tensor(out=Aak_rk[par][:], in0=banks[3][:, 0:256], in1=mask3[:, 0:256], op=ALU.mult),
                  reads=["bank3", "mask3"], writes=[f"Aakrk{par}"])
            cx.op("dve", lambda e: e.tensor_tensor(out=PTm[q2][0][:], in0=banks[2][:, 256:384], in1=mask3[:, 256:384], op=ALU.mult),
                  reads=["bank2", "mask3"], writes=[f"PTm{q2}0"])
            cx.op("act", lambda e: e.activation(out=EbkT[par][:], in_=bankT[:, 128:384], func=AF.Copy), reads=["bankT"], writes=[f"EbkT{par}"])
            cx.op("pool", lambda e: e.tensor_tensor(out=Rm[q2][:], in0=Pm[q2][0][:], in1=ident[:], op=ALU.add), reads=[f"Pm{q2}0", "ident"], writes=[f"Rm{q2}"])
        def part1b(c):
            par = c % 3
            q2 = c % 2
            cur = 0
            for lvl in range(1, 6):
                nxt = 1 - cur
                if lvl < 5:
                    cx.op("pe", lambda e: e.matmul(banks[5][:, 0:128], PTm[q2][cur][:], Pm[q2][cur][:], start=True, stop=True),
                          reads=[f"PTm{q2}{cur}", f"Pm{q2}{cur}"], writes=["bank5"])
                cx.op("pe", lambda e: e.matmul(banks[6][:, 0:128], Pm[q2][cur][:], PTm[q2][cur][:], start=True, stop=True),
                      reads=[f"PTm{q2}{cur}", f"Pm{q2}{cur}"], writes=["bank6"])
                if lvl < 5:
                    cx.op("act", lambda e: e.activation(out=Pm[q2][nxt][:], in_=banks[5][:, 0:128], func=AF.Copy),
                          reads=["bank5"], writes=[f"Pm{q2}{nxt}"])
                cx.op("dve", lambda e: e.tensor_copy(out=PTm[q2][nxt][:], in_=banks[6][:, 0:128]), reads=["bank6"], writes=[f"PTm{q2}{nxt}"])
                cx.op("pe", lambda e: e.matmul(banks[4][:, 0:128], PTm[q2][nxt][:], Rm[q2][:], start=True, stop=True),
                      reads=[f"PTm{q2}{nxt}", f"Rm{q2}"], writes=["bank4"])
                cx.op("dve", lambda e: e.tensor_tensor(out=Rm[q2][:], in0=Rm[q2][:], in1=banks[4][:, 0:128], op=ALU.add),
                      reads=[f"Rm{q2}", "bank4"], writes=[f"Rm{q2}"])
                cur = nxt
            cx.op("act", lambda e: e.activation(out=Rb[par][:], in_=Rm[q2][:], func=AF.Copy), reads=[f"Rm{q2}"], writes=[f"Rb{par}"])

        def part2(c):
            cs = slice(c * RC, (c + 1) * RC)
            par = c % 3
            cx.op("pe", lambda e: e.matmul(banks[0][:, 0:64], Ear[par][:, 0:128], S0b[:], start=True, stop=False),
                  reads=[f"Ear{par}", "S0b"], writes=["bank0"])
            cx.op("pe", lambda e: e.matmul(banks[0][:, 0:64], Aak_rk[par][:, 0:128], Vs[par][:], start=False, stop=True),
                  reads=[f"Aakrk{par}", f"Vs{par}"], writes=["bank0"])
            cx.op("act", lambda e: e.activation(out=Xb[:], in_=banks[0][:, 0:64], func=AF.Copy), reads=["bank0"], writes=["Xb"])
            cx.op("pe", lambda e: e.matmul(banks[1][:, 0:64], Rb[par][:], Xb[:], start=True, stop=True), reads=[f"Rb{par}", "Xb"], writes=["bank1"])
            cx.op("act", lambda e: e.activation(out=Ub[:], in_=banks[1][:, 0:64], func=AF.Copy), reads=["bank1"], writes=["Ub"])
            cx.op("pe", lambda e: e.matmul(banks[0][:, 0:64], Ear[par][:, 128:256], S0b[:], start=True, stop=False),
                  reads=[f"Ear{par}", "S0b"], writes=["bank0"])
            cx.op("pe", lambda e: e.matmul(banks[0][:, 0:64], Arb[par][:], Ub[:], start=False, stop=False), reads=[f"Arb{par}", "Ub"], writes=["bank0"])
            cx.op("pe", lambda e: e.matmul(banks[0][:, 0:64], Aak_rk[par][:, 128:256], Vs[par][:], start=False, stop=True),
                  reads=[f"Aakrk{par}", f"Vs{par}"], writes=["bank0"])
            cx.op("pe", lambda e: e.matmul(banks[0][:, 64:66], Ez[par][:], onesb[:], start=True, stop=True),
                  reads=[f"Ez{par}", "onesb"], writes=["bank0"])
            cx.op("pe", lambda e: e.matmul(banks[1][:, 0:64], EbkT[par][:, 0:128], Ub[:], start=True, stop=False), reads=[f"EbkT{par}", "Ub"], writes=["bank1"])
            cx.op("pe", lambda e: e.matmul(banks[1][:, 0:64], EbkT[par][:, 128:256], Vs[par][:], start=False, stop=True), reads=[f"EbkT{par}", f"Vs{par}"], writes=["bank1"])
            cx.op("dve", lambda e: e.tensor_tensor(out=S32[:], in0=S32[:], in1=banks[1][:, 0:64], op=ALU.add), reads=["S32", "bank1"], writes=["S32"])
            wc = ecl[:, c * RC + RC - 1:c * RC + RC]
            cx.op("dve", lambda e: e.tensor_scalar(out=S32[:], in0=S32[:], scalar1=wc, scalar2=None, op0=ALU.mult),
                  reads=["S32", "ecl"], writes=["S32"])
            cx.op("pool", lambda e: e.tensor_copy(out=S0b[:], in_=S32[:]), reads=["S32"], writes=["S0b"])
            cx.op("act", lambda e: e.activation(out=ys[:], in_=banks[0][:, 0:64], func=AF.Copy, accum_out=stat[:, 0:1]),
                  reads=["bank0"], writes=["ys", "stat"])
            cx.op("act", lambda e: e.activation(out=ysq[:], in_=ys[:], func=AF.Square, accum_out=stat[:, 1:2]),
                  reads=["ys"], writes=["ysq", "stat"])
            cx.op("dve", lambda e: e.tensor_scalar(out=stat[:, 2:3], in0=stat[:, 0:1], scalar1=1.0 / 64, scalar2=None, op0=ALU.mult),
                  reads=["stat"], writes=["stat"])
            cx.op("dve", lambda e: e.tensor_tensor(out=stat[:, 3:4], in0=stat[:, 2:3], in1=stat[:, 2:3], op=ALU.mult),
                  reads=["stat"], writes=["stat"])
            cx.op("dve", lambda e: e.scalar_tensor_tensor(out=stat[:, 4:5], in0=stat[:, 1:2], scalar=1.0 / 64, in1=stat[:, 3:4],
                                                           op0=ALU.mult, op1=ALU.subtract), reads=["stat"], writes=["stat"])
            cx.op("act", lambda e: e.activation(out=stat[:, 5:6], in_=stat[:, 4:5], func=AF.Sqrt, bias=epsg[:], scale=1.0),
                  reads=["stat", "epsg"], writes=["stat"])
            cx.op("dve", lambda e: e.reciprocal(out=stat[:, 5:6], in_=stat[:, 5:6]), reads=["stat"], writes=["stat"])
            cx.op("dve", lambda e: e.tensor_scalar(out=yn[:], in0=ys[:], scalar1=stat[:, 2:3], scalar2=stat[:, 5:6],
                                                    op0=ALU.subtract, op1=ALU.mult), reads=["ys", "stat"], writes=["yn"])
            cx.op("pool", lambda e: e.tensor_tensor(out=yn[:], in0=yn[:], in1=lng[:, P, :], op=ALU.mult), reads=["yn", "lng_stack"], writes=["yn"])
            cx.op("pool", lambda e: e.tensor_tensor(out=yn[:], in0=yn[:], in1=lnb[:, P, :], op=ALU.add), reads=["yn", "lnb_stack"], writes=["yn"])
            cx.op("act", lambda e: e.activation(out=bon[:], in_=banks[0][:, 64:66], func=AF.Copy), reads=["bank0"], writes=["bon"])
            cx.op("dve", lambda e: e.scalar_tensor_tensor(out=yob[:], in0=Vs[par][:], scalar=bon[:, 0:1], in1=yn[:], op0=ALU.mult, op1=ALU.add),
                  reads=[f"Vs{par}", "bon", "yn"], writes=["yob"])
            for hd in range(2):
                R_ = slice(hd * 64, hd * 64 + 64)
                cx.op("pe", lambda e: e.transpose(bankT[R_, 512:576], yob[R_, :], identb[R_, hd * 64:hd * 64 + 64]),
                      reads=["yob", "identb"], writes=["bankT"])
            cx.op("dve", lambda e: e.tensor_tensor(out=yg[:, cs], in0=bankT[:, 512:576], in1=g_bf[:, cs], op=ALU.mult),
                  reads=["bankT", "g_bf"], writes=["yg"])
        NCH_ = SEQ // RC
        for it in cx.record(part1a, 0):
            cx.play(it)
        cx.play_interleaved(cx.record(part1b, 0), cx.record(part1a, 1))
        for c in range(NCH_):
            la = cx.record(part2, c)
            lb = cx.record(part1b, c + 1) if c + 1 < NCH_ else []
            lc = cx.record(part1a, c + 2) if c + 2 < NCH_ else []
            cx.play_interleaved3(la, lb, lc)
        cx.dma("sp", st_slots[P % 2], [(io["yg_tile"](P) if "yg_tile" in io else yg_d[:, P, :], yg[:])], reads=["yg"])
    cx.close_scope()
    cx.wait_all("sp")
    return nc


def prep_M1(inp, b, j, hT_full):
    m = {"hT": hT_full, "ident": np.eye(128, dtype=np.float32)}
    cs = slice(j * 1024, (j + 1) * 1024)
    m["wr_c"] = np.ascontiguousarray(inp["rwkv_w_r"][0][:, cs])
    m["wk_c"] = np.ascontiguousarray(inp["rwkv_w_k"][0][:, cs])
    m["wv_c"] = np.ascontiguousarray(inp["rwkv_w_v"][0][:, cs])
    m["w1"] = inp["rwkv_w1"][0]
    m["a1"] = inp["rwkv_a1"][0]
    m["g1"] = inp["rwkv_g1"][0]
    m["w2c"] = np.ascontiguousarray(inp["rwkv_w2"][0][:, cs])
    m["a2c"] = np.ascontiguousarray(inp["rwkv_a2"][0][:, cs])
    m["g2c"] = np.ascontiguousarray(inp["rwkv_g2"][0][:, cs].reshape(2, 128, 1024).transpose(1, 0, 2))
    m["muT"] = np.ascontiguousarray(inp["rwkv_mu"][0].reshape(6, KT, 128).transpose(2, 0, 1))
    for nm, key in (("w0T", "rwkv_w0"), ("a0T", "rwkv_a0"), ("k_kT", "rwkv_k_k"), ("k_aT", "rwkv_k_a")):
        m[nm] = np.ascontiguousarray(inp[key][0][cs].reshape(8, 128).T)
    m["r_kT"] = np.ascontiguousarray(inp["rwkv_r_k"][0].reshape(-1)[cs].reshape(8, 128).T)
    lg = inp["rwkv_ln_g"][0][cs].reshape(8, 2, 64)
    lb = inp["rwkv_ln_b"][0][cs].reshape(8, 2, 64)
    lng = np.zeros((128, 8, 64), np.float32)
    lnb = np.zeros((128, 8, 64), np.float32)
    for hd in range(2):
        lng[hd * 64:(hd + 1) * 64] = lg[None, :, hd, :]
        lnb[hd * 64:(hd + 1) * 64] = lb[None, :, hd, :]
    m["lng_stack"], m["lnb_stack"] = lng, lnb
    s_ = np.arange(64)
    blk = np.kron(np.eye(2, dtype=np.float32), np.ones((64, 64), np.float32))
    mS = np.kron(np.eye(2, dtype=np.float32), (s_[:, None] < s_[None, :]).astype(np.float32))
    mI = np.kron(np.eye(2, dtype=np.float32), (s_[:, None] <= s_[None, :]).astype(np.float32))
    m["mask3"] = np.ascontiguousarray(np.concatenate([mS, mI, mS.T], axis=1))
    m["blockones"] = blk
    rm = np.ones((128, SEQ), np.float32)
    rm[:, ::RC] = 0.0
    m["resetmask"] = rm
    return m


def _T_maps(inp, stages, xT, extra):
    maps = []
    for core in range(NCORES):
        b = core // 2
        m = {"xT": xT[core], "cT": fm(inp["c"][b])}
        for stg in stages:
            kind = stg["kind"]
            if kind == "ffn":
                l, s_ = stg["l"], stg["s"]
                fi = 0 if s_ == 0 else 1
                m[f"w_mod{l}"] = inp["w_mod"][l]
                m[f"b_modT{l}"] = fm(inp["b_mod"][l])
                m[f"norm_gT{l}{s_}"] = fm(inp["norm_g"][l, s_])
                m[f"ffn_w1_{l}{fi}"] = inp["ffn_w1"][l, fi]
                m[f"ffn_w3_{l}{fi}"] = inp["ffn_w3"][l, fi]
                m[f"ffn_w2_{l}{fi}"] = inp["ffn_w2"][l, fi]
            elif kind == "h_out":
                l = stg["l"]
                m[f"w_mod{l}"] = inp["w_mod"][l]
                m[f"b_modT{l}"] = fm(inp["b_mod"][l])
                m[f"norm_gT{l}1"] = fm(inp["norm_g"][l, 1])
            elif kind == "mix0_post":
                m["w_mod0"] = inp["w_mod"][0]
                m["b_modT0"] = fm(inp["b_mod"][0])
                m["glu_bT"] = fm(inp["s5_glu_b"][0])
                m["ssd_norm_gT"] = fm(inp["ssd_norm_g"][0])
                m["s5_glu_w"] = inp["s5_glu_w"][0]
                m["hyb_w_out"] = inp["hyb_w_out"][0]
            elif kind == "rwkv_post":
                m["w_mod1"] = inp["w_mod"][1]
                m["b_modT1"] = fm(inp["b_mod"][1])
                m["rwkv_w_o"] = inp["rwkv_w_o"][0]
            elif kind == "final":
                m["final_gT"] = fm(inp["final_g"])
        m.update(extra[core])
        maps.append(m)
    return maps


def _run(nc, maps):
    return run_bass_kernel_spmd(nc, maps, core_ids=list(range(NCORES))).results


def _only_declared(maps):
    keep = set(LAST_DRAM.keys())
    return [{k: v for k, v in m.items() if k in keep} for m in maps]


def _pair_cat_tokens(tiles, b):
    return np.ascontiguousarray(np.concatenate([tiles[2 * b], tiles[2 * b + 1]], axis=2))


GROUPS = [[0, 1], [2, 3], [4, 5], [6, 7]]
ST0 = [{"kind": "ffn", "l": 0, "s": 0}, {"kind": "h_out", "l": 0}, {"kind": "x_out"}]
ST1 = [{"kind": "mix0_post"}, {"kind": "ffn", "l": 0, "s": 2}, {"kind": "ffn", "l": 1, "s": 0},
       {"kind": "h_out", "l": 1}, {"kind": "x_out"}]
ST2 = [{"kind": "rwkv_post"}, {"kind": "ffn", "l": 1, "s": 2}, {"kind": "final"}]


def build_fused(upto=None):
    nc = bass.Bass("TRN2", target_bir_lowering=False)
    cx = Ctx(nc)
    banks = [cx.ps(f"bank{i}") for i in range(7)]
    cx.uid += 1
    bankT = nc.alloc_psum_tensor(f"bankT_{cx.uid}", [128, 1024], BF16)
    G = {"nc": nc, "cx": cx, "dram": {}, "banks": banks, "bankT": bankT}

    def idram(name, shape, dt):
        return nc.dram_tensor(name, list(shape), dt).ap()

    ncc = [0]

    def allgather(src, dst):
        sl = cx.slot("cc")
        nc.gpsimd.collective_compute("AllGather", ALU.bypass, replica_groups=GROUPS,
                                     ins=[src.opt()], outs=[dst.opt()]).then_inc(sl["sem"])
        sl["count"] += 1
        ncc[0] += 1
        cx._record((sl["sem"], sl["count"], sl["key"]), [], [f"cc{ncc[0]}"])
        return f"cc{ncc[0]}"

    CH = 4096

    def chunks(name, ncol, dt):
        n = ncol // CH
        return ([idram(f"{name}_s{i}", [128, CH], dt) for i in range(n)],
                [idram(f"{name}_g{i}", [256, CH], dt) for i in range(n)])

    def gather_all(snd, rcv):
        return [allgather(a, b) for a, b in zip(snd, rcv)]

    def h_out_pairs(snd):
        return lambda h: [(snd[c].rearrange("p (k t) -> p k t", k=4), h[:, 4 * c:4 * c + 4, :]) for c in range(4)]

    def h_loader(rcv):
        def f(tile, off):
            pairs = []
            for r in range(2):
                for c in range(4):
                    src = rcv[c][r * 128:(r + 1) * 128, :].rearrange("p (k t) -> p k t", k=4)
                    pairs.append((tile[:, 4 * c:4 * c + 4, off + r * TOK:off + (r + 1) * TOK], src))
            return pairs
        return f

    def tile_fn(snd):
        return lambda idx: snd[idx // 2][:, (idx % 2) * SEQ:(idx % 2 + 1) * SEQ]

    def gath_fn(rcv):
        return lambda r, lt, half: rcv[lt // 2][r * 128:(r + 1) * 128, (lt % 2) * SEQ + half * TOK:(lt % 2) * SEQ + (half + 1) * TOK]

    xs1 = idram("xs1", [128, KT, TOK], F32)
    xs2 = idram("xs2", [128, KT, TOK], F32)
    h0s, h0g = chunks("h0", KT * TOK, BF16)
    h1s, h1g = chunks("h1", KT * TOK, BF16)
    y0s, y0g = chunks("y0", 12 * SEQ, F32)
    y1s, y1g = chunks("y1", 8 * SEQ, BF16)

    def dbg(n, src, shape, dt):
        if upto != n:
            return False
        cx.barrier()
        o = nc.dram_tensor("dbg", list(shape), dt, kind="ExternalOutput").ap()
        cx.dma("sp", cx.slot("dbg"), [(o, src)])
        cx.wait_all("sp")
        return True

    global LAST_DRAM
    LAST_DRAM = G["dram"]
    build_T(ST0, G, io={"hT_out_pairs": h_out_pairs(h0s), "xT_out": xs1})
    if dbg(1, xs1, [128, KT, TOK], F32):
        return nc
    cx.new_phase()
    dep = gather_all(h0s, h0g)
    if dbg(2, h0g[3], [256, CH], BF16):
        return nc
    build_M0(G=G, io={"hT_pairs": h_loader(h0g), "y_tile": tile_fn(y0s), "dep": dep})
    if dbg(3, y0s[0], [128, CH], F32):
        return nc
    cx.new_phase()
    dep = gather_all(y0s, y0g)
    if dbg(4, y0g[5], [256, CH], F32):
        return nc
    build_T(ST1, G, io={"xT": xs1, "y_gath": gath_fn(y0g), "hT_out_pairs": h_out_pairs(h1s), "xT_out": xs2, "dep": dep})
    if dbg(5, xs2, [128, KT, TOK], F32):
        return nc
    cx.new_phase()
    dep = gather_all(h1s, h1g)
    build_M1(G=G, io={"hT_pairs": h_loader(h1g), "yg_tile": tile_fn(y1s), "dep": dep})
    if dbg(6, y1s[0], [128, CH], BF16):
        return nc
    cx.new_phase()
    dep = gather_all(y1s, y1g)
    build_T(ST2, G, io={"xT": xs2, "yg_gath": gath_fn(y1g), "dep": dep})
    cx.wait_all("sp")
    return nc


LAST_DRAM = {}


def kernel_unfused(**inputs):
    inp = {k: np.asarray(v) for k, v in inputs.items()}
    xT = to_xT(inp["x"].astype(np.float32, copy=False))
    st0 = [{"kind": "ffn", "l": 0, "s": 0}, {"kind": "h_out", "l": 0}, {"kind": "x_out"}]
    r = _run(build_T(st0), _T_maps(inp, st0, xT, [{}] * NCORES))
    xT = [np.asarray(q["xT_out"]) for q in r]
    hT = [np.asarray(q["hT_out"]) for q in r]
    maps = [prep_M0(inp, c // 2, c % 2, _pair_cat_tokens(hT, c // 2)) for c in range(NCORES)]
    r = _run(build_M0(), maps)
    y5 = [np.asarray(q["y5T"]) for q in r]
    ys = [np.asarray(q["ysT"]) for q in r]
    extra = []
    for c in range(NCORES):
        b, jt = c // 2, c % 2
        ts = slice(jt * TOK, (jt + 1) * TOK)
        extra.append({"y5T_in": np.ascontiguousarray(np.concatenate([y5[2 * b][:, :, ts], y5[2 * b + 1][:, :, ts]], axis=1)),
                      "ysT_in": np.ascontiguousarray(np.concatenate([ys[2 * b][:, :, ts], ys[2 * b + 1][:, :, ts]], axis=1))})
    st1 = [{"kind": "mix0_post"}, {"kind": "ffn", "l": 0, "s": 2}, {"kind": "ffn", "l": 1, "s": 0},
           {"kind": "h_out", "l": 1}, {"kind": "x_out"}]
    r = _run(build_T(st1), _T_maps(inp, st1, xT, extra))
    xT = [np.asarray(q["xT_out"]) for q in r]
    hT = [np.asarray(q["hT_out"]) for q in r]
    maps = [prep_M1(inp, c // 2, c % 2, _pair_cat_tokens(hT, c // 2)) for c in range(NCORES)]
    r = _run(build_M1(), maps)
    yg = [np.asarray(q["ygT"]) for q in r]
    extra = []
    for c in range(NCORES):
        b, jt = c // 2, c % 2
        ts = slice(jt * TOK, (jt + 1) * TOK)
        extra.append({"ygT_in": np.ascontiguousarray(np.concatenate([yg[2 * b][:, :, ts], yg[2 * b + 1][:, :, ts]], axis=1))})
    st2 = [{"kind": "rwkv_post"}, {"kind": "ffn", "l": 1, "s": 2}, {"kind": "final"}]
    r = _run(build_T(st2), _T_maps(inp, st2, xT, extra))
    out = from_xT([np.asarray(q["xT_out"]) for q in r])
    return out.astype(np.float32)


def fused_maps(inp):
    xT = to_xT(inp["x"].astype(np.float32, copy=False))
    maps = []
    for c in range(NCORES):
        b, j = c // 2, c % 2
        m = {}
        for st in (ST0, ST1, ST2):
            m.update(_T_maps(inp, st, xT, [{}] * NCORES)[c])
        m0 = prep_M0(inp, b, j, None)
        m1 = prep_M1(inp, b, j, None)
        m0.pop("hT")
        m1.pop("hT")
        m.update(m0)
        m.update(m1)
        sel = np.zeros((128, 2), np.float32)
        sel[:, j] = 1.0
        m["selT"] = sel
        maps.append(m)
    return maps


def kernel(**inputs):
    inp = {k: np.asarray(v) for k, v in inputs.items()}
    nc = build_fused()
    r = _run(nc, _only_declared(fused_maps(inp)))
    out = from_xT([np.asarray(q["xT_out"]) for q in r])
    return out.astype(np.float32)
```

```python
import numpy as np
import concourse.bass as bass
import concourse.mybir as mybir
from concourse.bass_utils import run_bass_kernel_spmd

F32 = mybir.dt.float32
BF16 = mybir.dt.bfloat16
AF = mybir.ActivationFunctionType
ALU = mybir.AluOpType

D = 2048
KT = 16
FFN = 5632
NCORES = 8
TOK = 1024
SEQ = 2048
EPS = 1e-6


SKIP_SELF = {"pe"}


class _Eng:
    def __init__(self, name, obj, sem):
        self.name, self.obj, self.sem = name, obj, sem
        self.count = 0
        self.seen = {}


class _Rec:
    def __getattr__(self, name):
        def f(*a, **k):
            self.call = (name, a, k)
            return self
        return f


class Ctx:
    def record(self, fn, *args):
        self.rec = []
        fn(*args)
        lst, self.rec = self.rec, None
        return lst

    def play(self, item):
        engname, (name, a, k), reads, writes = item
        return self.op(engname, lambda e: getattr(e, name)(*a, **k), reads, writes)

    def play_interleaved(self, la, lb):
        i = j = 0
        na, nb = len(la), len(lb)
        while i < na or j < nb:
            if i < na and (j >= nb or i * nb <= j * na):
                self.play(la[i])
                i += 1
            else:
                self.play(lb[j])
                j += 1

    def play_interleaved3(self, la, lb, lc):
        lists = [l for l in (la, lb, lc) if l]
        pos = [0] * len(lists)
        while any(p < len(l) for p, l in zip(pos, lists)):
            k = min((i for i in range(len(lists)) if pos[i] < len(lists[i])), key=lambda i: pos[i] / len(lists[i]))
            self.play(lists[k][pos[k]])
            pos[k] += 1

    def __init__(self, nc):
        self.nc = nc
        self.engs = {}
        for name, attr in (("pe", "tensor"), ("act", "scalar"), ("dve", "vector"),
                           ("pool", "gpsimd"), ("sp", "sync")):
            self.engs[name] = _Eng(name, getattr(nc, attr), nc.alloc_semaphore("sem_" + name))
        self.res = {}
        self.nslots = 0
        self.uid = 0
        self.stacks = []
        self.free_slots = []
        self.phase = 0
        self.rec = None

    def sb(self, name, shape, dtype=F32):
        self.uid += 1
        if self.stacks:
            return self.stacks[-1][0].enter_context(self.nc.sbuf_tensor(f"{name}_{self.uid}", list(shape), dtype))
        return self.nc.alloc_sbuf_tensor(f"{name}_{self.uid}", list(shape), dtype)

    def open_scope(self):
        import contextlib
        self.stacks.append((contextlib.ExitStack(), []))

    def close_scope(self):
        self.barrier()
        st, slots = self.stacks.pop()
        st.close()
        self.free_slots.extend(slots)

    def ps(self, name):
        self.uid += 1
        return self.nc.alloc_psum_tensor(f"{name}_{self.uid}", [128, 512], F32)

    def slot(self, name):
        if self.free_slots:
            sl = self.free_slots.pop()
        else:
            self.nslots += 1
            sl = {"sem": self.nc.alloc_semaphore(f"dsem_{name}_{self.nslots}"), "count": 0,
                  "key": f"slot{self.nslots}"}
        if self.stacks:
            self.stacks[-1][1].append(sl)
        return sl

    def _deps(self, reads, writes):
        deps = []
        for r in reads:
            st = self.res.get(r)
            if st and st["w"]:
                deps.append(st["w"])
            if st and r.startswith("bank"):
                deps.extend(st["r"].values())
        for w in writes:
            st = self.res.get(w)
            if st:
                if st["w"]:
                    deps.append(st["w"])
                deps.extend(st["r"].values())
        return deps

    def _wait(self, eng, deps, skip_self):
        for sem, val, key in deps:
            if skip_self and key == eng.name:
                continue
            if eng.seen.get(key, 0) < val:
                eng.obj.wait_ge(sem, val)
                eng.seen[key] = val

    def _record(self, tok, reads, writes):
        for r in reads:
            st = self.res.setdefault(r, {"w": None, "r": {}})
            st["r"][tok[2]] = tok
        for w in writes:
            self.res[w] = {"w": tok, "r": {}}

    def op(self, engname, emit, reads=(), writes=()):
        if self.rec is not None:
            r = _Rec()
            emit(r)
            self.rec.append((engname, r.call, tuple(reads), tuple(writes)))
            return None
        eng = self.engs[engname]
        self._wait(eng, self._deps(reads, writes), skip_self=(engname in SKIP_SELF))
        inst = emit(eng.obj)
        eng.count += 1
        inst.then_inc(eng.sem, 1)
        tok = (eng.sem, eng.count, engname)
        eng.seen[engname] = max(eng.seen.get(engname, 0), 0)
        self._record(tok, reads, writes)
        return tok

    def dma(self, qname, slot, pairs, reads=(), writes=(), **kw):
        eng = self.engs[qname]
        self._wait(eng, self._deps(reads, writes), skip_self=False)
        for out, in_ in pairs:
            eng.obj.dma_start(out=out, in_=in_, **kw).then_inc(slot["sem"], 16)
            slot["count"] += 16
        tok = (slot["sem"], slot["count"], slot["key"])
        self._record(tok, reads, writes)
        return tok

    def wait_all(self, engname):
        eng = self.engs[engname]
        deps = []
        for e in self.engs.values():
            if e.count:
                deps.append((e.sem, e.count, e.name))
        for st in self.res.values():
            if st["w"]:
                deps.append(st["w"])
            deps.extend(st["r"].values())
        self._wait(eng, deps, skip_self=False)

    def barrier(self):
        for n in self.engs:
            self.wait_all(n)

    def new_phase(self):
        self.barrier()
        self.phase += 1
        for e in self.engs.values():
            e.sem = self.nc.alloc_semaphore(f"sem_{e.name}_p{self.phase}")
            e.count = 0
            e.seen = {}
        self.res = {}


class WStream:
    NSLOT = 4
    ELEMS = 8192

    def __init__(self, cx, nslot=4, elems=8192):
        self.cx = cx
        self.NSLOT, self.ELEMS = nslot, elems
        self.tiles = [cx.sb(f"wslot{i}", [128, self.ELEMS], BF16) for i in range(self.NSLOT)]
        self.slots = [cx.slot(f"w{i}") for i in range(self.NSLOT)]
        self.plan = []
        self.issued = 0

    def add(self, src_ap, a, b):
        assert a * b <= self.ELEMS
        self.plan.append((src_ap, a, b))
        return len(self.plan) - 1

    def view(self, i):
        _, a, b = self.plan[i]
        t = self.tiles[i % self.NSLOT]
        return t[:, 0:a * b].rearrange("p (a b) -> p a b", a=a)

    def key(self, i):
        return f"wslot{i % self.NSLOT}"

    def _issue(self, i):
        src, a, b = self.plan[i]
        v = self.view(i)
        pairs = [(v[:, :, c0:min(b, c0 + 1024)], src[:, :, c0:min(b, c0 + 1024)]) for c0 in range(0, b, 1024)]
        self.cx.dma("pool", self.slots[i % self.NSLOT], pairs, writes=[self.key(i)])

    def need(self, i):
        upto = min(len(self.plan), i + self.NSLOT)
        while self.issued < upto:
            self._issue(self.issued)
            self.issued += 1


def _mk_env(G):
    if G is not None:
        return G["nc"], G["cx"], G["dram"], G["banks"], G["bankT"]
    nc = bass.Bass("TRN2", target_bir_lowering=False)
    cx = Ctx(nc)
    banks = [cx.ps(f"bank{i}") for i in range(7)]
    cx.uid += 1
    bankT = nc.alloc_psum_tensor(f"bankT_{cx.uid}", [128, 1024], BF16)
    return nc, cx, {}, banks, bankT


def build_T(stages, G=None, io=None):
    nc, cx, dram, banks, bankT = _mk_env(G)
    io = io or {}
    cx.open_scope()

    def din(name, shape, dtype=F32):
        if name in io:
            return io[name]
        if name not in dram:
            dram[name] = nc.dram_tensor(name, list(shape), dtype, kind="ExternalInput").ap()
        return dram[name]

    def dout(name, shape, dtype=F32):
        if name in io:
            return io[name]
        dram[name] = nc.dram_tensor(name, list(shape), dtype, kind="ExternalOutput").ap()
        return dram[name]

    xT_d = din("xT", [128, KT, TOK])
    cT_d = din("cT", [128, KT])

    x = cx.sb("x", [128, KT, TOK], F32)
    sq = [cx.sb(f"sq{i}", [128, TOK], F32) for i in range(2)]
    rstd = cx.sb("rstd", [128, TOK], F32)
    cact = cx.sb("cact", [128, KT], BF16)
    cin = cx.sb("cin", [128, KT], F32)
    ones = cx.sb("ones", [128, 128], F32)
    ws = WStream(cx)
    ld = cx.slot("ld")
    ld2 = cx.slot("ld2")

    cx.op("dve", lambda e: e.memset(ones[:], 1.0), writes=["ones"])
    cx.dma("sp", ld, [(x[:, 0:KT // 2, :], xT_d[:, 0:KT // 2, :]), (x[:, KT // 2:KT, :], xT_d[:, KT // 2:KT, :])],
           writes=["x"])
    cx.dma("sp", ld2, [(cin[:], cT_d)], writes=["cin"])
    cx.op("act", lambda e: e.activation(out=cact[:], in_=cin[:], func=AF.Silu),
          reads=["cin"], writes=["cact"])

    small_id = [0]

    def load_small(name, shape):
        small_id[0] += 1
        t = cx.sb(name, shape, F32)
        sl = cx.slot(name)
        cx.dma("sp", sl, [(t[:], din(name, shape))], writes=[name + str(small_id[0])])
        return t, name + str(small_id[0])

    def compute_mod(l, s, which):
        wmod = din(f"w_mod{l}", [D, 9 * D])
        bmod, bkey = load_small(f"b_modT{l}", [128, 9 * KT])
        out = cx.sb(f"mod{l}{s}", [128, 3, KT], F32)
        okey = f"mod{l}{s}_" + "".join(str(w) for w in which)
        blocks = []
        for j in which:
            for cb in range(D // 512):
                c0 = s * 3 * D + j * D + cb * 512
                src = wmod[:, c0:c0 + 512].rearrange("(kt p) c -> p kt c", p=128)
                blocks.append((j, cb, ws.add(src, KT, 512)))
        bank = banks[6]
        for (j, cb, bid) in blocks:
            ws.need(bid)
            wv = ws.view(bid)
            for ft in range(4):
                for kt in range(KT):
                    cx.op("pe", lambda e, ft=ft, kt=kt, wv=wv: e.matmul(
                        bank[:, ft:ft + 1], wv[:, kt, ft * 128:(ft + 1) * 128], cact[:, kt:kt + 1],
                        start=(kt == 0), stop=(kt == KT - 1)),
                        reads=[ws.key(bid), "cact"], writes=["bank6"])
            jj = s * 3 + j
            col = jj * KT + cb * 4
            cx.op("dve", lambda e, j=j, cb=cb, col=col: e.tensor_tensor(
                out=out[:, j, cb * 4:cb * 4 + 4], in0=bank[:, 0:4], in1=bmod[:, col:col + 4], op=ALU.add),
                reads=["bank6", bkey], writes=[okey])
        return out, okey

    def rms_stats(xkey="x"):
        for kt in range(KT):
            s_ = sq[kt % 2]
            cx.op("act", lambda e, kt=kt, s_=s_: e.activation(out=s_[:], in_=x[:, kt, :], func=AF.Square),
                  reads=[xkey], writes=[f"sq{kt % 2}"])
            for t in range(2):
                cx.op("pe", lambda e, kt=kt, t=t, s_=s_: e.matmul(
                    banks[4 + t][:], ones[:], s_[:, t * 512:(t + 1) * 512],
                    start=(kt == 0), stop=(kt == KT - 1)),
                    reads=[f"sq{kt % 2}", "ones"], writes=[f"bank{4 + t}"])
        for t in range(2):
            cx.op("act", lambda e, t=t: e.activation(out=rstd[:, t * 512:(t + 1) * 512], in_=banks[4 + t][:],
                                                      func=AF.Sqrt, scale=1.0 / D, bias=epsb[:]),
                  reads=[f"bank{4 + t}", "epsb"], writes=["rstd"])
        cx.op("dve", lambda e: e.reciprocal(out=rstd[:], in_=rstd[:]), reads=["rstd"], writes=["rstd"])

    epsb = cx.sb("epsb", [128, 1], F32)
    cx.op("dve", lambda e: e.memset(epsb[:], EPS), writes=["epsb"])

    def adaln(l, s, mod, mkey, dst, dkey):
        ng, ngkey = load_small(f"norm_gT{l}{s}", [128, KT])
        a = cx.sb(f"a{l}{s}", [128, KT], F32)
        akey = f"a{l}{s}"
        cx.op("dve", lambda e: e.scalar_tensor_tensor(out=a[:], in0=mod[:, 1, :], scalar=1.0, in1=ng[:],
                                                      op0=ALU.add, op1=ALU.mult),
              reads=[mkey, ngkey], writes=[akey])
        rms_stats()
        for kt in range(KT):
            s_ = sq[kt % 2]
            cx.op("dve", lambda e, kt=kt, s_=s_: e.scalar_tensor_tensor(
                out=s_[:], in0=x[:, kt, :], scalar=a[:, kt:kt + 1], in1=rstd[:],
                op0=ALU.mult, op1=ALU.mult),
                reads=["x", akey, "rstd"], writes=[f"sq{kt % 2}"])
            cx.op("act", lambda e, kt=kt, s_=s_: e.activation(
                out=dst[:, kt, :], in_=s_[:], func=AF.Identity, bias=mod[:, 0, kt:kt + 1], scale=1.0),
                reads=[f"sq{kt % 2}", mkey], writes=[dkey])

    def ffn(l, s):
        fi = 0 if s == 0 else 1
        w1 = din(f"ffn_w1_{l}{fi}", [D, FFN])
        w3 = din(f"ffn_w3_{l}{fi}", [D, FFN])
        w2 = din(f"ffn_w2_{l}{fi}", [FFN, D])
        cx.open_scope()
        h = cx.sb("h", [128, KT, TOK], BF16)
        g = [cx.sb(f"g{i}", [128, 4, TOK], BF16) for i in range(2)]
        silu_t = [cx.sb(f"silu{i}", [128, 512], F32) for i in range(2)]
        mod, mkey = MODS[(l, s)]
        adaln(l, s, mod, mkey, h, "h")
        hg = cx.sb(f"hg{l}{s}", [128, KT], F32)
        cx.op("dve", lambda e: e.tensor_scalar(out=hg[:], in0=mod[:, 2, :], scalar1=0.5, scalar2=None,
                                               op0=ALU.mult),
              reads=[mkey], writes=[f"hg{l}{s}"])
        NCH = FFN // 512
        blk = []
        for c in range(NCH):
            b1 = ws.add(w1[:, c * 512:(c + 1) * 512].rearrange("(kt p) c -> p kt c", p=128), KT, 512)
            b3 = ws.add(w3[:, c * 512:(c + 1) * 512].rearrange("(kt p) c -> p kt c", p=128), KT, 512)
            b2 = ws.add(w2[c * 512:(c + 1) * 512, :].rearrange("(kt p) c -> p kt c", p=128), 4, D)
            blk.append((b1, b3, b2))
        ev = 0
        for c in range(NCH):
            b1, b3, b2 = blk[c]
            gb = g[c % 2]
            gkey = f"g{c % 2}"
            ws.need(b1)
            w1v, w3v = ws.view(b1), ws.view(b3)
            for m in range(4):
                for t in range(2):
                    pa, pb = banks[t * 2], banks[t * 2 + 1]
                    ka, kb = f"bank{t * 2}", f"bank{t * 2 + 1}"
                    for kt in range(KT):
                        cx.op("pe", lambda e, kt=kt, m=m, t=t, pa=pa: e.matmul(
                            pa[:], w1v[:, kt, m * 128:(m + 1) * 128], h[:, kt, t * 512:(t + 1) * 512],
                            start=(kt == 0), stop=(kt == KT - 1)),
                            reads=[ws.key(b1), "h"], writes=[ka])
                    for kt in range(KT):
                        cx.op("pe", lambda e, kt=kt, m=m, t=t, pb=pb: e.matmul(
                            pb[:], w3v[:, kt, m * 128:(m + 1) * 128], h[:, kt, t * 512:(t + 1) * 512],
                            start=(kt == 0), stop=(kt == KT - 1)),
                            reads=[ws.key(b3), "h"], writes=[kb])
                    st_ = silu_t[ev % 2]
                    skey = f"silu{ev % 2}"
                    ev += 1
                    cx.op("act", lambda e, pa=pa, st_=st_: e.activation(out=st_[:], in_=pa[:], func=AF.Silu),
                          reads=[ka], writes=[skey])
                    cx.op("dve", lambda e, pb=pb, st_=st_, m=m, t=t, gb=gb: e.tensor_tensor(
                        out=gb[:, m, t * 512:(t + 1) * 512], in0=st_[:], in1=pb[:], op=ALU.mult),
                        reads=[skey, kb], writes=[gkey])
            ws.need(b2)
            w2v = ws.view(b2)
            for j in range(KT):
                for t in range(2):
                    bi = 4 + ((j * 2 + t) % 3)
                    po, ko = banks[bi], f"bank{bi}"
                    for m in range(4):
                        cx.op("pe", lambda e, m=m, j=j, t=t, po=po, gb=gb: e.matmul(
                            po[:], w2v[:, m, j * 128:(j + 1) * 128], gb[:, m, t * 512:(t + 1) * 512],
                            start=(m == 0), stop=(m == 3)),
                            reads=[ws.key(b2), gkey], writes=[ko])
                    cx.op("dve", lambda e, j=j, t=t, po=po: e.scalar_tensor_tensor(
                        out=x[:, j, t * 512:(t + 1) * 512], in0=po[:], scalar=hg[:, j:j + 1],
                        in1=x[:, j, t * 512:(t + 1) * 512], op0=ALU.mult, op1=ALU.add),
                        reads=[ko, f"hg{l}{s}"], writes=["x"])
        cx.close_scope()

    def outproj(wname, krows, src, skey, nkt, gate, gkey):
        wd = din(wname, [krows, D])
        blks = [ws.add(wd[:, j * 128:(j + 1) * 128].rearrange("(kt p) c -> p kt c", p=128), nkt, 128) for j in range(KT)]
        for j in range(KT):
            ws.need(blks[j])
            wv = ws.view(blks[j])
            for t in range(2):
                bi = 4 + ((j * 2 + t) % 3)
                po, ko = banks[bi], f"bank{bi}"
                for kt in range(nkt):
                    cx.op("pe", lambda e, kt=kt: e.matmul(po[:], wv[:, kt, :], src[:, kt, t * 512:(t + 1) * 512],
                                                           start=(kt == 0), stop=(kt == nkt - 1)),
                          reads=[ws.key(blks[j]), skey], writes=[ko])
                cx.op("dve", lambda e: e.scalar_tensor_tensor(
                    out=x[:, j, t * 512:(t + 1) * 512], in0=po[:], scalar=gate[:, j:j + 1],
                    in1=x[:, j, t * 512:(t + 1) * 512], op0=ALU.mult, op1=ALU.add),
                    reads=[ko, gkey], writes=["x"])

    def gath_select(gath, ntile, dests, gdt):
        selT, selk = load_small("selT", [128, 2])
        if gdt == F32:
            stA, kA = sq, ["sq0", "sq1"]
        else:
            stA, kA = [cx.sb(f"gsA{i}", [128, TOK], gdt) for i in range(2)], ["gsA0", "gsA1"]
        stB = [cx.sb(f"gsB{i}", [128, TOK], gdt) for i in range(2)]
        sls = [cx.slot(f"gs{i}") for i in range(2)]
        n = 0
        for dst, dkey, tiles in dests:
            for di, (r, lt) in enumerate(tiles):
                b_ = n % 2
                n += 1
                cx.dma("sp", sls[b_], [(stA[b_][:], gath(r, lt, 0)), (stB[b_][:], gath(r, lt, 1))],
                       reads=io.get("dep", []), writes=[kA[b_], f"gsB{b_}"])
                cx.op("dve", lambda e: e.tensor_scalar(out=stB[b_][:], in0=stB[b_][:], scalar1=selT[:, 1:2], scalar2=None, op0=ALU.mult),
                      reads=[f"gsB{b_}", selk], writes=[f"gsB{b_}"])
                cx.op("dve", lambda e: e.scalar_tensor_tensor(out=dst[:, di, :], in0=stA[b_][:], scalar=selT[:, 0:1], in1=stB[b_][:],
                                                               op0=ALU.mult, op1=ALU.add),
                      reads=[kA[b_], f"gsB{b_}", selk], writes=[dkey])

    def mix0_post():
        cx.open_scope()
        mod, mkey = MODS[(0, 1, "g")]
        y5b = cx.sb("y5b", [128, 8, TOK], BF16)
        ysb = cx.sb("ysb", [128, 16, TOK], BF16)
        sig = cx.sb("sig", [128, 8, 512], BF16)
        sl5, sls = cx.slot("y5in"), cx.slot("ysin")
        if "y_gath" not in io:
            y5_d = din("y5T_in", [128, 8, TOK])
            ys_d = din("ysT_in", [128, 16, TOK])
        if "y_gath" in io:
            gath_select(io["y_gath"], 12, [(y5b, "y5b", [(ft // 4, ft % 4) for ft in range(8)]),
                                           (ysb, "ysb", [(kt // 8, 4 + kt % 8) for kt in range(16)])], F32)
        else:
            cx.dma("pool", sl5, cast_pairs(y5b[:], y5_d), writes=["y5b"])
            cx.dma("pool", sls, cast_pairs(ysb[:], ys_d), writes=["ysb"])
        glub, gbk = load_small("glu_bT", [128, 8])
        sng, sgk = load_small("ssd_norm_gT", [128, 16])
        gw = din("s5_glu_w", [1024, 1024])
        blks = [ws.add(gw[:, j * 128:(j + 1) * 128].rearrange("(kt p) c -> p kt c", p=128), 8, 128) for j in range(8)]
        for t in range(2):
            for j in range(8):
                ws.need(blks[j])
                bi = j % 4
                po, ko = banks[bi], f"bank{bi}"
                wv = ws.view(blks[j])
                for kt in range(8):
                    cx.op("pe", lambda e, kt=kt: e.matmul(po[:], wv[:, kt, :], y5b[:, kt, t * 512:(t + 1) * 512],
                                                           start=(kt == 0), stop=(kt == 7)),
                          reads=[ws.key(blks[j]), "y5b"], writes=[ko])
                cx.op("act", lambda e: e.activation(out=sig[:, j, :], in_=po[:], func=AF.Sigmoid, bias=glub[:, j:j + 1], scale=1.0),
                      reads=[ko, gbk], writes=["sig"])
            if t == 0:
                blks = [ws.add(gw[:, j * 128:(j + 1) * 128].rearrange("(kt p) c -> p kt c", p=128), 8, 128) for j in range(8)]
            for j in range(8):
                cx.op("dve", lambda e: e.tensor_tensor(out=y5b[:, j, t * 512:(t + 1) * 512], in0=y5b[:, j, t * 512:(t + 1) * 512],
                                                        in1=sig[:, j, :], op=ALU.mult), reads=["y5b", "sig"], writes=["y5b"])
        for kt in range(KT):
            s_ = sq[kt % 2]
            cx.op("act", lambda e: e.activation(out=s_[:], in_=ysb[:, kt, :], func=AF.Square), reads=["ysb"], writes=[f"sq{kt % 2}"])
            for t in range(2):
                cx.op("pe", lambda e: e.matmul(banks[4 + t][:], ones[:], s_[:, t * 512:(t + 1) * 512], start=(kt == 0), stop=(kt == KT - 1)),
                      reads=[f"sq{kt % 2}", "ones"], writes=[f"bank{4 + t}"])
        for t in range(2):
            cx.op("act", lambda e: e.activation(out=rstd[:, t * 512:(t + 1) * 512], in_=banks[4 + t][:], func=AF.Sqrt, scale=1.0 / D, bias=epsb[:]),
                  reads=[f"bank{4 + t}", "epsb"], writes=["rstd"])
        cx.op("dve", lambda e: e.reciprocal(out=rstd[:], in_=rstd[:]), reads=["rstd"], writes=["rstd"])
        for kt in range(KT):
            cx.op("dve", lambda e: e.scalar_tensor_tensor(out=ysb[:, kt, :], in0=ysb[:, kt, :], scalar=sng[:, kt:kt + 1], in1=rstd[:],
                                                           op0=ALU.mult, op1=ALU.mult), reads=["ysb", sgk, "rstd"], writes=["ysb"])
        wd = din("hyb_w_out", [3072, D])
        blk5 = [ws.add(wd[0:1024, j * 128:(j + 1) * 128].rearrange("(kt p) c -> p kt c", p=128), 8, 128) for j in range(KT)]
        gate = mod[:, 2, :]
        for j in range(KT):
            blks_ = ws.add(wd[1024:3072, j * 128:(j + 1) * 128].rearrange("(kt p) c -> p kt c", p=128), 16, 128)
            blk5[j] = (blk5[j], blks_)
        for part in range(2):
            for j in range(KT):
                b_ = blk5[j][part]
                ws.need(b_)
                wv = ws.view(b_)
                nk = 8 if part == 0 else 16
                srcb, skey = (y5b, "y5b") if part == 0 else (ysb, "ysb")
                for t in range(2):
                    bi = 4 + ((j * 2 + t) % 3)
                    po, ko = banks[bi], f"bank{bi}"
                    for kt in range(nk):
                        cx.op("pe", lambda e, kt=kt: e.matmul(po[:], wv[:, kt, :], srcb[:, kt, t * 512:(t + 1) * 512],
                                                               start=(kt == 0), stop=(kt == nk - 1)),
                              reads=[ws.key(b_), skey], writes=[ko])
                    cx.op("dve", lambda e: e.scalar_tensor_tensor(
                        out=x[:, j, t * 512:(t + 1) * 512], in0=po[:], scalar=gate[:, j:j + 1],
                        in1=x[:, j, t * 512:(t + 1) * 512], op0=ALU.mult, op1=ALU.add),
                        reads=[ko, mkey], writes=["x"])
        cx.close_scope()

    def rwkv_post():
        cx.open_scope()
        mod, mkey = MODS[(1, 1, "g")]
        ygb = cx.sb("ygb", [128, 16, TOK], BF16)
        slg = cx.slot("ygin")
        if "yg_gath" in io:
            gath_select(io["yg_gath"], 8, [(ygb, "ygb", [(kt // 8, kt % 8) for kt in range(16)])], BF16)
        else:
            yg_d = din("ygT_in", [128, 16, TOK], BF16)
            cx.dma("sp", slg, [(ygb[:, 4 * i:4 * i + 4, :], yg_d[:, 4 * i:4 * i + 4, :]) for i in range(4)], writes=["ygb"])
        outproj("rwkv_w_o", D, ygb, "ygb", KT, mod[:, 2, :], mkey)
        cx.close_scope()

    st_slot = cx.slot("st")
    MODS = {}
    for stg in stages:
        kind = stg["kind"]
        if kind == "ffn":
            MODS[(stg["l"], stg["s"])] = compute_mod(stg["l"], stg["s"], (0, 1, 2))
        elif kind == "h_out":
            MODS[(stg["l"], 1, "h")] = compute_mod(stg["l"], 1, (0, 1))
        elif kind == "mix0_post":
            MODS[(0, 1, "g")] = compute_mod(0, 1, (2,))
        elif kind == "rwkv_post":
            MODS[(1, 1, "g")] = compute_mod(1, 1, (2,))
    for stg in stages:
        kind = stg["kind"]
        if kind == "ffn":
            ffn(stg["l"], stg["s"])
        elif kind == "mix0_post":
            mix0_post()
        elif kind == "rwkv_post":
            rwkv_post()
        elif kind == "h_out":
            l = stg["l"]
            cx.open_scope()
            h = cx.sb("h", [128, KT, TOK], BF16)
            mod, mkey = MODS[(l, 1, "h")]
            adaln(l, 1, mod, mkey, h, "h")
            if "hT_out_pairs" in io:
                cx.dma("sp", st_slot, io["hT_out_pairs"](h), reads=["h"])
            else:
                hout = dout("hT_out", [128, KT, TOK], BF16)
                cx.dma("sp", st_slot, [(hout[:, 0:KT // 2, :], h[:, 0:KT // 2, :]), (hout[:, KT // 2:KT, :], h[:, KT // 2:KT, :])],
                       reads=["h"])
            cx.close_scope()
        elif kind == "x_out":
            xout = dout("xT_out", [128, KT, TOK], F32)
            cx.dma("sp", st_slot, [(xout[:, 0:KT // 2, :], x[:, 0:KT // 2, :]), (xout[:, KT // 2:KT, :], x[:, KT // 2:KT, :])],
                   reads=["x"])
        elif kind == "final":
            fg, fkey = load_small("final_gT", [128, KT])
            rms_stats()
            for kt in range(KT):
                cx.op("dve", lambda e, kt=kt: e.scalar_tensor_tensor(
                    out=x[:, kt, :], in0=x[:, kt, :], scalar=fg[:, kt:kt + 1], in1=rstd[:],
                    op0=ALU.mult, op1=ALU.mult),
                    reads=["x", fkey, "rstd"], writes=["x"])
            xout = dout("xT_out", [128, KT, TOK], F32)
            cx.dma("sp", st_slot, [(xout[:, 0:KT // 2, :], x[:, 0:KT // 2, :]), (xout[:, KT // 2:KT, :], x[:, KT // 2:KT, :])],
                   reads=["x"])
    cx.close_scope()
    cx.wait_all("sp")
    return nc


def cast_pairs(dst, src):
    if len(dst.shape) == 2:
        n = dst.shape[1]
        return [(dst[:, c0:min(n, c0 + 1024)], src[:, c0:min(n, c0 + 1024)]) for c0 in range(0, n, 1024)]
    out = []
    for a in range(dst.shape[1]):
        n = dst.shape[2]
        for c0 in range(0, n, 1024):
            out.append((dst[:, a, c0:min(n, c0 + 1024)], src[:, a, c0:min(n, c0 + 1024)]))
    return out


def fm(v):
    v = np.asarray(v)
    return np.ascontiguousarray(v.reshape(-1, 128).T)


def to_xT(x):
    out = []
    for b in range(4):
        for j in range(2):
            xs = x[b, j * TOK:(j + 1) * TOK, :]
            out.append(np.ascontiguousarray(xs.T.reshape(KT, 128, TOK).transpose(1, 0, 2)))
    return out


def from_xT(tiles):
    x = np.empty((4, SEQ, D), np.float32)
    for b in range(4):
        for j in range(2):
            t = tiles[b * 2 + j]
            x[b, j * TOK:(j + 1) * TOK, :] = t.transpose(1, 0, 2).reshape(D, TOK).T
    return x


S5TC = 256
GELU_C = 0.7978845608028654
TWO_PI = 6.283185307179586


CHK_COUNT = 1
DBG_BANKS = [0, 1]
DBG_NODVE = False


class _Stop(Exception):
    pass


def build_M0(do_s5=True, do_ssd=True, stop=None, G=None, io=None):
    nc, cx, dram, banks, bankT = _mk_env(G)
    io = io or {}
    cx.open_scope()

    cnt = [CHK_COUNT]

    def chk(n):
        if stop == n:
            cnt[0] -= 1
            if cnt[0] <= 0:
                raise _Stop()
    try:
        _build_M0_body(nc, cx, dram, banks, bankT, io, do_s5, do_ssd, chk)
    except _Stop:
        pass
    while cx.stacks and stop is not None:
        cx.close_scope()
    if stop is None:
        cx.close_scope()
    cx.wait_all("sp")
    return nc


def _build_M0_body(nc, cx, dram, banks, bankT, io, do_s5, do_ssd, chk):

    def din(name, shape, dtype=F32):
        if name in io:
            return io[name]
        if name not in dram:
            dram[name] = nc.dram_tensor(name, list(shape), dtype, kind="ExternalInput").ap()
        return dram[name]

    def dout(name, shape, dtype=F32):
        if name in io:
            return io[name]
        dram[name] = nc.dram_tensor(name, list(shape), dtype, kind="ExternalOutput").ap()
        return dram[name]

    BF16S = "bf16_from_f32"

    def load(name, shape, dtype=F32, q="sp"):
        sdt, ddt = (BF16, F32) if dtype == BF16S else (dtype, dtype)
        t = cx.sb(name, shape, sdt)
        sl = cx.slot(name)
        pairs = cast_pairs(t[:], din(name, shape, ddt)) if dtype == BF16S else [(t[:], din(name, shape, ddt))]
        cx.dma(q, sl, pairs, writes=[name])
        return t

    w_d = din("w_in_c", [D, 3600])
    hT = cx.sb("hT", [128, KT, SEQ], BF16)
    sl = cx.slot("hT")
    if "hT_pairs" in io:
        cx.dma("sp", sl, io["hT_pairs"](hT, 0), reads=io.get("dep", []), writes=["hT"])
    else:
        hT_d = din("hT", [128, KT, SEQ], BF16)
        cx.dma("sp", sl, [(hT[:, 4 * i:4 * i + 4, :], hT_d[:, 4 * i:4 * i + 4, :]) for i in range(4)], writes=["hT"])
    ws = WStream(cx, nslot=4, elems=4096)
    ident = load("ident", [128, 128])
    identb = cx.sb("identb", [128, 128], BF16)
    cx.op("dve", lambda e: e.tensor_copy(out=identb[:], in_=ident[:]), reads=["ident"], writes=["identb"])
    st_slot = cx.slot("st")
    st_slot2 = cx.slot("st2")

    def proj(blk, col0, ncol_tiles, evac):
        wv = ws.view(blk)
        n = 0
        for ti in range(ncol_tiles):
            for tb in range(4):
                bk = DBG_BANKS[n % len(DBG_BANKS)]
                n += 1
                for kt in range(KT):
                    cx.op("pe", lambda e, kt=kt, ti=ti, tb=tb, bk=bk: e.matmul(
                        banks[bk][:], wv[:, kt, (col0 + ti) * 128:(col0 + ti + 1) * 128],
                        hT[:, kt, tb * 512:(tb + 1) * 512], start=(kt == 0), stop=(kt == KT - 1)),
                        reads=[ws.key(blk), "hT"], writes=[f"bank{bk}"])
                chk(53)
                evac(ti, tb, banks[bk], f"bank{bk}")
                chk(54)

    if do_s5:
        cx.open_scope()
        lre = load("s5_lre", [128, 16])
        lim = load("s5_lim", [128, 16])
        ldt = load("s5_ldt", [128, 16])
        d5 = load("s5_dT", [128, 4])
        cre = load("s5_cre", [128, 16, 128], BF16S, q="pool")
        cimn = load("s5_cim", [128, 16, 128], BF16S, q="pool")
        cx.op("dve", lambda e: e.tensor_scalar(out=cimn[:], in0=cimn[:], scalar1=-1.0, scalar2=None, op0=ALU.mult),
              reads=["s5_cim"], writes=["s5_cim"])
        bre = cx.sb("breT", [128, 16, 128], BF16)
        bim = cx.sb("bimT", [128, 16, 128], BF16)

        sm = {}

        def S(name):
            sm[name] = cx.sb("s5_" + name, [128, 16], F32)
            return sm[name]

        def tt(o, a, b, op, eng="dve"):
            cx.op(eng, lambda e: e.tensor_tensor(out=sm[o][:], in0=sm[a][:], in1=sm[b][:], op=op),
                  reads=["s5sm"], writes=["s5sm"])

        def ts(o, a, s1, op0, s2=None, op1=None):
            if op1 is None:
                cx.op("dve", lambda e: e.tensor_scalar(out=sm[o][:], in0=sm[a][:], scalar1=s1, scalar2=None, op0=op0),
                      reads=["s5sm"], writes=["s5sm"])
            else:
                cx.op("dve", lambda e: e.tensor_scalar(out=sm[o][:], in0=sm[a][:], scalar1=s1, scalar2=s2, op0=op0, op1=op1),
                      reads=["s5sm"], writes=["s5sm"])

        def act(o, a, func, scale=1.0):
            cx.op("act", lambda e: e.activation(out=sm[o][:], in_=sm[a][:], func=func, scale=scale),
                  reads=["s5sm"], writes=["s5sm"])

        chk(1)
        sm["lre"], sm["lim"], sm["ldt"] = lre, lim, ldt
        for n_ in ("lr", "dt", "mag", "ang", "cs", "sn", "t1", "t2", "t3", "den", "nr", "fre", "fim", "lbr", "lbi", "rden"):
            S(n_)
        cx.wait_all("dve")
        cx.wait_all("act")
        ts("lr", "lre", -1e-4, ALU.min)
        act("dt", "ldt", AF.Exp)
        tt("t1", "lr", "dt", ALU.mult)
        act("mag", "t1", AF.Exp)
        tt("ang", "lim", "dt", ALU.mult)

        def sincos(o, a, shift):
            ki = cx.sb("s5_ki", [128, 16], mybir.dt.int32)
            ts("t1", a, 1.0 / TWO_PI, ALU.mult, shift / TWO_PI, ALU.add)
            cx.op("dve", lambda e: e.tensor_copy(out=ki[:], in_=sm["t1"][:]), reads=["s5sm"], writes=["s5ki"])
            cx.op("dve", lambda e: e.tensor_copy(out=sm["t2"][:], in_=ki[:]), reads=["s5ki"], writes=["s5sm"])
            tt("t1", "t1", "t2", ALU.subtract)
            ts("t2", "t1", 0.5, ALU.is_gt)
            tt("t1", "t1", "t2", ALU.subtract)
            ts("t2", "t1", -0.5, ALU.is_lt)
            tt("t1", "t1", "t2", ALU.add)
            act(o, "t1", AF.Sin, scale=TWO_PI)

        sincos("sn", "ang", 0.0)
        sincos("cs", "ang", TWO_PI / 4)
        tt("lbr", "mag", "cs", ALU.mult)
        tt("lbi", "mag", "sn", ALU.mult)
        tt("t1", "lr", "lr", ALU.mult)
        tt("t2", "lim", "lim", ALU.mult)
        tt("den", "t1", "t2", ALU.add)
        cx.op("dve", lambda e: e.reciprocal(out=sm["rden"][:], in_=sm["den"][:]), reads=["s5sm"], writes=["s5sm"])
        ts("nr", "lbr", -1.0, ALU.add)
        tt("t1", "nr", "lr", ALU.mult)
        tt("t2", "lbi", "lim", ALU.mult)
        tt("t1", "t1", "t2", ALU.add)
        tt("fre", "t1", "rden", ALU.mult)
        tt("t1", "lbi", "lr", ALU.mult)
        tt("t2", "nr", "lim", ALU.mult)
        tt("t1", "t1", "t2", ALU.subtract)
        tt("fim", "t1", "rden", ALU.mult)
        S("nfim")
        ts("nfim", "fim", -1.0, ALU.mult)
        S("nsn")
        ts("nsn", "sn", -1.0, ALU.mult)

        chk(2)
        cx.open_scope()
        xbre = load("s5_xbre", [128, 16, 128], q="act")
        xbim = load("s5_xbim", [128, 16, 128], q="act")
        xt = [cx.sb(f"s5xt{i}", [128, 128], F32) for i in range(2)]
        for pr in range(16):
            for part, (A, fa, Bm, fb) in enumerate(((xbre, "fre", xbim, "nfim"), (xbim, "fre", xbre, "fim"))):
                t_ = xt[part]
                cx.op("dve", lambda e, pr=pr, A=A, fa=fa, t_=t_: e.tensor_scalar(
                    out=t_[:], in0=A[:, pr, :], scalar1=sm[fa][:, pr:pr + 1], scalar2=None, op0=ALU.mult),
                    reads=["s5sm", "s5_xbre", "s5_xbim"], writes=[f"s5xt{part}"])
                cx.op("dve", lambda e, pr=pr, Bm=Bm, fb=fb, t_=t_: e.scalar_tensor_tensor(
                    out=t_[:], in0=Bm[:, pr, :], scalar=sm[fb][:, pr:pr + 1], in1=t_[:], op0=ALU.mult, op1=ALU.add),
                    reads=["s5sm", "s5_xbre", "s5_xbim", f"s5xt{part}"], writes=[f"s5xt{part}"])
                cx.op("pe", lambda e, t_=t_, part=part: e.transpose(banks[2 + part][:, 0:128], t_[:], ident[:]),
                      reads=[f"s5xt{part}", "ident"], writes=[f"bank{2 + part}"])
                dst = bre if part == 0 else bim
                cx.op("act", lambda e, dst=dst, pr=pr, part=part: e.activation(
                    out=dst[:, pr, :], in_=banks[2 + part][:, 0:128], func=AF.Copy),
                    reads=[f"bank{2 + part}"], writes=["breT" if part == 0 else "bimT"])

        chk(3)
        cx.close_scope()
        chk(4)
        ctab = cx.sb("ctab", [128, 16, S5TC], F32)
        stab = cx.sb("stab", [128, 16, S5TC], F32)
        rho = cx.sb("rho", [128, 16, S5TC], F32)
        ec = cx.sb("ec", [128, 16], F32)
        es = cx.sb("es", [128, 16], F32)
        et = [cx.sb(f"et{i}", [128, 16], F32) for i in range(3)]
        cx.op("dve", lambda e: e.memset(ctab[:, :, 0:1], 1.0), writes=["tab"])
        cx.op("dve", lambda e: e.memset(stab[:, :, 0:1], 0.0), reads=["tab"], writes=["tab"])
        cx.op("dve", lambda e: e.tensor_copy(out=ec[:], in_=sm["cs"][:]), reads=["s5sm"], writes=["e"])
        cx.op("dve", lambda e: e.tensor_copy(out=es[:], in_=sm["sn"][:]), reads=["s5sm", "e"], writes=["e"])
        L = 1
        while L < S5TC:
            for pr in range(16):
                cx.op("dve", lambda e, pr=pr, L=L: e.tensor_scalar(
                    out=ctab[:, pr, L:2 * L], in0=ctab[:, pr, 0:L], scalar1=ec[:, pr:pr + 1], scalar2=None, op0=ALU.mult),
                    reads=["tab", "e"], writes=["tab"])
                cx.op("dve", lambda e, pr=pr, L=L: e.tensor_scalar(
                    out=stab[:, pr, L:2 * L], in0=ctab[:, pr, 0:L], scalar1=es[:, pr:pr + 1], scalar2=None, op0=ALU.mult),
                    reads=["tab", "e"], writes=["tab"])
            cx.op("dve", lambda e: e.tensor_scalar(out=et[0][:], in0=es[:], scalar1=-1.0, scalar2=None, op0=ALU.mult),
                  reads=["e"], writes=["et"])
            for pr in range(16):
                cx.op("dve", lambda e, pr=pr, L=L: e.scalar_tensor_tensor(
                    out=ctab[:, pr, L:2 * L], in0=stab[:, pr, 0:L], scalar=et[0][:, pr:pr + 1], in1=ctab[:, pr, L:2 * L],
                    op0=ALU.mult, op1=ALU.add), reads=["tab", "et"], writes=["tab"])
                cx.op("dve", lambda e, pr=pr, L=L: e.scalar_tensor_tensor(
                    out=stab[:, pr, L:2 * L], in0=stab[:, pr, 0:L], scalar=ec[:, pr:pr + 1], in1=stab[:, pr, L:2 * L],
                    op0=ALU.mult, op1=ALU.add), reads=["tab", "e"], writes=["tab"])
            cx.op("dve", lambda e: e.tensor_tensor(out=et[1][:], in0=ec[:], in1=ec[:], op=ALU.mult), reads=["e"], writes=["et1"])
            cx.op("dve", lambda e: e.tensor_tensor(out=et[2][:], in0=es[:], in1=es[:], op=ALU.mult), reads=["e"], writes=["et2"])
            cx.op("dve", lambda e: e.scalar_tensor_tensor(out=es[:], in0=es[:], scalar=2.0, in1=ec[:], op0=ALU.mult, op1=ALU.mult),
                  reads=["e"], writes=["e"])
            cx.op("dve", lambda e: e.tensor_tensor(out=ec[:], in0=et[1][:], in1=et[2][:], op=ALU.subtract),
                  reads=["et1", "et2", "e"], writes=["e"])
            L *= 2
        for pr in range(16):
            cx.op("act", lambda e, pr=pr: e.activation(out=rho[:, pr, :], in_=ctab[:, pr, :], func=AF.Identity,
                                                        scale=0.0, bias=sm["mag"][:, pr:pr + 1]),
                  reads=["tab", "s5sm"], writes=["rho"])

        chk(5)
        u32 = cx.sb("u32", [128, SEQ], F32)
        ubf = cx.sb("ubf", [128, SEQ], BF16)
        y5 = cx.sb("y5", [128, SEQ], F32)
        carry = [cx.sb(f"carry{i}", [128, 16], F32) for i in range(2)]
        cx.op("dve", lambda e: e.memset(carry[0][:], 0.0), writes=["carry"])
        cx.op("dve", lambda e: e.memset(carry[1][:], 0.0), reads=["carry"], writes=["carry"])
        chk(51)
        tmp = [[cx.sb(f"s5tmp{j}{i}", [128, S5TC], F32) for i in range(8)] for j in range(2)]
        sbf = [[cx.sb(f"s5sbf{i}{j}", [128, S5TC], BF16) for j in range(2)] for i in range(4)]
        gl = [cx.sb(f"s5gl{i}", [128, S5TC], F32) for i in range(4)]
        y5_d = None if "y_tile" in io else dout("y5T", [128, 4, SEQ])
        NCH = SEQ // S5TC
        for o in range(4):
            blk = ws.add(w_d[:, o * 128:(o + 1) * 128].rearrange("(kt p) c -> p kt c", p=128), KT, 128)
            ws.need(blk)
            chk(52)

            def ev_u(ti, tb, ps, key):
                cx.op("act", lambda e: e.activation(out=u32[:, tb * 512:(tb + 1) * 512], in_=ps[:], func=AF.Copy),
                      reads=[key], writes=["u32"])
                if not DBG_NODVE:
                    cx.op("dve", lambda e: e.tensor_copy(out=ubf[:, tb * 512:(tb + 1) * 512], in_=ps[:]),
                          reads=[key], writes=["ubf"])
            proj(blk, 0, 1, ev_u)
            chk(6)
            for ch in range(NCH):
                c0 = ch * S5TC
                if ch == 1:
                    chk(7)
                for pp in range(4):
                    pr = o * 4 + pp
                    kre, kim = f"bank{2 + (pp % 2) * 2}", f"bank{3 + (pp % 2) * 2}"
                    pre, pim = banks[2 + (pp % 2) * 2], banks[3 + (pp % 2) * 2]
                    cx.op("pe", lambda e: e.matmul(pre[:, 0:S5TC], bre[:, pr, :], ubf[:, c0:c0 + S5TC], start=True, stop=True),
                          reads=["breT", "ubf"], writes=[kre])
                    cx.op("pe", lambda e: e.matmul(pim[:, 0:S5TC], bim[:, pr, :], ubf[:, c0:c0 + S5TC], start=True, stop=True),
                          reads=["bimT", "ubf"], writes=[kim])
                    t = tmp[pp % 2]
                    tq = pp % 2
                    ct, stb = ctab[:, pr, :], stab[:, pr, :]
                    cx.op("dve", lambda e: e.tensor_tensor(out=t[0][:], in0=pre[:, 0:S5TC], in1=ct, op=ALU.mult),
                          reads=[kre, "tab"], writes=[f"t{tq}_0"])
                    cx.op("dve", lambda e: e.tensor_tensor(out=t[1][:], in0=pim[:, 0:S5TC], in1=stb, op=ALU.mult),
                          reads=[kim, "tab"], writes=[f"t{tq}_1"])
                    cx.op("dve", lambda e: e.tensor_tensor(out=t[2][:], in0=pim[:, 0:S5TC], in1=ct, op=ALU.mult),
                          reads=[kim, "tab"], writes=[f"t{tq}_2"])
                    cx.op("dve", lambda e: e.tensor_tensor(out=t[3][:], in0=pre[:, 0:S5TC], in1=stb, op=ALU.mult),
                          reads=[kre, "tab"], writes=[f"t{tq}_3"])
                    cx.op("dve", lambda e: e.tensor_tensor(out=t[0][:], in0=t[0][:], in1=t[1][:], op=ALU.add),
                          reads=[f"t{tq}_0", f"t{tq}_1"], writes=[f"t{tq}_0"])
                    cx.op("dve", lambda e: e.tensor_tensor(out=t[2][:], in0=t[2][:], in1=t[3][:], op=ALU.subtract),
                          reads=[f"t{tq}_2", f"t{tq}_3"], writes=[f"t{tq}_2"])
                    cx.op("dve", lambda e: e.tensor_tensor_scan(out=t[4][:], data0=rho[:, pr, :], data1=t[0][:],
                                                                 initial=carry[0][:, pr:pr + 1], op0=ALU.mult, op1=ALU.add),
                          reads=["rho", f"t{tq}_0", "carry"], writes=[f"t{tq}_4"])
                    cx.op("dve", lambda e: e.tensor_tensor_scan(out=t[5][:], data0=rho[:, pr, :], data1=t[2][:],
                                                                 initial=carry[1][:, pr:pr + 1], op0=ALU.mult, op1=ALU.add),
                          reads=["rho", f"t{tq}_2", "carry"], writes=[f"t{tq}_5"])
                    if ch < NCH - 1:
                        cx.op("dve", lambda e: e.tensor_scalar(out=et[1][:, 0:1], in0=t[5][:, S5TC - 1:S5TC], scalar1=es[:, pr:pr + 1],
                                                                scalar2=None, op0=ALU.mult), reads=[f"t{tq}_5", "e"], writes=["et1"])
                        cx.op("dve", lambda e: e.tensor_scalar(out=et[2][:, 0:1], in0=t[4][:, S5TC - 1:S5TC], scalar1=es[:, pr:pr + 1],
                                                                scalar2=None, op0=ALU.mult), reads=[f"t{tq}_4", "e"], writes=["et2"])
                        cx.op("dve", lambda e: e.scalar_tensor_tensor(out=carry[0][:, pr:pr + 1], in0=t[4][:, S5TC - 1:S5TC],
                                                                       scalar=ec[:, pr:pr + 1], in1=et[1][:, 0:1],
                                                                       op0=ALU.mult, op1=ALU.subtract),
                              reads=[f"t{tq}_4", "e", "et1", "carry"], writes=["carry"])
                        cx.op("dve", lambda e: e.scalar_tensor_tensor(out=carry[1][:, pr:pr + 1], in0=t[5][:, S5TC - 1:S5TC],
                                                                       scalar=ec[:, pr:pr + 1], in1=et[2][:, 0:1],
                                                                       op0=ALU.mult, op1=ALU.add),
                              reads=[f"t{tq}_5", "e", "et2", "carry"], writes=["carry"])
                    cx.op("pool", lambda e: e.tensor_tensor(out=t[6][:], in0=t[4][:], in1=ct, op=ALU.mult),
                          reads=[f"t{tq}_4", "tab"], writes=[f"t{tq}_6"])
                    cx.op("pool", lambda e: e.tensor_tensor(out=t[7][:], in0=t[5][:], in1=stb, op=ALU.mult),
                          reads=[f"t{tq}_5", "tab"], writes=[f"t{tq}_7"])
                    cx.op("pool", lambda e: e.tensor_tensor(out=sbf[pp][0][:], in0=t[6][:], in1=t[7][:], op=ALU.subtract),
                          reads=[f"t{tq}_6", f"t{tq}_7"], writes=[f"sbf{pp}0"])
                    cx.op("pool", lambda e: e.tensor_tensor(out=t[6][:], in0=t[4][:], in1=stb, op=ALU.mult),
                          reads=[f"t{tq}_4", "tab", f"t{tq}_6"], writes=[f"t{tq}_6"])
                    cx.op("pool", lambda e: e.tensor_tensor(out=t[7][:], in0=t[5][:], in1=ct, op=ALU.mult),
                          reads=[f"t{tq}_5", "tab", f"t{tq}_7"], writes=[f"t{tq}_7"])
                    cx.op("pool", lambda e: e.tensor_tensor(out=sbf[pp][1][:], in0=t[6][:], in1=t[7][:], op=ALU.add),
                          reads=[f"t{tq}_6", f"t{tq}_7"], writes=[f"sbf{pp}1"])
                py = banks[6]
                for pp in range(4):
                    pr = o * 4 + pp
                    cx.op("pe", lambda e: e.matmul(py[:, 0:S5TC], cre[:, pr, :], sbf[pp][0][:], start=(pp == 0), stop=False),
                          reads=["s5_cre", f"sbf{pp}0"], writes=["bank6"])
                    cx.op("pe", lambda e: e.matmul(py[:, 0:S5TC], cimn[:, pr, :], sbf[pp][1][:], start=False, stop=(pp == 3)),
                          reads=["s5_cim", f"sbf{pp}1"], writes=["bank6"])
                cx.op("dve", lambda e: e.scalar_tensor_tensor(out=gl[0][:], in0=u32[:, c0:c0 + S5TC], scalar=d5[:, o:o + 1],
                                                               in1=py[:, 0:S5TC], op0=ALU.mult, op1=ALU.add),
                      reads=["u32", "s5_dT", "bank6"], writes=["gl0"])
                cx.op("act", lambda e: e.activation(out=gl[1][:], in_=gl[0][:], func=AF.Square), reads=["gl0"], writes=["gl1"])
                cx.op("dve", lambda e: e.tensor_scalar(out=gl[1][:], in0=gl[1][:], scalar1=GELU_C * 0.044715, scalar2=GELU_C,
                                                        op0=ALU.mult, op1=ALU.add), reads=["gl1"], writes=["gl1"])
                cx.op("dve", lambda e: e.tensor_tensor(out=gl[1][:], in0=gl[1][:], in1=gl[0][:], op=ALU.mult),
                      reads=["gl1", "gl0"], writes=["gl1"])
                cx.op("act", lambda e: e.activation(out=gl[2][:], in_=gl[1][:], func=AF.Tanh), reads=["gl1"], writes=["gl2"])
                cx.op("act", lambda e: e.activation(out=gl[3][:], in_=gl[0][:], func=AF.Copy, scale=0.5), reads=["gl0"], writes=["gl3"])
                cx.op("dve", lambda e: e.scalar_tensor_tensor(out=y5[:, c0:c0 + S5TC], in0=gl[2][:], scalar=1.0, in1=gl[3][:],
                                                               op0=ALU.add, op1=ALU.mult),
                      reads=["gl2", "gl3"], writes=["y5"])
            cx.dma("sp", st_slot if o % 2 == 0 else st_slot2,
                   [(io["y_tile"](o) if "y_tile" in io else y5_d[:, o, :], y5[:])], reads=["y5"])
        cx.close_scope()

    if do_ssd:
        cx.open_scope()
        tri = load("tri", [128, 128])
        ones = load("ones128", [128, 128])
        maskneg = load("maskneg", [128, 128])
        cw = load("conv_wT", [128, 16, 4])
        cb = load("conv_bT", [128, 16])
        dtb = load("dt_bias_bc", [128, 16])
        alog = load("a_log_bc", [128, 16])
        dsk = load("ssd_dT", [128, 8])
        onec = cx.sb("onec", [128, 1], F32)
        cx.op("dve", lambda e: e.memset(onec[:], 1.0), writes=["onec"])
        abc = cx.sb("abc", [128, 16], F32)
        cx.op("act", lambda e: e.activation(out=abc[:], in_=alog[:], func=AF.Exp), reads=["a_log_bc"], writes=["abc"])
        cx.op("dve", lambda e: e.tensor_scalar(out=abc[:], in0=abc[:], scalar1=-1.0, scalar2=None, op0=ALU.mult),
              reads=["abc"], writes=["abc"])
        NC_ = SEQ // 128
        dt_all = cx.sb("dt_all", [128, NC_, 16], F32)
        adt = cx.sb("adt", [128, NC_, 16], F32)
        cum = cx.sb("cum", [128, NC_, 16], F32)
        dte = cx.sb("dte", [128, NC_, 16], F32)
        dectot = cx.sb("dectot", [128, NC_, 16], F32)
        dg = [cx.sb(f"dg{i}", [128, 128], F32) for i in range(2)]
        tsm = [cx.sb(f"tsm{i}", [128, 16], F32) for i in range(2)]
        bdt = ws.add(w_d[:, 3584:3600].rearrange("(kt p) c -> p kt c", p=128), KT, 16)
        ws.need(bdt)
        wdt = ws.view(bdt)
        b2 = banks[2]
        for c in range(NC_):
            for kt in range(KT):
                cx.op("pe", lambda e, kt=kt: e.matmul(b2[:, 0:16], hT[:, kt, c * 128:(c + 1) * 128], wdt[:, kt, :],
                                                       start=(kt == 0), stop=(kt == KT - 1)),
                      reads=[ws.key(bdt), "hT"], writes=["bank2"])
            cx.op("dve", lambda e: e.tensor_tensor(out=tsm[0][:], in0=b2[:, 0:16], in1=dtb[:], op=ALU.add),
                  reads=["bank2", "dt_bias_bc"], writes=["tsm0"])
            cx.op("act", lambda e: e.activation(out=tsm[0][:], in_=tsm[0][:], func=AF.Exp), reads=["tsm0"], writes=["tsm0"])
            cx.op("act", lambda e: e.activation(out=dt_all[:, c, :], in_=tsm[0][:], func=AF.Ln, bias=onec[:], scale=1.0),
                  reads=["tsm0", "onec"], writes=["dt_all"])
            cx.op("dve", lambda e: e.tensor_tensor(out=adt[:, c, :], in0=dt_all[:, c, :], in1=abc[:], op=ALU.mult),
                  reads=["dt_all", "abc"], writes=["adt"])
            cx.op("pe", lambda e: e.matmul(banks[3][:, 0:16], tri[:], adt[:, c, :], start=True, stop=True),
                  reads=["tri", "adt"], writes=["bank3"])
            cx.op("pe", lambda e: e.matmul(banks[4][:, 0:16], ones[:], adt[:, c, :], start=True, stop=True),
                  reads=["ones128", "adt"], writes=["bank4"])
            cx.op("act", lambda e: e.activation(out=cum[:, c, :], in_=banks[3][:, 0:16], func=AF.Copy), reads=["bank3"], writes=["cum"])
            cx.op("act", lambda e: e.activation(out=dectot[:, c, :], in_=banks[4][:, 0:16], func=AF.Exp), reads=["bank4"], writes=["dectot"])
            cx.op("dve", lambda e: e.tensor_tensor(out=tsm[1][:], in0=banks[4][:, 0:16], in1=cum[:, c, :], op=ALU.subtract),
                  reads=["bank4", "cum"], writes=["tsm1"])
            cx.op("act", lambda e: e.activation(out=dte[:, c, :], in_=tsm[1][:], func=AF.Exp), reads=["tsm1"], writes=["dte"])

        raw = cx.sb("raw", [128, 4, 3 + SEQ], F32)
        cx.op("dve", lambda e: e.memset(raw[:, :, 0:3], 0.0), writes=["raw"])
        sz = cx.sb("sz", [128, 2, SEQ], BF16)
        cv = [cx.sb("cv0", [128, SEQ], F32)]
        xs32 = cx.sb("xs32", [128, 2, SEQ], F32)
        xsb = cx.sb("xsb", [128, 2, SEQ], BF16)
        BT = cx.sb("BT", [128, SEQ], BF16)
        CT = cx.sb("CT", [128, SEQ], BF16)
        yout = raw[:, 0:2, 3:3 + SEQ]
        car32 = cx.sb("car32", [128, 4, 64], F32)
        carb = cx.sb("carb", [128, 4, 64], BF16)
        Btok = [cx.sb(f"Btok{i}", [128, 128], BF16) for i in range(3)]
        CBs = [cx.sb(f"CBs{i}", [128, 128], F32) for i in range(3)]
        xq = [cx.sb(f"xq{i}", [128, 256], BF16) for i in range(3)]
        xqd = [cx.sb(f"xqd{i}", [128, 256], BF16) for i in range(3)]
        dm32 = [cx.sb(f"dm32{i}", [128, 128], F32) for i in range(2)]
        dmx = [cx.sb(f"dmx{i}", [128, 128], F32) for i in range(2)]
        MT = [[cx.sb(f"MT{i}{j}", [128, 128], BF16) for j in range(4)] for i in range(3)]
        ebc = [cx.sb(f"ebc{i}", [128, 128], F32) for i in range(2)]
        Cs = [[cx.sb(f"Cs{i}{j}", [128, 128], BF16) for j in range(4)] for i in range(3)]
        ytmp = [cx.sb(f"ytmp{i}", [128, 128], F32) for i in range(2)]
        ys_d = None if "y_tile" in io else dout("ysT", [128, 8, SEQ])
        b3, b4, b5 = banks[3], banks[4], banks[5]
        for gg in range(4):
            base = 512 + gg * 768
            bx = ws.add(w_d[:, base:base + 256].rearrange("(kt p) c -> p kt c", p=128), KT, 256)
            bbc = ws.add(w_d[:, base + 256:base + 512].rearrange("(kt p) c -> p kt c", p=128), KT, 256)
            bz = ws.add(w_d[:, base + 512:base + 768].rearrange("(kt p) c -> p kt c", p=128), KT, 256)

            def ev_raw(off):
                def f(ti, tb, ps, key):
                    cx.op("act", lambda e: e.activation(out=raw[:, off + ti, 3 + tb * 512:3 + (tb + 1) * 512], in_=ps[:], func=AF.Copy),
                          reads=[key], writes=["raw"])
                return f

            def ev_z(ti, tb, ps, key):
                cx.op("act", lambda e: e.activation(out=sz[:, ti, tb * 512:(tb + 1) * 512], in_=ps[:], func=AF.Silu),
                      reads=[key], writes=["sz"])
            ws.need(bx)
            proj(bx, 0, 2, ev_raw(0))
            ws.need(bbc)
            proj(bbc, 0, 2, ev_raw(2))
            ws.need(bz)
            proj(bz, 0, 2, ev_z)
            for ti in range(4):
                tidx = gg * 4 + ti
                cvt = cv[0]
                ck = "cv0"
                cx.op("dve", lambda e: e.tensor_scalar(out=cvt[:], in0=raw[:, ti, 0:SEQ], scalar1=cw[:, tidx, 0:1], scalar2=None,
                                                        op0=ALU.mult), reads=["raw", "conv_wT"], writes=[ck])
                for jj in range(1, 4):
                    cx.op("dve", lambda e, jj=jj: e.scalar_tensor_tensor(out=cvt[:], in0=raw[:, ti, jj:jj + SEQ],
                                                                       scalar=cw[:, tidx, jj:jj + 1], in1=cvt[:],
                                                                       op0=ALU.mult, op1=ALU.add),
                          reads=["raw", "conv_wT", ck], writes=[ck])
                if ti < 2:
                    cx.op("act", lambda e: e.activation(out=xs32[:, ti, :], in_=cvt[:], func=AF.Silu, bias=cb[:, tidx:tidx + 1], scale=1.0),
                          reads=[ck, "conv_bT"], writes=["xs32"])
                    cx.op("pool", lambda e: e.tensor_copy(out=xsb[:, ti, :], in_=xs32[:, ti, :]), reads=["xs32"], writes=["xsb"])
                else:
                    dst, dk = (BT, "BT") if ti == 2 else (CT, "CT")
                    cx.op("act", lambda e: e.activation(out=dst[:], in_=cvt[:], func=AF.Silu, bias=cb[:, tidx:tidx + 1], scale=1.0),
                          reads=[ck, "conv_bT"], writes=[dk])
            cx.op("dve", lambda e: e.memset(car32[:], 0.0), reads=["car32"], writes=["car32"])
            cx.op("dve", lambda e: e.memset(carb[:], 0.0), reads=["carb"], writes=["carb"])
            def partA1(c):
                cs_ = slice(c * 128, (c + 1) * 128)
                par = c % 3
                cx.op("pe", lambda e: e.transpose(bankT[:, 0:128], BT[:, cs_], identb[:]), reads=["BT", "identb"], writes=["bankT"])
                cx.op("pe", lambda e: e.transpose(bankT[:, 128:256], xsb[:, 0, cs_], identb[:]), reads=["xsb", "identb"], writes=["bankT"])
                cx.op("pe", lambda e: e.transpose(bankT[:, 256:384], xsb[:, 1, cs_], identb[:]), reads=["xsb", "identb"], writes=["bankT"])
                cx.op("act", lambda e: e.activation(out=Btok[par][:], in_=bankT[:, 0:128], func=AF.Copy),
                      reads=["bankT"], writes=[f"Btok{par}"])
                for hh in range(4):
                    h_ = gg * 4 + hh
                    cx.op("dve", lambda e: e.tensor_scalar(out=xq[par][:, hh * 64:(hh + 1) * 64], in0=bankT[:, 128 + hh * 64:192 + hh * 64],
                                                            scalar1=dt_all[:, c, h_:h_ + 1], scalar2=None, op0=ALU.mult),
                          reads=["bankT", "dt_all"], writes=[f"xq{par}"])
                    cx.op("pool", lambda e: e.tensor_scalar(out=xqd[par][:, hh * 64:(hh + 1) * 64], in0=xq[par][:, hh * 64:(hh + 1) * 64],
                                                             scalar1=dte[:, c, h_:h_ + 1], scalar2=None, op0=ALU.mult),
                          reads=[f"xq{par}", "dte"], writes=[f"xqd{par}"])
                cx.op("pe", lambda e: e.matmul(b3[:, 0:128], BT[:, cs_], CT[:, cs_], start=True, stop=True),
                      reads=["BT", "CT"], writes=["bank3"])
                cx.op("act", lambda e: e.activation(out=CBs[par][:], in_=b3[:, 0:128], func=AF.Copy), reads=["bank3"], writes=[f"CBs{par}"])

            def partA2(c):
                cs_ = slice(c * 128, (c + 1) * 128)
                par = c % 3
                for hh in range(4):
                    h_ = gg * 4 + hh
                    hp = hh % 2
                    crow = banks[hh % 2][:, 0:128]
                    cx.op("act", lambda e: e.activation(out=dg[hp][:], in_=ident[:], func=AF.Identity, scale=cum[:, c, h_:h_ + 1]),
                          reads=["ident", "cum"], writes=[f"dg{hp}"])
                    cx.op("pe", lambda e: e.matmul(crow, ones[:], dg[hp][:], start=True, stop=True),
                          reads=["ones128", f"dg{hp}"], writes=[f"bank{hh % 2}"])
                    cx.op("dve", lambda e: e.scalar_tensor_tensor(out=dm32[hp][:], in0=crow, scalar=cum[:, c, h_:h_ + 1], in1=maskneg[:],
                                                                   op0=ALU.subtract, op1=ALU.add),
                          reads=[f"bank{hh % 2}", "cum", "maskneg"], writes=[f"dm32{hp}"])
                    cx.op("act", lambda e: e.activation(out=dmx[hp][:], in_=dm32[hp][:], func=AF.Exp), reads=[f"dm32{hp}"], writes=[f"dmx{hp}"])
                    cx.op("dve", lambda e: e.tensor_tensor(out=MT[par][hh][:], in0=dmx[hp][:], in1=CBs[par][:], op=ALU.mult),
                          reads=[f"dmx{hp}", f"CBs{par}"], writes=[f"MT{par}{hh}"])
                    cx.op("act", lambda e: e.activation(out=ebc[hp][:], in_=crow, func=AF.Exp), reads=[f"bank{hh % 2}"], writes=[f"ebc{hp}"])
                    cx.op("pool", lambda e: e.tensor_tensor(out=Cs[par][hh][:], in0=CT[:, cs_], in1=ebc[hp][:], op=ALU.mult),
                          reads=["CT", f"ebc{hp}"], writes=[f"Cs{par}{hh}"])

            def partB(c):
                cs_ = slice(c * 128, (c + 1) * 128)
                par = c % 3
                q2 = c % 2
                for hh in range(4):
                    pt, half = hh // 2, hh % 2
                    yo = banks[5 + q2][half * 64:(half + 1) * 64, pt * 128:(pt + 1) * 128]
                    cx.op("pe", lambda e: e.matmul(yo, xq[par][:, hh * 64:(hh + 1) * 64], MT[par][hh][:], start=True, stop=False),
                          reads=[f"xq{par}", f"MT{par}{hh}"], writes=[f"bank{5 + q2}"])
                    cx.op("pe", lambda e: e.matmul(yo, carb[:, hh, :], Cs[par][hh][:], start=False, stop=True),
                          reads=["carb", f"Cs{par}{hh}"], writes=[f"bank{5 + q2}"])
                cx.op("pe", lambda e: e.matmul(banks[2][:, 0:256], Btok[par][:], xqd[par][:], start=True, stop=True),
                      reads=[f"Btok{par}", f"xqd{par}"], writes=["bank2"])
                for hh in range(4):
                    h_ = gg * 4 + hh
                    cx.op("dve", lambda e: e.scalar_tensor_tensor(out=car32[:, hh, :], in0=car32[:, hh, :], scalar=dectot[:, c, h_:h_ + 1],
                                                                   in1=banks[2][:, hh * 64:(hh + 1) * 64], op0=ALU.mult, op1=ALU.add),
                          reads=["car32", "dectot", "bank2"], writes=["car32"])
                cx.op("pool", lambda e: e.tensor_copy(out=carb[:], in_=car32[:]), reads=["car32"], writes=["carb"])
                for pt in range(2):
                    cx.op("dve", lambda e: e.scalar_tensor_tensor(out=ytmp[pt][:], in0=xs32[:, pt, cs_], scalar=dsk[:, gg * 2 + pt:gg * 2 + pt + 1],
                                                                   in1=banks[5 + q2][:, pt * 128:(pt + 1) * 128],
                                                                   op0=ALU.mult, op1=ALU.add),
                          reads=["xs32", "ssd_dT", f"bank{5 + q2}"], writes=[f"ytmp{pt}"])
                    cx.op("pool", lambda e: e.tensor_tensor(out=yout[:, pt, cs_], in0=ytmp[pt][:], in1=sz[:, pt, cs_], op=ALU.mult),
                          reads=[f"ytmp{pt}", "sz"], writes=["raw"])

            for it_ in cx.record(partA1, 0):
                cx.play(it_)
            cx.play_interleaved(cx.record(partA2, 0), cx.record(partA1, 1))
            for c in range(NC_):
                la = cx.record(partB, c)
                lb = cx.record(partA2, c + 1) if c + 1 < NC_ else []
                lc = cx.record(partA1, c + 2) if c + 2 < NC_ else []
                cx.play_interleaved3(la, lb, lc)
            cx.dma("sp", st_slot if gg % 2 == 0 else st_slot2,
                   [((io["y_tile"](4 + gg * 2 + pt) if "y_tile" in io else ys_d[:, gg * 2 + pt, :]), yout[:, pt, :])
                    for pt in range(2)], reads=["raw"])
        cx.close_scope()


def prep_M0(inp, b, j, hT_full):
    m = {"hT": hT_full, "ident": np.eye(128, dtype=np.float32)}
    w = inp["hyb_w_in"][0]
    cols = [np.arange(j * 512, (j + 1) * 512)]
    for gg in range(4):
        G = j * 4 + gg
        cols.append(3072 + G * 256 + np.arange(256))
        cols.append(3072 + 2048 + G * 128 + np.arange(128))
        cols.append(3072 + 3072 + G * 128 + np.arange(128))
        cols.append(1024 + G * 256 + np.arange(256))
    cols.append(7168 + 16 * j + np.arange(16))
    cols = np.concatenate(cols)
    m["w_in_c"] = np.ascontiguousarray(w[:, cols])
    g0 = 32 * j
    lre = np.zeros((128, 16), np.float32)
    lim = np.zeros((128, 16), np.float32)
    ldt = np.zeros((128, 16), np.float32)
    xbre = np.zeros((128, 16, 128), np.float32)
    xbim = np.zeros((128, 16, 128), np.float32)
    cre = np.zeros((128, 16, 128), np.float32)
    cim = np.zeros((128, 16, 128), np.float32)
    for pr in range(16):
        pp = pr % 4
        for gi in range(2):
            g = g0 + 2 * pr + gi
            rows = slice(gi * 64, gi * 64 + 64)
            cs = slice(32 * pp + 16 * gi, 32 * pp + 16 * gi + 16)
            lre[rows, pr] = inp["s5_lambda_re"][0, g]
            lim[rows, pr] = inp["s5_lambda_im"][0, g]
            ldt[rows, pr] = inp["s5_log_dt"][0, g]
            xbre[rows, pr, cs] = inp["s5_b_re"][0, g]
            xbim[rows, pr, cs] = inp["s5_b_im"][0, g]
            cre[rows, pr, cs] = inp["s5_c_re"][0, g].T
            cim[rows, pr, cs] = inp["s5_c_im"][0, g].T
    m.update(s5_lre=lre, s5_lim=lim, s5_ldt=ldt, s5_xbre=xbre, s5_xbim=xbim, s5_cre=cre, s5_cim=cim)
    m["s5_dT"] = np.ascontiguousarray(inp["s5_d"][0, j * 512:(j + 1) * 512].reshape(4, 128).T)
    cwT = np.zeros((128, 16, 4), np.float32)
    cbT = np.zeros((128, 16), np.float32)
    dT = np.zeros((128, 8), np.float32)
    cwf, cbf = inp["ssd_conv_w"][0], inp["ssd_conv_b"][0]
    for gg in range(4):
        G = j * 4 + gg
        chans = [G * 256 + np.arange(128), G * 256 + 128 + np.arange(128),
                 2048 + G * 128 + np.arange(128), 3072 + G * 128 + np.arange(128)]
        for ti in range(4):
            cwT[:, gg * 4 + ti, :] = cwf[:, chans[ti]].T
            cbT[:, gg * 4 + ti] = cbf[chans[ti]]
        for pt in range(2):
            heads = (G * 256 + pt * 128 + np.arange(128)) // 64
            dT[:, gg * 2 + pt] = inp["ssd_d"][0][heads]
    hs = slice(16 * j, 16 * j + 16)
    m.update(conv_wT=cwT, conv_bT=cbT, ssd_dT=dT,
             dt_bias_bc=np.ascontiguousarray(np.broadcast_to(inp["ssd_dt_bias"][0, hs], (128, 16))),
             a_log_bc=np.ascontiguousarray(np.broadcast_to(inp["ssd_a_log"][0, hs], (128, 16))))
    tri = np.triu(np.ones((128, 128), np.float32))
    m["tri"] = tri
    m["ones128"] = np.ones((128, 128), np.float32)
    m["maskneg"] = np.where(np.arange(128)[None, :] >= np.arange(128)[:, None], 0.0, -30000.0).astype(np.float32)
    sel = np.zeros((16, 16, 128), np.float32)
    for h_ in range(16):
        sel[h_, h_, :] = 1.0
    m["sel"] = sel.reshape(16, 16 * 128)
    return m


RC = 64
LD_C = 0.6065306597126334
GN_EPS = 64e-5


def build_M1(G=None, io=None):
    nc, cx, dram, banks, bankT = _mk_env(G)
    io = io or {}
    cx.open_scope()

    def din(name, shape, dtype=F32):
        if name in io:
            return io[name]
        if name not in dram:
            dram[name] = nc.dram_tensor(name, list(shape), dtype, kind="ExternalInput").ap()
        return dram[name]

    def dout(name, shape, dtype=F32):
        if name in io:
            return io[name]
        dram[name] = nc.dram_tensor(name, list(shape), dtype, kind="ExternalOutput").ap()
        return dram[name]

    BF16S = "bf16_from_f32"

    def load(name, shape, dtype=F32, q="sp"):
        sdt, ddt = (BF16, F32) if dtype == BF16S else (dtype, dtype)
        t = cx.sb(name, shape, sdt)
        sl = cx.slot(name)
        pairs = cast_pairs(t[:], din(name, shape, ddt)) if dtype == BF16S else [(t[:], din(name, shape, ddt))]
        cx.dma(q, sl, pairs, writes=[name])
        return t

    hb = cx.sb("hbuf", [128, KT, SEQ + 1], BF16)
    cx.op("dve", lambda e: e.memset(hb[:, :, 0:1], 0.0), writes=["hb"])
    sl = cx.slot("hT")
    if "hT_pairs" in io:
        cx.dma("sp", sl, io["hT_pairs"](hb, 1), reads=["hb"] + io.get("dep", []), writes=["hb"])
    else:
        hT_d = din("hT", [128, KT, SEQ], BF16)
        cx.dma("sp", sl, [(hb[:, 4 * i:4 * i + 4, 1:SEQ + 1], hT_d[:, 4 * i:4 * i + 4, :]) for i in range(4)],
               reads=["hb"], writes=["hb"])
    ws = WStream(cx, nslot=2, elems=2048)
    ident = load("ident", [128, 128])
    identb = cx.sb("identb", [128, 128], BF16)
    cx.op("dve", lambda e: e.tensor_copy(out=identb[:], in_=ident[:]), reads=["ident"], writes=["identb"])
    mask3 = load("mask3", [128, 384])
    blockones = load("blockones", [128, 128])
    resetm = load("resetmask", [128, SEQ], BF16S, q="pool")
    muT = load("muT", [128, 6, KT])
    w0T = load("w0T", [128, 8])
    a0T = load("a0T", [128, 8])
    kkT = load("k_kT", [128, 8])
    kaT = load("k_aT", [128, 8])
    rkT = load("r_kT", [128, 8])
    lng = load("lng_stack", [128, 8, 64])
    lnb = load("lnb_stack", [128, 8, 64])
    w2c = load("w2c", [96, 1024], BF16S, q="pool")
    a2c = load("a2c", [96, 1024], BF16S, q="pool")
    g2c = load("g2c", [128, 2, 1024], BF16S, q="pool")
    onesb = cx.sb("onesb", [128, 2], BF16)
    cx.op("dve", lambda e: e.memset(onesb[:], 1.0), writes=["onesb"])
    epsg = cx.sb("epsg", [128, 1], F32)
    cx.op("dve", lambda e: e.memset(epsg[:], GN_EPS), writes=["epsg"])
    st_slots = [cx.slot("st0"), cx.slot("st1")]

    wder = [[cx.sb(f"wd{i}{j}", [128, KT, 128], BF16) for j in range(2)] for i in range(2)]
    nder = [0]

    def derive(blk, mu_i, ncol):
        i = nder[0] % 2
        nder[0] += 1
        wv = ws.view(blk)
        w1_, w2_ = wder[i][0], wder[i][1]
        for kt in range(KT):
            eng = "dve" if kt % 2 == 0 else "pool"
            cx.op(eng, lambda e, kt=kt: e.tensor_scalar(out=w2_[:, kt, 0:ncol], in0=wv[:, kt, :], scalar1=muT[:, mu_i, kt:kt + 1],
                                                        scalar2=None, op0=ALU.mult),
                  reads=[ws.key(blk), "muT"], writes=[f"wd{i}1"])
        cx.op("pool", lambda e: e.tensor_tensor(out=w1_[:, :, 0:ncol], in0=wv[:], in1=w2_[:, :, 0:ncol], op=ALU.subtract),
              reads=[ws.key(blk), f"wd{i}1"], writes=[f"wd{i}0"])
        return w1_, w2_, f"wd{i}0", f"wd{i}1"

    pj = [0]

    def proj2(der, ncol, evac):
        w1_, w2_, k1, k2 = der
        for tb in range(4):
            bk = pj[0] % 2
            pj[0] += 1
            for kt in range(KT):
                cx.op("pe", lambda e, kt=kt: e.matmul(banks[bk][0:ncol, :], w1_[:, kt, 0:ncol], hb[:, kt, 1 + tb * 512:1 + (tb + 1) * 512],
                                                       start=(kt == 0), stop=False),
                      reads=[k1, "hb"], writes=[f"bank{bk}"])
            for kt in range(KT):
                cx.op("pe", lambda e, kt=kt: e.matmul(banks[bk][0:ncol, :], w2_[:, kt, 0:ncol], hb[:, kt, tb * 512:(tb + 1) * 512],
                                                       start=False, stop=(kt == KT - 1)),
                      reads=[k2, "hb"], writes=[f"bank{bk}"])
            evac(tb, banks[bk], f"bank{bk}")

    def wblock(name, shape_cols, c0, ncol):
        src = din(name, [D, shape_cols])
        return ws.add(src[:, c0:c0 + ncol].rearrange("(kt p) c -> p kt c", p=128), KT, ncol)

    tw = cx.sb("tw", [96, SEQ], BF16)
    ta = cx.sb("ta", [96, SEQ], BF16)
    tg = cx.sb("tg", [128, 2, SEQ], BF16)
    b_w1 = wblock("w1", 96, 0, 96)
    b_a1 = wblock("a1", 96, 0, 96)
    b_g1 = [wblock("g1", 256, i * 128, 128) for i in range(2)]
    ws.need(b_w1)
    proj2(derive(b_w1, 1, 96), 96, lambda tb, ps, key: cx.op(
        "act", lambda e: e.activation(out=tw[:, tb * 512:(tb + 1) * 512], in_=ps[0:96, :], func=AF.Tanh), reads=[key], writes=["tw"]))
    ws.need(b_a1)
    proj2(derive(b_a1, 4, 96), 96, lambda tb, ps, key: cx.op(
        "act", lambda e: e.activation(out=ta[:, tb * 512:(tb + 1) * 512], in_=ps[0:96, :], func=AF.Copy), reads=[key], writes=["ta"]))
    for i in range(2):
        ws.need(b_g1[i])
        proj2(derive(b_g1[i], 5, 128), 128, lambda tb, ps, key, i=i: cx.op(
            "act", lambda e: e.activation(out=tg[:, i, tb * 512:(tb + 1) * 512], in_=ps[:], func=AF.Sigmoid), reads=[key], writes=["tg"]))

    r_bf = cx.sb("r_bf", [128, SEQ], BF16)
    k32 = cx.sb("k32", [128, SEQ], F32)
    v_bf = cx.sb("v_bf", [128, SEQ], BF16)
    a32 = cx.sb("a32", [128, SEQ], F32)
    kk32 = cx.sb("kk32", [128, SEQ], F32)
    ld32 = cx.sb("ld32", [128, SEQ], F32)
    cl32 = cx.sb("cl32", [128, SEQ], F32)
    ecl = cx.sb("ecl", [128, SEQ], F32)
    g_bf = cx.sb("g_bf", [128, SEQ], BF16)
    yg = cx.sb("yg", [128, SEQ], BF16)
    sqt = [cx.sb("sqt0", [128, 512], F32)] * 2
    Ear = [cx.sb(f"Ear{i}", [128, 256], BF16) for i in range(3)]
    Eb = [cx.sb(f"Eb{i}", [128, 128], BF16) for i in range(3)]
    Ek = [cx.sb(f"Ek{i}", [128, 128], BF16) for i in range(3)]
    Ez = [cx.sb(f"Ez{i}", [128, 128], BF16) for i in range(3)]
    for i in range(3):
        for t_, k_ in ((Ear[i], f"Ear{i}"), (Eb[i], f"Eb{i}"), (Ek[i], f"Ek{i}"), (Ez[i], f"Ez{i}")):
            cx.op("pool", lambda e, t_=t_: e.memset(t_[:], 0.0), writes=[k_])
    EbkT = [cx.sb(f"EbkT{i}", [128, 256], BF16) for i in range(3)]
    Pm = [[cx.sb(f"Pm{q}{i}", [128, 128], F32) for i in range(2)] for q in range(2)]
    PTm = [[cx.sb(f"PTm{q}{i}", [128, 128], F32) for i in range(2)] for q in range(2)]
    Rm = [cx.sb(f"Rm{q}", [128, 128], F32) for q in range(2)]
    Rb = [cx.sb(f"Rb{i}", [128, 128], BF16) for i in range(3)]
    Arb = [cx.sb(f"Arb{i}", [128, 128], BF16) for i in range(3)]
    Aak_rk = [cx.sb(f"Aakrk{i}", [128, 256], BF16) for i in range(3)]
    Vs = [cx.sb(f"Vs{i}", [128, 64], BF16) for i in range(3)]
    Xb = cx.sb("Xb", [128, 64], BF16)
    Ub = cx.sb("Ub", [128, 64], BF16)
    S32 = cx.sb("S32", [128, 64], F32)
    S0b = cx.sb("S0b", [128, 64], BF16)
    ys = cx.sb("ys", [128, 64], F32)
    ysq = cx.sb("ysq", [128, 64], F32)
    yn = cx.sb("yn", [128, 64], F32)
    yob = cx.sb("yob", [128, 64], BF16)
    stat = cx.sb("stat", [128, 8], F32)
    bon = cx.sb("bon", [128, 2], F32)
    yg_d = None if "yg_tile" in io else dout("ygT", [128, 8, SEQ], BF16)

    for P in range(8):
        c0 = P * 128
        b_r = wblock("wr_c", 1024, c0, 128)
        b_k = wblock("wk_c", 1024, c0, 128)
        b_v = wblock("wv_c", 1024, c0, 128)
        ws.need(b_r)
        proj2(derive(b_r, 0, 128), 128, lambda tb, ps, key: cx.op(
            "act", lambda e: e.activation(out=r_bf[:, tb * 512:(tb + 1) * 512], in_=ps[:], func=AF.Copy), reads=[key], writes=["r_bf"]))
        ws.need(b_k)
        proj2(derive(b_k, 2, 128), 128, lambda tb, ps, key: cx.op(
            "act", lambda e: e.activation(out=k32[:, tb * 512:(tb + 1) * 512], in_=ps[:], func=AF.Copy), reads=[key], writes=["k32"]))
        ws.need(b_v)
        proj2(derive(b_v, 3, 128), 128, lambda tb, ps, key: cx.op(
            "act", lambda e: e.activation(out=v_bf[:, tb * 512:(tb + 1) * 512], in_=ps[:], func=AF.Copy), reads=[key], writes=["v_bf"]))
        for tb in range(4):
            ts_ = slice(tb * 512, (tb + 1) * 512)
            bk = pj[0] % 2
            pj[0] += 1
            cx.op("pe", lambda e: e.matmul(banks[bk][:], w2c[:, c0:c0 + 128], tw[:, ts_], start=True, stop=True),
                  reads=["w2c", "tw"], writes=[f"bank{bk}"])
            cx.op("act", lambda e: e.activation(out=ld32[:, ts_], in_=banks[bk][:], func=AF.Sigmoid, bias=w0T[:, P:P + 1], scale=1.0),
                  reads=[f"bank{bk}", "w0T"], writes=["ld32"])
            bk = pj[0] % 2
            pj[0] += 1
            cx.op("pe", lambda e: e.matmul(banks[bk][:], a2c[:, c0:c0 + 128], ta[:, ts_], start=True, stop=True),
                  reads=["a2c", "ta"], writes=[f"bank{bk}"])
            cx.op("act", lambda e: e.activation(out=a32[:, ts_], in_=banks[bk][:], func=AF.Sigmoid, bias=a0T[:, P:P + 1], scale=1.0),
                  reads=[f"bank{bk}", "a0T"], writes=["a32"])
            bk = pj[0] % 2
            pj[0] += 1
            for i in range(2):
                cx.op("pe", lambda e, i=i: e.matmul(banks[bk][:], g2c[:, i, c0:c0 + 128], tg[:, i, ts_], start=(i == 0), stop=(i == 1)),
                      reads=["g2c", "tg"], writes=[f"bank{bk}"])
            cx.op("act", lambda e: e.activation(out=g_bf[:, ts_], in_=banks[bk][:], func=AF.Copy), reads=[f"bank{bk}"], writes=["g_bf"])
        cx.op("dve", lambda e: e.tensor_scalar(out=ld32[:], in0=ld32[:], scalar1=-LD_C, scalar2=None, op0=ALU.mult),
              reads=["ld32"], writes=["ld32"])
        cx.op("dve", lambda e: e.tensor_scalar(out=kk32[:], in0=k32[:], scalar1=kkT[:, P:P + 1], scalar2=None, op0=ALU.mult),
              reads=["k32", "k_kT"], writes=["kk32"])
        for tb in range(4):
            ts_ = slice(tb * 512, (tb + 1) * 512)
            sq_, sk = sqt[0], "sqt0"
            cx.op("act", lambda e: e.activation(out=sq_[:], in_=kk32[:, ts_], func=AF.Square), reads=["kk32"], writes=[sk])
            bk = pj[0] % 2
            pj[0] += 1
            cx.op("pe", lambda e: e.matmul(banks[bk][:], blockones[:], sq_[:], start=True, stop=True),
                  reads=["blockones", sk], writes=[f"bank{bk}"])
            cx.op("act", lambda e: e.activation(out=sq_[:], in_=banks[bk][:], func=AF.Sqrt), reads=[f"bank{bk}"], writes=[sk])
            cx.op("dve", lambda e: e.tensor_scalar(out=sq_[:], in0=sq_[:], scalar1=1e-12, scalar2=None, op0=ALU.max), reads=[sk], writes=[sk])
            cx.op("dve", lambda e: e.reciprocal(out=sq_[:], in_=sq_[:]), reads=[sk], writes=[sk])
            cx.op("dve", lambda e: e.tensor_tensor(out=kk32[:, ts_], in0=kk32[:, ts_], in1=sq_[:], op=ALU.mult),
                  reads=["kk32", sk], writes=["kk32"])
        cx.op("dve", lambda e: e.tensor_scalar(out=ecl[:], in0=a32[:], scalar1=-1.0, scalar2=kaT[:, P:P + 1], op0=ALU.add, op1=ALU.mult),
              reads=["a32", "k_aT"], writes=["ecl"])
        cx.op("dve", lambda e: e.scalar_tensor_tensor(out=k32[:], in0=ecl[:], scalar=1.0, in1=k32[:], op0=ALU.add, op1=ALU.mult),
              reads=["ecl", "k32"], writes=["k32"])
        cx.op("pool", lambda e: e.tensor_tensor(out=a32[:], in0=a32[:], in1=kk32[:], op=ALU.mult), reads=["a32", "kk32"], writes=["a32"])
        cx.op("dve", lambda e: e.tensor_tensor_scan(out=cl32[:], data0=resetm[:], data1=ld32[:], initial=0.0, op0=ALU.mult, op1=ALU.add),
              reads=["resetmask", "ld32"], writes=["cl32"])
        cx.op("pool", lambda e: e.tensor_tensor(out=ld32[:], in0=cl32[:], in1=ld32[:], op=ALU.subtract), reads=["cl32", "ld32"], writes=["ld32"])
        cx.op("act", lambda e: e.activation(out=ld32[:], in_=ld32[:], func=AF.Exp), reads=["ld32"], writes=["ld32"])
        cx.op("act", lambda e: e.activation(out=ecl[:], in_=cl32[:], func=AF.Exp), reads=["cl32", "ecl"], writes=["ecl"])
        cx.op("act", lambda e: e.activation(out=cl32[:], in_=cl32[:], func=AF.Exp, scale=-1.0), reads=["cl32"], writes=["cl32"])
        eclm, encl, beta, kfin = ld32, cl32, a32, k32
        cx.op("dve", lambda e: e.memset(S32[:], 0.0), reads=["S32"], writes=["S32"])
        cx.op("dve", lambda e: e.memset(S0b[:], 0.0), reads=["S0b"], writes=["S0b"])
        def part1a(c):
            cs = slice(c * RC, (c + 1) * RC)
            par = c % 3
            q2 = c % 2
            for hd in range(2):
                R_ = slice(hd * 64, hd * 64 + 64)
                e1, e2 = ("dve", "pool") if hd == 0 else ("pool", "dve")
                cx.op("dve", lambda e: e.scalar_tensor_tensor(out=Ear[par][R_, hd * 64:hd * 64 + 64], in0=kk32[R_, cs], scalar=-1.0, in1=eclm[R_, cs],
                                                               op0=ALU.mult, op1=ALU.mult),
                      reads=["kk32", "ld32"], writes=[f"Ear{par}"])
                cx.op(e2, lambda e: e.tensor_tensor(out=Ear[par][R_, 128 + hd * 64:128 + hd * 64 + 64], in0=r_bf[R_, cs], in1=ecl[R_, cs], op=ALU.mult),
                      reads=["r_bf", "ecl"], writes=[f"Ear{par}"])
                cx.op(e1, lambda e: e.tensor_tensor(out=Eb[par][R_, hd * 64:hd * 64 + 64], in0=beta[R_, cs], in1=encl[R_, cs], op=ALU.mult),
                      reads=["a32", "cl32"], writes=[f"Eb{par}"])
                cx.op(e2, lambda e: e.tensor_tensor(out=Ek[par][R_, hd * 64:hd * 64 + 64], in0=kfin[R_, cs], in1=encl[R_, cs], op=ALU.mult),
                      reads=["k32", "cl32"], writes=[f"Ek{par}"])
                cx.op("dve", lambda e: e.scalar_tensor_tensor(out=Ez[par][R_, hd * 64:hd * 64 + 64], in0=kfin[R_, cs], scalar=rkT[R_, P:P + 1],
                                                               in1=r_bf[R_, cs], op0=ALU.mult, op1=ALU.mult),
                      reads=["k32", "r_kT", "r_bf"], writes=[f"Ez{par}"])
                cx.op("pe", lambda e: e.transpose(bankT[R_, 0:64], v_bf[R_, cs], identb[R_, hd * 64:hd * 64 + 64]),
                      reads=["v_bf", "identb"], writes=["bankT"])
            cx.op("act", lambda e: e.activation(out=Vs[par][:], in_=bankT[:, 0:64], func=AF.Copy), reads=["bankT"], writes=[f"Vs{par}"])
            cx.op("pe", lambda e: e.matmul(banks[2][:, 0:256], Eb[par][:], Ear[par][:], start=True, stop=True),
                  reads=[f"Eb{par}", f"Ear{par}"], writes=["bank2"])
            cx.op("pe", lambda e: e.matmul(banks[3][:, 0:256], Ek[par][:], Ear[par][:], start=True, stop=True),
                  reads=[f"Ek{par}", f"Ear{par}"], writes=["bank3"])
            cx.op("pe", lambda e: e.matmul(banks[2][:, 256:384], Ear[par][:, 0:128], Eb[par][:], start=True, stop=True),
                  reads=[f"Eb{par}", f"Ear{par}"], writes=["bank2"])
            cx.op("pe", lambda e: e.transpose(bankT[:, 128:256], Eb[par][:], identb[:]), reads=[f"Eb{par}", "identb"], writes=["bankT"])
            cx.op("pe", lambda e: e.transpose(bankT[:, 256:384], Ek[par][:], identb[:]), reads=[f"Ek{par}", "identb"], writes=["bankT"])
            cx.op("dve", lambda e: e.tensor_tensor(out=Pm[q2][0][:], in0=banks[2][:, 0:128], in1=mask3[:, 0:128], op=ALU.mult),
                  reads=["bank2", "mask3"], writes=[f"Pm{q2}0"])
            cx.op("dve", lambda e: e.tensor_tensor(out=Arb[par][:], in0=banks[2][:, 128:256], in1=mask3[:, 128:256], op=ALU.mult),
                  reads=["bank2", "mask3"], writes=[f"Arb{par}"])
            cx.op("dve", lambda e: e.tensor_tensor(out=Aak_rk[par][:], in0=banks[3][:, 0:256], in1=mask3[:, 0:256], op=ALU.mult),
                  reads=["bank3", "mask3"], writes=[f"Aakrk{par}"])
            cx.op("dve", lambda e: e.tensor_tensor(out=PTm[q2][0][:], in0=banks[2][:, 256:384], in1=mask3[:, 256:384], op=ALU.mult),
                  reads=["bank2", "mask3"], writes=[f"PTm{q2}0"])
            cx.op("act", lambda e: e.activation(out=EbkT[par][:], in_=bankT[:, 128:384], func=AF.Copy), reads=["bankT"], writes=[f"EbkT{par}"])
            cx.op("pool", lambda e: e.tensor_tensor(out=Rm[q2][:], in0=Pm[q2][0][:], in1=ident[:], op=ALU.add), reads=[f"Pm{q2}0", "ident"], writes=[f"Rm{q2}"])
        def part1b(c):
            par = c % 3
            q2 = c % 2
            cur = 0
            for lvl in range(1, 6):
                nxt = 1 - cur
                if lvl < 5:
                    cx.op("pe", lambda e: e.matmul(banks[5][:, 0:128], PTm[q2][cur][:], Pm[q2][cur][:], start=True, stop=True),
                          reads=[f"PTm{q2}{cur}", f"Pm{q2}{cur}"], writes=["bank5"])
                cx.op("pe", lambda e: e.matmul(banks[6][:, 0:128], Pm[q2][cur][:], PTm[q2][cur][:], start=True, stop=True),
                      reads=[f"PTm{q2}{cur}", f"Pm{q2}{cur}"], writes=["bank6"])
                if lvl < 5:
                    cx.op("act", lambda e: e.activation(out=Pm[q2][nxt][:], in_=banks[5][:, 0:128], func=AF.Copy),
                          reads=["bank5"], writes=[f"Pm{q2}{nxt}"])
                cx.op("dve", lambda e: e.tensor_copy(out=PTm[q2][nxt][:], in_=banks[6][:, 0:128]), reads=["bank6"], writes=[f"PTm{q2}{nxt}"])
                cx.op("pe", lambda e: e.matmul(banks[4][:, 0:128], PTm[q2][nxt][:], Rm[q2][:], start=True, stop=True),
                      reads=[f"PTm{q2}{nxt}", f"Rm{q2}"], writes=["bank4"])
                cx.op("dve", lambda e: e.tensor_tensor(out=Rm[q2][:], in0=Rm[q2][:], in1=banks[4][:, 0:128], op=ALU.add),
                      reads=[f"Rm{q2}", "bank4"], writes=[f"Rm{q2}"])
                cur = nxt
            cx.op("act", lambda e: e.activation(out=Rb[par][:], in_=Rm[q2][:], func=AF.Copy), reads=[f"Rm{q2}"], writes=[f"Rb{par}"])

        def part2(c):
            cs = slice(c * RC, (c + 1) * RC)
            par = c % 3
            cx.op("pe", lambda e: e.matmul(banks[0][:, 0:64], Ear[par][:, 0:128], S0b[:], start=True, stop=False),
                  reads=[f"Ear{par}", "S0b"], writes=["bank0"])
            cx.op("pe", lambda e: e.matmul(banks[0][:, 0:64], Aak_rk[par][:, 0:128], Vs[par][:], start=False, stop=True),
                  reads=[f"Aakrk{par}", f"Vs{par}"], writes=["bank0"])
            cx.op("act", lambda e: e.activation(out=Xb[:], in_=banks[0][:, 0:64], func=AF.Copy), reads=["bank0"], writes=["Xb"])
            cx.op("pe", lambda e: e.matmul(banks[1][:, 0:64], Rb[par][:], Xb[:], start=True, stop=True), reads=[f"Rb{par}", "Xb"], writes=["bank1"])
            cx.op("act", lambda e: e.activation(out=Ub[:], in_=banks[1][:, 0:64], func=AF.Copy), reads=["bank1"], writes=["Ub"])
            cx.op("pe", lambda e: e.matmul(banks[0][:, 0:64], Ear[par][:, 128:256], S0b[:], start=True, stop=False),
                  reads=[f"Ear{par}", "S0b"], writes=["bank0"])
            cx.op("pe", lambda e: e.matmul(banks[0][:, 0:64], Arb[par][:], Ub[:], start=False, stop=False), reads=[f"Arb{par}", "Ub"], writes=["bank0"])
            cx.op("pe", lambda e: e.matmul(banks[0][:, 0:64], Aak_rk[par][:, 128:256], Vs[par][:], start=False, stop=True),
                  reads=[f"Aakrk{par}", f"Vs{par}"], writes=["bank0"])
            cx.op("pe", lambda e: e.matmul(banks[0][:, 64:66], Ez[par][:], onesb[:], start=True, stop=True),
                  reads=[f"Ez{par}", "onesb"], writes=["bank0"])
            cx.op("pe", lambda e: e.matmul(banks[1][:, 0:64], EbkT[par][:, 0:128], Ub[:], start=True, stop=False), reads=[f"EbkT{par}", "Ub"], writes=["bank1"])
            cx.op("pe", lambda e: e.matmul(banks[1][:, 0:64], EbkT[par][:, 128:256], Vs[par][:], start=False, stop=True), reads=[f"EbkT{par}", f"Vs{par}"], writes=["bank1"])
            cx.op("dve", lambda e: e.tensor_tensor(out=S32[:], in0=S32[:], in1=banks[1][:, 0:64], op=ALU.add), reads=["S32", "bank1"], writes=["S32"])
            wc = ecl[:, c * RC + RC - 1:c * RC + RC]
            cx.op("dve", lambda e: e.tensor_scalar(out=S32[:], in0=S32[:], scalar1=wc, scalar2=None, op0=ALU.mult),
                  reads=["S32", "ecl"], writes=["S32"])
            cx.op("pool", lambda e: e.tensor_copy(out=S0b[:], in_=S32[:]), reads=["S32"], writes=["S0b"])
            cx.op("act", lambda e: e.activation(out=ys[:], in_=banks[0][:, 0:64], func=AF.Copy, accum_out=stat[:, 0:1]),
                  reads=["bank0"], writes=["ys", "stat"])
            cx.op("act", lambda e: e.activation(out=ysq[:], in_=ys[:], func=AF.Square, accum_out=stat[:, 1:2]),
                  reads=["ys"], writes=["ysq", "stat"])
            cx.op("dve", lambda e: e.tensor_scalar(out=stat[:, 2:3], in0=stat[:, 0:1], scalar1=1.0 / 64, scalar2=None, op0=ALU.mult),
                  reads=["stat"], writes=["stat"])
            cx.op("dve", lambda e: e.tensor_tensor(out=stat[:, 3:4], in0=stat[:, 2:3], in1=stat[:, 2:3], op=ALU.mult),
                  reads=["stat"], writes=["stat"])
            cx.op("dve", lambda e: e.scalar_tensor_tensor(out=stat[:, 4:5], in0=stat[:, 1:2], scalar=1.0 / 64, in1=stat[:, 3:4],
                                                           op0=ALU.mult, op1=ALU.subtract), reads=["stat"], writes=["stat"])
            cx.op("act", lambda e: e.activation(out=stat[:, 5:6], in_=stat[:, 4:5], func=AF.Sqrt, bias=epsg[:], scale=1.0),
                  reads=["stat", "epsg"], writes=["stat"])
            cx.op("dve", lambda e: e.reciprocal(out=stat[:, 5:6], in_=stat[:, 5:6]), reads=["stat"], writes=["stat"])
            cx.op("dve", lambda e: e.tensor_scalar(out=yn[:], in0=ys[:], scalar1=stat[:, 2:3], scalar2=stat[:, 5:6],
                                                    op0=ALU.subtract, op1=ALU.mult), reads=["ys", "stat"], writes=["yn"])
            cx.op("pool", lambda e: e.tensor_tensor(out=yn[:], in0=yn[:], in1=lng[:, P, :], op=ALU.mult), reads=["yn", "lng_stack"], writes=["yn"])
            cx.op("pool", lambda e: e.tensor_tensor(out=yn[:], in0=yn[:], in1=lnb[:, P, :], op=ALU.add), reads=["yn", "lnb_stack"], writes=["yn"])
            cx.op("act", lambda e: e.activation(out=bon[:], in_=banks[0][:, 64:66], func=AF.Copy), reads=["bank0"], writes=["bon"])
            cx.op("dve", lambda e: e.scalar_tensor_tensor(out=yob[:], in0=Vs[par][:], scalar=bon[:, 0:1], in1=yn[:], op0=ALU.mult, op1=ALU.add),
                  reads=[f"Vs{par}", "bon", "yn"], writes=["yob"])
            for hd in range(2):
                R_ = slice(hd * 64, hd * 64 + 64)
                cx.op("pe", lambda e: e.transpose(bankT[R_, 512:576], yob[R_, :], identb[R_, hd * 64:hd * 64 + 64]),
                      reads=["yob", "identb"], writes=["bankT"])
            cx.op("dve", lambda e: e.tensor_tensor(out=yg[:, cs], in0=bankT[:, 512:576], in1=g_bf[:, cs], op=ALU.mult),
                  reads=["bankT", "g_bf"], writes=["yg"])
        NCH_ = SEQ // RC
        for it in cx.record(part1a, 0):
            cx.play(it)
        cx.play_interleaved(cx.record(part1b, 0), cx.record(part1a, 1))
        for c in range(NCH_):
            la = cx.record(part2, c)
            lb = cx.record(part1b, c + 1) if c + 1 < NCH_ else []
            lc = cx.record(part1a, c + 2) if c + 2 < NCH_ else []
            cx.play_interleaved3(la, lb, lc)
        cx.dma("sp", st_slots[P % 2], [(io["yg_tile"](P) if "yg_tile" in io else yg_d[:, P, :], yg[:])], reads=["yg"])
    cx.close_scope()
    cx.wait_all("sp")
    return nc


def prep_M1(inp, b, j, hT_full):
    m = {"hT": hT_full, "ident": np.eye(128, dtype=np.float32)}
    cs = slice(j * 1024, (j + 1) * 1024)
    m["wr_c"] = np.ascontiguousarray(inp["rwkv_w_r"][0][:, cs])
    m["wk_c"] = np.ascontiguousarray(inp["rwkv_w_k"][0][:, cs])
    m["wv_c"] = np.ascontiguousarray(inp["rwkv_w_v"][0][:, cs])
    m["w1"] = inp["rwkv_w1"][0]
    m["a1"] = inp["rwkv_a1"][0]
    m["g1"] = inp["rwkv_g1"][0]
    m["w2c"] = np.ascontiguousarray(inp["rwkv_w2"][0][:, cs])
    m["a2c"] = np.ascontiguousarray(inp["rwkv_a2"][0][:, cs])
    m["g2c"] = np.ascontiguousarray(inp["rwkv_g2"][0][:, cs].reshape(2, 128, 1024).transpose(1, 0, 2))
    m["muT"] = np.ascontiguousarray(inp["rwkv_mu"][0].reshape(6, KT, 128).transpose(2, 0, 1))
    for nm, key in (("w0T", "rwkv_w0"), ("a0T", "rwkv_a0"), ("k_kT", "rwkv_k_k"), ("k_aT", "rwkv_k_a")):
        m[nm] = np.ascontiguousarray(inp[key][0][cs].reshape(8, 128).T)
    m["r_kT"] = np.ascontiguousarray(inp["rwkv_r_k"][0].reshape(-1)[cs].reshape(8, 128).T)
    lg = inp["rwkv_ln_g"][0][cs].reshape(8, 2, 64)
    lb = inp["rwkv_ln_b"][0][cs].reshape(8, 2, 64)
    lng = np.zeros((128, 8, 64), np.float32)
    lnb = np.zeros((128, 8, 64), np.float32)
    for hd in range(2):
        lng[hd * 64:(hd + 1) * 64] = lg[None, :, hd, :]
        lnb[hd * 64:(hd + 1) * 64] = lb[None, :, hd, :]
    m["lng_stack"], m["lnb_stack"] = lng, lnb
    s_ = np.arange(64)
    blk = np.kron(np.eye(2, dtype=np.float32), np.ones((64, 64), np.float32))
    mS = np.kron(np.eye(2, dtype=np.float32), (s_[:, None] < s_[None, :]).astype(np.float32))
    mI = np.kron(np.eye(2, dtype=np.float32), (s_[:, None] <= s_[None, :]).astype(np.float32))
    m["mask3"] = np.ascontiguousarray(np.concatenate([mS, mI, mS.T], axis=1))
    m["blockones"] = blk
    rm = np.ones((128, SEQ), np.float32)
    rm[:, ::RC] = 0.0
    m["resetmask"] = rm
    return m


def _T_maps(inp, stages, xT, extra):
    maps = []
    for core in range(NCORES):
        b = core // 2
        m = {"xT": xT[core], "cT": fm(inp["c"][b])}
        for stg in stages:
            kind = stg["kind"]
            if kind == "ffn":
                l, s_ = stg["l"], stg["s"]
                fi = 0 if s_ == 0 else 1
                m[f"w_mod{l}"] = inp["w_mod"][l]
                m[f"b_modT{l}"] = fm(inp["b_mod"][l])
                m[f"norm_gT{l}{s_}"] = fm(inp["norm_g"][l, s_])
                m[f"ffn_w1_{l}{fi}"] = inp["ffn_w1"][l, fi]
                m[f"ffn_w3_{l}{fi}"] = inp["ffn_w3"][l, fi]
                m[f"ffn_w2_{l}{fi}"] = inp["ffn_w2"][l, fi]
            elif kind == "h_out":
                l = stg["l"]
                m[f"w_mod{l}"] = inp["w_mod"][l]
                m[f"b_modT{l}"] = fm(inp["b_mod"][l])
                m[f"norm_gT{l}1"] = fm(inp["norm_g"][l, 1])
            elif kind == "mix0_post":
                m["w_mod0"] = inp["w_mod"][0]
                m["b_modT0"] = fm(inp["b_mod"][0])
                m["glu_bT"] = fm(inp["s5_glu_b"][0])
                m["ssd_norm_gT"] = fm(inp["ssd_norm_g"][0])
                m["s5_glu_w"] = inp["s5_glu_w"][0]
                m["hyb_w_out"] = inp["hyb_w_out"][0]
            elif kind == "rwkv_post":
                m["w_mod1"] = inp["w_mod"][1]
                m["b_modT1"] = fm(inp["b_mod"][1])
                m["rwkv_w_o"] = inp["rwkv_w_o"][0]
            elif kind == "final":
                m["final_gT"] = fm(inp["final_g"])
        m.update(extra[core])
        maps.append(m)
    return maps


def _run(nc, maps):
    return run_bass_kernel_spmd(nc, maps, core_ids=list(range(NCORES))).results


def _only_declared(maps):
    keep = set(LAST_DRAM.keys())
    return [{k: v for k, v in m.items() if k in keep} for m in maps]


def _pair_cat_tokens(tiles, b):
    return np.ascontiguousarray(np.concatenate([tiles[2 * b], tiles[2 * b + 1]], axis=2))


GROUPS = [[0, 1], [2, 3], [4, 5], [6, 7]]
ST0 = [{"kind": "ffn", "l": 0, "s": 0}, {"kind": "h_out", "l": 0}, {"kind": "x_out"}]
ST1 = [{"kind": "mix0_post"}, {"kind": "ffn", "l": 0, "s": 2}, {"kind": "ffn", "l": 1, "s": 0},
       {"kind": "h_out", "l": 1}, {"kind": "x_out"}]
ST2 = [{"kind": "rwkv_post"}, {"kind": "ffn", "l": 1, "s": 2}, {"kind": "final"}]


def build_fused(upto=None):
    nc = bass.Bass("TRN2", target_bir_lowering=False)
    cx = Ctx(nc)
    banks = [cx.ps(f"bank{i}") for i in range(7)]
    cx.uid += 1
    bankT = nc.alloc_psum_tensor(f"bankT_{cx.uid}", [128, 1024], BF16)
    G = {"nc": nc, "cx": cx, "dram": {}, "banks": banks, "bankT": bankT}

    def idram(name, shape, dt):
        return nc.dram_tensor(name, list(shape), dt).ap()

    ncc = [0]

    def allgather(src, dst):
        sl = cx.slot("cc")
        nc.gpsimd.collective_compute("AllGather", ALU.bypass, replica_groups=GROUPS,
                                     ins=[src.opt()], outs=[dst.opt()]).then_inc(sl["sem"])
        sl["count"] += 1
        ncc[0] += 1
        cx._record((sl["sem"], sl["count"], sl["key"]), [], [f"cc{ncc[0]}"])
        return f"cc{ncc[0]}"

    CH = 4096

    def chunks(name, ncol, dt):
        n = ncol // CH
        return ([idram(f"{name}_s{i}", [128, CH], dt) for i in range(n)],
                [idram(f"{name}_g{i}", [256, CH], dt) for i in range(n)])

    def gather_all(snd, rcv):
        return [allgather(a, b) for a, b in zip(snd, rcv)]

    def h_out_pairs(snd):
        return lambda h: [(snd[c].rearrange("p (k t) -> p k t", k=4), h[:, 4 * c:4 * c + 4, :]) for c in range(4)]

    def h_loader(rcv):
        def f(tile, off):
            pairs = []
            for r in range(2):
                for c in range(4):
                    src = rcv[c][r * 128:(r + 1) * 128, :].rearrange("p (k t) -> p k t", k=4)
                    pairs.append((tile[:, 4 * c:4 * c + 4, off + r * TOK:off + (r + 1) * TOK], src))
            return pairs
        return f

    def tile_fn(snd):
        return lambda idx: snd[idx // 2][:, (idx % 2) * SEQ:(idx % 2 + 1) * SEQ]

    def gath_fn(rcv):
        return lambda r, lt, half: rcv[lt // 2][r * 128:(r + 1) * 128, (lt % 2) * SEQ + half * TOK:(lt % 2) * SEQ + (half + 1) * TOK]

    xs1 = idram("xs1", [128, KT, TOK], F32)
    xs2 = idram("xs2", [128, KT, TOK], F32)
    h0s, h0g = chunks("h0", KT * TOK, BF16)
    h1s, h1g = chunks("h1", KT * TOK, BF16)
    y0s, y0g = chunks("y0", 12 * SEQ, F32)
    y1s, y1g = chunks("y1", 8 * SEQ, BF16)

    def dbg(n, src, shape, dt):
        if upto != n:
            return False
        cx.barrier()
        o = nc.dram_tensor("dbg", list(shape), dt, kind="ExternalOutput").ap()
        cx.dma("sp", cx.slot("dbg"), [(o, src)])
        cx.wait_all("sp")
        return True

    global LAST_DRAM
    LAST_DRAM = G["dram"]
    build_T(ST0, G, io={"hT_out_pairs": h_out_pairs(h0s), "xT_out": xs1})
    if dbg(1, xs1, [128, KT, TOK], F32):
        return nc
    cx.new_phase()
    dep = gather_all(h0s, h0g)
    if dbg(2, h0g[3], [256, CH], BF16):
        return nc
    build_M0(G=G, io={"hT_pairs": h_loader(h0g), "y_tile": tile_fn(y0s), "dep": dep})
    if dbg(3, y0s[0], [128, CH], F32):
        return nc
    cx.new_phase()
    dep = gather_all(y0s, y0g)
    if dbg(4, y0g[5], [256, CH], F32):
        return nc
    build_T(ST1, G, io={"xT": xs1, "y_gath": gath_fn(y0g), "hT_out_pairs": h_out_pairs(h1s), "xT_out": xs2, "dep": dep})
    if dbg(5, xs2, [128, KT, TOK], F32):
        return nc
    cx.new_phase()
    dep = gather_all(h1s, h1g)
    build_M1(G=G, io={"hT_pairs": h_loader(h1g), "yg_tile": tile_fn(y1s), "dep": dep})
    if dbg(6, y1s[0], [128, CH], BF16):
        return nc
    cx.new_phase()
    dep = gather_all(y1s, y1g)
    build_T(ST2, G, io={"xT": xs2, "yg_gath": gath_fn(y1g), "dep": dep})
    cx.wait_all("sp")
    return nc


LAST_DRAM = {}


def kernel_unfused(**inputs):
    inp = {k: np.asarray(v) for k, v in inputs.items()}
    xT = to_xT(inp["x"].astype(np.float32, copy=False))
    st0 = [{"kind": "ffn", "l": 0, "s": 0}, {"kind": "h_out", "l": 0}, {"kind": "x_out"}]
    r = _run(build_T(st0), _T_maps(inp, st0, xT, [{}] * NCORES))
    xT = [np.asarray(q["xT_out"]) for q in r]
    hT = [np.asarray(q["hT_out"]) for q in r]
    maps = [prep_M0(inp, c // 2, c % 2, _pair_cat_tokens(hT, c // 2)) for c in range(NCORES)]
    r = _run(build_M0(), maps)
    y5 = [np.asarray(q["y5T"]) for q in r]
    ys = [np.asarray(q["ysT"]) for q in r]
    extra = []
    for c in range(NCORES):
        b, jt = c // 2, c % 2
        ts = slice(jt * TOK, (jt + 1) * TOK)
        extra.append({"y5T_in": np.ascontiguousarray(np.concatenate([y5[2 * b][:, :, ts], y5[2 * b + 1][:, :, ts]], axis=1)),
                      "ysT_in": np.ascontiguousarray(np.concatenate([ys[2 * b][:, :, ts], ys[2 * b + 1][:, :, ts]], axis=1))})
    st1 = [{"kind": "mix0_post"}, {"kind": "ffn", "l": 0, "s": 2}, {"kind": "ffn", "l": 1, "s": 0},
           {"kind": "h_out", "l": 1}, {"kind": "x_out"}]
    r = _run(build_T(st1), _T_maps(inp, st1, xT, extra))
    xT = [np.asarray(q["xT_out"]) for q in r]
    hT = [np.asarray(q["hT_out"]) for q in r]
    maps = [prep_M1(inp, c // 2, c % 2, _pair_cat_tokens(hT, c // 2)) for c in range(NCORES)]
    r = _run(build_M1(), maps)
    yg = [np.asarray(q["ygT"]) for q in r]
    extra = []
    for c in range(NCORES):
        b, jt = c // 2, c % 2
        ts = slice(jt * TOK, (jt + 1) * TOK)
        extra.append({"ygT_in": np.ascontiguousarray(np.concatenate([yg[2 * b][:, :, ts], yg[2 * b + 1][:, :, ts]], axis=1))})
    st2 = [{"kind": "rwkv_post"}, {"kind": "ffn", "l": 1, "s": 2}, {"kind": "final"}]
    r = _run(build_T(st2), _T_maps(inp, st2, xT, extra))
    out = from_xT([np.asarray(q["xT_out"]) for q in r])
    return out.astype(np.float32)


def fused_maps(inp):
    xT = to_xT(inp["x"].astype(np.float32, copy=False))
    maps = []
    for c in range(NCORES):
        b, j = c // 2, c % 2
        m = {}
        for st in (ST0, ST1, ST2):
            m.update(_T_maps(inp, st, xT, [{}] * NCORES)[c])
        m0 = prep_M0(inp, b, j, None)
        m1 = prep_M1(inp, b, j, None)
        m0.pop("hT")
        m1.pop("hT")
        m.update(m0)
        m.update(m1)
        sel = np.zeros((128, 2), np.float32)
        sel[:, j] = 1.0
        m["selT"] = sel
        maps.append(m)
    return maps


def kernel(**inputs):
    inp = {k: np.asarray(v) for k, v in inputs.items()}
    nc = build_fused()
    r = _run(nc, _only_declared(fused_maps(inp)))
    out = from_xT([np.asarray(q["xT_out"]) for q in r])
    return out.astype(np.float32)
```
